# Optimizing a Trainium2 kernel written in Bass

```python
import math
import jax, jax.numpy as jnp
from jax import lax
import numpy as np

D_MODEL = 1024
BATCH = 4
SEQ = 4096
DEPTH = 4

N_MIXERS = 4
EPS = 1e-6
NEG = -1e30
BIG = 1e30
T5_BUCKETS = 32
T5_MAX_DIST = 128
ATTN_HEADS = 16
HEAD_DIM = D_MODEL // ATTN_HEADS
KV_HEADS = 4
GQA = ATTN_HEADS // KV_HEADS
Q_BLOCK = 128
SWA_WINDOW = 128
RWKV_HEAD = 64
RWKV_HEADS = D_MODEL // RWKV_HEAD
RWKV_LORA_W = 64
RWKV_LORA_A = 64
RWKV_GN_EPS = 64e-5
NSA_CMP_LEN = 32
NSA_CMP_STRIDE = 16
NSA_CMP_HIDDEN = 128
NSA_SEL_LEN = 64
NSA_TOPK = 16
NSA_WINDOW = 512
NSA_SEL_QCHUNK = 64
LRU_WIDTH = 1280
LRU_BLOCKS = 16
LRU_BLOCK = LRU_WIDTH // LRU_BLOCKS
LRU_C = 8.0
CONV_WIDTH = 4

kernel_name = "hybrid_swa_rwkv7_nsa_rglru_trunk"


def _layers_of(m):
    return len(range(m, DEPTH, N_MIXERS))


def rms_norm(x, g):
    xf = x.astype(jnp.float32)
    y = xf * lax.rsqrt(jnp.mean(xf * xf, axis=-1, keepdims=True) + EPS)
    return (y * g.astype(jnp.float32)).astype(x.dtype)


def t5_bucket(dist):
    max_exact = T5_BUCKETS // 2
    d = jnp.maximum(dist, 0)
    df = jnp.maximum(d, 1).astype(jnp.float32)
    large = max_exact + (jnp.log(df / max_exact) / math.log(T5_MAX_DIST / max_exact)
                         * (T5_BUCKETS - max_exact)).astype(jnp.int32)
    large = jnp.minimum(large, T5_BUCKETS - 1)
    return jnp.where(d < max_exact, d, large)


def q_heads(t, B, T):
    return t.reshape(B, T, KV_HEADS, GQA, HEAD_DIM).transpose(0, 2, 3, 1, 4)


def kv_heads(t, B, T):
    return t.reshape(B, T, KV_HEADS, HEAD_DIM).transpose(0, 2, 1, 3)


def merge_heads(o, B, T):
    return o.transpose(0, 3, 1, 2, 4).reshape(B, T, ATTN_HEADS * HEAD_DIM)


def banded_attention(q, k, v, t5_table, window, sinks=None):
    B, G, R, T, Dh = q.shape
    nb = T // Q_BLOCK
    nprev = -(-window // Q_BLOCK)
    kc = (nprev + 1) * Q_BLOCK
    qb = q.reshape(B, G, R, nb, Q_BLOCK, Dh)
    pad = ((0, 0), (0, 0), (nprev * Q_BLOCK, 0), (0, 0))
    kp = jnp.pad(k, pad).reshape(B, G, nb + nprev, Q_BLOCK, Dh)
    vp = jnp.pad(v, pad).reshape(B, G, nb + nprev, Q_BLOCK, Dh)
    kb = jnp.concatenate([kp[:, :, j:j + nb] for j in range(nprev + 1)], axis=3)
    vb = jnp.concatenate([vp[:, :, j:j + nb] for j in range(nprev + 1)], axis=3)
    s = jnp.einsum('bgrnqd,bgnkd->bgrnqk', qb, kb,
                   preferred_element_type=jnp.float32) * (Dh ** -0.5)
    dist = nprev * Q_BLOCK + jnp.arange(Q_BLOCK)[:, None] - jnp.arange(kc)[None, :]
    kpos = (jnp.arange(nb)[:, None, None] - nprev) * Q_BLOCK + jnp.arange(kc)[None, None, :]
    valid = (dist >= 0) & (dist < window) & (kpos >= 0)
    bias = jnp.take(t5_table, t5_bucket(dist), axis=0)
    bias = jnp.moveaxis(bias, -1, 0).reshape(G, R, Q_BLOCK, kc).astype(jnp.float32)
    s = jnp.where(valid, s + bias[None, :, :, None], NEG)
    if sinks is None:
        p = jax.nn.softmax(s, axis=-1)
    else:
        sk = jnp.broadcast_to(sinks.astype(jnp.float32).reshape(1, G, R, 1, 1, 1), s.shape[:-1] + (1,))
        p = jax.nn.softmax(jnp.concatenate([s, sk], axis=-1), axis=-1)[..., :-1]
    o = jnp.einsum('bgrnqk,bgnkd->bgrnqd', p.astype(v.dtype), vb)
    return o.reshape(B, G, R, T, Dh)


def swa_sink_mixer(xn, w_in, sinks, w_out, t5_table):
    B, T, _ = xn.shape
    nq, nkv = ATTN_HEADS * HEAD_DIM, KV_HEADS * HEAD_DIM
    q, k, v, z = jnp.split(xn @ w_in, [nq, nq + nkv, nq + 2 * nkv], axis=-1)
    o = banded_attention(q_heads(q, B, T), kv_heads(k, B, T), kv_heads(v, B, T),
                         t5_table, SWA_WINDOW, sinks)
    return (merge_heads(o, B, T) * jax.nn.silu(z)) @ w_out


def rwkv7_mixer(xn, mu, w_in, w0, w1, w2, a0, a1, a2, k_k, k_a, r_k, lnx_w, lnx_b, w_out):
    B, T, D = xn.shape
    H, N = RWKV_HEADS, RWKV_HEAD
    C = H * N
    f32 = jnp.float32
    xx = jnp.pad(xn, ((0, 0), (1, 0), (0, 0)))[:, :-1] - xn
    lerp = xn[None] + xx[None] * mu[:, None, None, :]
    rkvz = jnp.einsum('sbtd,dsc->sbtc', lerp[:4], w_in.reshape(D, 4, C))
    r, k, v, z = rkvz[0], rkvz[1], rkvz[2], rkvz[3]
    w = -jax.nn.softplus(-(w0 + jnp.tanh(lerp[4] @ w1) @ w2)) - 0.5
    a = jax.nn.sigmoid(a0 + (lerp[5] @ a1) @ a2)
    hs = lambda t: t.astype(f32).reshape(B, T, H, N)
    kk = hs(k * k_k)
    kk = kk / jnp.maximum(jnp.sqrt(jnp.sum(kk * kk, axis=-1, keepdims=True)), 1e-12)
    k = hs(k * (1 + (a - 1) * k_a))
    r, v, a = hs(r), hs(v), hs(a)
    decay = jnp.exp(-jnp.exp(hs(w)))
    aa, bb = -kk, kk * a

    def step(S, inp):
        r_t, w_t, k_t, v_t, a_t, b_t = inp
        sa = jnp.einsum('bhij,bhj->bhi', S, a_t)
        S = S * w_t[:, :, None, :] + sa[..., None] * b_t[:, :, None, :] + v_t[..., None] * k_t[:, :, None, :]
        return S, jnp.einsum('bhij,bhj->bhi', S, r_t)

    xs = tuple(jnp.moveaxis(t, 1, 0) for t in (r, decay, k, v, aa, bb))
    _, y = lax.scan(step, jnp.zeros((B, H, N, N), f32), xs)
    y = jnp.moveaxis(y, 0, 1)
    mean = jnp.mean(y, axis=-1, keepdims=True)
    var = jnp.mean(jnp.square(y - mean), axis=-1, keepdims=True)
    y = ((y - mean) * lax.rsqrt(var + RWKV_GN_EPS)).reshape(B, T, C) * lnx_w.astype(f32) + lnx_b.astype(f32)
    bonus = jnp.sum(r * k * r_k.astype(f32), axis=-1, keepdims=True) * v
    y = (y + bonus.reshape(B, T, C)) * jax.nn.silu(z.astype(f32))
    return y.astype(xn.dtype) @ w_out


def nsa_mixer(xn, w_in, cmp_pos_k, cmp_k_w1, cmp_k_w2, cmp_pos_v, cmp_v_w1, cmp_v_w2, w_out, t5_table):
    B, T, _ = xn.shape
    G, R, Dh = KV_HEADS, GQA, HEAD_DIM
    f32 = jnp.float32
    nq, nkv = ATTN_HEADS * Dh, G * Dh
    splits = np.cumsum([nq] + [nkv] * 6 + [3 * ATTN_HEADS]).tolist()
    q, kc, vc, ks, vs, kw, vw, gates, z = jnp.split(xn @ w_in, splits, axis=-1)
    q = q_heads(q, B, T)
    kc, vc, ks, vs, kw, vw = (kv_heads(t, B, T) for t in (kc, vc, ks, vs, kw, vw))
    scale = Dh ** -0.5
    tpos = jnp.arange(T)

    n_cmp = (T - NSA_CMP_LEN) // NSA_CMP_STRIDE + 1
    tok_idx = np.arange(n_cmp)[:, None] * NSA_CMP_STRIDE + np.arange(NSA_CMP_LEN)[None, :]

    def compress(t, pos, w1, w2):
        blk = (t[:, :, tok_idx] + pos).reshape(B, G, n_cmp, NSA_CMP_LEN * Dh)
        return jax.nn.silu(blk @ w1) @ w2

    k_cmp = compress(kc, cmp_pos_k, cmp_k_w1, cmp_k_w2)
    v_cmp = compress(vc, cmp_pos_v, cmp_v_w1, cmp_v_w2)
    cmp_start = jnp.arange(n_cmp) * NSA_CMP_STRIDE
    cmp_end = cmp_start + NSA_CMP_LEN - 1
    cmp_ok = cmp_end[None, :] <= tpos[:, None]
    s_c = jnp.einsum('bgrtd,bgnd->bgrtn', q, k_cmp, preferred_element_type=f32) * scale
    p_c = jax.nn.softmax(jnp.where(cmp_ok, s_c, NEG), axis=-1) * cmp_ok
    o_cmp = jnp.einsum('bgrtn,bgnd->bgrtd', p_c.astype(v_cmp.dtype), v_cmp)

    n_sel = T // NSA_SEL_LEN
    k_top = min(NSA_TOPK, n_sel)
    sel_start = jnp.arange(n_sel) * NSA_SEL_LEN
    overlap = ((cmp_start[:, None] < sel_start[None, :] + NSA_SEL_LEN)
               & (cmp_end[:, None] >= sel_start[None, :])).astype(f32)
    imp = jnp.einsum('bgrtn,ns->bgts', p_c, overlap)
    blk = jnp.arange(n_sel)[None, :]
    cur = (tpos // NSA_SEL_LEN)[:, None]
    forced = (blk == 0) | (blk == cur) | (blk == cur - 1)
    future = sel_start[None, :] > tpos[:, None]
    imp = jnp.where(forced, BIG, jnp.where(future, NEG, imp))
    _, sel_idx = lax.top_k(imp, k_top)

    ks_blk = ks.reshape(B, G, n_sel, NSA_SEL_LEN, Dh)
    vs_blk = vs.reshape(B, G, n_sel, NSA_SEL_LEN, Dh)
    QC = NSA_SEL_QCHUNK
    nch = T // QC
    q_ch = jnp.moveaxis(q.reshape(B, G, R, nch, QC, Dh), 3, 0)
    idx_ch = jnp.moveaxis(sel_idx.reshape(B, G, nch, QC, k_top), 2, 0)
    pos_ch = tpos.reshape(nch, QC)
    bi = jnp.arange(B)[:, None, None, None]
    gi = jnp.arange(G)[None, :, None, None]
    gi5 = jnp.arange(G)[None, :, None, None, None]
    table_g = t5_table.reshape(T5_BUCKETS, G, R)

    def sel_chunk(args):
        qc, ic, pc = args
        kg = ks_blk[bi, gi, ic]
        vg = vs_blk[bi, gi, ic]
        kpos = ic[..., None] * NSA_SEL_LEN + jnp.arange(NSA_SEL_LEN)
        dist = pc[None, None, :, None, None] - kpos
        s = jnp.einsum('bgrqd,bgqskd->bgrqsk', qc, kg, preferred_element_type=f32) * scale
        bias = jnp.moveaxis(table_g[t5_bucket(dist), gi5], -1, 2).astype(f32)
        s = jnp.where((dist >= 0)[:, :, None], s + bias, NEG)
        p = jax.nn.softmax(s.reshape(B, G, R, QC, k_top * NSA_SEL_LEN), axis=-1)
        p = p.reshape(B, G, R, QC, k_top, NSA_SEL_LEN)
        return jnp.einsum('bgrqsk,bgqskd->bgrqd', p.astype(vg.dtype), vg)

    o_sel = lax.map(sel_chunk, (q_ch, idx_ch, pos_ch))
    o_sel = jnp.moveaxis(o_sel, 0, 3).reshape(B, G, R, T, Dh)

    o_win = banded_attention(q, kw, vw, t5_table, NSA_WINDOW)

    g = jax.nn.sigmoid(gates).reshape(B, T, 3, G, R).transpose(2, 0, 3, 4, 1)[..., None]
    o = g[0] * o_cmp + g[1] * o_sel + g[2] * o_win
    return (merge_heads(o, B, T) * jax.nn.silu(z)) @ w_out


def rglru_mixer(xn, w_in, conv_w, conv_b, gate_a_w, gate_a_b, gate_x_w, gate_x_b, lam, w_out):
    B, T, _ = xn.shape
    f32 = jnp.float32
    u, z = jnp.split(xn @ w_in, 2, axis=-1)
    u = lax.conv_general_dilated(u, conv_w[:, None, :], window_strides=(1,),
                                 padding=[(CONV_WIDTH - 1, 0)],
                                 dimension_numbers=('NWC', 'WIO', 'NWC'),
                                 feature_group_count=LRU_WIDTH) + conv_b
    ub = u.reshape(B, T, LRU_BLOCKS, LRU_BLOCK)
    r = jax.nn.sigmoid(jnp.einsum('btnc,ncd->btnd', ub, gate_a_w).reshape(B, T, LRU_WIDTH) + gate_a_b)
    i = jax.nn.sigmoid(jnp.einsum('btnc,ncd->btnd', ub, gate_x_w).reshape(B, T, LRU_WIDTH) + gate_x_b)
    log_a = -LRU_C * r.astype(f32) * jax.nn.softplus(-lam.astype(f32))
    a = jnp.exp(log_a)
    b = jnp.sqrt(-jnp.expm1(2.0 * log_a)) * (i * u).astype(f32)

    def combine(left, right):
        a1, b1 = left
        a2, b2 = right
        return a1 * a2, a2 * b1 + b2

    _, h = lax.associative_scan(combine, (a, b), axis=1)
    return (h.astype(xn.dtype) * jax.nn.silu(z)) @ w_out


def setup_inputs(seed: int = 0) -> dict:
    key = jax.random.key(seed)
    keys = iter(jax.random.split(key, 64))
    f32 = jnp.float32

    def nrm(shape, scale):
        return jax.random.normal(next(keys), shape, f32) * scale

    def dense(shape):
        return nrm(shape, shape[-2] ** -0.5)

    def unif(shape, lo, hi):
        return jax.random.uniform(next(keys), shape, f32, lo, hi)

    LA, LB, LC, LD = (_layers_of(m) for m in range(N_MIXERS))
    C = RWKV_HEADS * RWKV_HEAD
    nq, nkv = ATTN_HEADS * HEAD_DIM, KV_HEADS * HEAD_DIM
    a_cols = 2 * nq + 2 * nkv
    c_cols = 2 * nq + 6 * nkv + 3 * ATTN_HEADS
    lam_u = unif((LD, LRU_WIDTH), 0.9, 0.999)
    return {
        "x": nrm((BATCH, SEQ, D_MODEL), 1.0),
        "t5_table": nrm((T5_BUCKETS, ATTN_HEADS), 0.5),
        "norm_g": 1.0 + nrm((DEPTH, D_MODEL), 0.05),
        "final_g": 1.0 + nrm((D_MODEL,), 0.05),
        "a_w_in": dense((LA, D_MODEL, a_cols)),
        "a_sinks": nrm((LA, ATTN_HEADS), 0.5),
        "a_w_out": dense((LA, nq, D_MODEL)),
        "b_mu": unif((LB, 6, D_MODEL), 0.0, 1.0),
        "b_w_in": dense((LB, D_MODEL, 4 * C)),
        "b_w0": unif((LB, C), -6.0, 0.0),
        "b_w1": dense((LB, D_MODEL, RWKV_LORA_W)),
        "b_w2": dense((LB, RWKV_LORA_W, C)),
        "b_a0": nrm((LB, C), 0.5),
        "b_a1": dense((LB, D_MODEL, RWKV_LORA_A)),
        "b_a2": dense((LB, RWKV_LORA_A, C)),
        "b_k_k": 0.85 + nrm((LB, C), 0.1),
        "b_k_a": 1.0 + nrm((LB, C), 0.1),
        "b_r_k": nrm((LB, RWKV_HEADS, RWKV_HEAD), 0.1),
        "b_lnx_w": 1.0 + nrm((LB, C), 0.05),
        "b_lnx_b": nrm((LB, C), 0.01),
        "b_w_out": dense((LB, C, D_MODEL)),
        "c_w_in": dense((LC, D_MODEL, c_cols)),
        "c_cmp_pos_k": nrm((LC, NSA_CMP_LEN, HEAD_DIM), 0.1),
        "c_cmp_k_w1": dense((LC, NSA_CMP_LEN * HEAD_DIM, NSA_CMP_HIDDEN)),
        "c_cmp_k_w2": dense((LC, NSA_CMP_HIDDEN, HEAD_DIM)),
        "c_cmp_pos_v": nrm((LC, NSA_CMP_LEN, HEAD_DIM), 0.1),
        "c_cmp_v_w1": dense((LC, NSA_CMP_LEN * HEAD_DIM, NSA_CMP_HIDDEN)),
        "c_cmp_v_w2": dense((LC, NSA_CMP_HIDDEN, HEAD_DIM)),
        "c_w_out": dense((LC, nq, D_MODEL)),
        "d_w_in": dense((LD, D_MODEL, 2 * LRU_WIDTH)),
        "d_conv_w": nrm((LD, CONV_WIDTH, LRU_WIDTH), CONV_WIDTH ** -0.5),
        "d_conv_b": nrm((LD, LRU_WIDTH), 0.01),
        "d_gate_a_w": dense((LD, LRU_BLOCKS, LRU_BLOCK, LRU_BLOCK)),
        "d_gate_a_b": nrm((LD, LRU_WIDTH), 0.01),
        "d_gate_x_w": dense((LD, LRU_BLOCKS, LRU_BLOCK, LRU_BLOCK)),
        "d_gate_x_b": nrm((LD, LRU_WIDTH), 0.01),
        "d_lambda": jnp.log(lam_u) - jnp.log1p(-lam_u),
        "d_w_out": dense((LD, LRU_WIDTH, D_MODEL)),
    }


def reference(x, t5_table, norm_g, final_g,
              a_w_in, a_sinks, a_w_out,
              b_mu, b_w_in, b_w0, b_w1, b_w2, b_a0, b_a1, b_a2, b_k_k, b_k_a, b_r_k,
              b_lnx_w, b_lnx_b, b_w_out,
              c_w_in, c_cmp_pos_k, c_cmp_k_w1, c_cmp_k_w2, c_cmp_pos_v, c_cmp_v_w1, c_cmp_v_w2, c_w_out,
              d_w_in, d_conv_w, d_conv_b, d_gate_a_w, d_gate_a_b, d_gate_x_w, d_gate_x_b,
              d_lambda, d_w_out):
    for layer in range(DEPTH):
        m, j = layer % N_MIXERS, layer // N_MIXERS
        xn = rms_norm(x, norm_g[layer])
        if m == 0:
            y = swa_sink_mixer(xn, a_w_in[j], a_sinks[j], a_w_out[j], t5_table)
        elif m == 1:
            y = rwkv7_mixer(xn, b_mu[j], b_w_in[j], b_w0[j], b_w1[j], b_w2[j], b_a0[j], b_a1[j],
                            b_a2[j], b_k_k[j], b_k_a[j], b_r_k[j], b_lnx_w[j], b_lnx_b[j], b_w_out[j])
        elif m == 2:
            y = nsa_mixer(xn, c_w_in[j], c_cmp_pos_k[j], c_cmp_k_w1[j], c_cmp_k_w2[j],
                          c_cmp_pos_v[j], c_cmp_v_w1[j], c_cmp_v_w2[j], c_w_out[j], t5_table)
        else:
            y = rglru_mixer(xn, d_w_in[j], d_conv_w[j], d_conv_b[j], d_gate_a_w[j], d_gate_a_b[j],
                            d_gate_x_w[j], d_gate_x_b[j], d_lambda[j], d_w_out[j])
        x = x + y.astype(x.dtype)
    return rms_norm(x, final_g)
```

```python
import numpy as np
from contextlib import ExitStack
import concourse.bass as bass
import concourse.mybir as mybir
from concourse.bass_utils import run_bass_kernel_spmd

F32 = mybir.dt.float32
BF16 = mybir.dt.bfloat16
AF = mybir.ActivationFunctionType
ALU = mybir.AluOpType
AX = mybir.AxisListType


class Buf:
    __slots__ = ("name", "lw", "rd")

    def __init__(self, name):
        self.name = name
        self.lw = None
        self.rd = {}


class Sched:
    COMPUTE = ("pe", "act", "dve", "pool")

    def __init__(self, nc, es, ndma_slots=8):
        self.nc = nc
        self.es = es
        self.ops = []
        self.bufs = {}
        self.ndma = ndma_slots
        self.handles = {"pe": nc.tensor, "act": nc.scalar, "dve": nc.vector, "pool": nc.gpsimd, "sp": nc.sync}
        self.need = []
        self.seg_dma = []
        self.last_compute = {}
        self.barrier_deps = set()
        self.pending_barrier = {}

    def buf(self, name):
        b = self.bufs.get(name)
        if b is None:
            b = Buf(name)
            self.bufs[name] = b
        return b

    def _B(self, lst):
        out = []
        for x in lst:
            if isinstance(x, str):
                out.append(self.buf(x))
            elif isinstance(x, Buf):
                out.append(x)
            elif x is None:
                continue
            else:
                out.extend(self._B(x))
        return out

    def op(self, eng, fn, r=(), w=(), dma=False, cc=False):
        i = len(self.ops)
        R = self._B(r)
        W = self._B(w)
        deps = set()
        if self.pending_barrier.get(eng):
            deps |= self.barrier_deps
            self.pending_barrier[eng] = False
        if cc:
            deps |= set(self.seg_dma)
        for b in R:
            if b.lw is not None:
                deps.add(b.lw)
        for b in W:
            if b.lw is not None:
                deps.add(b.lw)
            for k, v in b.rd.items():
                deps.add(v)
        for b in W:
            b.lw = i
            b.rd = {}
        key = ("dma", i) if dma else eng
        for b in R:
            b.rd[key] = i
        self.ops.append(dict(eng=eng, fn=fn, deps=deps, dma=dma, cc=cc))
        if dma:
            self.seg_dma.append(i)
        elif eng in self.COMPUTE:
            self.last_compute[eng] = i
        return i

    def _init_state(self):
        nc = self.nc
        self.sems = {e: self.es.enter_context(nc.semaphore("sem_" + e)) for e in self.COMPUTE}
        self.dsems = {q: [self.es.enter_context(nc.semaphore(f"dsem_{q}_{k}")) for k in range(self.ndma)] for q in ("sp", "pool")}
        self.ccsem = self.es.enter_context(nc.semaphore("sem_cc"))
        self.cccount = 0
        self.duses = {q: [0] * self.ndma for q in ("sp", "pool")}
        self.dcount = {"sp": 0, "pool": 0}
        self.cnt = {e: 0 for e in self.COMPUTE}
        self.token = []
        self.waited = {e: {} for e in self.handles}
        self.nwaits = 0
        self.emitted = 0
        self.inited = True

    def barrier(self):
        deps = set(self.seg_dma)
        for e in self.COMPUTE:
            if e in self.last_compute:
                deps.add(self.last_compute[e])
        self.barrier_deps = deps
        self.pending_barrier = {e: True for e in self.handles}
        self.seg_dma = []
        self.bufs = {}

    def emit(self, final=True):
        nc = self.nc
        ops = self.ops
        if not getattr(self, "inited", False):
            self._init_state()
        start = self.emitted
        n = len(ops)
        need = self.need
        need.extend([False] * (n - len(need)))
        for i in range(start, n):
            o = ops[i]
            for d in o["deps"]:
                po = ops[d]
                if po["dma"]:
                    continue
                if po["eng"] != o["eng"] or o["dma"] or o["eng"] != "pe":
                    assert d >= start or need[d], "cross-segment dependency on an op without increment"
                    need[d] = True
        lastc = {}
        for i in range(start, n):
            if not ops[i]["dma"] and ops[i]["eng"] in self.COMPUTE:
                lastc[ops[i]["eng"]] = i
        for e, i in lastc.items():
            need[i] = True
        sems, dsems, duses, dcount, cnt, token, waited = self.sems, self.dsems, self.duses, self.dcount, self.cnt, self.token, self.waited
        token.extend([None] * (n - len(token)))
        for i in range(start, n):
            o = ops[i]
            e = o["eng"]
            h = self.handles[e]
            wd = waited[e]
            reqs = {}
            for d in o["deps"]:
                po = ops[d]
                if (not po["dma"]) and (not o["dma"]) and po["eng"] == e and e == "pe":
                    continue
                sem, val, sk = token[d]
                if wd.get(sk, 0) >= val:
                    continue
                if sk not in reqs or reqs[sk][1] < val:
                    reqs[sk] = (sem, val)
            is_cc = o.get("cc", False)
            if o["dma"] and not is_cc:
                q = e
                s = dcount[q] % self.ndma
                dcount[q] += 1
                dsk = ("d", q, s)
                prev = 16 * duses[q][s]
                if prev > 0 and wd.get(dsk, 0) < prev:
                    if dsk not in reqs or reqs[dsk][1] < prev:
                        reqs[dsk] = (dsems[q][s], prev)
            for rk, (rsem, rval) in reqs.items():
                h.wait_ge(rsem, rval)
                wd[rk] = rval
                self.nwaits += 1
            ins = o["fn"](h)
            if is_cc:
                self.cccount += 1
                ins.then_inc(self.ccsem, 1)
                token[i] = (self.ccsem, self.cccount, ("cc",))
            elif o["dma"]:
                duses[q][s] += 1
                ins.then_inc(dsems[q][s], 16)
                token[i] = (dsems[q][s], 16 * duses[q][s], dsk)
            else:
                if need[i]:
                    cnt[e] += 1
                    ins.then_inc(sems[e], 1)
                    token[i] = (sems[e], cnt[e], ("c", e))
                else:
                    token[i] = (sems[e], cnt[e] + 0, ("c", e))
            o["fn"] = None
        self.emitted = n
        if final:
            h = self.handles["sp"]
            for q in ("sp", "pool"):
                for s in range(self.ndma):
                    if duses[q][s] > 0:
                        h.wait_ge(dsems[q][s], 16 * duses[q][s])
            if self.cccount:
                h.wait_ge(self.ccsem, self.cccount)
        self.stats = dict(nops=len(ops), nwaits=self.nwaits, incs=dict(cnt))
        return self.stats


class CFG:
    T = 4096
    prefix = ""
    nc = None
    S = None
    override = {}


def get_nc():
    if CFG.nc is not None:
        return CFG.nc
    return bass.Bass("TRN2", target_bir_lowering=False)


def get_sched(nc, es):
    if CFG.S is not None:
        return CFG.S
    return Sched(nc, es)
D = 1024
EPS = 1e-6


class Ctx:
    pass


SB_USED = [0]


def mk(nc, es, name, shape, dt, psum=False):
    if not psum:
        n = 1
        for d_ in shape[1:]:
            n *= d_
        n *= (2 if dt == BF16 else 4)
        SB_USED[0] += (n + 31) // 32 * 32
        assert SB_USED[0] <= 190 * 1024, f"SBUF over budget at {name}: {SB_USED[0]}"
    if psum:
        return es.enter_context(nc.psum_tensor(CFG.prefix + name, shape, dt))
    return es.enter_context(nc.sbuf_tensor(CFG.prefix + name, shape, dt))


def src_rows(src, t):
    if callable(src):
        return src(t)
    return src[t * 128:(t + 1) * 128, :]


def dram_in(nc, name, shape, dt=F32):
    if name in CFG.override:
        return CFG.override[name]
    return nc.dram_tensor(CFG.prefix + name, list(shape), dt, kind="ExternalInput").ap()


def dram_out(nc, name, shape, dt=F32):
    if name in CFG.override:
        return CFG.override[name]
    return nc.dram_tensor(CFG.prefix + name, list(shape), dt, kind="ExternalOutput").ap()


class P1:
    def __init__(self, S, nc, es, srcs, xs_out, g_row, ident_d, ps_tr, name="p1"):
        self.S, self.nc, self.srcs, self.xs_out, self.ps_tr, self.name = S, nc, srcs, xs_out, ps_tr, name
        self.g_bc = mk(nc, es, name + "_g", [128, D], F32)
        self.ident = mk(nc, es, name + "_id", [128, 128], BF16)
        self.identf = mk(nc, es, name + "_idf", [128, 128], F32)
        g_bc, ident, identf = self.g_bc, self.ident, self.identf
        S.op("sp", lambda h: h.dma_start(out=g_bc[:], in_=g_row.partition_broadcast(128)), w=[name + "g"], dma=True)
        S.op("sp", lambda h: h.dma_start(out=identf[:], in_=ident_d), w=[name + "idf"], dma=True)
        S.op("dve", lambda h: h.tensor_copy(out=ident[:], in_=identf[:]), r=[name + "idf"], w=[name + "id"])
        self.NB = 2
        self.xt = [mk(nc, es, f"{name}_x{b}", [128, D], F32) for b in range(self.NB)]
        self.sq = mk(nc, es, name + "_sq", [128, D], BF16)
        self.xnb = [mk(nc, es, f"{name}_xn{b}", [128, D], BF16) for b in range(self.NB)]
        self.st = [mk(nc, es, f"{name}_st{b}", [128, 4], F32) for b in range(self.NB)]

    def tile(self, t, dst_ap, dst_buf):
        S, name = self.S, self.name
        xt, sq, xnb, st, g_bc, ident, ps_tr = self.xt, self.sq, self.xnb, self.st, self.g_bc, self.ident, self.ps_tr
        b = t % self.NB
        rows = slice(t * 128, (t + 1) * 128)
        xb = f"{name}x{b}"
        for k, src in enumerate(self.srcs):
            if k == 0:
                S.op("pool", lambda h, src=src: h.dma_start(out=xt[b][:], in_=src_rows(src, t)), w=[xb], dma=True)
            else:
                S.op("pool", lambda h, src=src: h.dma_start(out=xt[b][:], in_=src_rows(src, t), accum_op=ALU.add), r=[xb], w=[xb], dma=True)
        if self.xs_out is not None and len(self.srcs) > 1:
            S.op("sp", lambda h: h.dma_start(out=self.xs_out[rows, :], in_=xt[b][:]), r=[xb], dma=True)
        S.op("act", lambda h: h.activation(out=sq[:], in_=xt[b][:], func=AF.Square), r=[xb], w=[name + "sq"])
        S.op("dve", lambda h: h.tensor_reduce(out=st[b][:, 0:1], in_=sq[:], axis=AX.X, op=ALU.add), r=[name + "sq"], w=[f"{name}st{b}"])
        S.op("act", lambda h: h.activation(out=st[b][:, 1:2], in_=st[b][:, 0:1], func=AF.Sqrt, scale=1.0 / D, bias=EPS), r=[f"{name}st{b}"], w=[f"{name}st{b}"])
        S.op("dve", lambda h: h.reciprocal(out=st[b][:, 2:3], in_=st[b][:, 1:2]), r=[f"{name}st{b}"], w=[f"{name}st{b}r"])
        S.op("dve", lambda h: h.scalar_tensor_tensor(out=xnb[b][:], in0=xt[b][:], scalar=st[b][:, 2:3], in1=g_bc[:], op0=ALU.mult, op1=ALU.mult),
             r=[xb, f"{name}st{b}r", name + "g"], w=[f"{name}xn{b}"])
        for dc in range(8):
            S.op("pe", lambda h, dc=dc: h.transpose(out=ps_tr[:, dc * 128:(dc + 1) * 128], in_=xnb[b][:, dc * 128:(dc + 1) * 128], identity=ident[:]),
                 r=[f"{name}xn{b}", name + "id"], w=[name + "pstr"])
        S.op("act", lambda h: h.copy(out=dst_ap, in_=ps_tr[:].rearrange("p (c n) -> p c n", c=8)), r=[name + "pstr"], w=[dst_buf])


def phase1(S, nc, es, srcs, xs_out, g_row, ident_d, ps_tr, ntiles=None, xnT=None, name="p1"):
    if ntiles is None:
        ntiles = CFG.T // 128
    p1 = P1(S, nc, es, srcs, xs_out, g_row, ident_d, ps_tr, name)
    for t in range(ntiles):
        p1.tile(t, xnT[:, :, t * 128:(t + 1) * 128], f"xnT{t // 4}")
    return p1.ident


def dbg(S, nc, name, ap, shape, rbuf, dt=F32):
    o = nc.dram_tensor("dbg_" + name, list(shape), dt, kind="ExternalOutput").ap()
    S.op("sp", lambda h: h.dma_start(out=o, in_=ap), r=rbuf, dma=True)


NEGM = -30000.0


def build_A(nsrc=1, debug=False):
    T = CFG.T
    NT = T // 128
    nc = get_nc()
    es = ExitStack()
    S = get_sched(nc, es)
    srcs = [dram_in(nc, f"xin{k}", [T, D]) for k in range(nsrc)]
    xs_out = dram_out(nc, "xs", [T, D]) if nsrc > 1 else None
    g_row = dram_in(nc, "g", [1, D])
    ident_d = dram_in(nc, "ident", [128, 128])
    wq_d = dram_in(nc, "wq", [D, 512])
    wk_d = dram_in(nc, "wk", [D, 128])
    wv_d = dram_in(nc, "wv", [D, 128])
    wz_d = dram_in(nc, "wz", [D, 512])
    wo_d = dram_in(nc, "wo", [512, D])
    bias_d = dram_in(nc, "biasT", [128, 2, 2 * 4 * 128])
    sink_d = dram_in(nc, "sinks", [1, 8])
    p_out = dram_out(nc, "p", [T, D])

    xnT = mk(nc, es, "xnT", [128, 8, T], BF16)
    ps_tr = mk(nc, es, "ps_tr", [128, 1024], BF16, psum=True)
    ps_q = mk(nc, es, "ps_q", [128, 512], F32, psum=True)
    ps_z = mk(nc, es, "ps_z", [128, 512], F32, psum=True)
    ps_s = [mk(nc, es, f"ps_s{k}", [128, 512], F32, psum=True) for k in range(2)]
    ps_o = mk(nc, es, "ps_o", [128, 4, 128], F32, psum=True)
    ps_y = [mk(nc, es, f"ps_y{k}", [128, 512], F32, psum=True) for k in range(2)]
    ident = phase1(S, nc, es, srcs, xs_out, g_row, ident_d, ps_tr, xnT=xnT)

    wq = mk(nc, es, "wq_s", [128, 8, 512], BF16)
    wk = mk(nc, es, "wk_s", [128, 8, 128], BF16)
    wv = mk(nc, es, "wv_s", [128, 8, 128], BF16)
    wz = mk(nc, es, "wz_s", [128, 8, 512], BF16)
    wo = mk(nc, es, "wo_s", [128, 4, D], BF16)
    biasT = mk(nc, es, "biasT_s", [128, 2, 1024], BF16)
    esink = mk(nc, es, "esink", [128, 8], F32)
    for nm, t_, d_, pat in (("wq", wq, wq_d, "(c p) n -> p c n"), ("wk", wk, wk_d, "(c p) n -> p c n"), ("wv", wv, wv_d, "(c p) n -> p c n"),
                            ("wz", wz, wz_d, "(c p) n -> p c n"), ("wo", wo, wo_d, "(c p) n -> p c n")):
        S.op("pool", lambda h, t_=t_, d_=d_, pat=pat: h.dma_start(out=t_[:], in_=d_.rearrange(pat, p=128)), w=[nm], dma=True)
    S.op("pool", lambda h: h.dma_start(out=biasT[:], in_=bias_d), w=["biasT"], dma=True)
    S.op("sp", lambda h: h.dma_start(out=esink[:], in_=sink_d.partition_broadcast(128)), w=["esink"], dma=True)
    S.op("act", lambda h: h.activation(out=esink[:], in_=esink[:], func=AF.Exp), r=["esink"], w=["esink"])

    kT = mk(nc, es, "kT", [64, 2, T], BF16)
    vau = mk(nc, es, "vau", [128, NT, 2, 65], BF16)
    S.op("dve", lambda h: h.memset(vau[:, :, :, 64:65], 1.0), w=["vau_ones"])
    for g in range(2):
        for c in range(T // 512):
            tk = slice(c * 512, (c + 1) * 512)
            for dc in range(8):
                S.op("pe", lambda h, g=g, dc=dc, tk=tk: h.matmul(ps_q[0:64, :], lhsT=wk[:, dc, g * 64:(g + 1) * 64], rhs=xnT[:, dc, tk],
                                                                start=(dc == 0), stop=(dc == 7)), r=["wk", f"xnT{c}"], w=["ps_q"])
            S.op("act", lambda h, g=g, tk=tk: h.copy(out=kT[:, g, tk], in_=ps_q[0:64, :]), r=["ps_q"], w=[f"kT{c // 1}"])
    for t in range(NT):
        for dc in range(8):
            S.op("pe", lambda h, t=t, dc=dc: h.matmul(ps_z[:, 0:128], lhsT=xnT[:, dc, t * 128:(t + 1) * 128], rhs=wv[:, dc, :],
                                                      start=(dc == 0), stop=(dc == 7)), r=["wv", f"xnT{t // 4}"], w=["ps_z"])
        S.op("dve", lambda h, t=t: h.tensor_copy(out=vau[:, t, :, 0:64], in_=ps_z[:, 0:128].rearrange("p (g d) -> p g d", g=2)),
             r=["ps_z"], w=[f"vau{t}"])

    NB = 2
    qT = [mk(nc, es, f"qT{b}", [64, 2, 4, 128], BF16) for b in range(NB)]
    zs = [mk(nc, es, f"zs{b}", [128, 512], BF16) for b in range(NB)]
    pT = [mk(nc, es, f"pT{b}", [128, 512], BF16) for b in range(4)]
    yz = [mk(nc, es, f"yz{b}", [128, 512], BF16) for b in range(NB)]
    yzT = [mk(nc, es, f"yzT{b}", [128, 4, 128], BF16) for b in range(NB)]
    den = [mk(nc, es, f"den{b}", [128, 8], F32) for b in range(NB)]
    pt = [mk(nc, es, f"pt{b}", [128, D], F32) for b in range(NB)]
    pti = 0
    for qt in range(NT):
        b = qt % NB
        tq = slice(qt * 128, (qt + 1) * 128)
        xk = f"xnT{qt // 4}"
        for g in range(2):
            for r in range(4):
                for dc in range(8):
                    col = (g * 4 + r) * 64
                    S.op("pe", lambda h, g=g, r=r, dc=dc, col=col, tq=tq: h.matmul(ps_q[0:64, r * 128:(r + 1) * 128], lhsT=wq[:, dc, col:col + 64],
                                                                               rhs=xnT[:, dc, tq], start=(dc == 0), stop=(dc == 7)),
                         r=["wq", xk], w=["ps_q"])
            S.op("act", lambda h, g=g, b=b: h.activation(out=qT[b][:, g, :, :], in_=ps_q[0:64, :].rearrange("p (r n) -> p r n", r=4),
                                                        func=AF.Copy, scale=0.125), r=["ps_q"], w=[f"qT{b}_{g}"])
        for dc in range(8):
            S.op("pe", lambda h, dc=dc, tq=tq: h.matmul(ps_z[:, :], lhsT=xnT[:, dc, tq], rhs=wz[:, dc, :], start=(dc == 0), stop=(dc == 7)),
                 r=["wz", xk], w=["ps_z"])
        S.op("act", lambda h, b=b: h.activation(out=zs[b][:], in_=ps_z[:, :], func=AF.Silu), r=["ps_z"], w=[f"zs{b}"])
        for g in range(2):
            kts = [kt for kt in (qt - 1, qt) if kt >= 0]
            for kt in kts:
                cls = 0 if kt == qt else 1
                si = kt % 2
                pi = (g * 2 + si)
                S.op("pe", lambda h, g=g, kt=kt, si=si, b=b: h.matmul(ps_s[si][:, :], lhsT=kT[:, g, kt * 128:(kt + 1) * 128],
                                                                     rhs=qT[b][:, g, :, :].rearrange("p r n -> p (r n)"), start=True, stop=False),
                     r=[f"kT{kt // 4}", f"qT{b}_{g}"], w=[f"ps_s{si}"])
                S.op("pe", lambda h, g=g, cls=cls, si=si: h.matmul(ps_s[si][:, :], lhsT=ident[:], rhs=biasT[:, cls, g * 512:(g + 1) * 512],
                                                                  start=False, stop=True), r=["biasT", "p1id"], w=[f"ps_s{si}"])
                S.op("act", lambda h, si=si, pi=pi: h.activation(out=pT[pi][:], in_=ps_s[si][:, :], func=AF.Exp), r=[f"ps_s{si}"], w=[f"pT{pi}"])
            for r in range(4):
                for j, kt in enumerate(kts):
                    pi = (g * 2 + kt % 2)
                    S.op("pe", lambda h, g=g, r=r, kt=kt, pi=pi, j=j: h.matmul(ps_o[:, r, 0:65], lhsT=pT[pi][:, r * 128:(r + 1) * 128],
                                                                             rhs=vau[:, kt, g, :], start=(j == 0), stop=(j == len(kts) - 1)),
                         r=[f"pT{pi}", f"vau{kt}", "vau_ones"], w=["ps_o"])
            S.op("dve", lambda h, g=g, b=b: h.tensor_tensor(out=den[b][:, g * 4:(g + 1) * 4], in0=ps_o[:, :, 64], in1=esink[:, g * 4:(g + 1) * 4], op=ALU.add),
                 r=["ps_o", "esink"], w=[f"den{b}"])
            S.op("dve", lambda h, g=g, b=b: h.reciprocal(out=den[b][:, g * 4:(g + 1) * 4], in_=den[b][:, g * 4:(g + 1) * 4]), r=[f"den{b}"], w=[f"den{b}"])
            for r in range(4):
                col = (g * 4 + r) * 64
                S.op("dve", lambda h, g=g, r=r, b=b, col=col: h.scalar_tensor_tensor(out=yz[b][:, col:col + 64], in0=ps_o[:, r, 0:64],
                                                                                   scalar=den[b][:, g * 4 + r:g * 4 + r + 1], in1=zs[b][:, col:col + 64],
                                                                                   op0=ALU.mult, op1=ALU.mult),
                     r=["ps_o", f"den{b}", f"zs{b}"], w=[f"yz{b}"])
        for c in range(4):
            S.op("pe", lambda h, c=c, b=b: h.transpose(out=ps_tr[:, c * 128:(c + 1) * 128], in_=yz[b][:, c * 128:(c + 1) * 128], identity=ident[:]),
                 r=[f"yz{b}", "p1id"], w=["p1pstr"])
        S.op("act", lambda h, b=b: h.copy(out=yzT[b][:], in_=ps_tr[:, 0:512].rearrange("p (c n) -> p c n", c=4)), r=["p1pstr"], w=[f"yzT{b}"])
        for hf in range(2):
            for c in range(4):
                S.op("pe", lambda h, hf=hf, c=c, b=b: h.matmul(ps_y[hf][:, :], lhsT=yzT[b][:, c, :], rhs=wo[:, c, hf * 512:(hf + 1) * 512],
                                                              start=(c == 0), stop=(c == 3)), r=[f"yzT{b}", "wo"], w=[f"ps_y{hf}"])
            if hf == 0:
                S.op("act", lambda h, b=b: h.copy(out=pt[b][:, 0:512], in_=ps_y[0][:, :]), r=["ps_y0"], w=[f"pt{b}"])
            else:
                S.op("dve", lambda h, b=b: h.tensor_copy(out=pt[b][:, 512:1024], in_=ps_y[1][:, :]), r=["ps_y1"], w=[f"pt{b}"])
        S.op("sp", lambda h, b=b, tq=tq: h.dma_start(out=p_out[tq, :], in_=pt[b][:]), r=[f"pt{b}"], dma=True)
    return nc, es, S


def t5_bucket_np(d):
    import math
    d = np.maximum(d, 0)
    df = np.maximum(d, 1).astype(np.float32)
    large = 16 + (np.log(df / 16) / math.log(128 / 16) * 16).astype(np.int32)
    large = np.minimum(large, 31)
    return np.where(d < 16, d, large)


def prep_A(z, half):
    d = {}
    w_in = z['a_w_in'][0]
    d['wq'] = np.ascontiguousarray(w_in[:, half * 512:(half + 1) * 512])
    d['wk'] = np.ascontiguousarray(w_in[:, 1024 + half * 128:1024 + (half + 1) * 128])
    d['wv'] = np.ascontiguousarray(w_in[:, 1280 + half * 128:1280 + (half + 1) * 128])
    d['wz'] = np.ascontiguousarray(w_in[:, 1536 + half * 512:1536 + (half + 1) * 512])
    d['wo'] = np.ascontiguousarray(z['a_w_out'][0][half * 512:(half + 1) * 512, :])
    d['sinks'] = np.ascontiguousarray(z['a_sinks'][0][half * 8:(half + 1) * 8][None, :])
    table = z['t5_table']
    tk = np.arange(128)[:, None]
    tq = np.arange(128)[None, :]
    bias = np.zeros((128, 2, 2, 4, 128), np.float32)
    for cls in range(2):
        dist = tq - tk + 128 * cls
        valid = (dist >= 0) & (dist < 128)
        bk = t5_bucket_np(dist)
        for g in range(2):
            for r in range(4):
                hh = half * 8 + g * 4 + r
                bias[:, cls, g, r, :] = np.where(valid, table[bk, hh], NEGM)
    d['biasT'] = bias.reshape(128, 2, 1024)
    d['g'] = z['norm_g'][0:1].copy()
    d['ident'] = np.eye(128, dtype=np.float32)
    return d


KAP = 0.6065306597126334
GN_EPS = 64e-5


class _Stop(Exception):
    pass


def build_B(nsrc=1, MC=256, debug=False, stage=99):
    try:
        return _build_B(nsrc, MC, debug, stage)
    except _Stop as e:
        return e.args[0]


def _build_B(nsrc=1, MC=256, debug=False, stage=99):
    T = CFG.T
    NJ = MC // 64
    NMC = T // MC
    nc = get_nc()
    es = ExitStack()
    S = get_sched(nc, es)
    srcs = [dram_in(nc, f"xin{k}", [T, D]) for k in range(nsrc)]
    xs_out = dram_out(nc, "xs", [T, D]) if nsrc > 1 else None
    g_row = dram_in(nc, "g", [1, D])
    ident_d = dram_in(nc, "ident", [128, 128])
    w4_d = dram_in(nc, "w4", [4, D, 512])
    lw_d = dram_in(nc, "lw", [2, D, 64])
    l2_d = dram_in(nc, "l2", [2, 64, 512])
    wo_d = dram_in(nc, "wo", [512, D])
    mu_d = dram_in(nc, "muT", [128, 6, 8])
    vec_d = dram_in(nc, "vecs", [64, 8, 8])
    lnw_d = dram_in(nc, "lnw", [1, 512])
    lnb_d = dram_in(nc, "lnb", [1, 512])
    mg_d = dram_in(nc, "maskG", [128, 128])
    mnt_d = dram_in(nc, "maskNT", [64, 64])
    rm_d = dram_in(nc, "resetm", [64, MC])
    p_out = dram_out(nc, "p", [T, D])

    ps_tr = mk(nc, es, "ps_tr", [128, 1024], BF16, psum=True)
    ps_proj = mk(nc, es, "ps_proj", [128, 512], F32, psum=True)
    ps_tok = mk(nc, es, "ps_tok", [128, 512], F32, psum=True)
    ps_bv = mk(nc, es, "ps_bv", [128, 512], F32, psum=True)
    ps_g = mk(nc, es, "ps_g", [128, 512], F32, psum=True)
    ps_n = mk(nc, es, "ps_n", [128, 512], F32, psum=True)
    ps_rec = mk(nc, es, "ps_rec", [128, 512], F32, psum=True)
    ps_y = mk(nc, es, "ps_y", [128, 512], F32, psum=True)

    g_bc = mk(nc, es, "g_bc", [128, D], F32)
    identf = mk(nc, es, "identf", [128, 128], F32)
    ident = mk(nc, es, "identb", [128, 128], BF16)
    S.op("sp", lambda h: h.dma_start(out=g_bc[:], in_=g_row.partition_broadcast(128)), w=["g"], dma=True)
    S.op("sp", lambda h: h.dma_start(out=identf[:], in_=ident_d), w=["identf"], dma=True)
    S.op("dve", lambda h: h.tensor_copy(out=ident[:], in_=identf[:]), r=["identf"], w=["ident"])
    W4 = mk(nc, es, "W4", [128, 4, 8, 512], BF16)
    W4m = mk(nc, es, "W4m", [128, 4, 8, 512], BF16)
    LW = mk(nc, es, "LW", [128, 2, 8, 64], BF16)
    LWm = mk(nc, es, "LWm", [128, 2, 8, 64], BF16)
    L2 = mk(nc, es, "L2", [64, 2, 512], BF16)
    wo = mk(nc, es, "wo_s", [128, 4, D], BF16)
    muT = mk(nc, es, "muT_s", [128, 6, 8], F32)
    vec = mk(nc, es, "vec_s", [64, 8, 8], F32)
    lnw = mk(nc, es, "lnw_s", [64, 512], F32)
    lnb = mk(nc, es, "lnb_s", [64, 512], F32)
    maskG = mk(nc, es, "maskG_s", [128, 128], F32)
    maskNT = mk(nc, es, "maskNT_s", [64, 64], F32)
    resetm = mk(nc, es, "resetm_s", [64, MC], F32)
    ones64 = mk(nc, es, "ones64", [64, 64], F32)
    S.op("pool", lambda h: h.dma_start(out=W4[:], in_=w4_d.rearrange("s (c p) n -> p s c n", p=128)), w=["W4"], dma=True)
    S.op("pool", lambda h: h.dma_start(out=LW[:], in_=lw_d.rearrange("s (c p) n -> p s c n", p=128)), w=["LW"], dma=True)
    S.op("pool", lambda h: h.dma_start(out=L2[:], in_=l2_d.rearrange("s k n -> k s n")), w=["L2"], dma=True)
    S.op("pool", lambda h: h.dma_start(out=wo[:], in_=wo_d.rearrange("(c p) n -> p c n", p=128)), w=["wo"], dma=True)
    S.op("sp", lambda h: h.dma_start(out=muT[:], in_=mu_d), w=["muT"], dma=True)
    S.op("sp", lambda h: h.dma_start(out=vec[:], in_=vec_d), w=["vec"], dma=True)
    S.op("sp", lambda h: h.dma_start(out=lnw[:], in_=lnw_d.partition_broadcast(64)), w=["lnw"], dma=True)
    S.op("sp", lambda h: h.dma_start(out=lnb[:], in_=lnb_d.partition_broadcast(64)), w=["lnb"], dma=True)
    S.op("sp", lambda h: h.dma_start(out=maskG[:], in_=mg_d), w=["maskG"], dma=True)
    S.op("sp", lambda h: h.dma_start(out=maskNT[:], in_=mnt_d), w=["maskNT"], dma=True)
    S.op("sp", lambda h: h.dma_start(out=resetm[:], in_=rm_d), w=["resetm"], dma=True)
    S.op("dve", lambda h: h.memset(ones64[:], 1.0), w=["ones64"])
    for s in range(4):
        for dc in range(8):
            S.op("pool" if dc % 2 else "dve", lambda h, s=s, dc=dc: h.tensor_scalar(out=W4m[:, s, dc, :], in0=W4[:, s, dc, :], scalar1=muT[:, s, dc:dc + 1], scalar2=None, op0=ALU.mult),
                 r=["W4", "muT"], w=["W4m"])
    for s in range(2):
        for dc in range(8):
            S.op("dve", lambda h, s=s, dc=dc: h.tensor_scalar(out=LWm[:, s, dc, :], in0=LW[:, s, dc, :], scalar1=muT[:, 4 + s, dc:dc + 1], scalar2=None, op0=ALU.mult),
                 r=["LW", "muT"], w=["LWm"])

    XW = 64 + MC
    xnT = mk(nc, es, "xnT", [128, 8, XW], BF16)
    xxT = mk(nc, es, "xxT", [128, 8, XW], BF16)
    S.op("dve", lambda h: h.memset(xnT[:, :, 0:64], 0.0), w=["xnT"])
    NB = 2
    xt = [mk(nc, es, f"xt{b}", [128, D], F32) for b in range(NB)]
    sq = mk(nc, es, "sq", [128, D], BF16)
    xnb = [mk(nc, es, f"xnb{b}", [128, D], BF16) for b in range(NB)]
    st = [mk(nc, es, f"st{b}", [128, 4], F32) for b in range(NB)]
    h1T = mk(nc, es, "h1T", [64, 2, MC], BF16)
    vwin = mk(nc, es, "vwin", [64, NJ, 512], F32)
    uT = mk(nc, es, "uT", [64, NJ, 512], F32)
    zs = mk(nc, es, "zs", [64, NJ, 512], F32)
    y_all = mk(nc, es, "y_all", [64, NJ, 512], F32)
    bv_all = mk(nc, es, "bv_all", [64, NJ, 512], F32)

    def ft(nm):
        return mk(nc, es, nm, [64, MC], F32)
    r_f = ft("r_f"); k_f = ft("k_f"); sig = ft("sig"); alp = ft("alp"); kk = ft("kk"); t1 = ft("t1"); t2 = ft("t2")
    cs = ft("cs"); kmod = ft("kmod"); bal = ft("bal")
    cLs = mk(nc, es, "cLs", [64, NJ], F32)
    cLd = mk(nc, es, "cLd", [64, NJ], F32)
    G2 = 2
    AR = [mk(nc, es, f"AR{i}", [64, NJ, 128], F32) for i in range(G2)]
    BK = [mk(nc, es, f"BK{i}", [64, NJ, 128], F32) for i in range(G2)]
    BKe = [mk(nc, es, f"BKe{i}", [64, NJ, 128], F32) for i in range(G2)]
    Gm = [mk(nc, es, f"Gm{i}", [64, NJ, 256], F32) for i in range(G2)]
    Tm = [mk(nc, es, f"Tm{i}", [64, NJ, 64], F32) for i in range(G2)]
    BKeT = [mk(nc, es, f"BKeT{i}", [64, NJ, 128], F32) for i in range(G2)]
    dcL = [mk(nc, es, f"dcL{i}", [64, NJ, 64], F32) for i in range(G2)]
    rkrp = [mk(nc, es, f"rkrp{i}", [64, NJ, 64], F32) for i in range(G2)]
    Dg = [mk(nc, es, f"Dg{i}", [64, NJ, 64], F32) for i in range(G2)]
    bon = [mk(nc, es, f"bon{i}", [64, NJ], F32) for i in range(G2)]
    Nk = [mk(nc, es, f"Nk{i}", [64, 64], F32) for i in range(2)]
    NkT = [mk(nc, es, f"NkT{i}", [64, 64], F32) for i in range(2)]
    Pm = [mk(nc, es, f"Pm{i}", [64, 64], F32) for i in range(2)]
    ST = [[mk(nc, es, f"ST{h}_{i}", [64, 64], F32) for i in range(2)] for h in range(8)]
    WT = [mk(nc, es, f"WT{i}", [64, 64], F32) for i in range(2)]
    for h in range(8):
        S.op("dve", lambda hh, h=h: hh.memset(ST[h][0][:], 0.0), w=[f"ST{h}_0"])
    yn = mk(nc, es, "yn", [64, 512], F32)
    gst = mk(nc, es, "gst", [64, 4, 8], F32)
    yz = mk(nc, es, "yz", [64, 512], BF16)
    yzT = mk(nc, es, "yzT", [128, 4, 64], BF16)
    pt = mk(nc, es, "pt", [64, D], F32)

    def c3(t_):
        return t_[:].rearrange("p (c j) -> p c j", j=64)

    for mc in range(NMC):
        T0 = mc * MC
        if mc > 0:
            S.op("pool", lambda h: h.tensor_copy(out=xnT[:, :, 0:64], in_=xnT[:, :, MC:MC + 64]), r=["xnT"], w=["xnT"])
        for tl in range(MC // 128):
            t = (T0 // 128) + tl
            b = t % NB
            rows = slice(t * 128, (t + 1) * 128)
            for k, src in enumerate(srcs):
                if k == 0:
                    S.op("pool", lambda h, src=src, b=b, t=t: h.dma_start(out=xt[b][:], in_=src_rows(src, t)), w=[f"xt{b}"], dma=True)
                else:
                    S.op("pool", lambda h, src=src, b=b, t=t: h.dma_start(out=xt[b][:], in_=src_rows(src, t), accum_op=ALU.add),
                         r=[f"xt{b}"], w=[f"xt{b}"], dma=True)
            if xs_out is not None:
                S.op("sp", lambda h, b=b, rows=rows: h.dma_start(out=xs_out[rows, :], in_=xt[b][:]), r=[f"xt{b}"], dma=True)
            S.op("act", lambda h, b=b: h.activation(out=sq[:], in_=xt[b][:], func=AF.Square), r=[f"xt{b}"], w=["sq"])
            S.op("dve", lambda h, b=b: h.tensor_reduce(out=st[b][:, 0:1], in_=sq[:], axis=AX.X, op=ALU.add), r=["sq"], w=[f"st{b}"])
            S.op("act", lambda h, b=b: h.activation(out=st[b][:, 1:2], in_=st[b][:, 0:1], func=AF.Sqrt, scale=1.0 / D, bias=EPS), r=[f"st{b}"], w=[f"st{b}"])
            S.op("dve", lambda h, b=b: h.reciprocal(out=st[b][:, 2:3], in_=st[b][:, 1:2]), r=[f"st{b}"], w=[f"st{b}r"])
            S.op("dve", lambda h, b=b: h.scalar_tensor_tensor(out=xnb[b][:], in0=xt[b][:], scalar=st[b][:, 2:3], in1=g_bc[:], op0=ALU.mult, op1=ALU.mult),
                 r=[f"xt{b}", f"st{b}r", "g"], w=[f"xnb{b}"])
            for dc in range(8):
                S.op("pe", lambda h, dc=dc, b=b: h.transpose(out=ps_tr[:, dc * 128:(dc + 1) * 128], in_=xnb[b][:, dc * 128:(dc + 1) * 128], identity=ident[:]),
                     r=[f"xnb{b}", "ident"], w=["ps_tr"])
            S.op("act", lambda h, tl=tl: h.copy(out=xnT[:, :, 64 + tl * 128:64 + (tl + 1) * 128], in_=ps_tr[:].rearrange("p (c n) -> p c n", c=8)),
                 r=["ps_tr"], w=["xnT"])
        S.op("pool", lambda h: h.tensor_tensor(out=xxT[:, :, 1:XW], in0=xnT[:, :, 0:XW - 1], in1=xnT[:, :, 1:XW], op=ALU.subtract), r=["xnT"], w=["xxT"])
        tokc = slice(64, 64 + MC)
        if stage == 0:
            raise _Stop((nc, es, S))

        def proj_fm(ps_ap, Wt, Wm, sidx, cols, M):
            n = 0
            for (Wx, X, xb) in ((Wt, xnT, "xnT"), (Wm, xxT, "xxT")):
                for dc in range(8):
                    S.op("pe", lambda h, Wx=Wx, X=X, dc=dc, n=n: h.matmul(ps_ap, lhsT=Wx[:, sidx, dc, cols], rhs=X[:, dc, tokc], start=(n == 0), stop=(n == 15)),
                         r=["W4", "W4m", "LW", "LWm", xb], w=["ps_proj"])
                    n += 1
        for s in range(2):
            proj_fm(ps_proj[0:64, 0:MC], LW, LWm, s, slice(0, 64), 64)
            S.op("act", lambda h, s=s: h.activation(out=h1T[:, s, :], in_=ps_proj[0:64, 0:MC], func=(AF.Tanh if s == 0 else AF.Copy)), r=["ps_proj"], w=["h1T"])
        for j in range(NJ):
            n = 0
            for (Wi, X, xb) in ((W4, xnT, "xnT"), (W4m, xxT, "xxT")):
                for dc in range(8):
                    S.op("pe", lambda h, Wi=Wi, X=X, dc=dc, n=n, j=j: h.matmul(ps_tok[0:64, :], lhsT=X[:, dc, 64 + j * 64:128 + j * 64], rhs=Wi[:, 2, dc, :], start=(n == 0), stop=(n == 15)),
                         r=["W4", "W4m", xb], w=["ps_tok"])
                    n += 1
            S.op("act", lambda h, j=j: h.copy(out=vwin[:, j, :], in_=ps_tok[0:64, :]), r=["ps_tok"], w=[f"vwin{j}"])
            n = 0
            for (Wi, X, xb) in ((W4, xnT, "xnT"), (W4m, xxT, "xxT")):
                for dc in range(8):
                    S.op("pe", lambda h, Wi=Wi, X=X, dc=dc, n=n, j=j: h.matmul(ps_tok[0:64, :], lhsT=X[:, dc, 64 + j * 64:128 + j * 64], rhs=Wi[:, 3, dc, :], start=(n == 0), stop=(n == 15)),
                         r=["W4", "W4m", xb], w=["ps_tok"])
                    n += 1
            S.op("act", lambda h, j=j: h.activation(out=zs[:, j, :], in_=ps_tok[0:64, :], func=AF.Silu), r=["ps_tok"], w=["zs"])

        if stage == 1:
            raise _Stop((nc, es, S))
        for hd in range(8):
            gi = hd % G2
            hc = slice(hd * 64, (hd + 1) * 64)
            vp = lambda c: vec[:, hd, c:c + 1]
            proj_fm(ps_proj[0:64, 0:MC], W4, W4m, 0, hc, 64)
            S.op("act", lambda h: h.copy(out=r_f[:], in_=ps_proj[0:64, 0:MC]), r=["ps_proj"], w=["r_f"])
            proj_fm(ps_proj[0:64, 0:MC], W4, W4m, 1, hc, 64)
            S.op("act", lambda h: h.copy(out=k_f[:], in_=ps_proj[0:64, 0:MC]), r=["ps_proj"], w=["k_f"])
            S.op("pe", lambda h, hc=hc: h.matmul(ps_proj[0:64, 0:MC], lhsT=L2[:, 0, hc], rhs=h1T[:, 0, :], start=True, stop=True), r=["L2", "h1T"], w=["ps_proj"])
            S.op("act", lambda h, hd=hd: h.activation(out=sig[:], in_=ps_proj[0:64, 0:MC], func=AF.Sigmoid, bias=vec[:, hd, 0:1]), r=["ps_proj", "vec"], w=["sig"])
            S.op("pe", lambda h, hc=hc: h.matmul(ps_proj[0:64, 0:MC], lhsT=L2[:, 1, hc], rhs=h1T[:, 1, :], start=True, stop=True), r=["L2", "h1T"], w=["ps_proj"])
            S.op("act", lambda h, hd=hd: h.activation(out=alp[:], in_=ps_proj[0:64, 0:MC], func=AF.Sigmoid, bias=vec[:, hd, 1:2]), r=["ps_proj", "vec"], w=["alp"])
            S.op("dve", lambda h, hd=hd: h.tensor_scalar(out=kk[:], in0=k_f[:], scalar1=vec[:, hd, 2:3], scalar2=None, op0=ALU.mult), r=["k_f", "vec"], w=["kk"])
            S.op("pool", lambda h: h.tensor_tensor(out=t1[:], in0=kk[:], in1=kk[:], op=ALU.mult), r=["kk"], w=["t1"])
            S.op("pe", lambda h: h.matmul(ps_proj[0:64, 0:MC], lhsT=ones64[:], rhs=t1[:], start=True, stop=True), r=["ones64", "t1"], w=["ps_proj"])
            S.op("act", lambda h: h.activation(out=t2[:], in_=ps_proj[0:64, 0:MC], func=AF.Sqrt), r=["ps_proj"], w=["t2"])
            S.op("dve", lambda h: h.tensor_scalar(out=t2[:], in0=t2[:], scalar1=1e-12, scalar2=None, op0=ALU.max), r=["t2"], w=["t2"])
            S.op("dve", lambda h: h.reciprocal(out=t2[:], in_=t2[:]), r=["t2"], w=["t2"])
            S.op("dve", lambda h: h.tensor_tensor(out=kk[:], in0=kk[:], in1=t2[:], op=ALU.mult), r=["kk", "t2"], w=["kk"])
            S.op("dve", lambda h, hd=hd: h.tensor_scalar(out=t1[:], in0=alp[:], scalar1=1.0, scalar2=vec[:, hd, 3:4], op0=ALU.subtract, op1=ALU.mult), r=["alp", "vec"], w=["t1"])
            S.op("dve", lambda h: h.scalar_tensor_tensor(out=kmod[:], in0=t1[:], scalar=1.0, in1=k_f[:], op0=ALU.add, op1=ALU.mult), r=["t1", "k_f"], w=["kmod"])
            S.op("pool", lambda h: h.tensor_tensor(out=bal[:], in0=kk[:], in1=alp[:], op=ALU.mult), r=["kk", "alp"], w=["bal"])
            S.op("dve", lambda h: h.tensor_tensor_scan(out=cs[:], data0=resetm[:], data1=sig[:], initial=0.0, op0=ALU.mult, op1=ALU.add), r=["resetm", "sig"], w=["cs"])
            S.op("dve", lambda h: h.tensor_copy(out=cLs[:], in_=cs[:, 63::64]), r=["cs"], w=["cLs"])
            S.op("act", lambda h: h.activation(out=cLd[:], in_=cLs[:], func=AF.Exp, scale=-KAP), r=["cLs"], w=["cLd"])
            S.op("act", lambda h: h.activation(out=t1[:], in_=cs[:], func=AF.Exp, scale=-KAP), r=["cs"], w=["t1"])
            S.op("dve", lambda h, gi=gi: h.tensor_tensor(out=AR[gi][:, :, 64:128], in0=c3(r_f), in1=c3(t1), op=ALU.mult), r=["r_f", "t1"], w=[f"AR{gi}"])
            S.op("act", lambda h: h.activation(out=t2[:], in_=cs[:], func=AF.Exp, scale=KAP), r=["cs"], w=["t2"])
            S.op("dve", lambda h, gi=gi: h.tensor_tensor(out=BK[gi][:, :, 0:64], in0=c3(bal), in1=c3(t2), op=ALU.mult), r=["bal", "t2"], w=[f"BK{gi}"])
            S.op("pool", lambda h, gi=gi: h.tensor_tensor(out=BK[gi][:, :, 64:128], in0=c3(kmod), in1=c3(t2), op=ALU.mult), r=["kmod", "t2"], w=[f"BK{gi}"])
            S.op("pool", lambda h: h.tensor_tensor(out=t1[:], in0=cs[:], in1=sig[:], op=ALU.subtract), r=["cs", "sig"], w=["t1"])
            S.op("act", lambda h: h.activation(out=t1[:], in_=t1[:], func=AF.Exp, scale=-KAP), r=["t1"], w=["t1"])
            S.op("dve", lambda h, gi=gi: h.scalar_tensor_tensor(out=AR[gi][:, :, 0:64], in0=c3(kk), scalar=-1.0, in1=c3(t1), op0=ALU.mult, op1=ALU.mult),
                 r=["kk", "t1"], w=[f"AR{gi}"])
            S.op("dve", lambda h: h.tensor_tensor(out=c3(t2), in0=c3(cs), in1=cLs[:].unsqueeze(2).broadcast_to([64, NJ, 64]), op=ALU.subtract), r=["cs", "cLs"], w=["t2"])
            S.op("act", lambda h: h.activation(out=t2[:], in_=t2[:], func=AF.Exp, scale=KAP), r=["t2"], w=["t2"])
            S.op("dve", lambda h, gi=gi: h.tensor_tensor(out=BKe[gi][:, :, 0:64], in0=c3(bal), in1=c3(t2), op=ALU.mult), r=["bal", "t2"], w=[f"BKe{gi}"])
            S.op("pool", lambda h, gi=gi: h.tensor_tensor(out=BKe[gi][:, :, 64:128], in0=c3(kmod), in1=c3(t2), op=ALU.mult), r=["kmod", "t2"], w=[f"BKe{gi}"])
            S.op("dve", lambda h, gi=gi, hd=hd: h.scalar_tensor_tensor(out=rkrp[gi][:, :, :], in0=c3(r_f), scalar=vec[:, hd, 4:5], in1=c3(kmod), op0=ALU.mult, op1=ALU.mult),
                 r=["r_f", "kmod", "vec"], w=[f"rkrp{gi}"])
            if stage == 2:
                raise _Stop((nc, es, S))
            for j in range(NJ):
                S.op("pe", lambda h, gi=gi, j=j: h.matmul(ps_g[0:64, 0:128], lhsT=BK[gi][:, j, 0:64], rhs=AR[gi][:, j, :], start=True, stop=True), r=[f"BK{gi}", f"AR{gi}"], w=["ps_g"])
                S.op("pe", lambda h, gi=gi, j=j: h.matmul(ps_g[0:64, 128:256], lhsT=BK[gi][:, j, 64:128], rhs=AR[gi][:, j, :], start=True, stop=True), r=[f"BK{gi}", f"AR{gi}"], w=["ps_g"])
                S.op("pe", lambda h, gi=gi, j=j: h.matmul(ps_g[0:64, 256:320], lhsT=AR[gi][:, j, 0:64], rhs=BK[gi][:, j, 0:64], start=True, stop=True), r=[f"BK{gi}", f"AR{gi}"], w=["ps_g"])
                S.op("dve", lambda h, gi=gi, j=j: h.tensor_tensor(out=Gm[gi][:, j, 0:128], in0=ps_g[0:64, 0:128], in1=maskG[0:64, :], op=ALU.mult), r=["ps_g", "maskG"], w=[f"Gm{gi}"])
                S.op("dve", lambda h, gi=gi, j=j: h.tensor_tensor(out=Gm[gi][:, j, 128:256], in0=ps_g[0:64, 128:256], in1=maskG[0:64, :], op=ALU.mult), r=["ps_g", "maskG"], w=[f"Gm{gi}"])
                S.op("dve", lambda h: h.tensor_tensor(out=NkT[0][:], in0=ps_g[0:64, 256:320], in1=maskNT[:], op=ALU.mult), r=["ps_g", "maskNT"], w=["NkT0"])
                S.op("pool", lambda h, gi=gi, j=j: h.tensor_copy(out=Nk[0][:], in_=Gm[gi][:, j, 0:64]), r=[f"Gm{gi}"], w=["Nk0"])
                S.op("pool", lambda h, gi=gi, j=j: h.tensor_tensor(out=Pm[0][:], in0=Gm[gi][:, j, 0:64], in1=identf[0:64, 0:64], op=ALU.add), r=[f"Gm{gi}", "identf"], w=["Pm0"])
                cur = 0
                for sidx in range(5):
                    nx = 1 - cur
                    last = (sidx == 4)
                    S.op("pe", lambda h, cur=cur: h.matmul(ps_n[0:64, 0:64], lhsT=Nk[cur][:], rhs=NkT[cur][:], start=True, stop=True), r=[f"Nk{cur}", f"NkT{cur}"], w=["ps_n"])
                    S.op("act", lambda h, nx=nx: h.copy(out=NkT[nx][:], in_=ps_n[0:64, 0:64]), r=["ps_n"], w=[f"NkT{nx}"])
                    if not last:
                        S.op("pe", lambda h, cur=cur: h.matmul(ps_n[0:64, 64:128], lhsT=NkT[cur][:], rhs=Nk[cur][:], start=True, stop=True), r=[f"Nk{cur}", f"NkT{cur}"], w=["ps_n"])
                        S.op("act", lambda h, nx=nx: h.copy(out=Nk[nx][:], in_=ps_n[0:64, 64:128]), r=["ps_n"], w=[f"Nk{nx}"])
                    S.op("pe", lambda h, cur=cur, nx=nx: h.matmul(ps_n[0:64, 128:192], lhsT=NkT[nx][:], rhs=Pm[cur][:], start=True, stop=True), r=[f"NkT{nx}", f"Pm{cur}"], w=["ps_n"])
                    if last:
                        S.op("dve", lambda h, cur=cur, gi=gi, j=j: h.tensor_tensor(out=Tm[gi][:, j, :], in0=ps_n[0:64, 128:192], in1=Pm[cur][:], op=ALU.add), r=["ps_n", f"Pm{cur}"], w=[f"Tm{gi}"])
                    else:
                        S.op("dve", lambda h, cur=cur, nx=nx: h.tensor_tensor(out=Pm[nx][:], in0=ps_n[0:64, 128:192], in1=Pm[cur][:], op=ALU.add), r=["ps_n", f"Pm{cur}"], w=[f"Pm{nx}"])
                    cur = nx
                for q in range(2):
                    S.op("pe", lambda h, gi=gi, j=j, q=q: h.transpose(out=ps_g[0:64, 320 + q * 64:384 + q * 64], in_=BKe[gi][:, j, q * 64:(q + 1) * 64], identity=identf[0:64, 0:64]), r=[f"BKe{gi}", "identf"], w=["ps_g"])
                S.op("act", lambda h, gi=gi, j=j: h.copy(out=BKeT[gi][:, j, :], in_=ps_g[0:64, 320:448]), r=["ps_g"], w=[f"BKeT{gi}"])
                S.op("pool", lambda h, gi=gi, j=j: h.tensor_scalar(out=dcL[gi][:, j, :], in0=identf[0:64, 0:64], scalar1=cLd[:, j:j + 1], scalar2=None, op0=ALU.mult), r=["identf", "cLd"], w=[f"dcL{gi}"])
                S.op("pe", lambda h, gi=gi, j=j: h.matmul(ps_g[0:64, 448 + j:449 + j], lhsT=rkrp[gi][:, j, :], rhs=ones64[:, 0:1], start=True, stop=True), r=[f"rkrp{gi}", "ones64"], w=["ps_g"])
                S.op("act", lambda h, gi=gi, j=j: h.copy(out=bon[gi][:, j:j + 1], in_=ps_g[0:64, 448 + j:449 + j]), r=["ps_g"], w=[f"bon{gi}"])
            for j in range(NJ):
                S.op("dve", lambda h, gi=gi, j=j: h.tensor_scalar(out=Dg[gi][:, j, :], in0=identf[0:64, 0:64], scalar1=bon[gi][:, j:j + 1], scalar2=None, op0=ALU.mult),
                     r=["identf", f"bon{gi}"], w=[f"Dg{gi}"])
            if stage == 3:
                raise _Stop((nc, es, S))
            for j in range(NJ):
                gj = mc * NJ + j
                s_in = ST[hd][gj % 2]
                s_out = ST[hd][(gj + 1) % 2]
                sin_n = f"ST{hd}_{gj % 2}"
                sout_n = f"ST{hd}_{(gj + 1) % 2}"
                wi = gj % 2
                VT = vwin[:, j, hc]
                UT = uT[:, j, hc]
                vn = f"vwin{j}"
                un = f"uT{j}_{hd}"
                S.op("pe", lambda h, gi=gi, j=j, VT=VT, hc=hc: h.matmul(ps_bv[0:64, hc], lhsT=Dg[gi][:, j, :], rhs=VT, start=True, stop=True), r=[f"Dg{gi}", vn], w=["ps_bv"])
                S.op("act", lambda h, j=j, hc=hc: h.copy(out=bv_all[:, j, hc], in_=ps_bv[0:64, hc]), r=["ps_bv"], w=["bv_all"])
                S.op("pe", lambda h, gi=gi, j=j, s_in=s_in: h.matmul(ps_rec[0:64, 0:64], lhsT=AR[gi][:, j, 0:64], rhs=s_in[:], start=True, stop=False), r=[f"AR{gi}", sin_n], w=["ps_rec"])
                S.op("pe", lambda h, gi=gi, j=j, VT=VT: h.matmul(ps_rec[0:64, 0:64], lhsT=Gm[gi][:, j, 128:192], rhs=VT, start=False, stop=True), r=[f"Gm{gi}", vn], w=["ps_rec"])
                S.op("act", lambda h, wi=wi: h.copy(out=WT[wi][:], in_=ps_rec[0:64, 0:64]), r=["ps_rec"], w=[f"WT{wi}"])
                S.op("pe", lambda h, gi=gi, j=j, wi=wi: h.matmul(ps_rec[0:64, 64:128], lhsT=Tm[gi][:, j, :], rhs=WT[wi][:], start=True, stop=True), r=[f"Tm{gi}", f"WT{wi}"], w=["ps_rec"])
                S.op("act", lambda h, UT=UT: h.copy(out=UT, in_=ps_rec[0:64, 64:128]), r=["ps_rec"], w=[un])
                S.op("pe", lambda h, gi=gi, j=j, s_in=s_in, hc=hc: h.matmul(ps_y[0:64, hc], lhsT=AR[gi][:, j, 64:128], rhs=s_in[:], start=True, stop=False), r=[f"AR{gi}", sin_n], w=["ps_y"])
                S.op("pe", lambda h, gi=gi, j=j, hc=hc, UT=UT: h.matmul(ps_y[0:64, hc], lhsT=Gm[gi][:, j, 64:128], rhs=UT, start=False, stop=False), r=[f"Gm{gi}", un], w=["ps_y"])
                S.op("pe", lambda h, gi=gi, j=j, hc=hc, VT=VT: h.matmul(ps_y[0:64, hc], lhsT=Gm[gi][:, j, 192:256], rhs=VT, start=False, stop=True), r=[f"Gm{gi}", vn], w=["ps_y"])
                S.op("dve", lambda h, j=j, hc=hc: h.tensor_copy(out=y_all[:, j, hc], in_=ps_y[0:64, hc]), r=["ps_y"], w=["y_all"])
                S.op("pe", lambda h, gi=gi, j=j, s_in=s_in: h.matmul(ps_rec[0:64, 128:192], lhsT=dcL[gi][:, j, :], rhs=s_in[:], start=True, stop=False), r=[f"dcL{gi}", sin_n], w=["ps_rec"])
                S.op("pe", lambda h, gi=gi, j=j, UT=UT: h.matmul(ps_rec[0:64, 128:192], lhsT=BKeT[gi][:, j, 0:64], rhs=UT, start=False, stop=False), r=[f"BKeT{gi}", un], w=["ps_rec"])
                S.op("pe", lambda h, gi=gi, j=j, VT=VT: h.matmul(ps_rec[0:64, 128:192], lhsT=BKeT[gi][:, j, 64:128], rhs=VT, start=False, stop=True), r=[f"BKeT{gi}", vn], w=["ps_rec"])
                S.op("act", lambda h, s_out=s_out: h.copy(out=s_out[:], in_=ps_rec[0:64, 128:192]), r=["ps_rec"], w=[sout_n])
        if stage == 4:
            raise _Stop((nc, es, S))
        for j in range(NJ):
            y3 = y_all[:, j, :].rearrange("p (h v) -> p h v", h=8)
            S.op("dve", lambda h, y3=y3: h.tensor_reduce(out=gst[:, 0, :], in_=y3, axis=AX.X, op=ALU.add), r=["y_all"], w=["gst"])
            S.op("act", lambda h, j=j: h.activation(out=yn[:], in_=y_all[:, j, :], func=AF.Square), r=["y_all"], w=["yn"])
            S.op("dve", lambda h: h.tensor_reduce(out=gst[:, 1, :], in_=yn[:].rearrange("p (h v) -> p h v", h=8), axis=AX.X, op=ALU.add), r=["yn"], w=["gst"])
            S.op("dve", lambda h: h.tensor_scalar(out=gst[:, 0, :], in0=gst[:, 0, :], scalar1=1.0 / 64, scalar2=None, op0=ALU.mult), r=["gst"], w=["gst"])
            S.op("dve", lambda h: h.tensor_tensor(out=gst[:, 2, :], in0=gst[:, 0, :], in1=gst[:, 0, :], op=ALU.mult), r=["gst"], w=["gst"])
            S.op("dve", lambda h: h.scalar_tensor_tensor(out=gst[:, 1, :], in0=gst[:, 1, :], scalar=1.0 / 64, in1=gst[:, 2, :], op0=ALU.mult, op1=ALU.subtract), r=["gst"], w=["gst"])
            S.op("act", lambda h: h.activation(out=gst[:, 1, :], in_=gst[:, 1, :], func=AF.Sqrt, bias=GN_EPS), r=["gst"], w=["gst"])
            S.op("dve", lambda h: h.reciprocal(out=gst[:, 1, :], in_=gst[:, 1, :]), r=["gst"], w=["gst"])
            for hd in range(8):
                hc = slice(hd * 64, (hd + 1) * 64)
                S.op("dve", lambda h, j=j, hd=hd, hc=hc: h.tensor_scalar(out=yn[:, hc], in0=y_all[:, j, hc], scalar1=gst[:, 0, hd:hd + 1], scalar2=gst[:, 1, hd:hd + 1],
                                                                      op0=ALU.subtract, op1=ALU.mult), r=["y_all", "gst"], w=["yn"])
            S.op("pool", lambda h: h.tensor_tensor(out=yn[:], in0=yn[:], in1=lnw[:], op=ALU.mult), r=["yn", "lnw"], w=["yn"])
            S.op("pool", lambda h: h.tensor_tensor(out=yn[:], in0=yn[:], in1=lnb[:], op=ALU.add), r=["yn", "lnb"], w=["yn"])
            S.op("pool", lambda h, j=j: h.tensor_tensor(out=yn[:], in0=yn[:], in1=bv_all[:, j, :], op=ALU.add), r=["yn", "bv_all"], w=["yn"])
            S.op("dve", lambda h, j=j: h.tensor_tensor(out=yz[:], in0=yn[:], in1=zs[:, j, :], op=ALU.mult), r=["yn", "zs"], w=["yz"])
            for c in range(4):
                S.op("pe", lambda h, c=c: h.transpose(out=ps_tr[:, c * 64:(c + 1) * 64], in_=yz[:, c * 128:(c + 1) * 128], identity=ident[0:64, 0:64]), r=["yz", "ident"], w=["ps_tr"])
            S.op("act", lambda h: h.copy(out=yzT[:], in_=ps_tr[:, 0:256].rearrange("p (c n) -> p c n", c=4)), r=["ps_tr"], w=["yzT"])
            for hf in range(2):
                for c in range(4):
                    S.op("pe", lambda h, hf=hf, c=c: h.matmul(ps_tok[0:64, :], lhsT=yzT[:, c, :], rhs=wo[:, c, hf * 512:(hf + 1) * 512], start=(c == 0), stop=(c == 3)), r=["yzT", "wo"], w=["ps_tok"])
                S.op("act", lambda h, hf=hf: h.copy(out=pt[:, hf * 512:(hf + 1) * 512], in_=ps_tok[0:64, :]), r=["ps_tok"], w=["pt"])
            rows = slice(T0 + j * 64, T0 + (j + 1) * 64)
            S.op("sp", lambda h, rows=rows: h.dma_start(out=p_out[rows, :], in_=pt[:]), r=["pt"], dma=True)
    return nc, es, S


def prep_B(z, half, MC=256):
    d = {}
    w_in = z['b_w_in'][0]
    own = slice(half * 512, (half + 1) * 512)
    d['w4'] = np.ascontiguousarray(np.stack([w_in[:, s * 1024:(s + 1) * 1024][:, own] for s in range(4)]))
    d['lw'] = np.ascontiguousarray(np.stack([z['b_w1'][0], z['b_a1'][0]]))
    d['l2'] = np.ascontiguousarray(np.stack([z['b_w2'][0][:, own], z['b_a2'][0][:, own]]))
    d['wo'] = np.ascontiguousarray(z['b_w_out'][0][own, :])
    mu = z['b_mu'][0]
    d['muT'] = np.ascontiguousarray(mu.reshape(6, 8, 128).transpose(2, 0, 1))
    vecs = np.zeros((64, 8, 8), np.float32)
    def fm(v):
        return v[own].reshape(8, 64).T
    vecs[:, :, 0] = fm(z['b_w0'][0]); vecs[:, :, 1] = fm(z['b_a0'][0]); vecs[:, :, 2] = fm(z['b_k_k'][0]); vecs[:, :, 3] = fm(z['b_k_a'][0])
    vecs[:, :, 4] = fm(z['b_r_k'][0].reshape(-1))
    d['vecs'] = vecs
    d['lnw'] = np.ascontiguousarray(z['b_lnx_w'][0][own][None, :])
    d['lnb'] = np.ascontiguousarray(z['b_lnx_b'][0][own][None, :])
    j = np.arange(64)[:, None]; i = np.arange(64)[None, :]
    strict = (j < i).astype(np.float32); incl = (j <= i).astype(np.float32)
    row = np.concatenate([strict, incl], 1)
    d['maskG'] = np.ascontiguousarray(np.concatenate([row, row], 0))
    d['maskNT'] = np.ascontiguousarray(strict.T)
    rm = np.ones((64, MC), np.float32); rm[:, ::64] = 0.0
    d['resetm'] = rm
    d['g'] = z['norm_g'][1:2].copy()
    d['ident'] = np.eye(128, dtype=np.float32)
    return d


NEGM = -30000.0


def build_C(nsrc=1, debug=False):
    T = CFG.T
    NT = T // 128
    NCMP = T // 16 - 1
    NKT = (NCMP + 127) // 128
    nc = get_nc()
    es = ExitStack()
    S = get_sched(nc, es)
    srcs = [dram_in(nc, f"xin{k}", [T, D]) for k in range(nsrc)]
    xs_out = dram_out(nc, "xs", [T, D]) if nsrc > 1 else None
    g_row = dram_in(nc, "g", [1, D])
    ident_d = dram_in(nc, "ident", [128, 128])
    wq_d = dram_in(nc, "wq", [D, 512])
    wkv_d = dram_in(nc, "wkv", [D, 6, 128])
    wg_d = dram_in(nc, "wg", [D, 24])
    wz_d = dram_in(nc, "wz", [D, 512])
    wo_d = dram_in(nc, "wo", [512, D])
    w1_d = dram_in(nc, "w1", [2, 64, 32, 128])
    w2_d = dram_in(nc, "w2", [2, 128, 64])
    pos_d = dram_in(nc, "posT", [2, 64, 32])
    bias_d = dram_in(nc, "biasT", [128, 4, 1024])
    F4_d = dram_in(nc, "F4", [512, 512])
    ka_d = dram_in(nc, "keepadd", [NT, 128, 128])
    E_d = dram_in(nc, "E", [64, NT, 128])
    ov_d = dram_in(nc, "ovl", [128, 2, 64])
    p_out = dram_out(nc, "p", [T, D])

    ps_tr = mk(nc, es, "ps_tr", [128, 1024], BF16, psum=True)
    ps_q = mk(nc, es, "ps_q", [128, 512], F32, psum=True)
    ps_z = mk(nc, es, "ps_z", [128, 512], F32, psum=True)
    ps_s = [mk(nc, es, f"ps_s{k}", [128, 512], F32, psum=True) for k in range(2)]
    po_c = mk(nc, es, "po_c", [128, 4, 128], F32, psum=True)
    po_s = mk(nc, es, "po_s", [128, 4, 128], F32, psum=True)
    po_w = mk(nc, es, "po_w", [128, 4, 128], F32, psum=True)
    p1 = P1(S, nc, es, srcs, xs_out, g_row, ident_d, ps_tr)
    ident = p1.ident
    xnTt = [mk(nc, es, f"xnTt{b}", [128, 8, 128], BF16) for b in range(2)]

    wq = mk(nc, es, "wq_s", [128, 8, 512], BF16)
    wkv = mk(nc, es, "wkv_s", [128, 8, 6, 128], BF16)
    wg = mk(nc, es, "wg_s", [128, 8, 24], BF16)
    wz = mk(nc, es, "wz_s", [128, 8, 512], BF16)
    wo = mk(nc, es, "wo_s", [128, 4, D], BF16)
    w1 = mk(nc, es, "w1_s", [64, 2, 32, 128], BF16)
    w2 = mk(nc, es, "w2_s", [128, 2, 64], BF16)
    posT = mk(nc, es, "posT_s", [64, 2, 32], BF16)
    biasT = mk(nc, es, "biasT_s", [128, 4, 1024], BF16)
    Em = mk(nc, es, "E_s", [64, NT, 128], BF16)
    S.op("pool", lambda h: h.dma_start(out=wq[:], in_=wq_d.rearrange("(c p) n -> p c n", p=128)), w=["wq"], dma=True)
    S.op("pool", lambda h: h.dma_start(out=wkv[:], in_=wkv_d.rearrange("(c p) s n -> p c s n", p=128)), w=["wkv"], dma=True)
    S.op("pool", lambda h: h.dma_start(out=wg[:], in_=wg_d.rearrange("(c p) n -> p c n", p=128)), w=["wg"], dma=True)
    S.op("pool", lambda h: h.dma_start(out=wz[:], in_=wz_d.rearrange("(c p) n -> p c n", p=128)), w=["wz"], dma=True)
    S.op("pool", lambda h: h.dma_start(out=wo[:], in_=wo_d.rearrange("(c p) n -> p c n", p=128)), w=["wo"], dma=True)
    S.op("pool", lambda h: h.dma_start(out=w1[:], in_=w1_d.rearrange("s d l h -> d s l h")), w=["w1"], dma=True)
    S.op("pool", lambda h: h.dma_start(out=w2[:], in_=w2_d.rearrange("s h d -> h s d")), w=["w2"], dma=True)
    S.op("pool", lambda h: h.dma_start(out=posT[:], in_=pos_d.rearrange("s d l -> d s l")), w=["posT"], dma=True)
    S.op("pool", lambda h: h.dma_start(out=biasT[:], in_=bias_d), w=["biasT"], dma=True)
    S.op("pool", lambda h: h.dma_start(out=Em[:], in_=E_d), w=["E"], dma=True)

    kvT = mk(nc, es, "kvT", [64, 2, 2, T], BF16)
    roll = mk(nc, es, "roll", [64, 2, 2, 144], BF16)
    vau = mk(nc, es, "vau", [128, NT, 2, 2, 65], BF16)
    S.op("dve", lambda h: h.memset(vau[:, :, :, :, 64:65], 1.0), w=["vau_ones"])
    S.op("dve", lambda h: h.memset(roll[:], 0.0), w=["roll"])
    kcmpT = mk(nc, es, "kcmpT", [64, 2, 256], BF16)
    vcau = mk(nc, es, "vcau", [128, 2, 2, 65], BF16)
    ovl = mk(nc, es, "ovl_s", [128, 2, 64], BF16)
    hidn = mk(nc, es, "hidn", [128, 4, 8], BF16)
    hidv = mk(nc, es, "hidv", [128, 2, 256], BF16)
    pbias = mk(nc, es, "pbias", [128, 2], F32)
    S.op("dve", lambda h: h.memset(kcmpT[:], 0.0), w=["kcmpT"])
    S.op("dve", lambda h: h.memset(vcau[:], 0.0), w=["vcau"])
    S.op("dve", lambda h: h.memset(vcau[:, :, :, 64:65], 1.0), r=["vcau"], w=["vcau"])
    S.op("dve", lambda h: h.memset(hidv[:], 0.0), w=["hidv"])
    S.op("pool", lambda h: h.dma_start(out=ovl[:], in_=ov_d), w=["ovl"], dma=True)
    for s in range(2):
        for l in range(32):
            S.op("pe", lambda h, s=s, l=l: h.matmul(ps_z[:, s:s + 1], lhsT=w1[:, s, l, :], rhs=posT[:, s, l:l + 1], start=(l == 0), stop=(l == 31)),
                 r=["w1", "posT"], w=["ps_z"])
        S.op("act", lambda h, s=s: h.copy(out=pbias[:, s:s + 1], in_=ps_z[:, s:s + 1]), r=["ps_z"], w=["pbias"])

    NB = 2
    qT = [mk(nc, es, f"qT{b}", [64, 2, 4, 128], BF16) for b in range(NB)]
    zs = [mk(nc, es, f"zs{b}", [128, 512], BF16) for b in range(NB)]
    gt = [mk(nc, es, f"gt{b}", [128, 24], F32) for b in range(NB)]
    NP = 4
    pT = [mk(nc, es, f"pT{b}", [128, 512], BF16) for b in range(NP)]
    F4t = [mk(nc, es, f"F4t{b}", [128, 512], BF16) for b in range(2)]
    ka = [mk(nc, es, f"ka{b}", [128, 128], F32) for b in range(NB)]
    imp = mk(nc, es, "imp", [128, 64], F32)
    imp2 = mk(nc, es, "imp2", [128, 64], F32)
    m8 = mk(nc, es, "m8", [128, 16], F32)
    nsel = mk(nc, es, "nsel", [128, 64], BF16)
    nselT = mk(nc, es, "nselT", [64, 4, 128], BF16)
    rden = mk(nc, es, "rden", [128, 3, 4], F32)
    cf = mk(nc, es, "cf", [128, 3, 4], F32)
    y = mk(nc, es, "y", [128, 512], F32)
    yz = [mk(nc, es, f"yz{b}", [128, 512], BF16) for b in range(NB)]
    yzT = [mk(nc, es, f"yzT{b}", [128, 4, 128], BF16) for b in range(NB)]
    pt = [mk(nc, es, f"pt{b}", [128, D], F32) for b in range(NB)]
    pcount = [0]
    scount = [0]

    def st_tile(g, b, lhsT_ap, lhs_bufs, extra, rhs_aug, rhs_bufs, po, first, last, ncol=65):
        si = scount[0] % 2
        scount[0] += 1
        pi = pcount[0] % NP
        pcount[0] += 1
        n_extra = len(extra)
        S.op("pe", lambda h: h.matmul(ps_s[si][:, :], lhsT=lhsT_ap, rhs=qT[b][:, g, :, :].rearrange("p r n -> p (r n)"), start=True, stop=(n_extra == 0)),
             r=lhs_bufs + [f"qT{b}_{g}"], w=[f"ps_s{si}"])
        for j, (el, er, ebufs) in enumerate(extra):
            S.op("pe", lambda h, el=el, er=er, j=j: h.matmul(ps_s[si][:, :], lhsT=el, rhs=er, start=False, stop=(j == n_extra - 1)),
                 r=ebufs, w=[f"ps_s{si}"])
        S.op("act", lambda h: h.activation(out=pT[pi][:], in_=ps_s[si][:, :], func=AF.Exp), r=[f"ps_s{si}"], w=[f"pT{pi}"])
        return pi

    for qt in range(NT):
        b = qt % NB
        tq = slice(qt * 128, (qt + 1) * 128)
        xk = f"xnTt{qt % 2}"
        xn = xnTt[qt % 2]
        S.op("sp", lambda h, b=b, qt=qt: h.dma_start(out=ka[b][:], in_=ka_d[qt]), w=[f"ka{b}"], dma=True)
        p1.tile(qt, xn[:, :, :], xk)
        if qt > 0:
            S.op("pool", lambda h: h.tensor_copy(out=roll[:, :, :, 0:16], in_=roll[:, :, :, 128:144]), r=["roll"], w=["roll"])
        for grp, (wss, psx, nm) in enumerate((((0, 1), ps_q, "ps_q"), ((2, 4), ps_z, "ps_z"))):
            for si, ws in enumerate(wss):
                for g in range(2):
                    c0 = (si * 2 + g) * 128
                    for dc in range(8):
                        S.op("pe", lambda h, ws=ws, g=g, dc=dc, c0=c0, psx=psx, xn=xn: h.matmul(psx[0:64, c0:c0 + 128], lhsT=wkv[:, dc, ws, g * 64:(g + 1) * 64], rhs=xn[:, dc, :],
                                                                                     start=(dc == 0), stop=(dc == 7)), r=["wkv", xk], w=[nm])
            if grp == 0:
                S.op("act", lambda h, psx=psx: h.copy(out=roll[:, :, :, 16:144], in_=psx[0:64, :].rearrange("p (s g n) -> p s g n", s=2, g=2)), r=[nm], w=["roll"])
            else:
                S.op("act", lambda h, psx=psx, tq=tq: h.copy(out=kvT[:, :, :, tq], in_=psx[0:64, :].rearrange("p (s g n) -> p s g n", s=2, g=2)), r=[nm], w=[f"kvT_{qt // 4}"])
        for jj, ws in enumerate((3, 5)):
            for dc in range(8):
                S.op("pe", lambda h, dc=dc, ws=ws, jj=jj, xn=xn: h.matmul(ps_z[:, jj * 128:(jj + 1) * 128], lhsT=xn[:, dc, :], rhs=wkv[:, dc, ws, :],
                                                                     start=(dc == 0), stop=(dc == 7)), r=["wkv", xk], w=["ps_z"])
        S.op("dve", lambda h, qt=qt: h.tensor_copy(out=vau[:, qt, :, :, 0:64], in_=ps_z[:, 0:256].rearrange("p (j g d) -> p j g d", j=2, g=2)),
             r=["ps_z"], w=[f"vau{qt}"])
        m0 = 1 if qt == 0 else 0
        nb = 8 - m0
        n0 = 8 * qt - 1 + m0
        for s in range(2):
            for g in range(2):
                c0 = (s * 2 + g) * 8
                for l in range(32):
                    S.op("pe", lambda h, s=s, g=g, l=l, c0=c0, nb=nb, m0=m0: h.matmul(ps_q[:, c0:c0 + nb], lhsT=w1[:, s, l, :], rhs=roll[:, s, g, l + 16 * m0:l + 16 * 7 + 1:16],
                                                                         start=(l == 0), stop=(l == 31)), r=["w1", "roll"], w=["ps_q"])
        for s in range(2):
            S.op("act", lambda h, s=s, nb=nb: h.activation(out=hidn[:, s * 2:s * 2 + 2, 0:nb], in_=ps_q[:, s * 16:s * 16 + 16].rearrange("p (g n) -> p g n", g=2)[:, :, 0:nb],
                                                    func=AF.Silu, bias=pbias[:, s:s + 1]), r=["ps_q", "pbias"], w=["hidn"])
        for g in range(2):
            S.op("pe", lambda h, g=g, nb=nb: h.matmul(ps_z[0:64, g * 8:g * 8 + nb], lhsT=w2[:, 0, :], rhs=hidn[:, g, 0:nb], start=True, stop=True), r=["w2", "hidn"], w=["ps_z"])
        S.op("act", lambda h, nb=nb, n0=n0: h.copy(out=kcmpT[:, :, n0:n0 + nb], in_=ps_z[0:64, 0:16].rearrange("p (g n) -> p g n", g=2)[:, :, 0:nb]), r=["ps_z"], w=["kcmpT"])
        S.op("pool", lambda h, nb=nb, n0=n0: h.tensor_copy(out=hidv[:, :, n0:n0 + nb], in_=hidn[:, 2:4, 0:nb]), r=["hidn"], w=["hidv"])
        for nt in sorted(set([n0 // 128, (n0 + nb - 1) // 128])):
            for g in range(2):
                S.op("pe", lambda h, g=g, nt=nt: h.matmul(ps_z[:, 128 + g * 64:192 + g * 64], lhsT=hidv[:, g, nt * 128:(nt + 1) * 128], rhs=w2[:, 1, :], start=True, stop=True),
                     r=["w2", "hidv"], w=["ps_z"])
            S.op("act", lambda h, nt=nt: h.copy(out=vcau[:, nt, :, 0:64], in_=ps_z[:, 128:256].rearrange("p (g d) -> p g d", g=2)), r=["ps_z"], w=["vcau"])
        for g in range(2):
            for r in range(4):
                for dc in range(8):
                    col = (g * 4 + r) * 64
                    S.op("pe", lambda h, g=g, r=r, dc=dc, col=col, xn=xn: h.matmul(ps_q[0:64, r * 128:(r + 1) * 128], lhsT=wq[:, dc, col:col + 64],
                                                                               rhs=xn[:, dc, :], start=(dc == 0), stop=(dc == 7)),
                         r=["wq", xk], w=["ps_q"])
            S.op("act", lambda h, g=g, b=b: h.activation(out=qT[b][:, g, :, :], in_=ps_q[0:64, :].rearrange("p (r n) -> p r n", r=4),
                                                        func=AF.Copy, scale=0.125), r=["ps_q"], w=[f"qT{b}_{g}"])
        for dc in range(8):
            S.op("pe", lambda h, dc=dc, xn=xn: h.matmul(ps_z[:, :], lhsT=xn[:, dc, :], rhs=wz[:, dc, :], start=(dc == 0), stop=(dc == 7)),
                 r=["wz", xk], w=["ps_z"])
        S.op("act", lambda h, b=b: h.activation(out=zs[b][:], in_=ps_z[:, :], func=AF.Silu), r=["ps_z"], w=[f"zs{b}"])
        for dc in range(8):
            S.op("pe", lambda h, dc=dc, xn=xn: h.matmul(ps_z[:, 0:24], lhsT=xn[:, dc, :], rhs=wg[:, dc, :], start=(dc == 0), stop=(dc == 7)),
                 r=["wg", xk], w=["ps_z"])
        S.op("act", lambda h, b=b: h.activation(out=gt[b][:], in_=ps_z[:, 0:24], func=AF.Sigmoid), r=["ps_z"], w=[f"gt{b}"])
        cnts = []
        for nt in range(NKT):
            mmax = nt * 128 + 127 - 8 * qt
            mmin = nt * 128 - 8 * qt
            if mmin > 6:
                continue
            masked = mmax > -2
            cnts.append((nt, masked))
        for (nt, masked) in cnts:
            if masked:
                j0 = 128 * nt - 8 * qt + 248
                S.op("pool", lambda h, nt=nt, j0=j0: h.dma_start(out=F4t[nt][:], in_=F4_d[j0:j0 + 128, :]), w=[f"F4t{nt}"], dma=True)
        for g in range(2):
            pis = []
            for (nt, masked) in cnts:
                extra = [(ident[:], F4t[nt][:], ["p1id", f"F4t{nt}"])] if masked else []
                pi = st_tile(g, b, kcmpT[:, g, nt * 128:(nt + 1) * 128], ["kcmpT"], extra, None, None, None, None, None)
                pis.append((nt, pi))
            for r in range(4):
                for j, (nt, pi) in enumerate(pis):
                    S.op("pe", lambda h, g=g, r=r, nt=nt, pi=pi, j=j: h.matmul(po_c[:, r, 0:65], lhsT=pT[pi][:, r * 128:(r + 1) * 128], rhs=vcau[:, nt, g, :],
                                                                             start=(j == 0), stop=(j == len(pis) - 1)), r=[f"pT{pi}", "vcau"], w=["po_c"])
            for r in range(4):
                for j, (nt, pi) in enumerate(pis):
                    S.op("pe", lambda h, g=g, r=r, nt=nt, pi=pi, j=j: h.matmul(po_w[:, r, 0:64], lhsT=pT[pi][:, r * 128:(r + 1) * 128], rhs=ovl[:, nt, :],
                                                                             start=(j == 0), stop=(j == len(pis) - 1)), r=[f"pT{pi}", "ovl"], w=["po_w"])
            S.op("dve", lambda h: h.tensor_scalar(out=rden[:, 0, :], in0=po_c[:, :, 64], scalar1=1e-30, scalar2=None, op0=ALU.add), r=["po_c"], w=["rden0"])
            S.op("dve", lambda h: h.reciprocal(out=rden[:, 0, :], in_=rden[:, 0, :]), r=["rden0"], w=["rden0"])
            S.op("dve", lambda h: h.tensor_scalar(out=imp[:], in0=po_w[:, 0, 0:64], scalar1=rden[:, 0, 0:1], scalar2=None, op0=ALU.mult), r=["po_w", "rden0"], w=["imp"])
            for r in range(1, 4):
                S.op("dve", lambda h, r=r: h.scalar_tensor_tensor(out=imp[:], in0=po_w[:, r, 0:64], scalar=rden[:, 0, r:r + 1], in1=imp[:], op0=ALU.mult, op1=ALU.add),
                     r=["po_w", "rden0", "imp"], w=["imp"])
            S.op("dve", lambda h, b=b: h.tensor_tensor(out=imp[:], in0=imp[:], in1=ka[b][:, 0:64], op=ALU.mult), r=["imp", f"ka{b}"], w=["imp"])
            S.op("dve", lambda h, b=b: h.tensor_tensor(out=imp[:], in0=imp[:], in1=ka[b][:, 64:128], op=ALU.add), r=["imp", f"ka{b}"], w=["imp"])
            S.op("dve", lambda h: h.max(out=m8[:, 0:8], in_=imp[:]), r=["imp"], w=["m8"])
            S.op("dve", lambda h: h.match_replace(out=imp2[:], in_to_replace=m8[:, 0:8], in_values=imp[:], imm_value=-3.0e38), r=["imp", "m8"], w=["imp2"])
            S.op("dve", lambda h: h.max(out=m8[:, 8:16], in_=imp2[:]), r=["imp2"], w=["m8"])
            S.op("dve", lambda h: h.tensor_scalar(out=imp2[:], in0=imp[:], scalar1=m8[:, 15:16], scalar2=1.0, op0=ALU.is_ge, op1=ALU.subtract),
                 r=["imp", "m8"], w=["imp2"])
            S.op("dve", lambda h: h.tensor_scalar(out=nsel[:], in0=imp2[:], scalar1=-NEGM, scalar2=None, op0=ALU.mult), r=["imp2"], w=["nsel"])
            S.op("pe", lambda h: h.transpose(out=ps_tr[0:64, 0:128], in_=nsel[:], identity=ident[:]), r=["nsel", "p1id"], w=["p1pstr"])
            for r in range(4):
                S.op("act", lambda h, r=r: h.copy(out=nselT[:, r, :], in_=ps_tr[0:64, 0:128]), r=["p1pstr"], w=["nselT"])
            for kt in range(qt + 1):
                cls = 0 if kt == qt else (1 if kt == qt - 1 else 3)
                extra = [(Em[:, kt, :], nselT[:].rearrange("p r n -> p (r n)"), ["E", "nselT"]),
                         (ident[:], biasT[:, cls, g * 512:(g + 1) * 512], ["p1id", "biasT"])]
                pi = st_tile(g, b, kvT[:, 0, g, kt * 128:(kt + 1) * 128], [f"kvT_{kt // 4}"], extra, None, None, None, None, None)
                for r in range(4):
                    S.op("pe", lambda h, g=g, r=r, kt=kt, pi=pi: h.matmul(po_s[:, r, 0:65], lhsT=pT[pi][:, r * 128:(r + 1) * 128], rhs=vau[:, kt, 0, g, :],
                                                                        start=(kt == 0 and r == 0), stop=(kt == qt), skip_group_check=True),
                         r=[f"pT{pi}", f"vau{kt}", "vau_ones"], w=["po_s"])
            kts = [kt for kt in range(qt - 4, qt + 1) if kt >= 0]
            for j, kt in enumerate(kts):
                dq = qt - kt
                cls = 0 if dq == 0 else (1 if dq == 1 else (2 if dq == 4 else 3))
                extra = [(ident[:], biasT[:, cls, g * 512:(g + 1) * 512], ["p1id", "biasT"])]
                pi = st_tile(g, b, kvT[:, 1, g, kt * 128:(kt + 1) * 128], [f"kvT_{kt // 4}"], extra, None, None, None, None, None)
                for r in range(4):
                    S.op("pe", lambda h, g=g, r=r, kt=kt, pi=pi, j=j: h.matmul(po_w[:, r, 0:65], lhsT=pT[pi][:, r * 128:(r + 1) * 128], rhs=vau[:, kt, 1, g, :],
                                                                             start=(j == 0 and r == 0), stop=(j == len(kts) - 1), skip_group_check=True),
                         r=[f"pT{pi}", f"vau{kt}", "vau_ones"], w=["po_w"])
            S.op("dve", lambda h: h.reciprocal(out=rden[:, 1, :], in_=po_s[:, :, 64]), r=["po_s"], w=["rden1"])
            S.op("dve", lambda h: h.reciprocal(out=rden[:, 2, :], in_=po_w[:, :, 64]), r=["po_w"], w=["rden2"])
            for j in range(3):
                S.op("dve", lambda h, j=j, g=g, b=b: h.tensor_tensor(out=cf[:, j, :], in0=rden[:, j, :], in1=gt[b][:, j * 8 + g * 4:j * 8 + g * 4 + 4], op=ALU.mult),
                     r=[f"rden{j}", f"gt{b}"], w=["cf"])
            for r in range(4):
                col = (g * 4 + r) * 64
                S.op("dve", lambda h, r=r, col=col: h.tensor_scalar(out=y[:, col:col + 64], in0=po_c[:, r, 0:64], scalar1=cf[:, 0, r:r + 1], scalar2=None, op0=ALU.mult),
                     r=["po_c", "cf"], w=["y"])
                S.op("dve", lambda h, r=r, col=col: h.scalar_tensor_tensor(out=y[:, col:col + 64], in0=po_s[:, r, 0:64], scalar=cf[:, 1, r:r + 1], in1=y[:, col:col + 64],
                                                                          op0=ALU.mult, op1=ALU.add), r=["po_s", "cf", "y"], w=["y"])
                S.op("dve", lambda h, r=r, col=col: h.scalar_tensor_tensor(out=y[:, col:col + 64], in0=po_w[:, r, 0:64], scalar=cf[:, 2, r:r + 1], in1=y[:, col:col + 64],
                                                                          op0=ALU.mult, op1=ALU.add), r=["po_w", "cf", "y"], w=["y"])
        S.op("pool", lambda h, b=b: h.tensor_tensor(out=yz[b][:], in0=y[:], in1=zs[b][:], op=ALU.mult), r=["y", f"zs{b}"], w=[f"yz{b}"])
        for c in range(4):
            S.op("pe", lambda h, c=c, b=b: h.transpose(out=ps_tr[:, c * 128:(c + 1) * 128], in_=yz[b][:, c * 128:(c + 1) * 128], identity=ident[:]),
                 r=[f"yz{b}", "p1id"], w=["p1pstr"])
        S.op("act", lambda h, b=b: h.copy(out=yzT[b][:], in_=ps_tr[:, 0:512].rearrange("p (c n) -> p c n", c=4)), r=["p1pstr"], w=[f"yzT{b}"])
        for hf in range(2):
            psy = ps_q if hf == 0 else ps_z
            nm = "ps_q" if hf == 0 else "ps_z"
            for c in range(4):
                S.op("pe", lambda h, hf=hf, c=c, b=b, psy=psy: h.matmul(psy[:, :], lhsT=yzT[b][:, c, :], rhs=wo[:, c, hf * 512:(hf + 1) * 512],
                                                                       start=(c == 0), stop=(c == 3)), r=[f"yzT{b}", "wo"], w=[nm])
            if hf == 0:
                S.op("act", lambda h, b=b, psy=psy: h.copy(out=pt[b][:, 0:512], in_=psy[:, :]), r=[nm], w=[f"pt{b}"])
            else:
                S.op("dve", lambda h, b=b, psy=psy: h.tensor_copy(out=pt[b][:, 512:1024], in_=psy[:, :]), r=[nm], w=[f"pt{b}"])
        S.op("sp", lambda h, b=b, tq=tq: h.dma_start(out=p_out[tq, :], in_=pt[b][:]), r=[f"pt{b}"], dma=True)
    return nc, es, S


def prep_C(z, half, T):
    NT = T // 128
    d = {}
    w_in = z['c_w_in'][0]
    d['wq'] = np.ascontiguousarray(w_in[:, half * 512:(half + 1) * 512])
    kv = []
    for s in range(6):
        base = 1024 + s * 256 + half * 128
        kv.append(w_in[:, base:base + 128])
    d['wkv'] = np.ascontiguousarray(np.stack(kv, 1))
    gcols = np.concatenate([2560 + j * 16 + half * 8 + np.arange(8) for j in range(3)])
    d['wg'] = np.ascontiguousarray(w_in[:, gcols])
    d['wz'] = np.ascontiguousarray(w_in[:, 2608 + half * 512:2608 + (half + 1) * 512])
    d['wo'] = np.ascontiguousarray(z['c_w_out'][0][half * 512:(half + 1) * 512, :])
    w1 = np.stack([z['c_cmp_k_w1'][0], z['c_cmp_v_w1'][0]])
    d['w1'] = np.ascontiguousarray(w1.reshape(2, 32, 64, 128).transpose(0, 2, 1, 3))
    d['w2'] = np.ascontiguousarray(np.stack([z['c_cmp_k_w2'][0], z['c_cmp_v_w2'][0]]))
    d['posT'] = np.ascontiguousarray(np.stack([z['c_cmp_pos_k'][0].T, z['c_cmp_pos_v'][0].T]))
    table = z['t5_table']
    tk = np.arange(128)[:, None]
    tq = np.arange(128)[None, :]
    bias = np.zeros((128, 4, 2, 4, 128), np.float32)
    for g in range(2):
        for r in range(4):
            hh = half * 8 + g * 4 + r
            d0 = tq - tk
            bias[:, 0, g, r, :] = np.where(d0 >= 0, table[t5_bucket_np(d0), hh], NEGM)
            d1 = tq - tk + 128
            bias[:, 1, g, r, :] = table[t5_bucket_np(d1), hh]
            bias[:, 2, g, r, :] = np.where(tq < tk, table[31, hh], NEGM)
            bias[:, 3, g, r, :] = table[31, hh]
    d['biasT'] = bias.reshape(128, 4, 1024)
    j = np.arange(512)[:, None]
    F = np.where(16 * (j - 248) + 31 <= tq, 0.0, NEGM).astype(np.float32)
    d['F4'] = np.ascontiguousarray(np.tile(F, (1, 4)))
    ka = np.zeros((NT, 128, 128), np.float32)
    sblk = np.arange(64)[None, :]
    for qt in range(NT):
        t = qt * 128 + np.arange(128)[:, None]
        cur = t // 64
        forced = (sblk == 0) | (sblk == cur) | (sblk == cur - 1)
        future = sblk * 64 > t
        ka[qt, :, 0:64] = np.where(forced | future, 0.0, 1.0)
        ka[qt, :, 64:128] = np.where(forced, 1e30, np.where(future, -1e30, 0.0))
    d['keepadd'] = ka
    E = np.zeros((64, NT, 128), np.float32)
    for kt in range(NT):
        E[2 * kt, kt, 0:64] = 1.0
        E[2 * kt + 1, kt, 64:128] = 1.0
    d['E'] = E
    n = np.arange(256)[:, None]
    s = np.arange(64)[None, :]
    ov = ((16 * n < 64 * s + 64) & (16 * n + 31 >= 64 * s)).astype(np.float32)
    d['ovl'] = np.ascontiguousarray(ov.reshape(2, 128, 64).transpose(1, 0, 2))
    d['g'] = z['norm_g'][2:3].copy()
    d['ident'] = np.eye(128, dtype=np.float32)
    return d


NBLK = 8
BW = 80


def build_D(nsrc=1, TCH=1024, debug=False):
    T = CFG.T
    nc = get_nc()
    es = ExitStack()
    S = get_sched(nc, es)
    srcs = [dram_in(nc, f"xin{k}", [T, D]) for k in range(nsrc)]
    xs_out = dram_out(nc, "xs", [T, D]) if nsrc > 1 else None
    g_row = dram_in(nc, "g", [1, D])
    ident_d = dram_in(nc, "ident", [128, 128])
    wu_d = dram_in(nc, "wu", [D, NBLK * BW])
    wz_d = dram_in(nc, "wz", [D, NBLK * BW])
    wo_d = dram_in(nc, "wo", [NBLK * BW, D])
    ga_d = dram_in(nc, "ga", [NBLK, BW, BW])
    gx_d = dram_in(nc, "gx", [NBLK, BW, BW])
    vec_d = dram_in(nc, "vecs", [BW, NBLK, 8])
    p_out = dram_out(nc, "p", [T, D])

    xnT = mk(nc, es, "xnT", [128, 8, T], BF16)
    ps = [mk(nc, es, f"ps{k}", [128, 512], F32, psum=True) for k in range(7)]
    ps_tr = mk(nc, es, "ps_tr", [128, 1024], BF16, psum=True)
    phase1(S, nc, es, srcs, xs_out, g_row, ident_d, ps_tr, xnT=xnT)

    wu = mk(nc, es, "wu_s", [128, 8, NBLK * BW], BF16)
    wz = mk(nc, es, "wz_s", [128, 8, NBLK * BW], BF16)
    wo = mk(nc, es, "wo_s", [BW, NBLK, D], BF16)
    ga = mk(nc, es, "ga_s", [BW, NBLK, BW], F32)
    gx = mk(nc, es, "gx_s", [BW, NBLK, BW], F32)
    vec = mk(nc, es, "vec_s", [BW, NBLK, 8], F32)
    der = mk(nc, es, "der_s", [BW, NBLK, 4], F32)
    S.op("pool", lambda h: h.dma_start(out=wu[:], in_=wu_d.rearrange("(c p) n -> p c n", p=128)), w=["wu"], dma=True)
    S.op("pool", lambda h: h.dma_start(out=wz[:], in_=wz_d.rearrange("(c p) n -> p c n", p=128)), w=["wz"], dma=True)
    S.op("pool", lambda h: h.dma_start(out=wo[:], in_=wo_d.rearrange("(b p) n -> p b n", p=BW)), w=["wo"], dma=True)
    S.op("sp", lambda h: h.dma_start(out=ga[:], in_=ga_d.rearrange("b p n -> p b n")), w=["ga"], dma=True)
    S.op("sp", lambda h: h.dma_start(out=gx[:], in_=gx_d.rearrange("b p n -> p b n")), w=["gx"], dma=True)
    S.op("sp", lambda h: h.dma_start(out=vec[:], in_=vec_d), w=["vec"], dma=True)
    S.op("act", lambda h: h.activation(out=der[:, :, 0:1], in_=vec[:, :, 7:8], func=AF.Exp, scale=-1.0), r=["vec"], w=["der"])
    S.op("act", lambda h: h.activation(out=der[:, :, 1:2], in_=der[:, :, 0:1], func=AF.Ln, bias=1.0), r=["der"], w=["der"])
    S.op("act", lambda h: h.mul(out=der[:, :, 2:3], in_=der[:, :, 1:2], mul=-8.0), r=["der"], w=["der"])

    NW = 2
    def wt(nm, cols=TCH, dt=F32):
        return [mk(nc, es, f"{nm}{b}", [BW, cols], dt) for b in range(NW)]
    u_t = wt("u_t", TCH + 3)
    uc_t = wt("uc_t"); zs_t = wt("zs_t"); r_t = wt("r_t"); i_t = wt("i_t"); a_t = r_t; m_t = [mk(nc, es, "m_t0", [BW, TCH], F32)] * NW; h_t = uc_t
    hz = mk(nc, es, "hz", [BW, NBLK, TCH], BF16)
    hlast = mk(nc, es, "hlast", [BW, NBLK], F32)
    uhalo = mk(nc, es, "uhalo", [BW, NBLK, 3], F32)
    pt = [mk(nc, es, "pt0", [128, D], F32)] * 2
    S.op("dve", lambda h: h.memset(hlast[:], 0.0), w=["hlast"])
    for b in range(NW):
        S.op("dve", lambda h, b=b: h.memset(u_t[b][:, 0:3], 0.0), w=[f"u{b}"])
    it = 0
    for tch in range(T // TCH):
        t0 = tch * TCH
        for blk in range(NBLK):
            b = it % NW
            pb = (it - 1) % NW
            it += 1
            cs = slice(blk * BW, (blk + 1) * BW)
            for hf in range(2):
                tk = slice(t0 + hf * 512, t0 + (hf + 1) * 512)
                for dc in range(8):
                    S.op("pe", lambda h, hf=hf, dc=dc, tk=tk, cs=cs: h.matmul(ps[hf][0:BW, :], lhsT=wu[:, dc, cs], rhs=xnT[:, dc, tk],
                                                                         start=(dc == 0), stop=(dc == 7)),
                         r=["wu", f"xnT{(t0 + hf * 512) // 512}"], w=[f"ps{hf}"])
            for hf in range(2):
                tk = slice(t0 + hf * 512, t0 + (hf + 1) * 512)
                for dc in range(8):
                    S.op("pe", lambda h, hf=hf, dc=dc, tk=tk, cs=cs: h.matmul(ps[2 + hf][0:BW, :], lhsT=wz[:, dc, cs], rhs=xnT[:, dc, tk],
                                                                         start=(dc == 0), stop=(dc == 7)),
                         r=["wz", f"xnT{(t0 + hf * 512) // 512}"], w=[f"ps{2 + hf}"])
            S.op("pool", lambda h, b=b, blk=blk: h.tensor_copy(out=u_t[b][:, 0:3], in_=uhalo[:, blk, :]), r=["uhalo%d" % blk], w=[f"u{b}"]) if tch > 0 else None
            for hf in range(2):
                S.op("act", lambda h, hf=hf, b=b: h.copy(out=u_t[b][:, 3 + hf * 512:3 + (hf + 1) * 512], in_=ps[hf][0:BW, :]),
                     r=[f"ps{hf}"], w=[f"u{b}"])
            for hf in range(2):
                S.op("act", lambda h, hf=hf, b=b: h.activation(out=zs_t[b][:, hf * 512:(hf + 1) * 512], in_=ps[2 + hf][0:BW, :], func=AF.Silu),
                     r=[f"ps{2 + hf}"], w=[f"zs{b}"])
            S.op("pool", lambda h, b=b, blk=blk: h.tensor_copy(out=uhalo[:, blk, :], in_=u_t[b][:, TCH:TCH + 3]), r=[f"u{b}"], w=["uhalo%d" % blk])
            if debug and tch == 0 and blk == 0:
                dbg(S, nc, "xnT", xnT[:, :, 0:512], [128, 8, 512], ["xnT0"], BF16)
                dbg(S, nc, "u", u_t[b][:], [BW, TCH + 3], [f"u{b}"])
                dbg(S, nc, "zs", zs_t[b][:], [BW, TCH], [f"zs{b}"])
            S.op("dve", lambda h, b=b, blk=blk: h.tensor_scalar(out=uc_t[b][:], in0=u_t[b][:, 3:3 + TCH], scalar1=vec[:, blk, 3:4], scalar2=vec[:, blk, 4:5],
                                                               op0=ALU.mult, op1=ALU.add), r=[f"u{b}", "vec"], w=[f"uc{b}"])
            for j in range(3):
                S.op("dve", lambda h, b=b, blk=blk, j=j: h.scalar_tensor_tensor(out=uc_t[b][:], in0=u_t[b][:, j:j + TCH], scalar=vec[:, blk, j:j + 1],
                                                                               in1=uc_t[b][:], op0=ALU.mult, op1=ALU.add),
                     r=[f"u{b}", "vec"], w=[f"uc{b}"])
            for hf in range(2):
                S.op("pe", lambda h, hf=hf, b=b, blk=blk: h.matmul(ps[4][0:BW, :] if hf == 0 else ps[5][0:BW, :], lhsT=ga[:, blk, :],
                                                                  rhs=uc_t[b][:, hf * 512:(hf + 1) * 512], start=True, stop=True),
                     r=["ga", f"uc{b}"], w=[f"ps{4 + hf}"])
                S.op("act", lambda h, hf=hf, b=b, blk=blk: h.activation(out=r_t[b][:, hf * 512:(hf + 1) * 512], in_=ps[4 + hf][0:BW, :], func=AF.Sigmoid,
                                                                       bias=vec[:, blk, 5:6]), r=[f"ps{4 + hf}", "vec"], w=[f"r{b}"])
            for hf in range(2):
                S.op("pe", lambda h, hf=hf, b=b, blk=blk: h.matmul(ps[4 + hf][0:BW, :], lhsT=gx[:, blk, :],
                                                                  rhs=uc_t[b][:, hf * 512:(hf + 1) * 512], start=True, stop=True),
                     r=["gx", f"uc{b}"], w=[f"ps{4 + hf}"])
                S.op("act", lambda h, hf=hf, b=b, blk=blk: h.activation(out=i_t[b][:, hf * 512:(hf + 1) * 512], in_=ps[4 + hf][0:BW, :], func=AF.Sigmoid,
                                                                       bias=vec[:, blk, 6:7]), r=[f"ps{4 + hf}", "vec"], w=[f"i{b}"])
            if debug and tch == 0 and blk == 0:
                dbg(S, nc, "uc", uc_t[b][:], [BW, TCH], [f"uc{b}"])
                dbg(S, nc, "r", r_t[b][:], [BW, TCH], [f"r{b}"])
                dbg(S, nc, "i", i_t[b][:], [BW, TCH], [f"i{b}"])
                dbg(S, nc, "der", der[:], [BW, NBLK, 4], ["der"])
            S.op("act", lambda h, b=b, blk=blk: h.activation(out=a_t[b][:], in_=r_t[b][:], func=AF.Exp, scale=der[:, blk, 2:3]),
                 r=[f"r{b}", "der"], w=[f"r{b}"])
            S.op("pool", lambda h, b=b: h.tensor_tensor(out=m_t[b][:], in0=a_t[b][:], in1=a_t[b][:], op=ALU.mult), r=[f"r{b}"], w=["m0"])
            S.op("act", lambda h, b=b: h.activation(out=m_t[b][:], in_=m_t[b][:], func=AF.Sqrt, scale=-1.0, bias=1.0), r=["m0"], w=["m0"])
            S.op("pool", lambda h, b=b: h.tensor_tensor(out=i_t[b][:], in0=i_t[b][:], in1=uc_t[b][:], op=ALU.mult), r=[f"i{b}", f"uc{b}"], w=[f"i{b}"])
            S.op("pool", lambda h, b=b: h.tensor_tensor(out=i_t[b][:], in0=i_t[b][:], in1=m_t[b][:], op=ALU.mult), r=[f"i{b}", "m0"], w=[f"i{b}"])
            if debug and tch == 0 and blk == 0:
                dbg(S, nc, "a", r_t[b][:], [BW, TCH], [f"r{b}"])
                dbg(S, nc, "m", m_t[b][:], [BW, TCH], ["m0"])
                dbg(S, nc, "bt", i_t[b][:], [BW, TCH], [f"i{b}"])
            S.op("dve", lambda h, b=b, blk=blk: h.tensor_tensor_scan(out=h_t[b][:], data0=a_t[b][:], data1=i_t[b][:], initial=hlast[:, blk:blk + 1],
                                                                    op0=ALU.mult, op1=ALU.add), r=[f"r{b}", f"i{b}", "hlast"], w=[f"uc{b}"])
            S.op("dve", lambda h, b=b, blk=blk: h.tensor_copy(out=hlast[:, blk:blk + 1], in_=h_t[b][:, TCH - 1:TCH]), r=[f"uc{b}"], w=["hlast"])
            S.op("dve", lambda h, b=b, blk=blk: h.tensor_tensor(out=hz[:, blk, :], in0=h_t[b][:], in1=zs_t[b][:], op=ALU.mult),
                 r=[f"uc{b}", f"zs{b}"], w=["hz"])
        if debug and tch == 0:
            dbg(S, nc, "hz", hz[:], [BW, NBLK, TCH], ["hz"], BF16)
        for tl in range(TCH // 128):
            pbuf = tl % 2
            for hf in range(2):
                for blk in range(NBLK):
                    S.op("pe", lambda h, tl=tl, hf=hf, blk=blk: h.matmul(ps[hf][:, :], lhsT=hz[:, blk, tl * 128:(tl + 1) * 128],
                                                                        rhs=wo[:, blk, hf * 512:(hf + 1) * 512], start=(blk == 0), stop=(blk == NBLK - 1)),
                         r=["hz", "wo"], w=[f"ps{hf}"])
                S.op("act" if hf == 0 else "dve",
                     (lambda h, hf=hf, pbuf=pbuf: h.copy(out=pt[pbuf][:, hf * 512:(hf + 1) * 512], in_=ps[hf][:, :])) if hf == 0 else
                     (lambda h, hf=hf, pbuf=pbuf: h.tensor_copy(out=pt[pbuf][:, hf * 512:(hf + 1) * 512], in_=ps[hf][:, :])),
                     r=[f"ps{hf}"], w=["pt0"])
            rows = slice(t0 + tl * 128, t0 + (tl + 1) * 128)
            S.op("sp", lambda h, pbuf=pbuf, rows=rows: h.dma_start(out=p_out[rows, :], in_=pt[pbuf][:]), r=["pt0"], dma=True)
    return nc, es, S


def build_F(ntok=2048):
    nc = get_nc()
    es = ExitStack()
    S = get_sched(nc, es)
    srcs = [dram_in(nc, f"xin{k}", [ntok, D]) for k in range(3)]
    g_row = dram_in(nc, "g", [1, D])
    out_d = dram_out(nc, "out", [ntok, D])
    g_bc = mk(nc, es, "g_bc", [128, D], F32)
    S.op("sp", lambda h: h.dma_start(out=g_bc[:], in_=g_row.partition_broadcast(128)), w=["g"], dma=True)
    NB = 2
    xt = [mk(nc, es, f"xt{b}", [128, D], F32) for b in range(NB)]
    sq = mk(nc, es, "sq", [128, D], F32)
    ot = [mk(nc, es, f"ot{b}", [128, D], F32) for b in range(NB)]
    st = [mk(nc, es, f"st{b}", [128, 4], F32) for b in range(NB)]
    for t in range(ntok // 128):
        b = t % NB
        rows = slice(t * 128, (t + 1) * 128)
        for k, src in enumerate(srcs):
            if k == 0:
                S.op("pool", lambda h, src=src, b=b, t=t: h.dma_start(out=xt[b][:], in_=src_rows(src, t)), w=[f"xt{b}"], dma=True)
            else:
                S.op("pool", lambda h, src=src, b=b, t=t: h.dma_start(out=xt[b][:], in_=src_rows(src, t), accum_op=ALU.add), r=[f"xt{b}"], w=[f"xt{b}"], dma=True)
        S.op("act", lambda h, b=b: h.activation(out=sq[:], in_=xt[b][:], func=AF.Square), r=[f"xt{b}"], w=["sq"])
        S.op("dve", lambda h, b=b: h.tensor_reduce(out=st[b][:, 0:1], in_=sq[:], axis=AX.X, op=ALU.add), r=["sq"], w=[f"st{b}"])
        S.op("act", lambda h, b=b: h.activation(out=st[b][:, 1:2], in_=st[b][:, 0:1], func=AF.Sqrt, scale=1.0 / D, bias=EPS), r=[f"st{b}"], w=[f"st{b}"])
        S.op("dve", lambda h, b=b: h.reciprocal(out=st[b][:, 2:3], in_=st[b][:, 1:2]), r=[f"st{b}"], w=[f"st{b}r"])
        S.op("dve", lambda h, b=b: h.scalar_tensor_tensor(out=ot[b][:], in0=xt[b][:], scalar=st[b][:, 2:3], in1=g_bc[:], op0=ALU.mult, op1=ALU.mult),
             r=[f"xt{b}", f"st{b}r", "g"], w=[f"ot{b}"])
        S.op("sp", lambda h, b=b, rows=rows: h.dma_start(out=out_d[rows, :], in_=ot[b][:]), r=[f"ot{b}"], dma=True)
    return nc, es, S


def prep_D(z, half):
    LW = 1280
    blks = list(range(half * 8, half * 8 + 8))
    cols = np.concatenate([np.arange(b * 80, (b + 1) * 80) for b in blks])
    w_in = z['d_w_in'][0]
    d = {}
    d['wu'] = np.ascontiguousarray(w_in[:, cols])
    d['wz'] = np.ascontiguousarray(w_in[:, LW + cols])
    d['wo'] = np.ascontiguousarray(z['d_w_out'][0][cols, :])
    d['ga'] = np.ascontiguousarray(z['d_gate_a_w'][0][blks])
    d['gx'] = np.ascontiguousarray(z['d_gate_x_w'][0][blks])
    vecs = np.zeros((80, 8, 8), np.float32)

    def fm(v):
        return v[cols].reshape(8, 80).T
    for j in range(4):
        vecs[:, :, j] = fm(z['d_conv_w'][0][j])
    vecs[:, :, 4] = fm(z['d_conv_b'][0])
    vecs[:, :, 5] = fm(z['d_gate_a_b'][0])
    vecs[:, :, 6] = fm(z['d_gate_x_b'][0])
    vecs[:, :, 7] = fm(z['d_lambda'][0])
    d['vecs'] = vecs
    d['g'] = z['norm_g'][3:4].copy()
    d['ident'] = np.eye(128, dtype=np.float32)
    return d


PAIRS = [[0, 1], [2, 3], [4, 5], [6, 7]]


def build_fused(T=4096, nlayers=4):
    CFG.T = T
    nc = bass.Bass("TRN2", target_bir_lowering=False)
    top = ExitStack()
    S = Sched(nc, top)
    CFG.nc, CFG.S = nc, S
    x_d = nc.dram_tensor("x", [T, D], F32, kind="ExternalInput").ap()
    out_d = nc.dram_tensor("out", [T, D], F32, kind="ExternalOutput").ap()
    p = [nc.dram_tensor(f"p_i{l}", [T, D], F32) for l in range(4)]
    CH = 512
    NCH = T // CH
    pg = [[nc.dram_tensor(f"pg_i{l}_{k}", [2 * CH, D], F32) for k in range(NCH)] for l in range(4)]

    def gsrc(l, rank):
        return lambda t: pg[l][t // 4].ap()[rank * CH + (t % 4) * 128:rank * CH + (t % 4 + 1) * 128, :]
    xs = [nc.dram_tensor(f"xs_i{l}", [T, D], F32) for l in range(3)]
    layers = [("A", build_A, {}), ("B", build_B, dict(MC=128)), ("C", build_C, {}), ("D", build_D, {})]
    prev_x = x_d
    for l, (nm, fn, kw) in enumerate(layers[:nlayers]):
        CFG.prefix = nm + "_"
        SB_USED[0] = 0
        ov = {"p": p[l].ap()}
        if l == 0:
            ov["xin0"] = x_d
            nsrc = 1
        else:
            ov["xin0"] = prev_x
            ov["xin1"] = gsrc(l - 1, 0)
            ov["xin2"] = gsrc(l - 1, 1)
            ov["xs"] = xs[l - 1].ap()
            nsrc = 3
        CFG.override = ov
        _, es, _ = fn(nsrc, **kw)
        for k in range(NCH):
            S.op("pool", lambda h, l=l, k=k: h.collective_compute("AllGather", ALU.bypass, replica_groups=PAIRS, ins=[p[l].ap()[k * CH:(k + 1) * CH, :].opt()],
                                                               outs=[pg[l][k].ap().opt()]), dma=True, cc=True)
        S.emit(final=False)
        S.barrier()
        es.close()
        if l > 0:
            prev_x = xs[l - 1].ap()
    CFG.prefix = "F_"
    SB_USED[0] = 0
    CFG.override = {"xin0": prev_x, "xin1": gsrc(nlayers - 1, 0), "xin2": gsrc(nlayers - 1, 1), "out": out_d}
    _, es, _ = build_F(T)
    stats = S.emit(final=True)
    CFG.nc, CFG.S, CFG.override, CFG.prefix = None, None, {}, ""
    return nc, stats


def kernel(**inputs):
    z = {k: np.ascontiguousarray(np.asarray(v, dtype=np.float32)) for k, v in inputs.items()}
    T = 4096
    x = z['x']
    B = x.shape[0]
    nc, _ = build_fused(T)
    per_half = []
    for h in range(2):
        d = {}
        for pre, pd in (("A_", prep_A(z, h)), ("B_", prep_B(z, h, MC=128)), ("C_", prep_C(z, h, T)), ("D_", prep_D(z, h))):
            for k, v in pd.items():
                d[pre + k] = v
        d["F_g"] = z['final_g'][None, :].copy()
        per_half.append(d)
    in_maps = [dict(per_half[c % 2], x=x[c // 2]) for c in range(8)]
    res = run_bass_kernel_spmd(nc, in_maps, core_ids=list(range(8)))
    out = np.stack([res.results[2 * b]['out'] for b in range(B)]).astype(np.float32)
    return out
```

```python
import numpy as np
from contextlib import ExitStack
import concourse.bass as bass
import concourse.mybir as mybir
from concourse.bass_utils import run_bass_kernel_spmd

F32 = mybir.dt.float32
BF16 = mybir.dt.bfloat16
AF = mybir.ActivationFunctionType
ALU = mybir.AluOpType
AX = mybir.AxisListType


class Buf:
    __slots__ = ("name", "lw", "rd")

    def __init__(self, name):
        self.name = name
        self.lw = None
        self.rd = {}


class Sched:
    COMPUTE = ("pe", "act", "dve", "pool")

    def __init__(self, nc, es, ndma_slots=8):
        self.nc = nc
        self.es = es
        self.ops = []
        self.bufs = {}
        self.ndma = ndma_slots
        self.handles = {"pe": nc.tensor, "act": nc.scalar, "dve": nc.vector, "pool": nc.gpsimd, "sp": nc.sync}
        self.need = []
        self.seg_dma = []
        self.last_compute = {}
        self.barrier_deps = set()
        self.pending_barrier = {}

    def buf(self, name):
        b = self.bufs.get(name)
        if b is None:
            b = Buf(name)
            self.bufs[name] = b
        return b

    def _B(self, lst):
        out = []
        for x in lst:
            if isinstance(x, str):
                out.append(self.buf(x))
            elif isinstance(x, Buf):
                out.append(x)
            elif x is None:
                continue
            else:
                out.extend(self._B(x))
        return out

    def op(self, eng, fn, r=(), w=(), dma=False, cc=False):
        i = len(self.ops)
        R = self._B(r)
        W = self._B(w)
        deps = set()
        if self.pending_barrier.get(eng):
            deps |= self.barrier_deps
            self.pending_barrier[eng] = False
        if cc:
            deps |= set(self.seg_dma)
        for b in R:
            if b.lw is not None:
                deps.add(b.lw)
        for b in W:
            if b.lw is not None:
                deps.add(b.lw)
            for k, v in b.rd.items():
                deps.add(v)
        for b in W:
            b.lw = i
            b.rd = {}
        key = ("dma", i) if dma else eng
        for b in R:
            b.rd[key] = i
        self.ops.append(dict(eng=eng, fn=fn, deps=deps, dma=dma, cc=cc))
        if dma:
            self.seg_dma.append(i)
        elif eng in self.COMPUTE:
            self.last_compute[eng] = i
        return i

    def _init_state(self):
        nc = self.nc
        self.sems = {e: self.es.enter_context(nc.semaphore("sem_" + e)) for e in self.COMPUTE}
        self.dsems = {q: [self.es.enter_context(nc.semaphore(f"dsem_{q}_{k}")) for k in range(self.ndma)] for q in ("sp", "pool")}
        self.ccsem = self.es.enter_context(nc.semaphore("sem_cc"))
        self.cccount = 0
        self.duses = {q: [0] * self.ndma for q in ("sp", "pool")}
        self.dcount = {"sp": 0, "pool": 0}
        self.cnt = {e: 0 for e in self.COMPUTE}
        self.token = []
        self.waited = {e: {} for e in self.handles}
        self.nwaits = 0
        self.emitted = 0
        self.inited = True

    def barrier(self):
        deps = set(self.seg_dma)
        for e in self.COMPUTE:
            if e in self.last_compute:
                deps.add(self.last_compute[e])
        self.barrier_deps = deps
        self.pending_barrier = {e: True for e in self.handles}
        self.seg_dma = []
        self.bufs = {}

    def emit(self, final=True):
        nc = self.nc
        ops = self.ops
        if not getattr(self, "inited", False):
            self._init_state()
        start = self.emitted
        n = len(ops)
        need = self.need
        need.extend([False] * (n - len(need)))
        for i in range(start, n):
            o = ops[i]
            for d in o["deps"]:
                po = ops[d]
                if po["dma"]:
                    continue
                if po["eng"] != o["eng"] or o["dma"] or o["eng"] != "pe":
                    assert d >= start or need[d], "cross-segment dependency on an op without increment"
                    need[d] = True
        lastc = {}
        for i in range(start, n):
            if not ops[i]["dma"] and ops[i]["eng"] in self.COMPUTE:
                lastc[ops[i]["eng"]] = i
        for e, i in lastc.items():
            need[i] = True
        sems, dsems, duses, dcount, cnt, token, waited = self.sems, self.dsems, self.duses, self.dcount, self.cnt, self.token, self.waited
        token.extend([None] * (n - len(token)))
        for i in range(start, n):
            o = ops[i]
            e = o["eng"]
            h = self.handles[e]
            wd = waited[e]
            reqs = {}
            for d in o["deps"]:
                po = ops[d]
                if (not po["dma"]) and (not o["dma"]) and po["eng"] == e and e == "pe":
                    continue
                sem, val, sk = token[d]
                if wd.get(sk, 0) >= val:
                    continue
                if sk not in reqs or reqs[sk][1] < val:
                    reqs[sk] = (sem, val)
            is_cc = o.get("cc", False)
            if o["dma"] and not is_cc:
                q = e
                s = dcount[q] % self.ndma
                dcount[q] += 1
                dsk = ("d", q, s)
                prev = 16 * duses[q][s]
                if prev > 0 and wd.get(dsk, 0) < prev:
                    if dsk not in reqs or reqs[dsk][1] < prev:
                        reqs[dsk] = (dsems[q][s], prev)
            for rk, (rsem, rval) in reqs.items():
                h.wait_ge(rsem, rval)
                wd[rk] = rval
                self.nwaits += 1
            ins = o["fn"](h)
            if is_cc:
                self.cccount += 1
                ins.then_inc(self.ccsem, 1)
                token[i] = (self.ccsem, self.cccount, ("cc",))
            elif o["dma"]:
                duses[q][s] += 1
                ins.then_inc(dsems[q][s], 16)
                token[i] = (dsems[q][s], 16 * duses[q][s], dsk)
            else:
                if need[i]:
                    cnt[e] += 1
                    ins.then_inc(sems[e], 1)
                    token[i] = (sems[e], cnt[e], ("c", e))
                else:
                    token[i] = (sems[e], cnt[e] + 0, ("c", e))
            o["fn"] = None
        self.emitted = n
        if final:
            h = self.handles["sp"]
            for q in ("sp", "pool"):
                for s in range(self.ndma):
                    if duses[q][s] > 0:
                        h.wait_ge(dsems[q][s], 16 * duses[q][s])
            if self.cccount:
                h.wait_ge(self.ccsem, self.cccount)
        self.stats = dict(nops=len(ops), nwaits=self.nwaits, incs=dict(cnt))
        return self.stats


class Stream:
    def __init__(self):
        self.items = []

    def op(self, *a, **k):
        self.items.append((a, k))


def merge_streams(S, streams, chunk=1):
    idx = [0] * len(streams)
    live = True
    while live:
        live = False
        for i, st in enumerate(streams):
            for _ in range(chunk):
                if idx[i] < len(st.items):
                    a, k = st.items[idx[i]]
                    S.op(*a, **k)
                    idx[i] += 1
                    live = True


class CFG:
    T = 4096
    prefix = ""
    nc = None
    S = None
    override = {}


def get_nc():
    if CFG.nc is not None:
        return CFG.nc
    return bass.Bass("TRN2", target_bir_lowering=False)


def get_sched(nc, es):
    if CFG.S is not None:
        return CFG.S
    return Sched(nc, es)
D = 1024
EPS = 1e-6


class Ctx:
    pass


SB_USED = [0]


def mk(nc, es, name, shape, dt, psum=False):
    if not psum:
        n = 1
        for d_ in shape[1:]:
            n *= d_
        n *= (2 if dt == BF16 else 4)
        SB_USED[0] += (n + 31) // 32 * 32
        assert SB_USED[0] <= 190 * 1024, f"SBUF over budget at {name}: {SB_USED[0]}"
    if psum:
        return es.enter_context(nc.psum_tensor(CFG.prefix + name, shape, dt))
    return es.enter_context(nc.sbuf_tensor(CFG.prefix + name, shape, dt))


def src_rows(src, t):
    if callable(src):
        return src(t)
    return src[t * 128:(t + 1) * 128, :]


def dram_in(nc, name, shape, dt=F32):
    if name in CFG.override:
        return CFG.override[name]
    return nc.dram_tensor(CFG.prefix + name, list(shape), dt, kind="ExternalInput").ap()


def dram_out(nc, name, shape, dt=F32):
    if name in CFG.override:
        return CFG.override[name]
    return nc.dram_tensor(CFG.prefix + name, list(shape), dt, kind="ExternalOutput").ap()


class P1:
    def __init__(self, S, nc, es, srcs, xs_out, g_row, ident_d, ps_tr, name="p1"):
        self.S, self.nc, self.srcs, self.xs_out, self.ps_tr, self.name = S, nc, srcs, xs_out, ps_tr, name
        self.g_bc = mk(nc, es, name + "_g", [128, D], F32)
        self.ident = mk(nc, es, name + "_id", [128, 128], BF16)
        self.identf = mk(nc, es, name + "_idf", [128, 128], F32)
        g_bc, ident, identf = self.g_bc, self.ident, self.identf
        S.op("sp", lambda h: h.dma_start(out=g_bc[:], in_=g_row.partition_broadcast(128)), w=[name + "g"], dma=True)
        S.op("sp", lambda h: h.dma_start(out=identf[:], in_=ident_d), w=[name + "idf"], dma=True)
        S.op("dve", lambda h: h.tensor_copy(out=ident[:], in_=identf[:]), r=[name + "idf"], w=[name + "id"])
        self.NB = 2
        self.xt = [mk(nc, es, f"{name}_x{b}", [128, D], F32) for b in range(self.NB)]
        self.sq = mk(nc, es, name + "_sq", [128, D], BF16)
        self.xnb = [mk(nc, es, f"{name}_xn{b}", [128, D], BF16) for b in range(self.NB)]
        self.st = [mk(nc, es, f"{name}_st{b}", [128, 4], F32) for b in range(self.NB)]

    def tile(self, t, dst_ap, dst_buf):
        S, name = self.S, self.name
        xt, sq, xnb, st, g_bc, ident, ps_tr = self.xt, self.sq, self.xnb, self.st, self.g_bc, self.ident, self.ps_tr
        b = t % self.NB
        rows = slice(t * 128, (t + 1) * 128)
        xb = f"{name}x{b}"
        for k, src in enumerate(self.srcs):
            if k == 0:
                S.op("pool", lambda h, src=src: h.dma_start(out=xt[b][:], in_=src_rows(src, t)), w=[xb], dma=True)
            else:
                S.op("pool", lambda h, src=src: h.dma_start(out=xt[b][:], in_=src_rows(src, t), accum_op=ALU.add), r=[xb], w=[xb], dma=True)
        if self.xs_out is not None and len(self.srcs) > 1:
            S.op("sp", lambda h: h.dma_start(out=self.xs_out[rows, :], in_=xt[b][:]), r=[xb], dma=True)
        S.op("act", lambda h: h.activation(out=sq[:], in_=xt[b][:], func=AF.Square), r=[xb], w=[name + "sq"])
        S.op("dve", lambda h: h.tensor_reduce(out=st[b][:, 0:1], in_=sq[:], axis=AX.X, op=ALU.add), r=[name + "sq"], w=[f"{name}st{b}"])
        S.op("act", lambda h: h.activation(out=st[b][:, 1:2], in_=st[b][:, 0:1], func=AF.Sqrt, scale=1.0 / D, bias=EPS), r=[f"{name}st{b}"], w=[f"{name}st{b}"])
        S.op("dve", lambda h: h.reciprocal(out=st[b][:, 2:3], in_=st[b][:, 1:2]), r=[f"{name}st{b}"], w=[f"{name}st{b}r"])
        S.op("dve", lambda h: h.scalar_tensor_tensor(out=xnb[b][:], in0=xt[b][:], scalar=st[b][:, 2:3], in1=g_bc[:], op0=ALU.mult, op1=ALU.mult),
             r=[xb, f"{name}st{b}r", name + "g"], w=[f"{name}xn{b}"])
        for dc in range(8):
            S.op("pe", lambda h, dc=dc: h.transpose(out=ps_tr[:, dc * 128:(dc + 1) * 128], in_=xnb[b][:, dc * 128:(dc + 1) * 128], identity=ident[:]),
                 r=[f"{name}xn{b}", name + "id"], w=[name + "pstr"])
        S.op("act", lambda h: h.copy(out=dst_ap, in_=ps_tr[:].rearrange("p (c n) -> p c n", c=8)), r=[name + "pstr"], w=[dst_buf])


def phase1(S, nc, es, srcs, xs_out, g_row, ident_d, ps_tr, ntiles=None, xnT=None, name="p1"):
    if ntiles is None:
        ntiles = CFG.T // 128
    p1 = P1(S, nc, es, srcs, xs_out, g_row, ident_d, ps_tr, name)
    for t in range(ntiles):
        p1.tile(t, xnT[:, :, t * 128:(t + 1) * 128], f"xnT{t // 4}")
    return p1.ident


def dbg(S, nc, name, ap, shape, rbuf, dt=F32):
    o = nc.dram_tensor("dbg_" + name, list(shape), dt, kind="ExternalOutput").ap()
    S.op("sp", lambda h: h.dma_start(out=o, in_=ap), r=rbuf, dma=True)


NEGM = -30000.0


def build_A(nsrc=1, debug=False):
    T = CFG.T
    NT = T // 128
    nc = get_nc()
    es = ExitStack()
    S = get_sched(nc, es)
    srcs = [dram_in(nc, f"xin{k}", [T, D]) for k in range(nsrc)]
    xs_out = dram_out(nc, "xs", [T, D]) if nsrc > 1 else None
    g_row = dram_in(nc, "g", [1, D])
    ident_d = dram_in(nc, "ident", [128, 128])
    wq_d = dram_in(nc, "wq", [D, 512])
    wk_d = dram_in(nc, "wk", [D, 128])
    wv_d = dram_in(nc, "wv", [D, 128])
    wz_d = dram_in(nc, "wz", [D, 512])
    wo_d = dram_in(nc, "wo", [512, D])
    bias_d = dram_in(nc, "biasT", [128, 2, 2 * 4 * 128])
    sink_d = dram_in(nc, "sinks", [1, 8])
    p_out = dram_out(nc, "p", [T, D])

    xnT = mk(nc, es, "xnT", [128, 8, T], BF16)
    ps_tr = mk(nc, es, "ps_tr", [128, 1024], BF16, psum=True)
    ps_q = mk(nc, es, "ps_q", [128, 512], F32, psum=True)
    ps_z = mk(nc, es, "ps_z", [128, 512], F32, psum=True)
    ps_s = [mk(nc, es, f"ps_s{k}", [128, 512], F32, psum=True) for k in range(2)]
    ps_o = mk(nc, es, "ps_o", [128, 4, 128], F32, psum=True)
    ps_y = [mk(nc, es, f"ps_y{k}", [128, 512], F32, psum=True) for k in range(2)]
    ident = phase1(S, nc, es, srcs, xs_out, g_row, ident_d, ps_tr, xnT=xnT)

    wq = mk(nc, es, "wq_s", [128, 8, 512], BF16)
    wk = mk(nc, es, "wk_s", [128, 8, 128], BF16)
    wv = mk(nc, es, "wv_s", [128, 8, 128], BF16)
    wz = mk(nc, es, "wz_s", [128, 8, 512], BF16)
    wo = mk(nc, es, "wo_s", [128, 4, D], BF16)
    biasT = mk(nc, es, "biasT_s", [128, 2, 1024], BF16)
    esink = mk(nc, es, "esink", [128, 8], F32)
    for nm, t_, d_, pat in (("wq", wq, wq_d, "(c p) n -> p c n"), ("wk", wk, wk_d, "(c p) n -> p c n"), ("wv", wv, wv_d, "(c p) n -> p c n"),
                            ("wz", wz, wz_d, "(c p) n -> p c n"), ("wo", wo, wo_d, "(c p) n -> p c n")):
        S.op("pool", lambda h, t_=t_, d_=d_, pat=pat: h.dma_start(out=t_[:], in_=d_.rearrange(pat, p=128)), w=[nm], dma=True)
    S.op("pool", lambda h: h.dma_start(out=biasT[:], in_=bias_d), w=["biasT"], dma=True)
    S.op("sp", lambda h: h.dma_start(out=esink[:], in_=sink_d.partition_broadcast(128)), w=["esink"], dma=True)
    S.op("act", lambda h: h.activation(out=esink[:], in_=esink[:], func=AF.Exp), r=["esink"], w=["esink"])

    kT = mk(nc, es, "kT", [64, 2, T], BF16)
    vau = mk(nc, es, "vau", [128, NT, 2, 65], BF16)
    S.op("dve", lambda h: h.memset(vau[:, :, :, 64:65], 1.0), w=["vau_ones"])
    for g in range(2):
        for c in range(T // 512):
            tk = slice(c * 512, (c + 1) * 512)
            for dc in range(8):
                S.op("pe", lambda h, g=g, dc=dc, tk=tk: h.matmul(ps_q[0:64, :], lhsT=wk[:, dc, g * 64:(g + 1) * 64], rhs=xnT[:, dc, tk],
                                                                start=(dc == 0), stop=(dc == 7)), r=["wk", f"xnT{c}"], w=["ps_q"])
            S.op("act", lambda h, g=g, tk=tk: h.copy(out=kT[:, g, tk], in_=ps_q[0:64, :]), r=["ps_q"], w=[f"kT{c // 1}"])
    for t in range(NT):
        for dc in range(8):
            S.op("pe", lambda h, t=t, dc=dc: h.matmul(ps_z[:, 0:128], lhsT=xnT[:, dc, t * 128:(t + 1) * 128], rhs=wv[:, dc, :],
                                                      start=(dc == 0), stop=(dc == 7)), r=["wv", f"xnT{t // 4}"], w=["ps_z"])
        S.op("dve", lambda h, t=t: h.tensor_copy(out=vau[:, t, :, 0:64], in_=ps_z[:, 0:128].rearrange("p (g d) -> p g d", g=2)),
             r=["ps_z"], w=[f"vau{t}"])

    NB = 2
    qT = [mk(nc, es, f"qT{b}", [64, 2, 4, 128], BF16) for b in range(NB)]
    zs = [mk(nc, es, f"zs{b}", [128, 512], BF16) for b in range(NB)]
    pT = [mk(nc, es, f"pT{b}", [128, 512], BF16) for b in range(4)]
    yz = [mk(nc, es, f"yz{b}", [128, 512], BF16) for b in range(NB)]
    yzT = [mk(nc, es, f"yzT{b}", [128, 4, 128], BF16) for b in range(NB)]
    den = [mk(nc, es, f"den{b}", [128, 8], F32) for b in range(NB)]
    pt = [mk(nc, es, f"pt{b}", [128, D], F32) for b in range(NB)]
    pti = 0
    for qt in range(NT):
        b = qt % NB
        tq = slice(qt * 128, (qt + 1) * 128)
        xk = f"xnT{qt // 4}"
        for g in range(2):
            for r in range(4):
                for dc in range(8):
                    col = (g * 4 + r) * 64
                    S.op("pe", lambda h, g=g, r=r, dc=dc, col=col, tq=tq: h.matmul(ps_q[0:64, r * 128:(r + 1) * 128], lhsT=wq[:, dc, col:col + 64],
                                                                               rhs=xnT[:, dc, tq], start=(dc == 0), stop=(dc == 7)),
                         r=["wq", xk], w=["ps_q"])
            S.op("act", lambda h, g=g, b=b: h.activation(out=qT[b][:, g, :, :], in_=ps_q[0:64, :].rearrange("p (r n) -> p r n", r=4),
                                                        func=AF.Copy, scale=0.125), r=["ps_q"], w=[f"qT{b}_{g}"])
        for dc in range(8):
            S.op("pe", lambda h, dc=dc, tq=tq: h.matmul(ps_z[:, :], lhsT=xnT[:, dc, tq], rhs=wz[:, dc, :], start=(dc == 0), stop=(dc == 7)),
                 r=["wz", xk], w=["ps_z"])
        S.op("act", lambda h, b=b: h.activation(out=zs[b][:], in_=ps_z[:, :], func=AF.Silu), r=["ps_z"], w=[f"zs{b}"])
        for g in range(2):
            kts = [kt for kt in (qt - 1, qt) if kt >= 0]
            for kt in kts:
                cls = 0 if kt == qt else 1
                si = kt % 2
                pi = (g * 2 + si)
                S.op("pe", lambda h, g=g, kt=kt, si=si, b=b: h.matmul(ps_s[si][:, :], lhsT=kT[:, g, kt * 128:(kt + 1) * 128],
                                                                     rhs=qT[b][:, g, :, :].rearrange("p r n -> p (r n)"), start=True, stop=False),
                     r=[f"kT{kt // 4}", f"qT{b}_{g}"], w=[f"ps_s{si}"])
                S.op("pe", lambda h, g=g, cls=cls, si=si: h.matmul(ps_s[si][:, :], lhsT=ident[:], rhs=biasT[:, cls, g * 512:(g + 1) * 512],
                                                                  start=False, stop=True), r=["biasT", "p1id"], w=[f"ps_s{si}"])
                S.op("act", lambda h, si=si, pi=pi: h.activation(out=pT[pi][:], in_=ps_s[si][:, :], func=AF.Exp), r=[f"ps_s{si}"], w=[f"pT{pi}"])
            for r in range(4):
                for j, kt in enumerate(kts):
                    pi = (g * 2 + kt % 2)
                    S.op("pe", lambda h, g=g, r=r, kt=kt, pi=pi, j=j: h.matmul(ps_o[:, r, 0:65], lhsT=pT[pi][:, r * 128:(r + 1) * 128],
                                                                             rhs=vau[:, kt, g, :], start=(j == 0), stop=(j == len(kts) - 1)),
                         r=[f"pT{pi}", f"vau{kt}", "vau_ones"], w=["ps_o"])
            S.op("dve", lambda h, g=g, b=b: h.tensor_tensor(out=den[b][:, g * 4:(g + 1) * 4], in0=ps_o[:, :, 64], in1=esink[:, g * 4:(g + 1) * 4], op=ALU.add),
                 r=["ps_o", "esink"], w=[f"den{b}"])
            S.op("dve", lambda h, g=g, b=b: h.reciprocal(out=den[b][:, g * 4:(g + 1) * 4], in_=den[b][:, g * 4:(g + 1) * 4]), r=[f"den{b}"], w=[f"den{b}"])
            for r in range(4):
                col = (g * 4 + r) * 64
                S.op("dve", lambda h, g=g, r=r, b=b, col=col: h.scalar_tensor_tensor(out=yz[b][:, col:col + 64], in0=ps_o[:, r, 0:64],
                                                                                   scalar=den[b][:, g * 4 + r:g * 4 + r + 1], in1=zs[b][:, col:col + 64],
                                                                                   op0=ALU.mult, op1=ALU.mult),
                     r=["ps_o", f"den{b}", f"zs{b}"], w=[f"yz{b}"])
        for c in range(4):
            S.op("pe", lambda h, c=c, b=b: h.transpose(out=ps_tr[:, c * 128:(c + 1) * 128], in_=yz[b][:, c * 128:(c + 1) * 128], identity=ident[:]),
                 r=[f"yz{b}", "p1id"], w=["p1pstr"])
        S.op("act", lambda h, b=b: h.copy(out=yzT[b][:], in_=ps_tr[:, 0:512].rearrange("p (c n) -> p c n", c=4)), r=["p1pstr"], w=[f"yzT{b}"])
        for hf in range(2):
            for c in range(4):
                S.op("pe", lambda h, hf=hf, c=c, b=b: h.matmul(ps_y[hf][:, :], lhsT=yzT[b][:, c, :], rhs=wo[:, c, hf * 512:(hf + 1) * 512],
                                                              start=(c == 0), stop=(c == 3)), r=[f"yzT{b}", "wo"], w=[f"ps_y{hf}"])
            if hf == 0:
                S.op("act", lambda h, b=b: h.copy(out=pt[b][:, 0:512], in_=ps_y[0][:, :]), r=["ps_y0"], w=[f"pt{b}"])
            else:
                S.op("dve", lambda h, b=b: h.tensor_copy(out=pt[b][:, 512:1024], in_=ps_y[1][:, :]), r=["ps_y1"], w=[f"pt{b}"])
        S.op("sp", lambda h, b=b, tq=tq: h.dma_start(out=p_out[tq, :], in_=pt[b][:]), r=[f"pt{b}"], dma=True)
    return nc, es, S


def t5_bucket_np(d):
    import math
    d = np.maximum(d, 0)
    df = np.maximum(d, 1).astype(np.float32)
    large = 16 + (np.log(df / 16) / math.log(128 / 16) * 16).astype(np.int32)
    large = np.minimum(large, 31)
    return np.where(d < 16, d, large)


def prep_A(z, half):
    d = {}
    w_in = z['a_w_in'][0]
    d['wq'] = np.ascontiguousarray(w_in[:, half * 512:(half + 1) * 512])
    d['wk'] = np.ascontiguousarray(w_in[:, 1024 + half * 128:1024 + (half + 1) * 128])
    d['wv'] = np.ascontiguousarray(w_in[:, 1280 + half * 128:1280 + (half + 1) * 128])
    d['wz'] = np.ascontiguousarray(w_in[:, 1536 + half * 512:1536 + (half + 1) * 512])
    d['wo'] = np.ascontiguousarray(z['a_w_out'][0][half * 512:(half + 1) * 512, :])
    d['sinks'] = np.ascontiguousarray(z['a_sinks'][0][half * 8:(half + 1) * 8][None, :])
    table = z['t5_table']
    tk = np.arange(128)[:, None]
    tq = np.arange(128)[None, :]
    bias = np.zeros((128, 2, 2, 4, 128), np.float32)
    for cls in range(2):
        dist = tq - tk + 128 * cls
        valid = (dist >= 0) & (dist < 128)
        bk = t5_bucket_np(dist)
        for g in range(2):
            for r in range(4):
                hh = half * 8 + g * 4 + r
                bias[:, cls, g, r, :] = np.where(valid, table[bk, hh], NEGM)
    d['biasT'] = bias.reshape(128, 2, 1024)
    d['g'] = z['norm_g'][0:1].copy()
    d['ident'] = np.eye(128, dtype=np.float32)
    return d


KAP = 0.6065306597126334
GN_EPS = 64e-5


class _Stop(Exception):
    pass


def build_B(nsrc=1, MC=256, debug=False, stage=99):
    try:
        return _build_B(nsrc, MC, debug, stage)
    except _Stop as e:
        return e.args[0]


def _build_B(nsrc=1, MC=256, debug=False, stage=99):
    T = CFG.T
    NJ = MC // 64
    NMC = T // MC
    nc = get_nc()
    es = ExitStack()
    S = get_sched(nc, es)
    srcs = [dram_in(nc, f"xin{k}", [T, D]) for k in range(nsrc)]
    xs_out = dram_out(nc, "xs", [T, D]) if nsrc > 1 else None
    g_row = dram_in(nc, "g", [1, D])
    ident_d = dram_in(nc, "ident", [128, 128])
    w4_d = dram_in(nc, "w4", [4, D, 512])
    lw_d = dram_in(nc, "lw", [2, D, 64])
    l2_d = dram_in(nc, "l2", [2, 64, 512])
    wo_d = dram_in(nc, "wo", [512, D])
    mu_d = dram_in(nc, "muT", [128, 6, 8])
    vec_d = dram_in(nc, "vecs", [64, 8, 8])
    lnw_d = dram_in(nc, "lnw", [1, 512])
    lnb_d = dram_in(nc, "lnb", [1, 512])
    mg_d = dram_in(nc, "maskG", [128, 128])
    mnt_d = dram_in(nc, "maskNT", [64, 64])
    rm_d = dram_in(nc, "resetm", [64, MC])
    p_out = dram_out(nc, "p", [T, D])

    ps_tr = mk(nc, es, "ps_tr", [128, 1024], BF16, psum=True)
    ps_proj = mk(nc, es, "ps_proj", [128, 512], F32, psum=True)
    ps_tok = mk(nc, es, "ps_tok", [128, 512], F32, psum=True)
    ps_bv = mk(nc, es, "ps_bv", [128, 512], F32, psum=True)
    ps_g = mk(nc, es, "ps_g", [128, 512], F32, psum=True)
    ps_n = mk(nc, es, "ps_n", [128, 512], F32, psum=True)
    ps_rec = mk(nc, es, "ps_rec", [128, 512], F32, psum=True)
    ps_y = mk(nc, es, "ps_y", [128, 512], F32, psum=True)

    g_bc = mk(nc, es, "g_bc", [128, D], F32)
    identf = mk(nc, es, "identf", [128, 128], F32)
    ident = mk(nc, es, "identb", [128, 128], BF16)
    S.op("sp", lambda h: h.dma_start(out=g_bc[:], in_=g_row.partition_broadcast(128)), w=["g"], dma=True)
    S.op("sp", lambda h: h.dma_start(out=identf[:], in_=ident_d), w=["identf"], dma=True)
    S.op("dve", lambda h: h.tensor_copy(out=ident[:], in_=identf[:]), r=["identf"], w=["ident"])
    W4 = mk(nc, es, "W4", [128, 4, 8, 512], BF16)
    W4m = mk(nc, es, "W4m", [128, 4, 8, 512], BF16)
    LW = mk(nc, es, "LW", [128, 2, 8, 64], BF16)
    LWm = mk(nc, es, "LWm", [128, 2, 8, 64], BF16)
    L2 = mk(nc, es, "L2", [64, 2, 512], BF16)
    wo = mk(nc, es, "wo_s", [128, 4, D], BF16)
    muT = mk(nc, es, "muT_s", [128, 6, 8], F32)
    vec = mk(nc, es, "vec_s", [64, 8, 8], F32)
    lnw = mk(nc, es, "lnw_s", [64, 512], F32)
    lnb = mk(nc, es, "lnb_s", [64, 512], F32)
    maskG = mk(nc, es, "maskG_s", [128, 128], F32)
    maskNT = mk(nc, es, "maskNT_s", [64, 64], F32)
    resetm = mk(nc, es, "resetm_s", [64, MC], F32)
    ones64 = mk(nc, es, "ones64", [64, 64], F32)
    S.op("pool", lambda h: h.dma_start(out=W4[:], in_=w4_d.rearrange("s (c p) n -> p s c n", p=128)), w=["W4"], dma=True)
    S.op("pool", lambda h: h.dma_start(out=LW[:], in_=lw_d.rearrange("s (c p) n -> p s c n", p=128)), w=["LW"], dma=True)
    S.op("pool", lambda h: h.dma_start(out=L2[:], in_=l2_d.rearrange("s k n -> k s n")), w=["L2"], dma=True)
    S.op("pool", lambda h: h.dma_start(out=wo[:], in_=wo_d.rearrange("(c p) n -> p c n", p=128)), w=["wo"], dma=True)
    S.op("sp", lambda h: h.dma_start(out=muT[:], in_=mu_d), w=["muT"], dma=True)
    S.op("sp", lambda h: h.dma_start(out=vec[:], in_=vec_d), w=["vec"], dma=True)
    S.op("sp", lambda h: h.dma_start(out=lnw[:], in_=lnw_d.partition_broadcast(64)), w=["lnw"], dma=True)
    S.op("sp", lambda h: h.dma_start(out=lnb[:], in_=lnb_d.partition_broadcast(64)), w=["lnb"], dma=True)
    S.op("sp", lambda h: h.dma_start(out=maskG[:], in_=mg_d), w=["maskG"], dma=True)
    S.op("sp", lambda h: h.dma_start(out=maskNT[:], in_=mnt_d), w=["maskNT"], dma=True)
    S.op("sp", lambda h: h.dma_start(out=resetm[:], in_=rm_d), w=["resetm"], dma=True)
    S.op("dve", lambda h: h.memset(ones64[:], 1.0), w=["ones64"])
    for s in range(4):
        for dc in range(8):
            S.op("pool" if dc % 2 else "dve", lambda h, s=s, dc=dc: h.tensor_scalar(out=W4m[:, s, dc, :], in0=W4[:, s, dc, :], scalar1=muT[:, s, dc:dc + 1], scalar2=None, op0=ALU.mult),
                 r=["W4", "muT"], w=["W4m"])
    for s in range(2):
        for dc in range(8):
            S.op("dve", lambda h, s=s, dc=dc: h.tensor_scalar(out=LWm[:, s, dc, :], in0=LW[:, s, dc, :], scalar1=muT[:, 4 + s, dc:dc + 1], scalar2=None, op0=ALU.mult),
                 r=["LW", "muT"], w=["LWm"])

    XW = 64 + MC
    xnT = mk(nc, es, "xnT", [128, 8, XW], BF16)
    xxT = mk(nc, es, "xxT", [128, 8, XW], BF16)
    S.op("dve", lambda h: h.memset(xnT[:, :, 0:64], 0.0), w=["xnT"])
    NB = 2
    xt = [mk(nc, es, f"xt{b}", [128, D], F32) for b in range(NB)]
    sq = mk(nc, es, "sq", [128, D], BF16)
    xnb = [mk(nc, es, f"xnb{b}", [128, D], BF16) for b in range(NB)]
    st = [mk(nc, es, f"st{b}", [128, 4], F32) for b in range(NB)]
    h1T = mk(nc, es, "h1T", [64, 2, MC], BF16)
    vwin = mk(nc, es, "vwin", [64, NJ, 512], F32)
    uT = mk(nc, es, "uT", [64, NJ, 512], F32)
    zs = mk(nc, es, "zs", [64, NJ, 512], F32)
    y_all = mk(nc, es, "y_all", [64, NJ, 512], F32)
    bv_all = mk(nc, es, "bv_all", [64, NJ, 512], F32)

    def ft(nm):
        return mk(nc, es, nm, [64, MC], F32)
    r_f = ft("r_f"); k_f = ft("k_f"); sig = ft("sig"); alp = ft("alp"); kk = ft("kk"); t1 = ft("t1"); t2 = ft("t2")
    cs = ft("cs"); kmod = ft("kmod"); bal = ft("bal")
    cLs = mk(nc, es, "cLs", [64, NJ], F32)
    cLd = mk(nc, es, "cLd", [64, NJ], F32)
    G2 = 2
    AR = [mk(nc, es, f"AR{i}", [64, NJ, 128], F32) for i in range(G2)]
    BK = [mk(nc, es, f"BK{i}", [64, NJ, 128], F32) for i in range(G2)]
    BKe = [mk(nc, es, f"BKe{i}", [64, NJ, 128], F32) for i in range(G2)]
    Gm = [mk(nc, es, f"Gm{i}", [64, NJ, 256], F32) for i in range(G2)]
    Tm = [mk(nc, es, f"Tm{i}", [64, NJ, 64], F32) for i in range(G2)]
    BKeT = [mk(nc, es, f"BKeT{i}", [64, NJ, 128], F32) for i in range(G2)]
    dcL = [mk(nc, es, f"dcL{i}", [64, NJ, 64], F32) for i in range(G2)]
    rkrp = [mk(nc, es, f"rkrp{i}", [64, NJ, 64], F32) for i in range(G2)]
    Dg = [mk(nc, es, f"Dg{i}", [64, NJ, 64], F32) for i in range(G2)]
    bon = [mk(nc, es, f"bon{i}", [64, NJ], F32) for i in range(G2)]
    Nk = [[mk(nc, es, f"Nk{j}_{i}", [64, 64], F32) for i in range(2)] for j in range(NJ)]
    NkT = [[mk(nc, es, f"NkT{j}_{i}", [64, 64], F32) for i in range(2)] for j in range(NJ)]
    Pm = [[mk(nc, es, f"Pm{j}_{i}", [64, 64], F32) for i in range(2)] for j in range(NJ)]
    ST = [[mk(nc, es, f"ST{h}_{i}", [64, 64], F32) for i in range(2)] for h in range(8)]
    WT = [mk(nc, es, f"WT{i}", [64, 64], F32) for i in range(2)]
    for h in range(8):
        S.op("dve", lambda hh, h=h: hh.memset(ST[h][0][:], 0.0), w=[f"ST{h}_0"])
    yn = mk(nc, es, "yn", [64, 512], F32)
    gst = mk(nc, es, "gst", [64, 4, 8], F32)
    yz = mk(nc, es, "yz", [64, 512], BF16)
    yzT = mk(nc, es, "yzT", [128, 4, 64], BF16)
    pt = mk(nc, es, "pt", [64, D], F32)

    def c3(t_):
        return t_[:].rearrange("p (c j) -> p c j", j=64)

    for mc in range(NMC):
        T0 = mc * MC
        if mc > 0:
            S.op("pool", lambda h: h.tensor_copy(out=xnT[:, :, 0:64], in_=xnT[:, :, MC:MC + 64]), r=["xnT"], w=["xnT"])
        for tl in range(MC // 128):
            t = (T0 // 128) + tl
            b = t % NB
            rows = slice(t * 128, (t + 1) * 128)
            for k, src in enumerate(srcs):
                if k == 0:
                    S.op("pool", lambda h, src=src, b=b, t=t: h.dma_start(out=xt[b][:], in_=src_rows(src, t)), w=[f"xt{b}"], dma=True)
                else:
                    S.op("pool", lambda h, src=src, b=b, t=t: h.dma_start(out=xt[b][:], in_=src_rows(src, t), accum_op=ALU.add),
                         r=[f"xt{b}"], w=[f"xt{b}"], dma=True)
            if xs_out is not None:
                S.op("sp", lambda h, b=b, rows=rows: h.dma_start(out=xs_out[rows, :], in_=xt[b][:]), r=[f"xt{b}"], dma=True)
            S.op("act", lambda h, b=b: h.activation(out=sq[:], in_=xt[b][:], func=AF.Square), r=[f"xt{b}"], w=["sq"])
            S.op("dve", lambda h, b=b: h.tensor_reduce(out=st[b][:, 0:1], in_=sq[:], axis=AX.X, op=ALU.add), r=["sq"], w=[f"st{b}"])
            S.op("act", lambda h, b=b: h.activation(out=st[b][:, 1:2], in_=st[b][:, 0:1], func=AF.Sqrt, scale=1.0 / D, bias=EPS), r=[f"st{b}"], w=[f"st{b}"])
            S.op("dve", lambda h, b=b: h.reciprocal(out=st[b][:, 2:3], in_=st[b][:, 1:2]), r=[f"st{b}"], w=[f"st{b}r"])
            S.op("dve", lambda h, b=b: h.scalar_tensor_tensor(out=xnb[b][:], in0=xt[b][:], scalar=st[b][:, 2:3], in1=g_bc[:], op0=ALU.mult, op1=ALU.mult),
                 r=[f"xt{b}", f"st{b}r", "g"], w=[f"xnb{b}"])
            for dc in range(8):
                S.op("pe", lambda h, dc=dc, b=b: h.transpose(out=ps_tr[:, dc * 128:(dc + 1) * 128], in_=xnb[b][:, dc * 128:(dc + 1) * 128], identity=ident[:]),
                     r=[f"xnb{b}", "ident"], w=["ps_tr"])
            S.op("act", lambda h, tl=tl: h.copy(out=xnT[:, :, 64 + tl * 128:64 + (tl + 1) * 128], in_=ps_tr[:].rearrange("p (c n) -> p c n", c=8)),
                 r=["ps_tr"], w=["xnT"])
        S.op("pool", lambda h: h.tensor_tensor(out=xxT[:, :, 1:XW], in0=xnT[:, :, 0:XW - 1], in1=xnT[:, :, 1:XW], op=ALU.subtract), r=["xnT"], w=["xxT"])
        tokc = slice(64, 64 + MC)

        def proj_fm(S, ps_ap, Wt, Wm, sidx, cols, M):
            n = 0
            for (Wx, X, xb) in ((Wt, xnT, "xnT"), (Wm, xxT, "xxT")):
                for dc in range(8):
                    S.op("pe", lambda h, Wx=Wx, X=X, dc=dc, n=n: h.matmul(ps_ap, lhsT=Wx[:, sidx, dc, cols], rhs=X[:, dc, tokc], start=(n == 0), stop=(n == 15)),
                         r=["W4", "W4m", "LW", "LWm", xb], w=["ps_proj"])
                    n += 1
        for s in range(2):
            proj_fm(S, ps_proj[0:64, 0:MC], LW, LWm, s, slice(0, 64), 64)
            S.op("act", lambda h, s=s: h.activation(out=h1T[:, s, :], in_=ps_proj[0:64, 0:MC], func=(AF.Tanh if s == 0 else AF.Copy)), r=["ps_proj"], w=["h1T"])
        for j in range(NJ):
            n = 0
            for (Wi, X, xb) in ((W4, xnT, "xnT"), (W4m, xxT, "xxT")):
                for dc in range(8):
                    S.op("pe", lambda h, Wi=Wi, X=X, dc=dc, n=n, j=j: h.matmul(ps_tok[0:64, :], lhsT=X[:, dc, 64 + j * 64:128 + j * 64], rhs=Wi[:, 2, dc, :], start=(n == 0), stop=(n == 15)),
                         r=["W4", "W4m", xb], w=["ps_tok"])
                    n += 1
            S.op("act", lambda h, j=j: h.copy(out=vwin[:, j, :], in_=ps_tok[0:64, :]), r=["ps_tok"], w=[f"vwin{j}"])
            n = 0
            for (Wi, X, xb) in ((W4, xnT, "xnT"), (W4m, xxT, "xxT")):
                for dc in range(8):
                    S.op("pe", lambda h, Wi=Wi, X=X, dc=dc, n=n, j=j: h.matmul(ps_tok[0:64, :], lhsT=X[:, dc, 64 + j * 64:128 + j * 64], rhs=Wi[:, 3, dc, :], start=(n == 0), stop=(n == 15)),
                         r=["W4", "W4m", xb], w=["ps_tok"])
                    n += 1
            S.op("act", lambda h, j=j: h.activation(out=zs[:, j, :], in_=ps_tok[0:64, :], func=AF.Silu), r=["ps_tok"], w=["zs"])

        def head_pre(S, hd):
            gi = hd % G2
            hc = slice(hd * 64, (hd + 1) * 64)
            vp = lambda c: vec[:, hd, c:c + 1]
            proj_fm(S, ps_proj[0:64, 0:MC], W4, W4m, 0, hc, 64)
            S.op("act", lambda h: h.copy(out=r_f[:], in_=ps_proj[0:64, 0:MC]), r=["ps_proj"], w=["r_f"])
            proj_fm(S, ps_proj[0:64, 0:MC], W4, W4m, 1, hc, 64)
            S.op("act", lambda h: h.copy(out=k_f[:], in_=ps_proj[0:64, 0:MC]), r=["ps_proj"], w=["k_f"])
            S.op("pe", lambda h, hc=hc: h.matmul(ps_proj[0:64, 0:MC], lhsT=L2[:, 0, hc], rhs=h1T[:, 0, :], start=True, stop=True), r=["L2", "h1T"], w=["ps_proj"])
            S.op("act", lambda h, hd=hd: h.activation(out=sig[:], in_=ps_proj[0:64, 0:MC], func=AF.Sigmoid, bias=vec[:, hd, 0:1]), r=["ps_proj", "vec"], w=["sig"])
            S.op("pe", lambda h, hc=hc: h.matmul(ps_proj[0:64, 0:MC], lhsT=L2[:, 1, hc], rhs=h1T[:, 1, :], start=True, stop=True), r=["L2", "h1T"], w=["ps_proj"])
            S.op("act", lambda h, hd=hd: h.activation(out=alp[:], in_=ps_proj[0:64, 0:MC], func=AF.Sigmoid, bias=vec[:, hd, 1:2]), r=["ps_proj", "vec"], w=["alp"])
            S.op("dve", lambda h, hd=hd: h.tensor_scalar(out=kk[:], in0=k_f[:], scalar1=vec[:, hd, 2:3], scalar2=None, op0=ALU.mult), r=["k_f", "vec"], w=["kk"])
            S.op("pool", lambda h: h.tensor_tensor(out=t1[:], in0=kk[:], in1=kk[:], op=ALU.mult), r=["kk"], w=["t1"])
            S.op("pe", lambda h: h.matmul(ps_proj[0:64, 0:MC], lhsT=ones64[:], rhs=t1[:], start=True, stop=True), r=["ones64", "t1"], w=["ps_proj"])
            S.op("act", lambda h: h.activation(out=t2[:], in_=ps_proj[0:64, 0:MC], func=AF.Sqrt), r=["ps_proj"], w=["t2"])
            S.op("dve", lambda h: h.tensor_scalar(out=t2[:], in0=t2[:], scalar1=1e-12, scalar2=None, op0=ALU.max), r=["t2"], w=["t2"])
            S.op("dve", lambda h: h.reciprocal(out=t2[:], in_=t2[:]), r=["t2"], w=["t2"])
            S.op("dve", lambda h: h.tensor_tensor(out=kk[:], in0=kk[:], in1=t2[:], op=ALU.mult), r=["kk", "t2"], w=["kk"])
            S.op("dve", lambda h, hd=hd: h.tensor_scalar(out=t1[:], in0=alp[:], scalar1=1.0, scalar2=vec[:, hd, 3:4], op0=ALU.subtract, op1=ALU.mult), r=["alp", "vec"], w=["t1"])
            S.op("dve", lambda h: h.scalar_tensor_tensor(out=kmod[:], in0=t1[:], scalar=1.0, in1=k_f[:], op0=ALU.add, op1=ALU.mult), r=["t1", "k_f"], w=["kmod"])
            S.op("pool", lambda h: h.tensor_tensor(out=bal[:], in0=kk[:], in1=alp[:], op=ALU.mult), r=["kk", "alp"], w=["bal"])
            S.op("dve", lambda h: h.tensor_tensor_scan(out=cs[:], data0=resetm[:], data1=sig[:], initial=0.0, op0=ALU.mult, op1=ALU.add), r=["resetm", "sig"], w=["cs"])
            S.op("dve", lambda h: h.tensor_copy(out=cLs[:], in_=cs[:, 63::64]), r=["cs"], w=["cLs"])
            S.op("act", lambda h: h.activation(out=cLd[:], in_=cLs[:], func=AF.Exp, scale=-KAP), r=["cLs"], w=["cLd"])
            S.op("act", lambda h: h.activation(out=t1[:], in_=cs[:], func=AF.Exp, scale=-KAP), r=["cs"], w=["t1"])
            S.op("dve", lambda h, gi=gi: h.tensor_tensor(out=AR[gi][:, :, 64:128], in0=c3(r_f), in1=c3(t1), op=ALU.mult), r=["r_f", "t1"], w=[f"AR{gi}"])
            S.op("act", lambda h: h.activation(out=t2[:], in_=cs[:], func=AF.Exp, scale=KAP), r=["cs"], w=["t2"])
            S.op("dve", lambda h, gi=gi: h.tensor_tensor(out=BK[gi][:, :, 0:64], in0=c3(bal), in1=c3(t2), op=ALU.mult), r=["bal", "t2"], w=[f"BK{gi}"])
            S.op("pool", lambda h, gi=gi: h.tensor_tensor(out=BK[gi][:, :, 64:128], in0=c3(kmod), in1=c3(t2), op=ALU.mult), r=["kmod", "t2"], w=[f"BK{gi}"])
            S.op("pool", lambda h: h.tensor_tensor(out=t1[:], in0=cs[:], in1=sig[:], op=ALU.subtract), r=["cs", "sig"], w=["t1"])
            S.op("act", lambda h: h.activation(out=t1[:], in_=t1[:], func=AF.Exp, scale=-KAP), r=["t1"], w=["t1"])
            S.op("dve", lambda h, gi=gi: h.scalar_tensor_tensor(out=AR[gi][:, :, 0:64], in0=c3(kk), scalar=-1.0, in1=c3(t1), op0=ALU.mult, op1=ALU.mult),
                 r=["kk", "t1"], w=[f"AR{gi}"])
            S.op("dve", lambda h: h.tensor_tensor(out=c3(t2), in0=c3(cs), in1=cLs[:].unsqueeze(2).broadcast_to([64, NJ, 64]), op=ALU.subtract), r=["cs", "cLs"], w=["t2"])
            S.op("act", lambda h: h.activation(out=t2[:], in_=t2[:], func=AF.Exp, scale=KAP), r=["t2"], w=["t2"])
            S.op("dve", lambda h, gi=gi: h.tensor_tensor(out=BKe[gi][:, :, 0:64], in0=c3(bal), in1=c3(t2), op=ALU.mult), r=["bal", "t2"], w=[f"BKe{gi}"])
            S.op("pool", lambda h, gi=gi: h.tensor_tensor(out=BKe[gi][:, :, 64:128], in0=c3(kmod), in1=c3(t2), op=ALU.mult), r=["kmod", "t2"], w=[f"BKe{gi}"])
            S.op("dve", lambda h, gi=gi, hd=hd: h.scalar_tensor_tensor(out=rkrp[gi][:, :, :], in0=c3(r_f), scalar=vec[:, hd, 4:5], in1=c3(kmod), op0=ALU.mult, op1=ALU.mult),
                 r=["r_f", "kmod", "vec"], w=[f"rkrp{gi}"])
            def head_g(S, j):
                psn = ps_n if j == 0 else ps_proj
                psn_name = "ps_n" if j == 0 else "ps_proj"
                Nk_, NkT_, Pm_ = Nk[j], NkT[j], Pm[j]
                S.op("pe", lambda h, gi=gi, j=j: h.matmul(ps_g[0:64, 0:128], lhsT=BK[gi][:, j, 0:64], rhs=AR[gi][:, j, :], start=True, stop=True), r=[f"BK{gi}", f"AR{gi}"], w=["ps_g"])
                S.op("pe", lambda h, gi=gi, j=j: h.matmul(ps_g[0:64, 128:256], lhsT=BK[gi][:, j, 64:128], rhs=AR[gi][:, j, :], start=True, stop=True), r=[f"BK{gi}", f"AR{gi}"], w=["ps_g"])
                S.op("pe", lambda h, gi=gi, j=j: h.matmul(ps_g[0:64, 256:320], lhsT=AR[gi][:, j, 0:64], rhs=BK[gi][:, j, 0:64], start=True, stop=True), r=[f"BK{gi}", f"AR{gi}"], w=["ps_g"])
                S.op("dve", lambda h, gi=gi, j=j: h.tensor_tensor(out=Gm[gi][:, j, 0:128], in0=ps_g[0:64, 0:128], in1=maskG[0:64, :], op=ALU.mult), r=["ps_g", "maskG"], w=[f"Gm{gi}"])
                S.op("dve", lambda h, gi=gi, j=j: h.tensor_tensor(out=Gm[gi][:, j, 128:256], in0=ps_g[0:64, 128:256], in1=maskG[0:64, :], op=ALU.mult), r=["ps_g", "maskG"], w=[f"Gm{gi}"])
                S.op("dve", lambda h: h.tensor_tensor(out=NkT_[0][:], in0=ps_g[0:64, 256:320], in1=maskNT[:], op=ALU.mult), r=["ps_g", "maskNT"], w=[f"NkT{j}_0"])
                S.op("pool", lambda h, gi=gi, j=j: h.tensor_copy(out=Nk_[0][:], in_=Gm[gi][:, j, 0:64]), r=[f"Gm{gi}"], w=[f"Nk{j}_0"])
                S.op("pool", lambda h, gi=gi, j=j: h.tensor_tensor(out=Pm_[0][:], in0=Gm[gi][:, j, 0:64], in1=identf[0:64, 0:64], op=ALU.add), r=[f"Gm{gi}", "identf"], w=[f"Pm{j}_0"])
                for q in range(2):
                    S.op("pe", lambda h, gi=gi, j=j, q=q: h.transpose(out=ps_g[0:64, 320 + q * 64:384 + q * 64], in_=BKe[gi][:, j, q * 64:(q + 1) * 64], identity=identf[0:64, 0:64]), r=[f"BKe{gi}", "identf"], w=["ps_g"])
                S.op("act", lambda h, gi=gi, j=j: h.copy(out=BKeT[gi][:, j, :], in_=ps_g[0:64, 320:448]), r=["ps_g"], w=[f"BKeT{gi}"])
                S.op("pool", lambda h, gi=gi, j=j: h.tensor_scalar(out=dcL[gi][:, j, :], in0=identf[0:64, 0:64], scalar1=cLd[:, j:j + 1], scalar2=None, op0=ALU.mult), r=["identf", "cLd"], w=[f"dcL{gi}"])
                S.op("pe", lambda h, gi=gi, j=j: h.matmul(ps_g[0:64, 448 + j:449 + j], lhsT=rkrp[gi][:, j, :], rhs=ones64[:, 0:1], start=True, stop=True), r=[f"rkrp{gi}", "ones64"], w=["ps_g"])
                S.op("act", lambda h, gi=gi, j=j: h.copy(out=bon[gi][:, j:j + 1], in_=ps_g[0:64, 448 + j:449 + j]), r=["ps_g"], w=[f"bon{gi}"])
            def head_n(S, j):
                psn = ps_n if j == 0 else ps_proj
                psn_name = "ps_n" if j == 0 else "ps_proj"
                Nk_, NkT_, Pm_ = Nk[j], NkT[j], Pm[j]
                cur = 0
                for sidx in range(5):
                    nx = 1 - cur
                    last = (sidx == 4)
                    S.op("pe", lambda h, cur=cur: h.matmul(psn[0:64, 0:64], lhsT=Nk_[cur][:], rhs=NkT_[cur][:], start=True, stop=True), r=[f"Nk{j}_{cur}", f"NkT{j}_{cur}"], w=[psn_name])
                    S.op("act", lambda h, nx=nx: h.copy(out=NkT_[nx][:], in_=psn[0:64, 0:64]), r=[psn_name], w=[f"NkT{j}_{nx}"])
                    if not last:
                        S.op("pe", lambda h, cur=cur: h.matmul(psn[0:64, 64:128], lhsT=NkT_[cur][:], rhs=Nk_[cur][:], start=True, stop=True), r=[f"Nk{j}_{cur}", f"NkT{j}_{cur}"], w=[psn_name])
                        S.op("act", lambda h, nx=nx: h.copy(out=Nk_[nx][:], in_=psn[0:64, 64:128]), r=[psn_name], w=[f"Nk{j}_{nx}"])
                    S.op("pe", lambda h, cur=cur, nx=nx: h.matmul(psn[0:64, 128:192], lhsT=NkT_[nx][:], rhs=Pm_[cur][:], start=True, stop=True), r=[f"NkT{j}_{nx}", f"Pm{j}_{cur}"], w=[psn_name])
                    if last:
                        S.op("dve", lambda h, cur=cur, gi=gi, j=j: h.tensor_tensor(out=Tm[gi][:, j, :], in0=psn[0:64, 128:192], in1=Pm_[cur][:], op=ALU.add), r=[psn_name, f"Pm{j}_{cur}"], w=[f"Tm{gi}"])
                    else:
                        S.op("dve", lambda h, cur=cur, nx=nx: h.tensor_tensor(out=Pm_[nx][:], in0=psn[0:64, 128:192], in1=Pm_[cur][:], op=ALU.add), r=[psn_name, f"Pm{j}_{cur}"], w=[f"Pm{j}_{nx}"])
                    cur = nx
            for j in range(NJ):
                head_g(S, j)
            sj = [Stream() for _ in range(NJ)]
            for j in range(NJ):
                head_n(sj[j], j)
            merge_streams(S, sj)
            for j in range(NJ):
                S.op("dve", lambda h, gi=gi, j=j: h.tensor_scalar(out=Dg[gi][:, j, :], in0=identf[0:64, 0:64], scalar1=bon[gi][:, j:j + 1], scalar2=None, op0=ALU.mult),
                     r=["identf", f"bon{gi}"], w=[f"Dg{gi}"])
        def head_rec(S, hd):
            gi = hd % G2
            hc = slice(hd * 64, (hd + 1) * 64)
            for j in range(NJ):
                gj = mc * NJ + j
                s_in = ST[hd][gj % 2]
                s_out = ST[hd][(gj + 1) % 2]
                sin_n = f"ST{hd}_{gj % 2}"
                sout_n = f"ST{hd}_{(gj + 1) % 2}"
                wi = gj % 2
                VT = vwin[:, j, hc]
                UT = uT[:, j, hc]
                vn = f"vwin{j}"
                un = f"uT{j}_{hd}"
                S.op("pe", lambda h, gi=gi, j=j, VT=VT, hc=hc: h.matmul(ps_bv[0:64, hc], lhsT=Dg[gi][:, j, :], rhs=VT, start=True, stop=True), r=[f"Dg{gi}", vn], w=["ps_bv"])
                S.op("act", lambda h, j=j, hc=hc: h.copy(out=bv_all[:, j, hc], in_=ps_bv[0:64, hc]), r=["ps_bv"], w=["bv_all"])
                S.op("pe", lambda h, gi=gi, j=j, s_in=s_in: h.matmul(ps_rec[0:64, 0:64], lhsT=AR[gi][:, j, 0:64], rhs=s_in[:], start=True, stop=False), r=[f"AR{gi}", sin_n], w=["ps_rec"])
                S.op("pe", lambda h, gi=gi, j=j, VT=VT: h.matmul(ps_rec[0:64, 0:64], lhsT=Gm[gi][:, j, 128:192], rhs=VT, start=False, stop=True), r=[f"Gm{gi}", vn], w=["ps_rec"])
                S.op("act", lambda h, wi=wi: h.copy(out=WT[wi][:], in_=ps_rec[0:64, 0:64]), r=["ps_rec"], w=[f"WT{wi}"])
                S.op("pe", lambda h, gi=gi, j=j, wi=wi: h.matmul(ps_rec[0:64, 64:128], lhsT=Tm[gi][:, j, :], rhs=WT[wi][:], start=True, stop=True), r=[f"Tm{gi}", f"WT{wi}"], w=["ps_rec"])
                S.op("act", lambda h, UT=UT: h.copy(out=UT, in_=ps_rec[0:64, 64:128]), r=["ps_rec"], w=[un])
                S.op("pe", lambda h, gi=gi, j=j, s_in=s_in, hc=hc: h.matmul(ps_y[0:64, hc], lhsT=AR[gi][:, j, 64:128], rhs=s_in[:], start=True, stop=False), r=[f"AR{gi}", sin_n], w=["ps_y"])
                S.op("pe", lambda h, gi=gi, j=j, hc=hc, UT=UT: h.matmul(ps_y[0:64, hc], lhsT=Gm[gi][:, j, 64:128], rhs=UT, start=False, stop=False), r=[f"Gm{gi}", un], w=["ps_y"])
                S.op("pe", lambda h, gi=gi, j=j, hc=hc, VT=VT: h.matmul(ps_y[0:64, hc], lhsT=Gm[gi][:, j, 192:256], rhs=VT, start=False, stop=True), r=[f"Gm{gi}", vn], w=["ps_y"])
                S.op("dve", lambda h, j=j, hc=hc: h.tensor_copy(out=y_all[:, j, hc], in_=ps_y[0:64, hc]), r=["ps_y"], w=["y_all"])
                S.op("pe", lambda h, gi=gi, j=j, s_in=s_in: h.matmul(ps_rec[0:64, 128:192], lhsT=dcL[gi][:, j, :], rhs=s_in[:], start=True, stop=False), r=[f"dcL{gi}", sin_n], w=["ps_rec"])
                S.op("pe", lambda h, gi=gi, j=j, UT=UT: h.matmul(ps_rec[0:64, 128:192], lhsT=BKeT[gi][:, j, 0:64], rhs=UT, start=False, stop=False), r=[f"BKeT{gi}", un], w=["ps_rec"])
                S.op("pe", lambda h, gi=gi, j=j, VT=VT: h.matmul(ps_rec[0:64, 128:192], lhsT=BKeT[gi][:, j, 64:128], rhs=VT, start=False, stop=True), r=[f"BKeT{gi}", vn], w=["ps_rec"])
                S.op("act", lambda h, s_out=s_out: h.copy(out=s_out[:], in_=ps_rec[0:64, 128:192]), r=["ps_rec"], w=[sout_n])
        prev = None
        for hd in range(8):
            sa = Stream()
            head_pre(sa, hd)
            streams = [sa]
            if prev is not None:
                sb = Stream()
                head_rec(sb, prev)
                streams.append(sb)
            merge_streams(S, streams)
            prev = hd
        sb = Stream()
        head_rec(sb, prev)
        merge_streams(S, [sb])
        for j in range(NJ):
            y3 = y_all[:, j, :].rearrange("p (h v) -> p h v", h=8)
            S.op("dve", lambda h, y3=y3: h.tensor_reduce(out=gst[:, 0, :], in_=y3, axis=AX.X, op=ALU.add), r=["y_all"], w=["gst"])
            S.op("act", lambda h, j=j: h.activation(out=yn[:], in_=y_all[:, j, :], func=AF.Square), r=["y_all"], w=["yn"])
            S.op("dve", lambda h: h.tensor_reduce(out=gst[:, 1, :], in_=yn[:].rearrange("p (h v) -> p h v", h=8), axis=AX.X, op=ALU.add), r=["yn"], w=["gst"])
            S.op("dve", lambda h: h.tensor_scalar(out=gst[:, 0, :], in0=gst[:, 0, :], scalar1=1.0 / 64, scalar2=None, op0=ALU.mult), r=["gst"], w=["gst"])
            S.op("dve", lambda h: h.tensor_tensor(out=gst[:, 2, :], in0=gst[:, 0, :], in1=gst[:, 0, :], op=ALU.mult), r=["gst"], w=["gst"])
            S.op("dve", lambda h: h.scalar_tensor_tensor(out=gst[:, 1, :], in0=gst[:, 1, :], scalar=1.0 / 64, in1=gst[:, 2, :], op0=ALU.mult, op1=ALU.subtract), r=["gst"], w=["gst"])
            S.op("act", lambda h: h.activation(out=gst[:, 1, :], in_=gst[:, 1, :], func=AF.Sqrt, bias=GN_EPS), r=["gst"], w=["gst"])
            S.op("dve", lambda h: h.reciprocal(out=gst[:, 1, :], in_=gst[:, 1, :]), r=["gst"], w=["gst"])
            for hd in range(8):
                hc = slice(hd * 64, (hd + 1) * 64)
                S.op("dve", lambda h, j=j, hd=hd, hc=hc: h.tensor_scalar(out=yn[:, hc], in0=y_all[:, j, hc], scalar1=gst[:, 0, hd:hd + 1], scalar2=gst[:, 1, hd:hd + 1],
                                                                      op0=ALU.subtract, op1=ALU.mult), r=["y_all", "gst"], w=["yn"])
            S.op("pool", lambda h: h.tensor_tensor(out=yn[:], in0=yn[:], in1=lnw[:], op=ALU.mult), r=["yn", "lnw"], w=["yn"])
            S.op("pool", lambda h: h.tensor_tensor(out=yn[:], in0=yn[:], in1=lnb[:], op=ALU.add), r=["yn", "lnb"], w=["yn"])
            S.op("pool", lambda h, j=j: h.tensor_tensor(out=yn[:], in0=yn[:], in1=bv_all[:, j, :], op=ALU.add), r=["yn", "bv_all"], w=["yn"])
            S.op("dve", lambda h, j=j: h.tensor_tensor(out=yz[:], in0=yn[:], in1=zs[:, j, :], op=ALU.mult), r=["yn", "zs"], w=["yz"])
            for c in range(4):
                S.op("pe", lambda h, c=c: h.transpose(out=ps_tr[:, c * 64:(c + 1) * 64], in_=yz[:, c * 128:(c + 1) * 128], identity=ident[0:64, 0:64]), r=["yz", "ident"], w=["ps_tr"])
            S.op("act", lambda h: h.copy(out=yzT[:], in_=ps_tr[:, 0:256].rearrange("p (c n) -> p c n", c=4)), r=["ps_tr"], w=["yzT"])
            for hf in range(2):
                for c in range(4):
                    S.op("pe", lambda h, hf=hf, c=c: h.matmul(ps_tok[0:64, :], lhsT=yzT[:, c, :], rhs=wo[:, c, hf * 512:(hf + 1) * 512], start=(c == 0), stop=(c == 3)), r=["yzT", "wo"], w=["ps_tok"])
                S.op("act", lambda h, hf=hf: h.copy(out=pt[:, hf * 512:(hf + 1) * 512], in_=ps_tok[0:64, :]), r=["ps_tok"], w=["pt"])
            rows = slice(T0 + j * 64, T0 + (j + 1) * 64)
            S.op("sp", lambda h, rows=rows: h.dma_start(out=p_out[rows, :], in_=pt[:]), r=["pt"], dma=True)
    return nc, es, S


def prep_B(z, half, MC=256):
    d = {}
    w_in = z['b_w_in'][0]
    own = slice(half * 512, (half + 1) * 512)
    d['w4'] = np.ascontiguousarray(np.stack([w_in[:, s * 1024:(s + 1) * 1024][:, own] for s in range(4)]))
    d['lw'] = np.ascontiguousarray(np.stack([z['b_w1'][0], z['b_a1'][0]]))
    d['l2'] = np.ascontiguousarray(np.stack([z['b_w2'][0][:, own], z['b_a2'][0][:, own]]))
    d['wo'] = np.ascontiguousarray(z['b_w_out'][0][own, :])
    mu = z['b_mu'][0]
    d['muT'] = np.ascontiguousarray(mu.reshape(6, 8, 128).transpose(2, 0, 1))
    vecs = np.zeros((64, 8, 8), np.float32)
    def fm(v):
        return v[own].reshape(8, 64).T
    vecs[:, :, 0] = fm(z['b_w0'][0]); vecs[:, :, 1] = fm(z['b_a0'][0]); vecs[:, :, 2] = fm(z['b_k_k'][0]); vecs[:, :, 3] = fm(z['b_k_a'][0])
    vecs[:, :, 4] = fm(z['b_r_k'][0].reshape(-1))
    d['vecs'] = vecs
    d['lnw'] = np.ascontiguousarray(z['b_lnx_w'][0][own][None, :])
    d['lnb'] = np.ascontiguousarray(z['b_lnx_b'][0][own][None, :])
    j = np.arange(64)[:, None]; i = np.arange(64)[None, :]
    strict = (j < i).astype(np.float32); incl = (j <= i).astype(np.float32)
    row = np.concatenate([strict, incl], 1)
    d['maskG'] = np.ascontiguousarray(np.concatenate([row, row], 0))
    d['maskNT'] = np.ascontiguousarray(strict.T)
    rm = np.ones((64, MC), np.float32); rm[:, ::64] = 0.0
    d['resetm'] = rm
    d['g'] = z['norm_g'][1:2].copy()
    d['ident'] = np.eye(128, dtype=np.float32)
    return d


NEGM = -30000.0


def build_C(nsrc=1, debug=False):
    T = CFG.T
    NT = T // 128
    NCMP = T // 16 - 1
    NKT = (NCMP + 127) // 128
    nc = get_nc()
    es = ExitStack()
    S = get_sched(nc, es)
    srcs = [dram_in(nc, f"xin{k}", [T, D]) for k in range(nsrc)]
    xs_out = dram_out(nc, "xs", [T, D]) if nsrc > 1 else None
    g_row = dram_in(nc, "g", [1, D])
    ident_d = dram_in(nc, "ident", [128, 128])
    wq_d = dram_in(nc, "wq", [D, 512])
    wkv_d = dram_in(nc, "wkv", [D, 6, 128])
    wg_d = dram_in(nc, "wg", [D, 24])
    wz_d = dram_in(nc, "wz", [D, 512])
    wo_d = dram_in(nc, "wo", [512, D])
    w1_d = dram_in(nc, "w1", [2, 64, 32, 128])
    w2_d = dram_in(nc, "w2", [2, 128, 64])
    pos_d = dram_in(nc, "posT", [2, 64, 32])
    bias_d = dram_in(nc, "biasT", [128, 4, 1024])
    F4_d = dram_in(nc, "F4", [512, 512])
    ka_d = dram_in(nc, "keepadd", [NT, 128, 128])
    E_d = dram_in(nc, "E", [64, NT, 128])
    ov_d = dram_in(nc, "ovl", [128, 2, 64])
    p_out = dram_out(nc, "p", [T, D])

    ps_tr = mk(nc, es, "ps_tr", [128, 1024], BF16, psum=True)
    ps_q = mk(nc, es, "ps_q", [128, 512], F32, psum=True)
    ps_z = mk(nc, es, "ps_z", [128, 512], F32, psum=True)
    ps_s = [mk(nc, es, f"ps_s{k}", [128, 512], F32, psum=True) for k in range(2)]
    po_c = mk(nc, es, "po_c", [128, 4, 128], F32, psum=True)
    po_s = mk(nc, es, "po_s", [128, 4, 128], F32, psum=True)
    po_w = mk(nc, es, "po_w", [128, 4, 128], F32, psum=True)
    p1 = P1(S, nc, es, srcs, xs_out, g_row, ident_d, ps_tr)
    ident = p1.ident
    xnTt = [mk(nc, es, f"xnTt{b}", [128, 8, 128], BF16) for b in range(2)]

    wq = mk(nc, es, "wq_s", [128, 8, 512], BF16)
    wkv = mk(nc, es, "wkv_s", [128, 8, 6, 128], BF16)
    wg = mk(nc, es, "wg_s", [128, 8, 24], BF16)
    wz = mk(nc, es, "wz_s", [128, 8, 512], BF16)
    wo = mk(nc, es, "wo_s", [128, 4, D], BF16)
    w1 = mk(nc, es, "w1_s", [64, 2, 32, 128], BF16)
    w2 = mk(nc, es, "w2_s", [128, 2, 64], BF16)
    posT = mk(nc, es, "posT_s", [64, 2, 32], BF16)
    biasT = mk(nc, es, "biasT_s", [128, 4, 1024], BF16)
    Em = mk(nc, es, "E_s", [64, NT, 128], BF16)
    S.op("pool", lambda h: h.dma_start(out=wq[:], in_=wq_d.rearrange("(c p) n -> p c n", p=128)), w=["wq"], dma=True)
    S.op("pool", lambda h: h.dma_start(out=wkv[:], in_=wkv_d.rearrange("(c p) s n -> p c s n", p=128)), w=["wkv"], dma=True)
    S.op("pool", lambda h: h.dma_start(out=wg[:], in_=wg_d.rearrange("(c p) n -> p c n", p=128)), w=["wg"], dma=True)
    S.op("pool", lambda h: h.dma_start(out=wz[:], in_=wz_d.rearrange("(c p) n -> p c n", p=128)), w=["wz"], dma=True)
    S.op("pool", lambda h: h.dma_start(out=wo[:], in_=wo_d.rearrange("(c p) n -> p c n", p=128)), w=["wo"], dma=True)
    S.op("pool", lambda h: h.dma_start(out=w1[:], in_=w1_d.rearrange("s d l h -> d s l h")), w=["w1"], dma=True)
    S.op("pool", lambda h: h.dma_start(out=w2[:], in_=w2_d.rearrange("s h d -> h s d")), w=["w2"], dma=True)
    S.op("pool", lambda h: h.dma_start(out=posT[:], in_=pos_d.rearrange("s d l -> d s l")), w=["posT"], dma=True)
    S.op("pool", lambda h: h.dma_start(out=biasT[:], in_=bias_d), w=["biasT"], dma=True)
    S.op("pool", lambda h: h.dma_start(out=Em[:], in_=E_d), w=["E"], dma=True)

    kvT = mk(nc, es, "kvT", [64, 2, 2, T], BF16)
    roll = mk(nc, es, "roll", [64, 2, 2, 144], BF16)
    vau = mk(nc, es, "vau", [128, NT, 2, 2, 65], BF16)
    S.op("dve", lambda h: h.memset(vau[:, :, :, :, 64:65], 1.0), w=["vau_ones"])
    S.op("dve", lambda h: h.memset(roll[:], 0.0), w=["roll"])
    kcmpT = mk(nc, es, "kcmpT", [64, 2, 256], BF16)
    vcau = mk(nc, es, "vcau", [128, 2, 2, 65], BF16)
    ovl = mk(nc, es, "ovl_s", [128, 2, 64], BF16)
    hidn = mk(nc, es, "hidn", [128, 4, 8], BF16)
    hidv = mk(nc, es, "hidv", [128, 2, 256], BF16)
    pbias = mk(nc, es, "pbias", [128, 2], F32)
    S.op("dve", lambda h: h.memset(kcmpT[:], 0.0), w=["kcmpT"])
    S.op("dve", lambda h: h.memset(vcau[:], 0.0), w=["vcau"])
    S.op("dve", lambda h: h.memset(vcau[:, :, :, 64:65], 1.0), r=["vcau"], w=["vcau"])
    S.op("dve", lambda h: h.memset(hidv[:], 0.0), w=["hidv"])
    S.op("pool", lambda h: h.dma_start(out=ovl[:], in_=ov_d), w=["ovl"], dma=True)
    for s in range(2):
        for l in range(32):
            S.op("pe", lambda h, s=s, l=l: h.matmul(ps_z[:, s:s + 1], lhsT=w1[:, s, l, :], rhs=posT[:, s, l:l + 1], start=(l == 0), stop=(l == 31)),
                 r=["w1", "posT"], w=["ps_z"])
        S.op("act", lambda h, s=s: h.copy(out=pbias[:, s:s + 1], in_=ps_z[:, s:s + 1]), r=["ps_z"], w=["pbias"])

    NB = 2
    qT = [mk(nc, es, f"qT{b}", [64, 2, 4, 128], BF16) for b in range(NB)]
    zs = [mk(nc, es, f"zs{b}", [128, 512], BF16) for b in range(NB)]
    gt = [mk(nc, es, f"gt{b}", [128, 24], F32) for b in range(NB)]
    NP = 4
    pT = [mk(nc, es, f"pT{b}", [128, 512], BF16) for b in range(NP)]
    F4t = [mk(nc, es, f"F4t{b}", [128, 512], BF16) for b in range(2)]
    ka = [mk(nc, es, f"ka{b}", [128, 128], F32) for b in range(NB)]
    imp = mk(nc, es, "imp", [128, 64], F32)
    imp2 = mk(nc, es, "imp2", [128, 64], F32)
    m8 = mk(nc, es, "m8", [128, 16], F32)
    nsel = mk(nc, es, "nsel", [128, 64], BF16)
    nselT = mk(nc, es, "nselT", [64, 4, 128], BF16)
    rden = mk(nc, es, "rden", [128, 3, 4], F32)
    cf = mk(nc, es, "cf", [128, 3, 4], F32)
    y = mk(nc, es, "y", [128, 512], F32)
    yz = [mk(nc, es, f"yz{b}", [128, 512], BF16) for b in range(NB)]
    yzT = [mk(nc, es, f"yzT{b}", [128, 4, 128], BF16) for b in range(NB)]
    pt = [mk(nc, es, f"pt{b}", [128, D], F32) for b in range(NB)]
    pcount = [0]
    scount = [0]

    def st_tile(g, b, lhsT_ap, lhs_bufs, extra, rhs_aug, rhs_bufs, po, first, last, ncol=65):
        si = scount[0] % 2
        scount[0] += 1
        pi = pcount[0] % NP
        pcount[0] += 1
        n_extra = len(extra)
        S.op("pe", lambda h: h.matmul(ps_s[si][:, :], lhsT=lhsT_ap, rhs=qT[b][:, g, :, :].rearrange("p r n -> p (r n)"), start=True, stop=(n_extra == 0)),
             r=lhs_bufs + [f"qT{b}_{g}"], w=[f"ps_s{si}"])
        for j, (el, er, ebufs) in enumerate(extra):
            S.op("pe", lambda h, el=el, er=er, j=j: h.matmul(ps_s[si][:, :], lhsT=el, rhs=er, start=False, stop=(j == n_extra - 1)),
                 r=ebufs, w=[f"ps_s{si}"])
        S.op("act", lambda h: h.activation(out=pT[pi][:], in_=ps_s[si][:, :], func=AF.Exp), r=[f"ps_s{si}"], w=[f"pT{pi}"])
        return pi

    for qt in range(NT):
        b = qt % NB
        tq = slice(qt * 128, (qt + 1) * 128)
        xk = f"xnTt{qt % 2}"
        xn = xnTt[qt % 2]
        S.op("sp", lambda h, b=b, qt=qt: h.dma_start(out=ka[b][:], in_=ka_d[qt]), w=[f"ka{b}"], dma=True)
        p1.tile(qt, xn[:, :, :], xk)
        if qt > 0:
            S.op("pool", lambda h: h.tensor_copy(out=roll[:, :, :, 0:16], in_=roll[:, :, :, 128:144]), r=["roll"], w=["roll"])
        for grp, (wss, psx, nm) in enumerate((((0, 1), ps_q, "ps_q"), ((2, 4), ps_z, "ps_z"))):
            for si, ws in enumerate(wss):
                for g in range(2):
                    c0 = (si * 2 + g) * 128
                    for dc in range(8):
                        S.op("pe", lambda h, ws=ws, g=g, dc=dc, c0=c0, psx=psx, xn=xn: h.matmul(psx[0:64, c0:c0 + 128], lhsT=wkv[:, dc, ws, g * 64:(g + 1) * 64], rhs=xn[:, dc, :],
                                                                                     start=(dc == 0), stop=(dc == 7)), r=["wkv", xk], w=[nm])
            if grp == 0:
                S.op("act", lambda h, psx=psx: h.copy(out=roll[:, :, :, 16:144], in_=psx[0:64, :].rearrange("p (s g n) -> p s g n", s=2, g=2)), r=[nm], w=["roll"])
            else:
                S.op("act", lambda h, psx=psx, tq=tq: h.copy(out=kvT[:, :, :, tq], in_=psx[0:64, :].rearrange("p (s g n) -> p s g n", s=2, g=2)), r=[nm], w=[f"kvT_{qt // 4}"])
        for jj, ws in enumerate((3, 5)):
            for dc in range(8):
                S.op("pe", lambda h, dc=dc, ws=ws, jj=jj, xn=xn: h.matmul(ps_z[:, jj * 128:(jj + 1) * 128], lhsT=xn[:, dc, :], rhs=wkv[:, dc, ws, :],
                                                                     start=(dc == 0), stop=(dc == 7)), r=["wkv", xk], w=["ps_z"])
        S.op("dve", lambda h, qt=qt: h.tensor_copy(out=vau[:, qt, :, :, 0:64], in_=ps_z[:, 0:256].rearrange("p (j g d) -> p j g d", j=2, g=2)),
             r=["ps_z"], w=[f"vau{qt}"])
        m0 = 1 if qt == 0 else 0
        nb = 8 - m0
        n0 = 8 * qt - 1 + m0
        for s in range(2):
            for g in range(2):
                c0 = (s * 2 + g) * 8
                for l in range(32):
                    S.op("pe", lambda h, s=s, g=g, l=l, c0=c0, nb=nb, m0=m0: h.matmul(ps_q[:, c0:c0 + nb], lhsT=w1[:, s, l, :], rhs=roll[:, s, g, l + 16 * m0:l + 16 * 7 + 1:16],
                                                                         start=(l == 0), stop=(l == 31)), r=["w1", "roll"], w=["ps_q"])
        for s in range(2):
            S.op("act", lambda h, s=s, nb=nb: h.activation(out=hidn[:, s * 2:s * 2 + 2, 0:nb], in_=ps_q[:, s * 16:s * 16 + 16].rearrange("p (g n) -> p g n", g=2)[:, :, 0:nb],
                                                    func=AF.Silu, bias=pbias[:, s:s + 1]), r=["ps_q", "pbias"], w=["hidn"])
        for g in range(2):
            S.op("pe", lambda h, g=g, nb=nb: h.matmul(ps_z[0:64, g * 8:g * 8 + nb], lhsT=w2[:, 0, :], rhs=hidn[:, g, 0:nb], start=True, stop=True), r=["w2", "hidn"], w=["ps_z"])
        S.op("act", lambda h, nb=nb, n0=n0: h.copy(out=kcmpT[:, :, n0:n0 + nb], in_=ps_z[0:64, 0:16].rearrange("p (g n) -> p g n", g=2)[:, :, 0:nb]), r=["ps_z"], w=["kcmpT"])
        S.op("pool", lambda h, nb=nb, n0=n0: h.tensor_copy(out=hidv[:, :, n0:n0 + nb], in_=hidn[:, 2:4, 0:nb]), r=["hidn"], w=["hidv"])
        for nt in sorted(set([n0 // 128, (n0 + nb - 1) // 128])):
            for g in range(2):
                S.op("pe", lambda h, g=g, nt=nt: h.matmul(ps_z[:, 128 + g * 64:192 + g * 64], lhsT=hidv[:, g, nt * 128:(nt + 1) * 128], rhs=w2[:, 1, :], start=True, stop=True),
                     r=["w2", "hidv"], w=["ps_z"])
            S.op("act", lambda h, nt=nt: h.copy(out=vcau[:, nt, :, 0:64], in_=ps_z[:, 128:256].rearrange("p (g d) -> p g d", g=2)), r=["ps_z"], w=["vcau"])
        for g in range(2):
            for r in range(4):
                for dc in range(8):
                    col = (g * 4 + r) * 64
                    S.op("pe", lambda h, g=g, r=r, dc=dc, col=col, xn=xn: h.matmul(ps_q[0:64, r * 128:(r + 1) * 128], lhsT=wq[:, dc, col:col + 64],
                                                                               rhs=xn[:, dc, :], start=(dc == 0), stop=(dc == 7)),
                         r=["wq", xk], w=["ps_q"])
            S.op("act", lambda h, g=g, b=b: h.activation(out=qT[b][:, g, :, :], in_=ps_q[0:64, :].rearrange("p (r n) -> p r n", r=4),
                                                        func=AF.Copy, scale=0.125), r=["ps_q"], w=[f"qT{b}_{g}"])
        for dc in range(8):
            S.op("pe", lambda h, dc=dc, xn=xn: h.matmul(ps_z[:, :], lhsT=xn[:, dc, :], rhs=wz[:, dc, :], start=(dc == 0), stop=(dc == 7)),
                 r=["wz", xk], w=["ps_z"])
        S.op("act", lambda h, b=b: h.activation(out=zs[b][:], in_=ps_z[:, :], func=AF.Silu), r=["ps_z"], w=[f"zs{b}"])
        for dc in range(8):
            S.op("pe", lambda h, dc=dc, xn=xn: h.matmul(ps_z[:, 0:24], lhsT=xn[:, dc, :], rhs=wg[:, dc, :], start=(dc == 0), stop=(dc == 7)),
                 r=["wg", xk], w=["ps_z"])
        S.op("act", lambda h, b=b: h.activation(out=gt[b][:], in_=ps_z[:, 0:24], func=AF.Sigmoid), r=["ps_z"], w=[f"gt{b}"])
        cnts = []
        for nt in range(NKT):
            mmax = nt * 128 + 127 - 8 * qt
            mmin = nt * 128 - 8 * qt
            if mmin > 6:
                continue
            masked = mmax > -2
            cnts.append((nt, masked))
        for (nt, masked) in cnts:
            if masked:
                j0 = 128 * nt - 8 * qt + 248
                S.op("pool", lambda h, nt=nt, j0=j0: h.dma_start(out=F4t[nt][:], in_=F4_d[j0:j0 + 128, :]), w=[f"F4t{nt}"], dma=True)
        for g in range(2):
            pis = []
            for (nt, masked) in cnts:
                extra = [(ident[:], F4t[nt][:], ["p1id", f"F4t{nt}"])] if masked else []
                pi = st_tile(g, b, kcmpT[:, g, nt * 128:(nt + 1) * 128], ["kcmpT"], extra, None, None, None, None, None)
                pis.append((nt, pi))
            for r in range(4):
                for j, (nt, pi) in enumerate(pis):
                    S.op("pe", lambda h, g=g, r=r, nt=nt, pi=pi, j=j: h.matmul(po_c[:, r, 0:65], lhsT=pT[pi][:, r * 128:(r + 1) * 128], rhs=vcau[:, nt, g, :],
                                                                             start=(j == 0), stop=(j == len(pis) - 1)), r=[f"pT{pi}", "vcau"], w=["po_c"])
            for r in range(4):
                for j, (nt, pi) in enumerate(pis):
                    S.op("pe", lambda h, g=g, r=r, nt=nt, pi=pi, j=j: h.matmul(po_w[:, r, 0:64], lhsT=pT[pi][:, r * 128:(r + 1) * 128], rhs=ovl[:, nt, :],
                                                                             start=(j == 0), stop=(j == len(pis) - 1)), r=[f"pT{pi}", "ovl"], w=["po_w"])
            S.op("dve", lambda h: h.tensor_scalar(out=rden[:, 0, :], in0=po_c[:, :, 64], scalar1=1e-30, scalar2=None, op0=ALU.add), r=["po_c"], w=["rden0"])
            S.op("dve", lambda h: h.reciprocal(out=rden[:, 0, :], in_=rden[:, 0, :]), r=["rden0"], w=["rden0"])
            S.op("dve", lambda h: h.tensor_scalar(out=imp[:], in0=po_w[:, 0, 0:64], scalar1=rden[:, 0, 0:1], scalar2=None, op0=ALU.mult), r=["po_w", "rden0"], w=["imp"])
            for r in range(1, 4):
                S.op("dve", lambda h, r=r: h.scalar_tensor_tensor(out=imp[:], in0=po_w[:, r, 0:64], scalar=rden[:, 0, r:r + 1], in1=imp[:], op0=ALU.mult, op1=ALU.add),
                     r=["po_w", "rden0", "imp"], w=["imp"])
            S.op("dve", lambda h, b=b: h.tensor_tensor(out=imp[:], in0=imp[:], in1=ka[b][:, 0:64], op=ALU.mult), r=["imp", f"ka{b}"], w=["imp"])
            S.op("dve", lambda h, b=b: h.tensor_tensor(out=imp[:], in0=imp[:], in1=ka[b][:, 64:128], op=ALU.add), r=["imp", f"ka{b}"], w=["imp"])
            S.op("dve", lambda h: h.max(out=m8[:, 0:8], in_=imp[:]), r=["imp"], w=["m8"])
            S.op("dve", lambda h: h.match_replace(out=imp2[:], in_to_replace=m8[:, 0:8], in_values=imp[:], imm_value=-3.0e38), r=["imp", "m8"], w=["imp2"])
            S.op("dve", lambda h: h.max(out=m8[:, 8:16], in_=imp2[:]), r=["imp2"], w=["m8"])
            S.op("dve", lambda h: h.tensor_scalar(out=imp2[:], in0=imp[:], scalar1=m8[:, 15:16], scalar2=1.0, op0=ALU.is_ge, op1=ALU.subtract),
                 r=["imp", "m8"], w=["imp2"])
            S.op("dve", lambda h: h.tensor_scalar(out=nsel[:], in0=imp2[:], scalar1=-NEGM, scalar2=None, op0=ALU.mult), r=["imp2"], w=["nsel"])
            S.op("pe", lambda h: h.transpose(out=ps_tr[0:64, 0:128], in_=nsel[:], identity=ident[:]), r=["nsel", "p1id"], w=["p1pstr"])
            for r in range(4):
                S.op("act", lambda h, r=r: h.copy(out=nselT[:, r, :], in_=ps_tr[0:64, 0:128]), r=["p1pstr"], w=["nselT"])
            kts = [kt for kt in range(qt - 4, qt + 1) if kt >= 0]
            jobs = [("s", kt, kt) for kt in range(qt + 1)] + [("w", kt, jj) for jj, kt in enumerate(kts)]

            def do_S(job, g=g, b=b, qt=qt):
                kind, kt, jj = job
                if kind == "s":
                    cls = 0 if kt == qt else (1 if kt == qt - 1 else 3)
                    extra = [(Em[:, kt, :], nselT[:].rearrange("p r n -> p (r n)"), ["E", "nselT"]),
                             (ident[:], biasT[:, cls, g * 512:(g + 1) * 512], ["p1id", "biasT"])]
                    return st_tile(g, b, kvT[:, 0, g, kt * 128:(kt + 1) * 128], [f"kvT_{kt // 4}"], extra, None, None, None, None, None)
                dq = qt - kt
                cls = 0 if dq == 0 else (1 if dq == 1 else (2 if dq == 4 else 3))
                extra = [(ident[:], biasT[:, cls, g * 512:(g + 1) * 512], ["p1id", "biasT"])]
                return st_tile(g, b, kvT[:, 1, g, kt * 128:(kt + 1) * 128], [f"kvT_{kt // 4}"], extra, None, None, None, None, None)

            def do_PV(job, pi, g=g, qt=qt, nk=len(kts)):
                kind, kt, jj = job
                for r in range(4):
                    if kind == "s":
                        S.op("pe", lambda h, r=r: h.matmul(po_s[:, r, 0:65], lhsT=pT[pi][:, r * 128:(r + 1) * 128], rhs=vau[:, kt, 0, g, :],
                                                           start=(kt == 0 and r == 0), stop=(kt == qt), skip_group_check=True),
                             r=[f"pT{pi}", f"vau{kt}", "vau_ones"], w=["po_s"])
                    else:
                        S.op("pe", lambda h, r=r: h.matmul(po_w[:, r, 0:65], lhsT=pT[pi][:, r * 128:(r + 1) * 128], rhs=vau[:, kt, 1, g, :],
                                                           start=(jj == 0 and r == 0), stop=(jj == nk - 1), skip_group_check=True),
                             r=[f"pT{pi}", f"vau{kt}", "vau_ones"], w=["po_w"])

            pend = None
            for job in jobs:
                pi_ = do_S(job)
                if pend is not None:
                    do_PV(*pend)
                pend = (job, pi_)
            do_PV(*pend)
            S.op("dve", lambda h: h.reciprocal(out=rden[:, 1, :], in_=po_s[:, :, 64]), r=["po_s"], w=["rden1"])
            S.op("dve", lambda h: h.reciprocal(out=rden[:, 2, :], in_=po_w[:, :, 64]), r=["po_w"], w=["rden2"])
            for j in range(3):
                S.op("dve", lambda h, j=j, g=g, b=b: h.tensor_tensor(out=cf[:, j, :], in0=rden[:, j, :], in1=gt[b][:, j * 8 + g * 4:j * 8 + g * 4 + 4], op=ALU.mult),
                     r=[f"rden{j}", f"gt{b}"], w=["cf"])
            for r in range(4):
                col = (g * 4 + r) * 64
                S.op("dve", lambda h, r=r, col=col: h.tensor_scalar(out=y[:, col:col + 64], in0=po_c[:, r, 0:64], scalar1=cf[:, 0, r:r + 1], scalar2=None, op0=ALU.mult),
                     r=["po_c", "cf"], w=["y"])
                S.op("dve", lambda h, r=r, col=col: h.scalar_tensor_tensor(out=y[:, col:col + 64], in0=po_s[:, r, 0:64], scalar=cf[:, 1, r:r + 1], in1=y[:, col:col + 64],
                                                                          op0=ALU.mult, op1=ALU.add), r=["po_s", "cf", "y"], w=["y"])
                S.op("dve", lambda h, r=r, col=col: h.scalar_tensor_tensor(out=y[:, col:col + 64], in0=po_w[:, r, 0:64], scalar=cf[:, 2, r:r + 1], in1=y[:, col:col + 64],
                                                                          op0=ALU.mult, op1=ALU.add), r=["po_w", "cf", "y"], w=["y"])
        S.op("pool", lambda h, b=b: h.tensor_tensor(out=yz[b][:], in0=y[:], in1=zs[b][:], op=ALU.mult), r=["y", f"zs{b}"], w=[f"yz{b}"])
        for c in range(4):
            S.op("pe", lambda h, c=c, b=b: h.transpose(out=ps_tr[:, c * 128:(c + 1) * 128], in_=yz[b][:, c * 128:(c + 1) * 128], identity=ident[:]),
                 r=[f"yz{b}", "p1id"], w=["p1pstr"])
        S.op("act", lambda h, b=b: h.copy(out=yzT[b][:], in_=ps_tr[:, 0:512].rearrange("p (c n) -> p c n", c=4)), r=["p1pstr"], w=[f"yzT{b}"])
        for hf in range(2):
            psy = ps_q if hf == 0 else ps_z
            nm = "ps_q" if hf == 0 else "ps_z"
            for c in range(4):
                S.op("pe", lambda h, hf=hf, c=c, b=b, psy=psy: h.matmul(psy[:, :], lhsT=yzT[b][:, c, :], rhs=wo[:, c, hf * 512:(hf + 1) * 512],
                                                                       start=(c == 0), stop=(c == 3)), r=[f"yzT{b}", "wo"], w=[nm])
            if hf == 0:
                S.op("act", lambda h, b=b, psy=psy: h.copy(out=pt[b][:, 0:512], in_=psy[:, :]), r=[nm], w=[f"pt{b}"])
            else:
                S.op("dve", lambda h, b=b, psy=psy: h.tensor_copy(out=pt[b][:, 512:1024], in_=psy[:, :]), r=[nm], w=[f"pt{b}"])
        S.op("sp", lambda h, b=b, tq=tq: h.dma_start(out=p_out[tq, :], in_=pt[b][:]), r=[f"pt{b}"], dma=True)
    return nc, es, S


def prep_C(z, half, T):
    NT = T // 128
    d = {}
    w_in = z['c_w_in'][0]
    d['wq'] = np.ascontiguousarray(w_in[:, half * 512:(half + 1) * 512])
    kv = []
    for s in range(6):
        base = 1024 + s * 256 + half * 128
        kv.append(w_in[:, base:base + 128])
    d['wkv'] = np.ascontiguousarray(np.stack(kv, 1))
    gcols = np.concatenate([2560 + j * 16 + half * 8 + np.arange(8) for j in range(3)])
    d['wg'] = np.ascontiguousarray(w_in[:, gcols])
    d['wz'] = np.ascontiguousarray(w_in[:, 2608 + half * 512:2608 + (half + 1) * 512])
    d['wo'] = np.ascontiguousarray(z['c_w_out'][0][half * 512:(half + 1) * 512, :])
    w1 = np.stack([z['c_cmp_k_w1'][0], z['c_cmp_v_w1'][0]])
    d['w1'] = np.ascontiguousarray(w1.reshape(2, 32, 64, 128).transpose(0, 2, 1, 3))
    d['w2'] = np.ascontiguousarray(np.stack([z['c_cmp_k_w2'][0], z['c_cmp_v_w2'][0]]))
    d['posT'] = np.ascontiguousarray(np.stack([z['c_cmp_pos_k'][0].T, z['c_cmp_pos_v'][0].T]))
    table = z['t5_table']
    tk = np.arange(128)[:, None]
    tq = np.arange(128)[None, :]
    bias = np.zeros((128, 4, 2, 4, 128), np.float32)
    for g in range(2):
        for r in range(4):
            hh = half * 8 + g * 4 + r
            d0 = tq - tk
            bias[:, 0, g, r, :] = np.where(d0 >= 0, table[t5_bucket_np(d0), hh], NEGM)
            d1 = tq - tk + 128
            bias[:, 1, g, r, :] = table[t5_bucket_np(d1), hh]
            bias[:, 2, g, r, :] = np.where(tq < tk, table[31, hh], NEGM)
            bias[:, 3, g, r, :] = table[31, hh]
    d['biasT'] = bias.reshape(128, 4, 1024)
    j = np.arange(512)[:, None]
    F = np.where(16 * (j - 248) + 31 <= tq, 0.0, NEGM).astype(np.float32)
    d['F4'] = np.ascontiguousarray(np.tile(F, (1, 4)))
    ka = np.zeros((NT, 128, 128), np.float32)
    sblk = np.arange(64)[None, :]
    for qt in range(NT):
        t = qt * 128 + np.arange(128)[:, None]
        cur = t // 64
        forced = (sblk == 0) | (sblk == cur) | (sblk == cur - 1)
        future = sblk * 64 > t
        ka[qt, :, 0:64] = np.where(forced | future, 0.0, 1.0)
        ka[qt, :, 64:128] = np.where(forced, 1e30, np.where(future, -1e30, 0.0))
    d['keepadd'] = ka
    E = np.zeros((64, NT, 128), np.float32)
    for kt in range(NT):
        E[2 * kt, kt, 0:64] = 1.0
        E[2 * kt + 1, kt, 64:128] = 1.0
    d['E'] = E
    n = np.arange(256)[:, None]
    s = np.arange(64)[None, :]
    ov = ((16 * n < 64 * s + 64) & (16 * n + 31 >= 64 * s)).astype(np.float32)
    d['ovl'] = np.ascontiguousarray(ov.reshape(2, 128, 64).transpose(1, 0, 2))
    d['g'] = z['norm_g'][2:3].copy()
    d['ident'] = np.eye(128, dtype=np.float32)
    return d


NBLK = 8
BW = 80


def build_D(nsrc=1, TCH=1024, debug=False):
    T = CFG.T
    nc = get_nc()
    es = ExitStack()
    S = get_sched(nc, es)
    srcs = [dram_in(nc, f"xin{k}", [T, D]) for k in range(nsrc)]
    xs_out = dram_out(nc, "xs", [T, D]) if nsrc > 1 else None
    g_row = dram_in(nc, "g", [1, D])
    ident_d = dram_in(nc, "ident", [128, 128])
    wu_d = dram_in(nc, "wu", [D, NBLK * BW])
    wz_d = dram_in(nc, "wz", [D, NBLK * BW])
    wo_d = dram_in(nc, "wo", [NBLK * BW, D])
    ga_d = dram_in(nc, "ga", [NBLK, BW, BW])
    gx_d = dram_in(nc, "gx", [NBLK, BW, BW])
    vec_d = dram_in(nc, "vecs", [BW, NBLK, 8])
    p_out = dram_out(nc, "p", [T, D])

    xnT = mk(nc, es, "xnT", [128, 8, T], BF16)
    ps = [mk(nc, es, f"ps{k}", [128, 512], F32, psum=True) for k in range(7)]
    ps_tr = mk(nc, es, "ps_tr", [128, 1024], BF16, psum=True)
    phase1(S, nc, es, srcs, xs_out, g_row, ident_d, ps_tr, xnT=xnT)

    wu = mk(nc, es, "wu_s", [128, 8, NBLK * BW], BF16)
    wz = mk(nc, es, "wz_s", [128, 8, NBLK * BW], BF16)
    wo = mk(nc, es, "wo_s", [BW, NBLK, D], BF16)
    ga = mk(nc, es, "ga_s", [BW, NBLK, BW], F32)
    gx = mk(nc, es, "gx_s", [BW, NBLK, BW], F32)
    vec = mk(nc, es, "vec_s", [BW, NBLK, 8], F32)
    der = mk(nc, es, "der_s", [BW, NBLK, 4], F32)
    S.op("pool", lambda h: h.dma_start(out=wu[:], in_=wu_d.rearrange("(c p) n -> p c n", p=128)), w=["wu"], dma=True)
    S.op("pool", lambda h: h.dma_start(out=wz[:], in_=wz_d.rearrange("(c p) n -> p c n", p=128)), w=["wz"], dma=True)
    S.op("pool", lambda h: h.dma_start(out=wo[:], in_=wo_d.rearrange("(b p) n -> p b n", p=BW)), w=["wo"], dma=True)
    S.op("sp", lambda h: h.dma_start(out=ga[:], in_=ga_d.rearrange("b p n -> p b n")), w=["ga"], dma=True)
    S.op("sp", lambda h: h.dma_start(out=gx[:], in_=gx_d.rearrange("b p n -> p b n")), w=["gx"], dma=True)
    S.op("sp", lambda h: h.dma_start(out=vec[:], in_=vec_d), w=["vec"], dma=True)
    S.op("act", lambda h: h.activation(out=der[:, :, 0:1], in_=vec[:, :, 7:8], func=AF.Exp, scale=-1.0), r=["vec"], w=["der"])
    S.op("act", lambda h: h.activation(out=der[:, :, 1:2], in_=der[:, :, 0:1], func=AF.Ln, bias=1.0), r=["der"], w=["der"])
    S.op("act", lambda h: h.mul(out=der[:, :, 2:3], in_=der[:, :, 1:2], mul=-8.0), r=["der"], w=["der"])

    NW = 2
    def wt(nm, cols=TCH, dt=F32):
        return [mk(nc, es, f"{nm}{b}", [BW, cols], dt) for b in range(NW)]
    u_t = wt("u_t", TCH + 3)
    uc_t = wt("uc_t"); zs_t = wt("zs_t"); r_t = wt("r_t"); i_t = wt("i_t"); a_t = r_t; m_t = [mk(nc, es, "m_t0", [BW, TCH], F32)] * NW; h_t = uc_t
    hz = mk(nc, es, "hz", [BW, NBLK, TCH], BF16)
    hlast = mk(nc, es, "hlast", [BW, NBLK], F32)
    uhalo = mk(nc, es, "uhalo", [BW, NBLK, 3], F32)
    pt = [mk(nc, es, "pt0", [128, D], F32)] * 2
    S.op("dve", lambda h: h.memset(hlast[:], 0.0), w=["hlast"])
    for b in range(NW):
        S.op("dve", lambda h, b=b: h.memset(u_t[b][:, 0:3], 0.0), w=[f"u{b}"])
    it = 0
    for tch in range(T // TCH):
        t0 = tch * TCH
        for blk in range(NBLK):
            b = it % NW
            pb = (it - 1) % NW
            it += 1
            cs = slice(blk * BW, (blk + 1) * BW)
            for hf in range(2):
                tk = slice(t0 + hf * 512, t0 + (hf + 1) * 512)
                for dc in range(8):
                    S.op("pe", lambda h, hf=hf, dc=dc, tk=tk, cs=cs: h.matmul(ps[hf][0:BW, :], lhsT=wu[:, dc, cs], rhs=xnT[:, dc, tk],
                                                                         start=(dc == 0), stop=(dc == 7)),
                         r=["wu", f"xnT{(t0 + hf * 512) // 512}"], w=[f"ps{hf}"])
            for hf in range(2):
                tk = slice(t0 + hf * 512, t0 + (hf + 1) * 512)
                for dc in range(8):
                    S.op("pe", lambda h, hf=hf, dc=dc, tk=tk, cs=cs: h.matmul(ps[2 + hf][0:BW, :], lhsT=wz[:, dc, cs], rhs=xnT[:, dc, tk],
                                                                         start=(dc == 0), stop=(dc == 7)),
                         r=["wz", f"xnT{(t0 + hf * 512) // 512}"], w=[f"ps{2 + hf}"])
            S.op("pool", lambda h, b=b, blk=blk: h.tensor_copy(out=u_t[b][:, 0:3], in_=uhalo[:, blk, :]), r=["uhalo%d" % blk], w=[f"u{b}"]) if tch > 0 else None
            for hf in range(2):
                S.op("act", lambda h, hf=hf, b=b: h.copy(out=u_t[b][:, 3 + hf * 512:3 + (hf + 1) * 512], in_=ps[hf][0:BW, :]),
                     r=[f"ps{hf}"], w=[f"u{b}"])
            for hf in range(2):
                S.op("act", lambda h, hf=hf, b=b: h.activation(out=zs_t[b][:, hf * 512:(hf + 1) * 512], in_=ps[2 + hf][0:BW, :], func=AF.Silu),
                     r=[f"ps{2 + hf}"], w=[f"zs{b}"])
            S.op("pool", lambda h, b=b, blk=blk: h.tensor_copy(out=uhalo[:, blk, :], in_=u_t[b][:, TCH:TCH + 3]), r=[f"u{b}"], w=["uhalo%d" % blk])
            if debug and tch == 0 and blk == 0:
                dbg(S, nc, "xnT", xnT[:, :, 0:512], [128, 8, 512], ["xnT0"], BF16)
                dbg(S, nc, "u", u_t[b][:], [BW, TCH + 3], [f"u{b}"])
                dbg(S, nc, "zs", zs_t[b][:], [BW, TCH], [f"zs{b}"])
            S.op("dve", lambda h, b=b, blk=blk: h.tensor_scalar(out=uc_t[b][:], in0=u_t[b][:, 3:3 + TCH], scalar1=vec[:, blk, 3:4], scalar2=vec[:, blk, 4:5],
                                                               op0=ALU.mult, op1=ALU.add), r=[f"u{b}", "vec"], w=[f"uc{b}"])
            for j in range(3):
                S.op("dve", lambda h, b=b, blk=blk, j=j: h.scalar_tensor_tensor(out=uc_t[b][:], in0=u_t[b][:, j:j + TCH], scalar=vec[:, blk, j:j + 1],
                                                                               in1=uc_t[b][:], op0=ALU.mult, op1=ALU.add),
                     r=[f"u{b}", "vec"], w=[f"uc{b}"])
            for hf in range(2):
                S.op("pe", lambda h, hf=hf, b=b, blk=blk: h.matmul(ps[4][0:BW, :] if hf == 0 else ps[5][0:BW, :], lhsT=ga[:, blk, :],
                                                                  rhs=uc_t[b][:, hf * 512:(hf + 1) * 512], start=True, stop=True),
                     r=["ga", f"uc{b}"], w=[f"ps{4 + hf}"])
                S.op("act", lambda h, hf=hf, b=b, blk=blk: h.activation(out=r_t[b][:, hf * 512:(hf + 1) * 512], in_=ps[4 + hf][0:BW, :], func=AF.Sigmoid,
                                                                       bias=vec[:, blk, 5:6]), r=[f"ps{4 + hf}", "vec"], w=[f"r{b}"])
            for hf in range(2):
                S.op("pe", lambda h, hf=hf, b=b, blk=blk: h.matmul(ps[4 + hf][0:BW, :], lhsT=gx[:, blk, :],
                                                                  rhs=uc_t[b][:, hf * 512:(hf + 1) * 512], start=True, stop=True),
                     r=["gx", f"uc{b}"], w=[f"ps{4 + hf}"])
                S.op("act", lambda h, hf=hf, b=b, blk=blk: h.activation(out=i_t[b][:, hf * 512:(hf + 1) * 512], in_=ps[4 + hf][0:BW, :], func=AF.Sigmoid,
                                                                       bias=vec[:, blk, 6:7]), r=[f"ps{4 + hf}", "vec"], w=[f"i{b}"])
            if debug and tch == 0 and blk == 0:
                dbg(S, nc, "uc", uc_t[b][:], [BW, TCH], [f"uc{b}"])
                dbg(S, nc, "r", r_t[b][:], [BW, TCH], [f"r{b}"])
                dbg(S, nc, "i", i_t[b][:], [BW, TCH], [f"i{b}"])
                dbg(S, nc, "der", der[:], [BW, NBLK, 4], ["der"])
            S.op("act", lambda h, b=b, blk=blk: h.activation(out=a_t[b][:], in_=r_t[b][:], func=AF.Exp, scale=der[:, blk, 2:3]),
                 r=[f"r{b}", "der"], w=[f"r{b}"])
            S.op("pool", lambda h, b=b: h.tensor_tensor(out=m_t[b][:], in0=a_t[b][:], in1=a_t[b][:], op=ALU.mult), r=[f"r{b}"], w=["m0"])
            S.op("act", lambda h, b=b: h.activation(out=m_t[b][:], in_=m_t[b][:], func=AF.Sqrt, scale=-1.0, bias=1.0), r=["m0"], w=["m0"])
            S.op("pool", lambda h, b=b: h.tensor_tensor(out=i_t[b][:], in0=i_t[b][:], in1=uc_t[b][:], op=ALU.mult), r=[f"i{b}", f"uc{b}"], w=[f"i{b}"])
            S.op("pool", lambda h, b=b: h.tensor_tensor(out=i_t[b][:], in0=i_t[b][:], in1=m_t[b][:], op=ALU.mult), r=[f"i{b}", "m0"], w=[f"i{b}"])
            if debug and tch == 0 and blk == 0:
                dbg(S, nc, "a", r_t[b][:], [BW, TCH], [f"r{b}"])
                dbg(S, nc, "m", m_t[b][:], [BW, TCH], ["m0"])
                dbg(S, nc, "bt", i_t[b][:], [BW, TCH], [f"i{b}"])
            S.op("dve", lambda h, b=b, blk=blk: h.tensor_tensor_scan(out=h_t[b][:], data0=a_t[b][:], data1=i_t[b][:], initial=hlast[:, blk:blk + 1],
                                                                    op0=ALU.mult, op1=ALU.add), r=[f"r{b}", f"i{b}", "hlast"], w=[f"uc{b}"])
            S.op("dve", lambda h, b=b, blk=blk: h.tensor_copy(out=hlast[:, blk:blk + 1], in_=h_t[b][:, TCH - 1:TCH]), r=[f"uc{b}"], w=["hlast"])
            S.op("dve", lambda h, b=b, blk=blk: h.tensor_tensor(out=hz[:, blk, :], in0=h_t[b][:], in1=zs_t[b][:], op=ALU.mult),
                 r=[f"uc{b}", f"zs{b}"], w=["hz"])
        if debug and tch == 0:
            dbg(S, nc, "hz", hz[:], [BW, NBLK, TCH], ["hz"], BF16)
        for tl in range(TCH // 128):
            pbuf = tl % 2
            for hf in range(2):
                for blk in range(NBLK):
                    S.op("pe", lambda h, tl=tl, hf=hf, blk=blk: h.matmul(ps[hf][:, :], lhsT=hz[:, blk, tl * 128:(tl + 1) * 128],
                                                                        rhs=wo[:, blk, hf * 512:(hf + 1) * 512], start=(blk == 0), stop=(blk == NBLK - 1)),
                         r=["hz", "wo"], w=[f"ps{hf}"])
                S.op("act" if hf == 0 else "dve",
                     (lambda h, hf=hf, pbuf=pbuf: h.copy(out=pt[pbuf][:, hf * 512:(hf + 1) * 512], in_=ps[hf][:, :])) if hf == 0 else
                     (lambda h, hf=hf, pbuf=pbuf: h.tensor_copy(out=pt[pbuf][:, hf * 512:(hf + 1) * 512], in_=ps[hf][:, :])),
                     r=[f"ps{hf}"], w=["pt0"])
            rows = slice(t0 + tl * 128, t0 + (tl + 1) * 128)
            S.op("sp", lambda h, pbuf=pbuf, rows=rows: h.dma_start(out=p_out[rows, :], in_=pt[pbuf][:]), r=["pt0"], dma=True)
    return nc, es, S


def build_F(ntok=2048):
    nc = get_nc()
    es = ExitStack()
    S = get_sched(nc, es)
    srcs = [dram_in(nc, f"xin{k}", [ntok, D]) for k in range(3)]
    g_row = dram_in(nc, "g", [1, D])
    out_d = dram_out(nc, "out", [ntok, D])
    g_bc = mk(nc, es, "g_bc", [128, D], F32)
    S.op("sp", lambda h: h.dma_start(out=g_bc[:], in_=g_row.partition_broadcast(128)), w=["g"], dma=True)
    NB = 2
    xt = [mk(nc, es, f"xt{b}", [128, D], F32) for b in range(NB)]
    sq = mk(nc, es, "sq", [128, D], F32)
    ot = [mk(nc, es, f"ot{b}", [128, D], F32) for b in range(NB)]
    st = [mk(nc, es, f"st{b}", [128, 4], F32) for b in range(NB)]
    for t in range(ntok // 128):
        b = t % NB
        rows = slice(t * 128, (t + 1) * 128)
        for k, src in enumerate(srcs):
            if k == 0:
                S.op("pool", lambda h, src=src, b=b, t=t: h.dma_start(out=xt[b][:], in_=src_rows(src, t)), w=[f"xt{b}"], dma=True)
            else:
                S.op("pool", lambda h, src=src, b=b, t=t: h.dma_start(out=xt[b][:], in_=src_rows(src, t), accum_op=ALU.add), r=[f"xt{b}"], w=[f"xt{b}"], dma=True)
        S.op("act", lambda h, b=b: h.activation(out=sq[:], in_=xt[b][:], func=AF.Square), r=[f"xt{b}"], w=["sq"])
        S.op("dve", lambda h, b=b: h.tensor_reduce(out=st[b][:, 0:1], in_=sq[:], axis=AX.X, op=ALU.add), r=["sq"], w=[f"st{b}"])
        S.op("act", lambda h, b=b: h.activation(out=st[b][:, 1:2], in_=st[b][:, 0:1], func=AF.Sqrt, scale=1.0 / D, bias=EPS), r=[f"st{b}"], w=[f"st{b}"])
        S.op("dve", lambda h, b=b: h.reciprocal(out=st[b][:, 2:3], in_=st[b][:, 1:2]), r=[f"st{b}"], w=[f"st{b}r"])
        S.op("dve", lambda h, b=b: h.scalar_tensor_tensor(out=ot[b][:], in0=xt[b][:], scalar=st[b][:, 2:3], in1=g_bc[:], op0=ALU.mult, op1=ALU.mult),
             r=[f"xt{b}", f"st{b}r", "g"], w=[f"ot{b}"])
        S.op("sp", lambda h, b=b, rows=rows: h.dma_start(out=out_d[rows, :], in_=ot[b][:]), r=[f"ot{b}"], dma=True)
    return nc, es, S


def prep_D(z, half):
    LW = 1280
    blks = list(range(half * 8, half * 8 + 8))
    cols = np.concatenate([np.arange(b * 80, (b + 1) * 80) for b in blks])
    w_in = z['d_w_in'][0]
    d = {}
    d['wu'] = np.ascontiguousarray(w_in[:, cols])
    d['wz'] = np.ascontiguousarray(w_in[:, LW + cols])
    d['wo'] = np.ascontiguousarray(z['d_w_out'][0][cols, :])
    d['ga'] = np.ascontiguousarray(z['d_gate_a_w'][0][blks])
    d['gx'] = np.ascontiguousarray(z['d_gate_x_w'][0][blks])
    vecs = np.zeros((80, 8, 8), np.float32)

    def fm(v):
        return v[cols].reshape(8, 80).T
    for j in range(4):
        vecs[:, :, j] = fm(z['d_conv_w'][0][j])
    vecs[:, :, 4] = fm(z['d_conv_b'][0])
    vecs[:, :, 5] = fm(z['d_gate_a_b'][0])
    vecs[:, :, 6] = fm(z['d_gate_x_b'][0])
    vecs[:, :, 7] = fm(z['d_lambda'][0])
    d['vecs'] = vecs
    d['g'] = z['norm_g'][3:4].copy()
    d['ident'] = np.eye(128, dtype=np.float32)
    return d


PAIRS = [[0, 1], [2, 3], [4, 5], [6, 7]]


def build_fused(T=4096, nlayers=4):
    CFG.T = T
    nc = bass.Bass("TRN2", target_bir_lowering=False)
    top = ExitStack()
    S = Sched(nc, top)
    CFG.nc, CFG.S = nc, S
    x_d = nc.dram_tensor("x", [T, D], F32, kind="ExternalInput").ap()
    out_d = nc.dram_tensor("out", [T, D], F32, kind="ExternalOutput").ap()
    p = [nc.dram_tensor(f"p_i{l}", [T, D], F32) for l in range(4)]
    CH = 512
    NCH = T // CH
    pg = [[nc.dram_tensor(f"pg_i{l}_{k}", [2 * CH, D], F32) for k in range(NCH)] for l in range(4)]

    def gsrc(l, rank):
        return lambda t: pg[l][t // 4].ap()[rank * CH + (t % 4) * 128:rank * CH + (t % 4 + 1) * 128, :]
    xs = [nc.dram_tensor(f"xs_i{l}", [T, D], F32) for l in range(3)]
    layers = [("A", build_A, {}), ("B", build_B, dict(MC=128)), ("C", build_C, {}), ("D", build_D, {})]
    prev_x = x_d
    for l, (nm, fn, kw) in enumerate(layers[:nlayers]):
        CFG.prefix = nm + "_"
        SB_USED[0] = 0
        ov = {"p": p[l].ap()}
        if l == 0:
            ov["xin0"] = x_d
            nsrc = 1
        else:
            ov["xin0"] = prev_x
            ov["xin1"] = gsrc(l - 1, 0)
            ov["xin2"] = gsrc(l - 1, 1)
            ov["xs"] = xs[l - 1].ap()
            nsrc = 3
        CFG.override = ov
        _, es, _ = fn(nsrc, **kw)
        for k in range(NCH):
            S.op("pool", lambda h, l=l, k=k: h.collective_compute("AllGather", ALU.bypass, replica_groups=PAIRS, ins=[p[l].ap()[k * CH:(k + 1) * CH, :].opt()],
                                                               outs=[pg[l][k].ap().opt()]), dma=True, cc=True)
        S.emit(final=False)
        S.barrier()
        es.close()
        if l > 0:
            prev_x = xs[l - 1].ap()
    CFG.prefix = "F_"
    SB_USED[0] = 0
    CFG.override = {"xin0": prev_x, "xin1": gsrc(nlayers - 1, 0), "xin2": gsrc(nlayers - 1, 1), "out": out_d}
    _, es, _ = build_F(T)
    stats = S.emit(final=True)
    CFG.nc, CFG.S, CFG.override, CFG.prefix = None, None, {}, ""
    return nc, stats


def kernel(**inputs):
    z = {k: np.ascontiguousarray(np.asarray(v, dtype=np.float32)) for k, v in inputs.items()}
    T = 4096
    x = z['x']
    B = x.shape[0]
    nc, _ = build_fused(T)
    per_half = []
    for h in range(2):
        d = {}
        for pre, pd in (("A_", prep_A(z, h)), ("B_", prep_B(z, h, MC=128)), ("C_", prep_C(z, h, T)), ("D_", prep_D(z, h))):
            for k, v in pd.items():
                d[pre + k] = v
        d["F_g"] = z['final_g'][None, :].copy()
        per_half.append(d)
    in_maps = [dict(per_half[c % 2], x=x[c // 2]) for c in range(8)]
    res = run_bass_kernel_spmd(nc, in_maps, core_ids=list(range(8)))
    out = np.stack([res.results[2 * b]['out'] for b in range(B)]).astype(np.float32)
    return out
```

```python
import numpy as np
from contextlib import ExitStack
import concourse.bass as bass
import concourse.mybir as mybir
from concourse.bass_utils import run_bass_kernel_spmd

F32 = mybir.dt.float32
BF16 = mybir.dt.bfloat16
AF = mybir.ActivationFunctionType
ALU = mybir.AluOpType
AX = mybir.AxisListType


class Buf:
    __slots__ = ("name", "lw", "rd")

    def __init__(self, name):
        self.name = name
        self.lw = None
        self.rd = {}


class Sched:
    COMPUTE = ("pe", "act", "dve", "pool")

    def __init__(self, nc, es, ndma_slots=8):
        self.nc = nc
        self.es = es
        self.ops = []
        self.bufs = {}
        self.ndma = ndma_slots
        self.handles = {"pe": nc.tensor, "act": nc.scalar, "dve": nc.vector, "pool": nc.gpsimd, "sp": nc.sync}
        self.need = []
        self.seg_dma = []
        self.last_compute = {}
        self.barrier_deps = set()
        self.pending_barrier = {}

    def buf(self, name):
        b = self.bufs.get(name)
        if b is None:
            b = Buf(name)
            self.bufs[name] = b
        return b

    def _B(self, lst):
        out = []
        for x in lst:
            if isinstance(x, str):
                out.append(self.buf(x))
            elif isinstance(x, Buf):
                out.append(x)
            elif x is None:
                continue
            else:
                out.extend(self._B(x))
        return out

    def op(self, eng, fn, r=(), w=(), dma=False, cc=False):
        i = len(self.ops)
        R = self._B(r)
        W = self._B(w)
        deps = set()
        if self.pending_barrier.get(eng):
            deps |= self.barrier_deps
            self.pending_barrier[eng] = False
        if cc:
            deps |= set(self.seg_dma)
        for b in R:
            if b.lw is not None:
                deps.add(b.lw)
        for b in W:
            if b.lw is not None:
                deps.add(b.lw)
            for k, v in b.rd.items():
                deps.add(v)
        for b in W:
            b.lw = i
            b.rd = {}
        key = ("dma", i) if dma else eng
        for b in R:
            b.rd[key] = i
        self.ops.append(dict(eng=eng, fn=fn, deps=deps, dma=dma, cc=cc))
        if dma:
            self.seg_dma.append(i)
        elif eng in self.COMPUTE:
            self.last_compute[eng] = i
        return i

    def _init_state(self):
        nc = self.nc
        self.sems = {e: self.es.enter_context(nc.semaphore("sem_" + e)) for e in self.COMPUTE}
        self.dsems = {q: [self.es.enter_context(nc.semaphore(f"dsem_{q}_{k}")) for k in range(self.ndma)] for q in ("sp", "pool")}
        self.ccsem = self.es.enter_context(nc.semaphore("sem_cc"))
        self.cccount = 0
        self.duses = {q: [0] * self.ndma for q in ("sp", "pool")}
        self.dcount = {"sp": 0, "pool": 0}
        self.cnt = {e: 0 for e in self.COMPUTE}
        self.token = []
        self.waited = {e: {} for e in self.handles}
        self.nwaits = 0
        self.emitted = 0
        self.inited = True

    def barrier(self):
        deps = set(self.seg_dma)
        for e in self.COMPUTE:
            if e in self.last_compute:
                deps.add(self.last_compute[e])
        self.barrier_deps = deps
        self.pending_barrier = {e: True for e in self.handles}
        self.seg_dma = []
        self.bufs = {}

    def emit(self, final=True):
        nc = self.nc
        ops = self.ops
        if not getattr(self, "inited", False):
            self._init_state()
        start = self.emitted
        n = len(ops)
        need = self.need
        need.extend([False] * (n - len(need)))
        for i in range(start, n):
            o = ops[i]
            for d in o["deps"]:
                po = ops[d]
                if po["dma"]:
                    continue
                if po["eng"] != o["eng"] or o["dma"] or o["eng"] != "pe":
                    assert d >= start or need[d], "cross-segment dependency on an op without increment"
                    need[d] = True
        lastc = {}
        for i in range(start, n):
            if not ops[i]["dma"] and ops[i]["eng"] in self.COMPUTE:
                lastc[ops[i]["eng"]] = i
        for e, i in lastc.items():
            need[i] = True
        sems, dsems, duses, dcount, cnt, token, waited = self.sems, self.dsems, self.duses, self.dcount, self.cnt, self.token, self.waited
        token.extend([None] * (n - len(token)))
        for i in range(start, n):
            o = ops[i]
            e = o["eng"]
            h = self.handles[e]
            wd = waited[e]
            reqs = {}
            for d in o["deps"]:
                po = ops[d]
                if (not po["dma"]) and (not o["dma"]) and po["eng"] == e and e == "pe":
                    continue
                sem, val, sk = token[d]
                if wd.get(sk, 0) >= val:
                    continue
                if sk not in reqs or reqs[sk][1] < val:
                    reqs[sk] = (sem, val)
            is_cc = o.get("cc", False)
            if o["dma"] and not is_cc:
                q = e
                s = dcount[q] % self.ndma
                dcount[q] += 1
                dsk = ("d", q, s)
                prev = 16 * duses[q][s]
                if prev > 0 and wd.get(dsk, 0) < prev:
                    if dsk not in reqs or reqs[dsk][1] < prev:
                        reqs[dsk] = (dsems[q][s], prev)
            for rk, (rsem, rval) in reqs.items():
                h.wait_ge(rsem, rval)
                wd[rk] = rval
                self.nwaits += 1
            ins = o["fn"](h)
            if is_cc:
                self.cccount += 1
                ins.then_inc(self.ccsem, 1)
                token[i] = (self.ccsem, self.cccount, ("cc",))
            elif o["dma"]:
                duses[q][s] += 1
                ins.then_inc(dsems[q][s], 16)
                token[i] = (dsems[q][s], 16 * duses[q][s], dsk)
            else:
                if need[i]:
                    cnt[e] += 1
                    ins.then_inc(sems[e], 1)
                    token[i] = (sems[e], cnt[e], ("c", e))
                else:
                    token[i] = (sems[e], cnt[e] + 0, ("c", e))
            o["fn"] = None
        self.emitted = n
        if final:
            h = self.handles["sp"]
            for q in ("sp", "pool"):
                for s in range(self.ndma):
                    if duses[q][s] > 0:
                        h.wait_ge(dsems[q][s], 16 * duses[q][s])
            if self.cccount:
                h.wait_ge(self.ccsem, self.cccount)
        self.stats = dict(nops=len(ops), nwaits=self.nwaits, incs=dict(cnt))
        return self.stats


class Stream:
    def __init__(self):
        self.items = []

    def op(self, *a, **k):
        self.items.append((a, k))


def merge_streams(S, streams, chunk=1):
    idx = [0] * len(streams)
    live = True
    while live:
        live = False
        for i, st in enumerate(streams):
            for _ in range(chunk):
                if idx[i] < len(st.items):
                    a, k = st.items[idx[i]]
                    S.op(*a, **k)
                    idx[i] += 1
                    live = True


class CFG:
    T = 4096
    prefix = ""
    nc = None
    S = None
    override = {}


def get_nc():
    if CFG.nc is not None:
        return CFG.nc
    return bass.Bass("TRN2", target_bir_lowering=False)


def get_sched(nc, es):
    if CFG.S is not None:
        return CFG.S
    return Sched(nc, es)
D = 1024
EPS = 1e-6


class Ctx:
    pass


SB_USED = [0]


def mk(nc, es, name, shape, dt, psum=False):
    if not psum:
        n = 1
        for d_ in shape[1:]:
            n *= d_
        n *= (2 if dt == BF16 else 4)
        SB_USED[0] += (n + 31) // 32 * 32
        assert SB_USED[0] <= 190 * 1024, f"SBUF over budget at {name}: {SB_USED[0]}"
    if psum:
        return es.enter_context(nc.psum_tensor(CFG.prefix + name, shape, dt))
    return es.enter_context(nc.sbuf_tensor(CFG.prefix + name, shape, dt))


def src_rows(src, t):
    if callable(src):
        return src(t)
    return src[t * 128:(t + 1) * 128, :]


def dram_in(nc, name, shape, dt=F32):
    if name in CFG.override:
        return CFG.override[name]
    return nc.dram_tensor(CFG.prefix + name, list(shape), dt, kind="ExternalInput").ap()


def dram_out(nc, name, shape, dt=F32):
    if name in CFG.override:
        return CFG.override[name]
    return nc.dram_tensor(CFG.prefix + name, list(shape), dt, kind="ExternalOutput").ap()


class P1:
    def __init__(self, S, nc, es, srcs, xs_out, g_row, ident_d, ps_tr, name="p1"):
        self.S, self.nc, self.srcs, self.xs_out, self.ps_tr, self.name = S, nc, srcs, xs_out, ps_tr, name
        self.g_bc = mk(nc, es, name + "_g", [128, D], F32)
        self.ident = mk(nc, es, name + "_id", [128, 128], BF16)
        self.identf = mk(nc, es, name + "_idf", [128, 128], F32)
        g_bc, ident, identf = self.g_bc, self.ident, self.identf
        S.op("sp", lambda h: h.dma_start(out=g_bc[:], in_=g_row.partition_broadcast(128)), w=[name + "g"], dma=True)
        S.op("sp", lambda h: h.dma_start(out=identf[:], in_=ident_d), w=[name + "idf"], dma=True)
        S.op("dve", lambda h: h.tensor_copy(out=ident[:], in_=identf[:]), r=[name + "idf"], w=[name + "id"])
        self.NB = 2
        self.xt = [mk(nc, es, f"{name}_x{b}", [128, D], F32) for b in range(self.NB)]
        self.sq = mk(nc, es, name + "_sq", [128, D], BF16)
        self.xnb = [mk(nc, es, f"{name}_xn{b}", [128, D], BF16) for b in range(self.NB)]
        self.st = [mk(nc, es, f"{name}_st{b}", [128, 4], F32) for b in range(self.NB)]

    def tile(self, t, dst_ap, dst_buf):
        S, name = self.S, self.name
        xt, sq, xnb, st, g_bc, ident, ps_tr = self.xt, self.sq, self.xnb, self.st, self.g_bc, self.ident, self.ps_tr
        b = t % self.NB
        rows = slice(t * 128, (t + 1) * 128)
        xb = f"{name}x{b}"
        for k, src in enumerate(self.srcs):
            if k == 0:
                S.op("pool", lambda h, src=src: h.dma_start(out=xt[b][:], in_=src_rows(src, t)), w=[xb], dma=True)
            else:
                S.op("pool", lambda h, src=src: h.dma_start(out=xt[b][:], in_=src_rows(src, t), accum_op=ALU.add), r=[xb], w=[xb], dma=True)
        if self.xs_out is not None and len(self.srcs) > 1:
            S.op("sp", lambda h: h.dma_start(out=self.xs_out[rows, :], in_=xt[b][:]), r=[xb], dma=True)
        S.op("act", lambda h: h.activation(out=sq[:], in_=xt[b][:], func=AF.Square), r=[xb], w=[name + "sq"])
        S.op("dve", lambda h: h.tensor_reduce(out=st[b][:, 0:1], in_=sq[:], axis=AX.X, op=ALU.add), r=[name + "sq"], w=[f"{name}st{b}"])
        S.op("act", lambda h: h.activation(out=st[b][:, 1:2], in_=st[b][:, 0:1], func=AF.Sqrt, scale=1.0 / D, bias=EPS), r=[f"{name}st{b}"], w=[f"{name}st{b}"])
        S.op("dve", lambda h: h.reciprocal(out=st[b][:, 2:3], in_=st[b][:, 1:2]), r=[f"{name}st{b}"], w=[f"{name}st{b}r"])
        S.op("dve", lambda h: h.scalar_tensor_tensor(out=xnb[b][:], in0=xt[b][:], scalar=st[b][:, 2:3], in1=g_bc[:], op0=ALU.mult, op1=ALU.mult),
             r=[xb, f"{name}st{b}r", name + "g"], w=[f"{name}xn{b}"])
        for dc in range(8):
            S.op("pe", lambda h, dc=dc: h.transpose(out=ps_tr[:, dc * 128:(dc + 1) * 128], in_=xnb[b][:, dc * 128:(dc + 1) * 128], identity=ident[:]),
                 r=[f"{name}xn{b}", name + "id"], w=[name + "pstr"])
        S.op("act", lambda h: h.copy(out=dst_ap, in_=ps_tr[:].rearrange("p (c n) -> p c n", c=8)), r=[name + "pstr"], w=[dst_buf])


def phase1(S, nc, es, srcs, xs_out, g_row, ident_d, ps_tr, ntiles=None, xnT=None, name="p1"):
    if ntiles is None:
        ntiles = CFG.T // 128
    p1 = P1(S, nc, es, srcs, xs_out, g_row, ident_d, ps_tr, name)
    for t in range(ntiles):
        p1.tile(t, xnT[:, :, t * 128:(t + 1) * 128], f"xnT{t // 4}")
    return p1.ident


def dbg(S, nc, name, ap, shape, rbuf, dt=F32):
    o = nc.dram_tensor("dbg_" + name, list(shape), dt, kind="ExternalOutput").ap()
    S.op("sp", lambda h: h.dma_start(out=o, in_=ap), r=rbuf, dma=True)


NEGM = -30000.0


def build_A(nsrc=1, debug=False):
    T = CFG.T
    NT = T // 128
    nc = get_nc()
    es = ExitStack()
    S = get_sched(nc, es)
    srcs = [dram_in(nc, f"xin{k}", [T, D]) for k in range(nsrc)]
    xs_out = dram_out(nc, "xs", [T, D]) if nsrc > 1 else None
    g_row = dram_in(nc, "g", [1, D])
    ident_d = dram_in(nc, "ident", [128, 128])
    wq_d = dram_in(nc, "wq", [D, 512])
    wk_d = dram_in(nc, "wk", [D, 128])
    wv_d = dram_in(nc, "wv", [D, 128])
    wz_d = dram_in(nc, "wz", [D, 512])
    wo_d = dram_in(nc, "wo", [512, D])
    bias_d = dram_in(nc, "biasT", [128, 2, 2 * 4 * 128])
    sink_d = dram_in(nc, "sinks", [1, 8])
    p_out = dram_out(nc, "p", [T, D])

    xnT = mk(nc, es, "xnT", [128, 8, T], BF16)
    ps_tr = mk(nc, es, "ps_tr", [128, 1024], BF16, psum=True)
    ps_q = mk(nc, es, "ps_q", [128, 512], F32, psum=True)
    ps_z = mk(nc, es, "ps_z", [128, 512], F32, psum=True)
    ps_s = [mk(nc, es, f"ps_s{k}", [128, 512], F32, psum=True) for k in range(2)]
    ps_o = mk(nc, es, "ps_o", [128, 4, 128], F32, psum=True)
    ps_y = [mk(nc, es, f"ps_y{k}", [128, 512], F32, psum=True) for k in range(2)]
    ident = phase1(S, nc, es, srcs, xs_out, g_row, ident_d, ps_tr, xnT=xnT)

    wq = mk(nc, es, "wq_s", [128, 8, 512], BF16)
    wk = mk(nc, es, "wk_s", [128, 8, 128], BF16)
    wv = mk(nc, es, "wv_s", [128, 8, 128], BF16)
    wz = mk(nc, es, "wz_s", [128, 8, 512], BF16)
    wo = mk(nc, es, "wo_s", [128, 4, D], BF16)
    biasT = mk(nc, es, "biasT_s", [128, 2, 1024], BF16)
    esink = mk(nc, es, "esink", [128, 8], F32)
    for nm, t_, d_, pat in (("wq", wq, wq_d, "(c p) n -> p c n"), ("wk", wk, wk_d, "(c p) n -> p c n"), ("wv", wv, wv_d, "(c p) n -> p c n"),
                            ("wz", wz, wz_d, "(c p) n -> p c n"), ("wo", wo, wo_d, "(c p) n -> p c n")):
        S.op("pool", lambda h, t_=t_, d_=d_, pat=pat: h.dma_start(out=t_[:], in_=d_.rearrange(pat, p=128)), w=[nm], dma=True)
    S.op("pool", lambda h: h.dma_start(out=biasT[:], in_=bias_d), w=["biasT"], dma=True)
    S.op("sp", lambda h: h.dma_start(out=esink[:], in_=sink_d.partition_broadcast(128)), w=["esink"], dma=True)
    S.op("act", lambda h: h.activation(out=esink[:], in_=esink[:], func=AF.Exp), r=["esink"], w=["esink"])

    kT = mk(nc, es, "kT", [64, 2, T], BF16)
    vau = mk(nc, es, "vau", [128, NT, 2, 65], BF16)
    S.op("dve", lambda h: h.memset(vau[:, :, :, 64:65], 1.0), w=["vau_ones"])
    for g in range(2):
        for c in range(T // 512):
            tk = slice(c * 512, (c + 1) * 512)
            for dc in range(8):
                S.op("pe", lambda h, g=g, dc=dc, tk=tk: h.matmul(ps_q[0:64, :], lhsT=wk[:, dc, g * 64:(g + 1) * 64], rhs=xnT[:, dc, tk],
                                                                start=(dc == 0), stop=(dc == 7)), r=["wk", f"xnT{c}"], w=["ps_q"])
            S.op("act", lambda h, g=g, tk=tk: h.copy(out=kT[:, g, tk], in_=ps_q[0:64, :]), r=["ps_q"], w=[f"kT{c // 1}"])
    for t in range(NT):
        for dc in range(8):
            S.op("pe", lambda h, t=t, dc=dc: h.matmul(ps_z[:, 0:128], lhsT=xnT[:, dc, t * 128:(t + 1) * 128], rhs=wv[:, dc, :],
                                                      start=(dc == 0), stop=(dc == 7)), r=["wv", f"xnT{t // 4}"], w=["ps_z"])
        S.op("dve", lambda h, t=t: h.tensor_copy(out=vau[:, t, :, 0:64], in_=ps_z[:, 0:128].rearrange("p (g d) -> p g d", g=2)),
             r=["ps_z"], w=[f"vau{t}"])

    NB = 2
    qT = [mk(nc, es, f"qT{b}", [64, 2, 4, 128], BF16) for b in range(NB)]
    zs = [mk(nc, es, f"zs{b}", [128, 512], BF16) for b in range(NB)]
    pT = [mk(nc, es, f"pT{b}", [128, 512], BF16) for b in range(4)]
    yz = [mk(nc, es, f"yz{b}", [128, 512], BF16) for b in range(NB)]
    yzT = [mk(nc, es, f"yzT{b}", [128, 4, 128], BF16) for b in range(NB)]
    den = [mk(nc, es, f"den{b}", [128, 8], F32) for b in range(NB)]
    pt = [mk(nc, es, f"pt{b}", [128, D], F32) for b in range(NB)]
    pti = 0
    for qt in range(NT):
        b = qt % NB
        tq = slice(qt * 128, (qt + 1) * 128)
        xk = f"xnT{qt // 4}"
        for g in range(2):
            for r in range(4):
                for dc in range(8):
                    col = (g * 4 + r) * 64
                    S.op("pe", lambda h, g=g, r=r, dc=dc, col=col, tq=tq: h.matmul(ps_q[0:64, r * 128:(r + 1) * 128], lhsT=wq[:, dc, col:col + 64],
                                                                               rhs=xnT[:, dc, tq], start=(dc == 0), stop=(dc == 7)),
                         r=["wq", xk], w=["ps_q"])
            S.op("act", lambda h, g=g, b=b: h.activation(out=qT[b][:, g, :, :], in_=ps_q[0:64, :].rearrange("p (r n) -> p r n", r=4),
                                                        func=AF.Copy, scale=0.125), r=["ps_q"], w=[f"qT{b}_{g}"])
        for dc in range(8):
            S.op("pe", lambda h, dc=dc, tq=tq: h.matmul(ps_z[:, :], lhsT=xnT[:, dc, tq], rhs=wz[:, dc, :], start=(dc == 0), stop=(dc == 7)),
                 r=["wz", xk], w=["ps_z"])
        S.op("act", lambda h, b=b: h.activation(out=zs[b][:], in_=ps_z[:, :], func=AF.Silu), r=["ps_z"], w=[f"zs{b}"])
        for g in range(2):
            kts = [kt for kt in (qt - 1, qt) if kt >= 0]
            for kt in kts:
                cls = 0 if kt == qt else 1
                si = kt % 2
                pi = (g * 2 + si)
                S.op("pe", lambda h, g=g, kt=kt, si=si, b=b: h.matmul(ps_s[si][:, :], lhsT=kT[:, g, kt * 128:(kt + 1) * 128],
                                                                     rhs=qT[b][:, g, :, :].rearrange("p r n -> p (r n)"), start=True, stop=False),
                     r=[f"kT{kt // 4}", f"qT{b}_{g}"], w=[f"ps_s{si}"])
                S.op("pe", lambda h, g=g, cls=cls, si=si: h.matmul(ps_s[si][:, :], lhsT=ident[:], rhs=biasT[:, cls, g * 512:(g + 1) * 512],
                                                                  start=False, stop=True), r=["biasT", "p1id"], w=[f"ps_s{si}"])
                S.op("act", lambda h, si=si, pi=pi: h.activation(out=pT[pi][:], in_=ps_s[si][:, :], func=AF.Exp), r=[f"ps_s{si}"], w=[f"pT{pi}"])
            for r in range(4):
                for j, kt in enumerate(kts):
                    pi = (g * 2 + kt % 2)
                    S.op("pe", lambda h, g=g, r=r, kt=kt, pi=pi, j=j: h.matmul(ps_o[:, r, 0:65], lhsT=pT[pi][:, r * 128:(r + 1) * 128],
                                                                             rhs=vau[:, kt, g, :], start=(j == 0), stop=(j == len(kts) - 1)),
                         r=[f"pT{pi}", f"vau{kt}", "vau_ones"], w=["ps_o"])
            S.op("dve", lambda h, g=g, b=b: h.tensor_tensor(out=den[b][:, g * 4:(g + 1) * 4], in0=ps_o[:, :, 64], in1=esink[:, g * 4:(g + 1) * 4], op=ALU.add),
                 r=["ps_o", "esink"], w=[f"den{b}"])
            S.op("dve", lambda h, g=g, b=b: h.reciprocal(out=den[b][:, g * 4:(g + 1) * 4], in_=den[b][:, g * 4:(g + 1) * 4]), r=[f"den{b}"], w=[f"den{b}"])
            for r in range(4):
                col = (g * 4 + r) * 64
                S.op("dve", lambda h, g=g, r=r, b=b, col=col: h.scalar_tensor_tensor(out=yz[b][:, col:col + 64], in0=ps_o[:, r, 0:64],
                                                                                   scalar=den[b][:, g * 4 + r:g * 4 + r + 1], in1=zs[b][:, col:col + 64],
                                                                                   op0=ALU.mult, op1=ALU.mult),
                     r=["ps_o", f"den{b}", f"zs{b}"], w=[f"yz{b}"])
        for c in range(4):
            S.op("pe", lambda h, c=c, b=b: h.transpose(out=ps_tr[:, c * 128:(c + 1) * 128], in_=yz[b][:, c * 128:(c + 1) * 128], identity=ident[:]),
                 r=[f"yz{b}", "p1id"], w=["p1pstr"])
        S.op("act", lambda h, b=b: h.copy(out=yzT[b][:], in_=ps_tr[:, 0:512].rearrange("p (c n) -> p c n", c=4)), r=["p1pstr"], w=[f"yzT{b}"])
        for hf in range(2):
            for c in range(4):
                S.op("pe", lambda h, hf=hf, c=c, b=b: h.matmul(ps_y[hf][:, :], lhsT=yzT[b][:, c, :], rhs=wo[:, c, hf * 512:(hf + 1) * 512],
                                                              start=(c == 0), stop=(c == 3)), r=[f"yzT{b}", "wo"], w=[f"ps_y{hf}"])
            if hf == 0:
                S.op("act", lambda h, b=b: h.copy(out=pt[b][:, 0:512], in_=ps_y[0][:, :]), r=["ps_y0"], w=[f"pt{b}"])
            else:
                S.op("dve", lambda h, b=b: h.tensor_copy(out=pt[b][:, 512:1024], in_=ps_y[1][:, :]), r=["ps_y1"], w=[f"pt{b}"])
        S.op("sp", lambda h, b=b, tq=tq: h.dma_start(out=p_out[tq, :], in_=pt[b][:]), r=[f"pt{b}"], dma=True)
    return nc, es, S


def t5_bucket_np(d):
    import math
    d = np.maximum(d, 0)
    df = np.maximum(d, 1).astype(np.float32)
    large = 16 + (np.log(df / 16) / math.log(128 / 16) * 16).astype(np.int32)
    large = np.minimum(large, 31)
    return np.where(d < 16, d, large)


def prep_A(z, half):
    d = {}
    w_in = z['a_w_in'][0]
    d['wq'] = np.ascontiguousarray(w_in[:, half * 512:(half + 1) * 512])
    d['wk'] = np.ascontiguousarray(w_in[:, 1024 + half * 128:1024 + (half + 1) * 128])
    d['wv'] = np.ascontiguousarray(w_in[:, 1280 + half * 128:1280 + (half + 1) * 128])
    d['wz'] = np.ascontiguousarray(w_in[:, 1536 + half * 512:1536 + (half + 1) * 512])
    d['wo'] = np.ascontiguousarray(z['a_w_out'][0][half * 512:(half + 1) * 512, :])
    d['sinks'] = np.ascontiguousarray(z['a_sinks'][0][half * 8:(half + 1) * 8][None, :])
    table = z['t5_table']
    tk = np.arange(128)[:, None]
    tq = np.arange(128)[None, :]
    bias = np.zeros((128, 2, 2, 4, 128), np.float32)
    for cls in range(2):
        dist = tq - tk + 128 * cls
        valid = (dist >= 0) & (dist < 128)
        bk = t5_bucket_np(dist)
        for g in range(2):
            for r in range(4):
                hh = half * 8 + g * 4 + r
                bias[:, cls, g, r, :] = np.where(valid, table[bk, hh], NEGM)
    d['biasT'] = bias.reshape(128, 2, 1024)
    d['g'] = z['norm_g'][0:1].copy()
    d['ident'] = np.eye(128, dtype=np.float32)
    return d


KAP = 0.6065306597126334
GN_EPS = 64e-5


class _Stop(Exception):
    pass


def build_B(nsrc=1, MC=256, debug=False, stage=99):
    try:
        return _build_B(nsrc, MC, debug, stage)
    except _Stop as e:
        return e.args[0]


def _build_B(nsrc=1, MC=256, debug=False, stage=99):
    T = CFG.T
    NJ = MC // 64
    NMC = T // MC
    nc = get_nc()
    es = ExitStack()
    S = get_sched(nc, es)
    srcs = [dram_in(nc, f"xin{k}", [T, D]) for k in range(nsrc)]
    xs_out = dram_out(nc, "xs", [T, D]) if nsrc > 1 else None
    g_row = dram_in(nc, "g", [1, D])
    ident_d = dram_in(nc, "ident", [128, 128])
    w4_d = dram_in(nc, "w4", [4, D, 512])
    lw_d = dram_in(nc, "lw", [2, D, 64])
    l2_d = dram_in(nc, "l2", [2, 64, 512])
    wo_d = dram_in(nc, "wo", [512, D])
    mu_d = dram_in(nc, "muT", [128, 6, 8])
    vec_d = dram_in(nc, "vecs", [64, 8, 8])
    lnw_d = dram_in(nc, "lnw", [1, 512])
    lnb_d = dram_in(nc, "lnb", [1, 512])
    mg_d = dram_in(nc, "maskG", [128, 128])
    mnt_d = dram_in(nc, "maskNT", [64, 64])
    rm_d = dram_in(nc, "resetm", [64, MC])
    p_out = dram_out(nc, "p", [T, D])

    ps_tr = mk(nc, es, "ps_tr", [128, 1024], BF16, psum=True)
    ps_proj = mk(nc, es, "ps_proj", [128, 512], F32, psum=True)
    ps_tok = mk(nc, es, "ps_tok", [128, 512], F32, psum=True)
    ps_bv = mk(nc, es, "ps_bv", [128, 512], F32, psum=True)
    ps_g = mk(nc, es, "ps_g", [128, 512], F32, psum=True)
    ps_n = mk(nc, es, "ps_n", [128, 512], F32, psum=True)
    ps_rec = mk(nc, es, "ps_rec", [128, 512], F32, psum=True)
    ps_y = mk(nc, es, "ps_y", [128, 512], F32, psum=True)

    g_bc = mk(nc, es, "g_bc", [128, D], F32)
    identf = mk(nc, es, "identf", [128, 128], F32)
    ident = mk(nc, es, "identb", [128, 128], BF16)
    S.op("sp", lambda h: h.dma_start(out=g_bc[:], in_=g_row.partition_broadcast(128)), w=["g"], dma=True)
    S.op("sp", lambda h: h.dma_start(out=identf[:], in_=ident_d), w=["identf"], dma=True)
    S.op("dve", lambda h: h.tensor_copy(out=ident[:], in_=identf[:]), r=["identf"], w=["ident"])
    W4 = mk(nc, es, "W4", [128, 4, 8, 512], BF16)
    W4m = mk(nc, es, "W4m", [128, 4, 8, 512], BF16)
    LW = mk(nc, es, "LW", [128, 2, 8, 64], BF16)
    LWm = mk(nc, es, "LWm", [128, 2, 8, 64], BF16)
    L2 = mk(nc, es, "L2", [64, 2, 512], BF16)
    wo = mk(nc, es, "wo_s", [128, 4, D], BF16)
    muT = mk(nc, es, "muT_s", [128, 6, 8], F32)
    vec = mk(nc, es, "vec_s", [64, 8, 8], F32)
    lnw = mk(nc, es, "lnw_s", [64, 512], F32)
    lnb = mk(nc, es, "lnb_s", [64, 512], F32)
    maskG = mk(nc, es, "maskG_s", [128, 128], F32)
    maskNT = mk(nc, es, "maskNT_s", [64, 64], F32)
    resetm = mk(nc, es, "resetm_s", [64, MC], F32)
    ones64 = mk(nc, es, "ones64", [64, 64], F32)
    S.op("pool", lambda h: h.dma_start(out=W4[:], in_=w4_d.rearrange("s (c p) n -> p s c n", p=128)), w=["W4"], dma=True)
    S.op("pool", lambda h: h.dma_start(out=LW[:], in_=lw_d.rearrange("s (c p) n -> p s c n", p=128)), w=["LW"], dma=True)
    S.op("pool", lambda h: h.dma_start(out=L2[:], in_=l2_d.rearrange("s k n -> k s n")), w=["L2"], dma=True)
    S.op("pool", lambda h: h.dma_start(out=wo[:], in_=wo_d.rearrange("(c p) n -> p c n", p=128)), w=["wo"], dma=True)
    S.op("sp", lambda h: h.dma_start(out=muT[:], in_=mu_d), w=["muT"], dma=True)
    S.op("sp", lambda h: h.dma_start(out=vec[:], in_=vec_d), w=["vec"], dma=True)
    S.op("sp", lambda h: h.dma_start(out=lnw[:], in_=lnw_d.partition_broadcast(64)), w=["lnw"], dma=True)
    S.op("sp", lambda h: h.dma_start(out=lnb[:], in_=lnb_d.partition_broadcast(64)), w=["lnb"], dma=True)
    S.op("sp", lambda h: h.dma_start(out=maskG[:], in_=mg_d), w=["maskG"], dma=True)
    S.op("sp", lambda h: h.dma_start(out=maskNT[:], in_=mnt_d), w=["maskNT"], dma=True)
    S.op("sp", lambda h: h.dma_start(out=resetm[:], in_=rm_d), w=["resetm"], dma=True)
    S.op("dve", lambda h: h.memset(ones64[:], 1.0), w=["ones64"])
    for s in range(4):
        for dc in range(8):
            S.op("pool" if dc % 2 else "dve", lambda h, s=s, dc=dc: h.tensor_scalar(out=W4m[:, s, dc, :], in0=W4[:, s, dc, :], scalar1=muT[:, s, dc:dc + 1], scalar2=None, op0=ALU.mult),
                 r=["W4", "muT"], w=["W4m"])
    for s in range(2):
        for dc in range(8):
            S.op("dve", lambda h, s=s, dc=dc: h.tensor_scalar(out=LWm[:, s, dc, :], in0=LW[:, s, dc, :], scalar1=muT[:, 4 + s, dc:dc + 1], scalar2=None, op0=ALU.mult),
                 r=["LW", "muT"], w=["LWm"])

    XW = 64 + MC
    xnT = mk(nc, es, "xnT", [128, 8, XW], BF16)
    xxT = mk(nc, es, "xxT", [128, 8, XW], BF16)
    S.op("dve", lambda h: h.memset(xnT[:, :, 0:64], 0.0), w=["xnT"])
    NB = 2
    xt = [mk(nc, es, f"xt{b}", [128, D], F32) for b in range(NB)]
    sq = mk(nc, es, "sq", [128, D], BF16)
    xnb = [mk(nc, es, f"xnb{b}", [128, D], BF16) for b in range(NB)]
    st = [mk(nc, es, f"st{b}", [128, 4], F32) for b in range(NB)]
    h1T = mk(nc, es, "h1T", [64, 2, MC], BF16)
    vwin = mk(nc, es, "vwin", [64, NJ, 512], F32)
    uT = mk(nc, es, "uT", [64, NJ, 512], F32)
    zs = mk(nc, es, "zs", [64, NJ, 512], F32)
    y_all = mk(nc, es, "y_all", [64, NJ, 512], F32)
    bv_all = mk(nc, es, "bv_all", [64, NJ, 512], F32)

    def ft(nm):
        return mk(nc, es, nm, [64, MC], F32)
    r_f = ft("r_f"); k_f = ft("k_f"); sig = ft("sig"); alp = ft("alp"); kk = ft("kk"); t1 = ft("t1"); t2 = ft("t2")
    cs = ft("cs"); kmod = ft("kmod"); bal = ft("bal")
    cLs = mk(nc, es, "cLs", [64, NJ], F32)
    G2 = 3
    cLd = [mk(nc, es, f"cLd{i}", [64, NJ], F32) for i in range(G2)]
    AR = [mk(nc, es, f"AR{i}", [64, NJ, 128], F32) for i in range(G2)]
    BK = [mk(nc, es, f"BK{i}", [64, NJ, 128], F32) for i in range(G2)]
    BKe = [mk(nc, es, f"BKe{i}", [64, NJ, 128], F32) for i in range(G2)]
    Gm = [mk(nc, es, f"Gm{i}", [64, NJ, 256], F32) for i in range(G2)]
    Tm = [mk(nc, es, f"Tm{i}", [64, NJ, 64], F32) for i in range(G2)]
    BKeT = [mk(nc, es, f"BKeT{i}", [64, NJ, 128], F32) for i in range(G2)]
    dcL = [mk(nc, es, f"dcL{i}", [64, NJ, 64], F32) for i in range(G2)]
    rkrp = [mk(nc, es, f"rkrp{i}", [64, NJ, 64], F32) for i in range(G2)]
    Dg = [mk(nc, es, f"Dg{i}", [64, NJ, 64], F32) for i in range(G2)]
    bon = [mk(nc, es, f"bon{i}", [64, NJ], F32) for i in range(G2)]
    Nk = [[mk(nc, es, f"Nk{j}_{i}", [64, 64], F32) for i in range(2)] for j in range(NJ)]
    NkT = [[mk(nc, es, f"NkT{j}_{i}", [64, 64], F32) for i in range(2)] for j in range(NJ)]
    Pm = [[mk(nc, es, f"Pm{j}_{i}", [64, 64], F32) for i in range(2)] for j in range(NJ)]
    ST = [[mk(nc, es, f"ST{h}_{i}", [64, 64], F32) for i in range(2)] for h in range(8)]
    WT = [mk(nc, es, f"WT{i}", [64, 64], F32) for i in range(2)]
    for h in range(8):
        S.op("dve", lambda hh, h=h: hh.memset(ST[h][0][:], 0.0), w=[f"ST{h}_0"])
    yn = mk(nc, es, "yn", [64, 512], F32)
    gst = mk(nc, es, "gst", [64, 4, 8], F32)
    yz = mk(nc, es, "yz", [64, 512], BF16)
    yzT = mk(nc, es, "yzT", [128, 4, 64], BF16)
    pt = mk(nc, es, "pt", [64, D], F32)

    def c3(t_):
        return t_[:].rearrange("p (c j) -> p c j", j=64)

    for mc in range(NMC):
        T0 = mc * MC
        if mc > 0:
            S.op("pool", lambda h: h.tensor_copy(out=xnT[:, :, 0:64], in_=xnT[:, :, MC:MC + 64]), r=["xnT"], w=["xnT"])
        for tl in range(MC // 128):
            t = (T0 // 128) + tl
            b = t % NB
            rows = slice(t * 128, (t + 1) * 128)
            for k, src in enumerate(srcs):
                if k == 0:
                    S.op("pool", lambda h, src=src, b=b, t=t: h.dma_start(out=xt[b][:], in_=src_rows(src, t)), w=[f"xt{b}"], dma=True)
                else:
                    S.op("pool", lambda h, src=src, b=b, t=t: h.dma_start(out=xt[b][:], in_=src_rows(src, t), accum_op=ALU.add),
                         r=[f"xt{b}"], w=[f"xt{b}"], dma=True)
            if xs_out is not None:
                S.op("sp", lambda h, b=b, rows=rows: h.dma_start(out=xs_out[rows, :], in_=xt[b][:]), r=[f"xt{b}"], dma=True)
            S.op("act", lambda h, b=b: h.activation(out=sq[:], in_=xt[b][:], func=AF.Square), r=[f"xt{b}"], w=["sq"])
            S.op("dve", lambda h, b=b: h.tensor_reduce(out=st[b][:, 0:1], in_=sq[:], axis=AX.X, op=ALU.add), r=["sq"], w=[f"st{b}"])
            S.op("act", lambda h, b=b: h.activation(out=st[b][:, 1:2], in_=st[b][:, 0:1], func=AF.Sqrt, scale=1.0 / D, bias=EPS), r=[f"st{b}"], w=[f"st{b}"])
            S.op("dve", lambda h, b=b: h.reciprocal(out=st[b][:, 2:3], in_=st[b][:, 1:2]), r=[f"st{b}"], w=[f"st{b}r"])
            S.op("dve", lambda h, b=b: h.scalar_tensor_tensor(out=xnb[b][:], in0=xt[b][:], scalar=st[b][:, 2:3], in1=g_bc[:], op0=ALU.mult, op1=ALU.mult),
                 r=[f"xt{b}", f"st{b}r", "g"], w=[f"xnb{b}"])
            for dc in range(8):
                S.op("pe", lambda h, dc=dc, b=b: h.transpose(out=ps_tr[:, dc * 128:(dc + 1) * 128], in_=xnb[b][:, dc * 128:(dc + 1) * 128], identity=ident[:]),
                     r=[f"xnb{b}", "ident"], w=["ps_tr"])
            S.op("act", lambda h, tl=tl: h.copy(out=xnT[:, :, 64 + tl * 128:64 + (tl + 1) * 128], in_=ps_tr[:].rearrange("p (c n) -> p c n", c=8)),
                 r=["ps_tr"], w=["xnT"])
        S.op("pool", lambda h: h.tensor_tensor(out=xxT[:, :, 1:XW], in0=xnT[:, :, 0:XW - 1], in1=xnT[:, :, 1:XW], op=ALU.subtract), r=["xnT"], w=["xxT"])
        tokc = slice(64, 64 + MC)

        def proj_fm(S, ps_ap, Wt, Wm, sidx, cols, M):
            n = 0
            for (Wx, X, xb) in ((Wt, xnT, "xnT"), (Wm, xxT, "xxT")):
                for dc in range(8):
                    S.op("pe", lambda h, Wx=Wx, X=X, dc=dc, n=n: h.matmul(ps_ap, lhsT=Wx[:, sidx, dc, cols], rhs=X[:, dc, tokc], start=(n == 0), stop=(n == 15)),
                         r=["W4", "W4m", "LW", "LWm", xb], w=["ps_proj"])
                    n += 1
        for s in range(2):
            proj_fm(S, ps_proj[0:64, 0:MC], LW, LWm, s, slice(0, 64), 64)
            S.op("act", lambda h, s=s: h.activation(out=h1T[:, s, :], in_=ps_proj[0:64, 0:MC], func=(AF.Tanh if s == 0 else AF.Copy)), r=["ps_proj"], w=["h1T"])
        for j in range(NJ):
            n = 0
            for (Wi, X, xb) in ((W4, xnT, "xnT"), (W4m, xxT, "xxT")):
                for dc in range(8):
                    S.op("pe", lambda h, Wi=Wi, X=X, dc=dc, n=n, j=j: h.matmul(ps_tok[0:64, :], lhsT=X[:, dc, 64 + j * 64:128 + j * 64], rhs=Wi[:, 2, dc, :], start=(n == 0), stop=(n == 15)),
                         r=["W4", "W4m", xb], w=["ps_tok"])
                    n += 1
            S.op("act", lambda h, j=j: h.copy(out=vwin[:, j, :], in_=ps_tok[0:64, :]), r=["ps_tok"], w=[f"vwin{j}"])
            n = 0
            for (Wi, X, xb) in ((W4, xnT, "xnT"), (W4m, xxT, "xxT")):
                for dc in range(8):
                    S.op("pe", lambda h, Wi=Wi, X=X, dc=dc, n=n, j=j: h.matmul(ps_tok[0:64, :], lhsT=X[:, dc, 64 + j * 64:128 + j * 64], rhs=Wi[:, 3, dc, :], start=(n == 0), stop=(n == 15)),
                         r=["W4", "W4m", xb], w=["ps_tok"])
                    n += 1
            S.op("act", lambda h, j=j: h.activation(out=zs[:, j, :], in_=ps_tok[0:64, :], func=AF.Silu), r=["ps_tok"], w=["zs"])

        def head_prep(S, hd):
            gi = hd % G2
            hc = slice(hd * 64, (hd + 1) * 64)
            vp = lambda c: vec[:, hd, c:c + 1]
            proj_fm(S, ps_proj[0:64, 0:MC], W4, W4m, 0, hc, 64)
            S.op("act", lambda h: h.copy(out=r_f[:], in_=ps_proj[0:64, 0:MC]), r=["ps_proj"], w=["r_f"])
            proj_fm(S, ps_proj[0:64, 0:MC], W4, W4m, 1, hc, 64)
            S.op("act", lambda h: h.copy(out=k_f[:], in_=ps_proj[0:64, 0:MC]), r=["ps_proj"], w=["k_f"])
            S.op("pe", lambda h, hc=hc: h.matmul(ps_proj[0:64, 0:MC], lhsT=L2[:, 0, hc], rhs=h1T[:, 0, :], start=True, stop=True), r=["L2", "h1T"], w=["ps_proj"])
            S.op("act", lambda h, hd=hd: h.activation(out=sig[:], in_=ps_proj[0:64, 0:MC], func=AF.Sigmoid, bias=vec[:, hd, 0:1]), r=["ps_proj", "vec"], w=["sig"])
            S.op("pe", lambda h, hc=hc: h.matmul(ps_proj[0:64, 0:MC], lhsT=L2[:, 1, hc], rhs=h1T[:, 1, :], start=True, stop=True), r=["L2", "h1T"], w=["ps_proj"])
            S.op("act", lambda h, hd=hd: h.activation(out=alp[:], in_=ps_proj[0:64, 0:MC], func=AF.Sigmoid, bias=vec[:, hd, 1:2]), r=["ps_proj", "vec"], w=["alp"])
            S.op("dve", lambda h, hd=hd: h.tensor_scalar(out=kk[:], in0=k_f[:], scalar1=vec[:, hd, 2:3], scalar2=None, op0=ALU.mult), r=["k_f", "vec"], w=["kk"])
            S.op("pool", lambda h: h.tensor_tensor(out=t1[:], in0=kk[:], in1=kk[:], op=ALU.mult), r=["kk"], w=["t1"])
            S.op("pe", lambda h: h.matmul(ps_proj[0:64, 0:MC], lhsT=ones64[:], rhs=t1[:], start=True, stop=True), r=["ones64", "t1"], w=["ps_proj"])
            S.op("act", lambda h: h.activation(out=t2[:], in_=ps_proj[0:64, 0:MC], func=AF.Sqrt), r=["ps_proj"], w=["t2"])
            S.op("dve", lambda h: h.tensor_scalar(out=t2[:], in0=t2[:], scalar1=1e-12, scalar2=None, op0=ALU.max), r=["t2"], w=["t2"])
            S.op("dve", lambda h: h.reciprocal(out=t2[:], in_=t2[:]), r=["t2"], w=["t2"])
            S.op("dve", lambda h: h.tensor_tensor(out=kk[:], in0=kk[:], in1=t2[:], op=ALU.mult), r=["kk", "t2"], w=["kk"])
            S.op("dve", lambda h, hd=hd: h.tensor_scalar(out=t1[:], in0=alp[:], scalar1=1.0, scalar2=vec[:, hd, 3:4], op0=ALU.subtract, op1=ALU.mult), r=["alp", "vec"], w=["t1"])
            S.op("dve", lambda h: h.scalar_tensor_tensor(out=kmod[:], in0=t1[:], scalar=1.0, in1=k_f[:], op0=ALU.add, op1=ALU.mult), r=["t1", "k_f"], w=["kmod"])
            S.op("pool", lambda h: h.tensor_tensor(out=bal[:], in0=kk[:], in1=alp[:], op=ALU.mult), r=["kk", "alp"], w=["bal"])
            S.op("dve", lambda h: h.tensor_tensor_scan(out=cs[:], data0=resetm[:], data1=sig[:], initial=0.0, op0=ALU.mult, op1=ALU.add), r=["resetm", "sig"], w=["cs"])
            S.op("dve", lambda h: h.tensor_copy(out=cLs[:], in_=cs[:, 63::64]), r=["cs"], w=["cLs"])
            S.op("act", lambda h, gi=gi: h.activation(out=cLd[gi][:], in_=cLs[:], func=AF.Exp, scale=-KAP), r=["cLs"], w=[f"cLd{gi}"])
            S.op("act", lambda h: h.activation(out=t1[:], in_=cs[:], func=AF.Exp, scale=-KAP), r=["cs"], w=["t1"])
            S.op("dve", lambda h, gi=gi: h.tensor_tensor(out=AR[gi][:, :, 64:128], in0=c3(r_f), in1=c3(t1), op=ALU.mult), r=["r_f", "t1"], w=[f"AR{gi}"])
            S.op("act", lambda h: h.activation(out=t2[:], in_=cs[:], func=AF.Exp, scale=KAP), r=["cs"], w=["t2"])
            S.op("dve", lambda h, gi=gi: h.tensor_tensor(out=BK[gi][:, :, 0:64], in0=c3(bal), in1=c3(t2), op=ALU.mult), r=["bal", "t2"], w=[f"BK{gi}"])
            S.op("pool", lambda h, gi=gi: h.tensor_tensor(out=BK[gi][:, :, 64:128], in0=c3(kmod), in1=c3(t2), op=ALU.mult), r=["kmod", "t2"], w=[f"BK{gi}"])
            S.op("pool", lambda h: h.tensor_tensor(out=t1[:], in0=cs[:], in1=sig[:], op=ALU.subtract), r=["cs", "sig"], w=["t1"])
            S.op("act", lambda h: h.activation(out=t1[:], in_=t1[:], func=AF.Exp, scale=-KAP), r=["t1"], w=["t1"])
            S.op("dve", lambda h, gi=gi: h.scalar_tensor_tensor(out=AR[gi][:, :, 0:64], in0=c3(kk), scalar=-1.0, in1=c3(t1), op0=ALU.mult, op1=ALU.mult),
                 r=["kk", "t1"], w=[f"AR{gi}"])
            S.op("dve", lambda h: h.tensor_tensor(out=c3(t2), in0=c3(cs), in1=cLs[:].unsqueeze(2).broadcast_to([64, NJ, 64]), op=ALU.subtract), r=["cs", "cLs"], w=["t2"])
            S.op("act", lambda h: h.activation(out=t2[:], in_=t2[:], func=AF.Exp, scale=KAP), r=["t2"], w=["t2"])
            S.op("dve", lambda h, gi=gi: h.tensor_tensor(out=BKe[gi][:, :, 0:64], in0=c3(bal), in1=c3(t2), op=ALU.mult), r=["bal", "t2"], w=[f"BKe{gi}"])
            S.op("pool", lambda h, gi=gi: h.tensor_tensor(out=BKe[gi][:, :, 64:128], in0=c3(kmod), in1=c3(t2), op=ALU.mult), r=["kmod", "t2"], w=[f"BKe{gi}"])
            S.op("dve", lambda h, gi=gi, hd=hd: h.scalar_tensor_tensor(out=rkrp[gi][:, :, :], in0=c3(r_f), scalar=vec[:, hd, 4:5], in1=c3(kmod), op0=ALU.mult, op1=ALU.mult),
                 r=["r_f", "kmod", "vec"], w=[f"rkrp{gi}"])
        def head_gn(S, hd):
            gi = hd % G2
            hc = slice(hd * 64, (hd + 1) * 64)
            def head_g(S, j):
                psn = ps_n if j == 0 else ps_tok
                psn_name = "ps_n" if j == 0 else "ps_tok"
                Nk_, NkT_, Pm_ = Nk[j], NkT[j], Pm[j]
                S.op("pe", lambda h, gi=gi, j=j: h.matmul(ps_g[0:64, 0:128], lhsT=BK[gi][:, j, 0:64], rhs=AR[gi][:, j, :], start=True, stop=True), r=[f"BK{gi}", f"AR{gi}"], w=["ps_g"])
                S.op("pe", lambda h, gi=gi, j=j: h.matmul(ps_g[0:64, 128:256], lhsT=BK[gi][:, j, 64:128], rhs=AR[gi][:, j, :], start=True, stop=True), r=[f"BK{gi}", f"AR{gi}"], w=["ps_g"])
                S.op("pe", lambda h, gi=gi, j=j: h.matmul(ps_g[0:64, 256:320], lhsT=AR[gi][:, j, 0:64], rhs=BK[gi][:, j, 0:64], start=True, stop=True), r=[f"BK{gi}", f"AR{gi}"], w=["ps_g"])
                S.op("dve", lambda h, gi=gi, j=j: h.tensor_tensor(out=Gm[gi][:, j, 0:128], in0=ps_g[0:64, 0:128], in1=maskG[0:64, :], op=ALU.mult), r=["ps_g", "maskG"], w=[f"Gm{gi}"])
                S.op("dve", lambda h, gi=gi, j=j: h.tensor_tensor(out=Gm[gi][:, j, 128:256], in0=ps_g[0:64, 128:256], in1=maskG[0:64, :], op=ALU.mult), r=["ps_g", "maskG"], w=[f"Gm{gi}"])
                S.op("dve", lambda h: h.tensor_tensor(out=NkT_[0][:], in0=ps_g[0:64, 256:320], in1=maskNT[:], op=ALU.mult), r=["ps_g", "maskNT"], w=[f"NkT{j}_0"])
                S.op("pool", lambda h, gi=gi, j=j: h.tensor_copy(out=Nk_[0][:], in_=Gm[gi][:, j, 0:64]), r=[f"Gm{gi}"], w=[f"Nk{j}_0"])
                S.op("pool", lambda h, gi=gi, j=j: h.tensor_tensor(out=Pm_[0][:], in0=Gm[gi][:, j, 0:64], in1=identf[0:64, 0:64], op=ALU.add), r=[f"Gm{gi}", "identf"], w=[f"Pm{j}_0"])
                for q in range(2):
                    S.op("pe", lambda h, gi=gi, j=j, q=q: h.transpose(out=ps_g[0:64, 320 + q * 64:384 + q * 64], in_=BKe[gi][:, j, q * 64:(q + 1) * 64], identity=identf[0:64, 0:64]), r=[f"BKe{gi}", "identf"], w=["ps_g"])
                S.op("act", lambda h, gi=gi, j=j: h.copy(out=BKeT[gi][:, j, :], in_=ps_g[0:64, 320:448]), r=["ps_g"], w=[f"BKeT{gi}"])
                S.op("pool", lambda h, gi=gi, j=j: h.tensor_scalar(out=dcL[gi][:, j, :], in0=identf[0:64, 0:64], scalar1=cLd[gi][:, j:j + 1], scalar2=None, op0=ALU.mult), r=["identf", f"cLd{gi}"], w=[f"dcL{gi}"])
                S.op("pe", lambda h, gi=gi, j=j: h.matmul(ps_g[0:64, 448 + j:449 + j], lhsT=rkrp[gi][:, j, :], rhs=ones64[:, 0:1], start=True, stop=True), r=[f"rkrp{gi}", "ones64"], w=["ps_g"])
                S.op("act", lambda h, gi=gi, j=j: h.copy(out=bon[gi][:, j:j + 1], in_=ps_g[0:64, 448 + j:449 + j]), r=["ps_g"], w=[f"bon{gi}"])
            def head_n(S, j):
                psn = ps_n if j == 0 else ps_tok
                psn_name = "ps_n" if j == 0 else "ps_tok"
                Nk_, NkT_, Pm_ = Nk[j], NkT[j], Pm[j]
                cur = 0
                for sidx in range(5):
                    nx = 1 - cur
                    last = (sidx == 4)
                    S.op("pe", lambda h, cur=cur: h.matmul(psn[0:64, 0:64], lhsT=Nk_[cur][:], rhs=NkT_[cur][:], start=True, stop=True), r=[f"Nk{j}_{cur}", f"NkT{j}_{cur}"], w=[psn_name])
                    S.op("act", lambda h, nx=nx: h.copy(out=NkT_[nx][:], in_=psn[0:64, 0:64]), r=[psn_name], w=[f"NkT{j}_{nx}"])
                    if not last:
                        S.op("pe", lambda h, cur=cur: h.matmul(psn[0:64, 64:128], lhsT=NkT_[cur][:], rhs=Nk_[cur][:], start=True, stop=True), r=[f"Nk{j}_{cur}", f"NkT{j}_{cur}"], w=[psn_name])
                        S.op("act", lambda h, nx=nx: h.copy(out=Nk_[nx][:], in_=psn[0:64, 64:128]), r=[psn_name], w=[f"Nk{j}_{nx}"])
                    S.op("pe", lambda h, cur=cur, nx=nx: h.matmul(psn[0:64, 128:192], lhsT=NkT_[nx][:], rhs=Pm_[cur][:], start=True, stop=True), r=[f"NkT{j}_{nx}", f"Pm{j}_{cur}"], w=[psn_name])
                    if last:
                        S.op("dve", lambda h, cur=cur, gi=gi, j=j: h.tensor_tensor(out=Tm[gi][:, j, :], in0=psn[0:64, 128:192], in1=Pm_[cur][:], op=ALU.add), r=[psn_name, f"Pm{j}_{cur}"], w=[f"Tm{gi}"])
                    else:
                        S.op("dve", lambda h, cur=cur, nx=nx: h.tensor_tensor(out=Pm_[nx][:], in0=psn[0:64, 128:192], in1=Pm_[cur][:], op=ALU.add), r=[psn_name, f"Pm{j}_{cur}"], w=[f"Pm{j}_{nx}"])
                    cur = nx
            for j in range(NJ):
                head_g(S, j)
            sj = [Stream() for _ in range(NJ)]
            for j in range(NJ):
                head_n(sj[j], j)
            merge_streams(S, sj)
            for j in range(NJ):
                S.op("dve", lambda h, gi=gi, j=j: h.tensor_scalar(out=Dg[gi][:, j, :], in0=identf[0:64, 0:64], scalar1=bon[gi][:, j:j + 1], scalar2=None, op0=ALU.mult),
                     r=["identf", f"bon{gi}"], w=[f"Dg{gi}"])
        def head_rec(S, hd):
            gi = hd % G2
            hc = slice(hd * 64, (hd + 1) * 64)
            for j in range(NJ):
                gj = mc * NJ + j
                s_in = ST[hd][gj % 2]
                s_out = ST[hd][(gj + 1) % 2]
                sin_n = f"ST{hd}_{gj % 2}"
                sout_n = f"ST{hd}_{(gj + 1) % 2}"
                wi = gj % 2
                VT = vwin[:, j, hc]
                UT = uT[:, j, hc]
                vn = f"vwin{j}"
                un = f"uT{j}_{hd}"
                S.op("pe", lambda h, gi=gi, j=j, VT=VT, hc=hc: h.matmul(ps_bv[0:64, hc], lhsT=Dg[gi][:, j, :], rhs=VT, start=True, stop=True), r=[f"Dg{gi}", vn], w=["ps_bv"])
                S.op("act", lambda h, j=j, hc=hc: h.copy(out=bv_all[:, j, hc], in_=ps_bv[0:64, hc]), r=["ps_bv"], w=["bv_all"])
                S.op("pe", lambda h, gi=gi, j=j, s_in=s_in: h.matmul(ps_rec[0:64, 0:64], lhsT=AR[gi][:, j, 0:64], rhs=s_in[:], start=True, stop=False), r=[f"AR{gi}", sin_n], w=["ps_rec"])
                S.op("pe", lambda h, gi=gi, j=j, VT=VT: h.matmul(ps_rec[0:64, 0:64], lhsT=Gm[gi][:, j, 128:192], rhs=VT, start=False, stop=True), r=[f"Gm{gi}", vn], w=["ps_rec"])
                S.op("act", lambda h, wi=wi: h.copy(out=WT[wi][:], in_=ps_rec[0:64, 0:64]), r=["ps_rec"], w=[f"WT{wi}"])
                S.op("pe", lambda h, gi=gi, j=j, wi=wi: h.matmul(ps_rec[0:64, 64:128], lhsT=Tm[gi][:, j, :], rhs=WT[wi][:], start=True, stop=True), r=[f"Tm{gi}", f"WT{wi}"], w=["ps_rec"])
                S.op("act", lambda h, UT=UT: h.copy(out=UT, in_=ps_rec[0:64, 64:128]), r=["ps_rec"], w=[un])
                S.op("pe", lambda h, gi=gi, j=j, s_in=s_in, hc=hc: h.matmul(ps_y[0:64, hc], lhsT=AR[gi][:, j, 64:128], rhs=s_in[:], start=True, stop=False), r=[f"AR{gi}", sin_n], w=["ps_y"])
                S.op("pe", lambda h, gi=gi, j=j, hc=hc, UT=UT: h.matmul(ps_y[0:64, hc], lhsT=Gm[gi][:, j, 64:128], rhs=UT, start=False, stop=False), r=[f"Gm{gi}", un], w=["ps_y"])
                S.op("pe", lambda h, gi=gi, j=j, hc=hc, VT=VT: h.matmul(ps_y[0:64, hc], lhsT=Gm[gi][:, j, 192:256], rhs=VT, start=False, stop=True), r=[f"Gm{gi}", vn], w=["ps_y"])
                S.op("dve", lambda h, j=j, hc=hc: h.tensor_copy(out=y_all[:, j, hc], in_=ps_y[0:64, hc]), r=["ps_y"], w=["y_all"])
                S.op("pe", lambda h, gi=gi, j=j, s_in=s_in: h.matmul(ps_rec[0:64, 128:192], lhsT=dcL[gi][:, j, :], rhs=s_in[:], start=True, stop=False), r=[f"dcL{gi}", sin_n], w=["ps_rec"])
                S.op("pe", lambda h, gi=gi, j=j, UT=UT: h.matmul(ps_rec[0:64, 128:192], lhsT=BKeT[gi][:, j, 0:64], rhs=UT, start=False, stop=False), r=[f"BKeT{gi}", un], w=["ps_rec"])
                S.op("pe", lambda h, gi=gi, j=j, VT=VT: h.matmul(ps_rec[0:64, 128:192], lhsT=BKeT[gi][:, j, 64:128], rhs=VT, start=False, stop=True), r=[f"BKeT{gi}", vn], w=["ps_rec"])
                S.op("act", lambda h, s_out=s_out: h.copy(out=s_out[:], in_=ps_rec[0:64, 128:192]), r=["ps_rec"], w=[sout_n])
        for step in range(8 + 2):
            streams = []
            if step < 8:
                st_ = Stream()
                head_prep(st_, step)
                streams.append(st_)
            if 0 <= step - 1 < 8:
                st_ = Stream()
                head_gn(st_, step - 1)
                streams.append(st_)
            if 0 <= step - 2 < 8:
                st_ = Stream()
                head_rec(st_, step - 2)
                streams.append(st_)
            merge_streams(S, streams)
        for j in range(NJ):
            y3 = y_all[:, j, :].rearrange("p (h v) -> p h v", h=8)
            S.op("dve", lambda h, y3=y3: h.tensor_reduce(out=gst[:, 0, :], in_=y3, axis=AX.X, op=ALU.add), r=["y_all"], w=["gst"])
            S.op("act", lambda h, j=j: h.activation(out=yn[:], in_=y_all[:, j, :], func=AF.Square), r=["y_all"], w=["yn"])
            S.op("dve", lambda h: h.tensor_reduce(out=gst[:, 1, :], in_=yn[:].rearrange("p (h v) -> p h v", h=8), axis=AX.X, op=ALU.add), r=["yn"], w=["gst"])
            S.op("dve", lambda h: h.tensor_scalar(out=gst[:, 0, :], in0=gst[:, 0, :], scalar1=1.0 / 64, scalar2=None, op0=ALU.mult), r=["gst"], w=["gst"])
            S.op("dve", lambda h: h.tensor_tensor(out=gst[:, 2, :], in0=gst[:, 0, :], in1=gst[:, 0, :], op=ALU.mult), r=["gst"], w=["gst"])
            S.op("dve", lambda h: h.scalar_tensor_tensor(out=gst[:, 1, :], in0=gst[:, 1, :], scalar=1.0 / 64, in1=gst[:, 2, :], op0=ALU.mult, op1=ALU.subtract), r=["gst"], w=["gst"])
            S.op("act", lambda h: h.activation(out=gst[:, 1, :], in_=gst[:, 1, :], func=AF.Sqrt, bias=GN_EPS), r=["gst"], w=["gst"])
            S.op("dve", lambda h: h.reciprocal(out=gst[:, 1, :], in_=gst[:, 1, :]), r=["gst"], w=["gst"])
            for hd in range(8):
                hc = slice(hd * 64, (hd + 1) * 64)
                S.op("dve", lambda h, j=j, hd=hd, hc=hc: h.tensor_scalar(out=yn[:, hc], in0=y_all[:, j, hc], scalar1=gst[:, 0, hd:hd + 1], scalar2=gst[:, 1, hd:hd + 1],
                                                                      op0=ALU.subtract, op1=ALU.mult), r=["y_all", "gst"], w=["yn"])
            S.op("pool", lambda h: h.tensor_tensor(out=yn[:], in0=yn[:], in1=lnw[:], op=ALU.mult), r=["yn", "lnw"], w=["yn"])
            S.op("pool", lambda h: h.tensor_tensor(out=yn[:], in0=yn[:], in1=lnb[:], op=ALU.add), r=["yn", "lnb"], w=["yn"])
            S.op("pool", lambda h, j=j: h.tensor_tensor(out=yn[:], in0=yn[:], in1=bv_all[:, j, :], op=ALU.add), r=["yn", "bv_all"], w=["yn"])
            S.op("dve", lambda h, j=j: h.tensor_tensor(out=yz[:], in0=yn[:], in1=zs[:, j, :], op=ALU.mult), r=["yn", "zs"], w=["yz"])
            for c in range(4):
                S.op("pe", lambda h, c=c: h.transpose(out=ps_tr[:, c * 64:(c + 1) * 64], in_=yz[:, c * 128:(c + 1) * 128], identity=ident[0:64, 0:64]), r=["yz", "ident"], w=["ps_tr"])
            S.op("act", lambda h: h.copy(out=yzT[:], in_=ps_tr[:, 0:256].rearrange("p (c n) -> p c n", c=4)), r=["ps_tr"], w=["yzT"])
            for hf in range(2):
                for c in range(4):
                    S.op("pe", lambda h, hf=hf, c=c: h.matmul(ps_tok[0:64, :], lhsT=yzT[:, c, :], rhs=wo[:, c, hf * 512:(hf + 1) * 512], start=(c == 0), stop=(c == 3)), r=["yzT", "wo"], w=["ps_tok"])
                S.op("act", lambda h, hf=hf: h.copy(out=pt[:, hf * 512:(hf + 1) * 512], in_=ps_tok[0:64, :]), r=["ps_tok"], w=["pt"])
            rows = slice(T0 + j * 64, T0 + (j + 1) * 64)
            S.op("sp", lambda h, rows=rows: h.dma_start(out=p_out[rows, :], in_=pt[:]), r=["pt"], dma=True)
    return nc, es, S


def prep_B(z, half, MC=256):
    d = {}
    w_in = z['b_w_in'][0]
    own = slice(half * 512, (half + 1) * 512)
    d['w4'] = np.ascontiguousarray(np.stack([w_in[:, s * 1024:(s + 1) * 1024][:, own] for s in range(4)]))
    d['lw'] = np.ascontiguousarray(np.stack([z['b_w1'][0], z['b_a1'][0]]))
    d['l2'] = np.ascontiguousarray(np.stack([z['b_w2'][0][:, own], z['b_a2'][0][:, own]]))
    d['wo'] = np.ascontiguousarray(z['b_w_out'][0][own, :])
    mu = z['b_mu'][0]
    d['muT'] = np.ascontiguousarray(mu.reshape(6, 8, 128).transpose(2, 0, 1))
    vecs = np.zeros((64, 8, 8), np.float32)
    def fm(v):
        return v[own].reshape(8, 64).T
    vecs[:, :, 0] = fm(z['b_w0'][0]); vecs[:, :, 1] = fm(z['b_a0'][0]); vecs[:, :, 2] = fm(z['b_k_k'][0]); vecs[:, :, 3] = fm(z['b_k_a'][0])
    vecs[:, :, 4] = fm(z['b_r_k'][0].reshape(-1))
    d['vecs'] = vecs
    d['lnw'] = np.ascontiguousarray(z['b_lnx_w'][0][own][None, :])
    d['lnb'] = np.ascontiguousarray(z['b_lnx_b'][0][own][None, :])
    j = np.arange(64)[:, None]; i = np.arange(64)[None, :]
    strict = (j < i).astype(np.float32); incl = (j <= i).astype(np.float32)
    row = np.concatenate([strict, incl], 1)
    d['maskG'] = np.ascontiguousarray(np.concatenate([row, row], 0))
    d['maskNT'] = np.ascontiguousarray(strict.T)
    rm = np.ones((64, MC), np.float32); rm[:, ::64] = 0.0
    d['resetm'] = rm
    d['g'] = z['norm_g'][1:2].copy()
    d['ident'] = np.eye(128, dtype=np.float32)
    return d


NEGM = -30000.0


def build_C(nsrc=1, debug=False):
    T = CFG.T
    NT = T // 128
    NCMP = T // 16 - 1
    NKT = (NCMP + 127) // 128
    nc = get_nc()
    es = ExitStack()
    S = get_sched(nc, es)
    srcs = [dram_in(nc, f"xin{k}", [T, D]) for k in range(nsrc)]
    xs_out = dram_out(nc, "xs", [T, D]) if nsrc > 1 else None
    g_row = dram_in(nc, "g", [1, D])
    ident_d = dram_in(nc, "ident", [128, 128])
    wq_d = dram_in(nc, "wq", [D, 512])
    wkv_d = dram_in(nc, "wkv", [D, 6, 128])
    wg_d = dram_in(nc, "wg", [D, 24])
    wz_d = dram_in(nc, "wz", [D, 512])
    wo_d = dram_in(nc, "wo", [512, D])
    w1_d = dram_in(nc, "w1", [2, 64, 32, 128])
    w2_d = dram_in(nc, "w2", [2, 128, 64])
    pos_d = dram_in(nc, "posT", [2, 64, 32])
    bias_d = dram_in(nc, "biasT", [128, 4, 1024])
    F4_d = dram_in(nc, "F4", [512, 512])
    ka_d = dram_in(nc, "keepadd", [NT, 128, 128])
    E_d = dram_in(nc, "E", [64, NT, 128])
    ov_d = dram_in(nc, "ovl", [128, 2, 64])
    p_out = dram_out(nc, "p", [T, D])

    ps_tr = mk(nc, es, "ps_tr", [128, 1024], BF16, psum=True)
    ps_q = mk(nc, es, "ps_q", [128, 512], F32, psum=True)
    ps_z = mk(nc, es, "ps_z", [128, 512], F32, psum=True)
    ps_s = [mk(nc, es, f"ps_s{k}", [128, 512], F32, psum=True) for k in range(2)]
    po_c = mk(nc, es, "po_c", [128, 4, 128], F32, psum=True)
    po_s = mk(nc, es, "po_s", [128, 4, 128], F32, psum=True)
    po_w = mk(nc, es, "po_w", [128, 4, 128], F32, psum=True)
    p1 = P1(S, nc, es, srcs, xs_out, g_row, ident_d, ps_tr)
    ident = p1.ident
    xnTt = [mk(nc, es, f"xnTt{b}", [128, 8, 128], BF16) for b in range(2)]

    wq = mk(nc, es, "wq_s", [128, 8, 512], BF16)
    wkv = mk(nc, es, "wkv_s", [128, 8, 6, 128], BF16)
    wg = mk(nc, es, "wg_s", [128, 8, 24], BF16)
    wz = mk(nc, es, "wz_s", [128, 8, 512], BF16)
    wo = mk(nc, es, "wo_s", [128, 4, D], BF16)
    w1 = mk(nc, es, "w1_s", [64, 2, 32, 128], BF16)
    w2 = mk(nc, es, "w2_s", [128, 2, 64], BF16)
    posT = mk(nc, es, "posT_s", [64, 2, 32], BF16)
    biasT = mk(nc, es, "biasT_s", [128, 4, 1024], BF16)
    Em = mk(nc, es, "E_s", [64, NT, 128], BF16)
    S.op("pool", lambda h: h.dma_start(out=wq[:], in_=wq_d.rearrange("(c p) n -> p c n", p=128)), w=["wq"], dma=True)
    S.op("pool", lambda h: h.dma_start(out=wkv[:], in_=wkv_d.rearrange("(c p) s n -> p c s n", p=128)), w=["wkv"], dma=True)
    S.op("pool", lambda h: h.dma_start(out=wg[:], in_=wg_d.rearrange("(c p) n -> p c n", p=128)), w=["wg"], dma=True)
    S.op("pool", lambda h: h.dma_start(out=wz[:], in_=wz_d.rearrange("(c p) n -> p c n", p=128)), w=["wz"], dma=True)
    S.op("pool", lambda h: h.dma_start(out=wo[:], in_=wo_d.rearrange("(c p) n -> p c n", p=128)), w=["wo"], dma=True)
    S.op("pool", lambda h: h.dma_start(out=w1[:], in_=w1_d.rearrange("s d l h -> d s l h")), w=["w1"], dma=True)
    S.op("pool", lambda h: h.dma_start(out=w2[:], in_=w2_d.rearrange("s h d -> h s d")), w=["w2"], dma=True)
    S.op("pool", lambda h: h.dma_start(out=posT[:], in_=pos_d.rearrange("s d l -> d s l")), w=["posT"], dma=True)
    S.op("pool", lambda h: h.dma_start(out=biasT[:], in_=bias_d), w=["biasT"], dma=True)
    S.op("pool", lambda h: h.dma_start(out=Em[:], in_=E_d), w=["E"], dma=True)

    kvT = mk(nc, es, "kvT", [64, 2, 2, T], BF16)
    roll = mk(nc, es, "roll", [64, 2, 2, 144], BF16)
    vau = mk(nc, es, "vau", [128, NT, 2, 2, 65], BF16)
    S.op("dve", lambda h: h.memset(vau[:, :, :, :, 64:65], 1.0), w=["vau_ones"])
    S.op("dve", lambda h: h.memset(roll[:], 0.0), w=["roll"])
    kcmpT = mk(nc, es, "kcmpT", [64, 2, 256], BF16)
    vcau = mk(nc, es, "vcau", [128, 2, 2, 65], BF16)
    ovl = mk(nc, es, "ovl_s", [128, 2, 64], BF16)
    hidn = mk(nc, es, "hidn", [128, 4, 8], BF16)
    hidv = mk(nc, es, "hidv", [128, 2, 256], BF16)
    pbias = mk(nc, es, "pbias", [128, 2], F32)
    S.op("dve", lambda h: h.memset(kcmpT[:], 0.0), w=["kcmpT"])
    S.op("dve", lambda h: h.memset(vcau[:], 0.0), w=["vcau"])
    S.op("dve", lambda h: h.memset(vcau[:, :, :, 64:65], 1.0), r=["vcau"], w=["vcau"])
    S.op("dve", lambda h: h.memset(hidv[:], 0.0), w=["hidv"])
    S.op("pool", lambda h: h.dma_start(out=ovl[:], in_=ov_d), w=["ovl"], dma=True)
    for s in range(2):
        for l in range(32):
            S.op("pe", lambda h, s=s, l=l: h.matmul(ps_z[:, s:s + 1], lhsT=w1[:, s, l, :], rhs=posT[:, s, l:l + 1], start=(l == 0), stop=(l == 31)),
                 r=["w1", "posT"], w=["ps_z"])
        S.op("act", lambda h, s=s: h.copy(out=pbias[:, s:s + 1], in_=ps_z[:, s:s + 1]), r=["ps_z"], w=["pbias"])

    NB = 2
    qT = [mk(nc, es, f"qT{b}", [64, 2, 4, 128], BF16) for b in range(NB)]
    zs = [mk(nc, es, f"zs{b}", [128, 512], BF16) for b in range(NB)]
    gt = [mk(nc, es, f"gt{b}", [128, 24], F32) for b in range(NB)]
    NP = 4
    pT = [mk(nc, es, f"pT{b}", [128, 512], BF16) for b in range(NP)]
    F4t = [mk(nc, es, f"F4t{b}", [128, 512], BF16) for b in range(2)]
    ka = [mk(nc, es, f"ka{b}", [128, 128], F32) for b in range(NB)]
    imp = mk(nc, es, "imp", [128, 64], F32)
    imp2 = mk(nc, es, "imp2", [128, 64], F32)
    m8 = mk(nc, es, "m8", [128, 16], F32)
    nsel = mk(nc, es, "nsel", [128, 64], BF16)
    nselT = mk(nc, es, "nselT", [64, 4, 128], BF16)
    rden = mk(nc, es, "rden", [128, 3, 4], F32)
    cf = mk(nc, es, "cf", [128, 3, 4], F32)
    y = mk(nc, es, "y", [128, 512], F32)
    yz = [mk(nc, es, f"yz{b}", [128, 512], BF16) for b in range(NB)]
    yzT = [mk(nc, es, f"yzT{b}", [128, 4, 128], BF16) for b in range(NB)]
    pt = [mk(nc, es, f"pt{b}", [128, D], F32) for b in range(NB)]
    pcount = [0]
    scount = [0]

    def st_tile(g, b, lhsT_ap, lhs_bufs, extra, rhs_aug, rhs_bufs, po, first, last, ncol=65):
        si = scount[0] % 2
        scount[0] += 1
        pi = pcount[0] % NP
        pcount[0] += 1
        n_extra = len(extra)
        S.op("pe", lambda h: h.matmul(ps_s[si][:, :], lhsT=lhsT_ap, rhs=qT[b][:, g, :, :].rearrange("p r n -> p (r n)"), start=True, stop=(n_extra == 0)),
             r=lhs_bufs + [f"qT{b}_{g}"], w=[f"ps_s{si}"])
        for j, (el, er, ebufs) in enumerate(extra):
            S.op("pe", lambda h, el=el, er=er, j=j: h.matmul(ps_s[si][:, :], lhsT=el, rhs=er, start=False, stop=(j == n_extra - 1)),
                 r=ebufs, w=[f"ps_s{si}"])
        S.op("act", lambda h: h.activation(out=pT[pi][:], in_=ps_s[si][:, :], func=AF.Exp), r=[f"ps_s{si}"], w=[f"pT{pi}"])
        return pi

    for qt in range(NT):
        b = qt % NB
        tq = slice(qt * 128, (qt + 1) * 128)
        xk = f"xnTt{qt % 2}"
        xn = xnTt[qt % 2]
        S.op("sp", lambda h, b=b, qt=qt: h.dma_start(out=ka[b][:], in_=ka_d[qt]), w=[f"ka{b}"], dma=True)
        p1.tile(qt, xn[:, :, :], xk)
        if qt > 0:
            S.op("pool", lambda h: h.tensor_copy(out=roll[:, :, :, 0:16], in_=roll[:, :, :, 128:144]), r=["roll"], w=["roll"])
        for grp, (wss, psx, nm) in enumerate((((0, 1), ps_q, "ps_q"), ((2, 4), ps_z, "ps_z"))):
            for si, ws in enumerate(wss):
                for g in range(2):
                    c0 = (si * 2 + g) * 128
                    for dc in range(8):
                        S.op("pe", lambda h, ws=ws, g=g, dc=dc, c0=c0, psx=psx, xn=xn: h.matmul(psx[0:64, c0:c0 + 128], lhsT=wkv[:, dc, ws, g * 64:(g + 1) * 64], rhs=xn[:, dc, :],
                                                                                     start=(dc == 0), stop=(dc == 7)), r=["wkv", xk], w=[nm])
            if grp == 0:
                S.op("act", lambda h, psx=psx: h.copy(out=roll[:, :, :, 16:144], in_=psx[0:64, :].rearrange("p (s g n) -> p s g n", s=2, g=2)), r=[nm], w=["roll"])
            else:
                S.op("act", lambda h, psx=psx, tq=tq: h.copy(out=kvT[:, :, :, tq], in_=psx[0:64, :].rearrange("p (s g n) -> p s g n", s=2, g=2)), r=[nm], w=[f"kvT_{qt // 4}"])
        for jj, ws in enumerate((3, 5)):
            for dc in range(8):
                S.op("pe", lambda h, dc=dc, ws=ws, jj=jj, xn=xn: h.matmul(ps_z[:, jj * 128:(jj + 1) * 128], lhsT=xn[:, dc, :], rhs=wkv[:, dc, ws, :],
                                                                     start=(dc == 0), stop=(dc == 7)), r=["wkv", xk], w=["ps_z"])
        S.op("dve", lambda h, qt=qt: h.tensor_copy(out=vau[:, qt, :, :, 0:64], in_=ps_z[:, 0:256].rearrange("p (j g d) -> p j g d", j=2, g=2)),
             r=["ps_z"], w=[f"vau{qt}"])
        m0 = 1 if qt == 0 else 0
        nb = 8 - m0
        n0 = 8 * qt - 1 + m0
        for s in range(2):
            for g in range(2):
                c0 = (s * 2 + g) * 8
                for l in range(32):
                    S.op("pe", lambda h, s=s, g=g, l=l, c0=c0, nb=nb, m0=m0: h.matmul(ps_q[:, c0:c0 + nb], lhsT=w1[:, s, l, :], rhs=roll[:, s, g, l + 16 * m0:l + 16 * 7 + 1:16],
                                                                         start=(l == 0), stop=(l == 31)), r=["w1", "roll"], w=["ps_q"])
        for s in range(2):
            S.op("act", lambda h, s=s, nb=nb: h.activation(out=hidn[:, s * 2:s * 2 + 2, 0:nb], in_=ps_q[:, s * 16:s * 16 + 16].rearrange("p (g n) -> p g n", g=2)[:, :, 0:nb],
                                                    func=AF.Silu, bias=pbias[:, s:s + 1]), r=["ps_q", "pbias"], w=["hidn"])
        for g in range(2):
            S.op("pe", lambda h, g=g, nb=nb: h.matmul(ps_z[0:64, g * 8:g * 8 + nb], lhsT=w2[:, 0, :], rhs=hidn[:, g, 0:nb], start=True, stop=True), r=["w2", "hidn"], w=["ps_z"])
        S.op("act", lambda h, nb=nb, n0=n0: h.copy(out=kcmpT[:, :, n0:n0 + nb], in_=ps_z[0:64, 0:16].rearrange("p (g n) -> p g n", g=2)[:, :, 0:nb]), r=["ps_z"], w=["kcmpT"])
        S.op("pool", lambda h, nb=nb, n0=n0: h.tensor_copy(out=hidv[:, :, n0:n0 + nb], in_=hidn[:, 2:4, 0:nb]), r=["hidn"], w=["hidv"])
        for nt in sorted(set([n0 // 128, (n0 + nb - 1) // 128])):
            for g in range(2):
                S.op("pe", lambda h, g=g, nt=nt: h.matmul(ps_z[:, 128 + g * 64:192 + g * 64], lhsT=hidv[:, g, nt * 128:(nt + 1) * 128], rhs=w2[:, 1, :], start=True, stop=True),
                     r=["w2", "hidv"], w=["ps_z"])
            S.op("act", lambda h, nt=nt: h.copy(out=vcau[:, nt, :, 0:64], in_=ps_z[:, 128:256].rearrange("p (g d) -> p g d", g=2)), r=["ps_z"], w=["vcau"])
        for g in range(2):
            for r in range(4):
                for dc in range(8):
                    col = (g * 4 + r) * 64
                    S.op("pe", lambda h, g=g, r=r, dc=dc, col=col, xn=xn: h.matmul(ps_q[0:64, r * 128:(r + 1) * 128], lhsT=wq[:, dc, col:col + 64],
                                                                               rhs=xn[:, dc, :], start=(dc == 0), stop=(dc == 7)),
                         r=["wq", xk], w=["ps_q"])
            S.op("act", lambda h, g=g, b=b: h.activation(out=qT[b][:, g, :, :], in_=ps_q[0:64, :].rearrange("p (r n) -> p r n", r=4),
                                                        func=AF.Copy, scale=0.125), r=["ps_q"], w=[f"qT{b}_{g}"])
        for dc in range(8):
            S.op("pe", lambda h, dc=dc, xn=xn: h.matmul(ps_z[:, :], lhsT=xn[:, dc, :], rhs=wz[:, dc, :], start=(dc == 0), stop=(dc == 7)),
                 r=["wz", xk], w=["ps_z"])
        S.op("act", lambda h, b=b: h.activation(out=zs[b][:], in_=ps_z[:, :], func=AF.Silu), r=["ps_z"], w=[f"zs{b}"])
        for dc in range(8):
            S.op("pe", lambda h, dc=dc, xn=xn: h.matmul(ps_z[:, 0:24], lhsT=xn[:, dc, :], rhs=wg[:, dc, :], start=(dc == 0), stop=(dc == 7)),
                 r=["wg", xk], w=["ps_z"])
        S.op("act", lambda h, b=b: h.activation(out=gt[b][:], in_=ps_z[:, 0:24], func=AF.Sigmoid), r=["ps_z"], w=[f"gt{b}"])
        cnts = []
        for nt in range(NKT):
            mmax = nt * 128 + 127 - 8 * qt
            mmin = nt * 128 - 8 * qt
            if mmin > 6:
                continue
            masked = mmax > -2
            cnts.append((nt, masked))
        for (nt, masked) in cnts:
            if masked:
                j0 = 128 * nt - 8 * qt + 248
                S.op("pool", lambda h, nt=nt, j0=j0: h.dma_start(out=F4t[nt][:], in_=F4_d[j0:j0 + 128, :]), w=[f"F4t{nt}"], dma=True)
        for g in range(2):
            pis = []
            for (nt, masked) in cnts:
                extra = [(ident[:], F4t[nt][:], ["p1id", f"F4t{nt}"])] if masked else []
                pi = st_tile(g, b, kcmpT[:, g, nt * 128:(nt + 1) * 128], ["kcmpT"], extra, None, None, None, None, None)
                pis.append((nt, pi))
            for r in range(4):
                for j, (nt, pi) in enumerate(pis):
                    S.op("pe", lambda h, g=g, r=r, nt=nt, pi=pi, j=j: h.matmul(po_c[:, r, 0:65], lhsT=pT[pi][:, r * 128:(r + 1) * 128], rhs=vcau[:, nt, g, :],
                                                                             start=(j == 0), stop=(j == len(pis) - 1)), r=[f"pT{pi}", "vcau"], w=["po_c"])
            for r in range(4):
                for j, (nt, pi) in enumerate(pis):
                    S.op("pe", lambda h, g=g, r=r, nt=nt, pi=pi, j=j: h.matmul(po_w[:, r, 0:64], lhsT=pT[pi][:, r * 128:(r + 1) * 128], rhs=ovl[:, nt, :],
                                                                             start=(j == 0), stop=(j == len(pis) - 1)), r=[f"pT{pi}", "ovl"], w=["po_w"])
            S.op("dve", lambda h: h.tensor_scalar(out=rden[:, 0, :], in0=po_c[:, :, 64], scalar1=1e-30, scalar2=None, op0=ALU.add), r=["po_c"], w=["rden0"])
            S.op("dve", lambda h: h.reciprocal(out=rden[:, 0, :], in_=rden[:, 0, :]), r=["rden0"], w=["rden0"])
            S.op("dve", lambda h: h.tensor_scalar(out=imp[:], in0=po_w[:, 0, 0:64], scalar1=rden[:, 0, 0:1], scalar2=None, op0=ALU.mult), r=["po_w", "rden0"], w=["imp"])
            for r in range(1, 4):
                S.op("dve", lambda h, r=r: h.scalar_tensor_tensor(out=imp[:], in0=po_w[:, r, 0:64], scalar=rden[:, 0, r:r + 1], in1=imp[:], op0=ALU.mult, op1=ALU.add),
                     r=["po_w", "rden0", "imp"], w=["imp"])
            S.op("dve", lambda h, b=b: h.tensor_tensor(out=imp[:], in0=imp[:], in1=ka[b][:, 0:64], op=ALU.mult), r=["imp", f"ka{b}"], w=["imp"])
            S.op("dve", lambda h, b=b: h.tensor_tensor(out=imp[:], in0=imp[:], in1=ka[b][:, 64:128], op=ALU.add), r=["imp", f"ka{b}"], w=["imp"])
            S.op("dve", lambda h: h.max(out=m8[:, 0:8], in_=imp[:]), r=["imp"], w=["m8"])
            S.op("dve", lambda h: h.match_replace(out=imp2[:], in_to_replace=m8[:, 0:8], in_values=imp[:], imm_value=-3.0e38), r=["imp", "m8"], w=["imp2"])
            S.op("dve", lambda h: h.max(out=m8[:, 8:16], in_=imp2[:]), r=["imp2"], w=["m8"])
            S.op("dve", lambda h: h.tensor_scalar(out=imp2[:], in0=imp[:], scalar1=m8[:, 15:16], scalar2=1.0, op0=ALU.is_ge, op1=ALU.subtract),
                 r=["imp", "m8"], w=["imp2"])
            S.op("dve", lambda h: h.tensor_scalar(out=nsel[:], in0=imp2[:], scalar1=-NEGM, scalar2=None, op0=ALU.mult), r=["imp2"], w=["nsel"])
            S.op("pe", lambda h: h.transpose(out=ps_tr[0:64, 0:128], in_=nsel[:], identity=ident[:]), r=["nsel", "p1id"], w=["p1pstr"])
            for r in range(4):
                S.op("act", lambda h, r=r: h.copy(out=nselT[:, r, :], in_=ps_tr[0:64, 0:128]), r=["p1pstr"], w=["nselT"])
            kts = [kt for kt in range(qt - 4, qt + 1) if kt >= 0]
            jobs = [("s", kt, kt) for kt in range(qt + 1)] + [("w", kt, jj) for jj, kt in enumerate(kts)]

            def do_S(job, g=g, b=b, qt=qt):
                kind, kt, jj = job
                if kind == "s":
                    cls = 0 if kt == qt else (1 if kt == qt - 1 else 3)
                    extra = [(Em[:, kt, :], nselT[:].rearrange("p r n -> p (r n)"), ["E", "nselT"]),
                             (ident[:], biasT[:, cls, g * 512:(g + 1) * 512], ["p1id", "biasT"])]
                    return st_tile(g, b, kvT[:, 0, g, kt * 128:(kt + 1) * 128], [f"kvT_{kt // 4}"], extra, None, None, None, None, None)
                dq = qt - kt
                cls = 0 if dq == 0 else (1 if dq == 1 else (2 if dq == 4 else 3))
                extra = [(ident[:], biasT[:, cls, g * 512:(g + 1) * 512], ["p1id", "biasT"])]
                return st_tile(g, b, kvT[:, 1, g, kt * 128:(kt + 1) * 128], [f"kvT_{kt // 4}"], extra, None, None, None, None, None)

            def do_PV(job, pi, g=g, qt=qt, nk=len(kts)):
                kind, kt, jj = job
                for r in range(4):
                    if kind == "s":
                        S.op("pe", lambda h, r=r: h.matmul(po_s[:, r, 0:65], lhsT=pT[pi][:, r * 128:(r + 1) * 128], rhs=vau[:, kt, 0, g, :],
                                                           start=(kt == 0 and r == 0), stop=(kt == qt), skip_group_check=True),
                             r=[f"pT{pi}", f"vau{kt}", "vau_ones"], w=["po_s"])
                    else:
                        S.op("pe", lambda h, r=r: h.matmul(po_w[:, r, 0:65], lhsT=pT[pi][:, r * 128:(r + 1) * 128], rhs=vau[:, kt, 1, g, :],
                                                           start=(jj == 0 and r == 0), stop=(jj == nk - 1), skip_group_check=True),
                             r=[f"pT{pi}", f"vau{kt}", "vau_ones"], w=["po_w"])

            pend = None
            for job in jobs:
                pi_ = do_S(job)
                if pend is not None:
                    do_PV(*pend)
                pend = (job, pi_)
            do_PV(*pend)
            S.op("dve", lambda h: h.reciprocal(out=rden[:, 1, :], in_=po_s[:, :, 64]), r=["po_s"], w=["rden1"])
            S.op("dve", lambda h: h.reciprocal(out=rden[:, 2, :], in_=po_w[:, :, 64]), r=["po_w"], w=["rden2"])
            for j in range(3):
                S.op("dve", lambda h, j=j, g=g, b=b: h.tensor_tensor(out=cf[:, j, :], in0=rden[:, j, :], in1=gt[b][:, j * 8 + g * 4:j * 8 + g * 4 + 4], op=ALU.mult),
                     r=[f"rden{j}", f"gt{b}"], w=["cf"])
            for r in range(4):
                col = (g * 4 + r) * 64
                S.op("dve", lambda h, r=r, col=col: h.tensor_scalar(out=y[:, col:col + 64], in0=po_c[:, r, 0:64], scalar1=cf[:, 0, r:r + 1], scalar2=None, op0=ALU.mult),
                     r=["po_c", "cf"], w=["y"])
                S.op("dve", lambda h, r=r, col=col: h.scalar_tensor_tensor(out=y[:, col:col + 64], in0=po_s[:, r, 0:64], scalar=cf[:, 1, r:r + 1], in1=y[:, col:col + 64],
                                                                          op0=ALU.mult, op1=ALU.add), r=["po_s", "cf", "y"], w=["y"])
                S.op("dve", lambda h, r=r, col=col: h.scalar_tensor_tensor(out=y[:, col:col + 64], in0=po_w[:, r, 0:64], scalar=cf[:, 2, r:r + 1], in1=y[:, col:col + 64],
                                                                          op0=ALU.mult, op1=ALU.add), r=["po_w", "cf", "y"], w=["y"])
        S.op("pool", lambda h, b=b: h.tensor_tensor(out=yz[b][:], in0=y[:], in1=zs[b][:], op=ALU.mult), r=["y", f"zs{b}"], w=[f"yz{b}"])
        for c in range(4):
            S.op("pe", lambda h, c=c, b=b: h.transpose(out=ps_tr[:, c * 128:(c + 1) * 128], in_=yz[b][:, c * 128:(c + 1) * 128], identity=ident[:]),
                 r=[f"yz{b}", "p1id"], w=["p1pstr"])
        S.op("act", lambda h, b=b: h.copy(out=yzT[b][:], in_=ps_tr[:, 0:512].rearrange("p (c n) -> p c n", c=4)), r=["p1pstr"], w=[f"yzT{b}"])
        for hf in range(2):
            psy = ps_q if hf == 0 else ps_z
            nm = "ps_q" if hf == 0 else "ps_z"
            for c in range(4):
                S.op("pe", lambda h, hf=hf, c=c, b=b, psy=psy: h.matmul(psy[:, :], lhsT=yzT[b][:, c, :], rhs=wo[:, c, hf * 512:(hf + 1) * 512],
                                                                       start=(c == 0), stop=(c == 3)), r=[f"yzT{b}", "wo"], w=[nm])
            if hf == 0:
                S.op("act", lambda h, b=b, psy=psy: h.copy(out=pt[b][:, 0:512], in_=psy[:, :]), r=[nm], w=[f"pt{b}"])
            else:
                S.op("dve", lambda h, b=b, psy=psy: h.tensor_copy(out=pt[b][:, 512:1024], in_=psy[:, :]), r=[nm], w=[f"pt{b}"])
        S.op("sp", lambda h, b=b, tq=tq: h.dma_start(out=p_out[tq, :], in_=pt[b][:]), r=[f"pt{b}"], dma=True)
    return nc, es, S


def prep_C(z, half, T):
    NT = T // 128
    d = {}
    w_in = z['c_w_in'][0]
    d['wq'] = np.ascontiguousarray(w_in[:, half * 512:(half + 1) * 512])
    kv = []
    for s in range(6):
        base = 1024 + s * 256 + half * 128
        kv.append(w_in[:, base:base + 128])
    d['wkv'] = np.ascontiguousarray(np.stack(kv, 1))
    gcols = np.concatenate([2560 + j * 16 + half * 8 + np.arange(8) for j in range(3)])
    d['wg'] = np.ascontiguousarray(w_in[:, gcols])
    d['wz'] = np.ascontiguousarray(w_in[:, 2608 + half * 512:2608 + (half + 1) * 512])
    d['wo'] = np.ascontiguousarray(z['c_w_out'][0][half * 512:(half + 1) * 512, :])
    w1 = np.stack([z['c_cmp_k_w1'][0], z['c_cmp_v_w1'][0]])
    d['w1'] = np.ascontiguousarray(w1.reshape(2, 32, 64, 128).transpose(0, 2, 1, 3))
    d['w2'] = np.ascontiguousarray(np.stack([z['c_cmp_k_w2'][0], z['c_cmp_v_w2'][0]]))
    d['posT'] = np.ascontiguousarray(np.stack([z['c_cmp_pos_k'][0].T, z['c_cmp_pos_v'][0].T]))
    table = z['t5_table']
    tk = np.arange(128)[:, None]
    tq = np.arange(128)[None, :]
    bias = np.zeros((128, 4, 2, 4, 128), np.float32)
    for g in range(2):
        for r in range(4):
            hh = half * 8 + g * 4 + r
            d0 = tq - tk
            bias[:, 0, g, r, :] = np.where(d0 >= 0, table[t5_bucket_np(d0), hh], NEGM)
            d1 = tq - tk + 128
            bias[:, 1, g, r, :] = table[t5_bucket_np(d1), hh]
            bias[:, 2, g, r, :] = np.where(tq < tk, table[31, hh], NEGM)
            bias[:, 3, g, r, :] = table[31, hh]
    d['biasT'] = bias.reshape(128, 4, 1024)
    j = np.arange(512)[:, None]
    F = np.where(16 * (j - 248) + 31 <= tq, 0.0, NEGM).astype(np.float32)
    d['F4'] = np.ascontiguousarray(np.tile(F, (1, 4)))
    ka = np.zeros((NT, 128, 128), np.float32)
    sblk = np.arange(64)[None, :]
    for qt in range(NT):
        t = qt * 128 + np.arange(128)[:, None]
        cur = t // 64
        forced = (sblk == 0) | (sblk == cur) | (sblk == cur - 1)
        future = sblk * 64 > t
        ka[qt, :, 0:64] = np.where(forced | future, 0.0, 1.0)
        ka[qt, :, 64:128] = np.where(forced, 1e30, np.where(future, -1e30, 0.0))
    d['keepadd'] = ka
    E = np.zeros((64, NT, 128), np.float32)
    for kt in range(NT):
        E[2 * kt, kt, 0:64] = 1.0
        E[2 * kt + 1, kt, 64:128] = 1.0
    d['E'] = E
    n = np.arange(256)[:, None]
    s = np.arange(64)[None, :]
    ov = ((16 * n < 64 * s + 64) & (16 * n + 31 >= 64 * s)).astype(np.float32)
    d['ovl'] = np.ascontiguousarray(ov.reshape(2, 128, 64).transpose(1, 0, 2))
    d['g'] = z['norm_g'][2:3].copy()
    d['ident'] = np.eye(128, dtype=np.float32)
    return d


NBLK = 8
BW = 80


def build_D(nsrc=1, TCH=1024, debug=False):
    T = CFG.T
    nc = get_nc()
    es = ExitStack()
    S = get_sched(nc, es)
    srcs = [dram_in(nc, f"xin{k}", [T, D]) for k in range(nsrc)]
    xs_out = dram_out(nc, "xs", [T, D]) if nsrc > 1 else None
    g_row = dram_in(nc, "g", [1, D])
    ident_d = dram_in(nc, "ident", [128, 128])
    wu_d = dram_in(nc, "wu", [D, NBLK * BW])
    wz_d = dram_in(nc, "wz", [D, NBLK * BW])
    wo_d = dram_in(nc, "wo", [NBLK * BW, D])
    ga_d = dram_in(nc, "ga", [NBLK, BW, BW])
    gx_d = dram_in(nc, "gx", [NBLK, BW, BW])
    vec_d = dram_in(nc, "vecs", [BW, NBLK, 8])
    p_out = dram_out(nc, "p", [T, D])

    xnT = mk(nc, es, "xnT", [128, 8, T], BF16)
    ps = [mk(nc, es, f"ps{k}", [128, 512], F32, psum=True) for k in range(7)]
    ps_tr = mk(nc, es, "ps_tr", [128, 1024], BF16, psum=True)
    phase1(S, nc, es, srcs, xs_out, g_row, ident_d, ps_tr, xnT=xnT)

    wu = mk(nc, es, "wu_s", [128, 8, NBLK * BW], BF16)
    wz = mk(nc, es, "wz_s", [128, 8, NBLK * BW], BF16)
    wo = mk(nc, es, "wo_s", [BW, NBLK, D], BF16)
    ga = mk(nc, es, "ga_s", [BW, NBLK, BW], F32)
    gx = mk(nc, es, "gx_s", [BW, NBLK, BW], F32)
    vec = mk(nc, es, "vec_s", [BW, NBLK, 8], F32)
    der = mk(nc, es, "der_s", [BW, NBLK, 4], F32)
    S.op("pool", lambda h: h.dma_start(out=wu[:], in_=wu_d.rearrange("(c p) n -> p c n", p=128)), w=["wu"], dma=True)
    S.op("pool", lambda h: h.dma_start(out=wz[:], in_=wz_d.rearrange("(c p) n -> p c n", p=128)), w=["wz"], dma=True)
    S.op("pool", lambda h: h.dma_start(out=wo[:], in_=wo_d.rearrange("(b p) n -> p b n", p=BW)), w=["wo"], dma=True)
    S.op("sp", lambda h: h.dma_start(out=ga[:], in_=ga_d.rearrange("b p n -> p b n")), w=["ga"], dma=True)
    S.op("sp", lambda h: h.dma_start(out=gx[:], in_=gx_d.rearrange("b p n -> p b n")), w=["gx"], dma=True)
    S.op("sp", lambda h: h.dma_start(out=vec[:], in_=vec_d), w=["vec"], dma=True)
    S.op("act", lambda h: h.activation(out=der[:, :, 0:1], in_=vec[:, :, 7:8], func=AF.Exp, scale=-1.0), r=["vec"], w=["der"])
    S.op("act", lambda h: h.activation(out=der[:, :, 1:2], in_=der[:, :, 0:1], func=AF.Ln, bias=1.0), r=["der"], w=["der"])
    S.op("act", lambda h: h.mul(out=der[:, :, 2:3], in_=der[:, :, 1:2], mul=-8.0), r=["der"], w=["der"])

    NW = 2
    def wt(nm, cols=TCH, dt=F32):
        return [mk(nc, es, f"{nm}{b}", [BW, cols], dt) for b in range(NW)]
    u_t = wt("u_t", TCH + 3)
    uc_t = wt("uc_t"); zs_t = wt("zs_t"); r_t = wt("r_t"); i_t = wt("i_t"); a_t = r_t; m_t = [mk(nc, es, "m_t0", [BW, TCH], F32)] * NW; h_t = uc_t
    hz = mk(nc, es, "hz", [BW, NBLK, TCH], BF16)
    hlast = mk(nc, es, "hlast", [BW, NBLK], F32)
    uhalo = mk(nc, es, "uhalo", [BW, NBLK, 3], F32)
    pt = [mk(nc, es, "pt0", [128, D], F32)] * 2
    S.op("dve", lambda h: h.memset(hlast[:], 0.0), w=["hlast"])
    for b in range(NW):
        S.op("dve", lambda h, b=b: h.memset(u_t[b][:, 0:3], 0.0), w=[f"u{b}"])
    it = 0
    for tch in range(T // TCH):
        t0 = tch * TCH
        for blk in range(NBLK):
            b = it % NW
            pb = (it - 1) % NW
            it += 1
            cs = slice(blk * BW, (blk + 1) * BW)
            for hf in range(2):
                tk = slice(t0 + hf * 512, t0 + (hf + 1) * 512)
                for dc in range(8):
                    S.op("pe", lambda h, hf=hf, dc=dc, tk=tk, cs=cs: h.matmul(ps[hf][0:BW, :], lhsT=wu[:, dc, cs], rhs=xnT[:, dc, tk],
                                                                         start=(dc == 0), stop=(dc == 7)),
                         r=["wu", f"xnT{(t0 + hf * 512) // 512}"], w=[f"ps{hf}"])
            for hf in range(2):
                tk = slice(t0 + hf * 512, t0 + (hf + 1) * 512)
                for dc in range(8):
                    S.op("pe", lambda h, hf=hf, dc=dc, tk=tk, cs=cs: h.matmul(ps[2 + hf][0:BW, :], lhsT=wz[:, dc, cs], rhs=xnT[:, dc, tk],
                                                                         start=(dc == 0), stop=(dc == 7)),
                         r=["wz", f"xnT{(t0 + hf * 512) // 512}"], w=[f"ps{2 + hf}"])
            S.op("pool", lambda h, b=b, blk=blk: h.tensor_copy(out=u_t[b][:, 0:3], in_=uhalo[:, blk, :]), r=["uhalo%d" % blk], w=[f"u{b}"]) if tch > 0 else None
            for hf in range(2):
                S.op("act", lambda h, hf=hf, b=b: h.copy(out=u_t[b][:, 3 + hf * 512:3 + (hf + 1) * 512], in_=ps[hf][0:BW, :]),
                     r=[f"ps{hf}"], w=[f"u{b}"])
            for hf in range(2):
                S.op("act", lambda h, hf=hf, b=b: h.activation(out=zs_t[b][:, hf * 512:(hf + 1) * 512], in_=ps[2 + hf][0:BW, :], func=AF.Silu),
                     r=[f"ps{2 + hf}"], w=[f"zs{b}"])
            S.op("pool", lambda h, b=b, blk=blk: h.tensor_copy(out=uhalo[:, blk, :], in_=u_t[b][:, TCH:TCH + 3]), r=[f"u{b}"], w=["uhalo%d" % blk])
            if debug and tch == 0 and blk == 0:
                dbg(S, nc, "xnT", xnT[:, :, 0:512], [128, 8, 512], ["xnT0"], BF16)
                dbg(S, nc, "u", u_t[b][:], [BW, TCH + 3], [f"u{b}"])
                dbg(S, nc, "zs", zs_t[b][:], [BW, TCH], [f"zs{b}"])
            S.op("dve", lambda h, b=b, blk=blk: h.tensor_scalar(out=uc_t[b][:], in0=u_t[b][:, 3:3 + TCH], scalar1=vec[:, blk, 3:4], scalar2=vec[:, blk, 4:5],
                                                               op0=ALU.mult, op1=ALU.add), r=[f"u{b}", "vec"], w=[f"uc{b}"])
            for j in range(3):
                S.op("dve", lambda h, b=b, blk=blk, j=j: h.scalar_tensor_tensor(out=uc_t[b][:], in0=u_t[b][:, j:j + TCH], scalar=vec[:, blk, j:j + 1],
                                                                               in1=uc_t[b][:], op0=ALU.mult, op1=ALU.add),
                     r=[f"u{b}", "vec"], w=[f"uc{b}"])
            for hf in range(2):
                S.op("pe", lambda h, hf=hf, b=b, blk=blk: h.matmul(ps[4][0:BW, :] if hf == 0 else ps[5][0:BW, :], lhsT=ga[:, blk, :],
                                                                  rhs=uc_t[b][:, hf * 512:(hf + 1) * 512], start=True, stop=True),
                     r=["ga", f"uc{b}"], w=[f"ps{4 + hf}"])
                S.op("act", lambda h, hf=hf, b=b, blk=blk: h.activation(out=r_t[b][:, hf * 512:(hf + 1) * 512], in_=ps[4 + hf][0:BW, :], func=AF.Sigmoid,
                                                                       bias=vec[:, blk, 5:6]), r=[f"ps{4 + hf}", "vec"], w=[f"r{b}"])
            for hf in range(2):
                S.op("pe", lambda h, hf=hf, b=b, blk=blk: h.matmul(ps[4 + hf][0:BW, :], lhsT=gx[:, blk, :],
                                                                  rhs=uc_t[b][:, hf * 512:(hf + 1) * 512], start=True, stop=True),
                     r=["gx", f"uc{b}"], w=[f"ps{4 + hf}"])
                S.op("act", lambda h, hf=hf, b=b, blk=blk: h.activation(out=i_t[b][:, hf * 512:(hf + 1) * 512], in_=ps[4 + hf][0:BW, :], func=AF.Sigmoid,
                                                                       bias=vec[:, blk, 6:7]), r=[f"ps{4 + hf}", "vec"], w=[f"i{b}"])
            if debug and tch == 0 and blk == 0:
                dbg(S, nc, "uc", uc_t[b][:], [BW, TCH], [f"uc{b}"])
                dbg(S, nc, "r", r_t[b][:], [BW, TCH], [f"r{b}"])
                dbg(S, nc, "i", i_t[b][:], [BW, TCH], [f"i{b}"])
                dbg(S, nc, "der", der[:], [BW, NBLK, 4], ["der"])
            S.op("act", lambda h, b=b, blk=blk: h.activation(out=a_t[b][:], in_=r_t[b][:], func=AF.Exp, scale=der[:, blk, 2:3]),
                 r=[f"r{b}", "der"], w=[f"r{b}"])
            S.op("pool", lambda h, b=b: h.tensor_tensor(out=m_t[b][:], in0=a_t[b][:], in1=a_t[b][:], op=ALU.mult), r=[f"r{b}"], w=["m0"])
            S.op("act", lambda h, b=b: h.activation(out=m_t[b][:], in_=m_t[b][:], func=AF.Sqrt, scale=-1.0, bias=1.0), r=["m0"], w=["m0"])
            S.op("pool", lambda h, b=b: h.tensor_tensor(out=i_t[b][:], in0=i_t[b][:], in1=uc_t[b][:], op=ALU.mult), r=[f"i{b}", f"uc{b}"], w=[f"i{b}"])
            S.op("pool", lambda h, b=b: h.tensor_tensor(out=i_t[b][:], in0=i_t[b][:], in1=m_t[b][:], op=ALU.mult), r=[f"i{b}", "m0"], w=[f"i{b}"])
            if debug and tch == 0 and blk == 0:
                dbg(S, nc, "a", r_t[b][:], [BW, TCH], [f"r{b}"])
                dbg(S, nc, "m", m_t[b][:], [BW, TCH], ["m0"])
                dbg(S, nc, "bt", i_t[b][:], [BW, TCH], [f"i{b}"])
            S.op("dve", lambda h, b=b, blk=blk: h.tensor_tensor_scan(out=h_t[b][:], data0=a_t[b][:], data1=i_t[b][:], initial=hlast[:, blk:blk + 1],
                                                                    op0=ALU.mult, op1=ALU.add), r=[f"r{b}", f"i{b}", "hlast"], w=[f"uc{b}"])
            S.op("dve", lambda h, b=b, blk=blk: h.tensor_copy(out=hlast[:, blk:blk + 1], in_=h_t[b][:, TCH - 1:TCH]), r=[f"uc{b}"], w=["hlast"])
            S.op("dve", lambda h, b=b, blk=blk: h.tensor_tensor(out=hz[:, blk, :], in0=h_t[b][:], in1=zs_t[b][:], op=ALU.mult),
                 r=[f"uc{b}", f"zs{b}"], w=["hz"])
        if debug and tch == 0:
            dbg(S, nc, "hz", hz[:], [BW, NBLK, TCH], ["hz"], BF16)
        for tl in range(TCH // 128):
            pbuf = tl % 2
            for hf in range(2):
                for blk in range(NBLK):
                    S.op("pe", lambda h, tl=tl, hf=hf, blk=blk: h.matmul(ps[hf][:, :], lhsT=hz[:, blk, tl * 128:(tl + 1) * 128],
                                                                        rhs=wo[:, blk, hf * 512:(hf + 1) * 512], start=(blk == 0), stop=(blk == NBLK - 1)),
                         r=["hz", "wo"], w=[f"ps{hf}"])
                S.op("act" if hf == 0 else "dve",
                     (lambda h, hf=hf, pbuf=pbuf: h.copy(out=pt[pbuf][:, hf * 512:(hf + 1) * 512], in_=ps[hf][:, :])) if hf == 0 else
                     (lambda h, hf=hf, pbuf=pbuf: h.tensor_copy(out=pt[pbuf][:, hf * 512:(hf + 1) * 512], in_=ps[hf][:, :])),
                     r=[f"ps{hf}"], w=["pt0"])
            rows = slice(t0 + tl * 128, t0 + (tl + 1) * 128)
            S.op("sp", lambda h, pbuf=pbuf, rows=rows: h.dma_start(out=p_out[rows, :], in_=pt[pbuf][:]), r=["pt0"], dma=True)
    return nc, es, S


def build_F(ntok=2048):
    nc = get_nc()
    es = ExitStack()
    S = get_sched(nc, es)
    srcs = [dram_in(nc, f"xin{k}", [ntok, D]) for k in range(3)]
    g_row = dram_in(nc, "g", [1, D])
    out_d = dram_out(nc, "out", [ntok, D])
    g_bc = mk(nc, es, "g_bc", [128, D], F32)
    S.op("sp", lambda h: h.dma_start(out=g_bc[:], in_=g_row.partition_broadcast(128)), w=["g"], dma=True)
    NB = 2
    xt = [mk(nc, es, f"xt{b}", [128, D], F32) for b in range(NB)]
    sq = mk(nc, es, "sq", [128, D], F32)
    ot = [mk(nc, es, f"ot{b}", [128, D], F32) for b in range(NB)]
    st = [mk(nc, es, f"st{b}", [128, 4], F32) for b in range(NB)]
    for t in range(ntok // 128):
        b = t % NB
        rows = slice(t * 128, (t + 1) * 128)
        for k, src in enumerate(srcs):
            if k == 0:
                S.op("pool", lambda h, src=src, b=b, t=t: h.dma_start(out=xt[b][:], in_=src_rows(src, t)), w=[f"xt{b}"], dma=True)
            else:
                S.op("pool", lambda h, src=src, b=b, t=t: h.dma_start(out=xt[b][:], in_=src_rows(src, t), accum_op=ALU.add), r=[f"xt{b}"], w=[f"xt{b}"], dma=True)
        S.op("act", lambda h, b=b: h.activation(out=sq[:], in_=xt[b][:], func=AF.Square), r=[f"xt{b}"], w=["sq"])
        S.op("dve", lambda h, b=b: h.tensor_reduce(out=st[b][:, 0:1], in_=sq[:], axis=AX.X, op=ALU.add), r=["sq"], w=[f"st{b}"])
        S.op("act", lambda h, b=b: h.activation(out=st[b][:, 1:2], in_=st[b][:, 0:1], func=AF.Sqrt, scale=1.0 / D, bias=EPS), r=[f"st{b}"], w=[f"st{b}"])
        S.op("dve", lambda h, b=b: h.reciprocal(out=st[b][:, 2:3], in_=st[b][:, 1:2]), r=[f"st{b}"], w=[f"st{b}r"])
        S.op("dve", lambda h, b=b: h.scalar_tensor_tensor(out=ot[b][:], in0=xt[b][:], scalar=st[b][:, 2:3], in1=g_bc[:], op0=ALU.mult, op1=ALU.mult),
             r=[f"xt{b}", f"st{b}r", "g"], w=[f"ot{b}"])
        S.op("sp", lambda h, b=b, rows=rows: h.dma_start(out=out_d[rows, :], in_=ot[b][:]), r=[f"ot{b}"], dma=True)
    return nc, es, S


def prep_D(z, half):
    LW = 1280
    blks = list(range(half * 8, half * 8 + 8))
    cols = np.concatenate([np.arange(b * 80, (b + 1) * 80) for b in blks])
    w_in = z['d_w_in'][0]
    d = {}
    d['wu'] = np.ascontiguousarray(w_in[:, cols])
    d['wz'] = np.ascontiguousarray(w_in[:, LW + cols])
    d['wo'] = np.ascontiguousarray(z['d_w_out'][0][cols, :])
    d['ga'] = np.ascontiguousarray(z['d_gate_a_w'][0][blks])
    d['gx'] = np.ascontiguousarray(z['d_gate_x_w'][0][blks])
    vecs = np.zeros((80, 8, 8), np.float32)

    def fm(v):
        return v[cols].reshape(8, 80).T
    for j in range(4):
        vecs[:, :, j] = fm(z['d_conv_w'][0][j])
    vecs[:, :, 4] = fm(z['d_conv_b'][0])
    vecs[:, :, 5] = fm(z['d_gate_a_b'][0])
    vecs[:, :, 6] = fm(z['d_gate_x_b'][0])
    vecs[:, :, 7] = fm(z['d_lambda'][0])
    d['vecs'] = vecs
    d['g'] = z['norm_g'][3:4].copy()
    d['ident'] = np.eye(128, dtype=np.float32)
    return d


PAIRS = [[0, 1], [2, 3], [4, 5], [6, 7]]


def build_fused(T=4096, nlayers=4):
    CFG.T = T
    nc = bass.Bass("TRN2", target_bir_lowering=False)
    top = ExitStack()
    S = Sched(nc, top)
    CFG.nc, CFG.S = nc, S
    x_d = nc.dram_tensor("x", [T, D], F32, kind="ExternalInput").ap()
    out_d = nc.dram_tensor("out", [T, D], F32, kind="ExternalOutput").ap()
    p = [nc.dram_tensor(f"p_i{l}", [T, D], F32) for l in range(4)]
    CH = 512
    NCH = T // CH
    pg = [[nc.dram_tensor(f"pg_i{l}_{k}", [2 * CH, D], F32) for k in range(NCH)] for l in range(4)]

    def gsrc(l, rank):
        return lambda t: pg[l][t // 4].ap()[rank * CH + (t % 4) * 128:rank * CH + (t % 4 + 1) * 128, :]
    xs = [nc.dram_tensor(f"xs_i{l}", [T, D], F32) for l in range(3)]
    layers = [("A", build_A, {}), ("B", build_B, dict(MC=128)), ("C", build_C, {}), ("D", build_D, {})]
    prev_x = x_d
    for l, (nm, fn, kw) in enumerate(layers[:nlayers]):
        CFG.prefix = nm + "_"
        SB_USED[0] = 0
        ov = {"p": p[l].ap()}
        if l == 0:
            ov["xin0"] = x_d
            nsrc = 1
        else:
            ov["xin0"] = prev_x
            ov["xin1"] = gsrc(l - 1, 0)
            ov["xin2"] = gsrc(l - 1, 1)
            ov["xs"] = xs[l - 1].ap()
            nsrc = 3
        CFG.override = ov
        _, es, _ = fn(nsrc, **kw)
        for k in range(NCH):
            S.op("pool", lambda h, l=l, k=k: h.collective_compute("AllGather", ALU.bypass, replica_groups=PAIRS, ins=[p[l].ap()[k * CH:(k + 1) * CH, :].opt()],
                                                               outs=[pg[l][k].ap().opt()]), dma=True, cc=True)
        S.emit(final=False)
        S.barrier()
        es.close()
        if l > 0:
            prev_x = xs[l - 1].ap()
    CFG.prefix = "F_"
    SB_USED[0] = 0
    CFG.override = {"xin0": prev_x, "xin1": gsrc(nlayers - 1, 0), "xin2": gsrc(nlayers - 1, 1), "out": out_d}
    _, es, _ = build_F(T)
    stats = S.emit(final=True)
    CFG.nc, CFG.S, CFG.override, CFG.prefix = None, None, {}, ""
    return nc, stats


def kernel(**inputs):
    z = {k: np.ascontiguousarray(np.asarray(v, dtype=np.float32)) for k, v in inputs.items()}
    T = 4096
    x = z['x']
    B = x.shape[0]
    nc, _ = build_fused(T)
    per_half = []
    for h in range(2):
        d = {}
        for pre, pd in (("A_", prep_A(z, h)), ("B_", prep_B(z, h, MC=128)), ("C_", prep_C(z, h, T)), ("D_", prep_D(z, h))):
            for k, v in pd.items():
                d[pre + k] = v
        d["F_g"] = z['final_g'][None, :].copy()
        per_half.append(d)
    in_maps = [dict(per_half[c % 2], x=x[c // 2]) for c in range(8)]
    res = run_bass_kernel_spmd(nc, in_maps, core_ids=list(range(8)))
    out = np.stack([res.results[2 * b]['out'] for b in range(B)]).astype(np.float32)
    return out
```

```python
import numpy as np
from contextlib import ExitStack
import concourse.bass as bass
import concourse.mybir as mybir
from concourse.bass_utils import run_bass_kernel_spmd

F32 = mybir.dt.float32
BF16 = mybir.dt.bfloat16
AF = mybir.ActivationFunctionType
ALU = mybir.AluOpType
AX = mybir.AxisListType


class Buf:
    __slots__ = ("name", "lw", "rd")

    def __init__(self, name):
        self.name = name
        self.lw = None
        self.rd = {}


class Sched:
    COMPUTE = ("pe", "act", "dve", "pool")

    def __init__(self, nc, es, ndma_slots=8):
        self.nc = nc
        self.es = es
        self.ops = []
        self.bufs = {}
        self.ndma = ndma_slots
        self.handles = {"pe": nc.tensor, "act": nc.scalar, "dve": nc.vector, "pool": nc.gpsimd, "sp": nc.sync}
        self.need = []
        self.seg_dma = []
        self.last_compute = {}
        self.barrier_deps = set()
        self.pending_barrier = {}

    def buf(self, name):
        b = self.bufs.get(name)
        if b is None:
            b = Buf(name)
            self.bufs[name] = b
        return b

    def _B(self, lst):
        out = []
        for x in lst:
            if isinstance(x, str):
                out.append(self.buf(x))
            elif isinstance(x, Buf):
                out.append(x)
            elif x is None:
                continue
            else:
                out.extend(self._B(x))
        return out

    def op(self, eng, fn, r=(), w=(), dma=False, cc=False):
        i = len(self.ops)
        R = self._B(r)
        W = self._B(w)
        deps = set()
        if self.pending_barrier.get(eng):
            deps |= self.barrier_deps
            self.pending_barrier[eng] = False
        if cc:
            deps |= set(self.seg_dma)
        raw = set()
        for b in R:
            if b.lw is not None:
                deps.add(b.lw)
                raw.add(b.lw)
        for b in W:
            if b.lw is not None:
                deps.add(b.lw)
            for k, v in b.rd.items():
                deps.add(v)
        for b in W:
            b.lw = i
            b.rd = {}
        key = ("dma", i) if dma else eng
        for b in R:
            b.rd[key] = i
        self.ops.append(dict(eng=eng, fn=fn, deps=deps, dma=dma, cc=cc, raw=raw))
        if dma:
            self.seg_dma.append(i)
        elif eng in self.COMPUTE:
            self.last_compute[eng] = i
        return i

    def _init_state(self):
        nc = self.nc
        self.sems = {e: self.es.enter_context(nc.semaphore("sem_" + e)) for e in self.COMPUTE}
        self.dsems = {q: [self.es.enter_context(nc.semaphore(f"dsem_{q}_{k}")) for k in range(self.ndma)] for q in ("sp", "pool")}
        self.ccsem = self.es.enter_context(nc.semaphore("sem_cc"))
        self.cccount = 0
        self.duses = {q: [0] * self.ndma for q in ("sp", "pool")}
        self.dcount = {"sp": 0, "pool": 0}
        self.cnt = {e: 0 for e in self.COMPUTE}
        self.token = []
        self.waited = {e: {} for e in self.handles}
        self.nwaits = 0
        self.emitted = 0
        self.inited = True

    def _skip(self, po, o, d):
        if po["dma"] or o["dma"] or po["eng"] != o["eng"]:
            return False
        e = o["eng"]
        if e == "pe":
            return True
        if e in ("act", "dve") and d not in o["raw"]:
            return True
        return False

    def barrier(self):
        deps = set(self.seg_dma)
        for e in self.COMPUTE:
            if e in self.last_compute:
                deps.add(self.last_compute[e])
        self.barrier_deps = deps
        self.pending_barrier = {e: True for e in self.handles}
        self.seg_dma = []
        self.bufs = {}

    def emit(self, final=True):
        nc = self.nc
        ops = self.ops
        if not getattr(self, "inited", False):
            self._init_state()
        start = self.emitted
        n = len(ops)
        need = self.need
        need.extend([False] * (n - len(need)))
        for i in range(start, n):
            o = ops[i]
            for d in o["deps"]:
                po = ops[d]
                if po["dma"]:
                    continue
                if self._skip(po, o, d):
                    continue
                assert d >= start or need[d], "cross-segment dependency on an op without increment"
                need[d] = True
        lastc = {}
        for i in range(start, n):
            if not ops[i]["dma"] and ops[i]["eng"] in self.COMPUTE:
                lastc[ops[i]["eng"]] = i
        for e, i in lastc.items():
            need[i] = True
        sems, dsems, duses, dcount, cnt, token, waited = self.sems, self.dsems, self.duses, self.dcount, self.cnt, self.token, self.waited
        token.extend([None] * (n - len(token)))
        for i in range(start, n):
            o = ops[i]
            e = o["eng"]
            h = self.handles[e]
            wd = waited[e]
            reqs = {}
            for d in o["deps"]:
                po = ops[d]
                if self._skip(po, o, d):
                    continue
                sem, val, sk = token[d]
                if wd.get(sk, 0) >= val:
                    continue
                if sk not in reqs or reqs[sk][1] < val:
                    reqs[sk] = (sem, val)
            is_cc = o.get("cc", False)
            if o["dma"] and not is_cc:
                q = e
                s = dcount[q] % self.ndma
                dcount[q] += 1
                dsk = ("d", q, s)
                prev = 16 * duses[q][s]
                if prev > 0 and wd.get(dsk, 0) < prev:
                    if dsk not in reqs or reqs[dsk][1] < prev:
                        reqs[dsk] = (dsems[q][s], prev)
            for rk, (rsem, rval) in reqs.items():
                h.wait_ge(rsem, rval)
                wd[rk] = rval
                self.nwaits += 1
            ins = o["fn"](h)
            if is_cc:
                self.cccount += 1
                ins.then_inc(self.ccsem, 1)
                token[i] = (self.ccsem, self.cccount, ("cc",))
            elif o["dma"]:
                duses[q][s] += 1
                ins.then_inc(dsems[q][s], 16)
                token[i] = (dsems[q][s], 16 * duses[q][s], dsk)
            else:
                if need[i]:
                    cnt[e] += 1
                    ins.then_inc(sems[e], 1)
                    token[i] = (sems[e], cnt[e], ("c", e))
                else:
                    token[i] = (sems[e], cnt[e] + 0, ("c", e))
            o["fn"] = None
        self.emitted = n
        if final:
            h = self.handles["sp"]
            for q in ("sp", "pool"):
                for s in range(self.ndma):
                    if duses[q][s] > 0:
                        h.wait_ge(dsems[q][s], 16 * duses[q][s])
            if self.cccount:
                h.wait_ge(self.ccsem, self.cccount)
        self.stats = dict(nops=len(ops), nwaits=self.nwaits, incs=dict(cnt))
        return self.stats


class Stream:
    def __init__(self):
        self.items = []

    def op(self, *a, **k):
        self.items.append((a, k))


def merge_streams(S, streams, chunk=1):
    idx = [0] * len(streams)
    live = True
    while live:
        live = False
        for i, st in enumerate(streams):
            for _ in range(chunk):
                if idx[i] < len(st.items):
                    a, k = st.items[idx[i]]
                    S.op(*a, **k)
                    idx[i] += 1
                    live = True


class CFG:
    T = 4096
    prefix = ""
    nc = None
    S = None
    override = {}


def get_nc():
    if CFG.nc is not None:
        return CFG.nc
    return bass.Bass("TRN2", target_bir_lowering=False)


def get_sched(nc, es):
    if CFG.S is not None:
        return CFG.S
    return Sched(nc, es)
D = 1024
EPS = 1e-6


class Ctx:
    pass


SB_USED = [0]


def mk(nc, es, name, shape, dt, psum=False):
    if not psum:
        n = 1
        for d_ in shape[1:]:
            n *= d_
        n *= (2 if dt == BF16 else 4)
        SB_USED[0] += (n + 31) // 32 * 32
        assert SB_USED[0] <= 190 * 1024, f"SBUF over budget at {name}: {SB_USED[0]}"
    if psum:
        return es.enter_context(nc.psum_tensor(CFG.prefix + name, shape, dt))
    return es.enter_context(nc.sbuf_tensor(CFG.prefix + name, shape, dt))


def src_rows(src, t):
    if callable(src):
        return src(t)
    return src[t * 128:(t + 1) * 128, :]


def dram_in(nc, name, shape, dt=F32):
    if name in CFG.override:
        return CFG.override[name]
    return nc.dram_tensor(CFG.prefix + name, list(shape), dt, kind="ExternalInput").ap()


def dram_out(nc, name, shape, dt=F32):
    if name in CFG.override:
        return CFG.override[name]
    return nc.dram_tensor(CFG.prefix + name, list(shape), dt, kind="ExternalOutput").ap()


class P1:
    def __init__(self, S, nc, es, srcs, xs_out, g_row, ident_d, ps_tr, name="p1"):
        self.S, self.nc, self.srcs, self.xs_out, self.ps_tr, self.name = S, nc, srcs, xs_out, ps_tr, name
        self.g_bc = mk(nc, es, name + "_g", [128, D], F32)
        self.ident = mk(nc, es, name + "_id", [128, 128], BF16)
        self.identf = mk(nc, es, name + "_idf", [128, 128], F32)
        g_bc, ident, identf = self.g_bc, self.ident, self.identf
        S.op("sp", lambda h: h.dma_start(out=g_bc[:], in_=g_row.partition_broadcast(128)), w=[name + "g"], dma=True)
        S.op("sp", lambda h: h.dma_start(out=identf[:], in_=ident_d), w=[name + "idf"], dma=True)
        S.op("dve", lambda h: h.tensor_copy(out=ident[:], in_=identf[:]), r=[name + "idf"], w=[name + "id"])
        self.NB = 2
        self.xt = [mk(nc, es, f"{name}_x{b}", [128, D], F32) for b in range(self.NB)]
        self.sq = mk(nc, es, name + "_sq", [128, D], BF16)
        self.xnb = [mk(nc, es, f"{name}_xn{b}", [128, D], BF16) for b in range(self.NB)]
        self.st = [mk(nc, es, f"{name}_st{b}", [128, 4], F32) for b in range(self.NB)]

    def tile(self, t, dst_ap, dst_buf):
        S, name = self.S, self.name
        xt, sq, xnb, st, g_bc, ident, ps_tr = self.xt, self.sq, self.xnb, self.st, self.g_bc, self.ident, self.ps_tr
        b = t % self.NB
        rows = slice(t * 128, (t + 1) * 128)
        xb = f"{name}x{b}"
        for k, src in enumerate(self.srcs):
            if k == 0:
                S.op("pool", lambda h, src=src: h.dma_start(out=xt[b][:], in_=src_rows(src, t)), w=[xb], dma=True)
            else:
                S.op("pool", lambda h, src=src: h.dma_start(out=xt[b][:], in_=src_rows(src, t), accum_op=ALU.add), r=[xb], w=[xb], dma=True)
        if self.xs_out is not None and len(self.srcs) > 1:
            S.op("sp", lambda h: h.dma_start(out=self.xs_out[rows, :], in_=xt[b][:]), r=[xb], dma=True)
        S.op("act", lambda h: h.activation(out=sq[:], in_=xt[b][:], func=AF.Square), r=[xb], w=[name + "sq"])
        S.op("dve", lambda h: h.tensor_reduce(out=st[b][:, 0:1], in_=sq[:], axis=AX.X, op=ALU.add), r=[name + "sq"], w=[f"{name}st{b}"])
        S.op("act", lambda h: h.activation(out=st[b][:, 1:2], in_=st[b][:, 0:1], func=AF.Sqrt, scale=1.0 / D, bias=EPS), r=[f"{name}st{b}"], w=[f"{name}st{b}"])
        S.op("dve", lambda h: h.reciprocal(out=st[b][:, 2:3], in_=st[b][:, 1:2]), r=[f"{name}st{b}"], w=[f"{name}st{b}r"])
        S.op("dve", lambda h: h.scalar_tensor_tensor(out=xnb[b][:], in0=xt[b][:], scalar=st[b][:, 2:3], in1=g_bc[:], op0=ALU.mult, op1=ALU.mult),
             r=[xb, f"{name}st{b}r", name + "g"], w=[f"{name}xn{b}"])
        for dc in range(8):
            S.op("pe", lambda h, dc=dc: h.transpose(out=ps_tr[:, dc * 128:(dc + 1) * 128], in_=xnb[b][:, dc * 128:(dc + 1) * 128], identity=ident[:]),
                 r=[f"{name}xn{b}", name + "id"], w=[name + "pstr"])
        S.op("act", lambda h: h.copy(out=dst_ap, in_=ps_tr[:].rearrange("p (c n) -> p c n", c=8)), r=[name + "pstr"], w=[dst_buf])


def phase1(S, nc, es, srcs, xs_out, g_row, ident_d, ps_tr, ntiles=None, xnT=None, name="p1"):
    if ntiles is None:
        ntiles = CFG.T // 128
    p1 = P1(S, nc, es, srcs, xs_out, g_row, ident_d, ps_tr, name)
    for t in range(ntiles):
        p1.tile(t, xnT[:, :, t * 128:(t + 1) * 128], f"xnT{t // 4}")
    return p1.ident


def dbg(S, nc, name, ap, shape, rbuf, dt=F32):
    o = nc.dram_tensor("dbg_" + name, list(shape), dt, kind="ExternalOutput").ap()
    S.op("sp", lambda h: h.dma_start(out=o, in_=ap), r=rbuf, dma=True)


NEGM = -30000.0


def build_A(nsrc=1, debug=False):
    T = CFG.T
    NT = T // 128
    nc = get_nc()
    es = ExitStack()
    S = get_sched(nc, es)
    srcs = [dram_in(nc, f"xin{k}", [T, D]) for k in range(nsrc)]
    xs_out = dram_out(nc, "xs", [T, D]) if nsrc > 1 else None
    g_row = dram_in(nc, "g", [1, D])
    ident_d = dram_in(nc, "ident", [128, 128])
    wq_d = dram_in(nc, "wq", [D, 512])
    wk_d = dram_in(nc, "wk", [D, 128])
    wv_d = dram_in(nc, "wv", [D, 128])
    wz_d = dram_in(nc, "wz", [D, 512])
    wo_d = dram_in(nc, "wo", [512, D])
    bias_d = dram_in(nc, "biasT", [128, 2, 2 * 4 * 128])
    sink_d = dram_in(nc, "sinks", [1, 8])
    p_out = dram_out(nc, "p", [T, D])

    xnT = mk(nc, es, "xnT", [128, 8, T], BF16)
    ps_tr = mk(nc, es, "ps_tr", [128, 1024], BF16, psum=True)
    ps_q = mk(nc, es, "ps_q", [128, 512], F32, psum=True)
    ps_z = mk(nc, es, "ps_z", [128, 512], F32, psum=True)
    ps_s = [mk(nc, es, f"ps_s{k}", [128, 512], F32, psum=True) for k in range(2)]
    ps_o = mk(nc, es, "ps_o", [128, 4, 128], F32, psum=True)
    ps_y = [mk(nc, es, f"ps_y{k}", [128, 512], F32, psum=True) for k in range(2)]
    ident = phase1(S, nc, es, srcs, xs_out, g_row, ident_d, ps_tr, xnT=xnT)

    wq = mk(nc, es, "wq_s", [128, 8, 512], BF16)
    wk = mk(nc, es, "wk_s", [128, 8, 128], BF16)
    wv = mk(nc, es, "wv_s", [128, 8, 128], BF16)
    wz = mk(nc, es, "wz_s", [128, 8, 512], BF16)
    wo = mk(nc, es, "wo_s", [128, 4, D], BF16)
    biasT = mk(nc, es, "biasT_s", [128, 2, 1024], BF16)
    esink = mk(nc, es, "esink", [128, 8], F32)
    for nm, t_, d_, pat in (("wq", wq, wq_d, "(c p) n -> p c n"), ("wk", wk, wk_d, "(c p) n -> p c n"), ("wv", wv, wv_d, "(c p) n -> p c n"),
                            ("wz", wz, wz_d, "(c p) n -> p c n"), ("wo", wo, wo_d, "(c p) n -> p c n")):
        S.op("pool", lambda h, t_=t_, d_=d_, pat=pat: h.dma_start(out=t_[:], in_=d_.rearrange(pat, p=128)), w=[nm], dma=True)
    S.op("pool", lambda h: h.dma_start(out=biasT[:], in_=bias_d), w=["biasT"], dma=True)
    S.op("sp", lambda h: h.dma_start(out=esink[:], in_=sink_d.partition_broadcast(128)), w=["esink"], dma=True)
    S.op("act", lambda h: h.activation(out=esink[:], in_=esink[:], func=AF.Exp), r=["esink"], w=["esink"])

    kT = mk(nc, es, "kT", [64, 2, T], BF16)
    vau = mk(nc, es, "vau", [128, NT, 2, 65], BF16)
    S.op("dve", lambda h: h.memset(vau[:, :, :, 64:65], 1.0), w=["vau_ones"])
    for g in range(2):
        for c in range(T // 512):
            tk = slice(c * 512, (c + 1) * 512)
            for dc in range(8):
                S.op("pe", lambda h, g=g, dc=dc, tk=tk: h.matmul(ps_q[0:64, :], lhsT=wk[:, dc, g * 64:(g + 1) * 64], rhs=xnT[:, dc, tk],
                                                                start=(dc == 0), stop=(dc == 7)), r=["wk", f"xnT{c}"], w=["ps_q"])
            S.op("act", lambda h, g=g, tk=tk: h.copy(out=kT[:, g, tk], in_=ps_q[0:64, :]), r=["ps_q"], w=[f"kT{c // 1}"])
    for t in range(NT):
        for dc in range(8):
            S.op("pe", lambda h, t=t, dc=dc: h.matmul(ps_z[:, 0:128], lhsT=xnT[:, dc, t * 128:(t + 1) * 128], rhs=wv[:, dc, :],
                                                      start=(dc == 0), stop=(dc == 7)), r=["wv", f"xnT{t // 4}"], w=["ps_z"])
        S.op("dve", lambda h, t=t: h.tensor_copy(out=vau[:, t, :, 0:64], in_=ps_z[:, 0:128].rearrange("p (g d) -> p g d", g=2)),
             r=["ps_z"], w=[f"vau{t}"])

    NB = 2
    qT = [mk(nc, es, f"qT{b}", [64, 2, 4, 128], BF16) for b in range(NB)]
    zs = [mk(nc, es, f"zs{b}", [128, 512], BF16) for b in range(NB)]
    pT = [mk(nc, es, f"pT{b}", [128, 512], BF16) for b in range(4)]
    yz = [mk(nc, es, f"yz{b}", [128, 512], BF16) for b in range(NB)]
    yzT = [mk(nc, es, f"yzT{b}", [128, 4, 128], BF16) for b in range(NB)]
    den = [mk(nc, es, f"den{b}", [128, 8], F32) for b in range(NB)]
    pt = [mk(nc, es, f"pt{b}", [128, D], F32) for b in range(NB)]
    pti = 0
    for qt in range(NT):
        b = qt % NB
        tq = slice(qt * 128, (qt + 1) * 128)
        xk = f"xnT{qt // 4}"
        for g in range(2):
            for r in range(4):
                for dc in range(8):
                    col = (g * 4 + r) * 64
                    S.op("pe", lambda h, g=g, r=r, dc=dc, col=col, tq=tq: h.matmul(ps_q[0:64, r * 128:(r + 1) * 128], lhsT=wq[:, dc, col:col + 64],
                                                                               rhs=xnT[:, dc, tq], start=(dc == 0), stop=(dc == 7)),
                         r=["wq", xk], w=["ps_q"])
            S.op("act", lambda h, g=g, b=b: h.activation(out=qT[b][:, g, :, :], in_=ps_q[0:64, :].rearrange("p (r n) -> p r n", r=4),
                                                        func=AF.Copy, scale=0.125), r=["ps_q"], w=[f"qT{b}_{g}"])
        for dc in range(8):
            S.op("pe", lambda h, dc=dc, tq=tq: h.matmul(ps_z[:, :], lhsT=xnT[:, dc, tq], rhs=wz[:, dc, :], start=(dc == 0), stop=(dc == 7)),
                 r=["wz", xk], w=["ps_z"])
        S.op("act", lambda h, b=b: h.activation(out=zs[b][:], in_=ps_z[:, :], func=AF.Silu), r=["ps_z"], w=[f"zs{b}"])
        for g in range(2):
            kts = [kt for kt in (qt - 1, qt) if kt >= 0]
            for kt in kts:
                cls = 0 if kt == qt else 1
                si = kt % 2
                pi = (g * 2 + si)
                S.op("pe", lambda h, g=g, kt=kt, si=si, b=b: h.matmul(ps_s[si][:, :], lhsT=kT[:, g, kt * 128:(kt + 1) * 128],
                                                                     rhs=qT[b][:, g, :, :].rearrange("p r n -> p (r n)"), start=True, stop=False),
                     r=[f"kT{kt // 4}", f"qT{b}_{g}"], w=[f"ps_s{si}"])
                S.op("pe", lambda h, g=g, cls=cls, si=si: h.matmul(ps_s[si][:, :], lhsT=ident[:], rhs=biasT[:, cls, g * 512:(g + 1) * 512],
                                                                  start=False, stop=True), r=["biasT", "p1id"], w=[f"ps_s{si}"])
                S.op("act", lambda h, si=si, pi=pi: h.activation(out=pT[pi][:], in_=ps_s[si][:, :], func=AF.Exp), r=[f"ps_s{si}"], w=[f"pT{pi}"])
            for r in range(4):
                for j, kt in enumerate(kts):
                    pi = (g * 2 + kt % 2)
                    S.op("pe", lambda h, g=g, r=r, kt=kt, pi=pi, j=j: h.matmul(ps_o[:, r, 0:65], lhsT=pT[pi][:, r * 128:(r + 1) * 128],
                                                                             rhs=vau[:, kt, g, :], start=(j == 0), stop=(j == len(kts) - 1)),
                         r=[f"pT{pi}", f"vau{kt}", "vau_ones"], w=["ps_o"])
            S.op("dve", lambda h, g=g, b=b: h.tensor_tensor(out=den[b][:, g * 4:(g + 1) * 4], in0=ps_o[:, :, 64], in1=esink[:, g * 4:(g + 1) * 4], op=ALU.add),
                 r=["ps_o", "esink"], w=[f"den{b}"])
            S.op("dve", lambda h, g=g, b=b: h.reciprocal(out=den[b][:, g * 4:(g + 1) * 4], in_=den[b][:, g * 4:(g + 1) * 4]), r=[f"den{b}"], w=[f"den{b}"])
            for r in range(4):
                col = (g * 4 + r) * 64
                S.op("dve", lambda h, g=g, r=r, b=b, col=col: h.scalar_tensor_tensor(out=yz[b][:, col:col + 64], in0=ps_o[:, r, 0:64],
                                                                                   scalar=den[b][:, g * 4 + r:g * 4 + r + 1], in1=zs[b][:, col:col + 64],
                                                                                   op0=ALU.mult, op1=ALU.mult),
                     r=["ps_o", f"den{b}", f"zs{b}"], w=[f"yz{b}"])
        for c in range(4):
            S.op("pe", lambda h, c=c, b=b: h.transpose(out=ps_tr[:, c * 128:(c + 1) * 128], in_=yz[b][:, c * 128:(c + 1) * 128], identity=ident[:]),
                 r=[f"yz{b}", "p1id"], w=["p1pstr"])
        S.op("act", lambda h, b=b: h.copy(out=yzT[b][:], in_=ps_tr[:, 0:512].rearrange("p (c n) -> p c n", c=4)), r=["p1pstr"], w=[f"yzT{b}"])
        for hf in range(2):
            for c in range(4):
                S.op("pe", lambda h, hf=hf, c=c, b=b: h.matmul(ps_y[hf][:, :], lhsT=yzT[b][:, c, :], rhs=wo[:, c, hf * 512:(hf + 1) * 512],
                                                              start=(c == 0), stop=(c == 3)), r=[f"yzT{b}", "wo"], w=[f"ps_y{hf}"])
            if hf == 0:
                S.op("act", lambda h, b=b: h.copy(out=pt[b][:, 0:512], in_=ps_y[0][:, :]), r=["ps_y0"], w=[f"pt{b}"])
            else:
                S.op("dve", lambda h, b=b: h.tensor_copy(out=pt[b][:, 512:1024], in_=ps_y[1][:, :]), r=["ps_y1"], w=[f"pt{b}"])
        S.op("sp", lambda h, b=b, tq=tq: h.dma_start(out=p_out[tq, :], in_=pt[b][:]), r=[f"pt{b}"], dma=True)
    return nc, es, S


def t5_bucket_np(d):
    import math
    d = np.maximum(d, 0)
    df = np.maximum(d, 1).astype(np.float32)
    large = 16 + (np.log(df / 16) / math.log(128 / 16) * 16).astype(np.int32)
    large = np.minimum(large, 31)
    return np.where(d < 16, d, large)


def prep_A(z, half):
    d = {}
    w_in = z['a_w_in'][0]
    d['wq'] = np.ascontiguousarray(w_in[:, half * 512:(half + 1) * 512])
    d['wk'] = np.ascontiguousarray(w_in[:, 1024 + half * 128:1024 + (half + 1) * 128])
    d['wv'] = np.ascontiguousarray(w_in[:, 1280 + half * 128:1280 + (half + 1) * 128])
    d['wz'] = np.ascontiguousarray(w_in[:, 1536 + half * 512:1536 + (half + 1) * 512])
    d['wo'] = np.ascontiguousarray(z['a_w_out'][0][half * 512:(half + 1) * 512, :])
    d['sinks'] = np.ascontiguousarray(z['a_sinks'][0][half * 8:(half + 1) * 8][None, :])
    table = z['t5_table']
    tk = np.arange(128)[:, None]
    tq = np.arange(128)[None, :]
    bias = np.zeros((128, 2, 2, 4, 128), np.float32)
    for cls in range(2):
        dist = tq - tk + 128 * cls
        valid = (dist >= 0) & (dist < 128)
        bk = t5_bucket_np(dist)
        for g in range(2):
            for r in range(4):
                hh = half * 8 + g * 4 + r
                bias[:, cls, g, r, :] = np.where(valid, table[bk, hh], NEGM)
    d['biasT'] = bias.reshape(128, 2, 1024)
    d['g'] = z['norm_g'][0:1].copy()
    d['ident'] = np.eye(128, dtype=np.float32)
    return d


KAP = 0.6065306597126334
GN_EPS = 64e-5


class _Stop(Exception):
    pass


def build_B(nsrc=1, MC=256, debug=False, stage=99):
    try:
        return _build_B(nsrc, MC, debug, stage)
    except _Stop as e:
        return e.args[0]


def _build_B(nsrc=1, MC=256, debug=False, stage=99):
    T = CFG.T
    NJ = MC // 64
    NMC = T // MC
    nc = get_nc()
    es = ExitStack()
    S = get_sched(nc, es)
    srcs = [dram_in(nc, f"xin{k}", [T, D]) for k in range(nsrc)]
    xs_out = dram_out(nc, "xs", [T, D]) if nsrc > 1 else None
    g_row = dram_in(nc, "g", [1, D])
    ident_d = dram_in(nc, "ident", [128, 128])
    w4_d = dram_in(nc, "w4", [4, D, 512])
    lw_d = dram_in(nc, "lw", [2, D, 64])
    l2_d = dram_in(nc, "l2", [2, 64, 512])
    wo_d = dram_in(nc, "wo", [512, D])
    mu_d = dram_in(nc, "muT", [128, 6, 8])
    vec_d = dram_in(nc, "vecs", [64, 8, 8])
    lnw_d = dram_in(nc, "lnw", [1, 512])
    lnb_d = dram_in(nc, "lnb", [1, 512])
    mg_d = dram_in(nc, "maskG", [128, 128])
    mnt_d = dram_in(nc, "maskNT", [64, 64])
    rm_d = dram_in(nc, "resetm", [64, MC])
    p_out = dram_out(nc, "p", [T, D])

    ps_tr = mk(nc, es, "ps_tr", [128, 1024], BF16, psum=True)
    ps_proj = mk(nc, es, "ps_proj", [128, 512], F32, psum=True)
    ps_tok = mk(nc, es, "ps_tok", [128, 512], F32, psum=True)
    ps_bv = mk(nc, es, "ps_bv", [128, 512], F32, psum=True)
    ps_g = mk(nc, es, "ps_g", [128, 512], F32, psum=True)
    ps_n = mk(nc, es, "ps_n", [128, 512], F32, psum=True)
    ps_rec = mk(nc, es, "ps_rec", [128, 512], F32, psum=True)
    ps_y = mk(nc, es, "ps_y", [128, 512], F32, psum=True)

    g_bc = mk(nc, es, "g_bc", [128, D], F32)
    identf = mk(nc, es, "identf", [128, 128], F32)
    ident = mk(nc, es, "identb", [128, 128], BF16)
    S.op("sp", lambda h: h.dma_start(out=g_bc[:], in_=g_row.partition_broadcast(128)), w=["g"], dma=True)
    S.op("sp", lambda h: h.dma_start(out=identf[:], in_=ident_d), w=["identf"], dma=True)
    S.op("dve", lambda h: h.tensor_copy(out=ident[:], in_=identf[:]), r=["identf"], w=["ident"])
    W4 = mk(nc, es, "W4", [128, 4, 8, 512], BF16)
    W4m = mk(nc, es, "W4m", [128, 4, 8, 512], BF16)
    LW = mk(nc, es, "LW", [128, 2, 8, 64], BF16)
    LWm = mk(nc, es, "LWm", [128, 2, 8, 64], BF16)
    L2 = mk(nc, es, "L2", [64, 2, 512], BF16)
    wo = mk(nc, es, "wo_s", [128, 4, D], BF16)
    muT = mk(nc, es, "muT_s", [128, 6, 8], F32)
    vec = mk(nc, es, "vec_s", [64, 8, 8], F32)
    lnw = mk(nc, es, "lnw_s", [64, 512], F32)
    lnb = mk(nc, es, "lnb_s", [64, 512], F32)
    maskG = mk(nc, es, "maskG_s", [128, 128], F32)
    maskNT = mk(nc, es, "maskNT_s", [64, 64], F32)
    resetm = mk(nc, es, "resetm_s", [64, MC], F32)
    ones64 = mk(nc, es, "ones64", [64, 64], F32)
    S.op("pool", lambda h: h.dma_start(out=W4[:], in_=w4_d.rearrange("s (c p) n -> p s c n", p=128)), w=["W4"], dma=True)
    S.op("pool", lambda h: h.dma_start(out=LW[:], in_=lw_d.rearrange("s (c p) n -> p s c n", p=128)), w=["LW"], dma=True)
    S.op("pool", lambda h: h.dma_start(out=L2[:], in_=l2_d.rearrange("s k n -> k s n")), w=["L2"], dma=True)
    S.op("pool", lambda h: h.dma_start(out=wo[:], in_=wo_d.rearrange("(c p) n -> p c n", p=128)), w=["wo"], dma=True)
    S.op("sp", lambda h: h.dma_start(out=muT[:], in_=mu_d), w=["muT"], dma=True)
    S.op("sp", lambda h: h.dma_start(out=vec[:], in_=vec_d), w=["vec"], dma=True)
    S.op("sp", lambda h: h.dma_start(out=lnw[:], in_=lnw_d.partition_broadcast(64)), w=["lnw"], dma=True)
    S.op("sp", lambda h: h.dma_start(out=lnb[:], in_=lnb_d.partition_broadcast(64)), w=["lnb"], dma=True)
    S.op("sp", lambda h: h.dma_start(out=maskG[:], in_=mg_d), w=["maskG"], dma=True)
    S.op("sp", lambda h: h.dma_start(out=maskNT[:], in_=mnt_d), w=["maskNT"], dma=True)
    S.op("sp", lambda h: h.dma_start(out=resetm[:], in_=rm_d), w=["resetm"], dma=True)
    S.op("dve", lambda h: h.memset(ones64[:], 1.0), w=["ones64"])
    for s in range(4):
        for dc in range(8):
            S.op("pool" if dc % 2 else "dve", lambda h, s=s, dc=dc: h.tensor_scalar(out=W4m[:, s, dc, :], in0=W4[:, s, dc, :], scalar1=muT[:, s, dc:dc + 1], scalar2=None, op0=ALU.mult),
                 r=["W4", "muT"], w=["W4m"])
    for s in range(2):
        for dc in range(8):
            S.op("dve", lambda h, s=s, dc=dc: h.tensor_scalar(out=LWm[:, s, dc, :], in0=LW[:, s, dc, :], scalar1=muT[:, 4 + s, dc:dc + 1], scalar2=None, op0=ALU.mult),
                 r=["LW", "muT"], w=["LWm"])

    XW = 64 + MC
    xnT = mk(nc, es, "xnT", [128, 8, XW], BF16)
    xxT = mk(nc, es, "xxT", [128, 8, XW], BF16)
    S.op("dve", lambda h: h.memset(xnT[:, :, 0:64], 0.0), w=["xnT"])
    NB = 2
    xt = [mk(nc, es, f"xt{b}", [128, D], F32) for b in range(NB)]
    sq = mk(nc, es, "sq", [128, D], BF16)
    xnb = [mk(nc, es, f"xnb{b}", [128, D], BF16) for b in range(NB)]
    st = [mk(nc, es, f"st{b}", [128, 4], F32) for b in range(NB)]
    h1T = mk(nc, es, "h1T", [64, 2, MC], BF16)
    vwin = mk(nc, es, "vwin", [64, NJ, 512], F32)
    uT = mk(nc, es, "uT", [64, NJ, 512], F32)
    zs = mk(nc, es, "zs", [64, NJ, 512], F32)
    y_all = mk(nc, es, "y_all", [64, NJ, 512], F32)
    bv_all = mk(nc, es, "bv_all", [64, NJ, 512], F32)

    def ft(nm):
        return mk(nc, es, nm, [64, MC], F32)
    r_f = ft("r_f"); k_f = ft("k_f"); sig = ft("sig"); alp = ft("alp"); kk = ft("kk"); t1 = ft("t1"); t2 = ft("t2")
    cs = ft("cs"); kmod = ft("kmod"); bal = ft("bal")
    cLs = mk(nc, es, "cLs", [64, NJ], F32)
    G2 = 3
    cLd = [mk(nc, es, f"cLd{i}", [64, NJ], F32) for i in range(G2)]
    AR = [mk(nc, es, f"AR{i}", [64, NJ, 128], F32) for i in range(G2)]
    BK = [mk(nc, es, f"BK{i}", [64, NJ, 128], F32) for i in range(G2)]
    BKe = [mk(nc, es, f"BKe{i}", [64, NJ, 128], F32) for i in range(G2)]
    Gm = [mk(nc, es, f"Gm{i}", [64, NJ, 256], F32) for i in range(G2)]
    Tm = [mk(nc, es, f"Tm{i}", [64, NJ, 64], F32) for i in range(G2)]
    BKeT = [mk(nc, es, f"BKeT{i}", [64, NJ, 128], F32) for i in range(G2)]
    dcL = [mk(nc, es, f"dcL{i}", [64, NJ, 64], F32) for i in range(G2)]
    rkrp = [mk(nc, es, f"rkrp{i}", [64, NJ, 64], F32) for i in range(G2)]
    Dg = [mk(nc, es, f"Dg{i}", [64, NJ, 64], F32) for i in range(G2)]
    bon = [mk(nc, es, f"bon{i}", [64, NJ], F32) for i in range(G2)]
    Nk = [[mk(nc, es, f"Nk{j}_{i}", [64, 64], F32) for i in range(2)] for j in range(NJ)]
    NkT = [[mk(nc, es, f"NkT{j}_{i}", [64, 64], F32) for i in range(2)] for j in range(NJ)]
    Pm = [[mk(nc, es, f"Pm{j}_{i}", [64, 64], F32) for i in range(2)] for j in range(NJ)]
    ST = [[mk(nc, es, f"ST{h}_{i}", [64, 64], F32) for i in range(2)] for h in range(8)]
    WT = [mk(nc, es, f"WT{i}", [64, 64], F32) for i in range(2)]
    for h in range(8):
        S.op("dve", lambda hh, h=h: hh.memset(ST[h][0][:], 0.0), w=[f"ST{h}_0"])
    yn = mk(nc, es, "yn", [64, 512], F32)
    gst = mk(nc, es, "gst", [64, 4, 8], F32)
    yz = mk(nc, es, "yz", [64, 512], BF16)
    yzT = mk(nc, es, "yzT", [128, 4, 64], BF16)
    pt = mk(nc, es, "pt", [64, D], F32)

    def c3(t_):
        return t_[:].rearrange("p (c j) -> p c j", j=64)

    for mc in range(NMC):
        T0 = mc * MC
        if mc > 0:
            S.op("pool", lambda h: h.tensor_copy(out=xnT[:, :, 0:64], in_=xnT[:, :, MC:MC + 64]), r=["xnT"], w=["xnT"])
        for tl in range(MC // 128):
            t = (T0 // 128) + tl
            b = t % NB
            rows = slice(t * 128, (t + 1) * 128)
            for k, src in enumerate(srcs):
                if k == 0:
                    S.op("pool", lambda h, src=src, b=b, t=t: h.dma_start(out=xt[b][:], in_=src_rows(src, t)), w=[f"xt{b}"], dma=True)
                else:
                    S.op("pool", lambda h, src=src, b=b, t=t: h.dma_start(out=xt[b][:], in_=src_rows(src, t), accum_op=ALU.add),
                         r=[f"xt{b}"], w=[f"xt{b}"], dma=True)
            if xs_out is not None:
                S.op("sp", lambda h, b=b, rows=rows: h.dma_start(out=xs_out[rows, :], in_=xt[b][:]), r=[f"xt{b}"], dma=True)
            S.op("act", lambda h, b=b: h.activation(out=sq[:], in_=xt[b][:], func=AF.Square), r=[f"xt{b}"], w=["sq"])
            S.op("dve", lambda h, b=b: h.tensor_reduce(out=st[b][:, 0:1], in_=sq[:], axis=AX.X, op=ALU.add), r=["sq"], w=[f"st{b}"])
            S.op("act", lambda h, b=b: h.activation(out=st[b][:, 1:2], in_=st[b][:, 0:1], func=AF.Sqrt, scale=1.0 / D, bias=EPS), r=[f"st{b}"], w=[f"st{b}"])
            S.op("dve", lambda h, b=b: h.reciprocal(out=st[b][:, 2:3], in_=st[b][:, 1:2]), r=[f"st{b}"], w=[f"st{b}r"])
            S.op("dve", lambda h, b=b: h.scalar_tensor_tensor(out=xnb[b][:], in0=xt[b][:], scalar=st[b][:, 2:3], in1=g_bc[:], op0=ALU.mult, op1=ALU.mult),
                 r=[f"xt{b}", f"st{b}r", "g"], w=[f"xnb{b}"])
            for dc in range(8):
                S.op("pe", lambda h, dc=dc, b=b: h.transpose(out=ps_tr[:, dc * 128:(dc + 1) * 128], in_=xnb[b][:, dc * 128:(dc + 1) * 128], identity=ident[:]),
                     r=[f"xnb{b}", "ident"], w=["ps_tr"])
            S.op("act", lambda h, tl=tl: h.copy(out=xnT[:, :, 64 + tl * 128:64 + (tl + 1) * 128], in_=ps_tr[:].rearrange("p (c n) -> p c n", c=8)),
                 r=["ps_tr"], w=["xnT"])
        S.op("pool", lambda h: h.tensor_tensor(out=xxT[:, :, 1:XW], in0=xnT[:, :, 0:XW - 1], in1=xnT[:, :, 1:XW], op=ALU.subtract), r=["xnT"], w=["xxT"])
        tokc = slice(64, 64 + MC)

        def proj_fm(S, ps_ap, Wt, Wm, sidx, cols, M):
            n = 0
            for (Wx, X, xb) in ((Wt, xnT, "xnT"), (Wm, xxT, "xxT")):
                for dc in range(8):
                    S.op("pe", lambda h, Wx=Wx, X=X, dc=dc, n=n: h.matmul(ps_ap, lhsT=Wx[:, sidx, dc, cols], rhs=X[:, dc, tokc], start=(n == 0), stop=(n == 15)),
                         r=["W4", "W4m", "LW", "LWm", xb], w=["ps_proj"])
                    n += 1
        for s in range(2):
            proj_fm(S, ps_proj[0:64, 0:MC], LW, LWm, s, slice(0, 64), 64)
            S.op("act", lambda h, s=s: h.activation(out=h1T[:, s, :], in_=ps_proj[0:64, 0:MC], func=(AF.Tanh if s == 0 else AF.Copy)), r=["ps_proj"], w=["h1T"])
        for j in range(NJ):
            n = 0
            for (Wi, X, xb) in ((W4, xnT, "xnT"), (W4m, xxT, "xxT")):
                for dc in range(8):
                    S.op("pe", lambda h, Wi=Wi, X=X, dc=dc, n=n, j=j: h.matmul(ps_tok[0:64, :], lhsT=X[:, dc, 64 + j * 64:128 + j * 64], rhs=Wi[:, 2, dc, :], start=(n == 0), stop=(n == 15)),
                         r=["W4", "W4m", xb], w=["ps_tok"])
                    n += 1
            S.op("act", lambda h, j=j: h.copy(out=vwin[:, j, :], in_=ps_tok[0:64, :]), r=["ps_tok"], w=[f"vwin{j}"])
            n = 0
            for (Wi, X, xb) in ((W4, xnT, "xnT"), (W4m, xxT, "xxT")):
                for dc in range(8):
                    S.op("pe", lambda h, Wi=Wi, X=X, dc=dc, n=n, j=j: h.matmul(ps_tok[0:64, :], lhsT=X[:, dc, 64 + j * 64:128 + j * 64], rhs=Wi[:, 3, dc, :], start=(n == 0), stop=(n == 15)),
                         r=["W4", "W4m", xb], w=["ps_tok"])
                    n += 1
            S.op("act", lambda h, j=j: h.activation(out=zs[:, j, :], in_=ps_tok[0:64, :], func=AF.Silu), r=["ps_tok"], w=["zs"])

        def head_prep(S, hd):
            gi = hd % G2
            hc = slice(hd * 64, (hd + 1) * 64)
            vp = lambda c: vec[:, hd, c:c + 1]
            proj_fm(S, ps_proj[0:64, 0:MC], W4, W4m, 0, hc, 64)
            S.op("act", lambda h: h.copy(out=r_f[:], in_=ps_proj[0:64, 0:MC]), r=["ps_proj"], w=["r_f"])
            proj_fm(S, ps_proj[0:64, 0:MC], W4, W4m, 1, hc, 64)
            S.op("act", lambda h: h.copy(out=k_f[:], in_=ps_proj[0:64, 0:MC]), r=["ps_proj"], w=["k_f"])
            S.op("pe", lambda h, hc=hc: h.matmul(ps_proj[0:64, 0:MC], lhsT=L2[:, 0, hc], rhs=h1T[:, 0, :], start=True, stop=True), r=["L2", "h1T"], w=["ps_proj"])
            S.op("act", lambda h, hd=hd: h.activation(out=sig[:], in_=ps_proj[0:64, 0:MC], func=AF.Sigmoid, bias=vec[:, hd, 0:1]), r=["ps_proj", "vec"], w=["sig"])
            S.op("pe", lambda h, hc=hc: h.matmul(ps_proj[0:64, 0:MC], lhsT=L2[:, 1, hc], rhs=h1T[:, 1, :], start=True, stop=True), r=["L2", "h1T"], w=["ps_proj"])
            S.op("act", lambda h, hd=hd: h.activation(out=alp[:], in_=ps_proj[0:64, 0:MC], func=AF.Sigmoid, bias=vec[:, hd, 1:2]), r=["ps_proj", "vec"], w=["alp"])
            S.op("dve", lambda h, hd=hd: h.tensor_scalar(out=kk[:], in0=k_f[:], scalar1=vec[:, hd, 2:3], scalar2=None, op0=ALU.mult), r=["k_f", "vec"], w=["kk"])
            S.op("pool", lambda h: h.tensor_tensor(out=t1[:], in0=kk[:], in1=kk[:], op=ALU.mult), r=["kk"], w=["t1"])
            S.op("pe", lambda h: h.matmul(ps_proj[0:64, 0:MC], lhsT=ones64[:], rhs=t1[:], start=True, stop=True), r=["ones64", "t1"], w=["ps_proj"])
            S.op("act", lambda h: h.activation(out=t2[:], in_=ps_proj[0:64, 0:MC], func=AF.Sqrt), r=["ps_proj"], w=["t2"])
            S.op("dve", lambda h: h.tensor_scalar(out=t2[:], in0=t2[:], scalar1=1e-12, scalar2=None, op0=ALU.max), r=["t2"], w=["t2"])
            S.op("dve", lambda h: h.reciprocal(out=t2[:], in_=t2[:]), r=["t2"], w=["t2"])
            S.op("dve", lambda h: h.tensor_tensor(out=kk[:], in0=kk[:], in1=t2[:], op=ALU.mult), r=["kk", "t2"], w=["kk"])
            S.op("dve", lambda h, hd=hd: h.tensor_scalar(out=t1[:], in0=alp[:], scalar1=1.0, scalar2=vec[:, hd, 3:4], op0=ALU.subtract, op1=ALU.mult), r=["alp", "vec"], w=["t1"])
            S.op("dve", lambda h: h.scalar_tensor_tensor(out=kmod[:], in0=t1[:], scalar=1.0, in1=k_f[:], op0=ALU.add, op1=ALU.mult), r=["t1", "k_f"], w=["kmod"])
            S.op("pool", lambda h: h.tensor_tensor(out=bal[:], in0=kk[:], in1=alp[:], op=ALU.mult), r=["kk", "alp"], w=["bal"])
            S.op("dve", lambda h: h.tensor_tensor_scan(out=cs[:], data0=resetm[:], data1=sig[:], initial=0.0, op0=ALU.mult, op1=ALU.add), r=["resetm", "sig"], w=["cs"])
            S.op("dve", lambda h: h.tensor_copy(out=cLs[:], in_=cs[:, 63::64]), r=["cs"], w=["cLs"])
            S.op("act", lambda h, gi=gi: h.activation(out=cLd[gi][:], in_=cLs[:], func=AF.Exp, scale=-KAP), r=["cLs"], w=[f"cLd{gi}"])
            S.op("act", lambda h: h.activation(out=t1[:], in_=cs[:], func=AF.Exp, scale=-KAP), r=["cs"], w=["t1"])
            S.op("dve", lambda h, gi=gi: h.tensor_tensor(out=AR[gi][:, :, 64:128], in0=c3(r_f), in1=c3(t1), op=ALU.mult), r=["r_f", "t1"], w=[f"AR{gi}"])
            S.op("act", lambda h: h.activation(out=t2[:], in_=cs[:], func=AF.Exp, scale=KAP), r=["cs"], w=["t2"])
            S.op("dve", lambda h, gi=gi: h.tensor_tensor(out=BK[gi][:, :, 0:64], in0=c3(bal), in1=c3(t2), op=ALU.mult), r=["bal", "t2"], w=[f"BK{gi}"])
            S.op("pool", lambda h, gi=gi: h.tensor_tensor(out=BK[gi][:, :, 64:128], in0=c3(kmod), in1=c3(t2), op=ALU.mult), r=["kmod", "t2"], w=[f"BK{gi}"])
            S.op("pool", lambda h: h.tensor_tensor(out=t1[:], in0=cs[:], in1=sig[:], op=ALU.subtract), r=["cs", "sig"], w=["t1"])
            S.op("act", lambda h: h.activation(out=t1[:], in_=t1[:], func=AF.Exp, scale=-KAP), r=["t1"], w=["t1"])
            S.op("dve", lambda h, gi=gi: h.scalar_tensor_tensor(out=AR[gi][:, :, 0:64], in0=c3(kk), scalar=-1.0, in1=c3(t1), op0=ALU.mult, op1=ALU.mult),
                 r=["kk", "t1"], w=[f"AR{gi}"])
            S.op("dve", lambda h: h.tensor_tensor(out=c3(t2), in0=c3(cs), in1=cLs[:].unsqueeze(2).broadcast_to([64, NJ, 64]), op=ALU.subtract), r=["cs", "cLs"], w=["t2"])
            S.op("act", lambda h: h.activation(out=t2[:], in_=t2[:], func=AF.Exp, scale=KAP), r=["t2"], w=["t2"])
            S.op("dve", lambda h, gi=gi: h.tensor_tensor(out=BKe[gi][:, :, 0:64], in0=c3(bal), in1=c3(t2), op=ALU.mult), r=["bal", "t2"], w=[f"BKe{gi}"])
            S.op("pool", lambda h, gi=gi: h.tensor_tensor(out=BKe[gi][:, :, 64:128], in0=c3(kmod), in1=c3(t2), op=ALU.mult), r=["kmod", "t2"], w=[f"BKe{gi}"])
            S.op("dve", lambda h, gi=gi, hd=hd: h.scalar_tensor_tensor(out=rkrp[gi][:, :, :], in0=c3(r_f), scalar=vec[:, hd, 4:5], in1=c3(kmod), op0=ALU.mult, op1=ALU.mult),
                 r=["r_f", "kmod", "vec"], w=[f"rkrp{gi}"])
        def head_gn(S, hd):
            gi = hd % G2
            hc = slice(hd * 64, (hd + 1) * 64)
            def head_g(S, j):
                psn = ps_n if j == 0 else ps_tok
                psn_name = "ps_n" if j == 0 else "ps_tok"
                Nk_, NkT_, Pm_ = Nk[j], NkT[j], Pm[j]
                S.op("pe", lambda h, gi=gi, j=j: h.matmul(ps_g[0:64, 0:128], lhsT=BK[gi][:, j, 0:64], rhs=AR[gi][:, j, :], start=True, stop=True), r=[f"BK{gi}", f"AR{gi}"], w=["ps_g"])
                S.op("pe", lambda h, gi=gi, j=j: h.matmul(ps_g[0:64, 128:256], lhsT=BK[gi][:, j, 64:128], rhs=AR[gi][:, j, :], start=True, stop=True), r=[f"BK{gi}", f"AR{gi}"], w=["ps_g"])
                S.op("pe", lambda h, gi=gi, j=j: h.matmul(ps_g[0:64, 256:320], lhsT=AR[gi][:, j, 0:64], rhs=BK[gi][:, j, 0:64], start=True, stop=True), r=[f"BK{gi}", f"AR{gi}"], w=["ps_g"])
                S.op("dve", lambda h, gi=gi, j=j: h.tensor_tensor(out=Gm[gi][:, j, 0:128], in0=ps_g[0:64, 0:128], in1=maskG[0:64, :], op=ALU.mult), r=["ps_g", "maskG"], w=[f"Gm{gi}"])
                S.op("dve", lambda h, gi=gi, j=j: h.tensor_tensor(out=Gm[gi][:, j, 128:256], in0=ps_g[0:64, 128:256], in1=maskG[0:64, :], op=ALU.mult), r=["ps_g", "maskG"], w=[f"Gm{gi}"])
                S.op("dve", lambda h: h.tensor_tensor(out=NkT_[0][:], in0=ps_g[0:64, 256:320], in1=maskNT[:], op=ALU.mult), r=["ps_g", "maskNT"], w=[f"NkT{j}_0"])
                S.op("pool", lambda h, gi=gi, j=j: h.tensor_copy(out=Nk_[0][:], in_=Gm[gi][:, j, 0:64]), r=[f"Gm{gi}"], w=[f"Nk{j}_0"])
                S.op("pool", lambda h, gi=gi, j=j: h.tensor_tensor(out=Pm_[0][:], in0=Gm[gi][:, j, 0:64], in1=identf[0:64, 0:64], op=ALU.add), r=[f"Gm{gi}", "identf"], w=[f"Pm{j}_0"])
                for q in range(2):
                    S.op("pe", lambda h, gi=gi, j=j, q=q: h.transpose(out=ps_g[0:64, 320 + q * 64:384 + q * 64], in_=BKe[gi][:, j, q * 64:(q + 1) * 64], identity=identf[0:64, 0:64]), r=[f"BKe{gi}", "identf"], w=["ps_g"])
                S.op("act", lambda h, gi=gi, j=j: h.copy(out=BKeT[gi][:, j, :], in_=ps_g[0:64, 320:448]), r=["ps_g"], w=[f"BKeT{gi}"])
                S.op("pool", lambda h, gi=gi, j=j: h.tensor_scalar(out=dcL[gi][:, j, :], in0=identf[0:64, 0:64], scalar1=cLd[gi][:, j:j + 1], scalar2=None, op0=ALU.mult), r=["identf", f"cLd{gi}"], w=[f"dcL{gi}"])
                S.op("pe", lambda h, gi=gi, j=j: h.matmul(ps_g[0:64, 448 + j:449 + j], lhsT=rkrp[gi][:, j, :], rhs=ones64[:, 0:1], start=True, stop=True), r=[f"rkrp{gi}", "ones64"], w=["ps_g"])
                S.op("act", lambda h, gi=gi, j=j: h.copy(out=bon[gi][:, j:j + 1], in_=ps_g[0:64, 448 + j:449 + j]), r=["ps_g"], w=[f"bon{gi}"])
            def head_n(S, j):
                psn = ps_n if j == 0 else ps_tok
                psn_name = "ps_n" if j == 0 else "ps_tok"
                Nk_, NkT_, Pm_ = Nk[j], NkT[j], Pm[j]
                cur = 0
                for sidx in range(5):
                    nx = 1 - cur
                    last = (sidx == 4)
                    S.op("pe", lambda h, cur=cur: h.matmul(psn[0:64, 0:64], lhsT=Nk_[cur][:], rhs=NkT_[cur][:], start=True, stop=True), r=[f"Nk{j}_{cur}", f"NkT{j}_{cur}"], w=[psn_name])
                    S.op("act", lambda h, nx=nx: h.copy(out=NkT_[nx][:], in_=psn[0:64, 0:64]), r=[psn_name], w=[f"NkT{j}_{nx}"])
                    if not last:
                        S.op("pe", lambda h, cur=cur: h.matmul(psn[0:64, 64:128], lhsT=NkT_[cur][:], rhs=Nk_[cur][:], start=True, stop=True), r=[f"Nk{j}_{cur}", f"NkT{j}_{cur}"], w=[psn_name])
                        S.op("act", lambda h, nx=nx: h.copy(out=Nk_[nx][:], in_=psn[0:64, 64:128]), r=[psn_name], w=[f"Nk{j}_{nx}"])
                    S.op("pe", lambda h, cur=cur, nx=nx: h.matmul(psn[0:64, 128:192], lhsT=NkT_[nx][:], rhs=Pm_[cur][:], start=True, stop=True), r=[f"NkT{j}_{nx}", f"Pm{j}_{cur}"], w=[psn_name])
                    if last:
                        S.op("dve", lambda h, cur=cur, gi=gi, j=j: h.tensor_tensor(out=Tm[gi][:, j, :], in0=psn[0:64, 128:192], in1=Pm_[cur][:], op=ALU.add), r=[psn_name, f"Pm{j}_{cur}"], w=[f"Tm{gi}"])
                    else:
                        S.op("dve", lambda h, cur=cur, nx=nx: h.tensor_tensor(out=Pm_[nx][:], in0=psn[0:64, 128:192], in1=Pm_[cur][:], op=ALU.add), r=[psn_name, f"Pm{j}_{cur}"], w=[f"Pm{j}_{nx}"])
                    cur = nx
            for j in range(NJ):
                head_g(S, j)
            sj = [Stream() for _ in range(NJ)]
            for j in range(NJ):
                head_n(sj[j], j)
            merge_streams(S, sj)
            for j in range(NJ):
                S.op("dve", lambda h, gi=gi, j=j: h.tensor_scalar(out=Dg[gi][:, j, :], in0=identf[0:64, 0:64], scalar1=bon[gi][:, j:j + 1], scalar2=None, op0=ALU.mult),
                     r=["identf", f"bon{gi}"], w=[f"Dg{gi}"])
        def head_rec(S, hd):
            gi = hd % G2
            hc = slice(hd * 64, (hd + 1) * 64)
            for j in range(NJ):
                gj = mc * NJ + j
                s_in = ST[hd][gj % 2]
                s_out = ST[hd][(gj + 1) % 2]
                sin_n = f"ST{hd}_{gj % 2}"
                sout_n = f"ST{hd}_{(gj + 1) % 2}"
                wi = gj % 2
                VT = vwin[:, j, hc]
                UT = uT[:, j, hc]
                vn = f"vwin{j}"
                un = f"uT{j}_{hd}"
                S.op("pe", lambda h, gi=gi, j=j, VT=VT, hc=hc: h.matmul(ps_bv[0:64, hc], lhsT=Dg[gi][:, j, :], rhs=VT, start=True, stop=True), r=[f"Dg{gi}", vn], w=["ps_bv"])
                S.op("act", lambda h, j=j, hc=hc: h.copy(out=bv_all[:, j, hc], in_=ps_bv[0:64, hc]), r=["ps_bv"], w=["bv_all"])
                S.op("pe", lambda h, gi=gi, j=j, s_in=s_in: h.matmul(ps_rec[0:64, 0:64], lhsT=AR[gi][:, j, 0:64], rhs=s_in[:], start=True, stop=False), r=[f"AR{gi}", sin_n], w=["ps_rec"])
                S.op("pe", lambda h, gi=gi, j=j, VT=VT: h.matmul(ps_rec[0:64, 0:64], lhsT=Gm[gi][:, j, 128:192], rhs=VT, start=False, stop=True), r=[f"Gm{gi}", vn], w=["ps_rec"])
                S.op("act", lambda h, wi=wi: h.copy(out=WT[wi][:], in_=ps_rec[0:64, 0:64]), r=["ps_rec"], w=[f"WT{wi}"])
                S.op("pe", lambda h, gi=gi, j=j, wi=wi: h.matmul(ps_rec[0:64, 64:128], lhsT=Tm[gi][:, j, :], rhs=WT[wi][:], start=True, stop=True), r=[f"Tm{gi}", f"WT{wi}"], w=["ps_rec"])
                S.op("act", lambda h, UT=UT: h.copy(out=UT, in_=ps_rec[0:64, 64:128]), r=["ps_rec"], w=[un])
                S.op("pe", lambda h, gi=gi, j=j, s_in=s_in, hc=hc: h.matmul(ps_y[0:64, hc], lhsT=AR[gi][:, j, 64:128], rhs=s_in[:], start=True, stop=False), r=[f"AR{gi}", sin_n], w=["ps_y"])
                S.op("pe", lambda h, gi=gi, j=j, hc=hc, UT=UT: h.matmul(ps_y[0:64, hc], lhsT=Gm[gi][:, j, 64:128], rhs=UT, start=False, stop=False), r=[f"Gm{gi}", un], w=["ps_y"])
                S.op("pe", lambda h, gi=gi, j=j, hc=hc, VT=VT: h.matmul(ps_y[0:64, hc], lhsT=Gm[gi][:, j, 192:256], rhs=VT, start=False, stop=True), r=[f"Gm{gi}", vn], w=["ps_y"])
                S.op("dve", lambda h, j=j, hc=hc: h.tensor_copy(out=y_all[:, j, hc], in_=ps_y[0:64, hc]), r=["ps_y"], w=["y_all"])
                S.op("pe", lambda h, gi=gi, j=j, s_in=s_in: h.matmul(ps_rec[0:64, 128:192], lhsT=dcL[gi][:, j, :], rhs=s_in[:], start=True, stop=False), r=[f"dcL{gi}", sin_n], w=["ps_rec"])
                S.op("pe", lambda h, gi=gi, j=j, UT=UT: h.matmul(ps_rec[0:64, 128:192], lhsT=BKeT[gi][:, j, 0:64], rhs=UT, start=False, stop=False), r=[f"BKeT{gi}", un], w=["ps_rec"])
                S.op("pe", lambda h, gi=gi, j=j, VT=VT: h.matmul(ps_rec[0:64, 128:192], lhsT=BKeT[gi][:, j, 64:128], rhs=VT, start=False, stop=True), r=[f"BKeT{gi}", vn], w=["ps_rec"])
                S.op("act", lambda h, s_out=s_out: h.copy(out=s_out[:], in_=ps_rec[0:64, 128:192]), r=["ps_rec"], w=[sout_n])
        for step in range(8 + 2):
            streams = []
            if step < 8:
                st_ = Stream()
                head_prep(st_, step)
                streams.append(st_)
            if 0 <= step - 1 < 8:
                st_ = Stream()
                head_gn(st_, step - 1)
                streams.append(st_)
            if 0 <= step - 2 < 8:
                st_ = Stream()
                head_rec(st_, step - 2)
                streams.append(st_)
            merge_streams(S, streams)
        for j in range(NJ):
            y3 = y_all[:, j, :].rearrange("p (h v) -> p h v", h=8)
            S.op("dve", lambda h, y3=y3: h.tensor_reduce(out=gst[:, 0, :], in_=y3, axis=AX.X, op=ALU.add), r=["y_all"], w=["gst"])
            S.op("act", lambda h, j=j: h.activation(out=yn[:], in_=y_all[:, j, :], func=AF.Square), r=["y_all"], w=["yn"])
            S.op("dve", lambda h: h.tensor_reduce(out=gst[:, 1, :], in_=yn[:].rearrange("p (h v) -> p h v", h=8), axis=AX.X, op=ALU.add), r=["yn"], w=["gst"])
            S.op("dve", lambda h: h.tensor_scalar(out=gst[:, 0, :], in0=gst[:, 0, :], scalar1=1.0 / 64, scalar2=None, op0=ALU.mult), r=["gst"], w=["gst"])
            S.op("dve", lambda h: h.tensor_tensor(out=gst[:, 2, :], in0=gst[:, 0, :], in1=gst[:, 0, :], op=ALU.mult), r=["gst"], w=["gst"])
            S.op("dve", lambda h: h.scalar_tensor_tensor(out=gst[:, 1, :], in0=gst[:, 1, :], scalar=1.0 / 64, in1=gst[:, 2, :], op0=ALU.mult, op1=ALU.subtract), r=["gst"], w=["gst"])
            S.op("act", lambda h: h.activation(out=gst[:, 1, :], in_=gst[:, 1, :], func=AF.Sqrt, bias=GN_EPS), r=["gst"], w=["gst"])
            S.op("dve", lambda h: h.reciprocal(out=gst[:, 1, :], in_=gst[:, 1, :]), r=["gst"], w=["gst"])
            for hd in range(8):
                hc = slice(hd * 64, (hd + 1) * 64)
                S.op("dve", lambda h, j=j, hd=hd, hc=hc: h.tensor_scalar(out=yn[:, hc], in0=y_all[:, j, hc], scalar1=gst[:, 0, hd:hd + 1], scalar2=gst[:, 1, hd:hd + 1],
                                                                      op0=ALU.subtract, op1=ALU.mult), r=["y_all", "gst"], w=["yn"])
            S.op("pool", lambda h: h.tensor_tensor(out=yn[:], in0=yn[:], in1=lnw[:], op=ALU.mult), r=["yn", "lnw"], w=["yn"])
            S.op("pool", lambda h: h.tensor_tensor(out=yn[:], in0=yn[:], in1=lnb[:], op=ALU.add), r=["yn", "lnb"], w=["yn"])
            S.op("pool", lambda h, j=j: h.tensor_tensor(out=yn[:], in0=yn[:], in1=bv_all[:, j, :], op=ALU.add), r=["yn", "bv_all"], w=["yn"])
            S.op("dve", lambda h, j=j: h.tensor_tensor(out=yz[:], in0=yn[:], in1=zs[:, j, :], op=ALU.mult), r=["yn", "zs"], w=["yz"])
            for c in range(4):
                S.op("pe", lambda h, c=c: h.transpose(out=ps_tr[:, c * 64:(c + 1) * 64], in_=yz[:, c * 128:(c + 1) * 128], identity=ident[0:64, 0:64]), r=["yz", "ident"], w=["ps_tr"])
            S.op("act", lambda h: h.copy(out=yzT[:], in_=ps_tr[:, 0:256].rearrange("p (c n) -> p c n", c=4)), r=["ps_tr"], w=["yzT"])
            for hf in range(2):
                for c in range(4):
                    S.op("pe", lambda h, hf=hf, c=c: h.matmul(ps_tok[0:64, :], lhsT=yzT[:, c, :], rhs=wo[:, c, hf * 512:(hf + 1) * 512], start=(c == 0), stop=(c == 3)), r=["yzT", "wo"], w=["ps_tok"])
                S.op("act", lambda h, hf=hf: h.copy(out=pt[:, hf * 512:(hf + 1) * 512], in_=ps_tok[0:64, :]), r=["ps_tok"], w=["pt"])
            rows = slice(T0 + j * 64, T0 + (j + 1) * 64)
            S.op("sp", lambda h, rows=rows: h.dma_start(out=p_out[rows, :], in_=pt[:]), r=["pt"], dma=True)
    return nc, es, S


def prep_B(z, half, MC=256):
    d = {}
    w_in = z['b_w_in'][0]
    own = slice(half * 512, (half + 1) * 512)
    d['w4'] = np.ascontiguousarray(np.stack([w_in[:, s * 1024:(s + 1) * 1024][:, own] for s in range(4)]))
    d['lw'] = np.ascontiguousarray(np.stack([z['b_w1'][0], z['b_a1'][0]]))
    d['l2'] = np.ascontiguousarray(np.stack([z['b_w2'][0][:, own], z['b_a2'][0][:, own]]))
    d['wo'] = np.ascontiguousarray(z['b_w_out'][0][own, :])
    mu = z['b_mu'][0]
    d['muT'] = np.ascontiguousarray(mu.reshape(6, 8, 128).transpose(2, 0, 1))
    vecs = np.zeros((64, 8, 8), np.float32)
    def fm(v):
        return v[own].reshape(8, 64).T
    vecs[:, :, 0] = fm(z['b_w0'][0]); vecs[:, :, 1] = fm(z['b_a0'][0]); vecs[:, :, 2] = fm(z['b_k_k'][0]); vecs[:, :, 3] = fm(z['b_k_a'][0])
    vecs[:, :, 4] = fm(z['b_r_k'][0].reshape(-1))
    d['vecs'] = vecs
    d['lnw'] = np.ascontiguousarray(z['b_lnx_w'][0][own][None, :])
    d['lnb'] = np.ascontiguousarray(z['b_lnx_b'][0][own][None, :])
    j = np.arange(64)[:, None]; i = np.arange(64)[None, :]
    strict = (j < i).astype(np.float32); incl = (j <= i).astype(np.float32)
    row = np.concatenate([strict, incl], 1)
    d['maskG'] = np.ascontiguousarray(np.concatenate([row, row], 0))
    d['maskNT'] = np.ascontiguousarray(strict.T)
    rm = np.ones((64, MC), np.float32); rm[:, ::64] = 0.0
    d['resetm'] = rm
    d['g'] = z['norm_g'][1:2].copy()
    d['ident'] = np.eye(128, dtype=np.float32)
    return d


NEGM = -30000.0


def build_C(nsrc=1, debug=False):
    T = CFG.T
    NT = T // 128
    NCMP = T // 16 - 1
    NKT = (NCMP + 127) // 128
    nc = get_nc()
    es = ExitStack()
    S = get_sched(nc, es)
    srcs = [dram_in(nc, f"xin{k}", [T, D]) for k in range(nsrc)]
    xs_out = dram_out(nc, "xs", [T, D]) if nsrc > 1 else None
    g_row = dram_in(nc, "g", [1, D])
    ident_d = dram_in(nc, "ident", [128, 128])
    wq_d = dram_in(nc, "wq", [D, 512])
    wkv_d = dram_in(nc, "wkv", [D, 6, 128])
    wg_d = dram_in(nc, "wg", [D, 24])
    wz_d = dram_in(nc, "wz", [D, 512])
    wo_d = dram_in(nc, "wo", [512, D])
    w1_d = dram_in(nc, "w1", [2, 64, 32, 128])
    w2_d = dram_in(nc, "w2", [2, 128, 64])
    pos_d = dram_in(nc, "posT", [2, 64, 32])
    bias_d = dram_in(nc, "biasT", [128, 4, 1024])
    F4_d = dram_in(nc, "F4", [512, 512])
    ka_d = dram_in(nc, "keepadd", [NT, 128, 128])
    E_d = dram_in(nc, "E", [64, NT, 128])
    ov_d = dram_in(nc, "ovl", [128, 2, 64])
    p_out = dram_out(nc, "p", [T, D])

    ps_tr = mk(nc, es, "ps_tr", [128, 1024], BF16, psum=True)
    ps_q = mk(nc, es, "ps_q", [128, 512], F32, psum=True)
    ps_z = mk(nc, es, "ps_z", [128, 512], F32, psum=True)
    ps_s = [mk(nc, es, f"ps_s{k}", [128, 512], F32, psum=True) for k in range(2)]
    po_c = mk(nc, es, "po_c", [128, 4, 128], F32, psum=True)
    po_s = mk(nc, es, "po_s", [128, 4, 128], F32, psum=True)
    po_w = mk(nc, es, "po_w", [128, 4, 128], F32, psum=True)
    p1 = P1(S, nc, es, srcs, xs_out, g_row, ident_d, ps_tr)
    ident = p1.ident
    xnTt = [mk(nc, es, f"xnTt{b}", [128, 8, 128], BF16) for b in range(2)]

    wq = mk(nc, es, "wq_s", [128, 8, 512], BF16)
    wkv = mk(nc, es, "wkv_s", [128, 8, 6, 128], BF16)
    wg = mk(nc, es, "wg_s", [128, 8, 24], BF16)
    wz = mk(nc, es, "wz_s", [128, 8, 512], BF16)
    wo = mk(nc, es, "wo_s", [128, 4, D], BF16)
    w1 = mk(nc, es, "w1_s", [64, 2, 32, 128], BF16)
    w2 = mk(nc, es, "w2_s", [128, 2, 64], BF16)
    posT = mk(nc, es, "posT_s", [64, 2, 32], BF16)
    biasT = mk(nc, es, "biasT_s", [128, 4, 1024], BF16)
    Em = mk(nc, es, "E_s", [64, NT, 128], BF16)
    S.op("pool", lambda h: h.dma_start(out=wq[:], in_=wq_d.rearrange("(c p) n -> p c n", p=128)), w=["wq"], dma=True)
    S.op("pool", lambda h: h.dma_start(out=wkv[:], in_=wkv_d.rearrange("(c p) s n -> p c s n", p=128)), w=["wkv"], dma=True)
    S.op("pool", lambda h: h.dma_start(out=wg[:], in_=wg_d.rearrange("(c p) n -> p c n", p=128)), w=["wg"], dma=True)
    S.op("pool", lambda h: h.dma_start(out=wz[:], in_=wz_d.rearrange("(c p) n -> p c n", p=128)), w=["wz"], dma=True)
    S.op("pool", lambda h: h.dma_start(out=wo[:], in_=wo_d.rearrange("(c p) n -> p c n", p=128)), w=["wo"], dma=True)
    S.op("pool", lambda h: h.dma_start(out=w1[:], in_=w1_d.rearrange("s d l h -> d s l h")), w=["w1"], dma=True)
    S.op("pool", lambda h: h.dma_start(out=w2[:], in_=w2_d.rearrange("s h d -> h s d")), w=["w2"], dma=True)
    S.op("pool", lambda h: h.dma_start(out=posT[:], in_=pos_d.rearrange("s d l -> d s l")), w=["posT"], dma=True)
    S.op("pool", lambda h: h.dma_start(out=biasT[:], in_=bias_d), w=["biasT"], dma=True)
    S.op("pool", lambda h: h.dma_start(out=Em[:], in_=E_d), w=["E"], dma=True)

    kvT = mk(nc, es, "kvT", [64, 2, 2, T], BF16)
    roll = mk(nc, es, "roll", [64, 2, 2, 144], BF16)
    vau = mk(nc, es, "vau", [128, NT, 2, 2, 65], BF16)
    S.op("dve", lambda h: h.memset(vau[:, :, :, :, 64:65], 1.0), w=["vau_ones"])
    S.op("dve", lambda h: h.memset(roll[:], 0.0), w=["roll"])
    kcmpT = mk(nc, es, "kcmpT", [64, 2, 256], BF16)
    vcau = mk(nc, es, "vcau", [128, 2, 2, 65], BF16)
    ovl = mk(nc, es, "ovl_s", [128, 2, 64], BF16)
    hidn = mk(nc, es, "hidn", [128, 4, 8], BF16)
    hidv = mk(nc, es, "hidv", [128, 2, 256], BF16)
    pbias = mk(nc, es, "pbias", [128, 2], F32)
    S.op("dve", lambda h: h.memset(kcmpT[:], 0.0), w=["kcmpT"])
    S.op("dve", lambda h: h.memset(vcau[:], 0.0), w=["vcau"])
    S.op("dve", lambda h: h.memset(vcau[:, :, :, 64:65], 1.0), r=["vcau"], w=["vcau"])
    S.op("dve", lambda h: h.memset(hidv[:], 0.0), w=["hidv"])
    S.op("pool", lambda h: h.dma_start(out=ovl[:], in_=ov_d), w=["ovl"], dma=True)
    for s in range(2):
        for l in range(32):
            S.op("pe", lambda h, s=s, l=l: h.matmul(ps_z[:, s:s + 1], lhsT=w1[:, s, l, :], rhs=posT[:, s, l:l + 1], start=(l == 0), stop=(l == 31)),
                 r=["w1", "posT"], w=["ps_z"])
        S.op("act", lambda h, s=s: h.copy(out=pbias[:, s:s + 1], in_=ps_z[:, s:s + 1]), r=["ps_z"], w=["pbias"])

    NB = 2
    qT = [mk(nc, es, f"qT{b}", [64, 2, 4, 128], BF16) for b in range(NB)]
    zs = [mk(nc, es, f"zs{b}", [128, 512], BF16) for b in range(NB)]
    gt = [mk(nc, es, f"gt{b}", [128, 24], F32) for b in range(NB)]
    NP = 4
    pT = [mk(nc, es, f"pT{b}", [128, 512], BF16) for b in range(NP)]
    F4t = [mk(nc, es, f"F4t{b}", [128, 512], BF16) for b in range(2)]
    ka = [mk(nc, es, f"ka{b}", [128, 128], F32) for b in range(NB)]
    imp = mk(nc, es, "imp", [128, 64], F32)
    imp2 = mk(nc, es, "imp2", [128, 64], F32)
    m8 = mk(nc, es, "m8", [128, 16], F32)
    nsel = mk(nc, es, "nsel", [128, 64], BF16)
    nselT = mk(nc, es, "nselT", [64, 4, 128], BF16)
    rden = mk(nc, es, "rden", [128, 3, 4], F32)
    cf = mk(nc, es, "cf", [128, 3, 4], F32)
    y = mk(nc, es, "y", [128, 512], F32)
    yz = [mk(nc, es, f"yz{b}", [128, 512], BF16) for b in range(NB)]
    yzT = [mk(nc, es, f"yzT{b}", [128, 4, 128], BF16) for b in range(NB)]
    pt = [mk(nc, es, f"pt{b}", [128, D], F32) for b in range(NB)]
    pcount = [0]
    scount = [0]

    def st_tile(g, b, lhsT_ap, lhs_bufs, extra, rhs_aug, rhs_bufs, po, first, last, ncol=65):
        si = scount[0] % 2
        scount[0] += 1
        pi = pcount[0] % NP
        pcount[0] += 1
        n_extra = len(extra)
        S.op("pe", lambda h: h.matmul(ps_s[si][:, :], lhsT=lhsT_ap, rhs=qT[b][:, g, :, :].rearrange("p r n -> p (r n)"), start=True, stop=(n_extra == 0)),
             r=lhs_bufs + [f"qT{b}_{g}"], w=[f"ps_s{si}"])
        for j, (el, er, ebufs) in enumerate(extra):
            S.op("pe", lambda h, el=el, er=er, j=j: h.matmul(ps_s[si][:, :], lhsT=el, rhs=er, start=False, stop=(j == n_extra - 1)),
                 r=ebufs, w=[f"ps_s{si}"])
        S.op("act", lambda h: h.activation(out=pT[pi][:], in_=ps_s[si][:, :], func=AF.Exp), r=[f"ps_s{si}"], w=[f"pT{pi}"])
        return pi

    for qt in range(NT):
        b = qt % NB
        tq = slice(qt * 128, (qt + 1) * 128)
        xk = f"xnTt{qt % 2}"
        xn = xnTt[qt % 2]
        S.op("sp", lambda h, b=b, qt=qt: h.dma_start(out=ka[b][:], in_=ka_d[qt]), w=[f"ka{b}"], dma=True)
        p1.tile(qt, xn[:, :, :], xk)
        if qt > 0:
            S.op("pool", lambda h: h.tensor_copy(out=roll[:, :, :, 0:16], in_=roll[:, :, :, 128:144]), r=["roll"], w=["roll"])
        for grp, (wss, psx, nm) in enumerate((((0, 1), ps_q, "ps_q"), ((2, 4), ps_z, "ps_z"))):
            for si, ws in enumerate(wss):
                for g in range(2):
                    c0 = (si * 2 + g) * 128
                    for dc in range(8):
                        S.op("pe", lambda h, ws=ws, g=g, dc=dc, c0=c0, psx=psx, xn=xn: h.matmul(psx[0:64, c0:c0 + 128], lhsT=wkv[:, dc, ws, g * 64:(g + 1) * 64], rhs=xn[:, dc, :],
                                                                                     start=(dc == 0), stop=(dc == 7)), r=["wkv", xk], w=[nm])
            if grp == 0:
                S.op("act", lambda h, psx=psx: h.copy(out=roll[:, :, :, 16:144], in_=psx[0:64, :].rearrange("p (s g n) -> p s g n", s=2, g=2)), r=[nm], w=["roll"])
            else:
                S.op("act", lambda h, psx=psx, tq=tq: h.copy(out=kvT[:, :, :, tq], in_=psx[0:64, :].rearrange("p (s g n) -> p s g n", s=2, g=2)), r=[nm], w=[f"kvT_{qt // 4}"])
        for jj, ws in enumerate((3, 5)):
            for dc in range(8):
                S.op("pe", lambda h, dc=dc, ws=ws, jj=jj, xn=xn: h.matmul(ps_z[:, jj * 128:(jj + 1) * 128], lhsT=xn[:, dc, :], rhs=wkv[:, dc, ws, :],
                                                                     start=(dc == 0), stop=(dc == 7)), r=["wkv", xk], w=["ps_z"])
        S.op("dve", lambda h, qt=qt: h.tensor_copy(out=vau[:, qt, :, :, 0:64], in_=ps_z[:, 0:256].rearrange("p (j g d) -> p j g d", j=2, g=2)),
             r=["ps_z"], w=[f"vau{qt}"])
        m0 = 1 if qt == 0 else 0
        nb = 8 - m0
        n0 = 8 * qt - 1 + m0
        for s in range(2):
            for g in range(2):
                c0 = (s * 2 + g) * 8
                for l in range(32):
                    S.op("pe", lambda h, s=s, g=g, l=l, c0=c0, nb=nb, m0=m0: h.matmul(ps_q[:, c0:c0 + nb], lhsT=w1[:, s, l, :], rhs=roll[:, s, g, l + 16 * m0:l + 16 * 7 + 1:16],
                                                                         start=(l == 0), stop=(l == 31)), r=["w1", "roll"], w=["ps_q"])
        for s in range(2):
            S.op("act", lambda h, s=s, nb=nb: h.activation(out=hidn[:, s * 2:s * 2 + 2, 0:nb], in_=ps_q[:, s * 16:s * 16 + 16].rearrange("p (g n) -> p g n", g=2)[:, :, 0:nb],
                                                    func=AF.Silu, bias=pbias[:, s:s + 1]), r=["ps_q", "pbias"], w=["hidn"])
        for g in range(2):
            S.op("pe", lambda h, g=g, nb=nb: h.matmul(ps_z[0:64, g * 8:g * 8 + nb], lhsT=w2[:, 0, :], rhs=hidn[:, g, 0:nb], start=True, stop=True), r=["w2", "hidn"], w=["ps_z"])
        S.op("act", lambda h, nb=nb, n0=n0: h.copy(out=kcmpT[:, :, n0:n0 + nb], in_=ps_z[0:64, 0:16].rearrange("p (g n) -> p g n", g=2)[:, :, 0:nb]), r=["ps_z"], w=["kcmpT"])
        S.op("pool", lambda h, nb=nb, n0=n0: h.tensor_copy(out=hidv[:, :, n0:n0 + nb], in_=hidn[:, 2:4, 0:nb]), r=["hidn"], w=["hidv"])
        for nt in sorted(set([n0 // 128, (n0 + nb - 1) // 128])):
            for g in range(2):
                S.op("pe", lambda h, g=g, nt=nt: h.matmul(ps_z[:, 128 + g * 64:192 + g * 64], lhsT=hidv[:, g, nt * 128:(nt + 1) * 128], rhs=w2[:, 1, :], start=True, stop=True),
                     r=["w2", "hidv"], w=["ps_z"])
            S.op("act", lambda h, nt=nt: h.copy(out=vcau[:, nt, :, 0:64], in_=ps_z[:, 128:256].rearrange("p (g d) -> p g d", g=2)), r=["ps_z"], w=["vcau"])
        for g in range(2):
            for r in range(4):
                for dc in range(8):
                    col = (g * 4 + r) * 64
                    S.op("pe", lambda h, g=g, r=r, dc=dc, col=col, xn=xn: h.matmul(ps_q[0:64, r * 128:(r + 1) * 128], lhsT=wq[:, dc, col:col + 64],
                                                                               rhs=xn[:, dc, :], start=(dc == 0), stop=(dc == 7)),
                         r=["wq", xk], w=["ps_q"])
            S.op("act", lambda h, g=g, b=b: h.activation(out=qT[b][:, g, :, :], in_=ps_q[0:64, :].rearrange("p (r n) -> p r n", r=4),
                                                        func=AF.Copy, scale=0.125), r=["ps_q"], w=[f"qT{b}_{g}"])
        for dc in range(8):
            S.op("pe", lambda h, dc=dc, xn=xn: h.matmul(ps_z[:, :], lhsT=xn[:, dc, :], rhs=wz[:, dc, :], start=(dc == 0), stop=(dc == 7)),
                 r=["wz", xk], w=["ps_z"])
        S.op("act", lambda h, b=b: h.activation(out=zs[b][:], in_=ps_z[:, :], func=AF.Silu), r=["ps_z"], w=[f"zs{b}"])
        for dc in range(8):
            S.op("pe", lambda h, dc=dc, xn=xn: h.matmul(ps_z[:, 0:24], lhsT=xn[:, dc, :], rhs=wg[:, dc, :], start=(dc == 0), stop=(dc == 7)),
                 r=["wg", xk], w=["ps_z"])
        S.op("act", lambda h, b=b: h.activation(out=gt[b][:], in_=ps_z[:, 0:24], func=AF.Sigmoid), r=["ps_z"], w=[f"gt{b}"])
        cnts = []
        for nt in range(NKT):
            mmax = nt * 128 + 127 - 8 * qt
            mmin = nt * 128 - 8 * qt
            if mmin > 6:
                continue
            masked = mmax > -2
            cnts.append((nt, masked))
        for (nt, masked) in cnts:
            if masked:
                j0 = 128 * nt - 8 * qt + 248
                S.op("pool", lambda h, nt=nt, j0=j0: h.dma_start(out=F4t[nt][:], in_=F4_d[j0:j0 + 128, :]), w=[f"F4t{nt}"], dma=True)
        for g in range(2):
            pis = []
            for (nt, masked) in cnts:
                extra = [(ident[:], F4t[nt][:], ["p1id", f"F4t{nt}"])] if masked else []
                pi = st_tile(g, b, kcmpT[:, g, nt * 128:(nt + 1) * 128], ["kcmpT"], extra, None, None, None, None, None)
                pis.append((nt, pi))
            for r in range(4):
                for j, (nt, pi) in enumerate(pis):
                    S.op("pe", lambda h, g=g, r=r, nt=nt, pi=pi, j=j: h.matmul(po_c[:, r, 0:65], lhsT=pT[pi][:, r * 128:(r + 1) * 128], rhs=vcau[:, nt, g, :],
                                                                             start=(j == 0), stop=(j == len(pis) - 1)), r=[f"pT{pi}", "vcau"], w=["po_c"])
            for r in range(4):
                for j, (nt, pi) in enumerate(pis):
                    S.op("pe", lambda h, g=g, r=r, nt=nt, pi=pi, j=j: h.matmul(po_w[:, r, 0:64], lhsT=pT[pi][:, r * 128:(r + 1) * 128], rhs=ovl[:, nt, :],
                                                                             start=(j == 0), stop=(j == len(pis) - 1)), r=[f"pT{pi}", "ovl"], w=["po_w"])
            S.op("dve", lambda h: h.tensor_scalar(out=rden[:, 0, :], in0=po_c[:, :, 64], scalar1=1e-30, scalar2=None, op0=ALU.add), r=["po_c"], w=["rden0"])
            S.op("dve", lambda h: h.reciprocal(out=rden[:, 0, :], in_=rden[:, 0, :]), r=["rden0"], w=["rden0"])
            S.op("dve", lambda h: h.tensor_scalar(out=imp[:], in0=po_w[:, 0, 0:64], scalar1=rden[:, 0, 0:1], scalar2=None, op0=ALU.mult), r=["po_w", "rden0"], w=["imp"])
            for r in range(1, 4):
                S.op("dve", lambda h, r=r: h.scalar_tensor_tensor(out=imp[:], in0=po_w[:, r, 0:64], scalar=rden[:, 0, r:r + 1], in1=imp[:], op0=ALU.mult, op1=ALU.add),
                     r=["po_w", "rden0", "imp"], w=["imp"])
            S.op("dve", lambda h, b=b: h.tensor_tensor(out=imp[:], in0=imp[:], in1=ka[b][:, 0:64], op=ALU.mult), r=["imp", f"ka{b}"], w=["imp"])
            S.op("dve", lambda h, b=b: h.tensor_tensor(out=imp[:], in0=imp[:], in1=ka[b][:, 64:128], op=ALU.add), r=["imp", f"ka{b}"], w=["imp"])
            S.op("dve", lambda h: h.max(out=m8[:, 0:8], in_=imp[:]), r=["imp"], w=["m8"])
            S.op("dve", lambda h: h.match_replace(out=imp2[:], in_to_replace=m8[:, 0:8], in_values=imp[:], imm_value=-3.0e38), r=["imp", "m8"], w=["imp2"])
            S.op("dve", lambda h: h.max(out=m8[:, 8:16], in_=imp2[:]), r=["imp2"], w=["m8"])
            S.op("dve", lambda h: h.tensor_scalar(out=imp2[:], in0=imp[:], scalar1=m8[:, 15:16], scalar2=1.0, op0=ALU.is_ge, op1=ALU.subtract),
                 r=["imp", "m8"], w=["imp2"])
            S.op("dve", lambda h: h.tensor_scalar(out=nsel[:], in0=imp2[:], scalar1=-NEGM, scalar2=None, op0=ALU.mult), r=["imp2"], w=["nsel"])
            S.op("pe", lambda h: h.transpose(out=ps_tr[0:64, 0:128], in_=nsel[:], identity=ident[:]), r=["nsel", "p1id"], w=["p1pstr"])
            for r in range(4):
                S.op("act", lambda h, r=r: h.copy(out=nselT[:, r, :], in_=ps_tr[0:64, 0:128]), r=["p1pstr"], w=["nselT"])
            kts = [kt for kt in range(qt - 4, qt + 1) if kt >= 0]
            jobs = [("s", kt, kt) for kt in range(qt + 1)] + [("w", kt, jj) for jj, kt in enumerate(kts)]

            def do_S(job, g=g, b=b, qt=qt):
                kind, kt, jj = job
                if kind == "s":
                    cls = 0 if kt == qt else (1 if kt == qt - 1 else 3)
                    extra = [(Em[:, kt, :], nselT[:].rearrange("p r n -> p (r n)"), ["E", "nselT"]),
                             (ident[:], biasT[:, cls, g * 512:(g + 1) * 512], ["p1id", "biasT"])]
                    return st_tile(g, b, kvT[:, 0, g, kt * 128:(kt + 1) * 128], [f"kvT_{kt // 4}"], extra, None, None, None, None, None)
                dq = qt - kt
                cls = 0 if dq == 0 else (1 if dq == 1 else (2 if dq == 4 else 3))
                extra = [(ident[:], biasT[:, cls, g * 512:(g + 1) * 512], ["p1id", "biasT"])]
                return st_tile(g, b, kvT[:, 1, g, kt * 128:(kt + 1) * 128], [f"kvT_{kt // 4}"], extra, None, None, None, None, None)

            def do_PV(job, pi, g=g, qt=qt, nk=len(kts)):
                kind, kt, jj = job
                for r in range(4):
                    if kind == "s":
                        S.op("pe", lambda h, r=r: h.matmul(po_s[:, r, 0:65], lhsT=pT[pi][:, r * 128:(r + 1) * 128], rhs=vau[:, kt, 0, g, :],
                                                           start=(kt == 0 and r == 0), stop=(kt == qt), skip_group_check=True),
                             r=[f"pT{pi}", f"vau{kt}", "vau_ones"], w=["po_s"])
                    else:
                        S.op("pe", lambda h, r=r: h.matmul(po_w[:, r, 0:65], lhsT=pT[pi][:, r * 128:(r + 1) * 128], rhs=vau[:, kt, 1, g, :],
                                                           start=(jj == 0 and r == 0), stop=(jj == nk - 1), skip_group_check=True),
                             r=[f"pT{pi}", f"vau{kt}", "vau_ones"], w=["po_w"])

            pend = None
            for job in jobs:
                pi_ = do_S(job)
                if pend is not None:
                    do_PV(*pend)
                pend = (job, pi_)
            do_PV(*pend)
            S.op("dve", lambda h: h.reciprocal(out=rden[:, 1, :], in_=po_s[:, :, 64]), r=["po_s"], w=["rden1"])
            S.op("dve", lambda h: h.reciprocal(out=rden[:, 2, :], in_=po_w[:, :, 64]), r=["po_w"], w=["rden2"])
            for j in range(3):
                S.op("dve", lambda h, j=j, g=g, b=b: h.tensor_tensor(out=cf[:, j, :], in0=rden[:, j, :], in1=gt[b][:, j * 8 + g * 4:j * 8 + g * 4 + 4], op=ALU.mult),
                     r=[f"rden{j}", f"gt{b}"], w=["cf"])
            for r in range(4):
                col = (g * 4 + r) * 64
                S.op("dve", lambda h, r=r, col=col: h.tensor_scalar(out=y[:, col:col + 64], in0=po_c[:, r, 0:64], scalar1=cf[:, 0, r:r + 1], scalar2=None, op0=ALU.mult),
                     r=["po_c", "cf"], w=["y"])
                S.op("dve", lambda h, r=r, col=col: h.scalar_tensor_tensor(out=y[:, col:col + 64], in0=po_s[:, r, 0:64], scalar=cf[:, 1, r:r + 1], in1=y[:, col:col + 64],
                                                                          op0=ALU.mult, op1=ALU.add), r=["po_s", "cf", "y"], w=["y"])
                S.op("dve", lambda h, r=r, col=col: h.scalar_tensor_tensor(out=y[:, col:col + 64], in0=po_w[:, r, 0:64], scalar=cf[:, 2, r:r + 1], in1=y[:, col:col + 64],
                                                                          op0=ALU.mult, op1=ALU.add), r=["po_w", "cf", "y"], w=["y"])
        S.op("pool", lambda h, b=b: h.tensor_tensor(out=yz[b][:], in0=y[:], in1=zs[b][:], op=ALU.mult), r=["y", f"zs{b}"], w=[f"yz{b}"])
        for c in range(4):
            S.op("pe", lambda h, c=c, b=b: h.transpose(out=ps_tr[:, c * 128:(c + 1) * 128], in_=yz[b][:, c * 128:(c + 1) * 128], identity=ident[:]),
                 r=[f"yz{b}", "p1id"], w=["p1pstr"])
        S.op("act", lambda h, b=b: h.copy(out=yzT[b][:], in_=ps_tr[:, 0:512].rearrange("p (c n) -> p c n", c=4)), r=["p1pstr"], w=[f"yzT{b}"])
        for hf in range(2):
            psy = ps_q if hf == 0 else ps_z
            nm = "ps_q" if hf == 0 else "ps_z"
            for c in range(4):
                S.op("pe", lambda h, hf=hf, c=c, b=b, psy=psy: h.matmul(psy[:, :], lhsT=yzT[b][:, c, :], rhs=wo[:, c, hf * 512:(hf + 1) * 512],
                                                                       start=(c == 0), stop=(c == 3)), r=[f"yzT{b}", "wo"], w=[nm])
            if hf == 0:
                S.op("act", lambda h, b=b, psy=psy: h.copy(out=pt[b][:, 0:512], in_=psy[:, :]), r=[nm], w=[f"pt{b}"])
            else:
                S.op("dve", lambda h, b=b, psy=psy: h.tensor_copy(out=pt[b][:, 512:1024], in_=psy[:, :]), r=[nm], w=[f"pt{b}"])
        S.op("sp", lambda h, b=b, tq=tq: h.dma_start(out=p_out[tq, :], in_=pt[b][:]), r=[f"pt{b}"], dma=True)
    return nc, es, S


def prep_C(z, half, T):
    NT = T // 128
    d = {}
    w_in = z['c_w_in'][0]
    d['wq'] = np.ascontiguousarray(w_in[:, half * 512:(half + 1) * 512])
    kv = []
    for s in range(6):
        base = 1024 + s * 256 + half * 128
        kv.append(w_in[:, base:base + 128])
    d['wkv'] = np.ascontiguousarray(np.stack(kv, 1))
    gcols = np.concatenate([2560 + j * 16 + half * 8 + np.arange(8) for j in range(3)])
    d['wg'] = np.ascontiguousarray(w_in[:, gcols])
    d['wz'] = np.ascontiguousarray(w_in[:, 2608 + half * 512:2608 + (half + 1) * 512])
    d['wo'] = np.ascontiguousarray(z['c_w_out'][0][half * 512:(half + 1) * 512, :])
    w1 = np.stack([z['c_cmp_k_w1'][0], z['c_cmp_v_w1'][0]])
    d['w1'] = np.ascontiguousarray(w1.reshape(2, 32, 64, 128).transpose(0, 2, 1, 3))
    d['w2'] = np.ascontiguousarray(np.stack([z['c_cmp_k_w2'][0], z['c_cmp_v_w2'][0]]))
    d['posT'] = np.ascontiguousarray(np.stack([z['c_cmp_pos_k'][0].T, z['c_cmp_pos_v'][0].T]))
    table = z['t5_table']
    tk = np.arange(128)[:, None]
    tq = np.arange(128)[None, :]
    bias = np.zeros((128, 4, 2, 4, 128), np.float32)
    for g in range(2):
        for r in range(4):
            hh = half * 8 + g * 4 + r
            d0 = tq - tk
            bias[:, 0, g, r, :] = np.where(d0 >= 0, table[t5_bucket_np(d0), hh], NEGM)
            d1 = tq - tk + 128
            bias[:, 1, g, r, :] = table[t5_bucket_np(d1), hh]
            bias[:, 2, g, r, :] = np.where(tq < tk, table[31, hh], NEGM)
            bias[:, 3, g, r, :] = table[31, hh]
    d['biasT'] = bias.reshape(128, 4, 1024)
    j = np.arange(512)[:, None]
    F = np.where(16 * (j - 248) + 31 <= tq, 0.0, NEGM).astype(np.float32)
    d['F4'] = np.ascontiguousarray(np.tile(F, (1, 4)))
    ka = np.zeros((NT, 128, 128), np.float32)
    sblk = np.arange(64)[None, :]
    for qt in range(NT):
        t = qt * 128 + np.arange(128)[:, None]
        cur = t // 64
        forced = (sblk == 0) | (sblk == cur) | (sblk == cur - 1)
        future = sblk * 64 > t
        ka[qt, :, 0:64] = np.where(forced | future, 0.0, 1.0)
        ka[qt, :, 64:128] = np.where(forced, 1e30, np.where(future, -1e30, 0.0))
    d['keepadd'] = ka
    E = np.zeros((64, NT, 128), np.float32)
    for kt in range(NT):
        E[2 * kt, kt, 0:64] = 1.0
        E[2 * kt + 1, kt, 64:128] = 1.0
    d['E'] = E
    n = np.arange(256)[:, None]
    s = np.arange(64)[None, :]
    ov = ((16 * n < 64 * s + 64) & (16 * n + 31 >= 64 * s)).astype(np.float32)
    d['ovl'] = np.ascontiguousarray(ov.reshape(2, 128, 64).transpose(1, 0, 2))
    d['g'] = z['norm_g'][2:3].copy()
    d['ident'] = np.eye(128, dtype=np.float32)
    return d


NBLK = 8
BW = 80


def build_D(nsrc=1, TCH=1024, debug=False):
    T = CFG.T
    nc = get_nc()
    es = ExitStack()
    S = get_sched(nc, es)
    srcs = [dram_in(nc, f"xin{k}", [T, D]) for k in range(nsrc)]
    xs_out = dram_out(nc, "xs", [T, D]) if nsrc > 1 else None
    g_row = dram_in(nc, "g", [1, D])
    ident_d = dram_in(nc, "ident", [128, 128])
    wu_d = dram_in(nc, "wu", [D, NBLK * BW])
    wz_d = dram_in(nc, "wz", [D, NBLK * BW])
    wo_d = dram_in(nc, "wo", [NBLK * BW, D])
    ga_d = dram_in(nc, "ga", [NBLK, BW, BW])
    gx_d = dram_in(nc, "gx", [NBLK, BW, BW])
    vec_d = dram_in(nc, "vecs", [BW, NBLK, 8])
    p_out = dram_out(nc, "p", [T, D])

    xnT = mk(nc, es, "xnT", [128, 8, T], BF16)
    ps = [mk(nc, es, f"ps{k}", [128, 512], F32, psum=True) for k in range(7)]
    ps_tr = mk(nc, es, "ps_tr", [128, 1024], BF16, psum=True)
    phase1(S, nc, es, srcs, xs_out, g_row, ident_d, ps_tr, xnT=xnT)

    wu = mk(nc, es, "wu_s", [128, 8, NBLK * BW], BF16)
    wz = mk(nc, es, "wz_s", [128, 8, NBLK * BW], BF16)
    wo = mk(nc, es, "wo_s", [BW, NBLK, D], BF16)
    ga = mk(nc, es, "ga_s", [BW, NBLK, BW], F32)
    gx = mk(nc, es, "gx_s", [BW, NBLK, BW], F32)
    vec = mk(nc, es, "vec_s", [BW, NBLK, 8], F32)
    der = mk(nc, es, "der_s", [BW, NBLK, 4], F32)
    S.op("pool", lambda h: h.dma_start(out=wu[:], in_=wu_d.rearrange("(c p) n -> p c n", p=128)), w=["wu"], dma=True)
    S.op("pool", lambda h: h.dma_start(out=wz[:], in_=wz_d.rearrange("(c p) n -> p c n", p=128)), w=["wz"], dma=True)
    S.op("pool", lambda h: h.dma_start(out=wo[:], in_=wo_d.rearrange("(b p) n -> p b n", p=BW)), w=["wo"], dma=True)
    S.op("sp", lambda h: h.dma_start(out=ga[:], in_=ga_d.rearrange("b p n -> p b n")), w=["ga"], dma=True)
    S.op("sp", lambda h: h.dma_start(out=gx[:], in_=gx_d.rearrange("b p n -> p b n")), w=["gx"], dma=True)
    S.op("sp", lambda h: h.dma_start(out=vec[:], in_=vec_d), w=["vec"], dma=True)
    S.op("act", lambda h: h.activation(out=der[:, :, 0:1], in_=vec[:, :, 7:8], func=AF.Exp, scale=-1.0), r=["vec"], w=["der"])
    S.op("act", lambda h: h.activation(out=der[:, :, 1:2], in_=der[:, :, 0:1], func=AF.Ln, bias=1.0), r=["der"], w=["der"])
    S.op("act", lambda h: h.mul(out=der[:, :, 2:3], in_=der[:, :, 1:2], mul=-8.0), r=["der"], w=["der"])

    NW = 2
    def wt(nm, cols=TCH, dt=F32):
        return [mk(nc, es, f"{nm}{b}", [BW, cols], dt) for b in range(NW)]
    u_t = wt("u_t", TCH + 3)
    uc_t = wt("uc_t"); zs_t = wt("zs_t"); r_t = wt("r_t"); i_t = wt("i_t"); a_t = r_t; m_t = [mk(nc, es, "m_t0", [BW, TCH], F32)] * NW; h_t = uc_t
    hz = mk(nc, es, "hz", [BW, NBLK, TCH], BF16)
    hlast = mk(nc, es, "hlast", [BW, NBLK], F32)
    uhalo = mk(nc, es, "uhalo", [BW, NBLK, 3], F32)
    pt = [mk(nc, es, "pt0", [128, D], F32)] * 2
    S.op("dve", lambda h: h.memset(hlast[:], 0.0), w=["hlast"])
    for b in range(NW):
        S.op("dve", lambda h, b=b: h.memset(u_t[b][:, 0:3], 0.0), w=[f"u{b}"])
    it = 0
    for tch in range(T // TCH):
        t0 = tch * TCH
        for blk in range(NBLK):
            b = it % NW
            pb = (it - 1) % NW
            it += 1
            cs = slice(blk * BW, (blk + 1) * BW)
            for hf in range(2):
                tk = slice(t0 + hf * 512, t0 + (hf + 1) * 512)
                for dc in range(8):
                    S.op("pe", lambda h, hf=hf, dc=dc, tk=tk, cs=cs: h.matmul(ps[hf][0:BW, :], lhsT=wu[:, dc, cs], rhs=xnT[:, dc, tk],
                                                                         start=(dc == 0), stop=(dc == 7)),
                         r=["wu", f"xnT{(t0 + hf * 512) // 512}"], w=[f"ps{hf}"])
            for hf in range(2):
                tk = slice(t0 + hf * 512, t0 + (hf + 1) * 512)
                for dc in range(8):
                    S.op("pe", lambda h, hf=hf, dc=dc, tk=tk, cs=cs: h.matmul(ps[2 + hf][0:BW, :], lhsT=wz[:, dc, cs], rhs=xnT[:, dc, tk],
                                                                         start=(dc == 0), stop=(dc == 7)),
                         r=["wz", f"xnT{(t0 + hf * 512) // 512}"], w=[f"ps{2 + hf}"])
            S.op("pool", lambda h, b=b, blk=blk: h.tensor_copy(out=u_t[b][:, 0:3], in_=uhalo[:, blk, :]), r=["uhalo%d" % blk], w=[f"u{b}"]) if tch > 0 else None
            for hf in range(2):
                S.op("act", lambda h, hf=hf, b=b: h.copy(out=u_t[b][:, 3 + hf * 512:3 + (hf + 1) * 512], in_=ps[hf][0:BW, :]),
                     r=[f"ps{hf}"], w=[f"u{b}"])
            for hf in range(2):
                S.op("act", lambda h, hf=hf, b=b: h.activation(out=zs_t[b][:, hf * 512:(hf + 1) * 512], in_=ps[2 + hf][0:BW, :], func=AF.Silu),
                     r=[f"ps{2 + hf}"], w=[f"zs{b}"])
            S.op("pool", lambda h, b=b, blk=blk: h.tensor_copy(out=uhalo[:, blk, :], in_=u_t[b][:, TCH:TCH + 3]), r=[f"u{b}"], w=["uhalo%d" % blk])
            if debug and tch == 0 and blk == 0:
                dbg(S, nc, "xnT", xnT[:, :, 0:512], [128, 8, 512], ["xnT0"], BF16)
                dbg(S, nc, "u", u_t[b][:], [BW, TCH + 3], [f"u{b}"])
                dbg(S, nc, "zs", zs_t[b][:], [BW, TCH], [f"zs{b}"])
            S.op("dve", lambda h, b=b, blk=blk: h.tensor_scalar(out=uc_t[b][:], in0=u_t[b][:, 3:3 + TCH], scalar1=vec[:, blk, 3:4], scalar2=vec[:, blk, 4:5],
                                                               op0=ALU.mult, op1=ALU.add), r=[f"u{b}", "vec"], w=[f"uc{b}"])
            for j in range(3):
                S.op("dve", lambda h, b=b, blk=blk, j=j: h.scalar_tensor_tensor(out=uc_t[b][:], in0=u_t[b][:, j:j + TCH], scalar=vec[:, blk, j:j + 1],
                                                                               in1=uc_t[b][:], op0=ALU.mult, op1=ALU.add),
                     r=[f"u{b}", "vec"], w=[f"uc{b}"])
            for hf in range(2):
                S.op("pe", lambda h, hf=hf, b=b, blk=blk: h.matmul(ps[4][0:BW, :] if hf == 0 else ps[5][0:BW, :], lhsT=ga[:, blk, :],
                                                                  rhs=uc_t[b][:, hf * 512:(hf + 1) * 512], start=True, stop=True),
                     r=["ga", f"uc{b}"], w=[f"ps{4 + hf}"])
                S.op("act", lambda h, hf=hf, b=b, blk=blk: h.activation(out=r_t[b][:, hf * 512:(hf + 1) * 512], in_=ps[4 + hf][0:BW, :], func=AF.Sigmoid,
                                                                       bias=vec[:, blk, 5:6]), r=[f"ps{4 + hf}", "vec"], w=[f"r{b}"])
            for hf in range(2):
                S.op("pe", lambda h, hf=hf, b=b, blk=blk: h.matmul(ps[4 + hf][0:BW, :], lhsT=gx[:, blk, :],
                                                                  rhs=uc_t[b][:, hf * 512:(hf + 1) * 512], start=True, stop=True),
                     r=["gx", f"uc{b}"], w=[f"ps{4 + hf}"])
                S.op("act", lambda h, hf=hf, b=b, blk=blk: h.activation(out=i_t[b][:, hf * 512:(hf + 1) * 512], in_=ps[4 + hf][0:BW, :], func=AF.Sigmoid,
                                                                       bias=vec[:, blk, 6:7]), r=[f"ps{4 + hf}", "vec"], w=[f"i{b}"])
            if debug and tch == 0 and blk == 0:
                dbg(S, nc, "uc", uc_t[b][:], [BW, TCH], [f"uc{b}"])
                dbg(S, nc, "r", r_t[b][:], [BW, TCH], [f"r{b}"])
                dbg(S, nc, "i", i_t[b][:], [BW, TCH], [f"i{b}"])
                dbg(S, nc, "der", der[:], [BW, NBLK, 4], ["der"])
            S.op("act", lambda h, b=b, blk=blk: h.activation(out=a_t[b][:], in_=r_t[b][:], func=AF.Exp, scale=der[:, blk, 2:3]),
                 r=[f"r{b}", "der"], w=[f"r{b}"])
            S.op("pool", lambda h, b=b: h.tensor_tensor(out=m_t[b][:], in0=a_t[b][:], in1=a_t[b][:], op=ALU.mult), r=[f"r{b}"], w=["m0"])
            S.op("act", lambda h, b=b: h.activation(out=m_t[b][:], in_=m_t[b][:], func=AF.Sqrt, scale=-1.0, bias=1.0), r=["m0"], w=["m0"])
            S.op("pool", lambda h, b=b: h.tensor_tensor(out=i_t[b][:], in0=i_t[b][:], in1=uc_t[b][:], op=ALU.mult), r=[f"i{b}", f"uc{b}"], w=[f"i{b}"])
            S.op("pool", lambda h, b=b: h.tensor_tensor(out=i_t[b][:], in0=i_t[b][:], in1=m_t[b][:], op=ALU.mult), r=[f"i{b}", "m0"], w=[f"i{b}"])
            if debug and tch == 0 and blk == 0:
                dbg(S, nc, "a", r_t[b][:], [BW, TCH], [f"r{b}"])
                dbg(S, nc, "m", m_t[b][:], [BW, TCH], ["m0"])
                dbg(S, nc, "bt", i_t[b][:], [BW, TCH], [f"i{b}"])
            S.op("dve", lambda h, b=b, blk=blk: h.tensor_tensor_scan(out=h_t[b][:], data0=a_t[b][:], data1=i_t[b][:], initial=hlast[:, blk:blk + 1],
                                                                    op0=ALU.mult, op1=ALU.add), r=[f"r{b}", f"i{b}", "hlast"], w=[f"uc{b}"])
            S.op("dve", lambda h, b=b, blk=blk: h.tensor_copy(out=hlast[:, blk:blk + 1], in_=h_t[b][:, TCH - 1:TCH]), r=[f"uc{b}"], w=["hlast"])
            S.op("dve", lambda h, b=b, blk=blk: h.tensor_tensor(out=hz[:, blk, :], in0=h_t[b][:], in1=zs_t[b][:], op=ALU.mult),
                 r=[f"uc{b}", f"zs{b}"], w=["hz"])
        if debug and tch == 0:
            dbg(S, nc, "hz", hz[:], [BW, NBLK, TCH], ["hz"], BF16)
        for tl in range(TCH // 128):
            pbuf = tl % 2
            for hf in range(2):
                for blk in range(NBLK):
                    S.op("pe", lambda h, tl=tl, hf=hf, blk=blk: h.matmul(ps[hf][:, :], lhsT=hz[:, blk, tl * 128:(tl + 1) * 128],
                                                                        rhs=wo[:, blk, hf * 512:(hf + 1) * 512], start=(blk == 0), stop=(blk == NBLK - 1)),
                         r=["hz", "wo"], w=[f"ps{hf}"])
                S.op("act" if hf == 0 else "dve",
                     (lambda h, hf=hf, pbuf=pbuf: h.copy(out=pt[pbuf][:, hf * 512:(hf + 1) * 512], in_=ps[hf][:, :])) if hf == 0 else
                     (lambda h, hf=hf, pbuf=pbuf: h.tensor_copy(out=pt[pbuf][:, hf * 512:(hf + 1) * 512], in_=ps[hf][:, :])),
                     r=[f"ps{hf}"], w=["pt0"])
            rows = slice(t0 + tl * 128, t0 + (tl + 1) * 128)
            S.op("sp", lambda h, pbuf=pbuf, rows=rows: h.dma_start(out=p_out[rows, :], in_=pt[pbuf][:]), r=["pt0"], dma=True)
    return nc, es, S


def build_F(ntok=2048):
    nc = get_nc()
    es = ExitStack()
    S = get_sched(nc, es)
    srcs = [dram_in(nc, f"xin{k}", [ntok, D]) for k in range(3)]
    g_row = dram_in(nc, "g", [1, D])
    out_d = dram_out(nc, "out", [ntok, D])
    g_bc = mk(nc, es, "g_bc", [128, D], F32)
    S.op("sp", lambda h: h.dma_start(out=g_bc[:], in_=g_row.partition_broadcast(128)), w=["g"], dma=True)
    NB = 2
    xt = [mk(nc, es, f"xt{b}", [128, D], F32) for b in range(NB)]
    sq = mk(nc, es, "sq", [128, D], F32)
    ot = [mk(nc, es, f"ot{b}", [128, D], F32) for b in range(NB)]
    st = [mk(nc, es, f"st{b}", [128, 4], F32) for b in range(NB)]
    for t in range(ntok // 128):
        b = t % NB
        rows = slice(t * 128, (t + 1) * 128)
        for k, src in enumerate(srcs):
            if k == 0:
                S.op("pool", lambda h, src=src, b=b, t=t: h.dma_start(out=xt[b][:], in_=src_rows(src, t)), w=[f"xt{b}"], dma=True)
            else:
                S.op("pool", lambda h, src=src, b=b, t=t: h.dma_start(out=xt[b][:], in_=src_rows(src, t), accum_op=ALU.add), r=[f"xt{b}"], w=[f"xt{b}"], dma=True)
        S.op("act", lambda h, b=b: h.activation(out=sq[:], in_=xt[b][:], func=AF.Square), r=[f"xt{b}"], w=["sq"])
        S.op("dve", lambda h, b=b: h.tensor_reduce(out=st[b][:, 0:1], in_=sq[:], axis=AX.X, op=ALU.add), r=["sq"], w=[f"st{b}"])
        S.op("act", lambda h, b=b: h.activation(out=st[b][:, 1:2], in_=st[b][:, 0:1], func=AF.Sqrt, scale=1.0 / D, bias=EPS), r=[f"st{b}"], w=[f"st{b}"])
        S.op("dve", lambda h, b=b: h.reciprocal(out=st[b][:, 2:3], in_=st[b][:, 1:2]), r=[f"st{b}"], w=[f"st{b}r"])
        S.op("dve", lambda h, b=b: h.scalar_tensor_tensor(out=ot[b][:], in0=xt[b][:], scalar=st[b][:, 2:3], in1=g_bc[:], op0=ALU.mult, op1=ALU.mult),
             r=[f"xt{b}", f"st{b}r", "g"], w=[f"ot{b}"])
        S.op("sp", lambda h, b=b, rows=rows: h.dma_start(out=out_d[rows, :], in_=ot[b][:]), r=[f"ot{b}"], dma=True)
    return nc, es, S


def prep_D(z, half):
    LW = 1280
    blks = list(range(half * 8, half * 8 + 8))
    cols = np.concatenate([np.arange(b * 80, (b + 1) * 80) for b in blks])
    w_in = z['d_w_in'][0]
    d = {}
    d['wu'] = np.ascontiguousarray(w_in[:, cols])
    d['wz'] = np.ascontiguousarray(w_in[:, LW + cols])
    d['wo'] = np.ascontiguousarray(z['d_w_out'][0][cols, :])
    d['ga'] = np.ascontiguousarray(z['d_gate_a_w'][0][blks])
    d['gx'] = np.ascontiguousarray(z['d_gate_x_w'][0][blks])
    vecs = np.zeros((80, 8, 8), np.float32)

    def fm(v):
        return v[cols].reshape(8, 80).T
    for j in range(4):
        vecs[:, :, j] = fm(z['d_conv_w'][0][j])
    vecs[:, :, 4] = fm(z['d_conv_b'][0])
    vecs[:, :, 5] = fm(z['d_gate_a_b'][0])
    vecs[:, :, 6] = fm(z['d_gate_x_b'][0])
    vecs[:, :, 7] = fm(z['d_lambda'][0])
    d['vecs'] = vecs
    d['g'] = z['norm_g'][3:4].copy()
    d['ident'] = np.eye(128, dtype=np.float32)
    return d


PAIRS = [[0, 1], [2, 3], [4, 5], [6, 7]]


def build_fused(T=4096, nlayers=4):
    CFG.T = T
    nc = bass.Bass("TRN2", target_bir_lowering=False)
    top = ExitStack()
    S = Sched(nc, top)
    CFG.nc, CFG.S = nc, S
    x_d = nc.dram_tensor("x", [T, D], F32, kind="ExternalInput").ap()
    out_d = nc.dram_tensor("out", [T, D], F32, kind="ExternalOutput").ap()
    p = [nc.dram_tensor(f"p_i{l}", [T, D], F32) for l in range(4)]
    CH = 512
    NCH = T // CH
    pg = [[nc.dram_tensor(f"pg_i{l}_{k}", [2 * CH, D], F32) for k in range(NCH)] for l in range(4)]

    def gsrc(l, rank):
        return lambda t: pg[l][t // 4].ap()[rank * CH + (t % 4) * 128:rank * CH + (t % 4 + 1) * 128, :]
    xs = [nc.dram_tensor(f"xs_i{l}", [T, D], F32) for l in range(3)]
    layers = [("A", build_A, {}), ("B", build_B, dict(MC=128)), ("C", build_C, {}), ("D", build_D, {})]
    prev_x = x_d
    for l, (nm, fn, kw) in enumerate(layers[:nlayers]):
        CFG.prefix = nm + "_"
        SB_USED[0] = 0
        ov = {"p": p[l].ap()}
        if l == 0:
            ov["xin0"] = x_d
            nsrc = 1
        else:
            ov["xin0"] = prev_x
            ov["xin1"] = gsrc(l - 1, 0)
            ov["xin2"] = gsrc(l - 1, 1)
            ov["xs"] = xs[l - 1].ap()
            nsrc = 3
        CFG.override = ov
        _, es, _ = fn(nsrc, **kw)
        for k in range(NCH):
            S.op("pool", lambda h, l=l, k=k: h.collective_compute("AllGather", ALU.bypass, replica_groups=PAIRS, ins=[p[l].ap()[k * CH:(k + 1) * CH, :].opt()],
                                                               outs=[pg[l][k].ap().opt()]), dma=True, cc=True)
        S.emit(final=False)
        S.barrier()
        es.close()
        if l > 0:
            prev_x = xs[l - 1].ap()
    CFG.prefix = "F_"
    SB_USED[0] = 0
    CFG.override = {"xin0": prev_x, "xin1": gsrc(nlayers - 1, 0), "xin2": gsrc(nlayers - 1, 1), "out": out_d}
    _, es, _ = build_F(T)
    stats = S.emit(final=True)
    CFG.nc, CFG.S, CFG.override, CFG.prefix = None, None, {}, ""
    return nc, stats


def kernel(**inputs):
    z = {k: np.ascontiguousarray(np.asarray(v, dtype=np.float32)) for k, v in inputs.items()}
    T = 4096
    x = z['x']
    B = x.shape[0]
    nc, _ = build_fused(T)
    per_half = []
    for h in range(2):
        d = {}
        for pre, pd in (("A_", prep_A(z, h)), ("B_", prep_B(z, h, MC=128)), ("C_", prep_C(z, h, T)), ("D_", prep_D(z, h))):
            for k, v in pd.items():
                d[pre + k] = v
        d["F_g"] = z['final_g'][None, :].copy()
        per_half.append(d)
    in_maps = [dict(per_half[c % 2], x=x[c // 2]) for c in range(8)]
    res = run_bass_kernel_spmd(nc, in_maps, core_ids=list(range(8)))
    out = np.stack([res.results[2 * b]['out'] for b in range(B)]).astype(np.float32)
    return out
```

```python
import numpy as np
from contextlib import ExitStack
import concourse.bass as bass
import concourse.mybir as mybir
from concourse.bass_utils import run_bass_kernel_spmd

F32 = mybir.dt.float32
BF16 = mybir.dt.bfloat16
AF = mybir.ActivationFunctionType
ALU = mybir.AluOpType
AX = mybir.AxisListType


class Buf:
    __slots__ = ("name", "lw", "rd")

    def __init__(self, name):
        self.name = name
        self.lw = None
        self.rd = {}


class Sched:
    COMPUTE = ("pe", "act", "dve", "pool")

    def __init__(self, nc, es, ndma_slots=8):
        self.nc = nc
        self.es = es
        self.ops = []
        self.bufs = {}
        self.ndma = ndma_slots
        self.handles = {"pe": nc.tensor, "act": nc.scalar, "dve": nc.vector, "pool": nc.gpsimd, "sp": nc.sync}
        self.need = []
        self.seg_dma = []
        self.last_compute = {}
        self.barrier_deps = set()
        self.pending_barrier = {}

    def buf(self, name):
        b = self.bufs.get(name)
        if b is None:
            b = Buf(name)
            self.bufs[name] = b
        return b

    def _B(self, lst):
        out = []
        for x in lst:
            if isinstance(x, str):
                out.append(self.buf(x))
            elif isinstance(x, Buf):
                out.append(x)
            elif x is None:
                continue
            else:
                out.extend(self._B(x))
        return out

    def op(self, eng, fn, r=(), w=(), dma=False, cc=False):
        i = len(self.ops)
        R = self._B(r)
        W = self._B(w)
        deps = set()
        if self.pending_barrier.get(eng):
            deps |= self.barrier_deps
            self.pending_barrier[eng] = False
        if cc:
            deps |= set(self.seg_dma)
        raw = set()
        for b in R:
            if b.lw is not None:
                deps.add(b.lw)
                raw.add(b.lw)
        for b in W:
            if b.lw is not None:
                deps.add(b.lw)
            for k, v in b.rd.items():
                deps.add(v)
        for b in W:
            b.lw = i
            b.rd = {}
        key = ("dma", i) if dma else eng
        for b in R:
            b.rd[key] = i
        self.ops.append(dict(eng=eng, fn=fn, deps=deps, dma=dma, cc=cc, raw=raw))
        if dma:
            self.seg_dma.append(i)
        elif eng in self.COMPUTE:
            self.last_compute[eng] = i
        return i

    def _init_state(self):
        nc = self.nc
        self.sems = {e: self.es.enter_context(nc.semaphore("sem_" + e)) for e in self.COMPUTE}
        self.dsems = {q: [self.es.enter_context(nc.semaphore(f"dsem_{q}_{k}")) for k in range(self.ndma)] for q in ("sp", "pool")}
        self.ccsem = self.es.enter_context(nc.semaphore("sem_cc"))
        self.cccount = 0
        self.duses = {q: [0] * self.ndma for q in ("sp", "pool")}
        self.dcount = {"sp": 0, "pool": 0}
        self.cnt = {e: 0 for e in self.COMPUTE}
        self.token = []
        self.waited = {e: {} for e in self.handles}
        self.nwaits = 0
        self.emitted = 0
        self.inited = True

    def _skip(self, po, o, d):
        if po["dma"] or o["dma"] or po["eng"] != o["eng"]:
            return False
        e = o["eng"]
        if e == "pe":
            return True
        if e in ("act", "dve") and d not in o["raw"]:
            return True
        return False

    def barrier(self):
        deps = set(self.seg_dma)
        for e in self.COMPUTE:
            if e in self.last_compute:
                deps.add(self.last_compute[e])
        self.barrier_deps = deps
        self.pending_barrier = {e: True for e in self.handles}
        self.seg_dma = []
        self.bufs = {}

    def emit(self, final=True):
        nc = self.nc
        ops = self.ops
        if not getattr(self, "inited", False):
            self._init_state()
        start = self.emitted
        n = len(ops)
        need = self.need
        need.extend([False] * (n - len(need)))
        for i in range(start, n):
            o = ops[i]
            for d in o["deps"]:
                po = ops[d]
                if po["dma"]:
                    continue
                if self._skip(po, o, d):
                    continue
                assert d >= start or need[d], "cross-segment dependency on an op without increment"
                need[d] = True
        lastc = {}
        for i in range(start, n):
            if not ops[i]["dma"] and ops[i]["eng"] in self.COMPUTE:
                lastc[ops[i]["eng"]] = i
        for e, i in lastc.items():
            need[i] = True
        sems, dsems, duses, dcount, cnt, token, waited = self.sems, self.dsems, self.duses, self.dcount, self.cnt, self.token, self.waited
        token.extend([None] * (n - len(token)))
        for i in range(start, n):
            o = ops[i]
            e = o["eng"]
            h = self.handles[e]
            wd = waited[e]
            reqs = {}
            for d in o["deps"]:
                po = ops[d]
                if self._skip(po, o, d):
                    continue
                sem, val, sk = token[d]
                if wd.get(sk, 0) >= val:
                    continue
                if sk not in reqs or reqs[sk][1] < val:
                    reqs[sk] = (sem, val)
            is_cc = o.get("cc", False)
            if o["dma"] and not is_cc:
                q = e
                s = dcount[q] % self.ndma
                dcount[q] += 1
                dsk = ("d", q, s)
                prev = 16 * duses[q][s]
                if prev > 0 and wd.get(dsk, 0) < prev:
                    if dsk not in reqs or reqs[dsk][1] < prev:
                        reqs[dsk] = (dsems[q][s], prev)
            for rk, (rsem, rval) in reqs.items():
                h.wait_ge(rsem, rval)
                wd[rk] = rval
                self.nwaits += 1
            ins = o["fn"](h)
            if is_cc:
                self.cccount += 1
                ins.then_inc(self.ccsem, 1)
                token[i] = (self.ccsem, self.cccount, ("cc",))
            elif o["dma"]:
                duses[q][s] += 1
                ins.then_inc(dsems[q][s], 16)
                token[i] = (dsems[q][s], 16 * duses[q][s], dsk)
            else:
                if need[i]:
                    cnt[e] += 1
                    ins.then_inc(sems[e], 1)
                    token[i] = (sems[e], cnt[e], ("c", e))
                else:
                    token[i] = (sems[e], cnt[e] + 0, ("c", e))
            o["fn"] = None
        self.emitted = n
        if final:
            h = self.handles["sp"]
            for q in ("sp", "pool"):
                for s in range(self.ndma):
                    if duses[q][s] > 0:
                        h.wait_ge(dsems[q][s], 16 * duses[q][s])
            if self.cccount:
                h.wait_ge(self.ccsem, self.cccount)
        self.stats = dict(nops=len(ops), nwaits=self.nwaits, incs=dict(cnt))
        return self.stats


class Stream:
    def __init__(self):
        self.items = []

    def op(self, *a, **k):
        self.items.append((a, k))


def merge_streams(S, streams, chunk=1):
    idx = [0] * len(streams)
    live = True
    while live:
        live = False
        for i, st in enumerate(streams):
            for _ in range(chunk):
                if idx[i] < len(st.items):
                    a, k = st.items[idx[i]]
                    S.op(*a, **k)
                    idx[i] += 1
                    live = True


class CFG:
    T = 4096
    prefix = ""
    nc = None
    S = None
    override = {}


def get_nc():
    if CFG.nc is not None:
        return CFG.nc
    return bass.Bass("TRN2", target_bir_lowering=False)


def get_sched(nc, es):
    if CFG.S is not None:
        return CFG.S
    return Sched(nc, es)
D = 1024
EPS = 1e-6


class Ctx:
    pass


SB_USED = [0]


def mk(nc, es, name, shape, dt, psum=False):
    if not psum:
        n = 1
        for d_ in shape[1:]:
            n *= d_
        n *= (2 if dt == BF16 else 4)
        SB_USED[0] += (n + 31) // 32 * 32
        assert SB_USED[0] <= 190 * 1024, f"SBUF over budget at {name}: {SB_USED[0]}"
    if psum:
        return es.enter_context(nc.psum_tensor(CFG.prefix + name, shape, dt))
    return es.enter_context(nc.sbuf_tensor(CFG.prefix + name, shape, dt))


def src_rows(src, t):
    if callable(src):
        return src(t)
    return src[t * 128:(t + 1) * 128, :]


def dram_in(nc, name, shape, dt=F32):
    if name in CFG.override:
        return CFG.override[name]
    return nc.dram_tensor(CFG.prefix + name, list(shape), dt, kind="ExternalInput").ap()


def dram_out(nc, name, shape, dt=F32):
    if name in CFG.override:
        return CFG.override[name]
    return nc.dram_tensor(CFG.prefix + name, list(shape), dt, kind="ExternalOutput").ap()


class P1:
    def __init__(self, S, nc, es, srcs, xs_out, g_row, ident_d, ps_tr, name="p1"):
        self.S, self.nc, self.srcs, self.xs_out, self.ps_tr, self.name = S, nc, srcs, xs_out, ps_tr, name
        self.g_bc = mk(nc, es, name + "_g", [128, D], F32)
        self.ident = mk(nc, es, name + "_id", [128, 128], BF16)
        self.identf = mk(nc, es, name + "_idf", [128, 128], F32)
        g_bc, ident, identf = self.g_bc, self.ident, self.identf
        S.op("sp", lambda h: h.dma_start(out=g_bc[:], in_=g_row.partition_broadcast(128)), w=[name + "g"], dma=True)
        S.op("sp", lambda h: h.dma_start(out=identf[:], in_=ident_d), w=[name + "idf"], dma=True)
        S.op("dve", lambda h: h.tensor_copy(out=ident[:], in_=identf[:]), r=[name + "idf"], w=[name + "id"])
        self.NB = 2
        self.xt = [mk(nc, es, f"{name}_x{b}", [128, D], F32) for b in range(self.NB)]
        self.sq = mk(nc, es, name + "_sq", [128, D], BF16)
        self.xnb = [mk(nc, es, f"{name}_xn{b}", [128, D], BF16) for b in range(self.NB)]
        self.st = [mk(nc, es, f"{name}_st{b}", [128, 4], F32) for b in range(self.NB)]

    def tile(self, t, dst_ap, dst_buf):
        S, name = self.S, self.name
        xt, sq, xnb, st, g_bc, ident, ps_tr = self.xt, self.sq, self.xnb, self.st, self.g_bc, self.ident, self.ps_tr
        b = t % self.NB
        rows = slice(t * 128, (t + 1) * 128)
        xb = f"{name}x{b}"
        for k, src in enumerate(self.srcs):
            if k == 0:
                S.op("pool", lambda h, src=src: h.dma_start(out=xt[b][:], in_=src_rows(src, t)), w=[xb], dma=True)
            else:
                S.op("pool", lambda h, src=src: h.dma_start(out=xt[b][:], in_=src_rows(src, t), accum_op=ALU.add), r=[xb], w=[xb], dma=True)
        if self.xs_out is not None and len(self.srcs) > 1:
            S.op("sp", lambda h: h.dma_start(out=self.xs_out[rows, :], in_=xt[b][:]), r=[xb], dma=True)
        S.op("act", lambda h: h.activation(out=sq[:], in_=xt[b][:], func=AF.Square), r=[xb], w=[name + "sq"])
        S.op("dve", lambda h: h.tensor_reduce(out=st[b][:, 0:1], in_=sq[:], axis=AX.X, op=ALU.add), r=[name + "sq"], w=[f"{name}st{b}"])
        S.op("act", lambda h: h.activation(out=st[b][:, 1:2], in_=st[b][:, 0:1], func=AF.Sqrt, scale=1.0 / D, bias=EPS), r=[f"{name}st{b}"], w=[f"{name}st{b}"])
        S.op("dve", lambda h: h.reciprocal(out=st[b][:, 2:3], in_=st[b][:, 1:2]), r=[f"{name}st{b}"], w=[f"{name}st{b}r"])
        S.op("dve", lambda h: h.scalar_tensor_tensor(out=xnb[b][:], in0=xt[b][:], scalar=st[b][:, 2:3], in1=g_bc[:], op0=ALU.mult, op1=ALU.mult),
             r=[xb, f"{name}st{b}r", name + "g"], w=[f"{name}xn{b}"])
        for dc in range(8):
            S.op("pe", lambda h, dc=dc: h.transpose(out=ps_tr[:, dc * 128:(dc + 1) * 128], in_=xnb[b][:, dc * 128:(dc + 1) * 128], identity=ident[:]),
                 r=[f"{name}xn{b}", name + "id"], w=[name + "pstr"])
        S.op("act", lambda h: h.copy(out=dst_ap, in_=ps_tr[:].rearrange("p (c n) -> p c n", c=8)), r=[name + "pstr"], w=[dst_buf])


def phase1(S, nc, es, srcs, xs_out, g_row, ident_d, ps_tr, ntiles=None, xnT=None, name="p1"):
    if ntiles is None:
        ntiles = CFG.T // 128
    p1 = P1(S, nc, es, srcs, xs_out, g_row, ident_d, ps_tr, name)
    for t in range(ntiles):
        p1.tile(t, xnT[:, :, t * 128:(t + 1) * 128], f"xnT{t // 4}")
    return p1.ident


def dbg(S, nc, name, ap, shape, rbuf, dt=F32):
    o = nc.dram_tensor("dbg_" + name, list(shape), dt, kind="ExternalOutput").ap()
    S.op("sp", lambda h: h.dma_start(out=o, in_=ap), r=rbuf, dma=True)


NEGM = -30000.0


def build_A(nsrc=1, debug=False):
    T = CFG.T
    NT = T // 128
    nc = get_nc()
    es = ExitStack()
    S = get_sched(nc, es)
    srcs = [dram_in(nc, f"xin{k}", [T, D]) for k in range(nsrc)]
    xs_out = dram_out(nc, "xs", [T, D]) if nsrc > 1 else None
    g_row = dram_in(nc, "g", [1, D])
    ident_d = dram_in(nc, "ident", [128, 128])
    wq_d = dram_in(nc, "wq", [D, 512])
    wk_d = dram_in(nc, "wk", [D, 128])
    wv_d = dram_in(nc, "wv", [D, 128])
    wz_d = dram_in(nc, "wz", [D, 512])
    wo_d = dram_in(nc, "wo", [512, D])
    bias_d = dram_in(nc, "biasT", [128, 2, 2 * 4 * 128])
    sink_d = dram_in(nc, "sinks", [1, 8])
    p_out = dram_out(nc, "p", [T, D])

    xnT = mk(nc, es, "xnT", [128, 8, T], BF16)
    ps_tr = mk(nc, es, "ps_tr", [128, 1024], BF16, psum=True)
    ps_q = mk(nc, es, "ps_q", [128, 512], F32, psum=True)
    ps_z = mk(nc, es, "ps_z", [128, 512], F32, psum=True)
    ps_s = [mk(nc, es, f"ps_s{k}", [128, 512], F32, psum=True) for k in range(2)]
    ps_o = mk(nc, es, "ps_o", [128, 4, 128], F32, psum=True)
    ps_y = [mk(nc, es, f"ps_y{k}", [128, 512], F32, psum=True) for k in range(2)]
    ident = phase1(S, nc, es, srcs, xs_out, g_row, ident_d, ps_tr, xnT=xnT)

    wq = mk(nc, es, "wq_s", [128, 8, 512], BF16)
    wk = mk(nc, es, "wk_s", [128, 8, 128], BF16)
    wv = mk(nc, es, "wv_s", [128, 8, 128], BF16)
    wz = mk(nc, es, "wz_s", [128, 8, 512], BF16)
    wo = mk(nc, es, "wo_s", [128, 4, D], BF16)
    biasT = mk(nc, es, "biasT_s", [128, 2, 1024], BF16)
    esink = mk(nc, es, "esink", [128, 8], F32)
    for nm, t_, d_, pat in (("wq", wq, wq_d, "(c p) n -> p c n"), ("wk", wk, wk_d, "(c p) n -> p c n"), ("wv", wv, wv_d, "(c p) n -> p c n"),
                            ("wz", wz, wz_d, "(c p) n -> p c n"), ("wo", wo, wo_d, "(c p) n -> p c n")):
        S.op("pool", lambda h, t_=t_, d_=d_, pat=pat: h.dma_start(out=t_[:], in_=d_.rearrange(pat, p=128)), w=[nm], dma=True)
    S.op("pool", lambda h: h.dma_start(out=biasT[:], in_=bias_d), w=["biasT"], dma=True)
    S.op("sp", lambda h: h.dma_start(out=esink[:], in_=sink_d.partition_broadcast(128)), w=["esink"], dma=True)
    S.op("act", lambda h: h.activation(out=esink[:], in_=esink[:], func=AF.Exp), r=["esink"], w=["esink"])

    kT = mk(nc, es, "kT", [64, 2, T], BF16)
    vau = mk(nc, es, "vau", [128, NT, 2, 65], BF16)
    S.op("dve", lambda h: h.memset(vau[:, :, :, 64:65], 1.0), w=["vau_ones"])
    for g in range(2):
        for c in range(T // 512):
            tk = slice(c * 512, (c + 1) * 512)
            for dc in range(8):
                S.op("pe", lambda h, g=g, dc=dc, tk=tk: h.matmul(ps_q[0:64, :], lhsT=wk[:, dc, g * 64:(g + 1) * 64], rhs=xnT[:, dc, tk],
                                                                start=(dc == 0), stop=(dc == 7)), r=["wk", f"xnT{c}"], w=["ps_q"])
            S.op("act", lambda h, g=g, tk=tk: h.copy(out=kT[:, g, tk], in_=ps_q[0:64, :]), r=["ps_q"], w=[f"kT{c // 1}"])
    for t in range(NT):
        for dc in range(8):
            S.op("pe", lambda h, t=t, dc=dc: h.matmul(ps_z[:, 0:128], lhsT=xnT[:, dc, t * 128:(t + 1) * 128], rhs=wv[:, dc, :],
                                                      start=(dc == 0), stop=(dc == 7)), r=["wv", f"xnT{t // 4}"], w=["ps_z"])
        S.op("dve", lambda h, t=t: h.tensor_copy(out=vau[:, t, :, 0:64], in_=ps_z[:, 0:128].rearrange("p (g d) -> p g d", g=2)),
             r=["ps_z"], w=[f"vau{t}"])

    NB = 2
    qT = [mk(nc, es, f"qT{b}", [64, 2, 4, 128], BF16) for b in range(NB)]
    zs = [mk(nc, es, f"zs{b}", [128, 512], BF16) for b in range(NB)]
    pT = [mk(nc, es, f"pT{b}", [128, 512], BF16) for b in range(4)]
    yz = [mk(nc, es, f"yz{b}", [128, 512], BF16) for b in range(NB)]
    yzT = [mk(nc, es, f"yzT{b}", [128, 4, 128], BF16) for b in range(NB)]
    den = [mk(nc, es, f"den{b}", [128, 8], F32) for b in range(NB)]
    pt = [mk(nc, es, f"pt{b}", [128, D], F32) for b in range(NB)]
    pti = 0
    for qt in range(NT):
        b = qt % NB
        tq = slice(qt * 128, (qt + 1) * 128)
        xk = f"xnT{qt // 4}"
        for g in range(2):
            for r in range(4):
                for dc in range(8):
                    col = (g * 4 + r) * 64
                    S.op("pe", lambda h, g=g, r=r, dc=dc, col=col, tq=tq: h.matmul(ps_q[0:64, r * 128:(r + 1) * 128], lhsT=wq[:, dc, col:col + 64],
                                                                               rhs=xnT[:, dc, tq], start=(dc == 0), stop=(dc == 7)),
                         r=["wq", xk], w=["ps_q"])
            S.op("act", lambda h, g=g, b=b: h.activation(out=qT[b][:, g, :, :], in_=ps_q[0:64, :].rearrange("p (r n) -> p r n", r=4),
                                                        func=AF.Copy, scale=0.125), r=["ps_q"], w=[f"qT{b}_{g}"])
        for dc in range(8):
            S.op("pe", lambda h, dc=dc, tq=tq: h.matmul(ps_z[:, :], lhsT=xnT[:, dc, tq], rhs=wz[:, dc, :], start=(dc == 0), stop=(dc == 7)),
                 r=["wz", xk], w=["ps_z"])
        S.op("act", lambda h, b=b: h.activation(out=zs[b][:], in_=ps_z[:, :], func=AF.Silu), r=["ps_z"], w=[f"zs{b}"])
        for g in range(2):
            kts = [kt for kt in (qt - 1, qt) if kt >= 0]
            for kt in kts:
                cls = 0 if kt == qt else 1
                si = kt % 2
                pi = (g * 2 + si)
                S.op("pe", lambda h, g=g, kt=kt, si=si, b=b: h.matmul(ps_s[si][:, :], lhsT=kT[:, g, kt * 128:(kt + 1) * 128],
                                                                     rhs=qT[b][:, g, :, :].rearrange("p r n -> p (r n)"), start=True, stop=False),
                     r=[f"kT{kt // 4}", f"qT{b}_{g}"], w=[f"ps_s{si}"])
                S.op("pe", lambda h, g=g, cls=cls, si=si: h.matmul(ps_s[si][:, :], lhsT=ident[:], rhs=biasT[:, cls, g * 512:(g + 1) * 512],
                                                                  start=False, stop=True), r=["biasT", "p1id"], w=[f"ps_s{si}"])
                S.op("act", lambda h, si=si, pi=pi: h.activation(out=pT[pi][:], in_=ps_s[si][:, :], func=AF.Exp), r=[f"ps_s{si}"], w=[f"pT{pi}"])
            for r in range(4):
                for j, kt in enumerate(kts):
                    pi = (g * 2 + kt % 2)
                    S.op("pe", lambda h, g=g, r=r, kt=kt, pi=pi, j=j: h.matmul(ps_o[:, r, 0:65], lhsT=pT[pi][:, r * 128:(r + 1) * 128],
                                                                             rhs=vau[:, kt, g, :], start=(j == 0), stop=(j == len(kts) - 1)),
                         r=[f"pT{pi}", f"vau{kt}", "vau_ones"], w=["ps_o"])
            S.op("dve", lambda h, g=g, b=b: h.tensor_tensor(out=den[b][:, g * 4:(g + 1) * 4], in0=ps_o[:, :, 64], in1=esink[:, g * 4:(g + 1) * 4], op=ALU.add),
                 r=["ps_o", "esink"], w=[f"den{b}"])
            S.op("dve", lambda h, g=g, b=b: h.reciprocal(out=den[b][:, g * 4:(g + 1) * 4], in_=den[b][:, g * 4:(g + 1) * 4]), r=[f"den{b}"], w=[f"den{b}"])
            for r in range(4):
                col = (g * 4 + r) * 64
                S.op("dve", lambda h, g=g, r=r, b=b, col=col: h.scalar_tensor_tensor(out=yz[b][:, col:col + 64], in0=ps_o[:, r, 0:64],
                                                                                   scalar=den[b][:, g * 4 + r:g * 4 + r + 1], in1=zs[b][:, col:col + 64],
                                                                                   op0=ALU.mult, op1=ALU.mult),
                     r=["ps_o", f"den{b}", f"zs{b}"], w=[f"yz{b}"])
        for c in range(4):
            S.op("pe", lambda h, c=c, b=b: h.transpose(out=ps_tr[:, c * 128:(c + 1) * 128], in_=yz[b][:, c * 128:(c + 1) * 128], identity=ident[:]),
                 r=[f"yz{b}", "p1id"], w=["p1pstr"])
        S.op("act", lambda h, b=b: h.copy(out=yzT[b][:], in_=ps_tr[:, 0:512].rearrange("p (c n) -> p c n", c=4)), r=["p1pstr"], w=[f"yzT{b}"])
        for hf in range(2):
            for c in range(4):
                S.op("pe", lambda h, hf=hf, c=c, b=b: h.matmul(ps_y[hf][:, :], lhsT=yzT[b][:, c, :], rhs=wo[:, c, hf * 512:(hf + 1) * 512],
                                                              start=(c == 0), stop=(c == 3)), r=[f"yzT{b}", "wo"], w=[f"ps_y{hf}"])
            if hf == 0:
                S.op("act", lambda h, b=b: h.copy(out=pt[b][:, 0:512], in_=ps_y[0][:, :]), r=["ps_y0"], w=[f"pt{b}"])
            else:
                S.op("dve", lambda h, b=b: h.tensor_copy(out=pt[b][:, 512:1024], in_=ps_y[1][:, :]), r=["ps_y1"], w=[f"pt{b}"])
        S.op("sp", lambda h, b=b, tq=tq: h.dma_start(out=p_out[tq, :], in_=pt[b][:]), r=[f"pt{b}"], dma=True)
    return nc, es, S


def t5_bucket_np(d):
    import math
    d = np.maximum(d, 0)
    df = np.maximum(d, 1).astype(np.float32)
    large = 16 + (np.log(df / 16) / math.log(128 / 16) * 16).astype(np.int32)
    large = np.minimum(large, 31)
    return np.where(d < 16, d, large)


def prep_A(z, half):
    d = {}
    w_in = z['a_w_in'][0]
    d['wq'] = np.ascontiguousarray(w_in[:, half * 512:(half + 1) * 512])
    d['wk'] = np.ascontiguousarray(w_in[:, 1024 + half * 128:1024 + (half + 1) * 128])
    d['wv'] = np.ascontiguousarray(w_in[:, 1280 + half * 128:1280 + (half + 1) * 128])
    d['wz'] = np.ascontiguousarray(w_in[:, 1536 + half * 512:1536 + (half + 1) * 512])
    d['wo'] = np.ascontiguousarray(z['a_w_out'][0][half * 512:(half + 1) * 512, :])
    d['sinks'] = np.ascontiguousarray(z['a_sinks'][0][half * 8:(half + 1) * 8][None, :])
    table = z['t5_table']
    tk = np.arange(128)[:, None]
    tq = np.arange(128)[None, :]
    bias = np.zeros((128, 2, 2, 4, 128), np.float32)
    for cls in range(2):
        dist = tq - tk + 128 * cls
        valid = (dist >= 0) & (dist < 128)
        bk = t5_bucket_np(dist)
        for g in range(2):
            for r in range(4):
                hh = half * 8 + g * 4 + r
                bias[:, cls, g, r, :] = np.where(valid, table[bk, hh], NEGM)
    d['biasT'] = bias.reshape(128, 2, 1024)
    d['g'] = z['norm_g'][0:1].copy()
    d['ident'] = np.eye(128, dtype=np.float32)
    return d


KAP = 0.6065306597126334
GN_EPS = 64e-5


class _Stop(Exception):
    pass


def build_B(nsrc=1, MC=256, debug=False, stage=99):
    try:
        return _build_B(nsrc, MC, debug, stage)
    except _Stop as e:
        return e.args[0]


def _build_B(nsrc=1, MC=256, debug=False, stage=99):
    T = CFG.T
    NJ = MC // 64
    NMC = T // MC
    nc = get_nc()
    es = ExitStack()
    S = get_sched(nc, es)
    srcs = [dram_in(nc, f"xin{k}", [T, D]) for k in range(nsrc)]
    xs_out = dram_out(nc, "xs", [T, D]) if nsrc > 1 else None
    g_row = dram_in(nc, "g", [1, D])
    ident_d = dram_in(nc, "ident", [128, 128])
    w4_d = dram_in(nc, "w4", [4, D, 512])
    lw_d = dram_in(nc, "lw", [2, D, 64])
    l2_d = dram_in(nc, "l2", [2, 64, 512])
    wo_d = dram_in(nc, "wo", [512, D])
    mu_d = dram_in(nc, "muT", [128, 6, 8])
    vec_d = dram_in(nc, "vecs", [64, 8, 8])
    lnw_d = dram_in(nc, "lnw", [1, 512])
    lnb_d = dram_in(nc, "lnb", [1, 512])
    mg_d = dram_in(nc, "maskG", [128, 128])
    mnt_d = dram_in(nc, "maskNT", [64, 64])
    rm_d = dram_in(nc, "resetm", [64, MC])
    p_out = dram_out(nc, "p", [T, D])

    ps_tr = mk(nc, es, "ps_tr", [128, 1024], BF16, psum=True)
    ps_proj = mk(nc, es, "ps_proj", [128, 512], F32, psum=True)
    ps_tok = mk(nc, es, "ps_tok", [128, 512], F32, psum=True)
    ps_bv = mk(nc, es, "ps_bv", [128, 512], F32, psum=True)
    ps_g = mk(nc, es, "ps_g", [128, 512], F32, psum=True)
    ps_n = mk(nc, es, "ps_n", [128, 512], F32, psum=True)
    ps_rec = mk(nc, es, "ps_rec", [128, 512], F32, psum=True)
    ps_y = mk(nc, es, "ps_y", [128, 512], F32, psum=True)

    g_bc = mk(nc, es, "g_bc", [128, D], F32)
    identf = mk(nc, es, "identf", [128, 128], F32)
    ident = mk(nc, es, "identb", [128, 128], BF16)
    S.op("sp", lambda h: h.dma_start(out=g_bc[:], in_=g_row.partition_broadcast(128)), w=["g"], dma=True)
    S.op("sp", lambda h: h.dma_start(out=identf[:], in_=ident_d), w=["identf"], dma=True)
    S.op("dve", lambda h: h.tensor_copy(out=ident[:], in_=identf[:]), r=["identf"], w=["ident"])
    W4 = mk(nc, es, "W4", [128, 4, 8, 512], BF16)
    W4m = mk(nc, es, "W4m", [128, 4, 8, 512], BF16)
    LW = mk(nc, es, "LW", [128, 2, 8, 64], BF16)
    LWm = mk(nc, es, "LWm", [128, 2, 8, 64], BF16)
    L2 = mk(nc, es, "L2", [64, 2, 512], BF16)
    wo = mk(nc, es, "wo_s", [128, 4, D], BF16)
    muT = mk(nc, es, "muT_s", [128, 6, 8], F32)
    vec = mk(nc, es, "vec_s", [64, 8, 8], F32)
    lnw = mk(nc, es, "lnw_s", [64, 512], F32)
    lnb = mk(nc, es, "lnb_s", [64, 512], F32)
    maskG = mk(nc, es, "maskG_s", [128, 128], F32)
    maskNT = mk(nc, es, "maskNT_s", [64, 64], F32)
    resetm = mk(nc, es, "resetm_s", [64, MC], F32)
    ones64 = mk(nc, es, "ones64", [64, 64], F32)
    S.op("pool", lambda h: h.dma_start(out=W4[:], in_=w4_d.rearrange("s (c p) n -> p s c n", p=128)), w=["W4"], dma=True)
    S.op("pool", lambda h: h.dma_start(out=LW[:], in_=lw_d.rearrange("s (c p) n -> p s c n", p=128)), w=["LW"], dma=True)
    S.op("pool", lambda h: h.dma_start(out=L2[:], in_=l2_d.rearrange("s k n -> k s n")), w=["L2"], dma=True)
    S.op("pool", lambda h: h.dma_start(out=wo[:], in_=wo_d.rearrange("(c p) n -> p c n", p=128)), w=["wo"], dma=True)
    S.op("sp", lambda h: h.dma_start(out=muT[:], in_=mu_d), w=["muT"], dma=True)
    S.op("sp", lambda h: h.dma_start(out=vec[:], in_=vec_d), w=["vec"], dma=True)
    S.op("sp", lambda h: h.dma_start(out=lnw[:], in_=lnw_d.partition_broadcast(64)), w=["lnw"], dma=True)
    S.op("sp", lambda h: h.dma_start(out=lnb[:], in_=lnb_d.partition_broadcast(64)), w=["lnb"], dma=True)
    S.op("sp", lambda h: h.dma_start(out=maskG[:], in_=mg_d), w=["maskG"], dma=True)
    S.op("sp", lambda h: h.dma_start(out=maskNT[:], in_=mnt_d), w=["maskNT"], dma=True)
    S.op("sp", lambda h: h.dma_start(out=resetm[:], in_=rm_d), w=["resetm"], dma=True)
    S.op("dve", lambda h: h.memset(ones64[:], 1.0), w=["ones64"])
    for s in range(4):
        for dc in range(8):
            S.op("pool" if dc % 2 else "dve", lambda h, s=s, dc=dc: h.tensor_scalar(out=W4m[:, s, dc, :], in0=W4[:, s, dc, :], scalar1=muT[:, s, dc:dc + 1], scalar2=None, op0=ALU.mult),
                 r=["W4", "muT"], w=["W4m"])
    for s in range(2):
        for dc in range(8):
            S.op("dve", lambda h, s=s, dc=dc: h.tensor_scalar(out=LWm[:, s, dc, :], in0=LW[:, s, dc, :], scalar1=muT[:, 4 + s, dc:dc + 1], scalar2=None, op0=ALU.mult),
                 r=["LW", "muT"], w=["LWm"])

    XW = 64 + MC
    xnT = mk(nc, es, "xnT", [128, 8, XW], BF16)
    xxT = mk(nc, es, "xxT", [128, 8, XW], BF16)
    S.op("dve", lambda h: h.memset(xnT[:, :, 0:64], 0.0), w=["xnT"])
    NB = 2
    xt = [mk(nc, es, f"xt{b}", [128, D], F32) for b in range(NB)]
    sq = mk(nc, es, "sq", [128, D], BF16)
    xnb = [mk(nc, es, f"xnb{b}", [128, D], BF16) for b in range(NB)]
    st = [mk(nc, es, f"st{b}", [128, 4], F32) for b in range(NB)]
    h1T = mk(nc, es, "h1T", [64, 2, MC], BF16)
    vwin = mk(nc, es, "vwin", [64, NJ, 512], F32)
    uT = mk(nc, es, "uT", [64, NJ, 512], F32)
    zs = mk(nc, es, "zs", [64, NJ, 512], F32)
    y_all = mk(nc, es, "y_all", [64, NJ, 512], F32)
    bv_all = mk(nc, es, "bv_all", [64, NJ, 512], F32)

    def ft(nm):
        return mk(nc, es, nm, [64, MC], F32)
    r_f = ft("r_f"); k_f = ft("k_f"); sig = ft("sig"); alp = ft("alp"); kk = ft("kk"); t1 = ft("t1"); t2 = ft("t2")
    cs = ft("cs"); kmod = ft("kmod"); bal = ft("bal"); e1 = ft("e1"); e2 = ft("e2"); e3 = ft("e3"); e4 = ft("e4")
    cLs = mk(nc, es, "cLs", [64, NJ], F32)
    G2 = 3
    cLd = [mk(nc, es, f"cLd{i}", [64, NJ], F32) for i in range(G2)]
    AR = [mk(nc, es, f"AR{i}", [64, NJ, 128], F32) for i in range(G2)]
    BK = [mk(nc, es, f"BK{i}", [64, NJ, 128], F32) for i in range(G2)]
    BKe = [mk(nc, es, f"BKe{i}", [64, NJ, 128], F32) for i in range(G2)]
    Gm = [mk(nc, es, f"Gm{i}", [64, NJ, 256], F32) for i in range(G2)]
    Tm = [mk(nc, es, f"Tm{i}", [64, NJ, 64], F32) for i in range(G2)]
    BKeT = [mk(nc, es, f"BKeT{i}", [64, NJ, 128], F32) for i in range(G2)]
    dcL = [mk(nc, es, f"dcL{i}", [64, NJ, 64], F32) for i in range(G2)]
    rkrp = [mk(nc, es, f"rkrp{i}", [64, NJ, 64], F32) for i in range(G2)]
    Dg = [mk(nc, es, f"Dg{i}", [64, NJ, 64], F32) for i in range(G2)]
    bon = [mk(nc, es, f"bon{i}", [64, NJ], F32) for i in range(G2)]
    Nk = [[mk(nc, es, f"Nk{j}_{i}", [64, 64], F32) for i in range(2)] for j in range(NJ)]
    NkT = [[mk(nc, es, f"NkT{j}_{i}", [64, 64], F32) for i in range(2)] for j in range(NJ)]
    Pm = [[mk(nc, es, f"Pm{j}_{i}", [64, 64], F32) for i in range(2)] for j in range(NJ)]
    ST = [[mk(nc, es, f"ST{h}_{i}", [64, 64], F32) for i in range(2)] for h in range(8)]
    WT = [mk(nc, es, f"WT{i}", [64, 64], F32) for i in range(2)]
    for h in range(8):
        S.op("dve", lambda hh, h=h: hh.memset(ST[h][0][:], 0.0), w=[f"ST{h}_0"])
    yn = mk(nc, es, "yn", [64, 512], F32)
    gst = mk(nc, es, "gst", [64, 4, 8], F32)
    yz = mk(nc, es, "yz", [64, 512], BF16)
    yzT = mk(nc, es, "yzT", [128, 4, 64], BF16)
    pt = mk(nc, es, "pt", [64, D], F32)

    def c3(t_):
        return t_[:].rearrange("p (c j) -> p c j", j=64)

    for mc in range(NMC):
        T0 = mc * MC
        if mc > 0:
            S.op("pool", lambda h: h.tensor_copy(out=xnT[:, :, 0:64], in_=xnT[:, :, MC:MC + 64]), r=["xnT"], w=["xnT"])
        for tl in range(MC // 128):
            t = (T0 // 128) + tl
            b = t % NB
            rows = slice(t * 128, (t + 1) * 128)
            for k, src in enumerate(srcs):
                if k == 0:
                    S.op("pool", lambda h, src=src, b=b, t=t: h.dma_start(out=xt[b][:], in_=src_rows(src, t)), w=[f"xt{b}"], dma=True)
                else:
                    S.op("pool", lambda h, src=src, b=b, t=t: h.dma_start(out=xt[b][:], in_=src_rows(src, t), accum_op=ALU.add),
                         r=[f"xt{b}"], w=[f"xt{b}"], dma=True)
            if xs_out is not None:
                S.op("sp", lambda h, b=b, rows=rows: h.dma_start(out=xs_out[rows, :], in_=xt[b][:]), r=[f"xt{b}"], dma=True)
            S.op("act", lambda h, b=b: h.activation(out=sq[:], in_=xt[b][:], func=AF.Square), r=[f"xt{b}"], w=["sq"])
            S.op("dve", lambda h, b=b: h.tensor_reduce(out=st[b][:, 0:1], in_=sq[:], axis=AX.X, op=ALU.add), r=["sq"], w=[f"st{b}"])
            S.op("act", lambda h, b=b: h.activation(out=st[b][:, 1:2], in_=st[b][:, 0:1], func=AF.Sqrt, scale=1.0 / D, bias=EPS), r=[f"st{b}"], w=[f"st{b}"])
            S.op("dve", lambda h, b=b: h.reciprocal(out=st[b][:, 2:3], in_=st[b][:, 1:2]), r=[f"st{b}"], w=[f"st{b}r"])
            S.op("dve", lambda h, b=b: h.scalar_tensor_tensor(out=xnb[b][:], in0=xt[b][:], scalar=st[b][:, 2:3], in1=g_bc[:], op0=ALU.mult, op1=ALU.mult),
                 r=[f"xt{b}", f"st{b}r", "g"], w=[f"xnb{b}"])
            for dc in range(8):
                S.op("pe", lambda h, dc=dc, b=b: h.transpose(out=ps_tr[:, dc * 128:(dc + 1) * 128], in_=xnb[b][:, dc * 128:(dc + 1) * 128], identity=ident[:]),
                     r=[f"xnb{b}", "ident"], w=["ps_tr"])
            S.op("act", lambda h, tl=tl: h.copy(out=xnT[:, :, 64 + tl * 128:64 + (tl + 1) * 128], in_=ps_tr[:].rearrange("p (c n) -> p c n", c=8)),
                 r=["ps_tr"], w=["xnT"])
        S.op("pool", lambda h: h.tensor_tensor(out=xxT[:, :, 1:XW], in0=xnT[:, :, 0:XW - 1], in1=xnT[:, :, 1:XW], op=ALU.subtract), r=["xnT"], w=["xxT"])
        tokc = slice(64, 64 + MC)

        def proj_fm(S, ps_ap, Wt, Wm, sidx, cols, M):
            n = 0
            for (Wx, X, xb) in ((Wt, xnT, "xnT"), (Wm, xxT, "xxT")):
                for dc in range(8):
                    S.op("pe", lambda h, Wx=Wx, X=X, dc=dc, n=n: h.matmul(ps_ap, lhsT=Wx[:, sidx, dc, cols], rhs=X[:, dc, tokc], start=(n == 0), stop=(n == 15)),
                         r=["W4", "W4m", "LW", "LWm", xb], w=["ps_proj"])
                    n += 1
        for s in range(2):
            proj_fm(S, ps_proj[0:64, 0:MC], LW, LWm, s, slice(0, 64), 64)
            S.op("act", lambda h, s=s: h.activation(out=h1T[:, s, :], in_=ps_proj[0:64, 0:MC], func=(AF.Tanh if s == 0 else AF.Copy)), r=["ps_proj"], w=["h1T"])
        for j in range(NJ):
            n = 0
            for (Wi, X, xb) in ((W4, xnT, "xnT"), (W4m, xxT, "xxT")):
                for dc in range(8):
                    S.op("pe", lambda h, Wi=Wi, X=X, dc=dc, n=n, j=j: h.matmul(ps_tok[0:64, :], lhsT=X[:, dc, 64 + j * 64:128 + j * 64], rhs=Wi[:, 2, dc, :], start=(n == 0), stop=(n == 15)),
                         r=["W4", "W4m", xb], w=["ps_tok"])
                    n += 1
            S.op("act", lambda h, j=j: h.copy(out=vwin[:, j, :], in_=ps_tok[0:64, :]), r=["ps_tok"], w=[f"vwin{j}"])
            n = 0
            for (Wi, X, xb) in ((W4, xnT, "xnT"), (W4m, xxT, "xxT")):
                for dc in range(8):
                    S.op("pe", lambda h, Wi=Wi, X=X, dc=dc, n=n, j=j: h.matmul(ps_tok[0:64, :], lhsT=X[:, dc, 64 + j * 64:128 + j * 64], rhs=Wi[:, 3, dc, :], start=(n == 0), stop=(n == 15)),
                         r=["W4", "W4m", xb], w=["ps_tok"])
                    n += 1
            S.op("act", lambda h, j=j: h.activation(out=zs[:, j, :], in_=ps_tok[0:64, :], func=AF.Silu), r=["ps_tok"], w=["zs"])

        def head_prep(S, hd):
            gi = hd % G2
            hc = slice(hd * 64, (hd + 1) * 64)
            X = Stream()
            Y = Stream()
            proj_fm(X, ps_proj[0:64, 0:MC], W4, W4m, 0, hc, 64)
            X.op("act", lambda h: h.copy(out=r_f[:], in_=ps_proj[0:64, 0:MC]), r=["ps_proj"], w=["r_f"])
            proj_fm(X, ps_proj[0:64, 0:MC], W4, W4m, 1, hc, 64)
            X.op("act", lambda h: h.copy(out=k_f[:], in_=ps_proj[0:64, 0:MC]), r=["ps_proj"], w=["k_f"])
            X.op("dve", lambda h, hd=hd: h.tensor_scalar(out=kk[:], in0=k_f[:], scalar1=vec[:, hd, 2:3], scalar2=None, op0=ALU.mult), r=["k_f", "vec"], w=["kk"])
            X.op("pool", lambda h: h.tensor_tensor(out=t1[:], in0=kk[:], in1=kk[:], op=ALU.mult), r=["kk"], w=["t1"])
            X.op("pe", lambda h: h.matmul(ps_proj[0:64, 0:MC], lhsT=ones64[:], rhs=t1[:], start=True, stop=True), r=["ones64", "t1"], w=["ps_proj"])
            X.op("act", lambda h: h.activation(out=t2[:], in_=ps_proj[0:64, 0:MC], func=AF.Sqrt), r=["ps_proj"], w=["t2"])
            X.op("dve", lambda h: h.tensor_scalar(out=t2[:], in0=t2[:], scalar1=1e-12, scalar2=None, op0=ALU.max), r=["t2"], w=["t2"])
            X.op("dve", lambda h: h.reciprocal(out=t2[:], in_=t2[:]), r=["t2"], w=["t2"])
            X.op("dve", lambda h: h.tensor_tensor(out=kk[:], in0=kk[:], in1=t2[:], op=ALU.mult), r=["kk", "t2"], w=["kk"])
            Y.op("pe", lambda h, hc=hc: h.matmul(ps_bv[0:64, 0:MC], lhsT=L2[:, 0, hc], rhs=h1T[:, 0, :], start=True, stop=True), r=["L2", "h1T"], w=["ps_bv"])
            Y.op("act", lambda h, hd=hd: h.activation(out=sig[:], in_=ps_bv[0:64, 0:MC], func=AF.Sigmoid, bias=vec[:, hd, 0:1]), r=["ps_bv", "vec"], w=["sig"])
            Y.op("pe", lambda h, hc=hc: h.matmul(ps_bv[0:64, 0:MC], lhsT=L2[:, 1, hc], rhs=h1T[:, 1, :], start=True, stop=True), r=["L2", "h1T"], w=["ps_bv"])
            Y.op("act", lambda h, hd=hd: h.activation(out=alp[:], in_=ps_bv[0:64, 0:MC], func=AF.Sigmoid, bias=vec[:, hd, 1:2]), r=["ps_bv", "vec"], w=["alp"])
            Y.op("dve", lambda h: h.tensor_tensor_scan(out=cs[:], data0=resetm[:], data1=sig[:], initial=0.0, op0=ALU.mult, op1=ALU.add), r=["resetm", "sig"], w=["cs"])
            Y.op("dve", lambda h: h.tensor_copy(out=cLs[:], in_=cs[:, 63::64]), r=["cs"], w=["cLs"])
            Y.op("act", lambda h, gi=gi: h.activation(out=cLd[gi][:], in_=cLs[:], func=AF.Exp, scale=-KAP), r=["cLs"], w=[f"cLd{gi}"])
            Y.op("act", lambda h: h.activation(out=e1[:], in_=cs[:], func=AF.Exp, scale=-KAP), r=["cs"], w=["e1"])
            Y.op("act", lambda h: h.activation(out=e2[:], in_=cs[:], func=AF.Exp, scale=KAP), r=["cs"], w=["e2"])
            Y.op("pool", lambda h: h.tensor_tensor(out=e3[:], in0=cs[:], in1=sig[:], op=ALU.subtract), r=["cs", "sig"], w=["e3"])
            Y.op("act", lambda h: h.activation(out=e3[:], in_=e3[:], func=AF.Exp, scale=-KAP), r=["e3"], w=["e3"])
            Y.op("dve", lambda h: h.tensor_tensor(out=c3(e4), in0=c3(cs), in1=cLs[:].unsqueeze(2).broadcast_to([64, NJ, 64]), op=ALU.subtract), r=["cs", "cLs"], w=["e4"])
            Y.op("act", lambda h: h.activation(out=e4[:], in_=e4[:], func=AF.Exp, scale=KAP), r=["e4"], w=["e4"])
            merge_streams(S, [X, Y])
            S.op("dve", lambda h, hd=hd: h.tensor_scalar(out=t1[:], in0=alp[:], scalar1=1.0, scalar2=vec[:, hd, 3:4], op0=ALU.subtract, op1=ALU.mult), r=["alp", "vec"], w=["t1"])
            S.op("dve", lambda h: h.scalar_tensor_tensor(out=kmod[:], in0=t1[:], scalar=1.0, in1=k_f[:], op0=ALU.add, op1=ALU.mult), r=["t1", "k_f"], w=["kmod"])
            S.op("pool", lambda h: h.tensor_tensor(out=bal[:], in0=kk[:], in1=alp[:], op=ALU.mult), r=["kk", "alp"], w=["bal"])
            S.op("pool", lambda h, gi=gi: h.tensor_tensor(out=AR[gi][:, :, 64:128], in0=c3(r_f), in1=c3(e1), op=ALU.mult), r=["r_f", "e1"], w=[f"AR{gi}"])
            S.op("dve", lambda h, gi=gi: h.scalar_tensor_tensor(out=AR[gi][:, :, 0:64], in0=c3(kk), scalar=-1.0, in1=c3(e3), op0=ALU.mult, op1=ALU.mult),
                 r=["kk", "e3"], w=[f"AR{gi}"])
            S.op("dve", lambda h, gi=gi: h.tensor_tensor(out=BK[gi][:, :, 0:64], in0=c3(bal), in1=c3(e2), op=ALU.mult), r=["bal", "e2"], w=[f"BK{gi}"])
            S.op("pool", lambda h, gi=gi: h.tensor_tensor(out=BK[gi][:, :, 64:128], in0=c3(kmod), in1=c3(e2), op=ALU.mult), r=["kmod", "e2"], w=[f"BK{gi}"])
            S.op("dve", lambda h, gi=gi: h.tensor_tensor(out=BKe[gi][:, :, 0:64], in0=c3(bal), in1=c3(e4), op=ALU.mult), r=["bal", "e4"], w=[f"BKe{gi}"])
            S.op("pool", lambda h, gi=gi: h.tensor_tensor(out=BKe[gi][:, :, 64:128], in0=c3(kmod), in1=c3(e4), op=ALU.mult), r=["kmod", "e4"], w=[f"BKe{gi}"])
            S.op("dve", lambda h, gi=gi, hd=hd: h.scalar_tensor_tensor(out=rkrp[gi][:, :, :], in0=c3(r_f), scalar=vec[:, hd, 4:5], in1=c3(kmod), op0=ALU.mult, op1=ALU.mult),
                 r=["r_f", "kmod", "vec"], w=[f"rkrp{gi}"])
        def head_gn(S, hd):
            gi = hd % G2
            hc = slice(hd * 64, (hd + 1) * 64)
            def head_g(S, j):
                psn = ps_n if j == 0 else ps_tok
                psn_name = "ps_n" if j == 0 else "ps_tok"
                Nk_, NkT_, Pm_ = Nk[j], NkT[j], Pm[j]
                S.op("pe", lambda h, gi=gi, j=j: h.matmul(ps_g[0:64, 0:128], lhsT=BK[gi][:, j, 0:64], rhs=AR[gi][:, j, :], start=True, stop=True), r=[f"BK{gi}", f"AR{gi}"], w=["ps_g"])
                S.op("pe", lambda h, gi=gi, j=j: h.matmul(ps_g[0:64, 128:256], lhsT=BK[gi][:, j, 64:128], rhs=AR[gi][:, j, :], start=True, stop=True), r=[f"BK{gi}", f"AR{gi}"], w=["ps_g"])
                S.op("pe", lambda h, gi=gi, j=j: h.matmul(ps_g[0:64, 256:320], lhsT=AR[gi][:, j, 0:64], rhs=BK[gi][:, j, 0:64], start=True, stop=True), r=[f"BK{gi}", f"AR{gi}"], w=["ps_g"])
                S.op("dve", lambda h, gi=gi, j=j: h.tensor_tensor(out=Gm[gi][:, j, 0:128], in0=ps_g[0:64, 0:128], in1=maskG[0:64, :], op=ALU.mult), r=["ps_g", "maskG"], w=[f"Gm{gi}"])
                S.op("dve", lambda h, gi=gi, j=j: h.tensor_tensor(out=Gm[gi][:, j, 128:256], in0=ps_g[0:64, 128:256], in1=maskG[0:64, :], op=ALU.mult), r=["ps_g", "maskG"], w=[f"Gm{gi}"])
                S.op("dve", lambda h: h.tensor_tensor(out=NkT_[0][:], in0=ps_g[0:64, 256:320], in1=maskNT[:], op=ALU.mult), r=["ps_g", "maskNT"], w=[f"NkT{j}_0"])
                S.op("pool", lambda h, gi=gi, j=j: h.tensor_copy(out=Nk_[0][:], in_=Gm[gi][:, j, 0:64]), r=[f"Gm{gi}"], w=[f"Nk{j}_0"])
                S.op("pool", lambda h, gi=gi, j=j: h.tensor_tensor(out=Pm_[0][:], in0=Gm[gi][:, j, 0:64], in1=identf[0:64, 0:64], op=ALU.add), r=[f"Gm{gi}", "identf"], w=[f"Pm{j}_0"])
                for q in range(2):
                    S.op("pe", lambda h, gi=gi, j=j, q=q: h.transpose(out=ps_g[0:64, 320 + q * 64:384 + q * 64], in_=BKe[gi][:, j, q * 64:(q + 1) * 64], identity=identf[0:64, 0:64]), r=[f"BKe{gi}", "identf"], w=["ps_g"])
                S.op("act", lambda h, gi=gi, j=j: h.copy(out=BKeT[gi][:, j, :], in_=ps_g[0:64, 320:448]), r=["ps_g"], w=[f"BKeT{gi}"])
                S.op("pool", lambda h, gi=gi, j=j: h.tensor_scalar(out=dcL[gi][:, j, :], in0=identf[0:64, 0:64], scalar1=cLd[gi][:, j:j + 1], scalar2=None, op0=ALU.mult), r=["identf", f"cLd{gi}"], w=[f"dcL{gi}"])
                S.op("pe", lambda h, gi=gi, j=j: h.matmul(ps_g[0:64, 448 + j:449 + j], lhsT=rkrp[gi][:, j, :], rhs=ones64[:, 0:1], start=True, stop=True), r=[f"rkrp{gi}", "ones64"], w=["ps_g"])
                S.op("act", lambda h, gi=gi, j=j: h.copy(out=bon[gi][:, j:j + 1], in_=ps_g[0:64, 448 + j:449 + j]), r=["ps_g"], w=[f"bon{gi}"])
            def head_n(S, j):
                psn = ps_n if j == 0 else ps_tok
                psn_name = "ps_n" if j == 0 else "ps_tok"
                Nk_, NkT_, Pm_ = Nk[j], NkT[j], Pm[j]
                cur = 0
                for sidx in range(5):
                    nx = 1 - cur
                    last = (sidx == 4)
                    S.op("pe", lambda h, cur=cur: h.matmul(psn[0:64, 0:64], lhsT=Nk_[cur][:], rhs=NkT_[cur][:], start=True, stop=True), r=[f"Nk{j}_{cur}", f"NkT{j}_{cur}"], w=[psn_name])
                    S.op("act", lambda h, nx=nx: h.copy(out=NkT_[nx][:], in_=psn[0:64, 0:64]), r=[psn_name], w=[f"NkT{j}_{nx}"])
                    if not last:
                        S.op("pe", lambda h, cur=cur: h.matmul(psn[0:64, 64:128], lhsT=NkT_[cur][:], rhs=Nk_[cur][:], start=True, stop=True), r=[f"Nk{j}_{cur}", f"NkT{j}_{cur}"], w=[psn_name])
                        S.op("act", lambda h, nx=nx: h.copy(out=Nk_[nx][:], in_=psn[0:64, 64:128]), r=[psn_name], w=[f"Nk{j}_{nx}"])
                    S.op("pe", lambda h, cur=cur, nx=nx: h.matmul(psn[0:64, 128:192], lhsT=NkT_[nx][:], rhs=Pm_[cur][:], start=True, stop=True), r=[f"NkT{j}_{nx}", f"Pm{j}_{cur}"], w=[psn_name])
                    if last:
                        S.op("dve", lambda h, cur=cur, gi=gi, j=j: h.tensor_tensor(out=Tm[gi][:, j, :], in0=psn[0:64, 128:192], in1=Pm_[cur][:], op=ALU.add), r=[psn_name, f"Pm{j}_{cur}"], w=[f"Tm{gi}"])
                    else:
                        S.op("dve", lambda h, cur=cur, nx=nx: h.tensor_tensor(out=Pm_[nx][:], in0=psn[0:64, 128:192], in1=Pm_[cur][:], op=ALU.add), r=[psn_name, f"Pm{j}_{cur}"], w=[f"Pm{j}_{nx}"])
                    cur = nx
            for j in range(NJ):
                head_g(S, j)
            sj = [Stream() for _ in range(NJ)]
            for j in range(NJ):
                head_n(sj[j], j)
            merge_streams(S, sj)
            for j in range(NJ):
                S.op("dve", lambda h, gi=gi, j=j: h.tensor_scalar(out=Dg[gi][:, j, :], in0=identf[0:64, 0:64], scalar1=bon[gi][:, j:j + 1], scalar2=None, op0=ALU.mult),
                     r=["identf", f"bon{gi}"], w=[f"Dg{gi}"])
        def head_rec(S, hd):
            gi = hd % G2
            hc = slice(hd * 64, (hd + 1) * 64)
            for j in range(NJ):
                gj = mc * NJ + j
                s_in = ST[hd][gj % 2]
                s_out = ST[hd][(gj + 1) % 2]
                sin_n = f"ST{hd}_{gj % 2}"
                sout_n = f"ST{hd}_{(gj + 1) % 2}"
                wi = gj % 2
                VT = vwin[:, j, hc]
                UT = uT[:, j, hc]
                vn = f"vwin{j}"
                un = f"uT{j}_{hd}"
                S.op("pe", lambda h, gi=gi, j=j, VT=VT, hc=hc: h.matmul(ps_rec[0:64, 192:256], lhsT=Dg[gi][:, j, :], rhs=VT, start=True, stop=True), r=[f"Dg{gi}", vn], w=["ps_rec"])
                S.op("act", lambda h, j=j, hc=hc: h.copy(out=bv_all[:, j, hc], in_=ps_rec[0:64, 192:256]), r=["ps_rec"], w=["bv_all"])
                S.op("pe", lambda h, gi=gi, j=j, s_in=s_in: h.matmul(ps_rec[0:64, 0:64], lhsT=AR[gi][:, j, 0:64], rhs=s_in[:], start=True, stop=False), r=[f"AR{gi}", sin_n], w=["ps_rec"])
                S.op("pe", lambda h, gi=gi, j=j, VT=VT: h.matmul(ps_rec[0:64, 0:64], lhsT=Gm[gi][:, j, 128:192], rhs=VT, start=False, stop=True), r=[f"Gm{gi}", vn], w=["ps_rec"])
                S.op("act", lambda h, wi=wi: h.copy(out=WT[wi][:], in_=ps_rec[0:64, 0:64]), r=["ps_rec"], w=[f"WT{wi}"])
                S.op("pe", lambda h, gi=gi, j=j, wi=wi: h.matmul(ps_rec[0:64, 64:128], lhsT=Tm[gi][:, j, :], rhs=WT[wi][:], start=True, stop=True), r=[f"Tm{gi}", f"WT{wi}"], w=["ps_rec"])
                S.op("act", lambda h, UT=UT: h.copy(out=UT, in_=ps_rec[0:64, 64:128]), r=["ps_rec"], w=[un])
                S.op("pe", lambda h, gi=gi, j=j, s_in=s_in, hc=hc: h.matmul(ps_y[0:64, hc], lhsT=AR[gi][:, j, 64:128], rhs=s_in[:], start=True, stop=False), r=[f"AR{gi}", sin_n], w=["ps_y"])
                S.op("pe", lambda h, gi=gi, j=j, hc=hc, UT=UT: h.matmul(ps_y[0:64, hc], lhsT=Gm[gi][:, j, 64:128], rhs=UT, start=False, stop=False), r=[f"Gm{gi}", un], w=["ps_y"])
                S.op("pe", lambda h, gi=gi, j=j, hc=hc, VT=VT: h.matmul(ps_y[0:64, hc], lhsT=Gm[gi][:, j, 192:256], rhs=VT, start=False, stop=True), r=[f"Gm{gi}", vn], w=["ps_y"])
                S.op("dve", lambda h, j=j, hc=hc: h.tensor_copy(out=y_all[:, j, hc], in_=ps_y[0:64, hc]), r=["ps_y"], w=["y_all"])
                S.op("pe", lambda h, gi=gi, j=j, s_in=s_in: h.matmul(ps_rec[0:64, 128:192], lhsT=dcL[gi][:, j, :], rhs=s_in[:], start=True, stop=False), r=[f"dcL{gi}", sin_n], w=["ps_rec"])
                S.op("pe", lambda h, gi=gi, j=j, UT=UT: h.matmul(ps_rec[0:64, 128:192], lhsT=BKeT[gi][:, j, 0:64], rhs=UT, start=False, stop=False), r=[f"BKeT{gi}", un], w=["ps_rec"])
                S.op("pe", lambda h, gi=gi, j=j, VT=VT: h.matmul(ps_rec[0:64, 128:192], lhsT=BKeT[gi][:, j, 64:128], rhs=VT, start=False, stop=True), r=[f"BKeT{gi}", vn], w=["ps_rec"])
                S.op("act", lambda h, s_out=s_out: h.copy(out=s_out[:], in_=ps_rec[0:64, 128:192]), r=["ps_rec"], w=[sout_n])
        for step in range(8 + 2):
            streams = []
            if step < 8:
                st_ = Stream()
                head_prep(st_, step)
                streams.append(st_)
            if 0 <= step - 1 < 8:
                st_ = Stream()
                head_gn(st_, step - 1)
                streams.append(st_)
            if 0 <= step - 2 < 8:
                st_ = Stream()
                head_rec(st_, step - 2)
                streams.append(st_)
            merge_streams(S, streams)
        for j in range(NJ):
            y3 = y_all[:, j, :].rearrange("p (h v) -> p h v", h=8)
            S.op("dve", lambda h, y3=y3: h.tensor_reduce(out=gst[:, 0, :], in_=y3, axis=AX.X, op=ALU.add), r=["y_all"], w=["gst"])
            S.op("act", lambda h, j=j: h.activation(out=yn[:], in_=y_all[:, j, :], func=AF.Square), r=["y_all"], w=["yn"])
            S.op("dve", lambda h: h.tensor_reduce(out=gst[:, 1, :], in_=yn[:].rearrange("p (h v) -> p h v", h=8), axis=AX.X, op=ALU.add), r=["yn"], w=["gst"])
            S.op("dve", lambda h: h.tensor_scalar(out=gst[:, 0, :], in0=gst[:, 0, :], scalar1=1.0 / 64, scalar2=None, op0=ALU.mult), r=["gst"], w=["gst"])
            S.op("dve", lambda h: h.tensor_tensor(out=gst[:, 2, :], in0=gst[:, 0, :], in1=gst[:, 0, :], op=ALU.mult), r=["gst"], w=["gst"])
            S.op("dve", lambda h: h.scalar_tensor_tensor(out=gst[:, 1, :], in0=gst[:, 1, :], scalar=1.0 / 64, in1=gst[:, 2, :], op0=ALU.mult, op1=ALU.subtract), r=["gst"], w=["gst"])
            S.op("act", lambda h: h.activation(out=gst[:, 1, :], in_=gst[:, 1, :], func=AF.Sqrt, bias=GN_EPS), r=["gst"], w=["gst"])
            S.op("dve", lambda h: h.reciprocal(out=gst[:, 1, :], in_=gst[:, 1, :]), r=["gst"], w=["gst"])
            for hd in range(8):
                hc = slice(hd * 64, (hd + 1) * 64)
                S.op("dve", lambda h, j=j, hd=hd, hc=hc: h.tensor_scalar(out=yn[:, hc], in0=y_all[:, j, hc], scalar1=gst[:, 0, hd:hd + 1], scalar2=gst[:, 1, hd:hd + 1],
                                                                      op0=ALU.subtract, op1=ALU.mult), r=["y_all", "gst"], w=["yn"])
            S.op("pool", lambda h: h.tensor_tensor(out=yn[:], in0=yn[:], in1=lnw[:], op=ALU.mult), r=["yn", "lnw"], w=["yn"])
            S.op("pool", lambda h: h.tensor_tensor(out=yn[:], in0=yn[:], in1=lnb[:], op=ALU.add), r=["yn", "lnb"], w=["yn"])
            S.op("pool", lambda h, j=j: h.tensor_tensor(out=yn[:], in0=yn[:], in1=bv_all[:, j, :], op=ALU.add), r=["yn", "bv_all"], w=["yn"])
            S.op("dve", lambda h, j=j: h.tensor_tensor(out=yz[:], in0=yn[:], in1=zs[:, j, :], op=ALU.mult), r=["yn", "zs"], w=["yz"])
            for c in range(4):
                S.op("pe", lambda h, c=c: h.transpose(out=ps_tr[:, c * 64:(c + 1) * 64], in_=yz[:, c * 128:(c + 1) * 128], identity=ident[0:64, 0:64]), r=["yz", "ident"], w=["ps_tr"])
            S.op("act", lambda h: h.copy(out=yzT[:], in_=ps_tr[:, 0:256].rearrange("p (c n) -> p c n", c=4)), r=["ps_tr"], w=["yzT"])
            for hf in range(2):
                for c in range(4):
                    S.op("pe", lambda h, hf=hf, c=c: h.matmul(ps_tok[0:64, :], lhsT=yzT[:, c, :], rhs=wo[:, c, hf * 512:(hf + 1) * 512], start=(c == 0), stop=(c == 3)), r=["yzT", "wo"], w=["ps_tok"])
                S.op("act", lambda h, hf=hf: h.copy(out=pt[:, hf * 512:(hf + 1) * 512], in_=ps_tok[0:64, :]), r=["ps_tok"], w=["pt"])
            rows = slice(T0 + j * 64, T0 + (j + 1) * 64)
            S.op("sp", lambda h, rows=rows: h.dma_start(out=p_out[rows, :], in_=pt[:]), r=["pt"], dma=True)
    return nc, es, S


def prep_B(z, half, MC=256):
    d = {}
    w_in = z['b_w_in'][0]
    own = slice(half * 512, (half + 1) * 512)
    d['w4'] = np.ascontiguousarray(np.stack([w_in[:, s * 1024:(s + 1) * 1024][:, own] for s in range(4)]))
    d['lw'] = np.ascontiguousarray(np.stack([z['b_w1'][0], z['b_a1'][0]]))
    d['l2'] = np.ascontiguousarray(np.stack([z['b_w2'][0][:, own], z['b_a2'][0][:, own]]))
    d['wo'] = np.ascontiguousarray(z['b_w_out'][0][own, :])
    mu = z['b_mu'][0]
    d['muT'] = np.ascontiguousarray(mu.reshape(6, 8, 128).transpose(2, 0, 1))
    vecs = np.zeros((64, 8, 8), np.float32)
    def fm(v):
        return v[own].reshape(8, 64).T
    vecs[:, :, 0] = fm(z['b_w0'][0]); vecs[:, :, 1] = fm(z['b_a0'][0]); vecs[:, :, 2] = fm(z['b_k_k'][0]); vecs[:, :, 3] = fm(z['b_k_a'][0])
    vecs[:, :, 4] = fm(z['b_r_k'][0].reshape(-1))
    d['vecs'] = vecs
    d['lnw'] = np.ascontiguousarray(z['b_lnx_w'][0][own][None, :])
    d['lnb'] = np.ascontiguousarray(z['b_lnx_b'][0][own][None, :])
    j = np.arange(64)[:, None]; i = np.arange(64)[None, :]
    strict = (j < i).astype(np.float32); incl = (j <= i).astype(np.float32)
    row = np.concatenate([strict, incl], 1)
    d['maskG'] = np.ascontiguousarray(np.concatenate([row, row], 0))
    d['maskNT'] = np.ascontiguousarray(strict.T)
    rm = np.ones((64, MC), np.float32); rm[:, ::64] = 0.0
    d['resetm'] = rm
    d['g'] = z['norm_g'][1:2].copy()
    d['ident'] = np.eye(128, dtype=np.float32)
    return d


NEGM = -30000.0


def build_C(nsrc=1, debug=False):
    T = CFG.T
    NT = T // 128
    NCMP = T // 16 - 1
    NKT = (NCMP + 127) // 128
    nc = get_nc()
    es = ExitStack()
    S = get_sched(nc, es)
    srcs = [dram_in(nc, f"xin{k}", [T, D]) for k in range(nsrc)]
    xs_out = dram_out(nc, "xs", [T, D]) if nsrc > 1 else None
    g_row = dram_in(nc, "g", [1, D])
    ident_d = dram_in(nc, "ident", [128, 128])
    wq_d = dram_in(nc, "wq", [D, 512])
    wkv_d = dram_in(nc, "wkv", [D, 6, 128])
    wg_d = dram_in(nc, "wg", [D, 24])
    wz_d = dram_in(nc, "wz", [D, 512])
    wo_d = dram_in(nc, "wo", [512, D])
    w1_d = dram_in(nc, "w1", [2, 64, 32, 128])
    w2_d = dram_in(nc, "w2", [2, 128, 64])
    pos_d = dram_in(nc, "posT", [2, 64, 32])
    bias_d = dram_in(nc, "biasT", [128, 4, 1024])
    F4_d = dram_in(nc, "F4", [512, 512])
    ka_d = dram_in(nc, "keepadd", [NT, 128, 128])
    E_d = dram_in(nc, "E", [64, NT, 128])
    ov_d = dram_in(nc, "ovl", [128, 2, 64])
    p_out = dram_out(nc, "p", [T, D])

    ps_tr = mk(nc, es, "ps_tr", [128, 1024], BF16, psum=True)
    ps_q = mk(nc, es, "ps_q", [128, 512], F32, psum=True)
    ps_z = mk(nc, es, "ps_z", [128, 512], F32, psum=True)
    ps_s = [mk(nc, es, f"ps_s{k}", [128, 512], F32, psum=True) for k in range(2)]
    po_c = mk(nc, es, "po_c", [128, 4, 128], F32, psum=True)
    po_s = mk(nc, es, "po_s", [128, 4, 128], F32, psum=True)
    po_w = mk(nc, es, "po_w", [128, 4, 128], F32, psum=True)
    p1 = P1(S, nc, es, srcs, xs_out, g_row, ident_d, ps_tr)
    ident = p1.ident
    xnTt = [mk(nc, es, f"xnTt{b}", [128, 8, 128], BF16) for b in range(2)]

    wq = mk(nc, es, "wq_s", [128, 8, 512], BF16)
    wkv = mk(nc, es, "wkv_s", [128, 8, 6, 128], BF16)
    wg = mk(nc, es, "wg_s", [128, 8, 24], BF16)
    wz = mk(nc, es, "wz_s", [128, 8, 512], BF16)
    wo = mk(nc, es, "wo_s", [128, 4, D], BF16)
    w1 = mk(nc, es, "w1_s", [64, 2, 32, 128], BF16)
    w2 = mk(nc, es, "w2_s", [128, 2, 64], BF16)
    posT = mk(nc, es, "posT_s", [64, 2, 32], BF16)
    biasT = mk(nc, es, "biasT_s", [128, 4, 1024], BF16)
    Em = mk(nc, es, "E_s", [64, NT, 128], BF16)
    S.op("pool", lambda h: h.dma_start(out=wq[:], in_=wq_d.rearrange("(c p) n -> p c n", p=128)), w=["wq"], dma=True)
    S.op("pool", lambda h: h.dma_start(out=wkv[:], in_=wkv_d.rearrange("(c p) s n -> p c s n", p=128)), w=["wkv"], dma=True)
    S.op("pool", lambda h: h.dma_start(out=wg[:], in_=wg_d.rearrange("(c p) n -> p c n", p=128)), w=["wg"], dma=True)
    S.op("pool", lambda h: h.dma_start(out=wz[:], in_=wz_d.rearrange("(c p) n -> p c n", p=128)), w=["wz"], dma=True)
    S.op("pool", lambda h: h.dma_start(out=wo[:], in_=wo_d.rearrange("(c p) n -> p c n", p=128)), w=["wo"], dma=True)
    S.op("pool", lambda h: h.dma_start(out=w1[:], in_=w1_d.rearrange("s d l h -> d s l h")), w=["w1"], dma=True)
    S.op("pool", lambda h: h.dma_start(out=w2[:], in_=w2_d.rearrange("s h d -> h s d")), w=["w2"], dma=True)
    S.op("pool", lambda h: h.dma_start(out=posT[:], in_=pos_d.rearrange("s d l -> d s l")), w=["posT"], dma=True)
    S.op("pool", lambda h: h.dma_start(out=biasT[:], in_=bias_d), w=["biasT"], dma=True)
    S.op("pool", lambda h: h.dma_start(out=Em[:], in_=E_d), w=["E"], dma=True)

    kvT = mk(nc, es, "kvT", [64, 2, 2, T], BF16)
    roll = mk(nc, es, "roll", [64, 2, 2, 144], BF16)
    vau = mk(nc, es, "vau", [128, NT, 2, 2, 65], BF16)
    S.op("dve", lambda h: h.memset(vau[:, :, :, :, 64:65], 1.0), w=["vau_ones"])
    S.op("dve", lambda h: h.memset(roll[:], 0.0), w=["roll"])
    kcmpT = mk(nc, es, "kcmpT", [64, 2, 256], BF16)
    vcau = mk(nc, es, "vcau", [128, 2, 2, 65], BF16)
    ovl = mk(nc, es, "ovl_s", [128, 2, 64], BF16)
    hidn = mk(nc, es, "hidn", [128, 4, 8], BF16)
    hidv = mk(nc, es, "hidv", [128, 2, 256], BF16)
    pbias = mk(nc, es, "pbias", [128, 2], F32)
    S.op("dve", lambda h: h.memset(kcmpT[:], 0.0), w=["kcmpT"])
    S.op("dve", lambda h: h.memset(vcau[:], 0.0), w=["vcau"])
    S.op("dve", lambda h: h.memset(vcau[:, :, :, 64:65], 1.0), r=["vcau"], w=["vcau"])
    S.op("dve", lambda h: h.memset(hidv[:], 0.0), w=["hidv"])
    S.op("pool", lambda h: h.dma_start(out=ovl[:], in_=ov_d), w=["ovl"], dma=True)
    for s in range(2):
        for l in range(32):
            S.op("pe", lambda h, s=s, l=l: h.matmul(ps_z[:, s:s + 1], lhsT=w1[:, s, l, :], rhs=posT[:, s, l:l + 1], start=(l == 0), stop=(l == 31)),
                 r=["w1", "posT"], w=["ps_z"])
        S.op("act", lambda h, s=s: h.copy(out=pbias[:, s:s + 1], in_=ps_z[:, s:s + 1]), r=["ps_z"], w=["pbias"])

    NB = 2
    qT = [mk(nc, es, f"qT{b}", [64, 2, 4, 128], BF16) for b in range(NB)]
    zs = [mk(nc, es, f"zs{b}", [128, 512], BF16) for b in range(NB)]
    gt = [mk(nc, es, f"gt{b}", [128, 24], F32) for b in range(NB)]
    NP = 4
    pT = [mk(nc, es, f"pT{b}", [128, 512], BF16) for b in range(NP)]
    F4t = [mk(nc, es, f"F4t{b}", [128, 512], BF16) for b in range(2)]
    ka = [mk(nc, es, f"ka{b}", [128, 128], F32) for b in range(NB)]
    imp = mk(nc, es, "imp", [128, 64], F32)
    imp2 = mk(nc, es, "imp2", [128, 64], F32)
    m8 = mk(nc, es, "m8", [128, 16], F32)
    nsel = mk(nc, es, "nsel", [128, 64], BF16)
    nselT = mk(nc, es, "nselT", [64, 4, 128], BF16)
    rden = mk(nc, es, "rden", [128, 3, 4], F32)
    cf = mk(nc, es, "cf", [128, 3, 4], F32)
    y = mk(nc, es, "y", [128, 512], F32)
    yz = [mk(nc, es, f"yz{b}", [128, 512], BF16) for b in range(NB)]
    yzT = [mk(nc, es, f"yzT{b}", [128, 4, 128], BF16) for b in range(NB)]
    pt = [mk(nc, es, f"pt{b}", [128, D], F32) for b in range(NB)]
    pcount = [0]
    scount = [0]

    def st_tile(g, b, lhsT_ap, lhs_bufs, extra, rhs_aug, rhs_bufs, po, first, last, ncol=65):
        si = scount[0] % 2
        scount[0] += 1
        pi = pcount[0] % NP
        pcount[0] += 1
        n_extra = len(extra)
        S.op("pe", lambda h: h.matmul(ps_s[si][:, :], lhsT=lhsT_ap, rhs=qT[b][:, g, :, :].rearrange("p r n -> p (r n)"), start=True, stop=(n_extra == 0)),
             r=lhs_bufs + [f"qT{b}_{g}"], w=[f"ps_s{si}"])
        for j, (el, er, ebufs) in enumerate(extra):
            S.op("pe", lambda h, el=el, er=er, j=j: h.matmul(ps_s[si][:, :], lhsT=el, rhs=er, start=False, stop=(j == n_extra - 1)),
                 r=ebufs, w=[f"ps_s{si}"])
        S.op("act", lambda h: h.activation(out=pT[pi][:], in_=ps_s[si][:, :], func=AF.Exp), r=[f"ps_s{si}"], w=[f"pT{pi}"])
        return pi

    for qt in range(NT):
        b = qt % NB
        tq = slice(qt * 128, (qt + 1) * 128)
        xk = f"xnTt{qt % 2}"
        xn = xnTt[qt % 2]
        S.op("sp", lambda h, b=b, qt=qt: h.dma_start(out=ka[b][:], in_=ka_d[qt]), w=[f"ka{b}"], dma=True)
        p1.tile(qt, xn[:, :, :], xk)
        if qt > 0:
            S.op("pool", lambda h: h.tensor_copy(out=roll[:, :, :, 0:16], in_=roll[:, :, :, 128:144]), r=["roll"], w=["roll"])
        for grp, (wss, psx, nm) in enumerate((((0, 1), ps_q, "ps_q"), ((2, 4), ps_z, "ps_z"))):
            for si, ws in enumerate(wss):
                for g in range(2):
                    c0 = (si * 2 + g) * 128
                    for dc in range(8):
                        S.op("pe", lambda h, ws=ws, g=g, dc=dc, c0=c0, psx=psx, xn=xn: h.matmul(psx[0:64, c0:c0 + 128], lhsT=wkv[:, dc, ws, g * 64:(g + 1) * 64], rhs=xn[:, dc, :],
                                                                                     start=(dc == 0), stop=(dc == 7)), r=["wkv", xk], w=[nm])
            if grp == 0:
                S.op("act", lambda h, psx=psx: h.copy(out=roll[:, :, :, 16:144], in_=psx[0:64, :].rearrange("p (s g n) -> p s g n", s=2, g=2)), r=[nm], w=["roll"])
            else:
                S.op("act", lambda h, psx=psx, tq=tq: h.copy(out=kvT[:, :, :, tq], in_=psx[0:64, :].rearrange("p (s g n) -> p s g n", s=2, g=2)), r=[nm], w=[f"kvT_{qt // 4}"])
        for jj, ws in enumerate((3, 5)):
            for dc in range(8):
                S.op("pe", lambda h, dc=dc, ws=ws, jj=jj, xn=xn: h.matmul(ps_z[:, jj * 128:(jj + 1) * 128], lhsT=xn[:, dc, :], rhs=wkv[:, dc, ws, :],
                                                                     start=(dc == 0), stop=(dc == 7)), r=["wkv", xk], w=["ps_z"])
        S.op("dve", lambda h, qt=qt: h.tensor_copy(out=vau[:, qt, :, :, 0:64], in_=ps_z[:, 0:256].rearrange("p (j g d) -> p j g d", j=2, g=2)),
             r=["ps_z"], w=[f"vau{qt}"])
        m0 = 1 if qt == 0 else 0
        nb = 8 - m0
        n0 = 8 * qt - 1 + m0
        for s in range(2):
            for g in range(2):
                c0 = (s * 2 + g) * 8
                for l in range(32):
                    S.op("pe", lambda h, s=s, g=g, l=l, c0=c0, nb=nb, m0=m0: h.matmul(ps_q[:, c0:c0 + nb], lhsT=w1[:, s, l, :], rhs=roll[:, s, g, l + 16 * m0:l + 16 * 7 + 1:16],
                                                                         start=(l == 0), stop=(l == 31)), r=["w1", "roll"], w=["ps_q"])
        for s in range(2):
            S.op("act", lambda h, s=s, nb=nb: h.activation(out=hidn[:, s * 2:s * 2 + 2, 0:nb], in_=ps_q[:, s * 16:s * 16 + 16].rearrange("p (g n) -> p g n", g=2)[:, :, 0:nb],
                                                    func=AF.Silu, bias=pbias[:, s:s + 1]), r=["ps_q", "pbias"], w=["hidn"])
        for g in range(2):
            S.op("pe", lambda h, g=g, nb=nb: h.matmul(ps_z[0:64, g * 8:g * 8 + nb], lhsT=w2[:, 0, :], rhs=hidn[:, g, 0:nb], start=True, stop=True), r=["w2", "hidn"], w=["ps_z"])
        S.op("act", lambda h, nb=nb, n0=n0: h.copy(out=kcmpT[:, :, n0:n0 + nb], in_=ps_z[0:64, 0:16].rearrange("p (g n) -> p g n", g=2)[:, :, 0:nb]), r=["ps_z"], w=["kcmpT"])
        S.op("pool", lambda h, nb=nb, n0=n0: h.tensor_copy(out=hidv[:, :, n0:n0 + nb], in_=hidn[:, 2:4, 0:nb]), r=["hidn"], w=["hidv"])
        for nt in sorted(set([n0 // 128, (n0 + nb - 1) // 128])):
            for g in range(2):
                S.op("pe", lambda h, g=g, nt=nt: h.matmul(ps_z[:, 128 + g * 64:192 + g * 64], lhsT=hidv[:, g, nt * 128:(nt + 1) * 128], rhs=w2[:, 1, :], start=True, stop=True),
                     r=["w2", "hidv"], w=["ps_z"])
            S.op("act", lambda h, nt=nt: h.copy(out=vcau[:, nt, :, 0:64], in_=ps_z[:, 128:256].rearrange("p (g d) -> p g d", g=2)), r=["ps_z"], w=["vcau"])
        for g in range(2):
            for r in range(4):
                for dc in range(8):
                    col = (g * 4 + r) * 64
                    S.op("pe", lambda h, g=g, r=r, dc=dc, col=col, xn=xn: h.matmul(ps_q[0:64, r * 128:(r + 1) * 128], lhsT=wq[:, dc, col:col + 64],
                                                                               rhs=xn[:, dc, :], start=(dc == 0), stop=(dc == 7)),
                         r=["wq", xk], w=["ps_q"])
            S.op("act", lambda h, g=g, b=b: h.activation(out=qT[b][:, g, :, :], in_=ps_q[0:64, :].rearrange("p (r n) -> p r n", r=4),
                                                        func=AF.Copy, scale=0.125), r=["ps_q"], w=[f"qT{b}_{g}"])
        for dc in range(8):
            S.op("pe", lambda h, dc=dc, xn=xn: h.matmul(ps_z[:, :], lhsT=xn[:, dc, :], rhs=wz[:, dc, :], start=(dc == 0), stop=(dc == 7)),
                 r=["wz", xk], w=["ps_z"])
        S.op("act", lambda h, b=b: h.activation(out=zs[b][:], in_=ps_z[:, :], func=AF.Silu), r=["ps_z"], w=[f"zs{b}"])
        for dc in range(8):
            S.op("pe", lambda h, dc=dc, xn=xn: h.matmul(ps_z[:, 0:24], lhsT=xn[:, dc, :], rhs=wg[:, dc, :], start=(dc == 0), stop=(dc == 7)),
                 r=["wg", xk], w=["ps_z"])
        S.op("act", lambda h, b=b: h.activation(out=gt[b][:], in_=ps_z[:, 0:24], func=AF.Sigmoid), r=["ps_z"], w=[f"gt{b}"])
        cnts = []
        for nt in range(NKT):
            mmax = nt * 128 + 127 - 8 * qt
            mmin = nt * 128 - 8 * qt
            if mmin > 6:
                continue
            masked = mmax > -2
            cnts.append((nt, masked))
        for (nt, masked) in cnts:
            if masked:
                j0 = 128 * nt - 8 * qt + 248
                S.op("pool", lambda h, nt=nt, j0=j0: h.dma_start(out=F4t[nt][:], in_=F4_d[j0:j0 + 128, :]), w=[f"F4t{nt}"], dma=True)
        for g in range(2):
            pis = []
            for (nt, masked) in cnts:
                extra = [(ident[:], F4t[nt][:], ["p1id", f"F4t{nt}"])] if masked else []
                pi = st_tile(g, b, kcmpT[:, g, nt * 128:(nt + 1) * 128], ["kcmpT"], extra, None, None, None, None, None)
                pis.append((nt, pi))
            for r in range(4):
                for j, (nt, pi) in enumerate(pis):
                    S.op("pe", lambda h, g=g, r=r, nt=nt, pi=pi, j=j: h.matmul(po_c[:, r, 0:65], lhsT=pT[pi][:, r * 128:(r + 1) * 128], rhs=vcau[:, nt, g, :],
                                                                             start=(j == 0), stop=(j == len(pis) - 1)), r=[f"pT{pi}", "vcau"], w=["po_c"])
            for r in range(4):
                for j, (nt, pi) in enumerate(pis):
                    S.op("pe", lambda h, g=g, r=r, nt=nt, pi=pi, j=j: h.matmul(po_w[:, r, 0:64], lhsT=pT[pi][:, r * 128:(r + 1) * 128], rhs=ovl[:, nt, :],
                                                                             start=(j == 0), stop=(j == len(pis) - 1)), r=[f"pT{pi}", "ovl"], w=["po_w"])
            S.op("dve", lambda h: h.tensor_scalar(out=rden[:, 0, :], in0=po_c[:, :, 64], scalar1=1e-30, scalar2=None, op0=ALU.add), r=["po_c"], w=["rden0"])
            S.op("dve", lambda h: h.reciprocal(out=rden[:, 0, :], in_=rden[:, 0, :]), r=["rden0"], w=["rden0"])
            S.op("dve", lambda h: h.tensor_scalar(out=imp[:], in0=po_w[:, 0, 0:64], scalar1=rden[:, 0, 0:1], scalar2=None, op0=ALU.mult), r=["po_w", "rden0"], w=["imp"])
            for r in range(1, 4):
                S.op("dve", lambda h, r=r: h.scalar_tensor_tensor(out=imp[:], in0=po_w[:, r, 0:64], scalar=rden[:, 0, r:r + 1], in1=imp[:], op0=ALU.mult, op1=ALU.add),
                     r=["po_w", "rden0", "imp"], w=["imp"])
            S.op("dve", lambda h, b=b: h.tensor_tensor(out=imp[:], in0=imp[:], in1=ka[b][:, 0:64], op=ALU.mult), r=["imp", f"ka{b}"], w=["imp"])
            S.op("dve", lambda h, b=b: h.tensor_tensor(out=imp[:], in0=imp[:], in1=ka[b][:, 64:128], op=ALU.add), r=["imp", f"ka{b}"], w=["imp"])
            S.op("dve", lambda h: h.max(out=m8[:, 0:8], in_=imp[:]), r=["imp"], w=["m8"])
            S.op("dve", lambda h: h.match_replace(out=imp2[:], in_to_replace=m8[:, 0:8], in_values=imp[:], imm_value=-3.0e38), r=["imp", "m8"], w=["imp2"])
            S.op("dve", lambda h: h.max(out=m8[:, 8:16], in_=imp2[:]), r=["imp2"], w=["m8"])
            S.op("dve", lambda h: h.tensor_scalar(out=imp2[:], in0=imp[:], scalar1=m8[:, 15:16], scalar2=1.0, op0=ALU.is_ge, op1=ALU.subtract),
                 r=["imp", "m8"], w=["imp2"])
            S.op("dve", lambda h: h.tensor_scalar(out=nsel[:], in0=imp2[:], scalar1=-NEGM, scalar2=None, op0=ALU.mult), r=["imp2"], w=["nsel"])
            S.op("pe", lambda h: h.transpose(out=ps_tr[0:64, 0:128], in_=nsel[:], identity=ident[:]), r=["nsel", "p1id"], w=["p1pstr"])
            for r in range(4):
                S.op("act", lambda h, r=r: h.copy(out=nselT[:, r, :], in_=ps_tr[0:64, 0:128]), r=["p1pstr"], w=["nselT"])
            kts = [kt for kt in range(qt - 4, qt + 1) if kt >= 0]
            jobs = [("s", kt, kt) for kt in range(qt + 1)] + [("w", kt, jj) for jj, kt in enumerate(kts)]

            def do_S(job, g=g, b=b, qt=qt):
                kind, kt, jj = job
                if kind == "s":
                    cls = 0 if kt == qt else (1 if kt == qt - 1 else 3)
                    extra = [(Em[:, kt, :], nselT[:].rearrange("p r n -> p (r n)"), ["E", "nselT"]),
                             (ident[:], biasT[:, cls, g * 512:(g + 1) * 512], ["p1id", "biasT"])]
                    return st_tile(g, b, kvT[:, 0, g, kt * 128:(kt + 1) * 128], [f"kvT_{kt // 4}"], extra, None, None, None, None, None)
                dq = qt - kt
                cls = 0 if dq == 0 else (1 if dq == 1 else (2 if dq == 4 else 3))
                extra = [(ident[:], biasT[:, cls, g * 512:(g + 1) * 512], ["p1id", "biasT"])]
                return st_tile(g, b, kvT[:, 1, g, kt * 128:(kt + 1) * 128], [f"kvT_{kt // 4}"], extra, None, None, None, None, None)

            def do_PV(job, pi, g=g, qt=qt, nk=len(kts)):
                kind, kt, jj = job
                for r in range(4):
                    if kind == "s":
                        S.op("pe", lambda h, r=r: h.matmul(po_s[:, r, 0:65], lhsT=pT[pi][:, r * 128:(r + 1) * 128], rhs=vau[:, kt, 0, g, :],
                                                           start=(kt == 0 and r == 0), stop=(kt == qt), skip_group_check=True),
                             r=[f"pT{pi}", f"vau{kt}", "vau_ones"], w=["po_s"])
                    else:
                        S.op("pe", lambda h, r=r: h.matmul(po_w[:, r, 0:65], lhsT=pT[pi][:, r * 128:(r + 1) * 128], rhs=vau[:, kt, 1, g, :],
                                                           start=(jj == 0 and r == 0), stop=(jj == nk - 1), skip_group_check=True),
                             r=[f"pT{pi}", f"vau{kt}", "vau_ones"], w=["po_w"])

            pend = None
            for job in jobs:
                pi_ = do_S(job)
                if pend is not None:
                    do_PV(*pend)
                pend = (job, pi_)
            do_PV(*pend)
            S.op("dve", lambda h: h.reciprocal(out=rden[:, 1, :], in_=po_s[:, :, 64]), r=["po_s"], w=["rden1"])
            S.op("dve", lambda h: h.reciprocal(out=rden[:, 2, :], in_=po_w[:, :, 64]), r=["po_w"], w=["rden2"])
            for j in range(3):
                S.op("dve", lambda h, j=j, g=g, b=b: h.tensor_tensor(out=cf[:, j, :], in0=rden[:, j, :], in1=gt[b][:, j * 8 + g * 4:j * 8 + g * 4 + 4], op=ALU.mult),
                     r=[f"rden{j}", f"gt{b}"], w=["cf"])
            for r in range(4):
                col = (g * 4 + r) * 64
                S.op("dve", lambda h, r=r, col=col: h.tensor_scalar(out=y[:, col:col + 64], in0=po_c[:, r, 0:64], scalar1=cf[:, 0, r:r + 1], scalar2=None, op0=ALU.mult),
                     r=["po_c", "cf"], w=["y"])
                S.op("dve", lambda h, r=r, col=col: h.scalar_tensor_tensor(out=y[:, col:col + 64], in0=po_s[:, r, 0:64], scalar=cf[:, 1, r:r + 1], in1=y[:, col:col + 64],
                                                                          op0=ALU.mult, op1=ALU.add), r=["po_s", "cf", "y"], w=["y"])
                S.op("dve", lambda h, r=r, col=col: h.scalar_tensor_tensor(out=y[:, col:col + 64], in0=po_w[:, r, 0:64], scalar=cf[:, 2, r:r + 1], in1=y[:, col:col + 64],
                                                                          op0=ALU.mult, op1=ALU.add), r=["po_w", "cf", "y"], w=["y"])
        S.op("pool", lambda h, b=b: h.tensor_tensor(out=yz[b][:], in0=y[:], in1=zs[b][:], op=ALU.mult), r=["y", f"zs{b}"], w=[f"yz{b}"])
        for c in range(4):
            S.op("pe", lambda h, c=c, b=b: h.transpose(out=ps_tr[:, c * 128:(c + 1) * 128], in_=yz[b][:, c * 128:(c + 1) * 128], identity=ident[:]),
                 r=[f"yz{b}", "p1id"], w=["p1pstr"])
        S.op("act", lambda h, b=b: h.copy(out=yzT[b][:], in_=ps_tr[:, 0:512].rearrange("p (c n) -> p c n", c=4)), r=["p1pstr"], w=[f"yzT{b}"])
        for hf in range(2):
            psy = ps_q if hf == 0 else ps_z
            nm = "ps_q" if hf == 0 else "ps_z"
            for c in range(4):
                S.op("pe", lambda h, hf=hf, c=c, b=b, psy=psy: h.matmul(psy[:, :], lhsT=yzT[b][:, c, :], rhs=wo[:, c, hf * 512:(hf + 1) * 512],
                                                                       start=(c == 0), stop=(c == 3)), r=[f"yzT{b}", "wo"], w=[nm])
            if hf == 0:
                S.op("act", lambda h, b=b, psy=psy: h.copy(out=pt[b][:, 0:512], in_=psy[:, :]), r=[nm], w=[f"pt{b}"])
            else:
                S.op("dve", lambda h, b=b, psy=psy: h.tensor_copy(out=pt[b][:, 512:1024], in_=psy[:, :]), r=[nm], w=[f"pt{b}"])
        S.op("sp", lambda h, b=b, tq=tq: h.dma_start(out=p_out[tq, :], in_=pt[b][:]), r=[f"pt{b}"], dma=True)
    return nc, es, S


def prep_C(z, half, T):
    NT = T // 128
    d = {}
    w_in = z['c_w_in'][0]
    d['wq'] = np.ascontiguousarray(w_in[:, half * 512:(half + 1) * 512])
    kv = []
    for s in range(6):
        base = 1024 + s * 256 + half * 128
        kv.append(w_in[:, base:base + 128])
    d['wkv'] = np.ascontiguousarray(np.stack(kv, 1))
    gcols = np.concatenate([2560 + j * 16 + half * 8 + np.arange(8) for j in range(3)])
    d['wg'] = np.ascontiguousarray(w_in[:, gcols])
    d['wz'] = np.ascontiguousarray(w_in[:, 2608 + half * 512:2608 + (half + 1) * 512])
    d['wo'] = np.ascontiguousarray(z['c_w_out'][0][half * 512:(half + 1) * 512, :])
    w1 = np.stack([z['c_cmp_k_w1'][0], z['c_cmp_v_w1'][0]])
    d['w1'] = np.ascontiguousarray(w1.reshape(2, 32, 64, 128).transpose(0, 2, 1, 3))
    d['w2'] = np.ascontiguousarray(np.stack([z['c_cmp_k_w2'][0], z['c_cmp_v_w2'][0]]))
    d['posT'] = np.ascontiguousarray(np.stack([z['c_cmp_pos_k'][0].T, z['c_cmp_pos_v'][0].T]))
    table = z['t5_table']
    tk = np.arange(128)[:, None]
    tq = np.arange(128)[None, :]
    bias = np.zeros((128, 4, 2, 4, 128), np.float32)
    for g in range(2):
        for r in range(4):
            hh = half * 8 + g * 4 + r
            d0 = tq - tk
            bias[:, 0, g, r, :] = np.where(d0 >= 0, table[t5_bucket_np(d0), hh], NEGM)
            d1 = tq - tk + 128
            bias[:, 1, g, r, :] = table[t5_bucket_np(d1), hh]
            bias[:, 2, g, r, :] = np.where(tq < tk, table[31, hh], NEGM)
            bias[:, 3, g, r, :] = table[31, hh]
    d['biasT'] = bias.reshape(128, 4, 1024)
    j = np.arange(512)[:, None]
    F = np.where(16 * (j - 248) + 31 <= tq, 0.0, NEGM).astype(np.float32)
    d['F4'] = np.ascontiguousarray(np.tile(F, (1, 4)))
    ka = np.zeros((NT, 128, 128), np.float32)
    sblk = np.arange(64)[None, :]
    for qt in range(NT):
        t = qt * 128 + np.arange(128)[:, None]
        cur = t // 64
        forced = (sblk == 0) | (sblk == cur) | (sblk == cur - 1)
        future = sblk * 64 > t
        ka[qt, :, 0:64] = np.where(forced | future, 0.0, 1.0)
        ka[qt, :, 64:128] = np.where(forced, 1e30, np.where(future, -1e30, 0.0))
    d['keepadd'] = ka
    E = np.zeros((64, NT, 128), np.float32)
    for kt in range(NT):
        E[2 * kt, kt, 0:64] = 1.0
        E[2 * kt + 1, kt, 64:128] = 1.0
    d['E'] = E
    n = np.arange(256)[:, None]
    s = np.arange(64)[None, :]
    ov = ((16 * n < 64 * s + 64) & (16 * n + 31 >= 64 * s)).astype(np.float32)
    d['ovl'] = np.ascontiguousarray(ov.reshape(2, 128, 64).transpose(1, 0, 2))
    d['g'] = z['norm_g'][2:3].copy()
    d['ident'] = np.eye(128, dtype=np.float32)
    return d


NBLK = 8
BW = 80


def build_D(nsrc=1, TCH=1024, debug=False):
    T = CFG.T
    nc = get_nc()
    es = ExitStack()
    S = get_sched(nc, es)
    srcs = [dram_in(nc, f"xin{k}", [T, D]) for k in range(nsrc)]
    xs_out = dram_out(nc, "xs", [T, D]) if nsrc > 1 else None
    g_row = dram_in(nc, "g", [1, D])
    ident_d = dram_in(nc, "ident", [128, 128])
    wu_d = dram_in(nc, "wu", [D, NBLK * BW])
    wz_d = dram_in(nc, "wz", [D, NBLK * BW])
    wo_d = dram_in(nc, "wo", [NBLK * BW, D])
    ga_d = dram_in(nc, "ga", [NBLK, BW, BW])
    gx_d = dram_in(nc, "gx", [NBLK, BW, BW])
    vec_d = dram_in(nc, "vecs", [BW, NBLK, 8])
    p_out = dram_out(nc, "p", [T, D])

    xnT = mk(nc, es, "xnT", [128, 8, T], BF16)
    ps = [mk(nc, es, f"ps{k}", [128, 512], F32, psum=True) for k in range(7)]
    ps_tr = mk(nc, es, "ps_tr", [128, 1024], BF16, psum=True)
    phase1(S, nc, es, srcs, xs_out, g_row, ident_d, ps_tr, xnT=xnT)

    wu = mk(nc, es, "wu_s", [128, 8, NBLK * BW], BF16)
    wz = mk(nc, es, "wz_s", [128, 8, NBLK * BW], BF16)
    wo = mk(nc, es, "wo_s", [BW, NBLK, D], BF16)
    ga = mk(nc, es, "ga_s", [BW, NBLK, BW], F32)
    gx = mk(nc, es, "gx_s", [BW, NBLK, BW], F32)
    vec = mk(nc, es, "vec_s", [BW, NBLK, 8], F32)
    der = mk(nc, es, "der_s", [BW, NBLK, 4], F32)
    S.op("pool", lambda h: h.dma_start(out=wu[:], in_=wu_d.rearrange("(c p) n -> p c n", p=128)), w=["wu"], dma=True)
    S.op("pool", lambda h: h.dma_start(out=wz[:], in_=wz_d.rearrange("(c p) n -> p c n", p=128)), w=["wz"], dma=True)
    S.op("pool", lambda h: h.dma_start(out=wo[:], in_=wo_d.rearrange("(b p) n -> p b n", p=BW)), w=["wo"], dma=True)
    S.op("sp", lambda h: h.dma_start(out=ga[:], in_=ga_d.rearrange("b p n -> p b n")), w=["ga"], dma=True)
    S.op("sp", lambda h: h.dma_start(out=gx[:], in_=gx_d.rearrange("b p n -> p b n")), w=["gx"], dma=True)
    S.op("sp", lambda h: h.dma_start(out=vec[:], in_=vec_d), w=["vec"], dma=True)
    S.op("act", lambda h: h.activation(out=der[:, :, 0:1], in_=vec[:, :, 7:8], func=AF.Exp, scale=-1.0), r=["vec"], w=["der"])
    S.op("act", lambda h: h.activation(out=der[:, :, 1:2], in_=der[:, :, 0:1], func=AF.Ln, bias=1.0), r=["der"], w=["der"])
    S.op("act", lambda h: h.mul(out=der[:, :, 2:3], in_=der[:, :, 1:2], mul=-8.0), r=["der"], w=["der"])

    NW = 2
    def wt(nm, cols=TCH, dt=F32):
        return [mk(nc, es, f"{nm}{b}", [BW, cols], dt) for b in range(NW)]
    u_t = wt("u_t", TCH + 3)
    uc_t = wt("uc_t"); zs_t = wt("zs_t"); r_t = wt("r_t"); i_t = wt("i_t"); a_t = r_t; m_t = [mk(nc, es, "m_t0", [BW, TCH], F32)] * NW; h_t = uc_t
    hz = mk(nc, es, "hz", [BW, NBLK, TCH], BF16)
    hlast = mk(nc, es, "hlast", [BW, NBLK], F32)
    uhalo = mk(nc, es, "uhalo", [BW, NBLK, 3], F32)
    pt = [mk(nc, es, "pt0", [128, D], F32)] * 2
    S.op("dve", lambda h: h.memset(hlast[:], 0.0), w=["hlast"])
    for b in range(NW):
        S.op("dve", lambda h, b=b: h.memset(u_t[b][:, 0:3], 0.0), w=[f"u{b}"])
    it = 0
    for tch in range(T // TCH):
        t0 = tch * TCH
        for blk in range(NBLK):
            b = it % NW
            pb = (it - 1) % NW
            it += 1
            cs = slice(blk * BW, (blk + 1) * BW)
            for hf in range(2):
                tk = slice(t0 + hf * 512, t0 + (hf + 1) * 512)
                for dc in range(8):
                    S.op("pe", lambda h, hf=hf, dc=dc, tk=tk, cs=cs: h.matmul(ps[hf][0:BW, :], lhsT=wu[:, dc, cs], rhs=xnT[:, dc, tk],
                                                                         start=(dc == 0), stop=(dc == 7)),
                         r=["wu", f"xnT{(t0 + hf * 512) // 512}"], w=[f"ps{hf}"])
            for hf in range(2):
                tk = slice(t0 + hf * 512, t0 + (hf + 1) * 512)
                for dc in range(8):
                    S.op("pe", lambda h, hf=hf, dc=dc, tk=tk, cs=cs: h.matmul(ps[2 + hf][0:BW, :], lhsT=wz[:, dc, cs], rhs=xnT[:, dc, tk],
                                                                         start=(dc == 0), stop=(dc == 7)),
                         r=["wz", f"xnT{(t0 + hf * 512) // 512}"], w=[f"ps{2 + hf}"])
            S.op("pool", lambda h, b=b, blk=blk: h.tensor_copy(out=u_t[b][:, 0:3], in_=uhalo[:, blk, :]), r=["uhalo%d" % blk], w=[f"u{b}"]) if tch > 0 else None
            for hf in range(2):
                S.op("act", lambda h, hf=hf, b=b: h.copy(out=u_t[b][:, 3 + hf * 512:3 + (hf + 1) * 512], in_=ps[hf][0:BW, :]),
                     r=[f"ps{hf}"], w=[f"u{b}"])
            for hf in range(2):
                S.op("act", lambda h, hf=hf, b=b: h.activation(out=zs_t[b][:, hf * 512:(hf + 1) * 512], in_=ps[2 + hf][0:BW, :], func=AF.Silu),
                     r=[f"ps{2 + hf}"], w=[f"zs{b}"])
            S.op("pool", lambda h, b=b, blk=blk: h.tensor_copy(out=uhalo[:, blk, :], in_=u_t[b][:, TCH:TCH + 3]), r=[f"u{b}"], w=["uhalo%d" % blk])
            if debug and tch == 0 and blk == 0:
                dbg(S, nc, "xnT", xnT[:, :, 0:512], [128, 8, 512], ["xnT0"], BF16)
                dbg(S, nc, "u", u_t[b][:], [BW, TCH + 3], [f"u{b}"])
                dbg(S, nc, "zs", zs_t[b][:], [BW, TCH], [f"zs{b}"])
            S.op("dve", lambda h, b=b, blk=blk: h.tensor_scalar(out=uc_t[b][:], in0=u_t[b][:, 3:3 + TCH], scalar1=vec[:, blk, 3:4], scalar2=vec[:, blk, 4:5],
                                                               op0=ALU.mult, op1=ALU.add), r=[f"u{b}", "vec"], w=[f"uc{b}"])
            for j in range(3):
                S.op("dve", lambda h, b=b, blk=blk, j=j: h.scalar_tensor_tensor(out=uc_t[b][:], in0=u_t[b][:, j:j + TCH], scalar=vec[:, blk, j:j + 1],
                                                                               in1=uc_t[b][:], op0=ALU.mult, op1=ALU.add),
                     r=[f"u{b}", "vec"], w=[f"uc{b}"])
            for hf in range(2):
                S.op("pe", lambda h, hf=hf, b=b, blk=blk: h.matmul(ps[4][0:BW, :] if hf == 0 else ps[5][0:BW, :], lhsT=ga[:, blk, :],
                                                                  rhs=uc_t[b][:, hf * 512:(hf + 1) * 512], start=True, stop=True),
                     r=["ga", f"uc{b}"], w=[f"ps{4 + hf}"])
                S.op("act", lambda h, hf=hf, b=b, blk=blk: h.activation(out=r_t[b][:, hf * 512:(hf + 1) * 512], in_=ps[4 + hf][0:BW, :], func=AF.Sigmoid,
                                                                       bias=vec[:, blk, 5:6]), r=[f"ps{4 + hf}", "vec"], w=[f"r{b}"])
            for hf in range(2):
                S.op("pe", lambda h, hf=hf, b=b, blk=blk: h.matmul(ps[4 + hf][0:BW, :], lhsT=gx[:, blk, :],
                                                                  rhs=uc_t[b][:, hf * 512:(hf + 1) * 512], start=True, stop=True),
                     r=["gx", f"uc{b}"], w=[f"ps{4 + hf}"])
                S.op("act", lambda h, hf=hf, b=b, blk=blk: h.activation(out=i_t[b][:, hf * 512:(hf + 1) * 512], in_=ps[4 + hf][0:BW, :], func=AF.Sigmoid,
                                                                       bias=vec[:, blk, 6:7]), r=[f"ps{4 + hf}", "vec"], w=[f"i{b}"])
            if debug and tch == 0 and blk == 0:
                dbg(S, nc, "uc", uc_t[b][:], [BW, TCH], [f"uc{b}"])
                dbg(S, nc, "r", r_t[b][:], [BW, TCH], [f"r{b}"])
                dbg(S, nc, "i", i_t[b][:], [BW, TCH], [f"i{b}"])
                dbg(S, nc, "der", der[:], [BW, NBLK, 4], ["der"])
            S.op("act", lambda h, b=b, blk=blk: h.activation(out=a_t[b][:], in_=r_t[b][:], func=AF.Exp, scale=der[:, blk, 2:3]),
                 r=[f"r{b}", "der"], w=[f"r{b}"])
            S.op("pool", lambda h, b=b: h.tensor_tensor(out=m_t[b][:], in0=a_t[b][:], in1=a_t[b][:], op=ALU.mult), r=[f"r{b}"], w=["m0"])
            S.op("act", lambda h, b=b: h.activation(out=m_t[b][:], in_=m_t[b][:], func=AF.Sqrt, scale=-1.0, bias=1.0), r=["m0"], w=["m0"])
            S.op("pool", lambda h, b=b: h.tensor_tensor(out=i_t[b][:], in0=i_t[b][:], in1=uc_t[b][:], op=ALU.mult), r=[f"i{b}", f"uc{b}"], w=[f"i{b}"])
            S.op("pool", lambda h, b=b: h.tensor_tensor(out=i_t[b][:], in0=i_t[b][:], in1=m_t[b][:], op=ALU.mult), r=[f"i{b}", "m0"], w=[f"i{b}"])
            if debug and tch == 0 and blk == 0:
                dbg(S, nc, "a", r_t[b][:], [BW, TCH], [f"r{b}"])
                dbg(S, nc, "m", m_t[b][:], [BW, TCH], ["m0"])
                dbg(S, nc, "bt", i_t[b][:], [BW, TCH], [f"i{b}"])
            S.op("dve", lambda h, b=b, blk=blk: h.tensor_tensor_scan(out=h_t[b][:], data0=a_t[b][:], data1=i_t[b][:], initial=hlast[:, blk:blk + 1],
                                                                    op0=ALU.mult, op1=ALU.add), r=[f"r{b}", f"i{b}", "hlast"], w=[f"uc{b}"])
            S.op("dve", lambda h, b=b, blk=blk: h.tensor_copy(out=hlast[:, blk:blk + 1], in_=h_t[b][:, TCH - 1:TCH]), r=[f"uc{b}"], w=["hlast"])
            S.op("dve", lambda h, b=b, blk=blk: h.tensor_tensor(out=hz[:, blk, :], in0=h_t[b][:], in1=zs_t[b][:], op=ALU.mult),
                 r=[f"uc{b}", f"zs{b}"], w=["hz"])
        if debug and tch == 0:
            dbg(S, nc, "hz", hz[:], [BW, NBLK, TCH], ["hz"], BF16)
        for tl in range(TCH // 128):
            pbuf = tl % 2
            for hf in range(2):
                for blk in range(NBLK):
                    S.op("pe", lambda h, tl=tl, hf=hf, blk=blk: h.matmul(ps[hf][:, :], lhsT=hz[:, blk, tl * 128:(tl + 1) * 128],
                                                                        rhs=wo[:, blk, hf * 512:(hf + 1) * 512], start=(blk == 0), stop=(blk == NBLK - 1)),
                         r=["hz", "wo"], w=[f"ps{hf}"])
                S.op("act" if hf == 0 else "dve",
                     (lambda h, hf=hf, pbuf=pbuf: h.copy(out=pt[pbuf][:, hf * 512:(hf + 1) * 512], in_=ps[hf][:, :])) if hf == 0 else
                     (lambda h, hf=hf, pbuf=pbuf: h.tensor_copy(out=pt[pbuf][:, hf * 512:(hf + 1) * 512], in_=ps[hf][:, :])),
                     r=[f"ps{hf}"], w=["pt0"])
            rows = slice(t0 + tl * 128, t0 + (tl + 1) * 128)
            S.op("sp", lambda h, pbuf=pbuf, rows=rows: h.dma_start(out=p_out[rows, :], in_=pt[pbuf][:]), r=["pt0"], dma=True)
    return nc, es, S


def build_F(ntok=2048):
    nc = get_nc()
    es = ExitStack()
    S = get_sched(nc, es)
    srcs = [dram_in(nc, f"xin{k}", [ntok, D]) for k in range(3)]
    g_row = dram_in(nc, "g", [1, D])
    out_d = dram_out(nc, "out", [ntok, D])
    g_bc = mk(nc, es, "g_bc", [128, D], F32)
    S.op("sp", lambda h: h.dma_start(out=g_bc[:], in_=g_row.partition_broadcast(128)), w=["g"], dma=True)
    NB = 2
    xt = [mk(nc, es, f"xt{b}", [128, D], F32) for b in range(NB)]
    sq = mk(nc, es, "sq", [128, D], F32)
    ot = [mk(nc, es, f"ot{b}", [128, D], F32) for b in range(NB)]
    st = [mk(nc, es, f"st{b}", [128, 4], F32) for b in range(NB)]
    for t in range(ntok // 128):
        b = t % NB
        rows = slice(t * 128, (t + 1) * 128)
        for k, src in enumerate(srcs):
            if k == 0:
                S.op("pool", lambda h, src=src, b=b, t=t: h.dma_start(out=xt[b][:], in_=src_rows(src, t)), w=[f"xt{b}"], dma=True)
            else:
                S.op("pool", lambda h, src=src, b=b, t=t: h.dma_start(out=xt[b][:], in_=src_rows(src, t), accum_op=ALU.add), r=[f"xt{b}"], w=[f"xt{b}"], dma=True)
        S.op("act", lambda h, b=b: h.activation(out=sq[:], in_=xt[b][:], func=AF.Square), r=[f"xt{b}"], w=["sq"])
        S.op("dve", lambda h, b=b: h.tensor_reduce(out=st[b][:, 0:1], in_=sq[:], axis=AX.X, op=ALU.add), r=["sq"], w=[f"st{b}"])
        S.op("act", lambda h, b=b: h.activation(out=st[b][:, 1:2], in_=st[b][:, 0:1], func=AF.Sqrt, scale=1.0 / D, bias=EPS), r=[f"st{b}"], w=[f"st{b}"])
        S.op("dve", lambda h, b=b: h.reciprocal(out=st[b][:, 2:3], in_=st[b][:, 1:2]), r=[f"st{b}"], w=[f"st{b}r"])
        S.op("dve", lambda h, b=b: h.scalar_tensor_tensor(out=ot[b][:], in0=xt[b][:], scalar=st[b][:, 2:3], in1=g_bc[:], op0=ALU.mult, op1=ALU.mult),
             r=[f"xt{b}", f"st{b}r", "g"], w=[f"ot{b}"])
        S.op("sp", lambda h, b=b, rows=rows: h.dma_start(out=out_d[rows, :], in_=ot[b][:]), r=[f"ot{b}"], dma=True)
    return nc, es, S


def prep_D(z, half):
    LW = 1280
    blks = list(range(half * 8, half * 8 + 8))
    cols = np.concatenate([np.arange(b * 80, (b + 1) * 80) for b in blks])
    w_in = z['d_w_in'][0]
    d = {}
    d['wu'] = np.ascontiguousarray(w_in[:, cols])
    d['wz'] = np.ascontiguousarray(w_in[:, LW + cols])
    d['wo'] = np.ascontiguousarray(z['d_w_out'][0][cols, :])
    d['ga'] = np.ascontiguousarray(z['d_gate_a_w'][0][blks])
    d['gx'] = np.ascontiguousarray(z['d_gate_x_w'][0][blks])
    vecs = np.zeros((80, 8, 8), np.float32)

    def fm(v):
        return v[cols].reshape(8, 80).T
    for j in range(4):
        vecs[:, :, j] = fm(z['d_conv_w'][0][j])
    vecs[:, :, 4] = fm(z['d_conv_b'][0])
    vecs[:, :, 5] = fm(z['d_gate_a_b'][0])
    vecs[:, :, 6] = fm(z['d_gate_x_b'][0])
    vecs[:, :, 7] = fm(z['d_lambda'][0])
    d['vecs'] = vecs
    d['g'] = z['norm_g'][3:4].copy()
    d['ident'] = np.eye(128, dtype=np.float32)
    return d


PAIRS = [[0, 1], [2, 3], [4, 5], [6, 7]]


def build_fused(T=4096, nlayers=4):
    CFG.T = T
    nc = bass.Bass("TRN2", target_bir_lowering=False)
    top = ExitStack()
    S = Sched(nc, top)
    CFG.nc, CFG.S = nc, S
    x_d = nc.dram_tensor("x", [T, D], F32, kind="ExternalInput").ap()
    out_d = nc.dram_tensor("out", [T, D], F32, kind="ExternalOutput").ap()
    p = [nc.dram_tensor(f"p_i{l}", [T, D], F32) for l in range(4)]
    CH = 512
    NCH = T // CH
    pg = [[nc.dram_tensor(f"pg_i{l}_{k}", [2 * CH, D], F32) for k in range(NCH)] for l in range(4)]

    def gsrc(l, rank):
        return lambda t: pg[l][t // 4].ap()[rank * CH + (t % 4) * 128:rank * CH + (t % 4 + 1) * 128, :]
    xs = [nc.dram_tensor(f"xs_i{l}", [T, D], F32) for l in range(3)]
    layers = [("A", build_A, {}), ("B", build_B, dict(MC=128)), ("C", build_C, {}), ("D", build_D, {})]
    prev_x = x_d
    for l, (nm, fn, kw) in enumerate(layers[:nlayers]):
        CFG.prefix = nm + "_"
        SB_USED[0] = 0
        ov = {"p": p[l].ap()}
        if l == 0:
            ov["xin0"] = x_d
            nsrc = 1
        else:
            ov["xin0"] = prev_x
            ov["xin1"] = gsrc(l - 1, 0)
            ov["xin2"] = gsrc(l - 1, 1)
            ov["xs"] = xs[l - 1].ap()
            nsrc = 3
        CFG.override = ov
        _, es, _ = fn(nsrc, **kw)
        for k in range(NCH):
            S.op("pool", lambda h, l=l, k=k: h.collective_compute("AllGather", ALU.bypass, replica_groups=PAIRS, ins=[p[l].ap()[k * CH:(k + 1) * CH, :].opt()],
                                                               outs=[pg[l][k].ap().opt()]), dma=True, cc=True)
        S.emit(final=False)
        S.barrier()
        es.close()
        if l > 0:
            prev_x = xs[l - 1].ap()
    CFG.prefix = "F_"
    SB_USED[0] = 0
    CFG.override = {"xin0": prev_x, "xin1": gsrc(nlayers - 1, 0), "xin2": gsrc(nlayers - 1, 1), "out": out_d}
    _, es, _ = build_F(T)
    stats = S.emit(final=True)
    CFG.nc, CFG.S, CFG.override, CFG.prefix = None, None, {}, ""
    return nc, stats


def kernel(**inputs):
    z = {k: np.ascontiguousarray(np.asarray(v, dtype=np.float32)) for k, v in inputs.items()}
    T = 4096
    x = z['x']
    B = x.shape[0]
    nc, _ = build_fused(T)
    per_half = []
    for h in range(2):
        d = {}
        for pre, pd in (("A_", prep_A(z, h)), ("B_", prep_B(z, h, MC=128)), ("C_", prep_C(z, h, T)), ("D_", prep_D(z, h))):
            for k, v in pd.items():
                d[pre + k] = v
        d["F_g"] = z['final_g'][None, :].copy()
        per_half.append(d)
    in_maps = [dict(per_half[c % 2], x=x[c // 2]) for c in range(8)]
    res = run_bass_kernel_spmd(nc, in_maps, core_ids=list(range(8)))
    out = np.stack([res.results[2 * b]['out'] for b in range(B)]).astype(np.float32)
    return out
```

```python
import numpy as np
from contextlib import ExitStack
import concourse.bass as bass
import concourse.mybir as mybir
from concourse.bass_utils import run_bass_kernel_spmd

F32 = mybir.dt.float32
BF16 = mybir.dt.bfloat16
AF = mybir.ActivationFunctionType
ALU = mybir.AluOpType
AX = mybir.AxisListType


class Buf:
    __slots__ = ("name", "lw", "rd")

    def __init__(self, name):
        self.name = name
        self.lw = None
        self.rd = {}


class Sched:
    COMPUTE = ("pe", "act", "dve", "pool")

    def __init__(self, nc, es, ndma_slots=8):
        self.nc = nc
        self.es = es
        self.ops = []
        self.bufs = {}
        self.ndma = ndma_slots
        self.handles = {"pe": nc.tensor, "act": nc.scalar, "dve": nc.vector, "pool": nc.gpsimd, "sp": nc.sync}
        self.need = []
        self.seg_dma = []
        self.last_compute = {}
        self.barrier_deps = set()
        self.pending_barrier = {}

    def buf(self, name):
        b = self.bufs.get(name)
        if b is None:
            b = Buf(name)
            self.bufs[name] = b
        return b

    def _B(self, lst):
        out = []
        for x in lst:
            if isinstance(x, str):
                out.append(self.buf(x))
            elif isinstance(x, Buf):
                out.append(x)
            elif x is None:
                continue
            else:
                out.extend(self._B(x))
        return out

    def op(self, eng, fn, r=(), w=(), dma=False, cc=False):
        i = len(self.ops)
        R = self._B(r)
        W = self._B(w)
        deps = set()
        if self.pending_barrier.get(eng):
            deps |= self.barrier_deps
            self.pending_barrier[eng] = False
        if cc:
            deps |= set(self.seg_dma)
        raw = set()
        for b in R:
            if b.lw is not None:
                deps.add(b.lw)
                raw.add(b.lw)
        for b in W:
            if b.lw is not None:
                deps.add(b.lw)
            for k, v in b.rd.items():
                deps.add(v)
        for b in W:
            b.lw = i
            b.rd = {}
        key = ("dma", i) if dma else eng
        for b in R:
            b.rd[key] = i
        self.ops.append(dict(eng=eng, fn=fn, deps=deps, dma=dma, cc=cc, raw=raw, wnames=[x for x in w if isinstance(x, str)] if cc else []))
        if dma:
            self.seg_dma.append(i)
        elif eng in self.COMPUTE:
            self.last_compute[eng] = i
        return i

    def _init_state(self):
        nc = self.nc
        self.sems = {e: self.es.enter_context(nc.semaphore("sem_" + e)) for e in self.COMPUTE}
        self.dsems = {q: [self.es.enter_context(nc.semaphore(f"dsem_{q}_{k}")) for k in range(self.ndma)] for q in ("sp", "pool")}
        self.ccsem = self.es.enter_context(nc.semaphore("sem_cc"))
        self.cccount = 0
        self.duses = {q: [0] * self.ndma for q in ("sp", "pool")}
        self.dcount = {"sp": 0, "pool": 0}
        self.cnt = {e: 0 for e in self.COMPUTE}
        self.token = []
        self.waited = {e: {} for e in self.handles}
        self.nwaits = 0
        self.emitted = 0
        self.inited = True

    def _skip(self, po, o, d):
        if po["dma"] or o["dma"] or po["eng"] != o["eng"]:
            return False
        e = o["eng"]
        if e == "pe":
            return True
        if e in ("act", "dve") and d not in o["raw"]:
            return True
        return False

    def barrier(self):
        deps = set(i for i in self.seg_dma if not self.ops[i].get("cc"))
        keep = [(i, self.ops[i].get("wnames", [])) for i in self.seg_dma if self.ops[i].get("cc")]
        for e in self.COMPUTE:
            if e in self.last_compute:
                deps.add(self.last_compute[e])
        self.barrier_deps = deps
        self.pending_barrier = {e: True for e in self.handles}
        self.seg_dma = []
        self.bufs = {}
        for i, names in keep:
            for nm in names:
                self.buf(nm).lw = i

    def emit(self, final=True):
        nc = self.nc
        ops = self.ops
        if not getattr(self, "inited", False):
            self._init_state()
        start = self.emitted
        n = len(ops)
        need = self.need
        need.extend([False] * (n - len(need)))
        for i in range(start, n):
            o = ops[i]
            for d in o["deps"]:
                po = ops[d]
                if po["dma"]:
                    continue
                if self._skip(po, o, d):
                    continue
                assert d >= start or need[d], "cross-segment dependency on an op without increment"
                need[d] = True
        lastc = {}
        for i in range(start, n):
            if not ops[i]["dma"] and ops[i]["eng"] in self.COMPUTE:
                lastc[ops[i]["eng"]] = i
        for e, i in lastc.items():
            need[i] = True
        sems, dsems, duses, dcount, cnt, token, waited = self.sems, self.dsems, self.duses, self.dcount, self.cnt, self.token, self.waited
        token.extend([None] * (n - len(token)))
        for i in range(start, n):
            o = ops[i]
            e = o["eng"]
            h = self.handles[e]
            wd = waited[e]
            reqs = {}
            for d in o["deps"]:
                po = ops[d]
                if self._skip(po, o, d):
                    continue
                sem, val, sk = token[d]
                if wd.get(sk, 0) >= val:
                    continue
                if sk not in reqs or reqs[sk][1] < val:
                    reqs[sk] = (sem, val)
            is_cc = o.get("cc", False)
            if o["dma"] and not is_cc:
                q = e
                s = dcount[q] % self.ndma
                dcount[q] += 1
                dsk = ("d", q, s)
                prev = 16 * duses[q][s]
                if prev > 0 and wd.get(dsk, 0) < prev:
                    if dsk not in reqs or reqs[dsk][1] < prev:
                        reqs[dsk] = (dsems[q][s], prev)
            for rk, (rsem, rval) in reqs.items():
                h.wait_ge(rsem, rval)
                wd[rk] = rval
                self.nwaits += 1
            ins = o["fn"](h)
            if is_cc:
                self.cccount += 1
                ins.then_inc(self.ccsem, 1)
                token[i] = (self.ccsem, self.cccount, ("cc",))
            elif o["dma"]:
                duses[q][s] += 1
                ins.then_inc(dsems[q][s], 16)
                token[i] = (dsems[q][s], 16 * duses[q][s], dsk)
            else:
                if need[i]:
                    cnt[e] += 1
                    ins.then_inc(sems[e], 1)
                    token[i] = (sems[e], cnt[e], ("c", e))
                else:
                    token[i] = (sems[e], cnt[e] + 0, ("c", e))
            o["fn"] = None
        self.emitted = n
        if final:
            h = self.handles["sp"]
            for q in ("sp", "pool"):
                for s in range(self.ndma):
                    if duses[q][s] > 0:
                        h.wait_ge(dsems[q][s], 16 * duses[q][s])
            if self.cccount:
                h.wait_ge(self.ccsem, self.cccount)
        self.stats = dict(nops=len(ops), nwaits=self.nwaits, incs=dict(cnt))
        return self.stats


class Stream:
    def __init__(self):
        self.items = []

    def op(self, *a, **k):
        self.items.append((a, k))


def merge_streams(S, streams, chunk=1):
    idx = [0] * len(streams)
    live = True
    while live:
        live = False
        for i, st in enumerate(streams):
            for _ in range(chunk):
                if idx[i] < len(st.items):
                    a, k = st.items[idx[i]]
                    S.op(*a, **k)
                    idx[i] += 1
                    live = True


class CFG:
    T = 4096
    prefix = ""
    nc = None
    S = None
    override = {}


def get_nc():
    if CFG.nc is not None:
        return CFG.nc
    return bass.Bass("TRN2", target_bir_lowering=False)


def get_sched(nc, es):
    if CFG.S is not None:
        return CFG.S
    return Sched(nc, es)
D = 1024
EPS = 1e-6


class Ctx:
    pass


SB_USED = [0]


def mk(nc, es, name, shape, dt, psum=False):
    if not psum:
        n = 1
        for d_ in shape[1:]:
            n *= d_
        n *= (2 if dt == BF16 else 4)
        SB_USED[0] += (n + 31) // 32 * 32
        assert SB_USED[0] <= 190 * 1024, f"SBUF over budget at {name}: {SB_USED[0]}"
    if psum:
        return es.enter_context(nc.psum_tensor(CFG.prefix + name, shape, dt))
    return es.enter_context(nc.sbuf_tensor(CFG.prefix + name, shape, dt))


def src_rows(src, t):
    if callable(src):
        return src(t)
    return src[t * 128:(t + 1) * 128, :]


def src_bufs(src, t):
    if callable(src) and hasattr(src, "buf"):
        return [src.buf(t)]
    return []


def dram_in(nc, name, shape, dt=F32):
    if name in CFG.override:
        return CFG.override[name]
    return nc.dram_tensor(CFG.prefix + name, list(shape), dt, kind="ExternalInput").ap()


def dram_out(nc, name, shape, dt=F32):
    if name in CFG.override:
        return CFG.override[name]
    return nc.dram_tensor(CFG.prefix + name, list(shape), dt, kind="ExternalOutput").ap()


class P1:
    def __init__(self, S, nc, es, srcs, xs_out, g_row, ident_d, ps_tr, name="p1"):
        self.S, self.nc, self.srcs, self.xs_out, self.ps_tr, self.name = S, nc, srcs, xs_out, ps_tr, name
        self.g_bc = mk(nc, es, name + "_g", [128, D], F32)
        self.ident = mk(nc, es, name + "_id", [128, 128], BF16)
        self.identf = mk(nc, es, name + "_idf", [128, 128], F32)
        g_bc, ident, identf = self.g_bc, self.ident, self.identf
        S.op("sp", lambda h: h.dma_start(out=g_bc[:], in_=g_row.partition_broadcast(128)), w=[name + "g"], dma=True)
        S.op("sp", lambda h: h.dma_start(out=identf[:], in_=ident_d), w=[name + "idf"], dma=True)
        S.op("dve", lambda h: h.tensor_copy(out=ident[:], in_=identf[:]), r=[name + "idf"], w=[name + "id"])
        self.NB = 2
        self.xt = [mk(nc, es, f"{name}_x{b}", [128, D], F32) for b in range(self.NB)]
        self.sq = mk(nc, es, name + "_sq", [128, D], BF16)
        self.xnb = [mk(nc, es, f"{name}_xn{b}", [128, D], BF16) for b in range(self.NB)]
        self.st = [mk(nc, es, f"{name}_st{b}", [128, 4], F32) for b in range(self.NB)]

    def tile(self, t, dst_ap, dst_buf):
        S, name = self.S, self.name
        xt, sq, xnb, st, g_bc, ident, ps_tr = self.xt, self.sq, self.xnb, self.st, self.g_bc, self.ident, self.ps_tr
        b = t % self.NB
        rows = slice(t * 128, (t + 1) * 128)
        xb = f"{name}x{b}"
        for k, src in enumerate(self.srcs):
            if k == 0:
                S.op("pool", lambda h, src=src: h.dma_start(out=xt[b][:], in_=src_rows(src, t)), r=src_bufs(src, t), w=[xb], dma=True)
            else:
                S.op("pool", lambda h, src=src: h.dma_start(out=xt[b][:], in_=src_rows(src, t), accum_op=ALU.add), r=[xb] + src_bufs(src, t), w=[xb], dma=True)
        if self.xs_out is not None and len(self.srcs) > 1:
            S.op("sp", lambda h: h.dma_start(out=self.xs_out[rows, :], in_=xt[b][:]), r=[xb], dma=True)
        S.op("act", lambda h: h.activation(out=sq[:], in_=xt[b][:], func=AF.Square), r=[xb], w=[name + "sq"])
        S.op("dve", lambda h: h.tensor_reduce(out=st[b][:, 0:1], in_=sq[:], axis=AX.X, op=ALU.add), r=[name + "sq"], w=[f"{name}st{b}"])
        S.op("act", lambda h: h.activation(out=st[b][:, 1:2], in_=st[b][:, 0:1], func=AF.Sqrt, scale=1.0 / D, bias=EPS), r=[f"{name}st{b}"], w=[f"{name}st{b}"])
        S.op("dve", lambda h: h.reciprocal(out=st[b][:, 2:3], in_=st[b][:, 1:2]), r=[f"{name}st{b}"], w=[f"{name}st{b}r"])
        S.op("dve", lambda h: h.scalar_tensor_tensor(out=xnb[b][:], in0=xt[b][:], scalar=st[b][:, 2:3], in1=g_bc[:], op0=ALU.mult, op1=ALU.mult),
             r=[xb, f"{name}st{b}r", name + "g"], w=[f"{name}xn{b}"])
        for dc in range(8):
            S.op("pe", lambda h, dc=dc: h.transpose(out=ps_tr[:, dc * 128:(dc + 1) * 128], in_=xnb[b][:, dc * 128:(dc + 1) * 128], identity=ident[:]),
                 r=[f"{name}xn{b}", name + "id"], w=[name + "pstr"])
        S.op("act", lambda h: h.copy(out=dst_ap, in_=ps_tr[:].rearrange("p (c n) -> p c n", c=8)), r=[name + "pstr"], w=[dst_buf])


def phase1(S, nc, es, srcs, xs_out, g_row, ident_d, ps_tr, ntiles=None, xnT=None, name="p1"):
    if ntiles is None:
        ntiles = CFG.T // 128
    p1 = P1(S, nc, es, srcs, xs_out, g_row, ident_d, ps_tr, name)
    for t in range(ntiles):
        p1.tile(t, xnT[:, :, t * 128:(t + 1) * 128], f"xnT{t // 4}")
    return p1.ident


def dbg(S, nc, name, ap, shape, rbuf, dt=F32):
    o = nc.dram_tensor("dbg_" + name, list(shape), dt, kind="ExternalOutput").ap()
    S.op("sp", lambda h: h.dma_start(out=o, in_=ap), r=rbuf, dma=True)


NEGM = -30000.0


def build_A(nsrc=1, debug=False):
    T = CFG.T
    NT = T // 128
    nc = get_nc()
    es = ExitStack()
    S = get_sched(nc, es)
    srcs = [dram_in(nc, f"xin{k}", [T, D]) for k in range(nsrc)]
    xs_out = dram_out(nc, "xs", [T, D]) if nsrc > 1 else None
    g_row = dram_in(nc, "g", [1, D])
    ident_d = dram_in(nc, "ident", [128, 128])
    wq_d = dram_in(nc, "wq", [D, 512])
    wk_d = dram_in(nc, "wk", [D, 128])
    wv_d = dram_in(nc, "wv", [D, 128])
    wz_d = dram_in(nc, "wz", [D, 512])
    wo_d = dram_in(nc, "wo", [512, D])
    bias_d = dram_in(nc, "biasT", [128, 2, 2 * 4 * 128])
    sink_d = dram_in(nc, "sinks", [1, 8])
    p_out = dram_out(nc, "p", [T, D])

    xnT = mk(nc, es, "xnT", [128, 8, T], BF16)
    ps_tr = mk(nc, es, "ps_tr", [128, 1024], BF16, psum=True)
    ps_q = mk(nc, es, "ps_q", [128, 512], F32, psum=True)
    ps_z = mk(nc, es, "ps_z", [128, 512], F32, psum=True)
    ps_s = [mk(nc, es, f"ps_s{k}", [128, 512], F32, psum=True) for k in range(2)]
    ps_o = mk(nc, es, "ps_o", [128, 4, 128], F32, psum=True)
    ps_y = [mk(nc, es, f"ps_y{k}", [128, 512], F32, psum=True) for k in range(2)]
    ident = phase1(S, nc, es, srcs, xs_out, g_row, ident_d, ps_tr, xnT=xnT)

    wq = mk(nc, es, "wq_s", [128, 8, 512], BF16)
    wk = mk(nc, es, "wk_s", [128, 8, 128], BF16)
    wv = mk(nc, es, "wv_s", [128, 8, 128], BF16)
    wz = mk(nc, es, "wz_s", [128, 8, 512], BF16)
    wo = mk(nc, es, "wo_s", [128, 4, D], BF16)
    biasT = mk(nc, es, "biasT_s", [128, 2, 1024], BF16)
    esink = mk(nc, es, "esink", [128, 8], F32)
    for nm, t_, d_, pat in (("wq", wq, wq_d, "(c p) n -> p c n"), ("wk", wk, wk_d, "(c p) n -> p c n"), ("wv", wv, wv_d, "(c p) n -> p c n"),
                            ("wz", wz, wz_d, "(c p) n -> p c n"), ("wo", wo, wo_d, "(c p) n -> p c n")):
        S.op("pool", lambda h, t_=t_, d_=d_, pat=pat: h.dma_start(out=t_[:], in_=d_.rearrange(pat, p=128)), w=[nm], dma=True)
    S.op("pool", lambda h: h.dma_start(out=biasT[:], in_=bias_d), w=["biasT"], dma=True)
    S.op("sp", lambda h: h.dma_start(out=esink[:], in_=sink_d.partition_broadcast(128)), w=["esink"], dma=True)
    S.op("act", lambda h: h.activation(out=esink[:], in_=esink[:], func=AF.Exp), r=["esink"], w=["esink"])

    kT = mk(nc, es, "kT", [64, 2, T], BF16)
    vau = mk(nc, es, "vau", [128, NT, 2, 65], BF16)
    S.op("dve", lambda h: h.memset(vau[:, :, :, 64:65], 1.0), w=["vau_ones"])
    for g in range(2):
        for c in range(T // 512):
            tk = slice(c * 512, (c + 1) * 512)
            for dc in range(8):
                S.op("pe", lambda h, g=g, dc=dc, tk=tk: h.matmul(ps_q[0:64, :], lhsT=wk[:, dc, g * 64:(g + 1) * 64], rhs=xnT[:, dc, tk],
                                                                start=(dc == 0), stop=(dc == 7)), r=["wk", f"xnT{c}"], w=["ps_q"])
            S.op("act", lambda h, g=g, tk=tk: h.copy(out=kT[:, g, tk], in_=ps_q[0:64, :]), r=["ps_q"], w=[f"kT{c // 1}"])
    for t in range(NT):
        for dc in range(8):
            S.op("pe", lambda h, t=t, dc=dc: h.matmul(ps_z[:, 0:128], lhsT=xnT[:, dc, t * 128:(t + 1) * 128], rhs=wv[:, dc, :],
                                                      start=(dc == 0), stop=(dc == 7)), r=["wv", f"xnT{t // 4}"], w=["ps_z"])
        S.op("dve", lambda h, t=t: h.tensor_copy(out=vau[:, t, :, 0:64], in_=ps_z[:, 0:128].rearrange("p (g d) -> p g d", g=2)),
             r=["ps_z"], w=[f"vau{t}"])

    NB = 2
    qT = [mk(nc, es, f"qT{b}", [64, 2, 4, 128], BF16) for b in range(NB)]
    zs = [mk(nc, es, f"zs{b}", [128, 512], BF16) for b in range(NB)]
    pT = [mk(nc, es, f"pT{b}", [128, 512], BF16) for b in range(4)]
    yz = [mk(nc, es, f"yz{b}", [128, 512], BF16) for b in range(NB)]
    yzT = [mk(nc, es, f"yzT{b}", [128, 4, 128], BF16) for b in range(NB)]
    den = [mk(nc, es, f"den{b}", [128, 8], F32) for b in range(NB)]
    pt = [mk(nc, es, f"pt{b}", [128, D], F32) for b in range(NB)]
    pti = 0
    for qt in range(NT):
        b = qt % NB
        tq = slice(qt * 128, (qt + 1) * 128)
        xk = f"xnT{qt // 4}"
        for g in range(2):
            for r in range(4):
                for dc in range(8):
                    col = (g * 4 + r) * 64
                    S.op("pe", lambda h, g=g, r=r, dc=dc, col=col, tq=tq: h.matmul(ps_q[0:64, r * 128:(r + 1) * 128], lhsT=wq[:, dc, col:col + 64],
                                                                               rhs=xnT[:, dc, tq], start=(dc == 0), stop=(dc == 7)),
                         r=["wq", xk], w=["ps_q"])
            S.op("act", lambda h, g=g, b=b: h.activation(out=qT[b][:, g, :, :], in_=ps_q[0:64, :].rearrange("p (r n) -> p r n", r=4),
                                                        func=AF.Copy, scale=0.125), r=["ps_q"], w=[f"qT{b}_{g}"])
        for dc in range(8):
            S.op("pe", lambda h, dc=dc, tq=tq: h.matmul(ps_z[:, :], lhsT=xnT[:, dc, tq], rhs=wz[:, dc, :], start=(dc == 0), stop=(dc == 7)),
                 r=["wz", xk], w=["ps_z"])
        S.op("act", lambda h, b=b: h.activation(out=zs[b][:], in_=ps_z[:, :], func=AF.Silu), r=["ps_z"], w=[f"zs{b}"])
        for g in range(2):
            kts = [kt for kt in (qt - 1, qt) if kt >= 0]
            for kt in kts:
                cls = 0 if kt == qt else 1
                si = kt % 2
                pi = (g * 2 + si)
                S.op("pe", lambda h, g=g, kt=kt, si=si, b=b: h.matmul(ps_s[si][:, :], lhsT=kT[:, g, kt * 128:(kt + 1) * 128],
                                                                     rhs=qT[b][:, g, :, :].rearrange("p r n -> p (r n)"), start=True, stop=False),
                     r=[f"kT{kt // 4}", f"qT{b}_{g}"], w=[f"ps_s{si}"])
                S.op("pe", lambda h, g=g, cls=cls, si=si: h.matmul(ps_s[si][:, :], lhsT=ident[:], rhs=biasT[:, cls, g * 512:(g + 1) * 512],
                                                                  start=False, stop=True), r=["biasT", "p1id"], w=[f"ps_s{si}"])
                S.op("act", lambda h, si=si, pi=pi: h.activation(out=pT[pi][:], in_=ps_s[si][:, :], func=AF.Exp), r=[f"ps_s{si}"], w=[f"pT{pi}"])
            for r in range(4):
                for j, kt in enumerate(kts):
                    pi = (g * 2 + kt % 2)
                    S.op("pe", lambda h, g=g, r=r, kt=kt, pi=pi, j=j: h.matmul(ps_o[:, r, 0:65], lhsT=pT[pi][:, r * 128:(r + 1) * 128],
                                                                             rhs=vau[:, kt, g, :], start=(j == 0), stop=(j == len(kts) - 1)),
                         r=[f"pT{pi}", f"vau{kt}", "vau_ones"], w=["ps_o"])
            S.op("dve", lambda h, g=g, b=b: h.tensor_tensor(out=den[b][:, g * 4:(g + 1) * 4], in0=ps_o[:, :, 64], in1=esink[:, g * 4:(g + 1) * 4], op=ALU.add),
                 r=["ps_o", "esink"], w=[f"den{b}"])
            S.op("dve", lambda h, g=g, b=b: h.reciprocal(out=den[b][:, g * 4:(g + 1) * 4], in_=den[b][:, g * 4:(g + 1) * 4]), r=[f"den{b}"], w=[f"den{b}"])
            for r in range(4):
                col = (g * 4 + r) * 64
                S.op("dve", lambda h, g=g, r=r, b=b, col=col: h.scalar_tensor_tensor(out=yz[b][:, col:col + 64], in0=ps_o[:, r, 0:64],
                                                                                   scalar=den[b][:, g * 4 + r:g * 4 + r + 1], in1=zs[b][:, col:col + 64],
                                                                                   op0=ALU.mult, op1=ALU.mult),
                     r=["ps_o", f"den{b}", f"zs{b}"], w=[f"yz{b}"])
        for c in range(4):
            S.op("pe", lambda h, c=c, b=b: h.transpose(out=ps_tr[:, c * 128:(c + 1) * 128], in_=yz[b][:, c * 128:(c + 1) * 128], identity=ident[:]),
                 r=[f"yz{b}", "p1id"], w=["p1pstr"])
        S.op("act", lambda h, b=b: h.copy(out=yzT[b][:], in_=ps_tr[:, 0:512].rearrange("p (c n) -> p c n", c=4)), r=["p1pstr"], w=[f"yzT{b}"])
        for hf in range(2):
            for c in range(4):
                S.op("pe", lambda h, hf=hf, c=c, b=b: h.matmul(ps_y[hf][:, :], lhsT=yzT[b][:, c, :], rhs=wo[:, c, hf * 512:(hf + 1) * 512],
                                                              start=(c == 0), stop=(c == 3)), r=[f"yzT{b}", "wo"], w=[f"ps_y{hf}"])
            if hf == 0:
                S.op("act", lambda h, b=b: h.copy(out=pt[b][:, 0:512], in_=ps_y[0][:, :]), r=["ps_y0"], w=[f"pt{b}"])
            else:
                S.op("dve", lambda h, b=b: h.tensor_copy(out=pt[b][:, 512:1024], in_=ps_y[1][:, :]), r=["ps_y1"], w=[f"pt{b}"])
        S.op("sp", lambda h, b=b, tq=tq: h.dma_start(out=p_out[tq, :], in_=pt[b][:]), r=[f"pt{b}"], dma=True)
    return nc, es, S


def t5_bucket_np(d):
    import math
    d = np.maximum(d, 0)
    df = np.maximum(d, 1).astype(np.float32)
    large = 16 + (np.log(df / 16) / math.log(128 / 16) * 16).astype(np.int32)
    large = np.minimum(large, 31)
    return np.where(d < 16, d, large)


def prep_A(z, half):
    d = {}
    w_in = z['a_w_in'][0]
    d['wq'] = np.ascontiguousarray(w_in[:, half * 512:(half + 1) * 512])
    d['wk'] = np.ascontiguousarray(w_in[:, 1024 + half * 128:1024 + (half + 1) * 128])
    d['wv'] = np.ascontiguousarray(w_in[:, 1280 + half * 128:1280 + (half + 1) * 128])
    d['wz'] = np.ascontiguousarray(w_in[:, 1536 + half * 512:1536 + (half + 1) * 512])
    d['wo'] = np.ascontiguousarray(z['a_w_out'][0][half * 512:(half + 1) * 512, :])
    d['sinks'] = np.ascontiguousarray(z['a_sinks'][0][half * 8:(half + 1) * 8][None, :])
    table = z['t5_table']
    tk = np.arange(128)[:, None]
    tq = np.arange(128)[None, :]
    bias = np.zeros((128, 2, 2, 4, 128), np.float32)
    for cls in range(2):
        dist = tq - tk + 128 * cls
        valid = (dist >= 0) & (dist < 128)
        bk = t5_bucket_np(dist)
        for g in range(2):
            for r in range(4):
                hh = half * 8 + g * 4 + r
                bias[:, cls, g, r, :] = np.where(valid, table[bk, hh], NEGM)
    d['biasT'] = bias.reshape(128, 2, 1024)
    d['g'] = z['norm_g'][0:1].copy()
    d['ident'] = np.eye(128, dtype=np.float32)
    return d


KAP = 0.6065306597126334
GN_EPS = 64e-5


class _Stop(Exception):
    pass


def build_B(nsrc=1, MC=256, debug=False, stage=99):
    try:
        return _build_B(nsrc, MC, debug, stage)
    except _Stop as e:
        return e.args[0]


def _build_B(nsrc=1, MC=256, debug=False, stage=99):
    T = CFG.T
    NJ = MC // 64
    NMC = T // MC
    nc = get_nc()
    es = ExitStack()
    S = get_sched(nc, es)
    srcs = [dram_in(nc, f"xin{k}", [T, D]) for k in range(nsrc)]
    xs_out = dram_out(nc, "xs", [T, D]) if nsrc > 1 else None
    g_row = dram_in(nc, "g", [1, D])
    ident_d = dram_in(nc, "ident", [128, 128])
    w4_d = dram_in(nc, "w4", [4, D, 512])
    lw_d = dram_in(nc, "lw", [2, D, 64])
    l2_d = dram_in(nc, "l2", [2, 64, 512])
    wo_d = dram_in(nc, "wo", [512, D])
    mu_d = dram_in(nc, "muT", [128, 6, 8])
    vec_d = dram_in(nc, "vecs", [64, 8, 8])
    lnw_d = dram_in(nc, "lnw", [1, 512])
    lnb_d = dram_in(nc, "lnb", [1, 512])
    mg_d = dram_in(nc, "maskG", [128, 128])
    mnt_d = dram_in(nc, "maskNT", [64, 64])
    rm_d = dram_in(nc, "resetm", [64, MC])
    p_out = dram_out(nc, "p", [T, D])

    ps_tr = mk(nc, es, "ps_tr", [128, 1024], BF16, psum=True)
    ps_proj = mk(nc, es, "ps_proj", [128, 512], F32, psum=True)
    ps_tok = mk(nc, es, "ps_tok", [128, 512], F32, psum=True)
    ps_bv = mk(nc, es, "ps_bv", [128, 512], F32, psum=True)
    ps_g = mk(nc, es, "ps_g", [128, 512], F32, psum=True)
    ps_n = mk(nc, es, "ps_n", [128, 512], F32, psum=True)
    ps_rec = mk(nc, es, "ps_rec", [128, 512], F32, psum=True)
    ps_y = mk(nc, es, "ps_y", [128, 512], F32, psum=True)

    g_bc = mk(nc, es, "g_bc", [128, D], F32)
    identf = mk(nc, es, "identf", [128, 128], F32)
    ident = mk(nc, es, "identb", [128, 128], BF16)
    S.op("sp", lambda h: h.dma_start(out=g_bc[:], in_=g_row.partition_broadcast(128)), w=["g"], dma=True)
    S.op("sp", lambda h: h.dma_start(out=identf[:], in_=ident_d), w=["identf"], dma=True)
    S.op("dve", lambda h: h.tensor_copy(out=ident[:], in_=identf[:]), r=["identf"], w=["ident"])
    W4 = mk(nc, es, "W4", [128, 4, 8, 512], BF16)
    W4m = mk(nc, es, "W4m", [128, 4, 8, 512], BF16)
    LW = mk(nc, es, "LW", [128, 2, 8, 64], BF16)
    LWm = mk(nc, es, "LWm", [128, 2, 8, 64], BF16)
    L2 = mk(nc, es, "L2", [64, 2, 512], BF16)
    wo = mk(nc, es, "wo_s", [128, 4, D], BF16)
    muT = mk(nc, es, "muT_s", [128, 6, 8], F32)
    vec = mk(nc, es, "vec_s", [64, 8, 8], F32)
    lnw = mk(nc, es, "lnw_s", [64, 512], F32)
    lnb = mk(nc, es, "lnb_s", [64, 512], F32)
    maskG = mk(nc, es, "maskG_s", [128, 128], F32)
    maskNT = mk(nc, es, "maskNT_s", [64, 64], F32)
    resetm = mk(nc, es, "resetm_s", [64, MC], F32)
    ones64 = mk(nc, es, "ones64", [64, 64], F32)
    S.op("pool", lambda h: h.dma_start(out=W4[:], in_=w4_d.rearrange("s (c p) n -> p s c n", p=128)), w=["W4"], dma=True)
    S.op("pool", lambda h: h.dma_start(out=LW[:], in_=lw_d.rearrange("s (c p) n -> p s c n", p=128)), w=["LW"], dma=True)
    S.op("pool", lambda h: h.dma_start(out=L2[:], in_=l2_d.rearrange("s k n -> k s n")), w=["L2"], dma=True)
    S.op("pool", lambda h: h.dma_start(out=wo[:], in_=wo_d.rearrange("(c p) n -> p c n", p=128)), w=["wo"], dma=True)
    S.op("sp", lambda h: h.dma_start(out=muT[:], in_=mu_d), w=["muT"], dma=True)
    S.op("sp", lambda h: h.dma_start(out=vec[:], in_=vec_d), w=["vec"], dma=True)
    S.op("sp", lambda h: h.dma_start(out=lnw[:], in_=lnw_d.partition_broadcast(64)), w=["lnw"], dma=True)
    S.op("sp", lambda h: h.dma_start(out=lnb[:], in_=lnb_d.partition_broadcast(64)), w=["lnb"], dma=True)
    S.op("sp", lambda h: h.dma_start(out=maskG[:], in_=mg_d), w=["maskG"], dma=True)
    S.op("sp", lambda h: h.dma_start(out=maskNT[:], in_=mnt_d), w=["maskNT"], dma=True)
    S.op("sp", lambda h: h.dma_start(out=resetm[:], in_=rm_d), w=["resetm"], dma=True)
    S.op("dve", lambda h: h.memset(ones64[:], 1.0), w=["ones64"])
    for s in range(4):
        for dc in range(8):
            S.op("pool" if dc % 2 else "dve", lambda h, s=s, dc=dc: h.tensor_scalar(out=W4m[:, s, dc, :], in0=W4[:, s, dc, :], scalar1=muT[:, s, dc:dc + 1], scalar2=None, op0=ALU.mult),
                 r=["W4", "muT"], w=["W4m"])
    for s in range(2):
        for dc in range(8):
            S.op("dve", lambda h, s=s, dc=dc: h.tensor_scalar(out=LWm[:, s, dc, :], in0=LW[:, s, dc, :], scalar1=muT[:, 4 + s, dc:dc + 1], scalar2=None, op0=ALU.mult),
                 r=["LW", "muT"], w=["LWm"])

    XW = 64 + MC
    xnT = mk(nc, es, "xnT", [128, 8, XW], BF16)
    xxT = mk(nc, es, "xxT", [128, 8, XW], BF16)
    S.op("dve", lambda h: h.memset(xnT[:, :, 0:64], 0.0), w=["xnT"])
    NB = 2
    xt = [mk(nc, es, f"xt{b}", [128, D], F32) for b in range(NB)]
    sq = mk(nc, es, "sq", [128, D], BF16)
    xnb = [mk(nc, es, f"xnb{b}", [128, D], BF16) for b in range(NB)]
    st = [mk(nc, es, f"st{b}", [128, 4], F32) for b in range(NB)]
    h1T = mk(nc, es, "h1T", [64, 2, MC], BF16)
    vwin = mk(nc, es, "vwin", [64, NJ, 512], F32)
    uT = mk(nc, es, "uT", [64, NJ, 512], F32)
    zs = mk(nc, es, "zs", [64, NJ, 512], F32)
    y_all = mk(nc, es, "y_all", [64, NJ, 512], F32)
    bv_all = mk(nc, es, "bv_all", [64, NJ, 512], F32)

    def ft(nm):
        return mk(nc, es, nm, [64, MC], F32)
    r_f = ft("r_f"); k_f = ft("k_f"); sig = ft("sig"); alp = ft("alp"); kk = ft("kk"); t1 = ft("t1"); t2 = ft("t2")
    cs = ft("cs"); kmod = ft("kmod"); bal = ft("bal"); e1 = ft("e1"); e2 = ft("e2"); e3 = ft("e3"); e4 = ft("e4")
    cLs = mk(nc, es, "cLs", [64, NJ], F32)
    G2 = 3
    cLd = [mk(nc, es, f"cLd{i}", [64, NJ], F32) for i in range(G2)]
    AR = [mk(nc, es, f"AR{i}", [64, NJ, 128], F32) for i in range(G2)]
    BK = [mk(nc, es, f"BK{i}", [64, NJ, 128], F32) for i in range(G2)]
    BKe = [mk(nc, es, f"BKe{i}", [64, NJ, 128], F32) for i in range(G2)]
    Gm = [mk(nc, es, f"Gm{i}", [64, NJ, 256], F32) for i in range(G2)]
    Tm = [mk(nc, es, f"Tm{i}", [64, NJ, 64], F32) for i in range(G2)]
    BKeT = [mk(nc, es, f"BKeT{i}", [64, NJ, 128], F32) for i in range(G2)]
    dcL = [mk(nc, es, f"dcL{i}", [64, NJ, 64], F32) for i in range(G2)]
    rkrp = [mk(nc, es, f"rkrp{i}", [64, NJ, 64], F32) for i in range(G2)]
    Dg = [mk(nc, es, f"Dg{i}", [64, NJ, 64], F32) for i in range(G2)]
    bon = [mk(nc, es, f"bon{i}", [64, NJ], F32) for i in range(G2)]
    Nk = [[mk(nc, es, f"Nk{j}_{i}", [64, 64], F32) for i in range(2)] for j in range(NJ)]
    NkT = [[mk(nc, es, f"NkT{j}_{i}", [64, 64], F32) for i in range(2)] for j in range(NJ)]
    Pm = [[mk(nc, es, f"Pm{j}_{i}", [64, 64], F32) for i in range(2)] for j in range(NJ)]
    ST = [[mk(nc, es, f"ST{h}_{i}", [64, 64], F32) for i in range(2)] for h in range(8)]
    WT = [mk(nc, es, f"WT{i}", [64, 64], F32) for i in range(2)]
    for h in range(8):
        S.op("dve", lambda hh, h=h: hh.memset(ST[h][0][:], 0.0), w=[f"ST{h}_0"])
    yn = mk(nc, es, "yn", [64, 512], F32)
    gst = mk(nc, es, "gst", [64, 4, 8], F32)
    yz = mk(nc, es, "yz", [64, 512], BF16)
    yzT = mk(nc, es, "yzT", [128, 4, 64], BF16)
    pt = mk(nc, es, "pt", [64, D], F32)

    def c3(t_):
        return t_[:].rearrange("p (c j) -> p c j", j=64)

    for mc in range(NMC):
        T0 = mc * MC
        if mc > 0:
            S.op("pool", lambda h: h.tensor_copy(out=xnT[:, :, 0:64], in_=xnT[:, :, MC:MC + 64]), r=["xnT"], w=["xnT"])
        for tl in range(MC // 128):
            t = (T0 // 128) + tl
            b = t % NB
            rows = slice(t * 128, (t + 1) * 128)
            for k, src in enumerate(srcs):
                if k == 0:
                    S.op("pool", lambda h, src=src, b=b, t=t: h.dma_start(out=xt[b][:], in_=src_rows(src, t)), r=src_bufs(src, t), w=[f"xt{b}"], dma=True)
                else:
                    S.op("pool", lambda h, src=src, b=b, t=t: h.dma_start(out=xt[b][:], in_=src_rows(src, t), accum_op=ALU.add),
                         r=[f"xt{b}"] + src_bufs(src, t), w=[f"xt{b}"], dma=True)
            if xs_out is not None:
                S.op("sp", lambda h, b=b, rows=rows: h.dma_start(out=xs_out[rows, :], in_=xt[b][:]), r=[f"xt{b}"], dma=True)
            S.op("act", lambda h, b=b: h.activation(out=sq[:], in_=xt[b][:], func=AF.Square), r=[f"xt{b}"], w=["sq"])
            S.op("dve", lambda h, b=b: h.tensor_reduce(out=st[b][:, 0:1], in_=sq[:], axis=AX.X, op=ALU.add), r=["sq"], w=[f"st{b}"])
            S.op("act", lambda h, b=b: h.activation(out=st[b][:, 1:2], in_=st[b][:, 0:1], func=AF.Sqrt, scale=1.0 / D, bias=EPS), r=[f"st{b}"], w=[f"st{b}"])
            S.op("dve", lambda h, b=b: h.reciprocal(out=st[b][:, 2:3], in_=st[b][:, 1:2]), r=[f"st{b}"], w=[f"st{b}r"])
            S.op("dve", lambda h, b=b: h.scalar_tensor_tensor(out=xnb[b][:], in0=xt[b][:], scalar=st[b][:, 2:3], in1=g_bc[:], op0=ALU.mult, op1=ALU.mult),
                 r=[f"xt{b}", f"st{b}r", "g"], w=[f"xnb{b}"])
            for dc in range(8):
                S.op("pe", lambda h, dc=dc, b=b: h.transpose(out=ps_tr[:, dc * 128:(dc + 1) * 128], in_=xnb[b][:, dc * 128:(dc + 1) * 128], identity=ident[:]),
                     r=[f"xnb{b}", "ident"], w=["ps_tr"])
            S.op("act", lambda h, tl=tl: h.copy(out=xnT[:, :, 64 + tl * 128:64 + (tl + 1) * 128], in_=ps_tr[:].rearrange("p (c n) -> p c n", c=8)),
                 r=["ps_tr"], w=["xnT"])
        S.op("pool", lambda h: h.tensor_tensor(out=xxT[:, :, 1:XW], in0=xnT[:, :, 0:XW - 1], in1=xnT[:, :, 1:XW], op=ALU.subtract), r=["xnT"], w=["xxT"])
        tokc = slice(64, 64 + MC)

        def proj_fm(S, ps_ap, Wt, Wm, sidx, cols, M):
            n = 0
            for (Wx, X, xb) in ((Wt, xnT, "xnT"), (Wm, xxT, "xxT")):
                for dc in range(8):
                    S.op("pe", lambda h, Wx=Wx, X=X, dc=dc, n=n: h.matmul(ps_ap, lhsT=Wx[:, sidx, dc, cols], rhs=X[:, dc, tokc], start=(n == 0), stop=(n == 15)),
                         r=["W4", "W4m", "LW", "LWm", xb], w=["ps_proj"])
                    n += 1
        for s in range(2):
            proj_fm(S, ps_proj[0:64, 0:MC], LW, LWm, s, slice(0, 64), 64)
            S.op("act", lambda h, s=s: h.activation(out=h1T[:, s, :], in_=ps_proj[0:64, 0:MC], func=(AF.Tanh if s == 0 else AF.Copy)), r=["ps_proj"], w=["h1T"])
        for j in range(NJ):
            n = 0
            for (Wi, X, xb) in ((W4, xnT, "xnT"), (W4m, xxT, "xxT")):
                for dc in range(8):
                    S.op("pe", lambda h, Wi=Wi, X=X, dc=dc, n=n, j=j: h.matmul(ps_tok[0:64, :], lhsT=X[:, dc, 64 + j * 64:128 + j * 64], rhs=Wi[:, 2, dc, :], start=(n == 0), stop=(n == 15)),
                         r=["W4", "W4m", xb], w=["ps_tok"])
                    n += 1
            S.op("act", lambda h, j=j: h.copy(out=vwin[:, j, :], in_=ps_tok[0:64, :]), r=["ps_tok"], w=[f"vwin{j}"])
            n = 0
            for (Wi, X, xb) in ((W4, xnT, "xnT"), (W4m, xxT, "xxT")):
                for dc in range(8):
                    S.op("pe", lambda h, Wi=Wi, X=X, dc=dc, n=n, j=j: h.matmul(ps_tok[0:64, :], lhsT=X[:, dc, 64 + j * 64:128 + j * 64], rhs=Wi[:, 3, dc, :], start=(n == 0), stop=(n == 15)),
                         r=["W4", "W4m", xb], w=["ps_tok"])
                    n += 1
            S.op("act", lambda h, j=j: h.activation(out=zs[:, j, :], in_=ps_tok[0:64, :], func=AF.Silu), r=["ps_tok"], w=["zs"])

        def head_prep(S, hd):
            gi = hd % G2
            hc = slice(hd * 64, (hd + 1) * 64)
            X = Stream()
            Y = Stream()
            proj_fm(X, ps_proj[0:64, 0:MC], W4, W4m, 0, hc, 64)
            X.op("act", lambda h: h.copy(out=r_f[:], in_=ps_proj[0:64, 0:MC]), r=["ps_proj"], w=["r_f"])
            proj_fm(X, ps_proj[0:64, 0:MC], W4, W4m, 1, hc, 64)
            X.op("act", lambda h: h.copy(out=k_f[:], in_=ps_proj[0:64, 0:MC]), r=["ps_proj"], w=["k_f"])
            X.op("dve", lambda h, hd=hd: h.tensor_scalar(out=kk[:], in0=k_f[:], scalar1=vec[:, hd, 2:3], scalar2=None, op0=ALU.mult), r=["k_f", "vec"], w=["kk"])
            X.op("pool", lambda h: h.tensor_tensor(out=t1[:], in0=kk[:], in1=kk[:], op=ALU.mult), r=["kk"], w=["t1"])
            X.op("pe", lambda h: h.matmul(ps_proj[0:64, 0:MC], lhsT=ones64[:], rhs=t1[:], start=True, stop=True), r=["ones64", "t1"], w=["ps_proj"])
            X.op("act", lambda h: h.activation(out=t2[:], in_=ps_proj[0:64, 0:MC], func=AF.Sqrt), r=["ps_proj"], w=["t2"])
            X.op("dve", lambda h: h.tensor_scalar(out=t2[:], in0=t2[:], scalar1=1e-12, scalar2=None, op0=ALU.max), r=["t2"], w=["t2"])
            X.op("dve", lambda h: h.reciprocal(out=t2[:], in_=t2[:]), r=["t2"], w=["t2"])
            X.op("dve", lambda h: h.tensor_tensor(out=kk[:], in0=kk[:], in1=t2[:], op=ALU.mult), r=["kk", "t2"], w=["kk"])
            Y.op("pe", lambda h, hc=hc: h.matmul(ps_bv[0:64, 0:MC], lhsT=L2[:, 0, hc], rhs=h1T[:, 0, :], start=True, stop=True), r=["L2", "h1T"], w=["ps_bv"])
            Y.op("act", lambda h, hd=hd: h.activation(out=sig[:], in_=ps_bv[0:64, 0:MC], func=AF.Sigmoid, bias=vec[:, hd, 0:1]), r=["ps_bv", "vec"], w=["sig"])
            Y.op("pe", lambda h, hc=hc: h.matmul(ps_bv[0:64, 0:MC], lhsT=L2[:, 1, hc], rhs=h1T[:, 1, :], start=True, stop=True), r=["L2", "h1T"], w=["ps_bv"])
            Y.op("act", lambda h, hd=hd: h.activation(out=alp[:], in_=ps_bv[0:64, 0:MC], func=AF.Sigmoid, bias=vec[:, hd, 1:2]), r=["ps_bv", "vec"], w=["alp"])
            Y.op("dve", lambda h: h.tensor_tensor_scan(out=cs[:], data0=resetm[:], data1=sig[:], initial=0.0, op0=ALU.mult, op1=ALU.add), r=["resetm", "sig"], w=["cs"])
            Y.op("dve", lambda h: h.tensor_copy(out=cLs[:], in_=cs[:, 63::64]), r=["cs"], w=["cLs"])
            Y.op("act", lambda h, gi=gi: h.activation(out=cLd[gi][:], in_=cLs[:], func=AF.Exp, scale=-KAP), r=["cLs"], w=[f"cLd{gi}"])
            Y.op("act", lambda h: h.activation(out=e1[:], in_=cs[:], func=AF.Exp, scale=-KAP), r=["cs"], w=["e1"])
            Y.op("act", lambda h: h.activation(out=e2[:], in_=cs[:], func=AF.Exp, scale=KAP), r=["cs"], w=["e2"])
            Y.op("pool", lambda h: h.tensor_tensor(out=e3[:], in0=cs[:], in1=sig[:], op=ALU.subtract), r=["cs", "sig"], w=["e3"])
            Y.op("act", lambda h: h.activation(out=e3[:], in_=e3[:], func=AF.Exp, scale=-KAP), r=["e3"], w=["e3"])
            Y.op("dve", lambda h: h.tensor_tensor(out=c3(e4), in0=c3(cs), in1=cLs[:].unsqueeze(2).broadcast_to([64, NJ, 64]), op=ALU.subtract), r=["cs", "cLs"], w=["e4"])
            Y.op("act", lambda h: h.activation(out=e4[:], in_=e4[:], func=AF.Exp, scale=KAP), r=["e4"], w=["e4"])
            merge_streams(S, [X, Y])
            S.op("dve", lambda h, hd=hd: h.tensor_scalar(out=t1[:], in0=alp[:], scalar1=1.0, scalar2=vec[:, hd, 3:4], op0=ALU.subtract, op1=ALU.mult), r=["alp", "vec"], w=["t1"])
            S.op("dve", lambda h: h.scalar_tensor_tensor(out=kmod[:], in0=t1[:], scalar=1.0, in1=k_f[:], op0=ALU.add, op1=ALU.mult), r=["t1", "k_f"], w=["kmod"])
            S.op("pool", lambda h: h.tensor_tensor(out=bal[:], in0=kk[:], in1=alp[:], op=ALU.mult), r=["kk", "alp"], w=["bal"])
            S.op("pool", lambda h, gi=gi: h.tensor_tensor(out=AR[gi][:, :, 64:128], in0=c3(r_f), in1=c3(e1), op=ALU.mult), r=["r_f", "e1"], w=[f"AR{gi}"])
            S.op("dve", lambda h, gi=gi: h.scalar_tensor_tensor(out=AR[gi][:, :, 0:64], in0=c3(kk), scalar=-1.0, in1=c3(e3), op0=ALU.mult, op1=ALU.mult),
                 r=["kk", "e3"], w=[f"AR{gi}"])
            S.op("dve", lambda h, gi=gi: h.tensor_tensor(out=BK[gi][:, :, 0:64], in0=c3(bal), in1=c3(e2), op=ALU.mult), r=["bal", "e2"], w=[f"BK{gi}"])
            S.op("pool", lambda h, gi=gi: h.tensor_tensor(out=BK[gi][:, :, 64:128], in0=c3(kmod), in1=c3(e2), op=ALU.mult), r=["kmod", "e2"], w=[f"BK{gi}"])
            S.op("dve", lambda h, gi=gi: h.tensor_tensor(out=BKe[gi][:, :, 0:64], in0=c3(bal), in1=c3(e4), op=ALU.mult), r=["bal", "e4"], w=[f"BKe{gi}"])
            S.op("pool", lambda h, gi=gi: h.tensor_tensor(out=BKe[gi][:, :, 64:128], in0=c3(kmod), in1=c3(e4), op=ALU.mult), r=["kmod", "e4"], w=[f"BKe{gi}"])
            S.op("dve", lambda h, gi=gi, hd=hd: h.scalar_tensor_tensor(out=rkrp[gi][:, :, :], in0=c3(r_f), scalar=vec[:, hd, 4:5], in1=c3(kmod), op0=ALU.mult, op1=ALU.mult),
                 r=["r_f", "kmod", "vec"], w=[f"rkrp{gi}"])
        def head_gn(S, hd):
            gi = hd % G2
            hc = slice(hd * 64, (hd + 1) * 64)
            def head_g(S, j):
                psn = ps_n if j == 0 else ps_tok
                psn_name = "ps_n" if j == 0 else "ps_tok"
                Nk_, NkT_, Pm_ = Nk[j], NkT[j], Pm[j]
                S.op("pe", lambda h, gi=gi, j=j: h.matmul(ps_g[0:64, 0:128], lhsT=BK[gi][:, j, 0:64], rhs=AR[gi][:, j, :], start=True, stop=True), r=[f"BK{gi}", f"AR{gi}"], w=["ps_g"])
                S.op("pe", lambda h, gi=gi, j=j: h.matmul(ps_g[0:64, 128:256], lhsT=BK[gi][:, j, 64:128], rhs=AR[gi][:, j, :], start=True, stop=True), r=[f"BK{gi}", f"AR{gi}"], w=["ps_g"])
                S.op("pe", lambda h, gi=gi, j=j: h.matmul(ps_g[0:64, 256:320], lhsT=AR[gi][:, j, 0:64], rhs=BK[gi][:, j, 0:64], start=True, stop=True), r=[f"BK{gi}", f"AR{gi}"], w=["ps_g"])
                S.op("dve", lambda h, gi=gi, j=j: h.tensor_tensor(out=Gm[gi][:, j, 0:128], in0=ps_g[0:64, 0:128], in1=maskG[0:64, :], op=ALU.mult), r=["ps_g", "maskG"], w=[f"Gm{gi}"])
                S.op("dve", lambda h, gi=gi, j=j: h.tensor_tensor(out=Gm[gi][:, j, 128:256], in0=ps_g[0:64, 128:256], in1=maskG[0:64, :], op=ALU.mult), r=["ps_g", "maskG"], w=[f"Gm{gi}"])
                S.op("dve", lambda h: h.tensor_tensor(out=NkT_[0][:], in0=ps_g[0:64, 256:320], in1=maskNT[:], op=ALU.mult), r=["ps_g", "maskNT"], w=[f"NkT{j}_0"])
                S.op("pool", lambda h, gi=gi, j=j: h.tensor_copy(out=Nk_[0][:], in_=Gm[gi][:, j, 0:64]), r=[f"Gm{gi}"], w=[f"Nk{j}_0"])
                S.op("pool", lambda h, gi=gi, j=j: h.tensor_tensor(out=Pm_[0][:], in0=Gm[gi][:, j, 0:64], in1=identf[0:64, 0:64], op=ALU.add), r=[f"Gm{gi}", "identf"], w=[f"Pm{j}_0"])
                for q in range(2):
                    S.op("pe", lambda h, gi=gi, j=j, q=q: h.transpose(out=ps_g[0:64, 320 + q * 64:384 + q * 64], in_=BKe[gi][:, j, q * 64:(q + 1) * 64], identity=identf[0:64, 0:64]), r=[f"BKe{gi}", "identf"], w=["ps_g"])
                S.op("act", lambda h, gi=gi, j=j: h.copy(out=BKeT[gi][:, j, :], in_=ps_g[0:64, 320:448]), r=["ps_g"], w=[f"BKeT{gi}"])
                S.op("pool", lambda h, gi=gi, j=j: h.tensor_scalar(out=dcL[gi][:, j, :], in0=identf[0:64, 0:64], scalar1=cLd[gi][:, j:j + 1], scalar2=None, op0=ALU.mult), r=["identf", f"cLd{gi}"], w=[f"dcL{gi}"])
                S.op("pe", lambda h, gi=gi, j=j: h.matmul(ps_g[0:64, 448 + j:449 + j], lhsT=rkrp[gi][:, j, :], rhs=ones64[:, 0:1], start=True, stop=True), r=[f"rkrp{gi}", "ones64"], w=["ps_g"])
                S.op("act", lambda h, gi=gi, j=j: h.copy(out=bon[gi][:, j:j + 1], in_=ps_g[0:64, 448 + j:449 + j]), r=["ps_g"], w=[f"bon{gi}"])
            def head_n(S, j):
                psn = ps_n if j == 0 else ps_tok
                psn_name = "ps_n" if j == 0 else "ps_tok"
                Nk_, NkT_, Pm_ = Nk[j], NkT[j], Pm[j]
                cur = 0
                for sidx in range(5):
                    nx = 1 - cur
                    last = (sidx == 4)
                    S.op("pe", lambda h, cur=cur: h.matmul(psn[0:64, 0:64], lhsT=Nk_[cur][:], rhs=NkT_[cur][:], start=True, stop=True), r=[f"Nk{j}_{cur}", f"NkT{j}_{cur}"], w=[psn_name])
                    S.op("act", lambda h, nx=nx: h.copy(out=NkT_[nx][:], in_=psn[0:64, 0:64]), r=[psn_name], w=[f"NkT{j}_{nx}"])
                    if not last:
                        S.op("pe", lambda h, cur=cur: h.matmul(psn[0:64, 64:128], lhsT=NkT_[cur][:], rhs=Nk_[cur][:], start=True, stop=True), r=[f"Nk{j}_{cur}", f"NkT{j}_{cur}"], w=[psn_name])
                        S.op("act", lambda h, nx=nx: h.copy(out=Nk_[nx][:], in_=psn[0:64, 64:128]), r=[psn_name], w=[f"Nk{j}_{nx}"])
                    S.op("pe", lambda h, cur=cur, nx=nx: h.matmul(psn[0:64, 128:192], lhsT=NkT_[nx][:], rhs=Pm_[cur][:], start=True, stop=True), r=[f"NkT{j}_{nx}", f"Pm{j}_{cur}"], w=[psn_name])
                    if last:
                        S.op("dve", lambda h, cur=cur, gi=gi, j=j: h.tensor_tensor(out=Tm[gi][:, j, :], in0=psn[0:64, 128:192], in1=Pm_[cur][:], op=ALU.add), r=[psn_name, f"Pm{j}_{cur}"], w=[f"Tm{gi}"])
                    else:
                        S.op("dve", lambda h, cur=cur, nx=nx: h.tensor_tensor(out=Pm_[nx][:], in0=psn[0:64, 128:192], in1=Pm_[cur][:], op=ALU.add), r=[psn_name, f"Pm{j}_{cur}"], w=[f"Pm{j}_{nx}"])
                    cur = nx
            for j in range(NJ):
                head_g(S, j)
            sj = [Stream() for _ in range(NJ)]
            for j in range(NJ):
                head_n(sj[j], j)
            merge_streams(S, sj)
            for j in range(NJ):
                S.op("dve", lambda h, gi=gi, j=j: h.tensor_scalar(out=Dg[gi][:, j, :], in0=identf[0:64, 0:64], scalar1=bon[gi][:, j:j + 1], scalar2=None, op0=ALU.mult),
                     r=["identf", f"bon{gi}"], w=[f"Dg{gi}"])
        def head_rec(S, hd):
            gi = hd % G2
            hc = slice(hd * 64, (hd + 1) * 64)
            for j in range(NJ):
                gj = mc * NJ + j
                s_in = ST[hd][gj % 2]
                s_out = ST[hd][(gj + 1) % 2]
                sin_n = f"ST{hd}_{gj % 2}"
                sout_n = f"ST{hd}_{(gj + 1) % 2}"
                wi = gj % 2
                VT = vwin[:, j, hc]
                UT = uT[:, j, hc]
                vn = f"vwin{j}"
                un = f"uT{j}_{hd}"
                S.op("pe", lambda h, gi=gi, j=j, VT=VT, hc=hc: h.matmul(ps_rec[0:64, 192:256], lhsT=Dg[gi][:, j, :], rhs=VT, start=True, stop=True), r=[f"Dg{gi}", vn], w=["ps_rec"])
                S.op("act", lambda h, j=j, hc=hc: h.copy(out=bv_all[:, j, hc], in_=ps_rec[0:64, 192:256]), r=["ps_rec"], w=["bv_all"])
                S.op("pe", lambda h, gi=gi, j=j, s_in=s_in: h.matmul(ps_rec[0:64, 0:64], lhsT=AR[gi][:, j, 0:64], rhs=s_in[:], start=True, stop=False), r=[f"AR{gi}", sin_n], w=["ps_rec"])
                S.op("pe", lambda h, gi=gi, j=j, VT=VT: h.matmul(ps_rec[0:64, 0:64], lhsT=Gm[gi][:, j, 128:192], rhs=VT, start=False, stop=True), r=[f"Gm{gi}", vn], w=["ps_rec"])
                S.op("act", lambda h, wi=wi: h.copy(out=WT[wi][:], in_=ps_rec[0:64, 0:64]), r=["ps_rec"], w=[f"WT{wi}"])
                S.op("pe", lambda h, gi=gi, j=j, wi=wi: h.matmul(ps_rec[0:64, 64:128], lhsT=Tm[gi][:, j, :], rhs=WT[wi][:], start=True, stop=True), r=[f"Tm{gi}", f"WT{wi}"], w=["ps_rec"])
                S.op("act", lambda h, UT=UT: h.copy(out=UT, in_=ps_rec[0:64, 64:128]), r=["ps_rec"], w=[un])
                S.op("pe", lambda h, gi=gi, j=j, s_in=s_in, hc=hc: h.matmul(ps_y[0:64, hc], lhsT=AR[gi][:, j, 64:128], rhs=s_in[:], start=True, stop=False), r=[f"AR{gi}", sin_n], w=["ps_y"])
                S.op("pe", lambda h, gi=gi, j=j, hc=hc, UT=UT: h.matmul(ps_y[0:64, hc], lhsT=Gm[gi][:, j, 64:128], rhs=UT, start=False, stop=False), r=[f"Gm{gi}", un], w=["ps_y"])
                S.op("pe", lambda h, gi=gi, j=j, hc=hc, VT=VT: h.matmul(ps_y[0:64, hc], lhsT=Gm[gi][:, j, 192:256], rhs=VT, start=False, stop=True), r=[f"Gm{gi}", vn], w=["ps_y"])
                S.op("dve", lambda h, j=j, hc=hc: h.tensor_copy(out=y_all[:, j, hc], in_=ps_y[0:64, hc]), r=["ps_y"], w=["y_all"])
                S.op("pe", lambda h, gi=gi, j=j, s_in=s_in: h.matmul(ps_rec[0:64, 128:192], lhsT=dcL[gi][:, j, :], rhs=s_in[:], start=True, stop=False), r=[f"dcL{gi}", sin_n], w=["ps_rec"])
                S.op("pe", lambda h, gi=gi, j=j, UT=UT: h.matmul(ps_rec[0:64, 128:192], lhsT=BKeT[gi][:, j, 0:64], rhs=UT, start=False, stop=False), r=[f"BKeT{gi}", un], w=["ps_rec"])
                S.op("pe", lambda h, gi=gi, j=j, VT=VT: h.matmul(ps_rec[0:64, 128:192], lhsT=BKeT[gi][:, j, 64:128], rhs=VT, start=False, stop=True), r=[f"BKeT{gi}", vn], w=["ps_rec"])
                S.op("act", lambda h, s_out=s_out: h.copy(out=s_out[:], in_=ps_rec[0:64, 128:192]), r=["ps_rec"], w=[sout_n])
        for step in range(8 + 2):
            streams = []
            if step < 8:
                st_ = Stream()
                head_prep(st_, step)
                streams.append(st_)
            if 0 <= step - 1 < 8:
                st_ = Stream()
                head_gn(st_, step - 1)
                streams.append(st_)
            if 0 <= step - 2 < 8:
                st_ = Stream()
                head_rec(st_, step - 2)
                streams.append(st_)
            merge_streams(S, streams)
        for j in range(NJ):
            y3 = y_all[:, j, :].rearrange("p (h v) -> p h v", h=8)
            S.op("dve", lambda h, y3=y3: h.tensor_reduce(out=gst[:, 0, :], in_=y3, axis=AX.X, op=ALU.add), r=["y_all"], w=["gst"])
            S.op("act", lambda h, j=j: h.activation(out=yn[:], in_=y_all[:, j, :], func=AF.Square), r=["y_all"], w=["yn"])
            S.op("dve", lambda h: h.tensor_reduce(out=gst[:, 1, :], in_=yn[:].rearrange("p (h v) -> p h v", h=8), axis=AX.X, op=ALU.add), r=["yn"], w=["gst"])
            S.op("dve", lambda h: h.tensor_scalar(out=gst[:, 0, :], in0=gst[:, 0, :], scalar1=1.0 / 64, scalar2=None, op0=ALU.mult), r=["gst"], w=["gst"])
            S.op("dve", lambda h: h.tensor_tensor(out=gst[:, 2, :], in0=gst[:, 0, :], in1=gst[:, 0, :], op=ALU.mult), r=["gst"], w=["gst"])
            S.op("dve", lambda h: h.scalar_tensor_tensor(out=gst[:, 1, :], in0=gst[:, 1, :], scalar=1.0 / 64, in1=gst[:, 2, :], op0=ALU.mult, op1=ALU.subtract), r=["gst"], w=["gst"])
            S.op("act", lambda h: h.activation(out=gst[:, 1, :], in_=gst[:, 1, :], func=AF.Sqrt, bias=GN_EPS), r=["gst"], w=["gst"])
            S.op("dve", lambda h: h.reciprocal(out=gst[:, 1, :], in_=gst[:, 1, :]), r=["gst"], w=["gst"])
            for hd in range(8):
                hc = slice(hd * 64, (hd + 1) * 64)
                S.op("dve", lambda h, j=j, hd=hd, hc=hc: h.tensor_scalar(out=yn[:, hc], in0=y_all[:, j, hc], scalar1=gst[:, 0, hd:hd + 1], scalar2=gst[:, 1, hd:hd + 1],
                                                                      op0=ALU.subtract, op1=ALU.mult), r=["y_all", "gst"], w=["yn"])
            S.op("pool", lambda h: h.tensor_tensor(out=yn[:], in0=yn[:], in1=lnw[:], op=ALU.mult), r=["yn", "lnw"], w=["yn"])
            S.op("pool", lambda h: h.tensor_tensor(out=yn[:], in0=yn[:], in1=lnb[:], op=ALU.add), r=["yn", "lnb"], w=["yn"])
            S.op("pool", lambda h, j=j: h.tensor_tensor(out=yn[:], in0=yn[:], in1=bv_all[:, j, :], op=ALU.add), r=["yn", "bv_all"], w=["yn"])
            S.op("dve", lambda h, j=j: h.tensor_tensor(out=yz[:], in0=yn[:], in1=zs[:, j, :], op=ALU.mult), r=["yn", "zs"], w=["yz"])
            for c in range(4):
                S.op("pe", lambda h, c=c: h.transpose(out=ps_tr[:, c * 64:(c + 1) * 64], in_=yz[:, c * 128:(c + 1) * 128], identity=ident[0:64, 0:64]), r=["yz", "ident"], w=["ps_tr"])
            S.op("act", lambda h: h.copy(out=yzT[:], in_=ps_tr[:, 0:256].rearrange("p (c n) -> p c n", c=4)), r=["ps_tr"], w=["yzT"])
            for hf in range(2):
                for c in range(4):
                    S.op("pe", lambda h, hf=hf, c=c: h.matmul(ps_tok[0:64, :], lhsT=yzT[:, c, :], rhs=wo[:, c, hf * 512:(hf + 1) * 512], start=(c == 0), stop=(c == 3)), r=["yzT", "wo"], w=["ps_tok"])
                S.op("act", lambda h, hf=hf: h.copy(out=pt[:, hf * 512:(hf + 1) * 512], in_=ps_tok[0:64, :]), r=["ps_tok"], w=["pt"])
            rows = slice(T0 + j * 64, T0 + (j + 1) * 64)
            S.op("sp", lambda h, rows=rows: h.dma_start(out=p_out[rows, :], in_=pt[:]), r=["pt"], dma=True)
    return nc, es, S


def prep_B(z, half, MC=256):
    d = {}
    w_in = z['b_w_in'][0]
    own = slice(half * 512, (half + 1) * 512)
    d['w4'] = np.ascontiguousarray(np.stack([w_in[:, s * 1024:(s + 1) * 1024][:, own] for s in range(4)]))
    d['lw'] = np.ascontiguousarray(np.stack([z['b_w1'][0], z['b_a1'][0]]))
    d['l2'] = np.ascontiguousarray(np.stack([z['b_w2'][0][:, own], z['b_a2'][0][:, own]]))
    d['wo'] = np.ascontiguousarray(z['b_w_out'][0][own, :])
    mu = z['b_mu'][0]
    d['muT'] = np.ascontiguousarray(mu.reshape(6, 8, 128).transpose(2, 0, 1))
    vecs = np.zeros((64, 8, 8), np.float32)
    def fm(v):
        return v[own].reshape(8, 64).T
    vecs[:, :, 0] = fm(z['b_w0'][0]); vecs[:, :, 1] = fm(z['b_a0'][0]); vecs[:, :, 2] = fm(z['b_k_k'][0]); vecs[:, :, 3] = fm(z['b_k_a'][0])
    vecs[:, :, 4] = fm(z['b_r_k'][0].reshape(-1))
    d['vecs'] = vecs
    d['lnw'] = np.ascontiguousarray(z['b_lnx_w'][0][own][None, :])
    d['lnb'] = np.ascontiguousarray(z['b_lnx_b'][0][own][None, :])
    j = np.arange(64)[:, None]; i = np.arange(64)[None, :]
    strict = (j < i).astype(np.float32); incl = (j <= i).astype(np.float32)
    row = np.concatenate([strict, incl], 1)
    d['maskG'] = np.ascontiguousarray(np.concatenate([row, row], 0))
    d['maskNT'] = np.ascontiguousarray(strict.T)
    rm = np.ones((64, MC), np.float32); rm[:, ::64] = 0.0
    d['resetm'] = rm
    d['g'] = z['norm_g'][1:2].copy()
    d['ident'] = np.eye(128, dtype=np.float32)
    return d


NEGM = -30000.0


def build_C(nsrc=1, debug=False):
    T = CFG.T
    NT = T // 128
    NCMP = T // 16 - 1
    NKT = (NCMP + 127) // 128
    nc = get_nc()
    es = ExitStack()
    S = get_sched(nc, es)
    srcs = [dram_in(nc, f"xin{k}", [T, D]) for k in range(nsrc)]
    xs_out = dram_out(nc, "xs", [T, D]) if nsrc > 1 else None
    g_row = dram_in(nc, "g", [1, D])
    ident_d = dram_in(nc, "ident", [128, 128])
    wq_d = dram_in(nc, "wq", [D, 512])
    wkv_d = dram_in(nc, "wkv", [D, 6, 128])
    wg_d = dram_in(nc, "wg", [D, 24])
    wz_d = dram_in(nc, "wz", [D, 512])
    wo_d = dram_in(nc, "wo", [512, D])
    w1_d = dram_in(nc, "w1", [2, 64, 32, 128])
    w2_d = dram_in(nc, "w2", [2, 128, 64])
    pos_d = dram_in(nc, "posT", [2, 64, 32])
    bias_d = dram_in(nc, "biasT", [128, 4, 1024])
    F4_d = dram_in(nc, "F4", [512, 512])
    ka_d = dram_in(nc, "keepadd", [NT, 128, 128])
    E_d = dram_in(nc, "E", [64, NT, 128])
    ov_d = dram_in(nc, "ovl", [128, 2, 64])
    p_out = dram_out(nc, "p", [T, D])

    ps_tr = mk(nc, es, "ps_tr", [128, 1024], BF16, psum=True)
    ps_q = mk(nc, es, "ps_q", [128, 512], F32, psum=True)
    ps_z = mk(nc, es, "ps_z", [128, 512], F32, psum=True)
    ps_s = [mk(nc, es, f"ps_s{k}", [128, 512], F32, psum=True) for k in range(2)]
    po_c = mk(nc, es, "po_c", [128, 4, 128], F32, psum=True)
    po_s = mk(nc, es, "po_s", [128, 4, 128], F32, psum=True)
    po_w = mk(nc, es, "po_w", [128, 4, 128], F32, psum=True)
    p1 = P1(S, nc, es, srcs, xs_out, g_row, ident_d, ps_tr)
    ident = p1.ident
    xnTt = [mk(nc, es, f"xnTt{b}", [128, 8, 128], BF16) for b in range(2)]

    wq = mk(nc, es, "wq_s", [128, 8, 512], BF16)
    wkv = mk(nc, es, "wkv_s", [128, 8, 6, 128], BF16)
    wg = mk(nc, es, "wg_s", [128, 8, 24], BF16)
    wz = mk(nc, es, "wz_s", [128, 8, 512], BF16)
    wo = mk(nc, es, "wo_s", [128, 4, D], BF16)
    w1 = mk(nc, es, "w1_s", [64, 2, 32, 128], BF16)
    w2 = mk(nc, es, "w2_s", [128, 2, 64], BF16)
    posT = mk(nc, es, "posT_s", [64, 2, 32], BF16)
    biasT = mk(nc, es, "biasT_s", [128, 4, 1024], BF16)
    Em = mk(nc, es, "E_s", [64, NT, 128], BF16)
    S.op("pool", lambda h: h.dma_start(out=wq[:], in_=wq_d.rearrange("(c p) n -> p c n", p=128)), w=["wq"], dma=True)
    S.op("pool", lambda h: h.dma_start(out=wkv[:], in_=wkv_d.rearrange("(c p) s n -> p c s n", p=128)), w=["wkv"], dma=True)
    S.op("pool", lambda h: h.dma_start(out=wg[:], in_=wg_d.rearrange("(c p) n -> p c n", p=128)), w=["wg"], dma=True)
    S.op("pool", lambda h: h.dma_start(out=wz[:], in_=wz_d.rearrange("(c p) n -> p c n", p=128)), w=["wz"], dma=True)
    S.op("pool", lambda h: h.dma_start(out=wo[:], in_=wo_d.rearrange("(c p) n -> p c n", p=128)), w=["wo"], dma=True)
    S.op("pool", lambda h: h.dma_start(out=w1[:], in_=w1_d.rearrange("s d l h -> d s l h")), w=["w1"], dma=True)
    S.op("pool", lambda h: h.dma_start(out=w2[:], in_=w2_d.rearrange("s h d -> h s d")), w=["w2"], dma=True)
    S.op("pool", lambda h: h.dma_start(out=posT[:], in_=pos_d.rearrange("s d l -> d s l")), w=["posT"], dma=True)
    S.op("pool", lambda h: h.dma_start(out=biasT[:], in_=bias_d), w=["biasT"], dma=True)
    S.op("pool", lambda h: h.dma_start(out=Em[:], in_=E_d), w=["E"], dma=True)

    kvT = mk(nc, es, "kvT", [64, 2, 2, T], BF16)
    roll = mk(nc, es, "roll", [64, 2, 2, 144], BF16)
    vau = mk(nc, es, "vau", [128, NT, 2, 2, 65], BF16)
    S.op("dve", lambda h: h.memset(vau[:, :, :, :, 64:65], 1.0), w=["vau_ones"])
    S.op("dve", lambda h: h.memset(roll[:], 0.0), w=["roll"])
    kcmpT = mk(nc, es, "kcmpT", [64, 2, 256], BF16)
    vcau = mk(nc, es, "vcau", [128, 2, 2, 65], BF16)
    ovl = mk(nc, es, "ovl_s", [128, 2, 64], BF16)
    hidn = mk(nc, es, "hidn", [128, 4, 8], BF16)
    hidv = mk(nc, es, "hidv", [128, 2, 256], BF16)
    pbias = mk(nc, es, "pbias", [128, 2], F32)
    S.op("dve", lambda h: h.memset(kcmpT[:], 0.0), w=["kcmpT"])
    S.op("dve", lambda h: h.memset(vcau[:], 0.0), w=["vcau"])
    S.op("dve", lambda h: h.memset(vcau[:, :, :, 64:65], 1.0), r=["vcau"], w=["vcau"])
    S.op("dve", lambda h: h.memset(hidv[:], 0.0), w=["hidv"])
    S.op("pool", lambda h: h.dma_start(out=ovl[:], in_=ov_d), w=["ovl"], dma=True)
    for s in range(2):
        for l in range(32):
            S.op("pe", lambda h, s=s, l=l: h.matmul(ps_z[:, s:s + 1], lhsT=w1[:, s, l, :], rhs=posT[:, s, l:l + 1], start=(l == 0), stop=(l == 31)),
                 r=["w1", "posT"], w=["ps_z"])
        S.op("act", lambda h, s=s: h.copy(out=pbias[:, s:s + 1], in_=ps_z[:, s:s + 1]), r=["ps_z"], w=["pbias"])

    NB = 2
    qT = [mk(nc, es, f"qT{b}", [64, 2, 4, 128], BF16) for b in range(NB)]
    zs = [mk(nc, es, f"zs{b}", [128, 512], BF16) for b in range(NB)]
    gt = [mk(nc, es, f"gt{b}", [128, 24], F32) for b in range(NB)]
    NP = 4
    pT = [mk(nc, es, f"pT{b}", [128, 512], BF16) for b in range(NP)]
    F4t = [mk(nc, es, f"F4t{b}", [128, 512], BF16) for b in range(2)]
    ka = [mk(nc, es, f"ka{b}", [128, 128], F32) for b in range(NB)]
    imp = mk(nc, es, "imp", [128, 64], F32)
    imp2 = mk(nc, es, "imp2", [128, 64], F32)
    m8 = mk(nc, es, "m8", [128, 16], F32)
    nsel = mk(nc, es, "nsel", [128, 64], BF16)
    nselT = mk(nc, es, "nselT", [64, 4, 128], BF16)
    rden = mk(nc, es, "rden", [128, 3, 4], F32)
    cf = mk(nc, es, "cf", [128, 3, 4], F32)
    y = mk(nc, es, "y", [128, 512], F32)
    yz = [mk(nc, es, f"yz{b}", [128, 512], BF16) for b in range(NB)]
    yzT = [mk(nc, es, f"yzT{b}", [128, 4, 128], BF16) for b in range(NB)]
    pt = [mk(nc, es, f"pt{b}", [128, D], F32) for b in range(NB)]
    pcount = [0]
    scount = [0]

    def st_tile(g, b, lhsT_ap, lhs_bufs, extra, rhs_aug, rhs_bufs, po, first, last, ncol=65):
        si = scount[0] % 2
        scount[0] += 1
        pi = pcount[0] % NP
        pcount[0] += 1
        n_extra = len(extra)
        S.op("pe", lambda h: h.matmul(ps_s[si][:, :], lhsT=lhsT_ap, rhs=qT[b][:, g, :, :].rearrange("p r n -> p (r n)"), start=True, stop=(n_extra == 0)),
             r=lhs_bufs + [f"qT{b}_{g}"], w=[f"ps_s{si}"])
        for j, (el, er, ebufs) in enumerate(extra):
            S.op("pe", lambda h, el=el, er=er, j=j: h.matmul(ps_s[si][:, :], lhsT=el, rhs=er, start=False, stop=(j == n_extra - 1)),
                 r=ebufs, w=[f"ps_s{si}"])
        S.op("act", lambda h: h.activation(out=pT[pi][:], in_=ps_s[si][:, :], func=AF.Exp), r=[f"ps_s{si}"], w=[f"pT{pi}"])
        return pi

    for qt in range(NT):
        b = qt % NB
        tq = slice(qt * 128, (qt + 1) * 128)
        xk = f"xnTt{qt % 2}"
        xn = xnTt[qt % 2]
        S.op("sp", lambda h, b=b, qt=qt: h.dma_start(out=ka[b][:], in_=ka_d[qt]), w=[f"ka{b}"], dma=True)
        p1.tile(qt, xn[:, :, :], xk)
        if qt > 0:
            S.op("pool", lambda h: h.tensor_copy(out=roll[:, :, :, 0:16], in_=roll[:, :, :, 128:144]), r=["roll"], w=["roll"])
        for grp, (wss, psx, nm) in enumerate((((0, 1), ps_q, "ps_q"), ((2, 4), ps_z, "ps_z"))):
            for si, ws in enumerate(wss):
                for g in range(2):
                    c0 = (si * 2 + g) * 128
                    for dc in range(8):
                        S.op("pe", lambda h, ws=ws, g=g, dc=dc, c0=c0, psx=psx, xn=xn: h.matmul(psx[0:64, c0:c0 + 128], lhsT=wkv[:, dc, ws, g * 64:(g + 1) * 64], rhs=xn[:, dc, :],
                                                                                     start=(dc == 0), stop=(dc == 7)), r=["wkv", xk], w=[nm])
            if grp == 0:
                S.op("act", lambda h, psx=psx: h.copy(out=roll[:, :, :, 16:144], in_=psx[0:64, :].rearrange("p (s g n) -> p s g n", s=2, g=2)), r=[nm], w=["roll"])
            else:
                S.op("act", lambda h, psx=psx, tq=tq: h.copy(out=kvT[:, :, :, tq], in_=psx[0:64, :].rearrange("p (s g n) -> p s g n", s=2, g=2)), r=[nm], w=[f"kvT_{qt // 4}"])
        for jj, ws in enumerate((3, 5)):
            for dc in range(8):
                S.op("pe", lambda h, dc=dc, ws=ws, jj=jj, xn=xn: h.matmul(ps_z[:, jj * 128:(jj + 1) * 128], lhsT=xn[:, dc, :], rhs=wkv[:, dc, ws, :],
                                                                     start=(dc == 0), stop=(dc == 7)), r=["wkv", xk], w=["ps_z"])
        S.op("dve", lambda h, qt=qt: h.tensor_copy(out=vau[:, qt, :, :, 0:64], in_=ps_z[:, 0:256].rearrange("p (j g d) -> p j g d", j=2, g=2)),
             r=["ps_z"], w=[f"vau{qt}"])
        m0 = 1 if qt == 0 else 0
        nb = 8 - m0
        n0 = 8 * qt - 1 + m0
        for s in range(2):
            for g in range(2):
                c0 = (s * 2 + g) * 8
                for l in range(32):
                    S.op("pe", lambda h, s=s, g=g, l=l, c0=c0, nb=nb, m0=m0: h.matmul(ps_q[:, c0:c0 + nb], lhsT=w1[:, s, l, :], rhs=roll[:, s, g, l + 16 * m0:l + 16 * 7 + 1:16],
                                                                         start=(l == 0), stop=(l == 31)), r=["w1", "roll"], w=["ps_q"])
        for s in range(2):
            S.op("act", lambda h, s=s, nb=nb: h.activation(out=hidn[:, s * 2:s * 2 + 2, 0:nb], in_=ps_q[:, s * 16:s * 16 + 16].rearrange("p (g n) -> p g n", g=2)[:, :, 0:nb],
                                                    func=AF.Silu, bias=pbias[:, s:s + 1]), r=["ps_q", "pbias"], w=["hidn"])
        for g in range(2):
            S.op("pe", lambda h, g=g, nb=nb: h.matmul(ps_z[0:64, g * 8:g * 8 + nb], lhsT=w2[:, 0, :], rhs=hidn[:, g, 0:nb], start=True, stop=True), r=["w2", "hidn"], w=["ps_z"])
        S.op("act", lambda h, nb=nb, n0=n0: h.copy(out=kcmpT[:, :, n0:n0 + nb], in_=ps_z[0:64, 0:16].rearrange("p (g n) -> p g n", g=2)[:, :, 0:nb]), r=["ps_z"], w=["kcmpT"])
        S.op("pool", lambda h, nb=nb, n0=n0: h.tensor_copy(out=hidv[:, :, n0:n0 + nb], in_=hidn[:, 2:4, 0:nb]), r=["hidn"], w=["hidv"])
        for nt in sorted(set([n0 // 128, (n0 + nb - 1) // 128])):
            for g in range(2):
                S.op("pe", lambda h, g=g, nt=nt: h.matmul(ps_z[:, 128 + g * 64:192 + g * 64], lhsT=hidv[:, g, nt * 128:(nt + 1) * 128], rhs=w2[:, 1, :], start=True, stop=True),
                     r=["w2", "hidv"], w=["ps_z"])
            S.op("act", lambda h, nt=nt: h.copy(out=vcau[:, nt, :, 0:64], in_=ps_z[:, 128:256].rearrange("p (g d) -> p g d", g=2)), r=["ps_z"], w=["vcau"])
        for g in range(2):
            for r in range(4):
                for dc in range(8):
                    col = (g * 4 + r) * 64
                    S.op("pe", lambda h, g=g, r=r, dc=dc, col=col, xn=xn: h.matmul(ps_q[0:64, r * 128:(r + 1) * 128], lhsT=wq[:, dc, col:col + 64],
                                                                               rhs=xn[:, dc, :], start=(dc == 0), stop=(dc == 7)),
                         r=["wq", xk], w=["ps_q"])
            S.op("act", lambda h, g=g, b=b: h.activation(out=qT[b][:, g, :, :], in_=ps_q[0:64, :].rearrange("p (r n) -> p r n", r=4),
                                                        func=AF.Copy, scale=0.125), r=["ps_q"], w=[f"qT{b}_{g}"])
        for dc in range(8):
            S.op("pe", lambda h, dc=dc, xn=xn: h.matmul(ps_z[:, :], lhsT=xn[:, dc, :], rhs=wz[:, dc, :], start=(dc == 0), stop=(dc == 7)),
                 r=["wz", xk], w=["ps_z"])
        S.op("act", lambda h, b=b: h.activation(out=zs[b][:], in_=ps_z[:, :], func=AF.Silu), r=["ps_z"], w=[f"zs{b}"])
        for dc in range(8):
            S.op("pe", lambda h, dc=dc, xn=xn: h.matmul(ps_z[:, 0:24], lhsT=xn[:, dc, :], rhs=wg[:, dc, :], start=(dc == 0), stop=(dc == 7)),
                 r=["wg", xk], w=["ps_z"])
        S.op("act", lambda h, b=b: h.activation(out=gt[b][:], in_=ps_z[:, 0:24], func=AF.Sigmoid), r=["ps_z"], w=[f"gt{b}"])
        cnts = []
        for nt in range(NKT):
            mmax = nt * 128 + 127 - 8 * qt
            mmin = nt * 128 - 8 * qt
            if mmin > 6:
                continue
            masked = mmax > -2
            cnts.append((nt, masked))
        for (nt, masked) in cnts:
            if masked:
                j0 = 128 * nt - 8 * qt + 248
                S.op("pool", lambda h, nt=nt, j0=j0: h.dma_start(out=F4t[nt][:], in_=F4_d[j0:j0 + 128, :]), w=[f"F4t{nt}"], dma=True)
        for g in range(2):
            pis = []
            for (nt, masked) in cnts:
                extra = [(ident[:], F4t[nt][:], ["p1id", f"F4t{nt}"])] if masked else []
                pi = st_tile(g, b, kcmpT[:, g, nt * 128:(nt + 1) * 128], ["kcmpT"], extra, None, None, None, None, None)
                pis.append((nt, pi))
            for r in range(4):
                for j, (nt, pi) in enumerate(pis):
                    S.op("pe", lambda h, g=g, r=r, nt=nt, pi=pi, j=j: h.matmul(po_c[:, r, 0:65], lhsT=pT[pi][:, r * 128:(r + 1) * 128], rhs=vcau[:, nt, g, :],
                                                                             start=(j == 0), stop=(j == len(pis) - 1)), r=[f"pT{pi}", "vcau"], w=["po_c"])
            for r in range(4):
                for j, (nt, pi) in enumerate(pis):
                    S.op("pe", lambda h, g=g, r=r, nt=nt, pi=pi, j=j: h.matmul(po_w[:, r, 0:64], lhsT=pT[pi][:, r * 128:(r + 1) * 128], rhs=ovl[:, nt, :],
                                                                             start=(j == 0), stop=(j == len(pis) - 1)), r=[f"pT{pi}", "ovl"], w=["po_w"])
            S.op("dve", lambda h: h.tensor_scalar(out=rden[:, 0, :], in0=po_c[:, :, 64], scalar1=1e-30, scalar2=None, op0=ALU.add), r=["po_c"], w=["rden0"])
            S.op("dve", lambda h: h.reciprocal(out=rden[:, 0, :], in_=rden[:, 0, :]), r=["rden0"], w=["rden0"])
            S.op("dve", lambda h: h.tensor_scalar(out=imp[:], in0=po_w[:, 0, 0:64], scalar1=rden[:, 0, 0:1], scalar2=None, op0=ALU.mult), r=["po_w", "rden0"], w=["imp"])
            for r in range(1, 4):
                S.op("dve", lambda h, r=r: h.scalar_tensor_tensor(out=imp[:], in0=po_w[:, r, 0:64], scalar=rden[:, 0, r:r + 1], in1=imp[:], op0=ALU.mult, op1=ALU.add),
                     r=["po_w", "rden0", "imp"], w=["imp"])
            S.op("dve", lambda h, b=b: h.tensor_tensor(out=imp[:], in0=imp[:], in1=ka[b][:, 0:64], op=ALU.mult), r=["imp", f"ka{b}"], w=["imp"])
            S.op("dve", lambda h, b=b: h.tensor_tensor(out=imp[:], in0=imp[:], in1=ka[b][:, 64:128], op=ALU.add), r=["imp", f"ka{b}"], w=["imp"])
            S.op("dve", lambda h: h.max(out=m8[:, 0:8], in_=imp[:]), r=["imp"], w=["m8"])
            S.op("dve", lambda h: h.match_replace(out=imp2[:], in_to_replace=m8[:, 0:8], in_values=imp[:], imm_value=-3.0e38), r=["imp", "m8"], w=["imp2"])
            S.op("dve", lambda h: h.max(out=m8[:, 8:16], in_=imp2[:]), r=["imp2"], w=["m8"])
            S.op("dve", lambda h: h.tensor_scalar(out=imp2[:], in0=imp[:], scalar1=m8[:, 15:16], scalar2=1.0, op0=ALU.is_ge, op1=ALU.subtract),
                 r=["imp", "m8"], w=["imp2"])
            S.op("dve", lambda h: h.tensor_scalar(out=nsel[:], in0=imp2[:], scalar1=-NEGM, scalar2=None, op0=ALU.mult), r=["imp2"], w=["nsel"])
            S.op("pe", lambda h: h.transpose(out=ps_tr[0:64, 0:128], in_=nsel[:], identity=ident[:]), r=["nsel", "p1id"], w=["p1pstr"])
            for r in range(4):
                S.op("act", lambda h, r=r: h.copy(out=nselT[:, r, :], in_=ps_tr[0:64, 0:128]), r=["p1pstr"], w=["nselT"])
            kts = [kt for kt in range(qt - 4, qt + 1) if kt >= 0]
            jobs = [("s", kt, kt) for kt in range(qt + 1)] + [("w", kt, jj) for jj, kt in enumerate(kts)]

            def do_S(job, g=g, b=b, qt=qt):
                kind, kt, jj = job
                if kind == "s":
                    cls = 0 if kt == qt else (1 if kt == qt - 1 else 3)
                    extra = [(Em[:, kt, :], nselT[:].rearrange("p r n -> p (r n)"), ["E", "nselT"]),
                             (ident[:], biasT[:, cls, g * 512:(g + 1) * 512], ["p1id", "biasT"])]
                    return st_tile(g, b, kvT[:, 0, g, kt * 128:(kt + 1) * 128], [f"kvT_{kt // 4}"], extra, None, None, None, None, None)
                dq = qt - kt
                cls = 0 if dq == 0 else (1 if dq == 1 else (2 if dq == 4 else 3))
                extra = [(ident[:], biasT[:, cls, g * 512:(g + 1) * 512], ["p1id", "biasT"])]
                return st_tile(g, b, kvT[:, 1, g, kt * 128:(kt + 1) * 128], [f"kvT_{kt // 4}"], extra, None, None, None, None, None)

            def do_PV(job, pi, g=g, qt=qt, nk=len(kts)):
                kind, kt, jj = job
                for r in range(4):
                    if kind == "s":
                        S.op("pe", lambda h, r=r: h.matmul(po_s[:, r, 0:65], lhsT=pT[pi][:, r * 128:(r + 1) * 128], rhs=vau[:, kt, 0, g, :],
                                                           start=(kt == 0 and r == 0), stop=(kt == qt), skip_group_check=True),
                             r=[f"pT{pi}", f"vau{kt}", "vau_ones"], w=["po_s"])
                    else:
                        S.op("pe", lambda h, r=r: h.matmul(po_w[:, r, 0:65], lhsT=pT[pi][:, r * 128:(r + 1) * 128], rhs=vau[:, kt, 1, g, :],
                                                           start=(jj == 0 and r == 0), stop=(jj == nk - 1), skip_group_check=True),
                             r=[f"pT{pi}", f"vau{kt}", "vau_ones"], w=["po_w"])

            pend = None
            for job in jobs:
                pi_ = do_S(job)
                if pend is not None:
                    do_PV(*pend)
                pend = (job, pi_)
            do_PV(*pend)
            S.op("dve", lambda h: h.reciprocal(out=rden[:, 1, :], in_=po_s[:, :, 64]), r=["po_s"], w=["rden1"])
            S.op("dve", lambda h: h.reciprocal(out=rden[:, 2, :], in_=po_w[:, :, 64]), r=["po_w"], w=["rden2"])
            for j in range(3):
                S.op("dve", lambda h, j=j, g=g, b=b: h.tensor_tensor(out=cf[:, j, :], in0=rden[:, j, :], in1=gt[b][:, j * 8 + g * 4:j * 8 + g * 4 + 4], op=ALU.mult),
                     r=[f"rden{j}", f"gt{b}"], w=["cf"])
            for r in range(4):
                col = (g * 4 + r) * 64
                S.op("dve", lambda h, r=r, col=col: h.tensor_scalar(out=y[:, col:col + 64], in0=po_c[:, r, 0:64], scalar1=cf[:, 0, r:r + 1], scalar2=None, op0=ALU.mult),
                     r=["po_c", "cf"], w=["y"])
                S.op("dve", lambda h, r=r, col=col: h.scalar_tensor_tensor(out=y[:, col:col + 64], in0=po_s[:, r, 0:64], scalar=cf[:, 1, r:r + 1], in1=y[:, col:col + 64],
                                                                          op0=ALU.mult, op1=ALU.add), r=["po_s", "cf", "y"], w=["y"])
                S.op("dve", lambda h, r=r, col=col: h.scalar_tensor_tensor(out=y[:, col:col + 64], in0=po_w[:, r, 0:64], scalar=cf[:, 2, r:r + 1], in1=y[:, col:col + 64],
                                                                          op0=ALU.mult, op1=ALU.add), r=["po_w", "cf", "y"], w=["y"])
        S.op("pool", lambda h, b=b: h.tensor_tensor(out=yz[b][:], in0=y[:], in1=zs[b][:], op=ALU.mult), r=["y", f"zs{b}"], w=[f"yz{b}"])
        for c in range(4):
            S.op("pe", lambda h, c=c, b=b: h.transpose(out=ps_tr[:, c * 128:(c + 1) * 128], in_=yz[b][:, c * 128:(c + 1) * 128], identity=ident[:]),
                 r=[f"yz{b}", "p1id"], w=["p1pstr"])
        S.op("act", lambda h, b=b: h.copy(out=yzT[b][:], in_=ps_tr[:, 0:512].rearrange("p (c n) -> p c n", c=4)), r=["p1pstr"], w=[f"yzT{b}"])
        for hf in range(2):
            psy = ps_q if hf == 0 else ps_z
            nm = "ps_q" if hf == 0 else "ps_z"
            for c in range(4):
                S.op("pe", lambda h, hf=hf, c=c, b=b, psy=psy: h.matmul(psy[:, :], lhsT=yzT[b][:, c, :], rhs=wo[:, c, hf * 512:(hf + 1) * 512],
                                                                       start=(c == 0), stop=(c == 3)), r=[f"yzT{b}", "wo"], w=[nm])
            if hf == 0:
                S.op("act", lambda h, b=b, psy=psy: h.copy(out=pt[b][:, 0:512], in_=psy[:, :]), r=[nm], w=[f"pt{b}"])
            else:
                S.op("dve", lambda h, b=b, psy=psy: h.tensor_copy(out=pt[b][:, 512:1024], in_=psy[:, :]), r=[nm], w=[f"pt{b}"])
        S.op("sp", lambda h, b=b, tq=tq: h.dma_start(out=p_out[tq, :], in_=pt[b][:]), r=[f"pt{b}"], dma=True)
    return nc, es, S


def prep_C(z, half, T):
    NT = T // 128
    d = {}
    w_in = z['c_w_in'][0]
    d['wq'] = np.ascontiguousarray(w_in[:, half * 512:(half + 1) * 512])
    kv = []
    for s in range(6):
        base = 1024 + s * 256 + half * 128
        kv.append(w_in[:, base:base + 128])
    d['wkv'] = np.ascontiguousarray(np.stack(kv, 1))
    gcols = np.concatenate([2560 + j * 16 + half * 8 + np.arange(8) for j in range(3)])
    d['wg'] = np.ascontiguousarray(w_in[:, gcols])
    d['wz'] = np.ascontiguousarray(w_in[:, 2608 + half * 512:2608 + (half + 1) * 512])
    d['wo'] = np.ascontiguousarray(z['c_w_out'][0][half * 512:(half + 1) * 512, :])
    w1 = np.stack([z['c_cmp_k_w1'][0], z['c_cmp_v_w1'][0]])
    d['w1'] = np.ascontiguousarray(w1.reshape(2, 32, 64, 128).transpose(0, 2, 1, 3))
    d['w2'] = np.ascontiguousarray(np.stack([z['c_cmp_k_w2'][0], z['c_cmp_v_w2'][0]]))
    d['posT'] = np.ascontiguousarray(np.stack([z['c_cmp_pos_k'][0].T, z['c_cmp_pos_v'][0].T]))
    table = z['t5_table']
    tk = np.arange(128)[:, None]
    tq = np.arange(128)[None, :]
    bias = np.zeros((128, 4, 2, 4, 128), np.float32)
    for g in range(2):
        for r in range(4):
            hh = half * 8 + g * 4 + r
            d0 = tq - tk
            bias[:, 0, g, r, :] = np.where(d0 >= 0, table[t5_bucket_np(d0), hh], NEGM)
            d1 = tq - tk + 128
            bias[:, 1, g, r, :] = table[t5_bucket_np(d1), hh]
            bias[:, 2, g, r, :] = np.where(tq < tk, table[31, hh], NEGM)
            bias[:, 3, g, r, :] = table[31, hh]
    d['biasT'] = bias.reshape(128, 4, 1024)
    j = np.arange(512)[:, None]
    F = np.where(16 * (j - 248) + 31 <= tq, 0.0, NEGM).astype(np.float32)
    d['F4'] = np.ascontiguousarray(np.tile(F, (1, 4)))
    ka = np.zeros((NT, 128, 128), np.float32)
    sblk = np.arange(64)[None, :]
    for qt in range(NT):
        t = qt * 128 + np.arange(128)[:, None]
        cur = t // 64
        forced = (sblk == 0) | (sblk == cur) | (sblk == cur - 1)
        future = sblk * 64 > t
        ka[qt, :, 0:64] = np.where(forced | future, 0.0, 1.0)
        ka[qt, :, 64:128] = np.where(forced, 1e30, np.where(future, -1e30, 0.0))
    d['keepadd'] = ka
    E = np.zeros((64, NT, 128), np.float32)
    for kt in range(NT):
        E[2 * kt, kt, 0:64] = 1.0
        E[2 * kt + 1, kt, 64:128] = 1.0
    d['E'] = E
    n = np.arange(256)[:, None]
    s = np.arange(64)[None, :]
    ov = ((16 * n < 64 * s + 64) & (16 * n + 31 >= 64 * s)).astype(np.float32)
    d['ovl'] = np.ascontiguousarray(ov.reshape(2, 128, 64).transpose(1, 0, 2))
    d['g'] = z['norm_g'][2:3].copy()
    d['ident'] = np.eye(128, dtype=np.float32)
    return d


NBLK = 8
BW = 80


def build_D(nsrc=1, TCH=1024, debug=False):
    T = CFG.T
    nc = get_nc()
    es = ExitStack()
    S = get_sched(nc, es)
    srcs = [dram_in(nc, f"xin{k}", [T, D]) for k in range(nsrc)]
    xs_out = dram_out(nc, "xs", [T, D]) if nsrc > 1 else None
    g_row = dram_in(nc, "g", [1, D])
    ident_d = dram_in(nc, "ident", [128, 128])
    wu_d = dram_in(nc, "wu", [D, NBLK * BW])
    wz_d = dram_in(nc, "wz", [D, NBLK * BW])
    wo_d = dram_in(nc, "wo", [NBLK * BW, D])
    ga_d = dram_in(nc, "ga", [NBLK, BW, BW])
    gx_d = dram_in(nc, "gx", [NBLK, BW, BW])
    vec_d = dram_in(nc, "vecs", [BW, NBLK, 8])
    p_out = dram_out(nc, "p", [T, D])

    xnT = mk(nc, es, "xnT", [128, 8, T], BF16)
    ps = [mk(nc, es, f"ps{k}", [128, 512], F32, psum=True) for k in range(7)]
    ps_tr = mk(nc, es, "ps_tr", [128, 1024], BF16, psum=True)
    phase1(S, nc, es, srcs, xs_out, g_row, ident_d, ps_tr, xnT=xnT)

    wu = mk(nc, es, "wu_s", [128, 8, NBLK * BW], BF16)
    wz = mk(nc, es, "wz_s", [128, 8, NBLK * BW], BF16)
    wo = mk(nc, es, "wo_s", [BW, NBLK, D], BF16)
    ga = mk(nc, es, "ga_s", [BW, NBLK, BW], F32)
    gx = mk(nc, es, "gx_s", [BW, NBLK, BW], F32)
    vec = mk(nc, es, "vec_s", [BW, NBLK, 8], F32)
    der = mk(nc, es, "der_s", [BW, NBLK, 4], F32)
    S.op("pool", lambda h: h.dma_start(out=wu[:], in_=wu_d.rearrange("(c p) n -> p c n", p=128)), w=["wu"], dma=True)
    S.op("pool", lambda h: h.dma_start(out=wz[:], in_=wz_d.rearrange("(c p) n -> p c n", p=128)), w=["wz"], dma=True)
    S.op("pool", lambda h: h.dma_start(out=wo[:], in_=wo_d.rearrange("(b p) n -> p b n", p=BW)), w=["wo"], dma=True)
    S.op("sp", lambda h: h.dma_start(out=ga[:], in_=ga_d.rearrange("b p n -> p b n")), w=["ga"], dma=True)
    S.op("sp", lambda h: h.dma_start(out=gx[:], in_=gx_d.rearrange("b p n -> p b n")), w=["gx"], dma=True)
    S.op("sp", lambda h: h.dma_start(out=vec[:], in_=vec_d), w=["vec"], dma=True)
    S.op("act", lambda h: h.activation(out=der[:, :, 0:1], in_=vec[:, :, 7:8], func=AF.Exp, scale=-1.0), r=["vec"], w=["der"])
    S.op("act", lambda h: h.activation(out=der[:, :, 1:2], in_=der[:, :, 0:1], func=AF.Ln, bias=1.0), r=["der"], w=["der"])
    S.op("act", lambda h: h.mul(out=der[:, :, 2:3], in_=der[:, :, 1:2], mul=-8.0), r=["der"], w=["der"])

    NW = 2
    def wt(nm, cols=TCH, dt=F32):
        return [mk(nc, es, f"{nm}{b}", [BW, cols], dt) for b in range(NW)]
    u_t = wt("u_t", TCH + 3)
    uc_t = wt("uc_t"); zs_t = wt("zs_t"); r_t = wt("r_t"); i_t = wt("i_t"); a_t = r_t; m_t = [mk(nc, es, "m_t0", [BW, TCH], F32)] * NW; h_t = uc_t
    hz = mk(nc, es, "hz", [BW, NBLK, TCH], BF16)
    hlast = mk(nc, es, "hlast", [BW, NBLK], F32)
    uhalo = mk(nc, es, "uhalo", [BW, NBLK, 3], F32)
    pt = [mk(nc, es, "pt0", [128, D], F32)] * 2
    S.op("dve", lambda h: h.memset(hlast[:], 0.0), w=["hlast"])
    for b in range(NW):
        S.op("dve", lambda h, b=b: h.memset(u_t[b][:, 0:3], 0.0), w=[f"u{b}"])
    it = 0
    for tch in range(T // TCH):
        t0 = tch * TCH
        for blk in range(NBLK):
            b = it % NW
            pb = (it - 1) % NW
            it += 1
            cs = slice(blk * BW, (blk + 1) * BW)
            for hf in range(2):
                tk = slice(t0 + hf * 512, t0 + (hf + 1) * 512)
                for dc in range(8):
                    S.op("pe", lambda h, hf=hf, dc=dc, tk=tk, cs=cs: h.matmul(ps[hf][0:BW, :], lhsT=wu[:, dc, cs], rhs=xnT[:, dc, tk],
                                                                         start=(dc == 0), stop=(dc == 7)),
                         r=["wu", f"xnT{(t0 + hf * 512) // 512}"], w=[f"ps{hf}"])
            for hf in range(2):
                tk = slice(t0 + hf * 512, t0 + (hf + 1) * 512)
                for dc in range(8):
                    S.op("pe", lambda h, hf=hf, dc=dc, tk=tk, cs=cs: h.matmul(ps[2 + hf][0:BW, :], lhsT=wz[:, dc, cs], rhs=xnT[:, dc, tk],
                                                                         start=(dc == 0), stop=(dc == 7)),
                         r=["wz", f"xnT{(t0 + hf * 512) // 512}"], w=[f"ps{2 + hf}"])
            S.op("pool", lambda h, b=b, blk=blk: h.tensor_copy(out=u_t[b][:, 0:3], in_=uhalo[:, blk, :]), r=["uhalo%d" % blk], w=[f"u{b}"]) if tch > 0 else None
            for hf in range(2):
                S.op("act", lambda h, hf=hf, b=b: h.copy(out=u_t[b][:, 3 + hf * 512:3 + (hf + 1) * 512], in_=ps[hf][0:BW, :]),
                     r=[f"ps{hf}"], w=[f"u{b}"])
            for hf in range(2):
                S.op("act", lambda h, hf=hf, b=b: h.activation(out=zs_t[b][:, hf * 512:(hf + 1) * 512], in_=ps[2 + hf][0:BW, :], func=AF.Silu),
                     r=[f"ps{2 + hf}"], w=[f"zs{b}"])
            S.op("pool", lambda h, b=b, blk=blk: h.tensor_copy(out=uhalo[:, blk, :], in_=u_t[b][:, TCH:TCH + 3]), r=[f"u{b}"], w=["uhalo%d" % blk])
            if debug and tch == 0 and blk == 0:
                dbg(S, nc, "xnT", xnT[:, :, 0:512], [128, 8, 512], ["xnT0"], BF16)
                dbg(S, nc, "u", u_t[b][:], [BW, TCH + 3], [f"u{b}"])
                dbg(S, nc, "zs", zs_t[b][:], [BW, TCH], [f"zs{b}"])
            S.op("dve", lambda h, b=b, blk=blk: h.tensor_scalar(out=uc_t[b][:], in0=u_t[b][:, 3:3 + TCH], scalar1=vec[:, blk, 3:4], scalar2=vec[:, blk, 4:5],
                                                               op0=ALU.mult, op1=ALU.add), r=[f"u{b}", "vec"], w=[f"uc{b}"])
            for j in range(3):
                S.op("dve", lambda h, b=b, blk=blk, j=j: h.scalar_tensor_tensor(out=uc_t[b][:], in0=u_t[b][:, j:j + TCH], scalar=vec[:, blk, j:j + 1],
                                                                               in1=uc_t[b][:], op0=ALU.mult, op1=ALU.add),
                     r=[f"u{b}", "vec"], w=[f"uc{b}"])
            for hf in range(2):
                S.op("pe", lambda h, hf=hf, b=b, blk=blk: h.matmul(ps[4][0:BW, :] if hf == 0 else ps[5][0:BW, :], lhsT=ga[:, blk, :],
                                                                  rhs=uc_t[b][:, hf * 512:(hf + 1) * 512], start=True, stop=True),
                     r=["ga", f"uc{b}"], w=[f"ps{4 + hf}"])
                S.op("act", lambda h, hf=hf, b=b, blk=blk: h.activation(out=r_t[b][:, hf * 512:(hf + 1) * 512], in_=ps[4 + hf][0:BW, :], func=AF.Sigmoid,
                                                                       bias=vec[:, blk, 5:6]), r=[f"ps{4 + hf}", "vec"], w=[f"r{b}"])
            for hf in range(2):
                S.op("pe", lambda h, hf=hf, b=b, blk=blk: h.matmul(ps[4 + hf][0:BW, :], lhsT=gx[:, blk, :],
                                                                  rhs=uc_t[b][:, hf * 512:(hf + 1) * 512], start=True, stop=True),
                     r=["gx", f"uc{b}"], w=[f"ps{4 + hf}"])
                S.op("act", lambda h, hf=hf, b=b, blk=blk: h.activation(out=i_t[b][:, hf * 512:(hf + 1) * 512], in_=ps[4 + hf][0:BW, :], func=AF.Sigmoid,
                                                                       bias=vec[:, blk, 6:7]), r=[f"ps{4 + hf}", "vec"], w=[f"i{b}"])
            if debug and tch == 0 and blk == 0:
                dbg(S, nc, "uc", uc_t[b][:], [BW, TCH], [f"uc{b}"])
                dbg(S, nc, "r", r_t[b][:], [BW, TCH], [f"r{b}"])
                dbg(S, nc, "i", i_t[b][:], [BW, TCH], [f"i{b}"])
                dbg(S, nc, "der", der[:], [BW, NBLK, 4], ["der"])
            S.op("act", lambda h, b=b, blk=blk: h.activation(out=a_t[b][:], in_=r_t[b][:], func=AF.Exp, scale=der[:, blk, 2:3]),
                 r=[f"r{b}", "der"], w=[f"r{b}"])
            S.op("pool", lambda h, b=b: h.tensor_tensor(out=m_t[b][:], in0=a_t[b][:], in1=a_t[b][:], op=ALU.mult), r=[f"r{b}"], w=["m0"])
            S.op("act", lambda h, b=b: h.activation(out=m_t[b][:], in_=m_t[b][:], func=AF.Sqrt, scale=-1.0, bias=1.0), r=["m0"], w=["m0"])
            S.op("pool", lambda h, b=b: h.tensor_tensor(out=i_t[b][:], in0=i_t[b][:], in1=uc_t[b][:], op=ALU.mult), r=[f"i{b}", f"uc{b}"], w=[f"i{b}"])
            S.op("pool", lambda h, b=b: h.tensor_tensor(out=i_t[b][:], in0=i_t[b][:], in1=m_t[b][:], op=ALU.mult), r=[f"i{b}", "m0"], w=[f"i{b}"])
            if debug and tch == 0 and blk == 0:
                dbg(S, nc, "a", r_t[b][:], [BW, TCH], [f"r{b}"])
                dbg(S, nc, "m", m_t[b][:], [BW, TCH], ["m0"])
                dbg(S, nc, "bt", i_t[b][:], [BW, TCH], [f"i{b}"])
            S.op("dve", lambda h, b=b, blk=blk: h.tensor_tensor_scan(out=h_t[b][:], data0=a_t[b][:], data1=i_t[b][:], initial=hlast[:, blk:blk + 1],
                                                                    op0=ALU.mult, op1=ALU.add), r=[f"r{b}", f"i{b}", "hlast"], w=[f"uc{b}"])
            S.op("dve", lambda h, b=b, blk=blk: h.tensor_copy(out=hlast[:, blk:blk + 1], in_=h_t[b][:, TCH - 1:TCH]), r=[f"uc{b}"], w=["hlast"])
            S.op("dve", lambda h, b=b, blk=blk: h.tensor_tensor(out=hz[:, blk, :], in0=h_t[b][:], in1=zs_t[b][:], op=ALU.mult),
                 r=[f"uc{b}", f"zs{b}"], w=["hz"])
        if debug and tch == 0:
            dbg(S, nc, "hz", hz[:], [BW, NBLK, TCH], ["hz"], BF16)
        for tl in range(TCH // 128):
            pbuf = tl % 2
            for hf in range(2):
                for blk in range(NBLK):
                    S.op("pe", lambda h, tl=tl, hf=hf, blk=blk: h.matmul(ps[hf][:, :], lhsT=hz[:, blk, tl * 128:(tl + 1) * 128],
                                                                        rhs=wo[:, blk, hf * 512:(hf + 1) * 512], start=(blk == 0), stop=(blk == NBLK - 1)),
                         r=["hz", "wo"], w=[f"ps{hf}"])
                S.op("act" if hf == 0 else "dve",
                     (lambda h, hf=hf, pbuf=pbuf: h.copy(out=pt[pbuf][:, hf * 512:(hf + 1) * 512], in_=ps[hf][:, :])) if hf == 0 else
                     (lambda h, hf=hf, pbuf=pbuf: h.tensor_copy(out=pt[pbuf][:, hf * 512:(hf + 1) * 512], in_=ps[hf][:, :])),
                     r=[f"ps{hf}"], w=["pt0"])
            rows = slice(t0 + tl * 128, t0 + (tl + 1) * 128)
            S.op("sp", lambda h, pbuf=pbuf, rows=rows: h.dma_start(out=p_out[rows, :], in_=pt[pbuf][:]), r=["pt0"], dma=True)
    return nc, es, S


def build_F(ntok=2048):
    nc = get_nc()
    es = ExitStack()
    S = get_sched(nc, es)
    srcs = [dram_in(nc, f"xin{k}", [ntok, D]) for k in range(3)]
    g_row = dram_in(nc, "g", [1, D])
    out_d = dram_out(nc, "out", [ntok, D])
    g_bc = mk(nc, es, "g_bc", [128, D], F32)
    S.op("sp", lambda h: h.dma_start(out=g_bc[:], in_=g_row.partition_broadcast(128)), w=["g"], dma=True)
    NB = 2
    xt = [mk(nc, es, f"xt{b}", [128, D], F32) for b in range(NB)]
    sq = mk(nc, es, "sq", [128, D], F32)
    ot = [mk(nc, es, f"ot{b}", [128, D], F32) for b in range(NB)]
    st = [mk(nc, es, f"st{b}", [128, 4], F32) for b in range(NB)]
    for t in range(ntok // 128):
        b = t % NB
        rows = slice(t * 128, (t + 1) * 128)
        for k, src in enumerate(srcs):
            if k == 0:
                S.op("pool", lambda h, src=src, b=b, t=t: h.dma_start(out=xt[b][:], in_=src_rows(src, t)), r=src_bufs(src, t), w=[f"xt{b}"], dma=True)
            else:
                S.op("pool", lambda h, src=src, b=b, t=t: h.dma_start(out=xt[b][:], in_=src_rows(src, t), accum_op=ALU.add), r=[f"xt{b}"] + src_bufs(src, t), w=[f"xt{b}"], dma=True)
        S.op("act", lambda h, b=b: h.activation(out=sq[:], in_=xt[b][:], func=AF.Square), r=[f"xt{b}"], w=["sq"])
        S.op("dve", lambda h, b=b: h.tensor_reduce(out=st[b][:, 0:1], in_=sq[:], axis=AX.X, op=ALU.add), r=["sq"], w=[f"st{b}"])
        S.op("act", lambda h, b=b: h.activation(out=st[b][:, 1:2], in_=st[b][:, 0:1], func=AF.Sqrt, scale=1.0 / D, bias=EPS), r=[f"st{b}"], w=[f"st{b}"])
        S.op("dve", lambda h, b=b: h.reciprocal(out=st[b][:, 2:3], in_=st[b][:, 1:2]), r=[f"st{b}"], w=[f"st{b}r"])
        S.op("dve", lambda h, b=b: h.scalar_tensor_tensor(out=ot[b][:], in0=xt[b][:], scalar=st[b][:, 2:3], in1=g_bc[:], op0=ALU.mult, op1=ALU.mult),
             r=[f"xt{b}", f"st{b}r", "g"], w=[f"ot{b}"])
        S.op("sp", lambda h, b=b, rows=rows: h.dma_start(out=out_d[rows, :], in_=ot[b][:]), r=[f"ot{b}"], dma=True)
    return nc, es, S


def prep_D(z, half):
    LW = 1280
    blks = list(range(half * 8, half * 8 + 8))
    cols = np.concatenate([np.arange(b * 80, (b + 1) * 80) for b in blks])
    w_in = z['d_w_in'][0]
    d = {}
    d['wu'] = np.ascontiguousarray(w_in[:, cols])
    d['wz'] = np.ascontiguousarray(w_in[:, LW + cols])
    d['wo'] = np.ascontiguousarray(z['d_w_out'][0][cols, :])
    d['ga'] = np.ascontiguousarray(z['d_gate_a_w'][0][blks])
    d['gx'] = np.ascontiguousarray(z['d_gate_x_w'][0][blks])
    vecs = np.zeros((80, 8, 8), np.float32)

    def fm(v):
        return v[cols].reshape(8, 80).T
    for j in range(4):
        vecs[:, :, j] = fm(z['d_conv_w'][0][j])
    vecs[:, :, 4] = fm(z['d_conv_b'][0])
    vecs[:, :, 5] = fm(z['d_gate_a_b'][0])
    vecs[:, :, 6] = fm(z['d_gate_x_b'][0])
    vecs[:, :, 7] = fm(z['d_lambda'][0])
    d['vecs'] = vecs
    d['g'] = z['norm_g'][3:4].copy()
    d['ident'] = np.eye(128, dtype=np.float32)
    return d


PAIRS = [[0, 1], [2, 3], [4, 5], [6, 7]]


def build_fused(T=4096, nlayers=4):
    CFG.T = T
    nc = bass.Bass("TRN2", target_bir_lowering=False)
    top = ExitStack()
    S = Sched(nc, top)
    CFG.nc, CFG.S = nc, S
    x_d = nc.dram_tensor("x", [T, D], F32, kind="ExternalInput").ap()
    out_d = nc.dram_tensor("out", [T, D], F32, kind="ExternalOutput").ap()
    p = [nc.dram_tensor(f"p_i{l}", [T, D], F32) for l in range(4)]
    CH = 512
    NCH = T // CH
    pg = [[nc.dram_tensor(f"pg_i{l}_{k}", [2 * CH, D], F32) for k in range(NCH)] for l in range(4)]

    def gsrc(l, rank):
        def f(t):
            return pg[l][t // 4].ap()[rank * CH + (t % 4) * 128:rank * CH + (t % 4 + 1) * 128, :]
        f.buf = lambda t: f"pg{l}_{t // 4}"
        return f
    xs = [nc.dram_tensor(f"xs_i{l}", [T, D], F32) for l in range(3)]
    layers = [("A", build_A, {}), ("B", build_B, dict(MC=128)), ("C", build_C, {}), ("D", build_D, {})]
    prev_x = x_d
    for l, (nm, fn, kw) in enumerate(layers[:nlayers]):
        CFG.prefix = nm + "_"
        SB_USED[0] = 0
        ov = {"p": p[l].ap()}
        if l == 0:
            ov["xin0"] = x_d
            nsrc = 1
        else:
            ov["xin0"] = prev_x
            ov["xin1"] = gsrc(l - 1, 0)
            ov["xin2"] = gsrc(l - 1, 1)
            ov["xs"] = xs[l - 1].ap()
            nsrc = 3
        CFG.override = ov
        _, es, _ = fn(nsrc, **kw)
        for k in range(NCH):
            S.op("pool", lambda h, l=l, k=k: h.collective_compute("AllGather", ALU.bypass, replica_groups=PAIRS, ins=[p[l].ap()[k * CH:(k + 1) * CH, :].opt()],
                                                               outs=[pg[l][k].ap().opt()]), w=[f"pg{l}_{k}"], dma=True, cc=True)
        S.emit(final=False)
        S.barrier()
        es.close()
        if l > 0:
            prev_x = xs[l - 1].ap()
    CFG.prefix = "F_"
    SB_USED[0] = 0
    CFG.override = {"xin0": prev_x, "xin1": gsrc(nlayers - 1, 0), "xin2": gsrc(nlayers - 1, 1), "out": out_d}
    _, es, _ = build_F(T)
    stats = S.emit(final=True)
    CFG.nc, CFG.S, CFG.override, CFG.prefix = None, None, {}, ""
    return nc, stats


def kernel(**inputs):
    z = {k: np.ascontiguousarray(np.asarray(v, dtype=np.float32)) for k, v in inputs.items()}
    T = 4096
    x = z['x']
    B = x.shape[0]
    nc, _ = build_fused(T)
    per_half = []
    for h in range(2):
        d = {}
        for pre, pd in (("A_", prep_A(z, h)), ("B_", prep_B(z, h, MC=128)), ("C_", prep_C(z, h, T)), ("D_", prep_D(z, h))):
            for k, v in pd.items():
                d[pre + k] = v
        d["F_g"] = z['final_g'][None, :].copy()
        per_half.append(d)
    in_maps = [dict(per_half[c % 2], x=x[c // 2]) for c in range(8)]
    res = run_bass_kernel_spmd(nc, in_maps, core_ids=list(range(8)))
    out = np.stack([res.results[2 * b]['out'] for b in range(B)]).astype(np.float32)
    return out
```

```python
import numpy as np
from contextlib import ExitStack
import concourse.bass as bass
import concourse.mybir as mybir
from concourse.bass_utils import run_bass_kernel_spmd

F32 = mybir.dt.float32
BF16 = mybir.dt.bfloat16
AF = mybir.ActivationFunctionType
ALU = mybir.AluOpType
AX = mybir.AxisListType


class Buf:
    __slots__ = ("name", "lw", "rd")

    def __init__(self, name):
        self.name = name
        self.lw = None
        self.rd = {}


class Sched:
    COMPUTE = ("pe", "act", "dve", "pool")

    def __init__(self, nc, es, ndma_slots=8):
        self.nc = nc
        self.es = es
        self.ops = []
        self.bufs = {}
        self.ndma = ndma_slots
        self.handles = {"pe": nc.tensor, "act": nc.scalar, "dve": nc.vector, "pool": nc.gpsimd, "sp": nc.sync}
        self.need = []
        self.seg_dma = []
        self.last_compute = {}
        self.barrier_deps = set()
        self.pending_barrier = {}

    def buf(self, name):
        b = self.bufs.get(name)
        if b is None:
            b = Buf(name)
            self.bufs[name] = b
        return b

    def _B(self, lst):
        out = []
        for x in lst:
            if isinstance(x, str):
                out.append(self.buf(x))
            elif isinstance(x, Buf):
                out.append(x)
            elif x is None:
                continue
            else:
                out.extend(self._B(x))
        return out

    def op(self, eng, fn, r=(), w=(), dma=False, cc=False):
        i = len(self.ops)
        R = self._B(r)
        W = self._B(w)
        deps = set()
        if self.pending_barrier.get(eng):
            deps |= self.barrier_deps
            self.pending_barrier[eng] = False
        if cc:
            deps |= set(self.seg_dma)
        raw = set()
        for b in R:
            if b.lw is not None:
                deps.add(b.lw)
                raw.add(b.lw)
        for b in W:
            if b.lw is not None:
                deps.add(b.lw)
            for k, v in b.rd.items():
                deps.add(v)
        for b in W:
            b.lw = i
            b.rd = {}
        key = ("dma", i) if dma else eng
        for b in R:
            b.rd[key] = i
        self.ops.append(dict(eng=eng, fn=fn, deps=deps, dma=dma, cc=cc, raw=raw, wnames=[x for x in w if isinstance(x, str)] if cc else []))
        if dma:
            self.seg_dma.append(i)
        elif eng in self.COMPUTE:
            self.last_compute[eng] = i
        return i

    def _init_state(self):
        nc = self.nc
        self.sems = {e: self.es.enter_context(nc.semaphore("sem_" + e)) for e in self.COMPUTE}
        self.dsems = {q: [self.es.enter_context(nc.semaphore(f"dsem_{q}_{k}")) for k in range(self.ndma)] for q in ("sp", "pool")}
        self.ccsem = self.es.enter_context(nc.semaphore("sem_cc"))
        self.cccount = 0
        self.duses = {q: [0] * self.ndma for q in ("sp", "pool")}
        self.dcount = {"sp": 0, "pool": 0}
        self.cnt = {e: 0 for e in self.COMPUTE}
        self.token = []
        self.waited = {e: {} for e in self.handles}
        self.nwaits = 0
        self.emitted = 0
        self.inited = True

    def _skip(self, po, o, d):
        if po["dma"] or o["dma"] or po["eng"] != o["eng"]:
            return False
        e = o["eng"]
        if e == "pe":
            return True
        if e in ("act", "dve") and d not in o["raw"]:
            return True
        return False

    def barrier(self):
        deps = set(i for i in self.seg_dma if not self.ops[i].get("cc"))
        keep = [(i, self.ops[i].get("wnames", [])) for i in self.seg_dma if self.ops[i].get("cc")]
        for e in self.COMPUTE:
            if e in self.last_compute:
                deps.add(self.last_compute[e])
        self.barrier_deps = deps
        self.pending_barrier = {e: True for e in self.handles}
        self.seg_dma = []
        self.bufs = {}
        for i, names in keep:
            for nm in names:
                self.buf(nm).lw = i

    def emit(self, final=True):
        nc = self.nc
        ops = self.ops
        if not getattr(self, "inited", False):
            self._init_state()
        start = self.emitted
        n = len(ops)
        need = self.need
        need.extend([False] * (n - len(need)))
        for i in range(start, n):
            o = ops[i]
            for d in o["deps"]:
                po = ops[d]
                if po["dma"]:
                    continue
                if self._skip(po, o, d):
                    continue
                assert d >= start or need[d], "cross-segment dependency on an op without increment"
                need[d] = True
        lastc = {}
        for i in range(start, n):
            if not ops[i]["dma"] and ops[i]["eng"] in self.COMPUTE:
                lastc[ops[i]["eng"]] = i
        for e, i in lastc.items():
            need[i] = True
        sems, dsems, duses, dcount, cnt, token, waited = self.sems, self.dsems, self.duses, self.dcount, self.cnt, self.token, self.waited
        token.extend([None] * (n - len(token)))
        for i in range(start, n):
            o = ops[i]
            e = o["eng"]
            h = self.handles[e]
            wd = waited[e]
            reqs = {}
            for d in o["deps"]:
                po = ops[d]
                if self._skip(po, o, d):
                    continue
                sem, val, sk = token[d]
                if wd.get(sk, 0) >= val:
                    continue
                if sk not in reqs or reqs[sk][1] < val:
                    reqs[sk] = (sem, val)
            is_cc = o.get("cc", False)
            if o["dma"] and not is_cc:
                q = e
                s = dcount[q] % self.ndma
                dcount[q] += 1
                dsk = ("d", q, s)
                prev = 16 * duses[q][s]
                if prev > 0 and wd.get(dsk, 0) < prev:
                    if dsk not in reqs or reqs[dsk][1] < prev:
                        reqs[dsk] = (dsems[q][s], prev)
            for rk, (rsem, rval) in reqs.items():
                h.wait_ge(rsem, rval)
                wd[rk] = rval
                self.nwaits += 1
            ins = o["fn"](h)
            if is_cc:
                self.cccount += 1
                ins.then_inc(self.ccsem, 1)
                token[i] = (self.ccsem, self.cccount, ("cc",))
            elif o["dma"]:
                duses[q][s] += 1
                ins.then_inc(dsems[q][s], 16)
                token[i] = (dsems[q][s], 16 * duses[q][s], dsk)
            else:
                if need[i]:
                    cnt[e] += 1
                    ins.then_inc(sems[e], 1)
                    token[i] = (sems[e], cnt[e], ("c", e))
                else:
                    token[i] = (sems[e], cnt[e] + 0, ("c", e))
            o["fn"] = None
        self.emitted = n
        if final:
            h = self.handles["sp"]
            for q in ("sp", "pool"):
                for s in range(self.ndma):
                    if duses[q][s] > 0:
                        h.wait_ge(dsems[q][s], 16 * duses[q][s])
            if self.cccount:
                h.wait_ge(self.ccsem, self.cccount)
        self.stats = dict(nops=len(ops), nwaits=self.nwaits, incs=dict(cnt))
        return self.stats


class Stream:
    def __init__(self):
        self.items = []

    def op(self, *a, **k):
        self.items.append((a, k))


def merge_streams(S, streams, chunk=1):
    idx = [0] * len(streams)
    live = True
    while live:
        live = False
        for i, st in enumerate(streams):
            for _ in range(chunk):
                if idx[i] < len(st.items):
                    a, k = st.items[idx[i]]
                    S.op(*a, **k)
                    idx[i] += 1
                    live = True


class CFG:
    T = 4096
    prefix = ""
    nc = None
    S = None
    override = {}


def get_nc():
    if CFG.nc is not None:
        return CFG.nc
    return bass.Bass("TRN2", target_bir_lowering=False)


def get_sched(nc, es):
    if CFG.S is not None:
        return CFG.S
    return Sched(nc, es)
D = 1024
EPS = 1e-6


class Ctx:
    pass


SB_USED = [0]


def mk(nc, es, name, shape, dt, psum=False):
    if not psum:
        n = 1
        for d_ in shape[1:]:
            n *= d_
        n *= (2 if dt == BF16 else 4)
        SB_USED[0] += (n + 31) // 32 * 32
        assert SB_USED[0] <= 190 * 1024, f"SBUF over budget at {name}: {SB_USED[0]}"
    if psum:
        return es.enter_context(nc.psum_tensor(CFG.prefix + name, shape, dt))
    return es.enter_context(nc.sbuf_tensor(CFG.prefix + name, shape, dt))


def src_rows(src, t):
    if callable(src):
        return src(t)
    return src[t * 128:(t + 1) * 128, :]


def src_bufs(src, t):
    if callable(src) and hasattr(src, "buf"):
        return [src.buf(t)]
    return []


def dram_in(nc, name, shape, dt=F32):
    if name in CFG.override:
        return CFG.override[name]
    return nc.dram_tensor(CFG.prefix + name, list(shape), dt, kind="ExternalInput").ap()


def dram_out(nc, name, shape, dt=F32):
    if name in CFG.override:
        return CFG.override[name]
    return nc.dram_tensor(CFG.prefix + name, list(shape), dt, kind="ExternalOutput").ap()


class P1:
    def __init__(self, S, nc, es, srcs, xs_out, g_row, ident_d, ps_tr, name="p1"):
        self.S, self.nc, self.srcs, self.xs_out, self.ps_tr, self.name = S, nc, srcs, xs_out, ps_tr, name
        self.g_bc = mk(nc, es, name + "_g", [128, D], F32)
        self.ident = mk(nc, es, name + "_id", [128, 128], BF16)
        self.identf = mk(nc, es, name + "_idf", [128, 128], F32)
        g_bc, ident, identf = self.g_bc, self.ident, self.identf
        S.op("sp", lambda h: h.dma_start(out=g_bc[:], in_=g_row.partition_broadcast(128)), w=[name + "g"], dma=True)
        S.op("sp", lambda h: h.dma_start(out=identf[:], in_=ident_d), w=[name + "idf"], dma=True)
        S.op("dve", lambda h: h.tensor_copy(out=ident[:], in_=identf[:]), r=[name + "idf"], w=[name + "id"])
        self.NB = 2
        self.xt = [mk(nc, es, f"{name}_x{b}", [128, D], F32) for b in range(self.NB)]
        self.sq = mk(nc, es, name + "_sq", [128, D], BF16)
        self.xnb = [mk(nc, es, f"{name}_xn{b}", [128, D], BF16) for b in range(self.NB)]
        self.st = [mk(nc, es, f"{name}_st{b}", [128, 4], F32) for b in range(self.NB)]

    def tile(self, t, dst_ap, dst_buf):
        S, name = self.S, self.name
        xt, sq, xnb, st, g_bc, ident, ps_tr = self.xt, self.sq, self.xnb, self.st, self.g_bc, self.ident, self.ps_tr
        b = t % self.NB
        rows = slice(t * 128, (t + 1) * 128)
        xb = f"{name}x{b}"
        for k, src in enumerate(self.srcs):
            if k == 0:
                S.op("pool", lambda h, src=src: h.dma_start(out=xt[b][:], in_=src_rows(src, t)), r=src_bufs(src, t), w=[xb], dma=True)
            else:
                S.op("pool", lambda h, src=src: h.dma_start(out=xt[b][:], in_=src_rows(src, t), accum_op=ALU.add), r=[xb] + src_bufs(src, t), w=[xb], dma=True)
        if self.xs_out is not None and len(self.srcs) > 1:
            S.op("sp", lambda h: h.dma_start(out=self.xs_out[rows, :], in_=xt[b][:]), r=[xb], dma=True)
        S.op("act", lambda h: h.activation(out=sq[:], in_=xt[b][:], func=AF.Square), r=[xb], w=[name + "sq"])
        S.op("dve", lambda h: h.tensor_reduce(out=st[b][:, 0:1], in_=sq[:], axis=AX.X, op=ALU.add), r=[name + "sq"], w=[f"{name}st{b}"])
        S.op("act", lambda h: h.activation(out=st[b][:, 1:2], in_=st[b][:, 0:1], func=AF.Sqrt, scale=1.0 / D, bias=EPS), r=[f"{name}st{b}"], w=[f"{name}st{b}"])
        S.op("dve", lambda h: h.reciprocal(out=st[b][:, 2:3], in_=st[b][:, 1:2]), r=[f"{name}st{b}"], w=[f"{name}st{b}r"])
        S.op("dve", lambda h: h.scalar_tensor_tensor(out=xnb[b][:], in0=xt[b][:], scalar=st[b][:, 2:3], in1=g_bc[:], op0=ALU.mult, op1=ALU.mult),
             r=[xb, f"{name}st{b}r", name + "g"], w=[f"{name}xn{b}"])
        for dc in range(8):
            S.op("pe", lambda h, dc=dc: h.transpose(out=ps_tr[:, dc * 128:(dc + 1) * 128], in_=xnb[b][:, dc * 128:(dc + 1) * 128], identity=ident[:]),
                 r=[f"{name}xn{b}", name + "id"], w=[name + "pstr"])
        S.op("act", lambda h: h.copy(out=dst_ap, in_=ps_tr[:].rearrange("p (c n) -> p c n", c=8)), r=[name + "pstr"], w=[dst_buf])


def phase1(S, nc, es, srcs, xs_out, g_row, ident_d, ps_tr, ntiles=None, xnT=None, name="p1"):
    if ntiles is None:
        ntiles = CFG.T // 128
    p1 = P1(S, nc, es, srcs, xs_out, g_row, ident_d, ps_tr, name)
    for t in range(ntiles):
        p1.tile(t, xnT[:, :, t * 128:(t + 1) * 128], f"xnT{t // 4}")
    return p1.ident


def dbg(S, nc, name, ap, shape, rbuf, dt=F32):
    o = nc.dram_tensor("dbg_" + name, list(shape), dt, kind="ExternalOutput").ap()
    S.op("sp", lambda h: h.dma_start(out=o, in_=ap), r=rbuf, dma=True)


NEGM = -30000.0


def build_A(nsrc=1, debug=False):
    T = CFG.T
    NT = T // 128
    nc = get_nc()
    es = ExitStack()
    S = get_sched(nc, es)
    srcs = [dram_in(nc, f"xin{k}", [T, D]) for k in range(nsrc)]
    xs_out = dram_out(nc, "xs", [T, D]) if nsrc > 1 else None
    g_row = dram_in(nc, "g", [1, D])
    ident_d = dram_in(nc, "ident", [128, 128])
    wq_d = dram_in(nc, "wq", [D, 512])
    wk_d = dram_in(nc, "wk", [D, 128])
    wv_d = dram_in(nc, "wv", [D, 128])
    wz_d = dram_in(nc, "wz", [D, 512])
    wo_d = dram_in(nc, "wo", [512, D])
    bias_d = dram_in(nc, "biasT", [128, 2, 2 * 4 * 128])
    sink_d = dram_in(nc, "sinks", [1, 8])
    p_out = dram_out(nc, "p", [T, D])

    xnT = mk(nc, es, "xnT", [128, 8, T], BF16)
    ps_tr = mk(nc, es, "ps_tr", [128, 1024], BF16, psum=True)
    ps_q = mk(nc, es, "ps_q", [128, 512], F32, psum=True)
    ps_z = mk(nc, es, "ps_z", [128, 512], F32, psum=True)
    ps_s = [mk(nc, es, f"ps_s{k}", [128, 512], F32, psum=True) for k in range(2)]
    ps_o = mk(nc, es, "ps_o", [128, 4, 128], F32, psum=True)
    ps_y = [mk(nc, es, f"ps_y{k}", [128, 512], F32, psum=True) for k in range(2)]
    ident = phase1(S, nc, es, srcs, xs_out, g_row, ident_d, ps_tr, xnT=xnT)

    wq = mk(nc, es, "wq_s", [128, 8, 512], BF16)
    wk = mk(nc, es, "wk_s", [128, 8, 128], BF16)
    wv = mk(nc, es, "wv_s", [128, 8, 128], BF16)
    wz = mk(nc, es, "wz_s", [128, 8, 512], BF16)
    wo = mk(nc, es, "wo_s", [128, 4, D], BF16)
    biasT = mk(nc, es, "biasT_s", [128, 2, 1024], BF16)
    esink = mk(nc, es, "esink", [128, 8], F32)
    for nm, t_, d_, pat in (("wq", wq, wq_d, "(c p) n -> p c n"), ("wk", wk, wk_d, "(c p) n -> p c n"), ("wv", wv, wv_d, "(c p) n -> p c n"),
                            ("wz", wz, wz_d, "(c p) n -> p c n"), ("wo", wo, wo_d, "(c p) n -> p c n")):
        S.op("pool", lambda h, t_=t_, d_=d_, pat=pat: h.dma_start(out=t_[:], in_=d_.rearrange(pat, p=128)), w=[nm], dma=True)
    S.op("pool", lambda h: h.dma_start(out=biasT[:], in_=bias_d), w=["biasT"], dma=True)
    S.op("sp", lambda h: h.dma_start(out=esink[:], in_=sink_d.partition_broadcast(128)), w=["esink"], dma=True)
    S.op("act", lambda h: h.activation(out=esink[:], in_=esink[:], func=AF.Exp), r=["esink"], w=["esink"])

    kT = mk(nc, es, "kT", [64, 2, T], BF16)
    vau = mk(nc, es, "vau", [128, NT, 2, 65], BF16)
    S.op("dve", lambda h: h.memset(vau[:, :, :, 64:65], 1.0), w=["vau_ones"])
    for g in range(2):
        for c in range(T // 512):
            tk = slice(c * 512, (c + 1) * 512)
            for dc in range(8):
                S.op("pe", lambda h, g=g, dc=dc, tk=tk: h.matmul(ps_q[0:64, :], lhsT=wk[:, dc, g * 64:(g + 1) * 64], rhs=xnT[:, dc, tk],
                                                                start=(dc == 0), stop=(dc == 7)), r=["wk", f"xnT{c}"], w=["ps_q"])
            S.op("act", lambda h, g=g, tk=tk: h.copy(out=kT[:, g, tk], in_=ps_q[0:64, :]), r=["ps_q"], w=[f"kT{c // 1}"])
    for t in range(NT):
        for dc in range(8):
            S.op("pe", lambda h, t=t, dc=dc: h.matmul(ps_z[:, 0:128], lhsT=xnT[:, dc, t * 128:(t + 1) * 128], rhs=wv[:, dc, :],
                                                      start=(dc == 0), stop=(dc == 7)), r=["wv", f"xnT{t // 4}"], w=["ps_z"])
        S.op("dve", lambda h, t=t: h.tensor_copy(out=vau[:, t, :, 0:64], in_=ps_z[:, 0:128].rearrange("p (g d) -> p g d", g=2)),
             r=["ps_z"], w=[f"vau{t}"])

    NB = 2
    qT = [mk(nc, es, f"qT{b}", [64, 2, 4, 128], BF16) for b in range(NB)]
    zs = [mk(nc, es, f"zs{b}", [128, 512], BF16) for b in range(NB)]
    pT = [mk(nc, es, f"pT{b}", [128, 512], BF16) for b in range(4)]
    yz = [mk(nc, es, f"yz{b}", [128, 512], BF16) for b in range(NB)]
    yzT = [mk(nc, es, f"yzT{b}", [128, 4, 128], BF16) for b in range(NB)]
    den = [mk(nc, es, f"den{b}", [128, 8], F32) for b in range(NB)]
    pt = [mk(nc, es, f"pt{b}", [128, D], F32) for b in range(NB)]
    pti = 0
    for qt in range(NT):
        b = qt % NB
        tq = slice(qt * 128, (qt + 1) * 128)
        xk = f"xnT{qt // 4}"
        for g in range(2):
            for r in range(4):
                for dc in range(8):
                    col = (g * 4 + r) * 64
                    S.op("pe", lambda h, g=g, r=r, dc=dc, col=col, tq=tq: h.matmul(ps_q[0:64, r * 128:(r + 1) * 128], lhsT=wq[:, dc, col:col + 64],
                                                                               rhs=xnT[:, dc, tq], start=(dc == 0), stop=(dc == 7)),
                         r=["wq", xk], w=["ps_q"])
            S.op("act", lambda h, g=g, b=b: h.activation(out=qT[b][:, g, :, :], in_=ps_q[0:64, :].rearrange("p (r n) -> p r n", r=4),
                                                        func=AF.Copy, scale=0.125), r=["ps_q"], w=[f"qT{b}_{g}"])
        for dc in range(8):
            S.op("pe", lambda h, dc=dc, tq=tq: h.matmul(ps_z[:, :], lhsT=xnT[:, dc, tq], rhs=wz[:, dc, :], start=(dc == 0), stop=(dc == 7)),
                 r=["wz", xk], w=["ps_z"])
        S.op("act", lambda h, b=b: h.activation(out=zs[b][:], in_=ps_z[:, :], func=AF.Silu), r=["ps_z"], w=[f"zs{b}"])
        for g in range(2):
            kts = [kt for kt in (qt - 1, qt) if kt >= 0]
            for kt in kts:
                cls = 0 if kt == qt else 1
                si = kt % 2
                pi = (g * 2 + si)
                S.op("pe", lambda h, g=g, kt=kt, si=si, b=b: h.matmul(ps_s[si][:, :], lhsT=kT[:, g, kt * 128:(kt + 1) * 128],
                                                                     rhs=qT[b][:, g, :, :].rearrange("p r n -> p (r n)"), start=True, stop=False),
                     r=[f"kT{kt // 4}", f"qT{b}_{g}"], w=[f"ps_s{si}"])
                S.op("pe", lambda h, g=g, cls=cls, si=si: h.matmul(ps_s[si][:, :], lhsT=ident[:], rhs=biasT[:, cls, g * 512:(g + 1) * 512],
                                                                  start=False, stop=True), r=["biasT", "p1id"], w=[f"ps_s{si}"])
                S.op("act", lambda h, si=si, pi=pi: h.activation(out=pT[pi][:], in_=ps_s[si][:, :], func=AF.Exp), r=[f"ps_s{si}"], w=[f"pT{pi}"])
            for r in range(4):
                for j, kt in enumerate(kts):
                    pi = (g * 2 + kt % 2)
                    S.op("pe", lambda h, g=g, r=r, kt=kt, pi=pi, j=j, nk=len(kts): h.matmul(ps_o[:, r, 0:65], lhsT=pT[pi][:, r * 128:(r + 1) * 128],
                                                                             rhs=vau[:, kt, g, :], start=(j == 0), stop=(j == nk - 1)),
                         r=[f"pT{pi}", f"vau{kt}", "vau_ones"], w=["ps_o"])
            S.op("dve", lambda h, g=g, b=b: h.tensor_tensor(out=den[b][:, g * 4:(g + 1) * 4], in0=ps_o[:, :, 64], in1=esink[:, g * 4:(g + 1) * 4], op=ALU.add),
                 r=["ps_o", "esink"], w=[f"den{b}"])
            S.op("dve", lambda h, g=g, b=b: h.reciprocal(out=den[b][:, g * 4:(g + 1) * 4], in_=den[b][:, g * 4:(g + 1) * 4]), r=[f"den{b}"], w=[f"den{b}"])
            for r in range(4):
                col = (g * 4 + r) * 64
                S.op("dve", lambda h, g=g, r=r, b=b, col=col: h.scalar_tensor_tensor(out=yz[b][:, col:col + 64], in0=ps_o[:, r, 0:64],
                                                                                   scalar=den[b][:, g * 4 + r:g * 4 + r + 1], in1=zs[b][:, col:col + 64],
                                                                                   op0=ALU.mult, op1=ALU.mult),
                     r=["ps_o", f"den{b}", f"zs{b}"], w=[f"yz{b}"])
        for c in range(4):
            S.op("pe", lambda h, c=c, b=b: h.transpose(out=ps_tr[:, c * 128:(c + 1) * 128], in_=yz[b][:, c * 128:(c + 1) * 128], identity=ident[:]),
                 r=[f"yz{b}", "p1id"], w=["p1pstr"])
        S.op("act", lambda h, b=b: h.copy(out=yzT[b][:], in_=ps_tr[:, 0:512].rearrange("p (c n) -> p c n", c=4)), r=["p1pstr"], w=[f"yzT{b}"])
        for hf in range(2):
            for c in range(4):
                S.op("pe", lambda h, hf=hf, c=c, b=b: h.matmul(ps_y[hf][:, :], lhsT=yzT[b][:, c, :], rhs=wo[:, c, hf * 512:(hf + 1) * 512],
                                                              start=(c == 0), stop=(c == 3)), r=[f"yzT{b}", "wo"], w=[f"ps_y{hf}"])
            if hf == 0:
                S.op("act", lambda h, b=b: h.copy(out=pt[b][:, 0:512], in_=ps_y[0][:, :]), r=["ps_y0"], w=[f"pt{b}"])
            else:
                S.op("dve", lambda h, b=b: h.tensor_copy(out=pt[b][:, 512:1024], in_=ps_y[1][:, :]), r=["ps_y1"], w=[f"pt{b}"])
        S.op("sp", lambda h, b=b, tq=tq: h.dma_start(out=p_out[tq, :], in_=pt[b][:]), r=[f"pt{b}"], dma=True)
    return nc, es, S


def t5_bucket_np(d):
    import math
    d = np.maximum(d, 0)
    df = np.maximum(d, 1).astype(np.float32)
    large = 16 + (np.log(df / 16) / math.log(128 / 16) * 16).astype(np.int32)
    large = np.minimum(large, 31)
    return np.where(d < 16, d, large)


def prep_A(z, half):
    d = {}
    w_in = z['a_w_in'][0]
    d['wq'] = np.ascontiguousarray(w_in[:, half * 512:(half + 1) * 512])
    d['wk'] = np.ascontiguousarray(w_in[:, 1024 + half * 128:1024 + (half + 1) * 128])
    d['wv'] = np.ascontiguousarray(w_in[:, 1280 + half * 128:1280 + (half + 1) * 128])
    d['wz'] = np.ascontiguousarray(w_in[:, 1536 + half * 512:1536 + (half + 1) * 512])
    d['wo'] = np.ascontiguousarray(z['a_w_out'][0][half * 512:(half + 1) * 512, :])
    d['sinks'] = np.ascontiguousarray(z['a_sinks'][0][half * 8:(half + 1) * 8][None, :])
    table = z['t5_table']
    tk = np.arange(128)[:, None]
    tq = np.arange(128)[None, :]
    bias = np.zeros((128, 2, 2, 4, 128), np.float32)
    for cls in range(2):
        dist = tq - tk + 128 * cls
        valid = (dist >= 0) & (dist < 128)
        bk = t5_bucket_np(dist)
        for g in range(2):
            for r in range(4):
                hh = half * 8 + g * 4 + r
                bias[:, cls, g, r, :] = np.where(valid, table[bk, hh], NEGM)
    d['biasT'] = bias.reshape(128, 2, 1024)
    d['g'] = z['norm_g'][0:1].copy()
    d['ident'] = np.eye(128, dtype=np.float32)
    return d


KAP = 0.6065306597126334
GN_EPS = 64e-5


class _Stop(Exception):
    pass


def build_B(nsrc=1, MC=256, debug=False, stage=99):
    try:
        return _build_B(nsrc, MC, debug, stage)
    except _Stop as e:
        return e.args[0]


def _build_B(nsrc=1, MC=256, debug=False, stage=99):
    T = CFG.T
    NJ = MC // 64
    NMC = T // MC
    nc = get_nc()
    es = ExitStack()
    S = get_sched(nc, es)
    srcs = [dram_in(nc, f"xin{k}", [T, D]) for k in range(nsrc)]
    xs_out = dram_out(nc, "xs", [T, D]) if nsrc > 1 else None
    g_row = dram_in(nc, "g", [1, D])
    ident_d = dram_in(nc, "ident", [128, 128])
    w4_d = dram_in(nc, "w4", [4, D, 512])
    lw_d = dram_in(nc, "lw", [2, D, 64])
    l2_d = dram_in(nc, "l2", [2, 64, 512])
    wo_d = dram_in(nc, "wo", [512, D])
    mu_d = dram_in(nc, "muT", [128, 6, 8])
    vec_d = dram_in(nc, "vecs", [64, 8, 8])
    lnw_d = dram_in(nc, "lnw", [1, 512])
    lnb_d = dram_in(nc, "lnb", [1, 512])
    mg_d = dram_in(nc, "maskG", [128, 128])
    mnt_d = dram_in(nc, "maskNT", [64, 64])
    rm_d = dram_in(nc, "resetm", [64, MC])
    p_out = dram_out(nc, "p", [T, D])

    ps_tr = mk(nc, es, "ps_tr", [128, 1024], BF16, psum=True)
    ps_proj = mk(nc, es, "ps_proj", [128, 512], F32, psum=True)
    ps_tok = mk(nc, es, "ps_tok", [128, 512], F32, psum=True)
    ps_bv = mk(nc, es, "ps_bv", [128, 512], F32, psum=True)
    ps_g = mk(nc, es, "ps_g", [128, 512], F32, psum=True)
    ps_n = mk(nc, es, "ps_n", [128, 512], F32, psum=True)
    ps_rec = mk(nc, es, "ps_rec", [128, 512], F32, psum=True)
    ps_y = mk(nc, es, "ps_y", [128, 512], F32, psum=True)

    g_bc = mk(nc, es, "g_bc", [128, D], F32)
    identf = mk(nc, es, "identf", [128, 128], F32)
    ident = mk(nc, es, "identb", [128, 128], BF16)
    S.op("sp", lambda h: h.dma_start(out=g_bc[:], in_=g_row.partition_broadcast(128)), w=["g"], dma=True)
    S.op("sp", lambda h: h.dma_start(out=identf[:], in_=ident_d), w=["identf"], dma=True)
    S.op("dve", lambda h: h.tensor_copy(out=ident[:], in_=identf[:]), r=["identf"], w=["ident"])
    W4 = mk(nc, es, "W4", [128, 4, 8, 512], BF16)
    W4m = mk(nc, es, "W4m", [128, 4, 8, 512], BF16)
    LW = mk(nc, es, "LW", [128, 2, 8, 64], BF16)
    LWm = mk(nc, es, "LWm", [128, 2, 8, 64], BF16)
    L2 = mk(nc, es, "L2", [64, 2, 512], BF16)
    wo = mk(nc, es, "wo_s", [128, 4, D], BF16)
    muT = mk(nc, es, "muT_s", [128, 6, 8], F32)
    vec = mk(nc, es, "vec_s", [64, 8, 8], F32)
    lnw = mk(nc, es, "lnw_s", [64, 512], F32)
    lnb = mk(nc, es, "lnb_s", [64, 512], F32)
    maskG = mk(nc, es, "maskG_s", [128, 128], F32)
    maskNT = mk(nc, es, "maskNT_s", [64, 64], F32)
    resetm = mk(nc, es, "resetm_s", [64, MC], F32)
    ones64 = mk(nc, es, "ones64", [64, 64], F32)
    S.op("pool", lambda h: h.dma_start(out=W4[:], in_=w4_d.rearrange("s (c p) n -> p s c n", p=128)), w=["W4"], dma=True)
    S.op("pool", lambda h: h.dma_start(out=LW[:], in_=lw_d.rearrange("s (c p) n -> p s c n", p=128)), w=["LW"], dma=True)
    S.op("pool", lambda h: h.dma_start(out=L2[:], in_=l2_d.rearrange("s k n -> k s n")), w=["L2"], dma=True)
    S.op("pool", lambda h: h.dma_start(out=wo[:], in_=wo_d.rearrange("(c p) n -> p c n", p=128)), w=["wo"], dma=True)
    S.op("sp", lambda h: h.dma_start(out=muT[:], in_=mu_d), w=["muT"], dma=True)
    S.op("sp", lambda h: h.dma_start(out=vec[:], in_=vec_d), w=["vec"], dma=True)
    S.op("sp", lambda h: h.dma_start(out=lnw[:], in_=lnw_d.partition_broadcast(64)), w=["lnw"], dma=True)
    S.op("sp", lambda h: h.dma_start(out=lnb[:], in_=lnb_d.partition_broadcast(64)), w=["lnb"], dma=True)
    S.op("sp", lambda h: h.dma_start(out=maskG[:], in_=mg_d), w=["maskG"], dma=True)
    S.op("sp", lambda h: h.dma_start(out=maskNT[:], in_=mnt_d), w=["maskNT"], dma=True)
    S.op("sp", lambda h: h.dma_start(out=resetm[:], in_=rm_d), w=["resetm"], dma=True)
    S.op("dve", lambda h: h.memset(ones64[:], 1.0), w=["ones64"])
    for s in range(4):
        for dc in range(8):
            S.op("pool" if dc % 2 else "dve", lambda h, s=s, dc=dc: h.tensor_scalar(out=W4m[:, s, dc, :], in0=W4[:, s, dc, :], scalar1=muT[:, s, dc:dc + 1], scalar2=None, op0=ALU.mult),
                 r=["W4", "muT"], w=["W4m"])
    for s in range(2):
        for dc in range(8):
            S.op("dve", lambda h, s=s, dc=dc: h.tensor_scalar(out=LWm[:, s, dc, :], in0=LW[:, s, dc, :], scalar1=muT[:, 4 + s, dc:dc + 1], scalar2=None, op0=ALU.mult),
                 r=["LW", "muT"], w=["LWm"])

    XW = 64 + MC
    xnT = mk(nc, es, "xnT", [128, 8, XW], BF16)
    xxT = mk(nc, es, "xxT", [128, 8, XW], BF16)
    S.op("dve", lambda h: h.memset(xnT[:, :, 0:64], 0.0), w=["xnT"])
    NB = 2
    xt = [mk(nc, es, f"xt{b}", [128, D], F32) for b in range(NB)]
    sq = mk(nc, es, "sq", [128, D], BF16)
    xnb = [mk(nc, es, f"xnb{b}", [128, D], BF16) for b in range(NB)]
    st = [mk(nc, es, f"st{b}", [128, 4], F32) for b in range(NB)]
    h1T = mk(nc, es, "h1T", [64, 2, MC], BF16)
    vwin = mk(nc, es, "vwin", [64, NJ, 512], F32)
    uT = mk(nc, es, "uT", [64, NJ, 512], F32)
    zs = mk(nc, es, "zs", [64, NJ, 512], F32)
    y_all = mk(nc, es, "y_all", [64, NJ, 512], F32)
    bv_all = mk(nc, es, "bv_all", [64, NJ, 512], F32)

    def ft(nm):
        return mk(nc, es, nm, [64, MC], F32)
    r_f = ft("r_f"); k_f = ft("k_f"); sig = ft("sig"); alp = ft("alp"); kk = ft("kk"); t1 = ft("t1"); t2 = ft("t2")
    cs = ft("cs"); kmod = ft("kmod"); bal = ft("bal"); e1 = ft("e1"); e2 = ft("e2"); e3 = ft("e3"); e4 = ft("e4")
    cLs = mk(nc, es, "cLs", [64, NJ], F32)
    G2 = 3
    cLd = [mk(nc, es, f"cLd{i}", [64, NJ], F32) for i in range(G2)]
    AR = [mk(nc, es, f"AR{i}", [64, NJ, 128], F32) for i in range(G2)]
    BK = [mk(nc, es, f"BK{i}", [64, NJ, 128], F32) for i in range(G2)]
    BKe = [mk(nc, es, f"BKe{i}", [64, NJ, 128], F32) for i in range(G2)]
    Gm = [mk(nc, es, f"Gm{i}", [64, NJ, 256], F32) for i in range(G2)]
    Tm = [mk(nc, es, f"Tm{i}", [64, NJ, 64], F32) for i in range(G2)]
    BKeT = [mk(nc, es, f"BKeT{i}", [64, NJ, 128], F32) for i in range(G2)]
    dcL = [mk(nc, es, f"dcL{i}", [64, NJ, 64], F32) for i in range(G2)]
    rkrp = [mk(nc, es, f"rkrp{i}", [64, NJ, 64], F32) for i in range(G2)]
    Dg = [mk(nc, es, f"Dg{i}", [64, NJ, 64], F32) for i in range(G2)]
    bon = [mk(nc, es, f"bon{i}", [64, NJ], F32) for i in range(G2)]
    Nk = [[mk(nc, es, f"Nk{j}_{i}", [64, 64], F32) for i in range(2)] for j in range(NJ)]
    NkT = [[mk(nc, es, f"NkT{j}_{i}", [64, 64], F32) for i in range(2)] for j in range(NJ)]
    Pm = [[mk(nc, es, f"Pm{j}_{i}", [64, 64], F32) for i in range(2)] for j in range(NJ)]
    ST = [[mk(nc, es, f"ST{h}_{i}", [64, 64], F32) for i in range(2)] for h in range(8)]
    WT = [mk(nc, es, f"WT{i}", [64, 64], F32) for i in range(2)]
    for h in range(8):
        S.op("dve", lambda hh, h=h: hh.memset(ST[h][0][:], 0.0), w=[f"ST{h}_0"])
    yn = mk(nc, es, "yn", [64, 512], F32)
    gst = mk(nc, es, "gst", [64, 4, 8], F32)
    yz = mk(nc, es, "yz", [64, 512], BF16)
    yzT = mk(nc, es, "yzT", [128, 4, 64], BF16)
    pt = mk(nc, es, "pt", [64, D], F32)

    def c3(t_):
        return t_[:].rearrange("p (c j) -> p c j", j=64)

    for mc in range(NMC):
        T0 = mc * MC
        if mc > 0:
            S.op("pool", lambda h: h.tensor_copy(out=xnT[:, :, 0:64], in_=xnT[:, :, MC:MC + 64]), r=["xnT"], w=["xnT"])
        for tl in range(MC // 128):
            t = (T0 // 128) + tl
            b = t % NB
            rows = slice(t * 128, (t + 1) * 128)
            for k, src in enumerate(srcs):
                if k == 0:
                    S.op("pool", lambda h, src=src, b=b, t=t: h.dma_start(out=xt[b][:], in_=src_rows(src, t)), r=src_bufs(src, t), w=[f"xt{b}"], dma=True)
                else:
                    S.op("pool", lambda h, src=src, b=b, t=t: h.dma_start(out=xt[b][:], in_=src_rows(src, t), accum_op=ALU.add),
                         r=[f"xt{b}"] + src_bufs(src, t), w=[f"xt{b}"], dma=True)
            if xs_out is not None:
                S.op("sp", lambda h, b=b, rows=rows: h.dma_start(out=xs_out[rows, :], in_=xt[b][:]), r=[f"xt{b}"], dma=True)
            S.op("act", lambda h, b=b: h.activation(out=sq[:], in_=xt[b][:], func=AF.Square), r=[f"xt{b}"], w=["sq"])
            S.op("dve", lambda h, b=b: h.tensor_reduce(out=st[b][:, 0:1], in_=sq[:], axis=AX.X, op=ALU.add), r=["sq"], w=[f"st{b}"])
            S.op("act", lambda h, b=b: h.activation(out=st[b][:, 1:2], in_=st[b][:, 0:1], func=AF.Sqrt, scale=1.0 / D, bias=EPS), r=[f"st{b}"], w=[f"st{b}"])
            S.op("dve", lambda h, b=b: h.reciprocal(out=st[b][:, 2:3], in_=st[b][:, 1:2]), r=[f"st{b}"], w=[f"st{b}r"])
            S.op("dve", lambda h, b=b: h.scalar_tensor_tensor(out=xnb[b][:], in0=xt[b][:], scalar=st[b][:, 2:3], in1=g_bc[:], op0=ALU.mult, op1=ALU.mult),
                 r=[f"xt{b}", f"st{b}r", "g"], w=[f"xnb{b}"])
            for dc in range(8):
                S.op("pe", lambda h, dc=dc, b=b: h.transpose(out=ps_tr[:, dc * 128:(dc + 1) * 128], in_=xnb[b][:, dc * 128:(dc + 1) * 128], identity=ident[:]),
                     r=[f"xnb{b}", "ident"], w=["ps_tr"])
            S.op("act", lambda h, tl=tl: h.copy(out=xnT[:, :, 64 + tl * 128:64 + (tl + 1) * 128], in_=ps_tr[:].rearrange("p (c n) -> p c n", c=8)),
                 r=["ps_tr"], w=["xnT"])
        S.op("pool", lambda h: h.tensor_tensor(out=xxT[:, :, 1:XW], in0=xnT[:, :, 0:XW - 1], in1=xnT[:, :, 1:XW], op=ALU.subtract), r=["xnT"], w=["xxT"])
        tokc = slice(64, 64 + MC)

        def proj_fm(S, ps_ap, Wt, Wm, sidx, cols, M):
            n = 0
            for (Wx, X, xb) in ((Wt, xnT, "xnT"), (Wm, xxT, "xxT")):
                for dc in range(8):
                    S.op("pe", lambda h, Wx=Wx, X=X, dc=dc, n=n: h.matmul(ps_ap, lhsT=Wx[:, sidx, dc, cols], rhs=X[:, dc, tokc], start=(n == 0), stop=(n == 15)),
                         r=["W4", "W4m", "LW", "LWm", xb], w=["ps_proj"])
                    n += 1
        for s in range(2):
            proj_fm(S, ps_proj[0:64, 0:MC], LW, LWm, s, slice(0, 64), 64)
            S.op("act", lambda h, s=s: h.activation(out=h1T[:, s, :], in_=ps_proj[0:64, 0:MC], func=(AF.Tanh if s == 0 else AF.Copy)), r=["ps_proj"], w=["h1T"])
        for j in range(NJ):
            n = 0
            for (Wi, X, xb) in ((W4, xnT, "xnT"), (W4m, xxT, "xxT")):
                for dc in range(8):
                    S.op("pe", lambda h, Wi=Wi, X=X, dc=dc, n=n, j=j: h.matmul(ps_tok[0:64, :], lhsT=X[:, dc, 64 + j * 64:128 + j * 64], rhs=Wi[:, 2, dc, :], start=(n == 0), stop=(n == 15)),
                         r=["W4", "W4m", xb], w=["ps_tok"])
                    n += 1
            S.op("act", lambda h, j=j: h.copy(out=vwin[:, j, :], in_=ps_tok[0:64, :]), r=["ps_tok"], w=[f"vwin{j}"])
            n = 0
            for (Wi, X, xb) in ((W4, xnT, "xnT"), (W4m, xxT, "xxT")):
                for dc in range(8):
                    S.op("pe", lambda h, Wi=Wi, X=X, dc=dc, n=n, j=j: h.matmul(ps_tok[0:64, :], lhsT=X[:, dc, 64 + j * 64:128 + j * 64], rhs=Wi[:, 3, dc, :], start=(n == 0), stop=(n == 15)),
                         r=["W4", "W4m", xb], w=["ps_tok"])
                    n += 1
            S.op("act", lambda h, j=j: h.activation(out=zs[:, j, :], in_=ps_tok[0:64, :], func=AF.Silu), r=["ps_tok"], w=["zs"])

        def head_prep(S, hd):
            gi = hd % G2
            hc = slice(hd * 64, (hd + 1) * 64)
            X = Stream()
            Y = Stream()
            proj_fm(X, ps_proj[0:64, 0:MC], W4, W4m, 0, hc, 64)
            X.op("act", lambda h: h.copy(out=r_f[:], in_=ps_proj[0:64, 0:MC]), r=["ps_proj"], w=["r_f"])
            proj_fm(X, ps_proj[0:64, 0:MC], W4, W4m, 1, hc, 64)
            X.op("act", lambda h: h.copy(out=k_f[:], in_=ps_proj[0:64, 0:MC]), r=["ps_proj"], w=["k_f"])
            X.op("dve", lambda h, hd=hd: h.tensor_scalar(out=kk[:], in0=k_f[:], scalar1=vec[:, hd, 2:3], scalar2=None, op0=ALU.mult), r=["k_f", "vec"], w=["kk"])
            X.op("pool", lambda h: h.tensor_tensor(out=t1[:], in0=kk[:], in1=kk[:], op=ALU.mult), r=["kk"], w=["t1"])
            X.op("pe", lambda h: h.matmul(ps_proj[0:64, 0:MC], lhsT=ones64[:], rhs=t1[:], start=True, stop=True), r=["ones64", "t1"], w=["ps_proj"])
            X.op("act", lambda h: h.activation(out=t2[:], in_=ps_proj[0:64, 0:MC], func=AF.Sqrt), r=["ps_proj"], w=["t2"])
            X.op("dve", lambda h: h.tensor_scalar(out=t2[:], in0=t2[:], scalar1=1e-12, scalar2=None, op0=ALU.max), r=["t2"], w=["t2"])
            X.op("dve", lambda h: h.reciprocal(out=t2[:], in_=t2[:]), r=["t2"], w=["t2"])
            X.op("dve", lambda h: h.tensor_tensor(out=kk[:], in0=kk[:], in1=t2[:], op=ALU.mult), r=["kk", "t2"], w=["kk"])
            Y.op("pe", lambda h, hc=hc: h.matmul(ps_bv[0:64, 0:MC], lhsT=L2[:, 0, hc], rhs=h1T[:, 0, :], start=True, stop=True), r=["L2", "h1T"], w=["ps_bv"])
            Y.op("act", lambda h, hd=hd: h.activation(out=sig[:], in_=ps_bv[0:64, 0:MC], func=AF.Sigmoid, bias=vec[:, hd, 0:1]), r=["ps_bv", "vec"], w=["sig"])
            Y.op("pe", lambda h, hc=hc: h.matmul(ps_bv[0:64, 0:MC], lhsT=L2[:, 1, hc], rhs=h1T[:, 1, :], start=True, stop=True), r=["L2", "h1T"], w=["ps_bv"])
            Y.op("act", lambda h, hd=hd: h.activation(out=alp[:], in_=ps_bv[0:64, 0:MC], func=AF.Sigmoid, bias=vec[:, hd, 1:2]), r=["ps_bv", "vec"], w=["alp"])
            Y.op("dve", lambda h: h.tensor_tensor_scan(out=cs[:], data0=resetm[:], data1=sig[:], initial=0.0, op0=ALU.mult, op1=ALU.add), r=["resetm", "sig"], w=["cs"])
            Y.op("dve", lambda h: h.tensor_copy(out=cLs[:], in_=cs[:, 63::64]), r=["cs"], w=["cLs"])
            Y.op("act", lambda h, gi=gi: h.activation(out=cLd[gi][:], in_=cLs[:], func=AF.Exp, scale=-KAP), r=["cLs"], w=[f"cLd{gi}"])
            Y.op("act", lambda h: h.activation(out=e1[:], in_=cs[:], func=AF.Exp, scale=-KAP), r=["cs"], w=["e1"])
            Y.op("act", lambda h: h.activation(out=e2[:], in_=cs[:], func=AF.Exp, scale=KAP), r=["cs"], w=["e2"])
            Y.op("pool", lambda h: h.tensor_tensor(out=e3[:], in0=cs[:], in1=sig[:], op=ALU.subtract), r=["cs", "sig"], w=["e3"])
            Y.op("act", lambda h: h.activation(out=e3[:], in_=e3[:], func=AF.Exp, scale=-KAP), r=["e3"], w=["e3"])
            Y.op("dve", lambda h: h.tensor_tensor(out=c3(e4), in0=c3(cs), in1=cLs[:].unsqueeze(2).broadcast_to([64, NJ, 64]), op=ALU.subtract), r=["cs", "cLs"], w=["e4"])
            Y.op("act", lambda h: h.activation(out=e4[:], in_=e4[:], func=AF.Exp, scale=KAP), r=["e4"], w=["e4"])
            merge_streams(S, [X, Y])
            S.op("dve", lambda h, hd=hd: h.tensor_scalar(out=t1[:], in0=alp[:], scalar1=1.0, scalar2=vec[:, hd, 3:4], op0=ALU.subtract, op1=ALU.mult), r=["alp", "vec"], w=["t1"])
            S.op("dve", lambda h: h.scalar_tensor_tensor(out=kmod[:], in0=t1[:], scalar=1.0, in1=k_f[:], op0=ALU.add, op1=ALU.mult), r=["t1", "k_f"], w=["kmod"])
            S.op("pool", lambda h: h.tensor_tensor(out=bal[:], in0=kk[:], in1=alp[:], op=ALU.mult), r=["kk", "alp"], w=["bal"])
            S.op("pool", lambda h, gi=gi: h.tensor_tensor(out=AR[gi][:, :, 64:128], in0=c3(r_f), in1=c3(e1), op=ALU.mult), r=["r_f", "e1"], w=[f"AR{gi}"])
            S.op("dve", lambda h, gi=gi: h.scalar_tensor_tensor(out=AR[gi][:, :, 0:64], in0=c3(kk), scalar=-1.0, in1=c3(e3), op0=ALU.mult, op1=ALU.mult),
                 r=["kk", "e3"], w=[f"AR{gi}"])
            S.op("dve", lambda h, gi=gi: h.tensor_tensor(out=BK[gi][:, :, 0:64], in0=c3(bal), in1=c3(e2), op=ALU.mult), r=["bal", "e2"], w=[f"BK{gi}"])
            S.op("pool", lambda h, gi=gi: h.tensor_tensor(out=BK[gi][:, :, 64:128], in0=c3(kmod), in1=c3(e2), op=ALU.mult), r=["kmod", "e2"], w=[f"BK{gi}"])
            S.op("dve", lambda h, gi=gi: h.tensor_tensor(out=BKe[gi][:, :, 0:64], in0=c3(bal), in1=c3(e4), op=ALU.mult), r=["bal", "e4"], w=[f"BKe{gi}"])
            S.op("pool", lambda h, gi=gi: h.tensor_tensor(out=BKe[gi][:, :, 64:128], in0=c3(kmod), in1=c3(e4), op=ALU.mult), r=["kmod", "e4"], w=[f"BKe{gi}"])
            S.op("dve", lambda h, gi=gi, hd=hd: h.scalar_tensor_tensor(out=rkrp[gi][:, :, :], in0=c3(r_f), scalar=vec[:, hd, 4:5], in1=c3(kmod), op0=ALU.mult, op1=ALU.mult),
                 r=["r_f", "kmod", "vec"], w=[f"rkrp{gi}"])
        def head_gn(S, hd):
            gi = hd % G2
            hc = slice(hd * 64, (hd + 1) * 64)
            def head_g(S, j):
                psn = ps_n if j == 0 else ps_tok
                psn_name = "ps_n" if j == 0 else "ps_tok"
                Nk_, NkT_, Pm_ = Nk[j], NkT[j], Pm[j]
                S.op("pe", lambda h, gi=gi, j=j: h.matmul(ps_g[0:64, 0:128], lhsT=BK[gi][:, j, 0:64], rhs=AR[gi][:, j, :], start=True, stop=True), r=[f"BK{gi}", f"AR{gi}"], w=["ps_g"])
                S.op("pe", lambda h, gi=gi, j=j: h.matmul(ps_g[0:64, 128:256], lhsT=BK[gi][:, j, 64:128], rhs=AR[gi][:, j, :], start=True, stop=True), r=[f"BK{gi}", f"AR{gi}"], w=["ps_g"])
                S.op("pe", lambda h, gi=gi, j=j: h.matmul(ps_g[0:64, 256:320], lhsT=AR[gi][:, j, 0:64], rhs=BK[gi][:, j, 0:64], start=True, stop=True), r=[f"BK{gi}", f"AR{gi}"], w=["ps_g"])
                S.op("dve", lambda h, gi=gi, j=j: h.tensor_tensor(out=Gm[gi][:, j, 0:128], in0=ps_g[0:64, 0:128], in1=maskG[0:64, :], op=ALU.mult), r=["ps_g", "maskG"], w=[f"Gm{gi}"])
                S.op("dve", lambda h, gi=gi, j=j: h.tensor_tensor(out=Gm[gi][:, j, 128:256], in0=ps_g[0:64, 128:256], in1=maskG[0:64, :], op=ALU.mult), r=["ps_g", "maskG"], w=[f"Gm{gi}"])
                S.op("dve", lambda h: h.tensor_tensor(out=NkT_[0][:], in0=ps_g[0:64, 256:320], in1=maskNT[:], op=ALU.mult), r=["ps_g", "maskNT"], w=[f"NkT{j}_0"])
                S.op("pool", lambda h, gi=gi, j=j: h.tensor_copy(out=Nk_[0][:], in_=Gm[gi][:, j, 0:64]), r=[f"Gm{gi}"], w=[f"Nk{j}_0"])
                S.op("pool", lambda h, gi=gi, j=j: h.tensor_tensor(out=Pm_[0][:], in0=Gm[gi][:, j, 0:64], in1=identf[0:64, 0:64], op=ALU.add), r=[f"Gm{gi}", "identf"], w=[f"Pm{j}_0"])
                for q in range(2):
                    S.op("pe", lambda h, gi=gi, j=j, q=q: h.transpose(out=ps_g[0:64, 320 + q * 64:384 + q * 64], in_=BKe[gi][:, j, q * 64:(q + 1) * 64], identity=identf[0:64, 0:64]), r=[f"BKe{gi}", "identf"], w=["ps_g"])
                S.op("act", lambda h, gi=gi, j=j: h.copy(out=BKeT[gi][:, j, :], in_=ps_g[0:64, 320:448]), r=["ps_g"], w=[f"BKeT{gi}"])
                S.op("pool", lambda h, gi=gi, j=j: h.tensor_scalar(out=dcL[gi][:, j, :], in0=identf[0:64, 0:64], scalar1=cLd[gi][:, j:j + 1], scalar2=None, op0=ALU.mult), r=["identf", f"cLd{gi}"], w=[f"dcL{gi}"])
                S.op("pe", lambda h, gi=gi, j=j: h.matmul(ps_g[0:64, 448 + j:449 + j], lhsT=rkrp[gi][:, j, :], rhs=ones64[:, 0:1], start=True, stop=True), r=[f"rkrp{gi}", "ones64"], w=["ps_g"])
                S.op("act", lambda h, gi=gi, j=j: h.copy(out=bon[gi][:, j:j + 1], in_=ps_g[0:64, 448 + j:449 + j]), r=["ps_g"], w=[f"bon{gi}"])
            def head_n(S, j):
                psn = ps_n if j == 0 else ps_tok
                psn_name = "ps_n" if j == 0 else "ps_tok"
                Nk_, NkT_, Pm_ = Nk[j], NkT[j], Pm[j]
                cur = 0
                for sidx in range(5):
                    nx = 1 - cur
                    last = (sidx == 4)
                    S.op("pe", lambda h, cur=cur: h.matmul(psn[0:64, 0:64], lhsT=Nk_[cur][:], rhs=NkT_[cur][:], start=True, stop=True), r=[f"Nk{j}_{cur}", f"NkT{j}_{cur}"], w=[psn_name])
                    S.op("act", lambda h, nx=nx: h.copy(out=NkT_[nx][:], in_=psn[0:64, 0:64]), r=[psn_name], w=[f"NkT{j}_{nx}"])
                    if not last:
                        S.op("pe", lambda h, cur=cur: h.matmul(psn[0:64, 64:128], lhsT=NkT_[cur][:], rhs=Nk_[cur][:], start=True, stop=True), r=[f"Nk{j}_{cur}", f"NkT{j}_{cur}"], w=[psn_name])
                        S.op("act", lambda h, nx=nx: h.copy(out=Nk_[nx][:], in_=psn[0:64, 64:128]), r=[psn_name], w=[f"Nk{j}_{nx}"])
                    S.op("pe", lambda h, cur=cur, nx=nx: h.matmul(psn[0:64, 128:192], lhsT=NkT_[nx][:], rhs=Pm_[cur][:], start=True, stop=True), r=[f"NkT{j}_{nx}", f"Pm{j}_{cur}"], w=[psn_name])
                    if last:
                        S.op("dve", lambda h, cur=cur, gi=gi, j=j: h.tensor_tensor(out=Tm[gi][:, j, :], in0=psn[0:64, 128:192], in1=Pm_[cur][:], op=ALU.add), r=[psn_name, f"Pm{j}_{cur}"], w=[f"Tm{gi}"])
                    else:
                        S.op("dve", lambda h, cur=cur, nx=nx: h.tensor_tensor(out=Pm_[nx][:], in0=psn[0:64, 128:192], in1=Pm_[cur][:], op=ALU.add), r=[psn_name, f"Pm{j}_{cur}"], w=[f"Pm{j}_{nx}"])
                    cur = nx
            for j in range(NJ):
                head_g(S, j)
            sj = [Stream() for _ in range(NJ)]
            for j in range(NJ):
                head_n(sj[j], j)
            merge_streams(S, sj)
            for j in range(NJ):
                S.op("dve", lambda h, gi=gi, j=j: h.tensor_scalar(out=Dg[gi][:, j, :], in0=identf[0:64, 0:64], scalar1=bon[gi][:, j:j + 1], scalar2=None, op0=ALU.mult),
                     r=["identf", f"bon{gi}"], w=[f"Dg{gi}"])
        def head_rec(S, hd):
            gi = hd % G2
            hc = slice(hd * 64, (hd + 1) * 64)
            for j in range(NJ):
                gj = mc * NJ + j
                s_in = ST[hd][gj % 2]
                s_out = ST[hd][(gj + 1) % 2]
                sin_n = f"ST{hd}_{gj % 2}"
                sout_n = f"ST{hd}_{(gj + 1) % 2}"
                wi = gj % 2
                VT = vwin[:, j, hc]
                UT = uT[:, j, hc]
                vn = f"vwin{j}"
                un = f"uT{j}_{hd}"
                S.op("pe", lambda h, gi=gi, j=j, VT=VT, hc=hc: h.matmul(ps_rec[0:64, 192:256], lhsT=Dg[gi][:, j, :], rhs=VT, start=True, stop=True), r=[f"Dg{gi}", vn], w=["ps_rec"])
                S.op("act", lambda h, j=j, hc=hc: h.copy(out=bv_all[:, j, hc], in_=ps_rec[0:64, 192:256]), r=["ps_rec"], w=["bv_all"])
                S.op("pe", lambda h, gi=gi, j=j, s_in=s_in: h.matmul(ps_rec[0:64, 0:64], lhsT=AR[gi][:, j, 0:64], rhs=s_in[:], start=True, stop=False), r=[f"AR{gi}", sin_n], w=["ps_rec"])
                S.op("pe", lambda h, gi=gi, j=j, VT=VT: h.matmul(ps_rec[0:64, 0:64], lhsT=Gm[gi][:, j, 128:192], rhs=VT, start=False, stop=True), r=[f"Gm{gi}", vn], w=["ps_rec"])
                S.op("act", lambda h, wi=wi: h.copy(out=WT[wi][:], in_=ps_rec[0:64, 0:64]), r=["ps_rec"], w=[f"WT{wi}"])
                S.op("pe", lambda h, gi=gi, j=j, wi=wi: h.matmul(ps_rec[0:64, 64:128], lhsT=Tm[gi][:, j, :], rhs=WT[wi][:], start=True, stop=True), r=[f"Tm{gi}", f"WT{wi}"], w=["ps_rec"])
                S.op("act", lambda h, UT=UT: h.copy(out=UT, in_=ps_rec[0:64, 64:128]), r=["ps_rec"], w=[un])
                S.op("pe", lambda h, gi=gi, j=j, s_in=s_in, hc=hc: h.matmul(ps_y[0:64, hc], lhsT=AR[gi][:, j, 64:128], rhs=s_in[:], start=True, stop=False), r=[f"AR{gi}", sin_n], w=["ps_y"])
                S.op("pe", lambda h, gi=gi, j=j, hc=hc, UT=UT: h.matmul(ps_y[0:64, hc], lhsT=Gm[gi][:, j, 64:128], rhs=UT, start=False, stop=False), r=[f"Gm{gi}", un], w=["ps_y"])
                S.op("pe", lambda h, gi=gi, j=j, hc=hc, VT=VT: h.matmul(ps_y[0:64, hc], lhsT=Gm[gi][:, j, 192:256], rhs=VT, start=False, stop=True), r=[f"Gm{gi}", vn], w=["ps_y"])
                S.op("dve", lambda h, j=j, hc=hc: h.tensor_copy(out=y_all[:, j, hc], in_=ps_y[0:64, hc]), r=["ps_y"], w=["y_all"])
                S.op("pe", lambda h, gi=gi, j=j, s_in=s_in: h.matmul(ps_rec[0:64, 128:192], lhsT=dcL[gi][:, j, :], rhs=s_in[:], start=True, stop=False), r=[f"dcL{gi}", sin_n], w=["ps_rec"])
                S.op("pe", lambda h, gi=gi, j=j, UT=UT: h.matmul(ps_rec[0:64, 128:192], lhsT=BKeT[gi][:, j, 0:64], rhs=UT, start=False, stop=False), r=[f"BKeT{gi}", un], w=["ps_rec"])
                S.op("pe", lambda h, gi=gi, j=j, VT=VT: h.matmul(ps_rec[0:64, 128:192], lhsT=BKeT[gi][:, j, 64:128], rhs=VT, start=False, stop=True), r=[f"BKeT{gi}", vn], w=["ps_rec"])
                S.op("act", lambda h, s_out=s_out: h.copy(out=s_out[:], in_=ps_rec[0:64, 128:192]), r=["ps_rec"], w=[sout_n])
        for step in range(8 + 2):
            streams = []
            if step < 8:
                st_ = Stream()
                head_prep(st_, step)
                streams.append(st_)
            if 0 <= step - 1 < 8:
                st_ = Stream()
                head_gn(st_, step - 1)
                streams.append(st_)
            if 0 <= step - 2 < 8:
                st_ = Stream()
                head_rec(st_, step - 2)
                streams.append(st_)
            merge_streams(S, streams)
        for j in range(NJ):
            y3 = y_all[:, j, :].rearrange("p (h v) -> p h v", h=8)
            S.op("dve", lambda h, y3=y3: h.tensor_reduce(out=gst[:, 0, :], in_=y3, axis=AX.X, op=ALU.add), r=["y_all"], w=["gst"])
            S.op("act", lambda h, j=j: h.activation(out=yn[:], in_=y_all[:, j, :], func=AF.Square), r=["y_all"], w=["yn"])
            S.op("dve", lambda h: h.tensor_reduce(out=gst[:, 1, :], in_=yn[:].rearrange("p (h v) -> p h v", h=8), axis=AX.X, op=ALU.add), r=["yn"], w=["gst"])
            S.op("dve", lambda h: h.tensor_scalar(out=gst[:, 0, :], in0=gst[:, 0, :], scalar1=1.0 / 64, scalar2=None, op0=ALU.mult), r=["gst"], w=["gst"])
            S.op("dve", lambda h: h.tensor_tensor(out=gst[:, 2, :], in0=gst[:, 0, :], in1=gst[:, 0, :], op=ALU.mult), r=["gst"], w=["gst"])
            S.op("dve", lambda h: h.scalar_tensor_tensor(out=gst[:, 1, :], in0=gst[:, 1, :], scalar=1.0 / 64, in1=gst[:, 2, :], op0=ALU.mult, op1=ALU.subtract), r=["gst"], w=["gst"])
            S.op("act", lambda h: h.activation(out=gst[:, 1, :], in_=gst[:, 1, :], func=AF.Sqrt, bias=GN_EPS), r=["gst"], w=["gst"])
            S.op("dve", lambda h: h.reciprocal(out=gst[:, 1, :], in_=gst[:, 1, :]), r=["gst"], w=["gst"])
            for hd in range(8):
                hc = slice(hd * 64, (hd + 1) * 64)
                S.op("dve", lambda h, j=j, hd=hd, hc=hc: h.tensor_scalar(out=yn[:, hc], in0=y_all[:, j, hc], scalar1=gst[:, 0, hd:hd + 1], scalar2=gst[:, 1, hd:hd + 1],
                                                                      op0=ALU.subtract, op1=ALU.mult), r=["y_all", "gst"], w=["yn"])
            S.op("pool", lambda h: h.tensor_tensor(out=yn[:], in0=yn[:], in1=lnw[:], op=ALU.mult), r=["yn", "lnw"], w=["yn"])
            S.op("pool", lambda h: h.tensor_tensor(out=yn[:], in0=yn[:], in1=lnb[:], op=ALU.add), r=["yn", "lnb"], w=["yn"])
            S.op("pool", lambda h, j=j: h.tensor_tensor(out=yn[:], in0=yn[:], in1=bv_all[:, j, :], op=ALU.add), r=["yn", "bv_all"], w=["yn"])
            S.op("dve", lambda h, j=j: h.tensor_tensor(out=yz[:], in0=yn[:], in1=zs[:, j, :], op=ALU.mult), r=["yn", "zs"], w=["yz"])
            for c in range(4):
                S.op("pe", lambda h, c=c: h.transpose(out=ps_tr[:, c * 64:(c + 1) * 64], in_=yz[:, c * 128:(c + 1) * 128], identity=ident[0:64, 0:64]), r=["yz", "ident"], w=["ps_tr"])
            S.op("act", lambda h: h.copy(out=yzT[:], in_=ps_tr[:, 0:256].rearrange("p (c n) -> p c n", c=4)), r=["ps_tr"], w=["yzT"])
            for hf in range(2):
                for c in range(4):
                    S.op("pe", lambda h, hf=hf, c=c: h.matmul(ps_tok[0:64, :], lhsT=yzT[:, c, :], rhs=wo[:, c, hf * 512:(hf + 1) * 512], start=(c == 0), stop=(c == 3)), r=["yzT", "wo"], w=["ps_tok"])
                S.op("act", lambda h, hf=hf: h.copy(out=pt[:, hf * 512:(hf + 1) * 512], in_=ps_tok[0:64, :]), r=["ps_tok"], w=["pt"])
            rows = slice(T0 + j * 64, T0 + (j + 1) * 64)
            S.op("sp", lambda h, rows=rows: h.dma_start(out=p_out[rows, :], in_=pt[:]), r=["pt"], dma=True)
    return nc, es, S


def prep_B(z, half, MC=256):
    d = {}
    w_in = z['b_w_in'][0]
    own = slice(half * 512, (half + 1) * 512)
    d['w4'] = np.ascontiguousarray(np.stack([w_in[:, s * 1024:(s + 1) * 1024][:, own] for s in range(4)]))
    d['lw'] = np.ascontiguousarray(np.stack([z['b_w1'][0], z['b_a1'][0]]))
    d['l2'] = np.ascontiguousarray(np.stack([z['b_w2'][0][:, own], z['b_a2'][0][:, own]]))
    d['wo'] = np.ascontiguousarray(z['b_w_out'][0][own, :])
    mu = z['b_mu'][0]
    d['muT'] = np.ascontiguousarray(mu.reshape(6, 8, 128).transpose(2, 0, 1))
    vecs = np.zeros((64, 8, 8), np.float32)
    def fm(v):
        return v[own].reshape(8, 64).T
    vecs[:, :, 0] = fm(z['b_w0'][0]); vecs[:, :, 1] = fm(z['b_a0'][0]); vecs[:, :, 2] = fm(z['b_k_k'][0]); vecs[:, :, 3] = fm(z['b_k_a'][0])
    vecs[:, :, 4] = fm(z['b_r_k'][0].reshape(-1))
    d['vecs'] = vecs
    d['lnw'] = np.ascontiguousarray(z['b_lnx_w'][0][own][None, :])
    d['lnb'] = np.ascontiguousarray(z['b_lnx_b'][0][own][None, :])
    j = np.arange(64)[:, None]; i = np.arange(64)[None, :]
    strict = (j < i).astype(np.float32); incl = (j <= i).astype(np.float32)
    row = np.concatenate([strict, incl], 1)
    d['maskG'] = np.ascontiguousarray(np.concatenate([row, row], 0))
    d['maskNT'] = np.ascontiguousarray(strict.T)
    rm = np.ones((64, MC), np.float32); rm[:, ::64] = 0.0
    d['resetm'] = rm
    d['g'] = z['norm_g'][1:2].copy()
    d['ident'] = np.eye(128, dtype=np.float32)
    return d


NEGM = -30000.0


def build_C(nsrc=1, debug=False):
    T = CFG.T
    NT = T // 128
    NCMP = T // 16 - 1
    NKT = (NCMP + 127) // 128
    nc = get_nc()
    es = ExitStack()
    S = get_sched(nc, es)
    srcs = [dram_in(nc, f"xin{k}", [T, D]) for k in range(nsrc)]
    xs_out = dram_out(nc, "xs", [T, D]) if nsrc > 1 else None
    g_row = dram_in(nc, "g", [1, D])
    ident_d = dram_in(nc, "ident", [128, 128])
    wq_d = dram_in(nc, "wq", [D, 512])
    wkv_d = dram_in(nc, "wkv", [D, 6, 128])
    wg_d = dram_in(nc, "wg", [D, 24])
    wz_d = dram_in(nc, "wz", [D, 512])
    wo_d = dram_in(nc, "wo", [512, D])
    w1_d = dram_in(nc, "w1", [2, 64, 32, 128])
    w2_d = dram_in(nc, "w2", [2, 128, 64])
    pos_d = dram_in(nc, "posT", [2, 64, 32])
    bias_d = dram_in(nc, "biasT", [128, 4, 1024])
    F4_d = dram_in(nc, "F4", [512, 512])
    ka_d = dram_in(nc, "keepadd", [NT, 128, 128])
    E_d = dram_in(nc, "E", [64, NT, 128])
    ov_d = dram_in(nc, "ovl", [128, 2, 64])
    p_out = dram_out(nc, "p", [T, D])

    ps_tr = mk(nc, es, "ps_tr", [128, 1024], BF16, psum=True)
    ps_q = mk(nc, es, "ps_q", [128, 512], F32, psum=True)
    ps_z = mk(nc, es, "ps_z", [128, 512], F32, psum=True)
    ps_s = [mk(nc, es, f"ps_s{k}", [128, 512], F32, psum=True) for k in range(2)]
    po_c = mk(nc, es, "po_c", [128, 4, 128], F32, psum=True)
    po_s = mk(nc, es, "po_s", [128, 4, 128], F32, psum=True)
    po_w = mk(nc, es, "po_w", [128, 4, 128], F32, psum=True)
    p1 = P1(S, nc, es, srcs, xs_out, g_row, ident_d, ps_tr)
    ident = p1.ident
    xnTt = [mk(nc, es, f"xnTt{b}", [128, 8, 128], BF16) for b in range(2)]

    wq = mk(nc, es, "wq_s", [128, 8, 512], BF16)
    wkv = mk(nc, es, "wkv_s", [128, 8, 6, 128], BF16)
    wg = mk(nc, es, "wg_s", [128, 8, 24], BF16)
    wz = mk(nc, es, "wz_s", [128, 8, 512], BF16)
    wo = mk(nc, es, "wo_s", [128, 4, D], BF16)
    w1 = mk(nc, es, "w1_s", [64, 2, 32, 128], BF16)
    w2 = mk(nc, es, "w2_s", [128, 2, 64], BF16)
    posT = mk(nc, es, "posT_s", [64, 2, 32], BF16)
    biasT = mk(nc, es, "biasT_s", [128, 4, 1024], BF16)
    Em = mk(nc, es, "E_s", [64, NT, 128], BF16)
    S.op("pool", lambda h: h.dma_start(out=wq[:], in_=wq_d.rearrange("(c p) n -> p c n", p=128)), w=["wq"], dma=True)
    S.op("pool", lambda h: h.dma_start(out=wkv[:], in_=wkv_d.rearrange("(c p) s n -> p c s n", p=128)), w=["wkv"], dma=True)
    S.op("pool", lambda h: h.dma_start(out=wg[:], in_=wg_d.rearrange("(c p) n -> p c n", p=128)), w=["wg"], dma=True)
    S.op("pool", lambda h: h.dma_start(out=wz[:], in_=wz_d.rearrange("(c p) n -> p c n", p=128)), w=["wz"], dma=True)
    S.op("pool", lambda h: h.dma_start(out=wo[:], in_=wo_d.rearrange("(c p) n -> p c n", p=128)), w=["wo"], dma=True)
    S.op("pool", lambda h: h.dma_start(out=w1[:], in_=w1_d.rearrange("s d l h -> d s l h")), w=["w1"], dma=True)
    S.op("pool", lambda h: h.dma_start(out=w2[:], in_=w2_d.rearrange("s h d -> h s d")), w=["w2"], dma=True)
    S.op("pool", lambda h: h.dma_start(out=posT[:], in_=pos_d.rearrange("s d l -> d s l")), w=["posT"], dma=True)
    S.op("pool", lambda h: h.dma_start(out=biasT[:], in_=bias_d), w=["biasT"], dma=True)
    S.op("pool", lambda h: h.dma_start(out=Em[:], in_=E_d), w=["E"], dma=True)

    kvT = mk(nc, es, "kvT", [64, 2, 2, T], BF16)
    roll = mk(nc, es, "roll", [64, 2, 2, 144], BF16)
    vau = mk(nc, es, "vau", [128, NT, 2, 2, 65], BF16)
    S.op("dve", lambda h: h.memset(vau[:, :, :, :, 64:65], 1.0), w=["vau_ones"])
    S.op("dve", lambda h: h.memset(roll[:], 0.0), w=["roll"])
    kcmpT = mk(nc, es, "kcmpT", [64, 2, 256], BF16)
    vcau = mk(nc, es, "vcau", [128, 2, 2, 65], BF16)
    ovl = mk(nc, es, "ovl_s", [128, 2, 64], BF16)
    hidn = mk(nc, es, "hidn", [128, 4, 8], BF16)
    hidv = mk(nc, es, "hidv", [128, 2, 256], BF16)
    pbias = mk(nc, es, "pbias", [128, 2], F32)
    S.op("dve", lambda h: h.memset(kcmpT[:], 0.0), w=["kcmpT"])
    S.op("dve", lambda h: h.memset(vcau[:], 0.0), w=["vcau"])
    S.op("dve", lambda h: h.memset(vcau[:, :, :, 64:65], 1.0), r=["vcau"], w=["vcau"])
    S.op("dve", lambda h: h.memset(hidv[:], 0.0), w=["hidv"])
    S.op("pool", lambda h: h.dma_start(out=ovl[:], in_=ov_d), w=["ovl"], dma=True)
    for s in range(2):
        for l in range(32):
            S.op("pe", lambda h, s=s, l=l: h.matmul(ps_z[:, s:s + 1], lhsT=w1[:, s, l, :], rhs=posT[:, s, l:l + 1], start=(l == 0), stop=(l == 31)),
                 r=["w1", "posT"], w=["ps_z"])
        S.op("act", lambda h, s=s: h.copy(out=pbias[:, s:s + 1], in_=ps_z[:, s:s + 1]), r=["ps_z"], w=["pbias"])

    NB = 2
    qT = [mk(nc, es, f"qT{b}", [64, 2, 4, 128], BF16) for b in range(NB)]
    zs = [mk(nc, es, f"zs{b}", [128, 512], BF16) for b in range(NB)]
    gt = [mk(nc, es, f"gt{b}", [128, 24], F32) for b in range(NB)]
    NP = 4
    pT = [mk(nc, es, f"pT{b}", [128, 512], BF16) for b in range(NP)]
    F4t = [mk(nc, es, f"F4t{b}", [128, 512], BF16) for b in range(2)]
    ka = [mk(nc, es, f"ka{b}", [128, 128], F32) for b in range(NB)]
    imp = mk(nc, es, "imp", [128, 64], F32)
    imp2 = mk(nc, es, "imp2", [128, 64], F32)
    m8 = mk(nc, es, "m8", [128, 16], F32)
    nsel = mk(nc, es, "nsel", [128, 64], BF16)
    nselT = mk(nc, es, "nselT", [64, 4, 128], BF16)
    rden = mk(nc, es, "rden", [128, 3, 4], F32)
    cf = mk(nc, es, "cf", [128, 3, 4], F32)
    y = mk(nc, es, "y", [128, 512], F32)
    yz = [mk(nc, es, f"yz{b}", [128, 512], BF16) for b in range(NB)]
    yzT = [mk(nc, es, f"yzT{b}", [128, 4, 128], BF16) for b in range(NB)]
    pt = [mk(nc, es, f"pt{b}", [128, D], F32) for b in range(NB)]
    pcount = [0]
    scount = [0]

    def st_tile(g, b, lhsT_ap, lhs_bufs, extra, rhs_aug, rhs_bufs, po, first, last, ncol=65):
        si = scount[0] % 2
        scount[0] += 1
        pi = pcount[0] % NP
        pcount[0] += 1
        n_extra = len(extra)
        S.op("pe", lambda h: h.matmul(ps_s[si][:, :], lhsT=lhsT_ap, rhs=qT[b][:, g, :, :].rearrange("p r n -> p (r n)"), start=True, stop=(n_extra == 0)),
             r=lhs_bufs + [f"qT{b}_{g}"], w=[f"ps_s{si}"])
        for j, (el, er, ebufs) in enumerate(extra):
            S.op("pe", lambda h, el=el, er=er, j=j: h.matmul(ps_s[si][:, :], lhsT=el, rhs=er, start=False, stop=(j == n_extra - 1)),
                 r=ebufs, w=[f"ps_s{si}"])
        S.op("act", lambda h: h.activation(out=pT[pi][:], in_=ps_s[si][:, :], func=AF.Exp), r=[f"ps_s{si}"], w=[f"pT{pi}"])
        return pi

    for qt in range(NT):
        b = qt % NB
        tq = slice(qt * 128, (qt + 1) * 128)
        xk = f"xnTt{qt % 2}"
        xn = xnTt[qt % 2]
        S.op("sp", lambda h, b=b, qt=qt: h.dma_start(out=ka[b][:], in_=ka_d[qt]), w=[f"ka{b}"], dma=True)
        p1.tile(qt, xn[:, :, :], xk)
        if qt > 0:
            S.op("pool", lambda h: h.tensor_copy(out=roll[:, :, :, 0:16], in_=roll[:, :, :, 128:144]), r=["roll"], w=["roll"])
        for grp, (wss, psx, nm) in enumerate((((0, 1), ps_q, "ps_q"), ((2, 4), ps_z, "ps_z"))):
            for si, ws in enumerate(wss):
                for g in range(2):
                    c0 = (si * 2 + g) * 128
                    for dc in range(8):
                        S.op("pe", lambda h, ws=ws, g=g, dc=dc, c0=c0, psx=psx, xn=xn: h.matmul(psx[0:64, c0:c0 + 128], lhsT=wkv[:, dc, ws, g * 64:(g + 1) * 64], rhs=xn[:, dc, :],
                                                                                     start=(dc == 0), stop=(dc == 7)), r=["wkv", xk], w=[nm])
            if grp == 0:
                S.op("act", lambda h, psx=psx: h.copy(out=roll[:, :, :, 16:144], in_=psx[0:64, :].rearrange("p (s g n) -> p s g n", s=2, g=2)), r=[nm], w=["roll"])
            else:
                S.op("act", lambda h, psx=psx, tq=tq: h.copy(out=kvT[:, :, :, tq], in_=psx[0:64, :].rearrange("p (s g n) -> p s g n", s=2, g=2)), r=[nm], w=[f"kvT_{qt // 4}"])
        for jj, ws in enumerate((3, 5)):
            for dc in range(8):
                S.op("pe", lambda h, dc=dc, ws=ws, jj=jj, xn=xn: h.matmul(ps_z[:, jj * 128:(jj + 1) * 128], lhsT=xn[:, dc, :], rhs=wkv[:, dc, ws, :],
                                                                     start=(dc == 0), stop=(dc == 7)), r=["wkv", xk], w=["ps_z"])
        S.op("dve", lambda h, qt=qt: h.tensor_copy(out=vau[:, qt, :, :, 0:64], in_=ps_z[:, 0:256].rearrange("p (j g d) -> p j g d", j=2, g=2)),
             r=["ps_z"], w=[f"vau{qt}"])
        m0 = 1 if qt == 0 else 0
        nb = 8 - m0
        n0 = 8 * qt - 1 + m0
        for s in range(2):
            for g in range(2):
                c0 = (s * 2 + g) * 8
                for l in range(32):
                    S.op("pe", lambda h, s=s, g=g, l=l, c0=c0, nb=nb, m0=m0: h.matmul(ps_q[:, c0:c0 + nb], lhsT=w1[:, s, l, :], rhs=roll[:, s, g, l + 16 * m0:l + 16 * 7 + 1:16],
                                                                         start=(l == 0), stop=(l == 31)), r=["w1", "roll"], w=["ps_q"])
        for s in range(2):
            S.op("act", lambda h, s=s, nb=nb: h.activation(out=hidn[:, s * 2:s * 2 + 2, 0:nb], in_=ps_q[:, s * 16:s * 16 + 16].rearrange("p (g n) -> p g n", g=2)[:, :, 0:nb],
                                                    func=AF.Silu, bias=pbias[:, s:s + 1]), r=["ps_q", "pbias"], w=["hidn"])
        for g in range(2):
            S.op("pe", lambda h, g=g, nb=nb: h.matmul(ps_z[0:64, g * 8:g * 8 + nb], lhsT=w2[:, 0, :], rhs=hidn[:, g, 0:nb], start=True, stop=True), r=["w2", "hidn"], w=["ps_z"])
        S.op("act", lambda h, nb=nb, n0=n0: h.copy(out=kcmpT[:, :, n0:n0 + nb], in_=ps_z[0:64, 0:16].rearrange("p (g n) -> p g n", g=2)[:, :, 0:nb]), r=["ps_z"], w=["kcmpT"])
        S.op("pool", lambda h, nb=nb, n0=n0: h.tensor_copy(out=hidv[:, :, n0:n0 + nb], in_=hidn[:, 2:4, 0:nb]), r=["hidn"], w=["hidv"])
        for nt in sorted(set([n0 // 128, (n0 + nb - 1) // 128])):
            for g in range(2):
                S.op("pe", lambda h, g=g, nt=nt: h.matmul(ps_z[:, 128 + g * 64:192 + g * 64], lhsT=hidv[:, g, nt * 128:(nt + 1) * 128], rhs=w2[:, 1, :], start=True, stop=True),
                     r=["w2", "hidv"], w=["ps_z"])
            S.op("act", lambda h, nt=nt: h.copy(out=vcau[:, nt, :, 0:64], in_=ps_z[:, 128:256].rearrange("p (g d) -> p g d", g=2)), r=["ps_z"], w=["vcau"])
        for g in range(2):
            for r in range(4):
                for dc in range(8):
                    col = (g * 4 + r) * 64
                    S.op("pe", lambda h, g=g, r=r, dc=dc, col=col, xn=xn: h.matmul(ps_q[0:64, r * 128:(r + 1) * 128], lhsT=wq[:, dc, col:col + 64],
                                                                               rhs=xn[:, dc, :], start=(dc == 0), stop=(dc == 7)),
                         r=["wq", xk], w=["ps_q"])
            S.op("act", lambda h, g=g, b=b: h.activation(out=qT[b][:, g, :, :], in_=ps_q[0:64, :].rearrange("p (r n) -> p r n", r=4),
                                                        func=AF.Copy, scale=0.125), r=["ps_q"], w=[f"qT{b}_{g}"])
        for dc in range(8):
            S.op("pe", lambda h, dc=dc, xn=xn: h.matmul(ps_z[:, :], lhsT=xn[:, dc, :], rhs=wz[:, dc, :], start=(dc == 0), stop=(dc == 7)),
                 r=["wz", xk], w=["ps_z"])
        S.op("act", lambda h, b=b: h.activation(out=zs[b][:], in_=ps_z[:, :], func=AF.Silu), r=["ps_z"], w=[f"zs{b}"])
        for dc in range(8):
            S.op("pe", lambda h, dc=dc, xn=xn: h.matmul(ps_z[:, 0:24], lhsT=xn[:, dc, :], rhs=wg[:, dc, :], start=(dc == 0), stop=(dc == 7)),
                 r=["wg", xk], w=["ps_z"])
        S.op("act", lambda h, b=b: h.activation(out=gt[b][:], in_=ps_z[:, 0:24], func=AF.Sigmoid), r=["ps_z"], w=[f"gt{b}"])
        cnts = []
        for nt in range(NKT):
            mmax = nt * 128 + 127 - 8 * qt
            mmin = nt * 128 - 8 * qt
            if mmin > 6:
                continue
            masked = mmax > -2
            cnts.append((nt, masked))
        for (nt, masked) in cnts:
            if masked:
                j0 = 128 * nt - 8 * qt + 248
                S.op("pool", lambda h, nt=nt, j0=j0: h.dma_start(out=F4t[nt][:], in_=F4_d[j0:j0 + 128, :]), w=[f"F4t{nt}"], dma=True)
        for g in range(2):
            pis = []
            for (nt, masked) in cnts:
                extra = [(ident[:], F4t[nt][:], ["p1id", f"F4t{nt}"])] if masked else []
                pi = st_tile(g, b, kcmpT[:, g, nt * 128:(nt + 1) * 128], ["kcmpT"], extra, None, None, None, None, None)
                pis.append((nt, pi))
            for r in range(4):
                for j, (nt, pi) in enumerate(pis):
                    S.op("pe", lambda h, g=g, r=r, nt=nt, pi=pi, j=j, npi=len(pis): h.matmul(po_c[:, r, 0:65], lhsT=pT[pi][:, r * 128:(r + 1) * 128], rhs=vcau[:, nt, g, :],
                                                                             start=(j == 0), stop=(j == npi - 1)), r=[f"pT{pi}", "vcau"], w=["po_c"])
            for r in range(4):
                for j, (nt, pi) in enumerate(pis):
                    S.op("pe", lambda h, g=g, r=r, nt=nt, pi=pi, j=j, npi=len(pis): h.matmul(po_w[:, r, 0:64], lhsT=pT[pi][:, r * 128:(r + 1) * 128], rhs=ovl[:, nt, :],
                                                                             start=(j == 0), stop=(j == npi - 1)), r=[f"pT{pi}", "ovl"], w=["po_w"])
            S.op("dve", lambda h: h.tensor_scalar(out=rden[:, 0, :], in0=po_c[:, :, 64], scalar1=1e-30, scalar2=None, op0=ALU.add), r=["po_c"], w=["rden0"])
            S.op("dve", lambda h: h.reciprocal(out=rden[:, 0, :], in_=rden[:, 0, :]), r=["rden0"], w=["rden0"])
            S.op("dve", lambda h: h.tensor_scalar(out=imp[:], in0=po_w[:, 0, 0:64], scalar1=rden[:, 0, 0:1], scalar2=None, op0=ALU.mult), r=["po_w", "rden0"], w=["imp"])
            for r in range(1, 4):
                S.op("dve", lambda h, r=r: h.scalar_tensor_tensor(out=imp[:], in0=po_w[:, r, 0:64], scalar=rden[:, 0, r:r + 1], in1=imp[:], op0=ALU.mult, op1=ALU.add),
                     r=["po_w", "rden0", "imp"], w=["imp"])
            S.op("dve", lambda h, b=b: h.tensor_tensor(out=imp[:], in0=imp[:], in1=ka[b][:, 0:64], op=ALU.mult), r=["imp", f"ka{b}"], w=["imp"])
            S.op("dve", lambda h, b=b: h.tensor_tensor(out=imp[:], in0=imp[:], in1=ka[b][:, 64:128], op=ALU.add), r=["imp", f"ka{b}"], w=["imp"])
            S.op("dve", lambda h: h.max(out=m8[:, 0:8], in_=imp[:]), r=["imp"], w=["m8"])
            S.op("dve", lambda h: h.match_replace(out=imp2[:], in_to_replace=m8[:, 0:8], in_values=imp[:], imm_value=-3.0e38), r=["imp", "m8"], w=["imp2"])
            S.op("dve", lambda h: h.max(out=m8[:, 8:16], in_=imp2[:]), r=["imp2"], w=["m8"])
            S.op("dve", lambda h: h.tensor_scalar(out=imp2[:], in0=imp[:], scalar1=m8[:, 15:16], scalar2=1.0, op0=ALU.is_ge, op1=ALU.subtract),
                 r=["imp", "m8"], w=["imp2"])
            S.op("dve", lambda h: h.tensor_scalar(out=nsel[:], in0=imp2[:], scalar1=-NEGM, scalar2=None, op0=ALU.mult), r=["imp2"], w=["nsel"])
            S.op("pe", lambda h: h.transpose(out=ps_tr[0:64, 0:128], in_=nsel[:], identity=ident[:]), r=["nsel", "p1id"], w=["p1pstr"])
            for r in range(4):
                S.op("act", lambda h, r=r: h.copy(out=nselT[:, r, :], in_=ps_tr[0:64, 0:128]), r=["p1pstr"], w=["nselT"])
            kts = [kt for kt in range(qt - 4, qt + 1) if kt >= 0]
            jobs = [("s", kt, kt) for kt in range(qt + 1)] + [("w", kt, jj) for jj, kt in enumerate(kts)]

            def do_S(job, g=g, b=b, qt=qt):
                kind, kt, jj = job
                if kind == "s":
                    cls = 0 if kt == qt else (1 if kt == qt - 1 else 3)
                    extra = [(Em[:, kt, :], nselT[:].rearrange("p r n -> p (r n)"), ["E", "nselT"]),
                             (ident[:], biasT[:, cls, g * 512:(g + 1) * 512], ["p1id", "biasT"])]
                    return st_tile(g, b, kvT[:, 0, g, kt * 128:(kt + 1) * 128], [f"kvT_{kt // 4}"], extra, None, None, None, None, None)
                dq = qt - kt
                cls = 0 if dq == 0 else (1 if dq == 1 else (2 if dq == 4 else 3))
                extra = [(ident[:], biasT[:, cls, g * 512:(g + 1) * 512], ["p1id", "biasT"])]
                return st_tile(g, b, kvT[:, 1, g, kt * 128:(kt + 1) * 128], [f"kvT_{kt // 4}"], extra, None, None, None, None, None)

            def do_PV(job, pi, g=g, qt=qt, nk=len(kts)):
                kind, kt, jj = job
                for r in range(4):
                    if kind == "s":
                        S.op("pe", lambda h, r=r: h.matmul(po_s[:, r, 0:65], lhsT=pT[pi][:, r * 128:(r + 1) * 128], rhs=vau[:, kt, 0, g, :],
                                                           start=(kt == 0 and r == 0), stop=(kt == qt), skip_group_check=True),
                             r=[f"pT{pi}", f"vau{kt}", "vau_ones"], w=["po_s"])
                    else:
                        S.op("pe", lambda h, r=r: h.matmul(po_w[:, r, 0:65], lhsT=pT[pi][:, r * 128:(r + 1) * 128], rhs=vau[:, kt, 1, g, :],
                                                           start=(jj == 0 and r == 0), stop=(jj == nk - 1), skip_group_check=True),
                             r=[f"pT{pi}", f"vau{kt}", "vau_ones"], w=["po_w"])

            pend = None
            for job in jobs:
                pi_ = do_S(job)
                if pend is not None:
                    do_PV(*pend)
                pend = (job, pi_)
            do_PV(*pend)
            S.op("dve", lambda h: h.reciprocal(out=rden[:, 1, :], in_=po_s[:, :, 64]), r=["po_s"], w=["rden1"])
            S.op("dve", lambda h: h.reciprocal(out=rden[:, 2, :], in_=po_w[:, :, 64]), r=["po_w"], w=["rden2"])
            for j in range(3):
                S.op("dve", lambda h, j=j, g=g, b=b: h.tensor_tensor(out=cf[:, j, :], in0=rden[:, j, :], in1=gt[b][:, j * 8 + g * 4:j * 8 + g * 4 + 4], op=ALU.mult),
                     r=[f"rden{j}", f"gt{b}"], w=["cf"])
            for r in range(4):
                col = (g * 4 + r) * 64
                S.op("dve", lambda h, r=r, col=col: h.tensor_scalar(out=y[:, col:col + 64], in0=po_c[:, r, 0:64], scalar1=cf[:, 0, r:r + 1], scalar2=None, op0=ALU.mult),
                     r=["po_c", "cf"], w=["y"])
                S.op("dve", lambda h, r=r, col=col: h.scalar_tensor_tensor(out=y[:, col:col + 64], in0=po_s[:, r, 0:64], scalar=cf[:, 1, r:r + 1], in1=y[:, col:col + 64],
                                                                          op0=ALU.mult, op1=ALU.add), r=["po_s", "cf", "y"], w=["y"])
                S.op("dve", lambda h, r=r, col=col: h.scalar_tensor_tensor(out=y[:, col:col + 64], in0=po_w[:, r, 0:64], scalar=cf[:, 2, r:r + 1], in1=y[:, col:col + 64],
                                                                          op0=ALU.mult, op1=ALU.add), r=["po_w", "cf", "y"], w=["y"])
        S.op("pool", lambda h, b=b: h.tensor_tensor(out=yz[b][:], in0=y[:], in1=zs[b][:], op=ALU.mult), r=["y", f"zs{b}"], w=[f"yz{b}"])
        for c in range(4):
            S.op("pe", lambda h, c=c, b=b: h.transpose(out=ps_tr[:, c * 128:(c + 1) * 128], in_=yz[b][:, c * 128:(c + 1) * 128], identity=ident[:]),
                 r=[f"yz{b}", "p1id"], w=["p1pstr"])
        S.op("act", lambda h, b=b: h.copy(out=yzT[b][:], in_=ps_tr[:, 0:512].rearrange("p (c n) -> p c n", c=4)), r=["p1pstr"], w=[f"yzT{b}"])
        for hf in range(2):
            psy = ps_q if hf == 0 else ps_z
            nm = "ps_q" if hf == 0 else "ps_z"
            for c in range(4):
                S.op("pe", lambda h, hf=hf, c=c, b=b, psy=psy: h.matmul(psy[:, :], lhsT=yzT[b][:, c, :], rhs=wo[:, c, hf * 512:(hf + 1) * 512],
                                                                       start=(c == 0), stop=(c == 3)), r=[f"yzT{b}", "wo"], w=[nm])
            if hf == 0:
                S.op("act", lambda h, b=b, psy=psy: h.copy(out=pt[b][:, 0:512], in_=psy[:, :]), r=[nm], w=[f"pt{b}"])
            else:
                S.op("dve", lambda h, b=b, psy=psy: h.tensor_copy(out=pt[b][:, 512:1024], in_=psy[:, :]), r=[nm], w=[f"pt{b}"])
        S.op("sp", lambda h, b=b, tq=tq: h.dma_start(out=p_out[tq, :], in_=pt[b][:]), r=[f"pt{b}"], dma=True)
    return nc, es, S


def prep_C(z, half, T):
    NT = T // 128
    d = {}
    w_in = z['c_w_in'][0]
    d['wq'] = np.ascontiguousarray(w_in[:, half * 512:(half + 1) * 512])
    kv = []
    for s in range(6):
        base = 1024 + s * 256 + half * 128
        kv.append(w_in[:, base:base + 128])
    d['wkv'] = np.ascontiguousarray(np.stack(kv, 1))
    gcols = np.concatenate([2560 + j * 16 + half * 8 + np.arange(8) for j in range(3)])
    d['wg'] = np.ascontiguousarray(w_in[:, gcols])
    d['wz'] = np.ascontiguousarray(w_in[:, 2608 + half * 512:2608 + (half + 1) * 512])
    d['wo'] = np.ascontiguousarray(z['c_w_out'][0][half * 512:(half + 1) * 512, :])
    w1 = np.stack([z['c_cmp_k_w1'][0], z['c_cmp_v_w1'][0]])
    d['w1'] = np.ascontiguousarray(w1.reshape(2, 32, 64, 128).transpose(0, 2, 1, 3))
    d['w2'] = np.ascontiguousarray(np.stack([z['c_cmp_k_w2'][0], z['c_cmp_v_w2'][0]]))
    d['posT'] = np.ascontiguousarray(np.stack([z['c_cmp_pos_k'][0].T, z['c_cmp_pos_v'][0].T]))
    table = z['t5_table']
    tk = np.arange(128)[:, None]
    tq = np.arange(128)[None, :]
    bias = np.zeros((128, 4, 2, 4, 128), np.float32)
    for g in range(2):
        for r in range(4):
            hh = half * 8 + g * 4 + r
            d0 = tq - tk
            bias[:, 0, g, r, :] = np.where(d0 >= 0, table[t5_bucket_np(d0), hh], NEGM)
            d1 = tq - tk + 128
            bias[:, 1, g, r, :] = table[t5_bucket_np(d1), hh]
            bias[:, 2, g, r, :] = np.where(tq < tk, table[31, hh], NEGM)
            bias[:, 3, g, r, :] = table[31, hh]
    d['biasT'] = bias.reshape(128, 4, 1024)
    j = np.arange(512)[:, None]
    F = np.where(16 * (j - 248) + 31 <= tq, 0.0, NEGM).astype(np.float32)
    d['F4'] = np.ascontiguousarray(np.tile(F, (1, 4)))
    ka = np.zeros((NT, 128, 128), np.float32)
    sblk = np.arange(64)[None, :]
    for qt in range(NT):
        t = qt * 128 + np.arange(128)[:, None]
        cur = t // 64
        forced = (sblk == 0) | (sblk == cur) | (sblk == cur - 1)
        future = sblk * 64 > t
        ka[qt, :, 0:64] = np.where(forced | future, 0.0, 1.0)
        ka[qt, :, 64:128] = np.where(forced, 1e30, np.where(future, -1e30, 0.0))
    d['keepadd'] = ka
    E = np.zeros((64, NT, 128), np.float32)
    for kt in range(NT):
        E[2 * kt, kt, 0:64] = 1.0
        E[2 * kt + 1, kt, 64:128] = 1.0
    d['E'] = E
    n = np.arange(256)[:, None]
    s = np.arange(64)[None, :]
    ov = ((16 * n < 64 * s + 64) & (16 * n + 31 >= 64 * s)).astype(np.float32)
    d['ovl'] = np.ascontiguousarray(ov.reshape(2, 128, 64).transpose(1, 0, 2))
    d['g'] = z['norm_g'][2:3].copy()
    d['ident'] = np.eye(128, dtype=np.float32)
    return d


NBLK = 8
BW = 80


def build_D(nsrc=1, TCH=1024, debug=False):
    T = CFG.T
    nc = get_nc()
    es = ExitStack()
    S = get_sched(nc, es)
    srcs = [dram_in(nc, f"xin{k}", [T, D]) for k in range(nsrc)]
    xs_out = dram_out(nc, "xs", [T, D]) if nsrc > 1 else None
    g_row = dram_in(nc, "g", [1, D])
    ident_d = dram_in(nc, "ident", [128, 128])
    wu_d = dram_in(nc, "wu", [D, NBLK * BW])
    wz_d = dram_in(nc, "wz", [D, NBLK * BW])
    wo_d = dram_in(nc, "wo", [NBLK * BW, D])
    ga_d = dram_in(nc, "ga", [NBLK, BW, BW])
    gx_d = dram_in(nc, "gx", [NBLK, BW, BW])
    vec_d = dram_in(nc, "vecs", [BW, NBLK, 8])
    p_out = dram_out(nc, "p", [T, D])

    xnT = mk(nc, es, "xnT", [128, 8, T], BF16)
    ps = [mk(nc, es, f"ps{k}", [128, 512], F32, psum=True) for k in range(7)]
    ps_tr = mk(nc, es, "ps_tr", [128, 1024], BF16, psum=True)
    phase1(S, nc, es, srcs, xs_out, g_row, ident_d, ps_tr, xnT=xnT)

    wu = mk(nc, es, "wu_s", [128, 8, NBLK * BW], BF16)
    wz = mk(nc, es, "wz_s", [128, 8, NBLK * BW], BF16)
    wo = mk(nc, es, "wo_s", [BW, NBLK, D], BF16)
    ga = mk(nc, es, "ga_s", [BW, NBLK, BW], F32)
    gx = mk(nc, es, "gx_s", [BW, NBLK, BW], F32)
    vec = mk(nc, es, "vec_s", [BW, NBLK, 8], F32)
    der = mk(nc, es, "der_s", [BW, NBLK, 4], F32)
    S.op("pool", lambda h: h.dma_start(out=wu[:], in_=wu_d.rearrange("(c p) n -> p c n", p=128)), w=["wu"], dma=True)
    S.op("pool", lambda h: h.dma_start(out=wz[:], in_=wz_d.rearrange("(c p) n -> p c n", p=128)), w=["wz"], dma=True)
    S.op("pool", lambda h: h.dma_start(out=wo[:], in_=wo_d.rearrange("(b p) n -> p b n", p=BW)), w=["wo"], dma=True)
    S.op("sp", lambda h: h.dma_start(out=ga[:], in_=ga_d.rearrange("b p n -> p b n")), w=["ga"], dma=True)
    S.op("sp", lambda h: h.dma_start(out=gx[:], in_=gx_d.rearrange("b p n -> p b n")), w=["gx"], dma=True)
    S.op("sp", lambda h: h.dma_start(out=vec[:], in_=vec_d), w=["vec"], dma=True)
    S.op("act", lambda h: h.activation(out=der[:, :, 0:1], in_=vec[:, :, 7:8], func=AF.Exp, scale=-1.0), r=["vec"], w=["der"])
    S.op("act", lambda h: h.activation(out=der[:, :, 1:2], in_=der[:, :, 0:1], func=AF.Ln, bias=1.0), r=["der"], w=["der"])
    S.op("act", lambda h: h.mul(out=der[:, :, 2:3], in_=der[:, :, 1:2], mul=-8.0), r=["der"], w=["der"])

    NW = 2
    def wt(nm, cols=TCH, dt=F32):
        return [mk(nc, es, f"{nm}{b}", [BW, cols], dt) for b in range(NW)]
    u_t = wt("u_t", TCH + 3)
    uc_t = wt("uc_t"); zs_t = wt("zs_t"); r_t = wt("r_t"); i_t = wt("i_t"); a_t = r_t; m_t = [mk(nc, es, "m_t0", [BW, TCH], F32)] * NW; h_t = uc_t
    hz = mk(nc, es, "hz", [BW, NBLK, TCH], BF16)
    hlast = mk(nc, es, "hlast", [BW, NBLK], F32)
    uhalo = mk(nc, es, "uhalo", [BW, NBLK, 3], F32)
    pt = [mk(nc, es, "pt0", [128, D], F32)] * 2
    S.op("dve", lambda h: h.memset(hlast[:], 0.0), w=["hlast"])
    for b in range(NW):
        S.op("dve", lambda h, b=b: h.memset(u_t[b][:, 0:3], 0.0), w=[f"u{b}"])
    it = 0
    for tch in range(T // TCH):
        t0 = tch * TCH
        for blk in range(NBLK):
            b = it % NW
            pb = (it - 1) % NW
            it += 1
            cs = slice(blk * BW, (blk + 1) * BW)
            for hf in range(2):
                tk = slice(t0 + hf * 512, t0 + (hf + 1) * 512)
                for dc in range(8):
                    S.op("pe", lambda h, hf=hf, dc=dc, tk=tk, cs=cs: h.matmul(ps[hf][0:BW, :], lhsT=wu[:, dc, cs], rhs=xnT[:, dc, tk],
                                                                         start=(dc == 0), stop=(dc == 7)),
                         r=["wu", f"xnT{(t0 + hf * 512) // 512}"], w=[f"ps{hf}"])
            for hf in range(2):
                tk = slice(t0 + hf * 512, t0 + (hf + 1) * 512)
                for dc in range(8):
                    S.op("pe", lambda h, hf=hf, dc=dc, tk=tk, cs=cs: h.matmul(ps[2 + hf][0:BW, :], lhsT=wz[:, dc, cs], rhs=xnT[:, dc, tk],
                                                                         start=(dc == 0), stop=(dc == 7)),
                         r=["wz", f"xnT{(t0 + hf * 512) // 512}"], w=[f"ps{2 + hf}"])
            S.op("pool", lambda h, b=b, blk=blk: h.tensor_copy(out=u_t[b][:, 0:3], in_=uhalo[:, blk, :]), r=["uhalo%d" % blk], w=[f"u{b}"]) if tch > 0 else None
            for hf in range(2):
                S.op("act", lambda h, hf=hf, b=b: h.copy(out=u_t[b][:, 3 + hf * 512:3 + (hf + 1) * 512], in_=ps[hf][0:BW, :]),
                     r=[f"ps{hf}"], w=[f"u{b}"])
            for hf in range(2):
                S.op("act", lambda h, hf=hf, b=b: h.activation(out=zs_t[b][:, hf * 512:(hf + 1) * 512], in_=ps[2 + hf][0:BW, :], func=AF.Silu),
                     r=[f"ps{2 + hf}"], w=[f"zs{b}"])
            S.op("pool", lambda h, b=b, blk=blk: h.tensor_copy(out=uhalo[:, blk, :], in_=u_t[b][:, TCH:TCH + 3]), r=[f"u{b}"], w=["uhalo%d" % blk])
            if debug and tch == 0 and blk == 0:
                dbg(S, nc, "xnT", xnT[:, :, 0:512], [128, 8, 512], ["xnT0"], BF16)
                dbg(S, nc, "u", u_t[b][:], [BW, TCH + 3], [f"u{b}"])
                dbg(S, nc, "zs", zs_t[b][:], [BW, TCH], [f"zs{b}"])
            S.op("dve", lambda h, b=b, blk=blk: h.tensor_scalar(out=uc_t[b][:], in0=u_t[b][:, 3:3 + TCH], scalar1=vec[:, blk, 3:4], scalar2=vec[:, blk, 4:5],
                                                               op0=ALU.mult, op1=ALU.add), r=[f"u{b}", "vec"], w=[f"uc{b}"])
            for j in range(3):
                S.op("dve", lambda h, b=b, blk=blk, j=j: h.scalar_tensor_tensor(out=uc_t[b][:], in0=u_t[b][:, j:j + TCH], scalar=vec[:, blk, j:j + 1],
                                                                               in1=uc_t[b][:], op0=ALU.mult, op1=ALU.add),
                     r=[f"u{b}", "vec"], w=[f"uc{b}"])
            for hf in range(2):
                S.op("pe", lambda h, hf=hf, b=b, blk=blk: h.matmul(ps[4][0:BW, :] if hf == 0 else ps[5][0:BW, :], lhsT=ga[:, blk, :],
                                                                  rhs=uc_t[b][:, hf * 512:(hf + 1) * 512], start=True, stop=True),
                     r=["ga", f"uc{b}"], w=[f"ps{4 + hf}"])
                S.op("act", lambda h, hf=hf, b=b, blk=blk: h.activation(out=r_t[b][:, hf * 512:(hf + 1) * 512], in_=ps[4 + hf][0:BW, :], func=AF.Sigmoid,
                                                                       bias=vec[:, blk, 5:6]), r=[f"ps{4 + hf}", "vec"], w=[f"r{b}"])
            for hf in range(2):
                S.op("pe", lambda h, hf=hf, b=b, blk=blk: h.matmul(ps[4 + hf][0:BW, :], lhsT=gx[:, blk, :],
                                                                  rhs=uc_t[b][:, hf * 512:(hf + 1) * 512], start=True, stop=True),
                     r=["gx", f"uc{b}"], w=[f"ps{4 + hf}"])
                S.op("act", lambda h, hf=hf, b=b, blk=blk: h.activation(out=i_t[b][:, hf * 512:(hf + 1) * 512], in_=ps[4 + hf][0:BW, :], func=AF.Sigmoid,
                                                                       bias=vec[:, blk, 6:7]), r=[f"ps{4 + hf}", "vec"], w=[f"i{b}"])
            if debug and tch == 0 and blk == 0:
                dbg(S, nc, "uc", uc_t[b][:], [BW, TCH], [f"uc{b}"])
                dbg(S, nc, "r", r_t[b][:], [BW, TCH], [f"r{b}"])
                dbg(S, nc, "i", i_t[b][:], [BW, TCH], [f"i{b}"])
                dbg(S, nc, "der", der[:], [BW, NBLK, 4], ["der"])
            S.op("act", lambda h, b=b, blk=blk: h.activation(out=a_t[b][:], in_=r_t[b][:], func=AF.Exp, scale=der[:, blk, 2:3]),
                 r=[f"r{b}", "der"], w=[f"r{b}"])
            S.op("pool", lambda h, b=b: h.tensor_tensor(out=m_t[b][:], in0=a_t[b][:], in1=a_t[b][:], op=ALU.mult), r=[f"r{b}"], w=["m0"])
            S.op("act", lambda h, b=b: h.activation(out=m_t[b][:], in_=m_t[b][:], func=AF.Sqrt, scale=-1.0, bias=1.0), r=["m0"], w=["m0"])
            S.op("pool", lambda h, b=b: h.tensor_tensor(out=i_t[b][:], in0=i_t[b][:], in1=uc_t[b][:], op=ALU.mult), r=[f"i{b}", f"uc{b}"], w=[f"i{b}"])
            S.op("pool", lambda h, b=b: h.tensor_tensor(out=i_t[b][:], in0=i_t[b][:], in1=m_t[b][:], op=ALU.mult), r=[f"i{b}", "m0"], w=[f"i{b}"])
            if debug and tch == 0 and blk == 0:
                dbg(S, nc, "a", r_t[b][:], [BW, TCH], [f"r{b}"])
                dbg(S, nc, "m", m_t[b][:], [BW, TCH], ["m0"])
                dbg(S, nc, "bt", i_t[b][:], [BW, TCH], [f"i{b}"])
            S.op("dve", lambda h, b=b, blk=blk: h.tensor_tensor_scan(out=h_t[b][:], data0=a_t[b][:], data1=i_t[b][:], initial=hlast[:, blk:blk + 1],
                                                                    op0=ALU.mult, op1=ALU.add), r=[f"r{b}", f"i{b}", "hlast"], w=[f"uc{b}"])
            S.op("dve", lambda h, b=b, blk=blk: h.tensor_copy(out=hlast[:, blk:blk + 1], in_=h_t[b][:, TCH - 1:TCH]), r=[f"uc{b}"], w=["hlast"])
            S.op("dve", lambda h, b=b, blk=blk: h.tensor_tensor(out=hz[:, blk, :], in0=h_t[b][:], in1=zs_t[b][:], op=ALU.mult),
                 r=[f"uc{b}", f"zs{b}"], w=["hz"])
        if debug and tch == 0:
            dbg(S, nc, "hz", hz[:], [BW, NBLK, TCH], ["hz"], BF16)
        for tl in range(TCH // 128):
            pbuf = tl % 2
            for hf in range(2):
                for blk in range(NBLK):
                    S.op("pe", lambda h, tl=tl, hf=hf, blk=blk: h.matmul(ps[hf][:, :], lhsT=hz[:, blk, tl * 128:(tl + 1) * 128],
                                                                        rhs=wo[:, blk, hf * 512:(hf + 1) * 512], start=(blk == 0), stop=(blk == NBLK - 1)),
                         r=["hz", "wo"], w=[f"ps{hf}"])
                S.op("act" if hf == 0 else "dve",
                     (lambda h, hf=hf, pbuf=pbuf: h.copy(out=pt[pbuf][:, hf * 512:(hf + 1) * 512], in_=ps[hf][:, :])) if hf == 0 else
                     (lambda h, hf=hf, pbuf=pbuf: h.tensor_copy(out=pt[pbuf][:, hf * 512:(hf + 1) * 512], in_=ps[hf][:, :])),
                     r=[f"ps{hf}"], w=["pt0"])
            rows = slice(t0 + tl * 128, t0 + (tl + 1) * 128)
            S.op("sp", lambda h, pbuf=pbuf, rows=rows: h.dma_start(out=p_out[rows, :], in_=pt[pbuf][:]), r=["pt0"], dma=True)
    return nc, es, S


def build_F(ntok=2048):
    nc = get_nc()
    es = ExitStack()
    S = get_sched(nc, es)
    srcs = [dram_in(nc, f"xin{k}", [ntok, D]) for k in range(3)]
    g_row = dram_in(nc, "g", [1, D])
    out_d = dram_out(nc, "out", [ntok, D])
    g_bc = mk(nc, es, "g_bc", [128, D], F32)
    S.op("sp", lambda h: h.dma_start(out=g_bc[:], in_=g_row.partition_broadcast(128)), w=["g"], dma=True)
    NB = 2
    xt = [mk(nc, es, f"xt{b}", [128, D], F32) for b in range(NB)]
    sq = mk(nc, es, "sq", [128, D], F32)
    ot = [mk(nc, es, f"ot{b}", [128, D], F32) for b in range(NB)]
    st = [mk(nc, es, f"st{b}", [128, 4], F32) for b in range(NB)]
    for t in range(ntok // 128):
        b = t % NB
        rows = slice(t * 128, (t + 1) * 128)
        for k, src in enumerate(srcs):
            if k == 0:
                S.op("pool", lambda h, src=src, b=b, t=t: h.dma_start(out=xt[b][:], in_=src_rows(src, t)), r=src_bufs(src, t), w=[f"xt{b}"], dma=True)
            else:
                S.op("pool", lambda h, src=src, b=b, t=t: h.dma_start(out=xt[b][:], in_=src_rows(src, t), accum_op=ALU.add), r=[f"xt{b}"] + src_bufs(src, t), w=[f"xt{b}"], dma=True)
        S.op("act", lambda h, b=b: h.activation(out=sq[:], in_=xt[b][:], func=AF.Square), r=[f"xt{b}"], w=["sq"])
        S.op("dve", lambda h, b=b: h.tensor_reduce(out=st[b][:, 0:1], in_=sq[:], axis=AX.X, op=ALU.add), r=["sq"], w=[f"st{b}"])
        S.op("act", lambda h, b=b: h.activation(out=st[b][:, 1:2], in_=st[b][:, 0:1], func=AF.Sqrt, scale=1.0 / D, bias=EPS), r=[f"st{b}"], w=[f"st{b}"])
        S.op("dve", lambda h, b=b: h.reciprocal(out=st[b][:, 2:3], in_=st[b][:, 1:2]), r=[f"st{b}"], w=[f"st{b}r"])
        S.op("dve", lambda h, b=b: h.scalar_tensor_tensor(out=ot[b][:], in0=xt[b][:], scalar=st[b][:, 2:3], in1=g_bc[:], op0=ALU.mult, op1=ALU.mult),
             r=[f"xt{b}", f"st{b}r", "g"], w=[f"ot{b}"])
        S.op("sp", lambda h, b=b, rows=rows: h.dma_start(out=out_d[rows, :], in_=ot[b][:]), r=[f"ot{b}"], dma=True)
    return nc, es, S


def prep_D(z, half):
    LW = 1280
    blks = list(range(half * 8, half * 8 + 8))
    cols = np.concatenate([np.arange(b * 80, (b + 1) * 80) for b in blks])
    w_in = z['d_w_in'][0]
    d = {}
    d['wu'] = np.ascontiguousarray(w_in[:, cols])
    d['wz'] = np.ascontiguousarray(w_in[:, LW + cols])
    d['wo'] = np.ascontiguousarray(z['d_w_out'][0][cols, :])
    d['ga'] = np.ascontiguousarray(z['d_gate_a_w'][0][blks])
    d['gx'] = np.ascontiguousarray(z['d_gate_x_w'][0][blks])
    vecs = np.zeros((80, 8, 8), np.float32)

    def fm(v):
        return v[cols].reshape(8, 80).T
    for j in range(4):
        vecs[:, :, j] = fm(z['d_conv_w'][0][j])
    vecs[:, :, 4] = fm(z['d_conv_b'][0])
    vecs[:, :, 5] = fm(z['d_gate_a_b'][0])
    vecs[:, :, 6] = fm(z['d_gate_x_b'][0])
    vecs[:, :, 7] = fm(z['d_lambda'][0])
    d['vecs'] = vecs
    d['g'] = z['norm_g'][3:4].copy()
    d['ident'] = np.eye(128, dtype=np.float32)
    return d


PAIRS = [[0, 1], [2, 3], [4, 5], [6, 7]]


def build_fused(T=4096, nlayers=4):
    CFG.T = T
    nc = bass.Bass("TRN2", target_bir_lowering=False)
    top = ExitStack()
    S = Sched(nc, top)
    CFG.nc, CFG.S = nc, S
    x_d = nc.dram_tensor("x", [T, D], F32, kind="ExternalInput").ap()
    out_d = nc.dram_tensor("out", [T, D], F32, kind="ExternalOutput").ap()
    p = [nc.dram_tensor(f"p_i{l}", [T, D], F32) for l in range(4)]
    CH = 512
    NCH = T // CH
    pg = [[nc.dram_tensor(f"pg_i{l}_{k}", [2 * CH, D], F32) for k in range(NCH)] for l in range(4)]

    def gsrc(l, rank):
        def f(t):
            return pg[l][t // 4].ap()[rank * CH + (t % 4) * 128:rank * CH + (t % 4 + 1) * 128, :]
        f.buf = lambda t: f"pg{l}_{t // 4}"
        return f
    xs = [nc.dram_tensor(f"xs_i{l}", [T, D], F32) for l in range(3)]
    layers = [("A", build_A, {}), ("B", build_B, dict(MC=128)), ("C", build_C, {}), ("D", build_D, {})]
    prev_x = x_d
    for l, (nm, fn, kw) in enumerate(layers[:nlayers]):
        CFG.prefix = nm + "_"
        SB_USED[0] = 0
        ov = {"p": p[l].ap()}
        if l == 0:
            ov["xin0"] = x_d
            nsrc = 1
        else:
            ov["xin0"] = prev_x
            ov["xin1"] = gsrc(l - 1, 0)
            ov["xin2"] = gsrc(l - 1, 1)
            ov["xs"] = xs[l - 1].ap()
            nsrc = 3
        CFG.override = ov
        _, es, _ = fn(nsrc, **kw)
        for k in range(NCH):
            S.op("pool", lambda h, l=l, k=k: h.collective_compute("AllGather", ALU.bypass, replica_groups=PAIRS, ins=[p[l].ap()[k * CH:(k + 1) * CH, :].opt()],
                                                               outs=[pg[l][k].ap().opt()]), w=[f"pg{l}_{k}"], dma=True, cc=True)
        S.emit(final=False)
        S.barrier()
        es.close()
        if l > 0:
            prev_x = xs[l - 1].ap()
    CFG.prefix = "F_"
    SB_USED[0] = 0
    CFG.override = {"xin0": prev_x, "xin1": gsrc(nlayers - 1, 0), "xin2": gsrc(nlayers - 1, 1), "out": out_d}
    _, es, _ = build_F(T)
    stats = S.emit(final=True)
    CFG.nc, CFG.S, CFG.override, CFG.prefix = None, None, {}, ""
    return nc, stats


def kernel(**inputs):
    z = {k: np.ascontiguousarray(np.asarray(v, dtype=np.float32)) for k, v in inputs.items()}
    T = 4096
    x = z['x']
    B = x.shape[0]
    nc, _ = build_fused(T)
    per_half = []
    for h in range(2):
        d = {}
        for pre, pd in (("A_", prep_A(z, h)), ("B_", prep_B(z, h, MC=128)), ("C_", prep_C(z, h, T)), ("D_", prep_D(z, h))):
            for k, v in pd.items():
                d[pre + k] = v
        d["F_g"] = z['final_g'][None, :].copy()
        per_half.append(d)
    in_maps = [dict(per_half[c % 2], x=x[c // 2]) for c in range(8)]
    res = run_bass_kernel_spmd(nc, in_maps, core_ids=list(range(8)))
    out = np.stack([res.results[2 * b]['out'] for b in range(B)]).astype(np.float32)
    return out
```
